# Optimizing a Trainium2 kernel written in Bass

```python
import jax, jax.numpy as jnp
from jax import lax
import numpy as np

D_MODEL = 1024
BATCH = 8
SEQ = 4096
DEPTH = 4

N_EVEN = (DEPTH + 1) // 2
N_ODD = DEPTH // 2
ROPE_THETA = 10000.0
EPS = 1e-6
NEG = -1e30
Q_BLOCK = 128

A_HEADS = 8
A_NOPE = 64
A_ROPE = 32
A_VD = 64
A_QLAT = 256
A_KVLAT = 128
A_WIDTH = A_HEADS * A_VD

B_HEADS = 8
B_KV_HEADS = 2
B_HD = 64
B_WIDTH = B_HEADS * B_HD
B_KV = B_KV_HEADS * B_HD
CMP_LEN = 32
CMP_STRIDE = 16
SEL_LEN = 64
N_SEL = 16
WINDOW = 512
NSA_Q_BLOCK = 64

C_HEADS = 8
C_HD = 64
C_WIDTH = C_HEADS * C_HD
IDX_HEADS = 8
IDX_HD = 32
TOPK_MAX = 256

D_HEADS = 4
D_QK = 64
D_VD = 128
D_QK_W = D_HEADS * D_QK
D_WIDTH = D_HEADS * D_VD
CONV_W = 4
CHUNK = 64

M_SLOTS = 256
M_HEADS = 4
M_HD = 64
M_WIDTH = M_HEADS * M_HD

MIX_WIDTH = A_WIDTH + B_WIDTH + M_WIDTH

EVEN_SIZES = (A_QLAT, A_KVLAT, A_ROPE, A_WIDTH,
              B_WIDTH, B_KV, B_KV, B_KV, B_KV, B_KV, B_KV, 3 * B_HEADS, B_WIDTH,
              M_WIDTH, M_WIDTH)
EVEN_COLS = A_QLAT + A_KVLAT + A_ROPE + A_WIDTH + 2 * B_WIDTH + 6 * B_KV + 3 * B_HEADS + 2 * M_WIDTH
ODD_SIZES = (C_WIDTH, C_HD, C_HD, IDX_HEADS * IDX_HD, IDX_HD, IDX_HEADS, C_WIDTH,
             D_QK_W, D_QK_W, D_WIDTH, D_HEADS, D_HEADS, D_WIDTH, D_WIDTH,
             M_WIDTH, M_WIDTH)
ODD_COLS = 2 * C_WIDTH + 2 * C_HD + IDX_HEADS * IDX_HD + IDX_HD + IDX_HEADS + 2 * D_QK_W + 3 * D_WIDTH + 2 * D_HEADS + 2 * M_WIDTH

kernel_name = "hybrid_mla_nsa_dsa_mlstm_trunk"


def rms_norm(x, g):
    xf = x.astype(jnp.float32)
    y = xf * lax.rsqrt(jnp.mean(xf * xf, axis=-1, keepdims=True) + EPS)
    return (y * g.astype(jnp.float32)).astype(x.dtype)


def rope(x, pos):
    d2 = x.shape[-1] // 2
    inv = ROPE_THETA ** (-jnp.arange(d2, dtype=jnp.float32) / d2)
    ang = pos.astype(jnp.float32)[..., None, None] * inv
    c, s = jnp.cos(ang), jnp.sin(ang)
    xf = x.astype(jnp.float32)
    x1, x2 = xf[..., :d2], xf[..., d2:]
    return jnp.concatenate([x1 * c - x2 * s, x1 * s + x2 * c], axis=-1).astype(x.dtype)


def split_cols(u, sizes):
    out, start = [], 0
    for n in sizes:
        out.append(u[..., start:start + n])
        start += n
    return out


def to_blocks(t, qb):
    return t.reshape(t.shape[0], t.shape[1] // qb, qb, *t.shape[2:]).swapaxes(0, 1)


def from_blocks(o):
    o = o.swapaxes(0, 1)
    return o.reshape(o.shape[0], o.shape[1] * o.shape[2], -1)


def causal_attention_blocked(q, k, v, scale):
    S = q.shape[1]
    kpos = jnp.arange(S)

    def block(args):
        qi, qx = args
        t = qi * Q_BLOCK + jnp.arange(Q_BLOCK)
        s = jnp.einsum('bqhd,bkhd->bhqk', qx, k).astype(jnp.float32) * scale
        s = jnp.where(kpos[None, :] <= t[:, None], s, NEG)
        p = jax.nn.softmax(s, axis=-1).astype(v.dtype)
        return jnp.einsum('bhqk,bkhd->bqhd', p, v)

    o = lax.map(block, (jnp.arange(S // Q_BLOCK), to_blocks(q, Q_BLOCK)))
    return from_blocks(o)


def mla_mixer(q_lat, kv_lat, k_rope, pos, q_lat_g, kv_lat_g, w_uq, w_ukv, q_norm_g, k_norm_g):
    B, S, _ = q_lat.shape
    q = (rms_norm(q_lat, q_lat_g) @ w_uq).reshape(B, S, A_HEADS, A_NOPE + A_ROPE)
    kv = (rms_norm(kv_lat, kv_lat_g) @ w_ukv).reshape(B, S, A_HEADS, A_NOPE + A_VD)
    q_nope = rms_norm(q[..., :A_NOPE], q_norm_g[:A_NOPE])
    q_pe = rope(rms_norm(q[..., A_NOPE:], q_norm_g[A_NOPE:]), pos)
    k_nope = rms_norm(kv[..., :A_NOPE], k_norm_g[:A_NOPE])
    k_pe = rope(rms_norm(k_rope[:, :, None, :], k_norm_g[A_NOPE:]), pos)
    k_pe = jnp.broadcast_to(k_pe, (B, S, A_HEADS, A_ROPE))
    qf = jnp.concatenate([q_nope, q_pe], axis=-1)
    kf = jnp.concatenate([k_nope, k_pe], axis=-1)
    v = kv[..., A_NOPE:]
    o = causal_attention_blocked(qf, kf, v, (A_NOPE + A_ROPE) ** -0.5)
    return o.reshape(B, S, A_WIDTH)


def nsa_mixer(q, k_c, v_c, k_s, v_s, k_w, v_w, gates, pos, q_g, k_g, cmp_pos, cmp_w1, cmp_w2):
    B, S, _ = q.shape
    G, R, hd = B_KV_HEADS, B_HEADS // B_KV_HEADS, B_HD
    scale = hd ** -0.5
    q = rope(rms_norm(q.reshape(B, S, B_HEADS, hd), q_g), pos)
    k_c, v_c, k_s, v_s, k_w, v_w = [t.reshape(B, S, G, hd) for t in (k_c, v_c, k_s, v_s, k_w, v_w)]
    k_s = rope(rms_norm(k_s, k_g[1]), pos)
    k_w = rope(rms_norm(k_w, k_g[2]), pos)

    n_cmp = (S - CMP_LEN) // CMP_STRIDE + 1
    tok = jnp.arange(n_cmp)[:, None] * CMP_STRIDE + jnp.arange(CMP_LEN)[None, :]
    cmp_end = tok[:, -1]

    def compress(t, pe, w1, w2):
        blk = t[:, tok] + pe[:, None, :]
        flat = blk.transpose(0, 1, 3, 2, 4).reshape(B, n_cmp, G, CMP_LEN * hd)
        return jax.nn.silu(flat @ w1) @ w2

    kcmp = rope(rms_norm(compress(k_c, cmp_pos[0], cmp_w1[0], cmp_w2[0]), k_g[0]), pos[:, cmp_end])
    vcmp = compress(v_c, cmp_pos[1], cmp_w1[1], cmp_w2[1])

    n_blk = S // SEL_LEN
    n_sel = min(N_SEL, n_blk)
    blk_start = jnp.arange(n_blk) * SEL_LEN
    overlap = ((tok[:, :1] <= blk_start[None, :] + SEL_LEN - 1)
               & (cmp_end[:, None] >= blk_start[None, :])).astype(jnp.float32)
    ksb = k_s.reshape(B, n_blk, SEL_LEN, G, hd).transpose(0, 3, 1, 2, 4)
    vsb = v_s.reshape(B, n_blk, SEL_LEN, G, hd).transpose(0, 3, 1, 2, 4)
    bi = jnp.arange(B)[:, None, None, None]
    gi = jnp.arange(G)[None, :, None, None]
    jb = jnp.arange(n_blk)

    kwp = jnp.pad(k_w, ((0, 0), (WINDOW, 0), (0, 0), (0, 0)))
    vwp = jnp.pad(v_w, ((0, 0), (WINDOW, 0), (0, 0), (0, 0)))

    qb = to_blocks(q.reshape(B, S, G, R, hd), NSA_Q_BLOCK)
    gb = to_blocks(jax.nn.sigmoid(gates.astype(jnp.float32)).reshape(B, S, G, R, 3), NSA_Q_BLOCK)

    def block(args):
        qi, qx, gx = args
        t = qi * NSA_Q_BLOCK + jnp.arange(NSA_Q_BLOCK)
        s = jnp.einsum('bqgrd,bngd->bgrqn', qx, kcmp).astype(jnp.float32) * scale
        m_c = cmp_end[None, :] <= t[:, None]
        p_c = jax.nn.softmax(jnp.where(m_c, s, NEG), axis=-1) * m_c
        o_c = jnp.einsum('bgrqn,bngd->bqgrd', p_c.astype(vcmp.dtype), vcmp)
        imp = jnp.einsum('bgrqn,nj->bgqj', p_c, overlap)
        cur = (t // SEL_LEN)[:, None]
        forced = jnp.where(jb[None, :] == cur, 3e4,
                           jnp.where(jb[None, :] == cur - 1, 2e4,
                                     jnp.where(jb[None, :] == 0, 1e4, 0.0)))
        adm = blk_start[None, :] <= t[:, None]
        score = jnp.where(adm, imp + forced, NEG)
        _, sel = lax.top_k(score, n_sel)
        kg = ksb[bi, gi, sel]
        vg = vsb[bi, gi, sel]
        s = jnp.einsum('bqgrd,bgqnld->bgrqnl', qx, kg).astype(jnp.float32) * scale
        kpos = sel[..., None] * SEL_LEN + jnp.arange(SEL_LEN)
        m_s = kpos <= t[None, None, :, None, None]
        s = jnp.where(m_s[:, :, None], s, NEG)
        p_s = jax.nn.softmax(s.reshape(*s.shape[:4], -1), axis=-1).reshape(s.shape)
        o_s = jnp.einsum('bgrqnl,bgqnld->bqgrd', p_s.astype(vg.dtype), vg)
        kx = lax.dynamic_slice_in_dim(kwp, qi * NSA_Q_BLOCK, WINDOW + NSA_Q_BLOCK, axis=1)
        vx = lax.dynamic_slice_in_dim(vwp, qi * NSA_Q_BLOCK, WINDOW + NSA_Q_BLOCK, axis=1)
        kp = qi * NSA_Q_BLOCK - WINDOW + jnp.arange(WINDOW + NSA_Q_BLOCK)
        m_w = (kp[None, :] <= t[:, None]) & (kp[None, :] > t[:, None] - WINDOW) & (kp[None, :] >= 0)
        s = jnp.einsum('bqgrd,bkgd->bgrqk', qx, kx).astype(jnp.float32) * scale
        p_w = jax.nn.softmax(jnp.where(m_w, s, NEG), axis=-1)
        o_w = jnp.einsum('bgrqk,bkgd->bqgrd', p_w.astype(vx.dtype), vx)
        o = gx[..., 0:1] * o_c + gx[..., 1:2] * o_s + gx[..., 2:3] * o_w
        return o.astype(qx.dtype)

    o = lax.map(block, (jnp.arange(S // NSA_Q_BLOCK), qb, gb))
    return from_blocks(o)


def dsa_mixer(q, k, v, iq, ik, iw, pos, q_g, k_g):
    B, S, _ = q.shape
    topk = min(TOPK_MAX, S // 4)
    scale = C_HD ** -0.5
    q = rope(rms_norm(q.reshape(B, S, C_HEADS, C_HD), q_g), pos)
    k = rope(rms_norm(k.reshape(B, S, 1, C_HD), k_g), pos)[:, :, 0]
    iq = rope(iq.reshape(B, S, IDX_HEADS, IDX_HD), pos)
    ik = rope(ik.reshape(B, S, 1, IDX_HD), pos)[:, :, 0]
    iw = iw * IDX_HEADS ** -0.5
    kpos = jnp.arange(S)
    bi = jnp.arange(B)[:, None, None]

    def block(args):
        qi, qx, iqx, iwx = args
        t = qi * Q_BLOCK + jnp.arange(Q_BLOCK)
        logits = jax.nn.relu(jnp.einsum('bqhd,bsd->bqhs', iqx, ik).astype(jnp.float32))
        score = jnp.einsum('bqh,bqhs->bqs', iwx.astype(jnp.float32), logits)
        score = jnp.where(kpos[None, None, :] <= t[None, :, None], score, NEG)
        _, sel = lax.top_k(score, topk)
        kg = k[bi, sel]
        vg = v[bi, sel]
        s = jnp.einsum('bqhd,bqkd->bhqk', qx, kg).astype(jnp.float32) * scale
        s = jnp.where((sel <= t[None, :, None])[:, None], s, NEG)
        p = jax.nn.softmax(s, axis=-1).astype(vg.dtype)
        return jnp.einsum('bhqk,bqkd->bqhd', p, vg)

    o = lax.map(block, (jnp.arange(S // Q_BLOCK), to_blocks(q, Q_BLOCK), to_blocks(iq, Q_BLOCK), to_blocks(iw, Q_BLOCK)))
    return from_blocks(o)


def mlstm_mixer(q, k, v, i_pre, f_pre, o_pre, conv_w, conv_b, i_bias, f_bias, h_norm_g):
    B, S, _ = q.shape
    dt = q.dtype
    H, dk, dv = D_HEADS, D_QK, D_VD
    qk = jnp.concatenate([q, k], axis=-1)
    y = lax.conv_general_dilated(qk, conv_w[:, None, :], window_strides=(1,), padding=[(CONV_W - 1, 0)],
                                 dimension_numbers=('NWC', 'WIO', 'NWC'), feature_group_count=qk.shape[-1])
    qk = jax.nn.silu(y + conv_b)
    f32 = jnp.float32
    qh = qk[..., :D_QK_W].reshape(B, S, H, dk).astype(f32)
    kh = qk[..., D_QK_W:].reshape(B, S, H, dk).astype(f32) * dk ** -0.5
    vh = v.reshape(B, S, H, dv).astype(f32)
    ig = (i_pre + i_bias).astype(f32)
    lf = jax.nn.log_sigmoid((f_pre + f_bias).astype(f32))
    nc = S // CHUNK

    def chunks(t):
        t = t.reshape(B, nc, CHUNK, H, *t.shape[3:])
        return jnp.moveaxis(jnp.moveaxis(t, 1, 0), 3, 2)

    tril = jnp.tri(CHUNK, dtype=bool)

    def step(carry, inp):
        C, n, m = carry
        qc, kc, vc, ic, lfc = inp
        b = jnp.cumsum(lfc, axis=-1)
        D = jnp.where(tril, b[..., :, None] - b[..., None, :] + ic[..., None, :], NEG)
        inter = b + m[..., None]
        m_t = jnp.maximum(inter, D.max(-1))
        a = jnp.exp(inter - m_t)
        w = jnp.einsum('bhtd,bhsd->bhts', qc, kc) * jnp.exp(D - m_t[..., None])
        num = a[..., None] * jnp.einsum('bhtd,bhdv->bhtv', qc, C) + jnp.einsum('bhts,bhsv->bhtv', w, vc)
        den = a * jnp.einsum('bhtd,bhd->bht', qc, n) + w.sum(-1)
        h = num / jnp.maximum(jnp.abs(den), jnp.exp(-m_t))[..., None]
        bL = b[..., -1]
        g = bL[..., None] - b + ic
        m_new = jnp.maximum(bL + m, g.max(-1))
        ws = jnp.exp(g - m_new[..., None])
        decay = jnp.exp(bL + m - m_new)
        C_new = decay[..., None, None] * C + jnp.einsum('bhs,bhsd,bhsv->bhdv', ws, kc, vc)
        n_new = decay[..., None] * n + jnp.einsum('bhs,bhsd->bhd', ws, kc)
        return (C_new, n_new, m_new), h

    init = (jnp.zeros((B, H, dk, dv), f32), jnp.zeros((B, H, dk), f32), jnp.zeros((B, H), f32))
    _, hs = lax.scan(step, init, (chunks(qh), chunks(kh), chunks(vh), chunks(ig), chunks(lf)))
    h = jnp.moveaxis(jnp.moveaxis(hs, 3, 2), 0, 1).reshape(B, S, H, dv)
    h = rms_norm(h, h_norm_g).reshape(B, S, D_WIDTH)
    return (jax.nn.sigmoid(o_pre.astype(f32)) * h).astype(dt)


def mem_xattn(q_raw, mem, mem_norm_g, w_kv, q_g, k_g):
    B, S, _ = q_raw.shape
    q = rms_norm(q_raw.reshape(B, S, M_HEADS, M_HD), q_g)
    kv = (rms_norm(mem, mem_norm_g) @ w_kv).reshape(B, mem.shape[1], 2, M_HEADS, M_HD)
    k = rms_norm(kv[:, :, 0], k_g)
    v = kv[:, :, 1]
    s = jnp.einsum('bqhd,bmhd->bhqm', q, k).astype(jnp.float32) * M_HD ** -0.5
    p = jax.nn.softmax(s, axis=-1).astype(v.dtype)
    return jnp.einsum('bhqm,bmhd->bqhd', p, v).reshape(B, S, M_WIDTH)


def setup_inputs(seed: int = 0) -> dict:
    key = jax.random.key(seed)
    keys = iter(jax.random.split(key, 40))
    f32 = jnp.float32

    def nrm(shape, scale):
        return jax.random.normal(next(keys), shape, f32) * scale

    def gain(shape):
        return 1.0 + 0.02 * jax.random.normal(next(keys), shape, f32)

    x = nrm((BATCH, SEQ, D_MODEL), 1.0)
    mem = nrm((BATCH, M_SLOTS, D_MODEL), 1.0)
    start = jax.random.randint(next(keys), (BATCH, 1), 0, 4096, dtype=jnp.int32)
    positions = start + jnp.arange(SEQ, dtype=jnp.int32)[None, :]
    return {
        "x": x,
        "mem": mem,
        "positions": positions,
        "ln_g": gain((DEPTH, D_MODEL)),
        "mem_norm_g": gain((DEPTH, D_MODEL)),
        "mem_w_kv": nrm((DEPTH, D_MODEL, 2 * M_WIDTH), D_MODEL ** -0.5),
        "mem_q_norm_g": gain((DEPTH, M_HD)),
        "mem_k_norm_g": gain((DEPTH, M_HD)),
        "w_out": nrm((DEPTH, MIX_WIDTH, D_MODEL), MIX_WIDTH ** -0.5),
        "even_w_in": nrm((N_EVEN, D_MODEL, EVEN_COLS), D_MODEL ** -0.5),
        "mla_q_lat_g": gain((N_EVEN, A_QLAT)),
        "mla_kv_lat_g": gain((N_EVEN, A_KVLAT)),
        "mla_w_uq": nrm((N_EVEN, A_QLAT, A_HEADS * (A_NOPE + A_ROPE)), A_QLAT ** -0.5),
        "mla_w_ukv": nrm((N_EVEN, A_KVLAT, A_HEADS * (A_NOPE + A_VD)), A_KVLAT ** -0.5),
        "mla_q_norm_g": gain((N_EVEN, A_NOPE + A_ROPE)),
        "mla_k_norm_g": gain((N_EVEN, A_NOPE + A_ROPE)),
        "nsa_q_norm_g": gain((N_EVEN, B_HD)),
        "nsa_k_norm_g": gain((N_EVEN, 3, B_HD)),
        "nsa_cmp_pos": nrm((N_EVEN, 2, CMP_LEN, B_HD), 0.02),
        "nsa_cmp_w1": nrm((N_EVEN, 2, CMP_LEN * B_HD, B_HD), (CMP_LEN * B_HD) ** -0.5),
        "nsa_cmp_w2": nrm((N_EVEN, 2, B_HD, B_HD), B_HD ** -0.5),
        "odd_w_in": nrm((N_ODD, D_MODEL, ODD_COLS), D_MODEL ** -0.5),
        "dsa_q_norm_g": gain((N_ODD, C_HD)),
        "dsa_k_norm_g": gain((N_ODD, C_HD)),
        "mlstm_conv_w": nrm((N_ODD, CONV_W, 2 * D_QK_W), CONV_W ** -0.5),
        "mlstm_conv_b": nrm((N_ODD, 2 * D_QK_W), 0.02),
        "mlstm_i_bias": nrm((N_ODD, D_HEADS), 0.1),
        "mlstm_f_bias": 3.0 + nrm((N_ODD, D_HEADS), 0.5),
        "mlstm_h_norm_g": gain((N_ODD, D_VD)),
    }


def reference(x, mem, positions, ln_g, mem_norm_g, mem_w_kv, mem_q_norm_g, mem_k_norm_g, w_out,
              even_w_in, mla_q_lat_g, mla_kv_lat_g, mla_w_uq, mla_w_ukv, mla_q_norm_g, mla_k_norm_g,
              nsa_q_norm_g, nsa_k_norm_g, nsa_cmp_pos, nsa_cmp_w1, nsa_cmp_w2,
              odd_w_in, dsa_q_norm_g, dsa_k_norm_g,
              mlstm_conv_w, mlstm_conv_b, mlstm_i_bias, mlstm_f_bias, mlstm_h_norm_g):
    for layer in range(DEPTH):
        h = rms_norm(x, ln_g[layer])
        li = layer // 2
        if layer % 2 == 0:
            u = h @ even_w_in[li]
            (a_ql, a_kvl, a_kr, a_gate, b_q, b_kc, b_vc, b_ks, b_vs, b_kw, b_vw, b_g, b_gate,
             m_q, m_gate) = split_cols(u, EVEN_SIZES)
            y_a = mla_mixer(a_ql, a_kvl, a_kr, positions, mla_q_lat_g[li], mla_kv_lat_g[li],
                            mla_w_uq[li], mla_w_ukv[li], mla_q_norm_g[li], mla_k_norm_g[li])
            y_b = nsa_mixer(b_q, b_kc, b_vc, b_ks, b_vs, b_kw, b_vw, b_g, positions,
                            nsa_q_norm_g[li], nsa_k_norm_g[li], nsa_cmp_pos[li], nsa_cmp_w1[li], nsa_cmp_w2[li])
            y1 = y_a * jax.nn.silu(a_gate)
            y2 = y_b * jax.nn.silu(b_gate)
        else:
            u = h @ odd_w_in[li]
            (c_q, c_k, c_v, c_iq, c_ik, c_iw, c_gate, d_q, d_k, d_v, d_i, d_f, d_o, d_gate,
             m_q, m_gate) = split_cols(u, ODD_SIZES)
            y_c = dsa_mixer(c_q, c_k, c_v, c_iq, c_ik, c_iw, positions, dsa_q_norm_g[li], dsa_k_norm_g[li])
            y_d = mlstm_mixer(d_q, d_k, d_v, d_i, d_f, d_o, mlstm_conv_w[li], mlstm_conv_b[li],
                              mlstm_i_bias[li], mlstm_f_bias[li], mlstm_h_norm_g[li])
            y1 = y_c * jax.nn.silu(c_gate)
            y2 = y_d * jax.nn.silu(d_gate)
        y_m = mem_xattn(m_q, mem, mem_norm_g[layer], mem_w_kv[layer], mem_q_norm_g[layer], mem_k_norm_g[layer])
        mix = jnp.concatenate([y1, y2, y_m * jax.nn.silu(m_gate)], axis=-1)
        x = x + mix @ w_out[layer]
    return x
```

```python
import math
import numpy as np
import ml_dtypes
from contextlib import ExitStack
import concourse.bass as bass
import concourse.mybir as mybir
from concourse.bass_utils import run_bass_kernel_spmd

F32 = mybir.dt.float32
BF16 = mybir.dt.bfloat16
I32 = mybir.dt.int32
AF = mybir.ActivationFunctionType
ALU = mybir.AluOpType
AX = mybir.AxisListType

D = 1024
BIG = 30000.0
EPS = 1e-6
EVEN_COLS = 3256
ODD_COLS = 4016
ENGS = ('pe', 'act', 'dve', 'pool', 'sp')
EPOCH = 16000
NDQ = 8


class Buf:
    __slots__ = ('name', 'w', 'r')

    def __init__(self, name=''):
        self.name = name
        self.w = None
        self.r = {}


class Sched:
    def __init__(self, nc, es):
        self.nc = nc
        self.es = es
        self.prog = {e: [] for e in ENGS}
        self.esem = {e: [] for e in ENGS}
        self.cnt = {e: 0 for e in ENGS}
        self.seen = {e: {} for e in ENGS}
        self.dq = ('sp', 'pool', 'act')
        self.dsem = {q: [es.enter_context(nc.semaphore(f'D{q}{i}')) for i in range(NDQ)] for q in self.dq}
        self.dcnt = {q: 0 for q in self.dq}
        self.ninst = 0

    def _semobj(self, key):
        if key[0] == 'E':
            return self.esem[key[1]][key[2]]
        return self.dsem[key[1]][key[2]]

    def op(self, e, fn, reads=(), writes=(), dma=False):
        deps = {}
        for b in reads:
            if b.w is not None:
                k, v = b.w
                if deps.get(k, 0) < v:
                    deps[k] = v
        for b in writes:
            if b.w is not None:
                k, v = b.w
                if deps.get(k, 0) < v:
                    deps[k] = v
            for k, v in b.r.items():
                if deps.get(k, 0) < v:
                    deps[k] = v
        waits = []
        seen = self.seen[e]
        for k, v in deps.items():
            if e == 'pe' and k[0] == 'E' and k[1] == 'pe':
                continue
            if seen.get(k, 0) >= v:
                continue
            seen[k] = v
            waits.append((self._semobj(k), v))
        if dma:
            j = self.dcnt[e]
            self.dcnt[e] += 1
            slot = j % NDQ
            val = 16 * (j // NDQ + 1)
            key = ('D', e, slot)
            if val > 16 and seen.get(key, 0) < val - 16:
                seen[key] = val - 16
                waits.append((self.dsem[e][slot], val - 16))
            sem = self.dsem[e][slot]
            inc = 16
        else:
            c = self.cnt[e]
            ep = c // EPOCH
            if ep >= len(self.esem[e]):
                self.esem[e].append(self.es.enter_context(self.nc.semaphore(f'S{e}{ep}')))
            self.cnt[e] += 1
            key = ('E', e, ep)
            val = c % EPOCH + 1
            sem = self.esem[e][ep]
            inc = 1
        ev = (key, val)
        self.ninst += 1

        def thunk(eng, waits=waits, fn=fn, sem=sem, inc=inc):
            for s, v in waits:
                eng.wait_ge(s, v)
            fn(eng).then_inc(sem, inc)
        self.prog[e].append(thunk)
        for b in reads:
            if b.r.get(key, 0) < val:
                b.r[key] = val
        for b in writes:
            b.w = ev
            b.r = {}
        return ev

    def barrier(self):
        evs = []
        for e in ENGS:
            c = self.cnt[e]
            if c > 0:
                ep = (c - 1) // EPOCH
                evs.append((('E', e, ep), (c - 1) % EPOCH + 1))
        for q in self.dq:
            n = self.dcnt[q]
            for slot in range(min(n, NDQ)):
                cntslot = (n - 1 - slot) // NDQ + 1
                evs.append((('D', q, slot), 16 * cntslot))
        for e in ENGS:
            waits = []
            for k, v in evs:
                if k[0] == 'E' and k[1] == e:
                    continue
                if self.seen[e].get(k, 0) >= v:
                    continue
                self.seen[e][k] = v
                waits.append((self._semobj(k), v))

            def thunk(eng, waits=waits):
                for s, v in waits:
                    eng.wait_ge(s, v)
            self.prog[e].append(thunk)

    def emit(self):
        nc = self.nc
        with nc.Block() as block:
            @block.tensor
            def _(eng):
                for t in self.prog['pe']:
                    t(eng)

            @block.scalar
            def _(eng):
                for t in self.prog['act']:
                    t(eng)

            @block.vector
            def _(eng):
                for t in self.prog['dve']:
                    t(eng)

            @block.gpsimd
            def _(eng):
                for t in self.prog['pool']:
                    t(eng)

            @block.sync
            def _(eng):
                for t in self.prog['sp']:
                    t(eng)


class Tl:
    __slots__ = ('t', 'b')

    def __init__(self, t, name):
        self.t = t
        self.b = Buf(name)

    def __getitem__(self, k):
        return self.t[k]


class Builder:
    def __init__(self, T, layers, dbg=()):
        self.T = T
        self.NT = T // 128
        self.layers = list(layers)
        self.dbg = set(dbg)
        self.nc = bass.Bass("TRN2", target_bir_lowering=False)
        self.uid = 0
        self.rec = None
        self.dbg_out = {}

    def din(self, name, shape, dt=F32):
        return self.nc.dram_tensor(name, list(shape), dt, kind="ExternalInput").ap()

    def dscr(self, name, shape, dt, nbuf=1):
        kind = "ExternalOutput" if name in self.dbg else "Internal"
        t = self.nc.dram_tensor(name, list(shape), dt, kind=kind).ap()
        tl = Tl(t, name)
        if nbuf > 1:
            tl.b = [Buf(f'{name}{i}') for i in range(nbuf)]
        return tl

    def sb(self, es, name, shape, dt):
        self.uid += 1
        t = es.enter_context(self.nc.sbuf_tensor(f'{name}_{self.uid}', list(shape), dt))
        return Tl(t, name)

    def op_(self, e, fn, reads=(), writes=(), dma=False):
        if self.rec is not None:
            self.rec.append((e, fn, tuple(reads), tuple(writes), dma))
        else:
            self.S.op(e, fn, reads=reads, writes=writes, dma=dma)

    def chains_begin(self, names):
        self._chains = {k: [] for k in names}

    def chain(self, name, scr=None):
        self.rec = self._chains[name]
        self.scr = scr if scr is not None else self.scr0

    def chains_emit(self):
        self.rec = None
        self.scr = self.scr0
        lists = [l for l in self._chains.values() if l]
        idx = [0] * len(lists)
        left = sum(len(l) for l in lists)
        while left:
            for i, l in enumerate(lists):
                if idx[i] < len(l):
                    e, fn, R, W, dma = l[idx[i]]
                    idx[i] += 1
                    left -= 1
                    self.S.op(e, fn, reads=R, writes=W, dma=dma)

    def new_scr(self, es, tag, w):
        sc = {}
        for k in ('sq', 'tmp', 'ra', 'rb'):
            sc[k] = self.sb(es, f'sc_{k}_{tag}', [128, w], F32)
        for k in ('ssq', 'ln', 'rs'):
            sc[k] = self.sb(es, f'sc_{k}_{tag}', [128, 16], F32)
        return sc

    def mm(self, out, lhsT, rhs, start, stop, R, W):
        self.op_('pe', lambda e: e.matmul(out, lhsT=lhsT, rhs=rhs, start=start, stop=stop,
                                           skip_group_check=True), reads=R, writes=W)

    def tr(self, out, in_, ident, R, W):
        self.op_('pe', lambda e: e.transpose(out=out, in_=in_, identity=ident), reads=R, writes=W)

    def act(self, out, in_, func, R, W, bias=None, scale=None, accum=None):
        kw = {}
        if bias is not None:
            kw['bias'] = bias
        if scale is not None:
            kw['scale'] = scale
        if accum is not None:
            kw['accum_out'] = accum
        self.op_('act', lambda e: e.activation(out=out, in_=in_, func=func, **kw), reads=R, writes=W)

    def tt(self, eng, out, in0, in1, op, R, W):
        self.op_(eng, lambda e: e.tensor_tensor(out=out, in0=in0, in1=in1, op=op), reads=R, writes=W)

    def ts(self, eng, out, in0, s1, op0, R, W, s2=None, op1=None, accum=None):
        kw = {}
        if op1 is not None:
            kw['op1'] = op1
        if accum is not None:
            kw['accum_out'] = accum
        self.op_(eng, lambda e: e.tensor_scalar(out=out, in0=in0, scalar1=s1, scalar2=s2, op0=op0, **kw),
                  reads=R, writes=W)

    def stt(self, out, in0, scalar, in1, op0, op1, R, W):
        self.op_('dve', lambda e: e.scalar_tensor_tensor(out=out, in0=in0, scalar=scalar, in1=in1,
                                                         op0=op0, op1=op1), reads=R, writes=W)

    def cp(self, eng, out, in_, R, W):
        if eng == 'act':
            self.op_('act', lambda e: e.copy(out=out, in_=in_), reads=R, writes=W)
        else:
            self.op_(eng, lambda e: e.tensor_copy(out=out, in_=in_), reads=R, writes=W)

    def red(self, out, in_, op, R, W):
        self.op_('dve', lambda e: e.tensor_reduce(out=out, in_=in_, axis=AX.X, op=op), reads=R, writes=W)

    def memset(self, eng, ap, val, W):
        self.op_(eng, lambda e: e.memset(ap, val), writes=W)

    def dma(self, q, out, in_, R, W, **kw):
        self.op_(q, lambda e: e.dma_start(out=out, in_=in_, **kw), reads=R, writes=W, dma=True)

    def bc_load(self, es, name, row_ap, d):
        t = self.sb(es, name, [128, d], F32)
        self.dma('sp', t[:], row_ap.to_broadcast([128, d]), [], [t.b])
        return t

    def wload(self, w, k, src, c0, c1):
        c = c0
        while c < c1:
            ce = min(c1, c + 2048)
            self.dma('pool', w[:, k, c:ce], src[:, c:ce], [], [w.b])
            c = ce

    def rstd(self, ssq, H, d, R):
        ln, rs = self.scr['ln'], self.scr['rs']
        self.act(ln[:, 0:H], ssq, AF.Ln, R, [ln.b], bias=self.eps_c[:, 0:1], scale=1.0 / d)
        self.act(rs[:, 0:H], ln[:, 0:H], AF.Exp, [ln.b], [rs.b], scale=-0.5)
        return rs[:, 0:H]

    def rmsn(self, src, H, d, g, dst, R, W):
        sq, ssq, tmp = self.scr['sq'], self.scr['ssq'], self.scr['tmp']
        sqv = sq[:, 0:H * d].rearrange("p (h c) -> p h c", h=H)
        self.tt('pool', sqv, src, src, ALU.mult, R, [sq.b])
        self.red(ssq[:, 0:H], sqv, ALU.add, [sq.b], [ssq.b])
        rs = self.rstd(ssq[:, 0:H], H, d, [ssq.b])
        tv = tmp[:, 0:H * d].rearrange("p (h c) -> p h c", h=H)
        self.tt('dve', tv, src, rs.unsqueeze(2).to_broadcast([128, H, d]), ALU.mult,
                list(R) + [self.scr['rs'].b], [tmp.b])
        self.tt('pool', dst, tv, g[:].unsqueeze(1).to_broadcast([128, H, d]), ALU.mult,
                [tmp.b, g.b], W)

    def rope(self, src, H, d2, n, dst, R, W):
        tab = self.rope32 if d2 == 32 else self.rope16
        cosv = tab[:, n, 0:d2]
        sinv = tab[:, n, d2:2 * d2]
        A, Bm = self.scr['ra'], self.scr['rb']
        s4 = src.rearrange("p h (two c) -> p h two c", two=2)
        d4 = dst.rearrange("p h (two c) -> p h two c", two=2)
        Av = A[:, 0:H * 2 * d2].rearrange("p (h two c) -> p h two c", h=H, two=2)
        Bv = Bm[:, 0:H * 2 * d2].rearrange("p (h two c) -> p h two c", h=H, two=2)
        cb = cosv.unsqueeze(1).unsqueeze(1).to_broadcast([128, H, 2, d2])
        sbv = sinv.unsqueeze(1).unsqueeze(1).to_broadcast([128, H, 2, d2])
        self.tt('dve', Av, s4, cb, ALU.mult, list(R) + [tab.b], [A.b])
        self.tt('pool', Bv, s4, sbv, ALU.mult, list(R) + [tab.b], [Bm.b])
        self.tt('dve', d4[:, :, 0, :], Av[:, :, 0, :], Bv[:, :, 1, :], ALU.subtract, [A.b, Bm.b], W)
        self.tt('pool', d4[:, :, 1, :], Bv[:, :, 0, :], Av[:, :, 1, :], ALU.add, [A.b, Bm.b], W)

    def attn_block(self, rhs_q, q_R, N, tiles, scale, acc_views, acc_bufs, mode='exp', LA=2, epilogue=None):
        started = set()
        nt = len(tiles)
        last_use = {}
        for i, tl in enumerate(tiles):
            for j in tl['subs']:
                last_use[j] = i
        Atiles = {}

        def emit_s(i):
            tl = tiles[i]
            bi = self.st_rr % len(self.st_banks)
            self.st_rr += 1
            bank, bb = self.st_banks[bi]
            c0 = tl['c0']
            masks = tl.get('masks', [])
            if mode != 'exp':
                E, eR = tl['Efn']()
            self.mm(bank[:, c0:N], tl['kT'], rhs_q[:, c0:N], True, len(masks) == 0, list(q_R) + tl['R'], [bb])
            for mi, (ml, mr, off, ncols, mR) in enumerate(masks):
                self.mm(bank[:, off:off + ncols], ml, mr, False, mi == len(masks) - 1, mR, [bb])
            ai = self.at_rr % len(self.at_tiles)
            self.at_rr += 1
            A = self.at_tiles[ai]
            Atiles[i] = A
            if mode == 'exp':
                self.act(A[:, c0:N], bank[:, c0:N], AF.Exp, [bb], [A.b], scale=scale)
            else:
                self.tt('dve', A[:, c0:N], bank[:, c0:N], E[:, c0:N], ALU.mult, [bb] + eR, [A.b])
                if tl.get('diag') is not None:
                    dj = tl['diag']
                    self.tt('pool', A[:, dj * 128:(dj + 1) * 128], A[:, dj * 128:(dj + 1) * 128],
                            self.tri01T[:], ALU.mult, [A.b, self.tri01T.b], [A.b])

        def emit_pv(i):
            tl = tiles[i]
            A = Atiles.pop(i)
            for j in tl['subs']:
                view, bk = acc_views[j]
                st = bk not in started
                started.add(bk)
                self.mm(view, A[:, j * 128:(j + 1) * 128], tl['V'], st, last_use[j] == i,
                        [A.b] + tl['R'], [acc_bufs[bk]])

        for step in range(nt):
            emit_s(step)
            self.pend.append(('pv', (lambda i=step: emit_pv(i))))
            self.npv += 1
            self._drain(LA)
        if epilogue is None:
            self.attn_flush()
            return None
        self.epi_id += 1
        eid = self.epi_id
        self.pend.append(('epi', epilogue, eid))
        self._drain(LA)
        return eid

    def _drain(self, limit):
        q = self.pend
        while q and (q[0][0] == 'epi' or self.npv > limit):
            it = q.pop(0)
            if it[0] == 'pv':
                self.npv -= 1
                it[1]()
            else:
                it[1]()
                self.epi_done.add(it[2])

    def attn_flush(self):
        self._drain(-1)

    def attn_sync(self, eid):
        while eid is not None and eid not in self.epi_done:
            q = self.pend
            it = q.pop(0)
            if it[0] == 'pv':
                self.npv -= 1
                it[1]()
            else:
                it[1]()
                self.epi_done.add(it[2])

    def build(self):
        nc = self.nc
        T, NT = self.T, self.NT
        with ExitStack() as es:
            self.S = S = Sched(nc, es)
            self._decl_inputs()
            self._decl_scratch()
            self.ps = []
            for i in range(8):
                t = es.enter_context(nc.psum_tensor(f"psb{i}", [128, 512], F32))
                self.ps.append(Tl(t, f'ps{i}'))
            self._consts(es)
            self._prep(es)
            S.barrier()
            xin = Tl(self.x_in, 'xin')
            xin.b = [Buf('xin')] * 1
            cur = xin
            for idx, L in enumerate(self.layers):
                last = idx == len(self.layers) - 1
                dst = self.out_t if last else self.xs[idx % 2]
                with ExitStack() as les:
                    if L % 2 == 0:
                        self.layer_even(les, L, cur, dst)
                    else:
                        self.layer_odd(les, L, cur, dst)
                S.barrier()
                cur = dst
            S.barrier()
            S.emit()
        return nc

    def _decl_inputs(self):
        T, NT = self.T, self.NT
        d = self.din
        self.x_in = d("x", [T, D])
        self.mem = d("mem", [256, D])
        self.pos_t = d("pos_t", [128, NT], I32)
        self.w = {}
        spec = dict(
            ln_g=[4, D], mem_norm_g=[4, D], mem_w_kv=[4, D, 512], mem_q_norm_g=[4, 64], mem_k_norm_g=[4, 64],
            w_out=[4, 1280, D], even_w_in=[2, D, EVEN_COLS], mla_q_lat_g=[2, 256], mla_kv_lat_g=[2, 128],
            mla_w_uq=[2, 256, 768], mla_w_ukv=[2, 128, 1024], mla_q_norm_g=[2, 96], mla_k_norm_g=[2, 96],
            nsa_q_norm_g=[2, 64], nsa_k_norm_g=[2, 3, 64], nsa_cmp_posT=[2, 2, 64, 32],
            nsa_cmp_w1=[2, 2, 2048, 64], nsa_cmp_w2=[2, 2, 64, 64], odd_w_in=[2, D, ODD_COLS],
            dsa_q_norm_g=[2, 64], dsa_k_norm_g=[2, 64], mlstm_conv_wT=[2, 512, 4], mlstm_conv_b=[2, 512],
            mlstm_i_bias=[2, 4], mlstm_f_bias=[2, 4], mlstm_h_norm_g=[2, 128])
        for k, shp in spec.items():
            self.w[k] = d(k, shp)
        ncp = self.ncmp_pad = max(1, T // 2048) * 128
        nb = self.n_blk = T // 64
        self.c = dict(
            ident=d("c_ident", [128, 128], BF16), identf=d("c_identf", [128, 128], F32),
            i4=d("c_i4", [128, 512], BF16), i8x2=d("c_i8x2", [128, 512], BF16),
            tri_neg=d("c_tri_neg", [128, 128], BF16), edge_neg=d("c_edge_neg", [128, 128], BF16),
            tri01T=d("c_tri01T", [128, 128], BF16), trinegf=d("c_trinegf", [128, 128], F32),
            invf=d("c_invf", [128, 32]), pw=d("c_pw", [128, 40]), cmpneg=d("c_cmpneg", [T, ncp], BF16),
            forced=d("c_forced", [T, nb]), overlap=d("c_overlap", [ncp, nb], BF16))

    def _decl_scratch(self):
        T, NT = self.T, self.NT
        s = self.dscr
        self.out_t = Tl(self.nc.dram_tensor("out", [T, D], F32, kind="ExternalOutput").ap(), 'out')
        self.out_t.b = [Buf(f'out{i}') for i in range(NT)]
        self.xs = [s("xs0", [T, D], F32, NT), s("xs1", [T, D], F32, NT)]
        self.rope_d = s("rope_d", [T, 64], F32)
        self.Y = s("Y", [T, 2304], F32)
        self.G = s("G", [T, 1792], BF16, NT)
        self.SG = s("SG", [T, 32], F32, NT)
        self.mla_qT = s("mla_qT", [8, 96, T], BF16)
        self.mla_kT = s("mla_kT", [8, 96, T], BF16)
        self.mla_v = s("mla_v", [8, 128, NT, 65], BF16)
        self.nsa_qT = s("nsa_qT", [2, NT, 64, 4, 128], BF16)
        self.nsa_kT = s("nsa_kT", [8, 64, T], BF16)
        self.nsa_v = s("nsa_v", [4, 128, NT, 65], BF16)
        self.mem_qT = s("mem_qT", [4, 64, T], BF16)
        self.dsa_qT = s("dsa_qT", [NT, 64, 8, 128], BF16)
        self.dsa_kT = s("dsa_kT", [64, T], BF16)
        self.dsa_v = s("dsa_v", [128, NT, 65], BF16)
        self.idx_qT = s("idx_qT", [8, 32, T], BF16)
        self.idx_kT = s("idx_kT", [32, T], BF16)
        self.idx_w = s("idx_w", [T, 8], F32)
        self.ml_raw = s("ml_raw", [512, T], F32)
        self.ml_if = s("ml_if", [8, T], F32)
        self.ml_qkT = s("ml_qkT", [512, T], BF16)
        self.ml_v = s("ml_v", [4, 128, NT, 129], BF16)
        self.ml_g = s("ml_g", [12, T], F32)

    def _consts(self, es):
        c = self.c
        def ld(name, shape, dt):
            t = self.sb(es, name, shape, dt)
            self.dma('sp', t[:], c[name], [], [t.b])
            return t
        self.ident = ld('ident', [128, 128], BF16)
        self.identf = ld('identf', [128, 128], F32)
        self.i4 = ld('i4', [128, 512], BF16)
        self.i8x2 = ld('i8x2', [128, 512], BF16)
        self.tri_neg = ld('tri_neg', [128, 128], BF16)
        self.edge_neg = ld('edge_neg', [128, 128], BF16)
        self.tri01T = ld('tri01T', [128, 128], BF16)
        self.trinegf = ld('trinegf', [128, 128], F32)
        self.invf = ld('invf', [128, 32], F32)
        self.pw = ld('pw', [128, 40], F32)
        self.eps_c = self.sb(es, 'eps_c', [128, 1], F32)
        self.memset('dve', self.eps_c[:], EPS, [self.eps_c.b])
        self.one_c = self.sb(es, 'one_c', [128, 1], F32)
        self.memset('dve', self.one_c[:], 1.0, [self.one_c.b])
        self.rope32 = self.sb(es, 'rope32', [128, self.NT, 64], F32)
        self.rope16 = self.sb(es, 'rope16', [128, self.NT, 32], F32)
        self.sc_sq = self.sb(es, 'sc_sq', [128, 512], F32)
        self.sc_tmp = self.sb(es, 'sc_tmp', [128, 512], F32)
        self.sc_ra = self.sb(es, 'sc_ra', [128, 512], F32)
        self.sc_rb = self.sb(es, 'sc_rb', [128, 512], F32)
        self.sc_ssq = self.sb(es, 'sc_ssq', [128, 16], F32)
        self.sc_ln = self.sb(es, 'sc_ln', [128, 16], F32)
        self.sc_rs = self.sb(es, 'sc_rs', [128, 16], F32)
        self.scr0 = dict(sq=self.sc_sq, tmp=self.sc_tmp, ra=self.sc_ra, rb=self.sc_rb, ssq=self.sc_ssq,
                         ln=self.sc_ln, rs=self.sc_rs)
        self.scr = self.scr0

    def _prep(self, es0):
        NT = self.NT
        PI = math.pi
        with ExitStack() as es:
            pi_t = self.sb(es, 'pos_i', [128, NT], I32)
            self.dma('sp', pi_t[:], self.pos_t, [], [pi_t.b])
            pf = self.sb(es, 'pos_f', [128, NT], F32)
            self.cp('dve', pf[:], pi_t[:], [pi_t.b], [pf.b])
            ang = self.sb(es, 'ang', [128, NT, 32], F32)
            self.tt('dve', ang[:], pf[:].unsqueeze(2).to_broadcast([128, NT, 32]),
                    self.invf[:].unsqueeze(1).to_broadcast([128, NT, 32]), ALU.mult,
                    [pf.b, self.invf.b], [ang.b])
            kf = self.sb(es, 'kf', [128, NT, 32], F32)
            ki = self.sb(es, 'ki', [128, NT, 32], I32)
            r = self.sb(es, 'r', [128, NT, 32], F32)
            m = self.sb(es, 'm', [128, NT, 32], F32)

            def wrap(buf):
                self.ts('dve', m[:], buf[:], PI, ALU.is_gt, [buf.b], [m.b], s2=-2 * PI, op1=ALU.mult)
                self.tt('dve', buf[:], buf[:], m[:], ALU.add, [buf.b, m.b], [buf.b])
                self.ts('dve', m[:], buf[:], -PI, ALU.is_lt, [buf.b], [m.b], s2=2 * PI, op1=ALU.mult)
                self.tt('dve', buf[:], buf[:], m[:], ALU.add, [buf.b, m.b], [buf.b])
                self.ts('dve', buf[:], buf[:], PI, ALU.min, [buf.b], [buf.b], s2=-PI, op1=ALU.max)
            self.ts('dve', kf[:], ang[:], 1.0 / (2 * PI), ALU.mult, [ang.b], [kf.b])
            self.cp('dve', ki[:], kf[:], [kf.b], [ki.b])
            self.cp('dve', kf[:], ki[:], [ki.b], [kf.b])
            C1 = 6.28125
            C2 = 2 * PI - C1
            self.stt(r[:], kf[:], -C1, ang[:], ALU.mult, ALU.add, [kf.b, ang.b], [r.b])
            self.stt(r[:], kf[:], -C2, r[:], ALU.mult, ALU.add, [kf.b, r.b], [r.b])
            wrap(r)
            self.act(self.rope32[:, :, 32:64], r[:], AF.Sin, [r.b], [self.rope32.b])
            self.ts('dve', r[:], r[:], PI / 2, ALU.add, [r.b], [r.b])
            wrap(r)
            self.act(self.rope32[:, :, 0:32], r[:], AF.Sin, [r.b], [self.rope32.b])
            self.cp('dve', self.rope16[:, :, 0:16], self.rope32[:, :, 0:32:2], [self.rope32.b], [self.rope16.b])
            self.cp('dve', self.rope16[:, :, 16:32], self.rope32[:, :, 32:64:2], [self.rope32.b], [self.rope16.b])
            self.dma('sp', self.rope_d[:].rearrange("(n p) c -> p n c", p=128), self.rope32[:],
                     [self.rope32.b], [self.rope_d.b])
            self.S.barrier()

    def load_win(self, es, L, w_in, ncols):
        win = self.sb(es, 'win', [128, 8, ncols], BF16)
        for k in range(8):
            self.wload(win, k, w_in[k * 128:(k + 1) * 128, :], 0, ncols)
        lng = self.bc_load(es, 'lng', self.w['ln_g'][L:L + 1, :], D)
        return win, lng

    def load_wout(self, es, L):
        wout = self.sb(es, 'wout', [128, 10, D], BF16)
        for k in range(10):
            self.wload(wout, k, self.w['w_out'][L, k * 128:(k + 1) * 128, :], 0, D)
        return wout

    def mem_kv(self, es, L):
        w = self.w
        kT = self.sb(es, 'memkT', [64, 4, 256], BF16)
        V = self.sb(es, 'memV', [128, 2, 4, 65], BF16)
        self.memset('pool', V[:], 1.0, [V.b])
        with ExitStack() as s2:
            wkv = self.sb(s2, 'wkv', [128, 8, 512], BF16)
            for k in range(8):
                self.wload(wkv, k, w['mem_w_kv'][L, k * 128:(k + 1) * 128, :], 0, 512)
            mg = self.bc_load(s2, 'mg', w['mem_norm_g'][L:L + 1, :], D)
            kg = self.bc_load(s2, 'kg', w['mem_k_norm_g'][L:L + 1, :], 64)
            mt = self.sb(s2, 'mt', [128, D], F32)
            junk = self.sb(s2, 'junk', [128, D], BF16)
            mh = self.sb(s2, 'mh', [128, D], BF16)
            mhT = self.sb(s2, 'mhT', [128, 8, 128], BF16)
            kv = self.sb(s2, 'kv', [128, 512], F32)
            kn = self.sb(s2, 'kn', [128, 256], BF16)
            ss = self.sb(s2, 'ss', [128, 1], F32)
            for i in range(2):
                self.dma('sp', mt[:], self.mem[i * 128:(i + 1) * 128, :], [], [mt.b])
                self.act(junk[:], mt[:], AF.Square, [mt.b], [junk.b, ss.b], accum=ss[:])
                rs = self.rstd(ss[:, 0:1], 1, D, [ss.b])
                self.stt(mh[:], mt[:], rs, mg[:], ALU.mult, ALU.mult, [mt.b, self.scr['rs'].b, mg.b], [mh.b])
                pT = self.ps[2]
                pTb = pT[:].bitcast(BF16)
                for k in range(8):
                    self.tr(pTb[:, k * 128:(k + 1) * 128], mh[:, k * 128:(k + 1) * 128], self.ident[:],
                            [mh.b, self.ident.b], [pT.b])
                self.cp('act', mhT[:].rearrange("p k c -> p (k c)"), pTb[:, 0:1024], [pT.b], [mhT.b])
                pU = self.ps[0]
                for k in range(8):
                    self.mm(pU[:, 0:512], mhT[:, k, :], wkv[:, k, :], k == 0, k == 7, [mhT.b, wkv.b], [pU.b])
                self.cp('act', kv[:], pU[:, 0:512], [pU.b], [kv.b])
                self.rmsn(kv[:, 0:256].rearrange("p (h c) -> p h c", h=4), 4, 64, kg,
                          kn[:].rearrange("p (h c) -> p h c", h=4), [kv.b], [kn.b])
                self.cp('dve', V[:, i, :, 0:64], kv[:, 256:512].rearrange("p (h c) -> p h c", h=4), [kv.b], [V.b])
                pK = self.ps[3]
                pKb = pK[:].bitcast(BF16)
                for h in range(4):
                    self.tr(pKb[0:64, h * 128:(h + 1) * 128], kn[:, h * 64:(h + 1) * 64], self.ident[:],
                            [kn.b, self.ident.b], [pK.b])
                self.cp('act', kT[:, :, i * 128:(i + 1) * 128],
                        pKb[0:64, 0:512].rearrange("p (h c) -> p h c", h=4), [pK.b], [kT.b])
            self.S.barrier()
        return kT, V

    def p1_front(self, n, x_src, xt, ht, hT, u, ss, junk, lng, win, ncols):
        sl = n % 2
        x_t, h_t, hT_t, u_t = xt[sl], ht[sl], hT[sl], u[sl]
        xb = x_src.b[n] if len(x_src.b) > 1 else x_src.b[0]
        self.dma('sp', x_t[:], x_src[n * 128:(n + 1) * 128, :], [xb], [x_t.b])
        self.act(junk[:], x_t[:], AF.Square, [x_t.b], [junk.b, ss.b], accum=ss[:])
        rs = self.rstd(ss[:, 0:1], 1, D, [ss.b])
        self.stt(h_t[:], x_t[:], rs, lng[:], ALU.mult, ALU.mult, [x_t.b, self.scr['rs'].b, lng.b], [h_t.b])
        pT = self.ps[2]
        pTb = pT[:].bitcast(BF16)
        for k in range(8):
            self.tr(pTb[:, k * 128:(k + 1) * 128], h_t[:, k * 128:(k + 1) * 128], self.ident[:],
                    [h_t.b, self.ident.b], [pT.b])
        self.cp('act', hT_t[:].rearrange("p k c -> p (k c)"), pTb[:, 0:1024], [pT.b], [hT_t.b])
        nchunk = (ncols + 511) // 512
        for c in range(nchunk):
            c0 = c * 512
            wd = min(512, ncols - c0)
            pU = self.ps[c % 2]
            for k in range(8):
                self.mm(pU[:, 0:wd], hT_t[:, k, :], win[:, k, c0:c0 + wd], k == 0, k == 7,
                        [hT_t.b, win.b], [pU.b])
            self.cp('act' if c % 2 == 0 else 'dve', u_t[:, c0:c0 + wd], pU[:, 0:wd], [pU.b], [u_t.b])
        return u_t

    def transposes_out(self, srcs, rows, stage, dst_ap, dst_b, pidx):
        pT = self.ps[pidx]
        pTb = pT[:].bitcast(BF16)
        k = len(srcs)
        for i, (ap, R) in enumerate(srcs):
            self.tr(pTb[0:rows, i * 128:(i + 1) * 128], ap, self.ident[:], list(R) + [self.ident.b], [pT.b])
        self.cp('act', stage[0:rows, 0:k, :], pTb[0:rows, 0:k * 128].rearrange("p (k c) -> p k c", k=k),
                [pT.b], [stage.b])
        self.dma('sp', dst_ap, stage[0:rows, 0:k, :], [stage.b], dst_b)

    def p3(self, es, L, x_src, x_dst, wout, mixfn):
        NT = self.NT
        xt = [self.sb(es, f'p3x{i}', [128, D], F32) for i in range(2)]
        mixT = [self.sb(es, f'p3mT{i}', [128, 10, 128], BF16) for i in range(2)]
        xo = [self.sb(es, f'p3o{i}', [128, D], F32) for i in range(2)]
        for n in range(NT):
            sl = n % 2
            mix = mixfn(n)
            xb = x_src.b[n] if len(x_src.b) > 1 else x_src.b[0]
            self.dma('sp', xt[sl][:], x_src[n * 128:(n + 1) * 128, :], [xb], [xt[sl].b])
            for half in range(2):
                pT = self.ps[2 + half]
                pTb = pT[:].bitcast(BF16)
                for k in range(5):
                    kk = half * 5 + k
                    self.tr(pTb[:, k * 128:(k + 1) * 128], mix[:, kk * 128:(kk + 1) * 128], self.ident[:],
                            [mix.b, self.ident.b], [pT.b])
                self.cp('act', mixT[sl][:, half * 5:half * 5 + 5, :].rearrange("p k c -> p (k c)"),
                        pTb[:, 0:640], [pT.b], [mixT[sl].b])
            for c in range(2):
                pU = self.ps[c]
                for k in range(10):
                    self.mm(pU[:, 0:512], mixT[sl][:, k, :], wout[:, k, c * 512:(c + 1) * 512], k == 0, k == 9,
                            [mixT[sl].b, wout.b], [pU.b])
                self.tt('dve', xo[sl][:, c * 512:(c + 1) * 512], pU[:, 0:512], xt[sl][:, c * 512:(c + 1) * 512],
                        ALU.add, [pU.b, xt[sl].b], [xo[sl].b])
            self.dma('sp', x_dst[n * 128:(n + 1) * 128, :], xo[sl][:], [xo[sl].b], [x_dst.b[n]])

    def attn_setup(self, es, dvp_two_banks=False):
        self.st_banks = [(self.ps[i], self.ps[i].b) for i in range(3)]
        self.st_rr = 0
        self.at_tiles = [self.sb(es, f'At{i}', [128, 512], BF16) for i in range(3)]
        self.at_rr = 0
        self.pend = []
        self.npv = 0
        self.epi_id = 0
        self.epi_done = set()

    def mem_attn(self, es, memkT, memV):
        T = self.T
        qT = [self.sb(es, f'mqT{i}', [64, T], BF16) for i in range(2)]
        ost = [self.sb(es, f'most{i}', [128, 4, 64], F32) for i in range(2)]
        rec = self.sb(es, 'mrec', [128, 4], F32)
        blk = 0
        for h in range(4):
            q = qT[h % 2]
            self.dma('sp', q[:], self.mem_qT[h], [self.mem_qT.b], [q.b])
            for cq in range(T // 512):
                set_i = blk % 2
                blk += 1
                accb = self.ps[3 + set_i]
                views = {j: (accb[:, j * 65:(j + 1) * 65], 0) for j in range(4)}
                tiles = [dict(kT=memkT[:, h, kt * 128:(kt + 1) * 128], V=memV[:, kt, h, :],
                              R=[memkT.b, memV.b], c0=0, subs=[0, 1, 2, 3]) for kt in range(2)]
                def epi(accb=accb, o=ost[set_i], cq=cq, h=h):
                    av = accb[:, 0:260].rearrange("p (j c) -> p j c", j=4)
                    self.op_('dve', lambda e, av=av: e.reciprocal(out=rec[:], in_=av[:, :, 64]), reads=[accb.b],
                             writes=[rec.b])
                    self.tt('dve', o[:], av[:, :, 0:64], rec[:].unsqueeze(2).to_broadcast([128, 4, 64]), ALU.mult,
                            [accb.b, rec.b], [o.b])
                    self.dma('sp', self.Y[cq * 512:(cq + 1) * 512, 1024 + h * 64:1024 + (h + 1) * 64]
                             .rearrange("(j p) c -> p j c", p=128), o[:], [o.b], [self.Y.b])
                self.attn_block(q[:, cq * 512:(cq + 1) * 512], [q.b], 512, tiles, 0.125, views, [accb.b], epilogue=epi)
        self.attn_flush()

    def layer_even(self, es, L, x_src, x_dst):
        li = L // 2
        w = self.w
        T, NT = self.T, self.NT
        S = self.S
        memkT, memV = self.mem_kv(es, L)
        with ExitStack() as p1:
            win, lng = self.load_win(p1, L, w['even_w_in'][li], EVEN_COLS)
            wuq = self.sb(p1, 'wuq', [128, 2, 768], BF16)
            for k in range(2):
                self.wload(wuq, k, w['mla_w_uq'][li, k * 128:(k + 1) * 128, :], 0, 768)
            wukv = self.sb(p1, 'wukv', [128, 1, 1024], BF16)
            self.wload(wukv, 0, w['mla_w_ukv'][li], 0, 1024)
            g_ql = self.bc_load(p1, 'g_ql', w['mla_q_lat_g'][li:li + 1, :], 256)
            g_kvl = self.bc_load(p1, 'g_kvl', w['mla_kv_lat_g'][li:li + 1, :], 128)
            g_qn = self.bc_load(p1, 'g_qn', w['mla_q_norm_g'][li:li + 1, 0:64], 64)
            g_qp = self.bc_load(p1, 'g_qp', w['mla_q_norm_g'][li:li + 1, 64:96], 32)
            g_kn = self.bc_load(p1, 'g_kn', w['mla_k_norm_g'][li:li + 1, 0:64], 64)
            g_kp = self.bc_load(p1, 'g_kp', w['mla_k_norm_g'][li:li + 1, 64:96], 32)
            g_bq = self.bc_load(p1, 'g_bq', w['nsa_q_norm_g'][li:li + 1, :], 64)
            g_ks = self.bc_load(p1, 'g_ks', w['nsa_k_norm_g'][li, 1:2, :], 64)
            g_kw = self.bc_load(p1, 'g_kw', w['nsa_k_norm_g'][li, 2:3, :], 64)
            g_mq = self.bc_load(p1, 'g_mq', w['mem_q_norm_g'][L:L + 1, :], 64)
            xt = [self.sb(p1, f'xt{i}', [128, D], F32) for i in range(2)]
            ht = [self.sb(p1, f'ht{i}', [128, D], BF16) for i in range(2)]
            hT = [self.sb(p1, f'hT{i}', [128, 8, 128], BF16) for i in range(2)]
            u = [self.sb(p1, f'u{i}', [128, EVEN_COLS], F32) for i in range(2)]
            ss = self.sb(p1, 'ss', [128, 1], F32)
            junk = self.sb(p1, 'junk', [128, D], BF16)
            latn = self.sb(p1, 'latn', [128, 384], BF16)
            latT = self.sb(p1, 'latT', [128, 3, 128], BF16)
            qsb = self.sb(p1, 'qsb', [128, 768], F32)
            kvsb = self.sb(p1, 'kvsb', [128, 1024], F32)
            qpe = self.sb(p1, 'qpe', [128, 256], F32)
            kpe = self.sb(p1, 'kpe', [128, 32], F32)
            kpeb = self.sb(p1, 'kpeb', [128, 32], BF16)
            qf = self.sb(p1, 'qf', [128, 8, 96], BF16)
            kfm = self.sb(p1, 'kfm', [128, 8, 96], BF16)
            vaug = self.sb(p1, 'vaug', [128, 8, 65], BF16)
            self.memset('pool', vaug[:], 1.0, [vaug.b])
            bqn = self.sb(p1, 'bqn', [128, 512], F32)
            bqf = self.sb(p1, 'bqf', [128, 512], BF16)
            kn2 = self.sb(p1, 'kn2', [128, 128], F32)
            kmisc = self.sb(p1, 'kmisc', [128, 8, 64], BF16)
            nv = self.sb(p1, 'nv', [128, 4, 65], BF16)
            self.memset('pool', nv[:], 1.0, [nv.b])
            mqf = self.sb(p1, 'mqf', [128, 256], BF16)
            gt = self.sb(p1, 'gt', [128, 1280], BF16)
            sg = self.sb(p1, 'sg', [128, 24], F32)
            stq = self.sb(p1, 'stq', [96, 8, 128], BF16)
            stk = self.sb(p1, 'stk', [96, 8, 128], BF16)
            stb = self.sb(p1, 'stb', [64, 8, 128], BF16)
            stm = self.sb(p1, 'stm', [64, 8, 128], BF16)
            stmq = self.sb(p1, 'stmq', [64, 4, 128], BF16)
            scrA = self.new_scr(p1, 'A', 512)
            scrB = self.new_scr(p1, 'B', 512)
            scrC = self.new_scr(p1, 'C', 128)
            for n in range(NT):
                ut = self.p1_front(n, x_src, xt, ht, hT, u, ss, junk, lng, win, EVEN_COLS)
                U = lambda a, b_: ut[:, a:b_]
                ub = [ut.b]
                tsl = slice(n * 128, (n + 1) * 128)
                self.chains_begin(['A', 'B', 'C', 'M', 'G'])
                self.chain('G')
                self.act(gt[:, 0:512], U(416, 928), AF.Silu, ub, [gt.b])
                self.act(gt[:, 512:1024], U(2232, 2744), AF.Silu, ub, [gt.b])
                self.act(gt[:, 1024:1280], U(3000, 3256), AF.Silu, ub, [gt.b])
                self.dma('sp', self.G[tsl, 0:1280], gt[:], [gt.b], [self.G.b[n]])
                self.act(sg[:], U(2208, 2232), AF.Sigmoid, ub, [sg.b])
                self.dma('sp', self.SG[tsl, 0:24], sg[:], [sg.b], [self.SG.b[n]])
                self.chain('A', scrA)
                self.rmsn(U(0, 256).rearrange("p (h c) -> p h c", h=1), 1, 256, g_ql,
                          latn[:, 0:256].rearrange("p (h c) -> p h c", h=1), ub, [latn.b])
                self.rmsn(U(256, 384).rearrange("p (h c) -> p h c", h=1), 1, 128, g_kvl,
                          latn[:, 256:384].rearrange("p (h c) -> p h c", h=1), ub, [latn.b])
                pT = self.ps[3]
                pTb = pT[:].bitcast(BF16)
                for k in range(3):
                    self.tr(pTb[:, k * 128:(k + 1) * 128], latn[:, k * 128:(k + 1) * 128], self.ident[:],
                            [latn.b, self.ident.b], [pT.b])
                self.cp('act', latT[:].rearrange("p k c -> p (k c)"), pTb[:, 0:384], [pT.b], [latT.b])
                pQ, pQ2, pK, pK2 = self.ps[3], self.ps[4], self.ps[5], self.ps[4]
                for k in range(2):
                    self.mm(pQ[:, 0:512], latT[:, k, :], wuq[:, k, 0:512], k == 0, k == 1, [latT.b, wuq.b], [pQ.b])
                for k in range(2):
                    self.mm(pQ2[:, 0:256], latT[:, k, :], wuq[:, k, 512:768], k == 0, k == 1, [latT.b, wuq.b], [pQ2.b])
                self.cp('act', qsb[:, 0:512], pQ[:, 0:512], [pQ.b], [qsb.b])
                self.cp('dve', qsb[:, 512:768], pQ2[:, 0:256], [pQ2.b], [qsb.b])
                self.mm(pK[:, 0:512], latT[:, 2, :], wukv[:, 0, 0:512], True, True, [latT.b, wukv.b], [pK.b])
                self.mm(pK2[:, 0:512], latT[:, 2, :], wukv[:, 0, 512:1024], True, True, [latT.b, wukv.b], [pK2.b])
                self.cp('act', kvsb[:, 0:512], pK[:, 0:512], [pK.b], [kvsb.b])
                self.cp('dve', kvsb[:, 512:1024], pK2[:, 0:512], [pK2.b], [kvsb.b])
                q3 = qsb[:].rearrange("p (h c) -> p h c", h=8)
                kv3 = kvsb[:].rearrange("p (h c) -> p h c", h=8)
                self.rmsn(q3[:, :, 0:64], 8, 64, g_qn, qf[:, :, 0:64], [qsb.b], [qf.b])
                qpe3 = qpe[:].rearrange("p (h c) -> p h c", h=8)
                self.rmsn(q3[:, :, 64:96], 8, 32, g_qp, qpe3, [qsb.b], [qpe.b])
                self.rope(qpe3, 8, 16, n, qf[:, :, 64:96], [qpe.b], [qf.b])
                self.rmsn(kv3[:, :, 0:64], 8, 64, g_kn, kfm[:, :, 0:64], [kvsb.b], [kfm.b])
                self.cp('dve', vaug[:, :, 0:64], kv3[:, :, 64:128], [kvsb.b], [vaug.b])
                kpe3 = kpe[:].rearrange("p (h c) -> p h c", h=1)
                self.rmsn(U(384, 416).rearrange("p (h c) -> p h c", h=1), 1, 32, g_kp, kpe3, ub, [kpe.b])
                self.rope(kpe3, 1, 16, n, kpeb[:].rearrange("p (h c) -> p h c", h=1), [kpe.b], [kpeb.b])
                self.cp('pool', kfm[:, :, 64:96], kpeb[:].unsqueeze(1).to_broadcast([128, 8, 32]), [kpeb.b], [kfm.b])
                self.transposes_out([(qf[:, h, :], [qf.b]) for h in range(8)], 96, stq,
                                    self.mla_qT[:, :, tsl].rearrange("h d t -> d h t"), [self.mla_qT.b], 3)
                self.transposes_out([(kfm[:, h, :], [kfm.b]) for h in range(8)], 96, stk,
                                    self.mla_kT[:, :, tsl].rearrange("h d t -> d h t"), [self.mla_kT.b], 5)
                self.dma('sp', self.mla_v[:, :, n, :].rearrange("h p c -> p h c"), vaug[:], [vaug.b], [self.mla_v.b])
                self.chain('B', scrB)
                bq3 = bqn[:].rearrange("p (h c) -> p h c", h=8)
                self.rmsn(U(928, 1440).rearrange("p (h c) -> p h c", h=8), 8, 64, g_bq, bq3, ub, [bqn.b])
                self.rope(bq3, 8, 32, n, bqf[:].rearrange("p (h c) -> p h c", h=8), [bqn.b], [bqf.b])
                self.chain('C', scrC)
                k23 = kn2[:].rearrange("p (h c) -> p h c", h=2)
                self.rmsn(U(1696, 1824).rearrange("p (h c) -> p h c", h=2), 2, 64, g_ks, k23, ub, [kn2.b])
                self.rope(k23, 2, 32, n, kmisc[:, 0:2, :], [kn2.b], [kmisc.b])
                self.rmsn(U(1952, 2080).rearrange("p (h c) -> p h c", h=2), 2, 64, g_kw, k23, ub, [kn2.b])
                self.rope(k23, 2, 32, n, kmisc[:, 2:4, :], [kn2.b], [kmisc.b])
                self.cp('pool', kmisc[:, 4:8, :], U(1440, 1696).rearrange("p (h c) -> p h c", h=4), ub, [kmisc.b])
                self.cp('dve', nv[:, 0:2, 0:64], U(1824, 1952).rearrange("p (h c) -> p h c", h=2), ub, [nv.b])
                self.cp('dve', nv[:, 2:4, 0:64], U(2080, 2208).rearrange("p (h c) -> p h c", h=2), ub, [nv.b])
                self.chain('B', scrB)
                self.transposes_out([(bqf[:, h * 64:(h + 1) * 64], [bqf.b]) for h in range(8)], 64, stb,
                                    self.nsa_qT[:, n].rearrange("g d r t -> d g r t"), [self.nsa_qT.b], 6)
                self.chain('C', scrC)
                self.transposes_out([(kmisc[:, i, :], [kmisc.b]) for i in range(8)], 64, stm,
                                    self.nsa_kT[:, :, tsl].rearrange("k d t -> d k t"), [self.nsa_kT.b], 7)
                self.dma('sp', self.nsa_v[:, :, n, :].rearrange("k p c -> p k c"), nv[:], [nv.b], [self.nsa_v.b])
                self.chain('B', scrB)
                self.rmsn(U(2744, 3000).rearrange("p (h c) -> p h c", h=4), 4, 64, g_mq,
                          mqf[:].rearrange("p (h c) -> p h c", h=4), ub, [mqf.b])
                self.transposes_out([(mqf[:, h * 64:(h + 1) * 64], [mqf.b]) for h in range(4)], 64, stmq,
                                    self.mem_qT[:, :, tsl].rearrange("h d t -> d h t"), [self.mem_qT.b], 6)
                self.chains_emit()
            S.barrier()
        kcmpT = self.sb(es, 'kcmpT', [64, 2, self.ncmp_pad], BF16)
        vcmp = self.sb(es, 'vcmp', [128, 2, self.ncmp_pad // 128, 129], BF16)
        self.nsa_compress(li, kcmpT, vcmp)
        S.barrier()
        with ExitStack() as pa:
            self.attn_setup(pa)
            self.mla_attn(pa)
            S.barrier()
        with ExitStack() as pa:
            self.attn_setup(pa)
            self.mem_attn(pa, memkT, memV)
            S.barrier()
        with ExitStack() as pa:
            self.attn_setup(pa)
            self.nsa_attn(pa, kcmpT, vcmp)
            S.barrier()
        with ExitStack() as p3:
            wout = self.load_wout(p3, L)
            yt = [self.sb(p3, f'yt{i}', [128, 2304], F32) for i in range(2)]
            gtt = [self.sb(p3, f'gtt{i}', [128, 1280], BF16) for i in range(2)]
            sgt = [self.sb(p3, f'sgt{i}', [128, 24], F32) for i in range(2)]
            mix = [self.sb(p3, f'mix{i}', [128, 1280], BF16) for i in range(2)]
            yb = self.sb(p3, 'yb', [128, 512], F32)
            yb2 = self.sb(p3, 'yb2', [128, 512], F32)

            def mixfn(n):
                sl = n % 2
                y, g, s_, m = yt[sl], gtt[sl], sgt[sl], mix[sl]
                tsl = slice(n * 128, (n + 1) * 128)
                self.dma('sp', y[:], self.Y[tsl, :], [self.Y.b], [y.b])
                self.dma('sp', g[:], self.G[tsl, 0:1280], [self.G.b[n]], [g.b])
                self.dma('sp', s_[:], self.SG[tsl, 0:24], [self.SG.b[n]], [s_.b])
                self.tt('dve', m[:, 0:512], y[:, 0:512], g[:, 0:512], ALU.mult, [y.b, g.b], [m.b])
                self.tt('pool', m[:, 1024:1280], y[:, 1024:1280], g[:, 1024:1280], ALU.mult, [y.b, g.b], [m.b])
                s3 = s_[:].rearrange("p (h c) -> p h c", c=3)
                y3 = lambda a: y[:, a:a + 512].rearrange("p (h c) -> p h c", h=8)
                b3 = yb[:].rearrange("p (h c) -> p h c", h=8)
                b23 = yb2[:].rearrange("p (h c) -> p h c", h=8)
                self.tt('dve', b3, y3(1280), s3[:, :, 0:1].to_broadcast([128, 8, 64]), ALU.mult, [y.b, s_.b], [yb.b])
                self.tt('pool', b23, y3(512), s3[:, :, 1:2].to_broadcast([128, 8, 64]), ALU.mult, [y.b, s_.b], [yb2.b])
                self.tt('dve', b3, b3, b23, ALU.add, [yb.b, yb2.b], [yb.b])
                self.tt('pool', b23, y3(1792), s3[:, :, 2:3].to_broadcast([128, 8, 64]), ALU.mult, [y.b, s_.b], [yb2.b])
                self.tt('dve', b3, b3, b23, ALU.add, [yb.b, yb2.b], [yb.b])
                self.tt('dve', m[:, 512:1024], yb[:], g[:, 512:1024], ALU.mult, [yb.b, g.b], [m.b])
                return m
            self.p3(p3, L, x_src, x_dst, wout, mixfn)
            S.barrier()

    def mla_attn(self, es):
        T, NT = self.T, self.NT
        qT = [self.sb(es, f'aqT{i}', [96, T], BF16) for i in range(2)]
        kT = [self.sb(es, f'akT{i}', [96, T], BF16) for i in range(2)]
        V = [self.sb(es, f'aV{i}', [128, NT, 65], BF16) for i in range(2)]
        ost = [self.sb(es, f'aost{i}', [128, 4, 64], F32) for i in range(2)]
        rec = self.sb(es, 'arec', [128, 4], F32)
        scale = 96 ** -0.5
        blk = 0
        for h in range(8):
            q, k, v = qT[h % 2], kT[h % 2], V[h % 2]
            self.dma('sp', q[:], self.mla_qT[h], [self.mla_qT.b], [q.b])
            self.dma('sp', k[:], self.mla_kT[h], [self.mla_kT.b], [k.b])
            self.dma('sp', v[:], self.mla_v[h], [self.mla_v.b], [v.b])
            for cq in range(T // 512):
                set_i = blk % 2
                blk += 1
                accb = self.ps[3 + set_i]
                views = {j: (accb[:, j * 65:(j + 1) * 65], 0) for j in range(4)}
                tiles = []
                for kt in range(4 * cq + 4):
                    vv = kt - 4 * cq
                    tl = dict(kT=k[:, kt * 128:(kt + 1) * 128], V=v[:, kt, :], R=[k.b, v.b],
                              c0=max(0, vv) * 128, subs=list(range(max(0, vv), 4)))
                    if vv >= 0:
                        tl['masks'] = [(self.tri_neg[:], self.ident[:], vv * 128, 128,
                                        [self.tri_neg.b, self.ident.b])]
                    tiles.append(tl)
                def epi(accb=accb, o=ost[set_i], cq=cq, h=h):
                    av = accb[:, 0:260].rearrange("p (j c) -> p j c", j=4)
                    self.op_('dve', lambda e, av=av: e.reciprocal(out=rec[:], in_=av[:, :, 64]), reads=[accb.b],
                             writes=[rec.b])
                    self.tt('dve', o[:], av[:, :, 0:64], rec[:].unsqueeze(2).to_broadcast([128, 4, 64]), ALU.mult,
                            [accb.b, rec.b], [o.b])
                    self.dma('sp', self.Y[cq * 512:(cq + 1) * 512, h * 64:(h + 1) * 64]
                             .rearrange("(j p) c -> p j c", p=128), o[:], [o.b], [self.Y.b])
                self.attn_block(q[:, cq * 512:(cq + 1) * 512], [q.b], 512, tiles, scale, views, [accb.b], epilogue=epi)
        self.attn_flush()

    def nsa_compress(self, li, kcmpT, vcmp):
        w = self.w
        T = self.T
        n_cmp = (T - 32) // 16 + 1
        ncp = self.ncmp_pad
        nct = ncp // 128
        self.memset('pool', kcmpT[:], 0.0, [kcmpT.b])
        self.memset('pool', vcmp[:], 0.0, [vcmp.b])
        with ExitStack() as es:
            w1 = self.sb(es, 'cw1', [64, 2, 32, 64], BF16)
            w2 = self.sb(es, 'cw2', [64, 2, 64], BF16)
            peT = self.sb(es, 'cpeT', [64, 2, 32], BF16)
            for kv in range(2):
                self.dma('pool', w1[:, kv], w['nsa_cmp_w1'][li, kv].rearrange("(l d) o -> d l o", d=64), [], [w1.b])
                self.dma('pool', w2[:, kv], w['nsa_cmp_w2'][li, kv], [], [w2.b])
                self.dma('pool', peT[:, kv], w['nsa_cmp_posT'][li, kv], [], [peT.b])
            g_kc = self.bc_load(es, 'g_kc', w['nsa_k_norm_g'][li, 0:1, :], 64)
            ovl = self.sb(es, 'ovl', [128, nct, self.n_blk], BF16)
            self.dma('sp', ovl[:], self.c['overlap'].rearrange("(k p) j -> p k j", p=128), [], [ovl.b])
            xT = [self.sb(es, f'cxT{i}', [64, T], BF16) for i in range(2)]
            bias = self.sb(es, 'cbias', [64, 1], F32)
            hid = self.sb(es, 'chid', [64, ncp], BF16)
            self.memset('pool', hid[:], 0.0, [hid.b])
            ctm = self.sb(es, 'ctm', [128, 64], F32)
            ctn = self.sb(es, 'ctn', [128, 64], F32)
            ctb = self.sb(es, 'ctb', [128, 64], BF16)
            rp = self.sb(es, 'crp', [128, 64], F32)
            it = 0
            for kv in range(2):
                for g in range(2):
                    x = xT[it % 2]
                    it += 1
                    self.dma('sp', x[:], self.nsa_kT[4 + kv * 2 + g], [self.nsa_kT.b], [x.b])
                    pH = self.ps[it % 2]
                    for l in range(32):
                        self.mm(pH[0:64, 0:n_cmp], w1[:, kv, l, :], x[:, l:l + 16 * (n_cmp - 1) + 1:16],
                                l == 0, False, [w1.b, x.b], [pH.b])
                        self.mm(pH[0:64, 511:512], w1[:, kv, l, :], peT[:, kv, l:l + 1], False, l == 31,
                                [w1.b, peT.b], [pH.b])
                    self.cp('dve', bias[:], pH[0:64, 511:512], [pH.b], [bias.b])
                    self.act(hid[:, 0:n_cmp], pH[0:64, 0:n_cmp], AF.Silu, [pH.b, bias.b], [hid.b], bias=bias[:, 0:1])
                    for kt in range(nct):
                        pO = self.ps[2 + kt % 2]
                        self.mm(pO[:, 0:64], hid[:, kt * 128:(kt + 1) * 128], w2[:, kv, :], True, True,
                                [hid.b, w2.b], [pO.b])
                        if kv == 1:
                            self.cp('act', vcmp[:, g, kt, 0:64], pO[:, 0:64], [pO.b], [vcmp.b])
                        else:
                            self.cp('act', ctm[:], pO[:, 0:64], [pO.b], [ctm.b])
                            self.rmsn(ctm[:].rearrange("p (h c) -> p h c", h=1), 1, 64, g_kc,
                                      ctn[:].rearrange("p (h c) -> p h c", h=1), [ctm.b], [ctn.b])
                            nrow = min(128, n_cmp - kt * 128)
                            r0 = 31 + 16 * 128 * kt
                            self.memset('dve', rp[:], 0.0, [rp.b])
                            self.dma('sp', rp[0:nrow, :], self.rope_d[r0:r0 + 16 * (nrow - 1) + 1:16, :],
                                     [self.rope_d.b], [rp.b])
                            A, Bm = self.sc_ra, self.sc_rb
                            c2 = ctn[:].rearrange("p (two c) -> p two c", two=2)
                            o2 = ctb[:].rearrange("p (two c) -> p two c", two=2)
                            Av = A[:, 0:64].rearrange("p (two c) -> p two c", two=2)
                            Bv = Bm[:, 0:64].rearrange("p (two c) -> p two c", two=2)
                            self.tt('dve', Av, c2, rp[:, 0:32].unsqueeze(1).to_broadcast([128, 2, 32]), ALU.mult,
                                    [ctn.b, rp.b], [A.b])
                            self.tt('dve', Bv, c2, rp[:, 32:64].unsqueeze(1).to_broadcast([128, 2, 32]), ALU.mult,
                                    [ctn.b, rp.b], [Bm.b])
                            self.tt('dve', o2[:, 0, :], Av[:, 0, :], Bv[:, 1, :], ALU.subtract, [A.b, Bm.b], [ctb.b])
                            self.tt('dve', o2[:, 1, :], Bv[:, 0, :], Av[:, 1, :], ALU.add, [A.b, Bm.b], [ctb.b])
                            pT = self.ps[4]
                            pTb = pT[:].bitcast(BF16)
                            self.tr(pTb[0:64, 0:128], ctb[:], self.ident[:], [ctb.b, self.ident.b], [pT.b])
                            self.cp('act', kcmpT[:, g, kt * 128:(kt + 1) * 128], pTb[0:64, 0:128], [pT.b], [kcmpT.b])
            for g in range(2):
                for kt in range(nct):
                    self.memset('pool', vcmp[:, g, kt, 64:65], 1.0, [vcmp.b])
                    self.cp('pool', vcmp[:, g, kt, 65:65 + self.n_blk], ovl[:, kt, :], [ovl.b], [vcmp.b])
            self.S.barrier()

    def nsa_attn(self, es, kcmpT, vcmp):
        T, NT = self.T, self.NT
        nb = self.n_blk
        nct = self.ncmp_pad // 128
        dvc = 65 + nb
        ksT = self.sb(es, 'ksT', [64, T], BF16)
        kwT = self.sb(es, 'kwT', [64, T], BF16)
        vs = self.sb(es, 'vs', [128, NT, 65], BF16)
        vw = self.sb(es, 'vw', [128, NT, 65], BF16)
        qt_ = [self.sb(es, f'nq{i}', [64, 512], BF16) for i in range(2)]
        cneg = [self.sb(es, f'cneg{i}', [128, self.ncmp_pad], BF16) for i in range(2)]
        forced = [self.sb(es, f'forced{i}', [128, nb], F32) for i in range(2)]
        negm = [self.sb(es, f'negm{i}', [128, T], BF16) for i in range(2)]
        rec = self.sb(es, 'nrec', [128, 4], F32)
        score = self.sb(es, 'nscore', [128, nb], F32)
        work = self.sb(es, 'nwork', [128, nb], F32)
        m8 = self.sb(es, 'nm8', [128, 8], F32)
        thr = self.sb(es, 'nthr', [128, 1], F32)
        nsel = self.sb(es, 'nsel', [128, nb], BF16)
        ost = [self.sb(es, f'nost{i}', [128, 3, 4, 64], F32) for i in range(2)]
        accA, accB, accS, accW = self.ps[3], self.ps[4], self.ps[5], self.ps[6]
        cmp_eid = {}

        def stage1(g, qt, sl):
            q, cn, fo, nm, o = qt_[sl], cneg[sl], forced[sl], negm[sl], ost[sl]
            tsl = slice(qt * 128, (qt + 1) * 128)
            self.dma('sp', q[:], self.nsa_qT[g, qt].rearrange("d r t -> d (r t)"), [self.nsa_qT.b], [q.b])
            self.dma('sp', cn[:], self.c['cmpneg'][tsl, :], [], [cn.b])
            self.dma('sp', fo[:], self.c['forced'][tsl, :], [], [fo.b])
            views = {j: ((accA if j < 2 else accB)[:, (j % 2) * dvc:(j % 2 + 1) * dvc], j // 2) for j in range(4)}
            tiles = []
            for kt in range(nct):
                if 16 * 128 * kt + 31 > qt * 128 + 127:
                    continue
                tiles.append(dict(kT=kcmpT[:, g, kt * 128:(kt + 1) * 128], V=vcmp[:, g, kt, 0:dvc],
                                  R=[kcmpT.b, vcmp.b], c0=0, subs=[0, 1, 2, 3],
                                  masks=[(cn[:, kt * 128:(kt + 1) * 128], self.i4[:], 0, 512, [cn.b, self.i4.b])]))
            if not tiles:
                tiles.append(dict(kT=kcmpT[:, g, 0:128], V=vcmp[:, g, 0, 0:dvc], R=[kcmpT.b, vcmp.b], c0=0,
                                  subs=[0, 1, 2, 3],
                                  masks=[(cn[:, 0:128], self.i4[:], 0, 512, [cn.b, self.i4.b])]))
            cmp_tiles = tiles
            viewsW = {j: (accW[:, j * 65:(j + 1) * 65], 0) for j in range(4)}
            tiles = []
            for kt in range(max(0, qt - 4), qt + 1):
                tl = dict(kT=kwT[:, kt * 128:(kt + 1) * 128], V=vw[:, kt, :], R=[kwT.b, vw.b], c0=0,
                          subs=[0, 1, 2, 3], masks=[])
                if kt == qt:
                    tl['masks'].append((self.tri_neg[:], self.i4[:], 0, 512, [self.tri_neg.b, self.i4.b]))
                if kt == qt - 4:
                    tl['masks'].append((self.edge_neg[:], self.i4[:], 0, 512, [self.edge_neg.b, self.i4.b]))
                tiles.append(tl)
            win_tiles = tiles

            def epi_cmp():
                for j in range(4):
                    ab = accA if j < 2 else accB
                    v_ = views[j][0]
                    self.ts('dve', rec[:, j:j + 1], v_[:, 64:65], 1e-30, ALU.max, [ab.b], [rec.b])
                self.op_('dve', lambda e: e.reciprocal(out=rec[:], in_=rec[:]), reads=[rec.b], writes=[rec.b])
                for j in range(4):
                    ab = accA if j < 2 else accB
                    v_ = views[j][0]
                    self.ts('dve', o[:, 0, j, :], v_[:, 0:64], rec[:, j:j + 1], ALU.mult, [ab.b, rec.b], [o.b])
                    if j == 0:
                        self.ts('dve', score[:], v_[:, 65:65 + nb], rec[:, 0:1], ALU.mult, [ab.b, rec.b], [score.b])
                    else:
                        self.stt(score[:], v_[:, 65:65 + nb], rec[:, j:j + 1], score[:], ALU.mult, ALU.add,
                                 [ab.b, rec.b, score.b], [score.b])
                self.tt('dve', score[:], score[:], fo[:], ALU.add, [score.b, fo.b], [score.b])
                self.op_('dve', lambda e: e.max(out=m8[:], in_=score[:]), reads=[score.b], writes=[m8.b])
                self.op_('dve', lambda e: e.match_replace(out=work[:], in_to_replace=m8[:], in_values=score[:],
                                                           imm_value=-3e38), reads=[score.b, m8.b], writes=[work.b])
                self.op_('dve', lambda e: e.max(out=m8[:], in_=work[:]), reads=[work.b], writes=[m8.b])
                self.ts('dve', thr[:], m8[:, 7:8], -1e29, ALU.max, [m8.b], [thr.b])
                self.ts('dve', nsel[:], score[:], thr[:, 0:1], ALU.is_lt, [score.b, thr.b], [nsel.b], s2=-BIG, op1=ALU.mult)
                nblk_need = (qt + 1) * 2
                self.cp('pool', nm[:, 0:nblk_need * 64].rearrange("p (j c) -> p j c", c=64),
                        nsel[:, 0:nblk_need].unsqueeze(2).to_broadcast([128, nblk_need, 64]), [nsel.b], [nm.b])
                self.tt('pool', nm[:, tsl], nm[:, tsl], self.tri_neg[:], ALU.add, [nm.b, self.tri_neg.b], [nm.b])

            def epi_win():
                av = accW[:, 0:260].rearrange("p (j c) -> p j c", j=4)
                self.op_('dve', lambda e, av=av: e.reciprocal(out=rec[:], in_=av[:, :, 64]), reads=[accW.b], writes=[rec.b])
                self.tt('dve', o[:, 2], av[:, :, 0:64], rec[:].unsqueeze(2).to_broadcast([128, 4, 64]), ALU.mult,
                        [accW.b, rec.b], [o.b])

            eid = self.attn_block(q[:], [q.b], 512, cmp_tiles, 0.125, views, [accA.b, accB.b], epilogue=epi_cmp)
            self.attn_block(q[:], [q.b], 512, win_tiles, 0.125, viewsW, [accW.b], epilogue=epi_win)
            cmp_eid[(g, qt)] = eid

        def stage2(g, qt, sl):
            q, nm, o = qt_[sl], negm[sl], ost[sl]
            tsl = slice(qt * 128, (qt + 1) * 128)
            self.attn_sync(cmp_eid[(g, qt)])
            views = {j: (accS[:, j * 65:(j + 1) * 65], 0) for j in range(4)}
            tiles = [dict(kT=ksT[:, kt * 128:(kt + 1) * 128], V=vs[:, kt, :], R=[ksT.b, vs.b], c0=0,
                          subs=[0, 1, 2, 3],
                          masks=[(nm[:, kt * 128:(kt + 1) * 128], self.i4[:], 0, 512, [nm.b, self.i4.b])])
                     for kt in range(qt + 1)]
            def epi_sel():
                av = accS[:, 0:260].rearrange("p (j c) -> p j c", j=4)
                self.op_('dve', lambda e, av=av: e.reciprocal(out=rec[:], in_=av[:, :, 64]), reads=[accS.b], writes=[rec.b])
                self.tt('dve', o[:, 1], av[:, :, 0:64], rec[:].unsqueeze(2).to_broadcast([128, 4, 64]), ALU.mult,
                        [accS.b, rec.b], [o.b])
                for bi, base in enumerate((1280, 512, 1792)):
                    self.dma('sp', self.Y[tsl, base + g * 256:base + (g + 1) * 256],
                             o[:, bi].rearrange("p r c -> p (r c)"), [o.b], [self.Y.b])
            self.attn_block(q[:], [q.b], 512, tiles, 0.125, views, [accS.b], epilogue=epi_sel)

        for g in range(2):
            self.dma('sp', ksT[:], self.nsa_kT[0 + g], [self.nsa_kT.b], [ksT.b])
            self.dma('sp', kwT[:], self.nsa_kT[2 + g], [self.nsa_kT.b], [kwT.b])
            self.dma('sp', vs[:], self.nsa_v[0 + g], [self.nsa_v.b], [vs.b])
            self.dma('sp', vw[:], self.nsa_v[2 + g], [self.nsa_v.b], [vw.b])
            stage1(g, 0, 0)
            for qt in range(NT):
                if qt + 1 < NT:
                    stage1(g, qt + 1, (qt + 1) % 2)
                stage2(g, qt, qt % 2)
            self.attn_flush()

    def layer_odd(self, es, L, x_src, x_dst):
        li = L // 2
        w = self.w
        T, NT = self.T, self.NT
        S = self.S
        memkT, memV = self.mem_kv(es, L)
        with ExitStack() as p1:
            win, lng = self.load_win(p1, L, w['odd_w_in'][li], ODD_COLS)
            g_cq = self.bc_load(p1, 'g_cq', w['dsa_q_norm_g'][li:li + 1, :], 64)
            g_ck = self.bc_load(p1, 'g_ck', w['dsa_k_norm_g'][li:li + 1, :], 64)
            g_mq = self.bc_load(p1, 'g_mq', w['mem_q_norm_g'][L:L + 1, :], 64)
            xt = [self.sb(p1, f'xt{i}', [128, D], F32) for i in range(2)]
            ht = [self.sb(p1, f'ht{i}', [128, D], BF16) for i in range(2)]
            hT = [self.sb(p1, f'hT{i}', [128, 8, 128], BF16) for i in range(2)]
            u = [self.sb(p1, f'u{i}', [128, ODD_COLS], F32) for i in range(2)]
            ss = self.sb(p1, 'ss', [128, 1], F32)
            junk = self.sb(p1, 'junk', [128, D], BF16)
            gt = self.sb(p1, 'gt', [128, 1792], BF16)
            cqn = self.sb(p1, 'cqn', [128, 512], F32)
            cqf = self.sb(p1, 'cqf', [128, 512], BF16)
            ckn = self.sb(p1, 'ckn', [128, 64], F32)
            ckf = self.sb(p1, 'ckf', [128, 64], BF16)
            cva = self.sb(p1, 'cva', [128, 65], BF16)
            self.memset('pool', cva[:], 1.0, [cva.b])
            iqf = self.sb(p1, 'iqf', [128, 256], BF16)
            ikf = self.sb(p1, 'ikf', [128, 32], BF16)
            iwt = self.sb(p1, 'iwt', [128, 8], F32)
            mlv = self.sb(p1, 'mlv', [128, 4, 129], BF16)
            self.memset('pool', mlv[:], 1.0, [mlv.b])
            mqf = self.sb(p1, 'mqf', [128, 256], BF16)
            stq = self.sb(p1, 'stq', [64, 8, 128], BF16)
            stk = self.sb(p1, 'stk', [64, 1, 128], BF16)
            sti = self.sb(p1, 'sti', [32, 8, 128], BF16)
            stik = self.sb(p1, 'stik', [32, 1, 128], BF16)
            stmq = self.sb(p1, 'stmq', [64, 4, 128], BF16)
            strw = self.sb(p1, 'strw', [128, 4, 128], F32)
            stif = self.sb(p1, 'stif', [8, 128], F32)
            scrA = self.new_scr(p1, 'A', 512)
            scrB = self.new_scr(p1, 'B', 256)
            for n in range(NT):
                ut = self.p1_front(n, x_src, xt, ht, hT, u, ss, junk, lng, win, ODD_COLS)
                U = lambda a, b_: ut[:, a:b_]
                ub = [ut.b]
                tsl = slice(n * 128, (n + 1) * 128)
                self.chains_begin(['A', 'B', 'L', 'M', 'G'])
                self.chain('G')
                self.act(gt[:, 0:512], U(936, 1448), AF.Silu, ub, [gt.b])
                self.act(gt[:, 512:1024], U(2992, 3504), AF.Silu, ub, [gt.b])
                self.act(gt[:, 1024:1280], U(3760, 4016), AF.Silu, ub, [gt.b])
                self.act(gt[:, 1280:1792], U(2480, 2992), AF.Sigmoid, ub, [gt.b])
                self.dma('sp', self.G[tsl, :], gt[:], [gt.b], [self.G.b[n]])
                self.chain('A', scrA)
                cq3 = cqn[:].rearrange("p (h c) -> p h c", h=8)
                self.rmsn(U(0, 512).rearrange("p (h c) -> p h c", h=8), 8, 64, g_cq, cq3, ub, [cqn.b])
                self.rope(cq3, 8, 32, n, cqf[:].rearrange("p (h c) -> p h c", h=8), [cqn.b], [cqf.b])
                ck3 = ckn[:].rearrange("p (h c) -> p h c", h=1)
                self.rmsn(U(512, 576).rearrange("p (h c) -> p h c", h=1), 1, 64, g_ck, ck3, ub, [ckn.b])
                self.rope(ck3, 1, 32, n, ckf[:].rearrange("p (h c) -> p h c", h=1), [ckn.b], [ckf.b])
                self.cp('dve', cva[:, 0:64], U(576, 640), ub, [cva.b])
                self.dma('sp', self.dsa_v[:, n, :], cva[:], [cva.b], [self.dsa_v.b])
                pT = self.ps[4]
                pTb = pT[:].bitcast(BF16)
                for h in range(8):
                    self.tr(pTb[0:64, h * 128:(h + 1) * 128], cqf[:, h * 64:(h + 1) * 64], self.ident[:],
                            [cqf.b, self.ident.b], [pT.b])
                self.cp('act', stq[:], pTb[0:64, 0:1024].rearrange("p (k c) -> p k c", k=8), [pT.b], [stq.b])
                self.dma('sp', self.dsa_qT[n], stq[:], [stq.b], [self.dsa_qT.b])
                self.transposes_out([(ckf[:], [ckf.b])], 64, stk,
                                    self.dsa_kT[:, tsl].rearrange("d (k t) -> d k t", k=1), [self.dsa_kT.b], 5)
                self.chain('B', scrB)
                self.rope(U(640, 896).rearrange("p (h c) -> p h c", h=8), 8, 16, n,
                          iqf[:].rearrange("p (h c) -> p h c", h=8), ub, [iqf.b])
                self.rope(U(896, 928).rearrange("p (h c) -> p h c", h=1), 1, 16, n,
                          ikf[:].rearrange("p (h c) -> p h c", h=1), ub, [ikf.b])
                self.ts('dve', iwt[:], U(928, 936), 8 ** -0.5, ALU.mult, ub, [iwt.b])
                self.dma('sp', self.idx_w[tsl, :], iwt[:], [iwt.b], [self.idx_w.b])
                self.transposes_out([(iqf[:, h * 32:(h + 1) * 32], [iqf.b]) for h in range(8)], 32, sti,
                                    self.idx_qT[:, :, tsl].rearrange("h d t -> d h t"), [self.idx_qT.b], 6)
                self.transposes_out([(ikf[:], [ikf.b])], 32, stik,
                                    self.idx_kT[:, tsl].rearrange("d (k t) -> d k t", k=1), [self.idx_kT.b], 7)
                self.chain('L')
                pR = self.ps[3]
                for k in range(4):
                    self.tr(pR[:, k * 128:(k + 1) * 128], U(1448 + k * 128, 1448 + (k + 1) * 128), self.identf[:],
                            ub + [self.identf.b], [pR.b])
                self.cp('act', strw[:].rearrange("p k c -> p (k c)"), pR[:, 0:512], [pR.b], [strw.b])
                self.dma('sp', self.ml_raw[:, tsl].rearrange("(k p) t -> p k t", p=128), strw[:], [strw.b],
                         [self.ml_raw.b])
                pI = self.ps[3]
                self.tr(pI[0:8, 0:128], U(2472, 2480), self.identf[:], ub + [self.identf.b], [pI.b])
                self.cp('act', stif[:], pI[0:8, 0:128], [pI.b], [stif.b])
                self.dma('sp', self.ml_if[:, tsl], stif[:], [stif.b], [self.ml_if.b])
                self.cp('dve', mlv[:, :, 0:128], U(1960, 2472).rearrange("p (h c) -> p h c", h=4), ub, [mlv.b])
                self.dma('sp', self.ml_v[:, :, n, :].rearrange("h p c -> p h c"), mlv[:], [mlv.b], [self.ml_v.b])
                self.chain('B', scrB)
                self.rmsn(U(3504, 3760).rearrange("p (h c) -> p h c", h=4), 4, 64, g_mq,
                          mqf[:].rearrange("p (h c) -> p h c", h=4), ub, [mqf.b])
                self.transposes_out([(mqf[:, h * 64:(h + 1) * 64], [mqf.b]) for h in range(4)], 64, stmq,
                                    self.mem_qT[:, :, tsl].rearrange("h d t -> d h t"), [self.mem_qT.b], 6)
                self.chains_emit()
            S.barrier()
        if 'stop_p1' in self.dbg:
            return
        self.mlstm_pre(li)
        S.barrier()
        if 'stop_pre' in self.dbg:
            return
        with ExitStack() as pa:
            self.attn_setup(pa)
            self.mlstm_attn(pa)
            S.barrier()
        if 'stop_ml' in self.dbg:
            return
        with ExitStack() as pa:
            self.attn_setup(pa)
            self.mem_attn(pa, memkT, memV)
            S.barrier()
        if 'stop_mem' in self.dbg:
            return
        with ExitStack() as pa:
            self.attn_setup(pa)
            self.dsa_attn(pa)
            S.barrier()
        if 'stop_dsa' in self.dbg:
            return
        with ExitStack() as p3:
            wout = self.load_wout(p3, L)
            g_h = self.bc_load(p3, 'g_h', w['mlstm_h_norm_g'][li:li + 1, :], 128)
            yt = [self.sb(p3, f'yt{i}', [128, 1280], F32) for i in range(2)]
            gtt = [self.sb(p3, f'gtt{i}', [128, 1792], BF16) for i in range(2)]
            mix = [self.sb(p3, f'mix{i}', [128, 1280], BF16) for i in range(2)]
            hn = self.sb(p3, 'hn', [128, 512], F32)

            def mixfn(n):
                sl = n % 2
                y, g, m = yt[sl], gtt[sl], mix[sl]
                tsl = slice(n * 128, (n + 1) * 128)
                self.dma('sp', y[:], self.Y[tsl, 0:1280], [self.Y.b], [y.b])
                self.dma('sp', g[:], self.G[tsl, :], [self.G.b[n]], [g.b])
                self.tt('dve', m[:, 0:512], y[:, 0:512], g[:, 0:512], ALU.mult, [y.b, g.b], [m.b])
                self.tt('pool', m[:, 1024:1280], y[:, 1024:1280], g[:, 1024:1280], ALU.mult, [y.b, g.b], [m.b])
                self.rmsn(y[:, 512:1024].rearrange("p (h c) -> p h c", h=4), 4, 128, g_h,
                          hn[:].rearrange("p (h c) -> p h c", h=4), [y.b], [hn.b])
                self.tt('dve', hn[:], hn[:], g[:, 1280:1792], ALU.mult, [hn.b, g.b], [hn.b])
                self.tt('dve', m[:, 512:1024], hn[:], g[:, 512:1024], ALU.mult, [hn.b, g.b], [m.b])
                return m
            self.p3(p3, L, x_src, x_dst, wout, mixfn)
            S.barrier()

    def mlstm_pre(self, li):
        w = self.w
        T = self.T
        with ExitStack() as es:
            xp = [self.sb(es, f'xp{i}', [128, T + 3], F32) for i in range(2)]
            y = self.sb(es, 'cy', [128, T], F32)
            yo = [self.sb(es, f'cyo{i}', [128, T], BF16) for i in range(2)]
            wc = self.sb(es, 'cwc', [128, 4, 4], F32)
            bc = self.sb(es, 'cbc', [128, 4], F32)
            for ck in range(4):
                self.dma('sp', wc[:, ck, :], w['mlstm_conv_wT'][li, ck * 128:(ck + 1) * 128, :], [], [wc.b])
                self.dma('sp', bc[:, ck:ck + 1], w['mlstm_conv_b'][li, ck * 128:(ck + 1) * 128].unsqueeze(1), [], [bc.b])
            for ck in range(4):
                x = xp[ck % 2]
                o = yo[ck % 2]
                self.memset('pool', x[:, 0:3], 0.0, [x.b])
                self.dma('sp', x[:, 3:T + 3], self.ml_raw[ck * 128:(ck + 1) * 128, :], [self.ml_raw.b], [x.b])
                self.ts('dve', y[:], x[:, 0:T], wc[:, ck, 0:1], ALU.mult, [x.b, wc.b, bc.b], [y.b],
                        s2=bc[:, ck:ck + 1], op1=ALU.add)
                for j in range(1, 4):
                    self.stt(y[:], x[:, j:j + T], wc[:, ck, j:j + 1], y[:], ALU.mult, ALU.add, [x.b, wc.b, y.b], [y.b])
                self.act(o[:], y[:], AF.Silu, [y.b], [o.b])
                self.dma('sp', self.ml_qkT[ck * 128:(ck + 1) * 128, :], o[:], [o.b], [self.ml_qkT.b])
            self.S.barrier()
        with ExitStack() as es:
            ig = self.sb(es, 'ig', [4, T], F32)
            fg = self.sb(es, 'fg', [4, T], F32)
            cs = self.sb(es, 'cs', [4, T], F32)
            a = self.sb(es, 'ga', [4, T], F32)
            Mt = self.sb(es, 'gM', [4, T], F32)
            ones = self.sb(es, 'gones', [4, T], F32)
            ib = self.sb(es, 'gib', [4, 1], F32)
            fb = self.sb(es, 'gfb', [4, 1], F32)
            self.memset('pool', ones[:], 1.0, [ones.b])
            self.dma('sp', ig[:], self.ml_if[0:4, :], [self.ml_if.b], [ig.b])
            self.dma('sp', fg[:], self.ml_if[4:8, :], [self.ml_if.b], [fg.b])
            self.dma('sp', ib[:], w['mlstm_i_bias'][li].unsqueeze(1), [], [ib.b])
            self.dma('sp', fb[:], w['mlstm_f_bias'][li].unsqueeze(1), [], [fb.b])
            self.ts('dve', fb[:], fb[:], -1.0, ALU.mult, [fb.b], [fb.b])
            self.act(fg[:], fg[:], AF.Exp, [fg.b, fb.b], [fg.b], bias=fb[:, 0:1], scale=-1.0)
            self.act(fg[:], fg[:], AF.Ln, [fg.b], [fg.b], bias=self.one_c[0:4, 0:1], scale=1.0)
            self.op_('dve', lambda e: e.tensor_tensor_scan(out=cs[:], data0=ones[:], data1=fg[:], initial=0.0,
                                                             op0=ALU.mult, op1=ALU.add),
                      reads=[ones.b, fg.b], writes=[cs.b])
            self.stt(a[:], ig[:], ib[:, 0:1], cs[:], ALU.add, ALU.add, [ig.b, ib.b, cs.b], [a.b])
            self.op_('dve', lambda e: e.tensor_tensor_scan(out=Mt[:], data0=a[:], data1=a[:], initial=0.0,
                                                             op0=ALU.max, op1=ALU.max),
                      reads=[a.b], writes=[Mt.b])
            self.tt('dve', cs[:], cs[:], Mt[:], ALU.subtract, [cs.b, Mt.b], [cs.b])
            self.act(cs[:], cs[:], AF.Exp, [cs.b], [cs.b])
            self.ts('dve', Mt[:], Mt[:], -1.0, ALU.mult, [Mt.b], [Mt.b])
            self.dma('sp', self.ml_g[0:4, :], a[:], [a.b], [self.ml_g.b])
            self.dma('sp', self.ml_g[4:8, :], Mt[:], [Mt.b], [self.ml_g.b])
            self.dma('sp', self.ml_g[8:12, :], cs[:], [cs.b], [self.ml_g.b])
            self.S.barrier()

    def mlstm_attn(self, es):
        T, NT = self.T, self.NT
        qT = [self.sb(es, f'lqT{i}', [64, T], BF16) for i in range(2)]
        kT = [self.sb(es, f'lkT{i}', [64, T], BF16) for i in range(2)]
        V = [self.sb(es, f'lV{i}', [128, NT, 129], BF16) for i in range(2)]
        nM = [self.sb(es, f'lnM{i}', [128, T], F32) for i in range(2)]
        ant = self.sb(es, 'lant', [NT, 2, 128], F32)
        atm = [self.sb(es, f'latm{i}', [128, 2, NT], F32) for i in range(2)]
        Et = [self.sb(es, f'lEt{i}', [128, 512], F32) for i in range(3)]
        ost = [self.sb(es, f'lost{i}', [128, 4, 128], F32) for i in range(2)]
        d2 = self.sb(es, 'ld2', [128, 4], F32)
        ecnt = [0]
        blk = 0
        LN8 = math.log(0.125)
        for h in range(4):
            q, k, v, nm, at = qT[h % 2], kT[h % 2], V[h % 2], nM[h % 2], atm[h % 2]
            self.dma('sp', q[:], self.ml_qkT[h * 64:(h + 1) * 64, :], [self.ml_qkT.b], [q.b])
            self.dma('sp', k[:], self.ml_qkT[256 + h * 64:256 + (h + 1) * 64, :], [self.ml_qkT.b], [k.b])
            self.dma('sp', v[:], self.ml_v[h], [self.ml_v.b], [v.b])
            self.dma('sp', nm[:], self.ml_g[4 + h:5 + h, :].to_broadcast([128, T]), [self.ml_g.b], [nm.b])
            self.dma('sp', ant[:, 0, :], self.ml_g[h, :].rearrange("(n p) -> n p", p=128), [self.ml_g.b], [ant.b])
            self.dma('sp', ant[:, 1, :], self.ml_g[8 + h, :].rearrange("(n p) -> n p", p=128), [self.ml_g.b], [ant.b])
            pA = self.ps[7]
            for i in range(2):
                self.tr(pA[:, i * NT:(i + 1) * NT], ant[:, i, :], self.identf[0:NT, 0:NT], [ant.b, self.identf.b], [pA.b])
            self.cp('act', at[:].rearrange("p a n -> p (a n)"), pA[:, 0:2 * NT], [pA.b], [at.b])
            self.ts('dve', at[:, 0, :], at[:, 0, :], LN8, ALU.add, [at.b], [at.b])
            for cq in range(T // 512):
                set_i = blk % 2
                blk += 1
                accA, accB = self.ps[3 + 2 * set_i], self.ps[4 + 2 * set_i]
                views = {j: ((accA if j < 2 else accB)[:, (j % 2) * 129:(j % 2 + 1) * 129], j // 2) for j in range(4)}
                tiles = []
                for kt in range(4 * cq + 4):
                    vv = kt - 4 * cq
                    c0 = max(0, vv) * 128

                    def efn(kt=kt, c0=c0, cq=cq, nm=nm, at=at):
                        E = Et[ecnt[0] % 3]
                        ecnt[0] += 1
                        self.act(E[:, c0:512], nm[:, cq * 512 + c0:(cq + 1) * 512], AF.Exp, [nm.b, at.b], [E.b],
                                 bias=at[:, 0, kt:kt + 1], scale=1.0)
                        return E, [E.b]
                    tl = dict(kT=k[:, kt * 128:(kt + 1) * 128], V=v[:, kt, :], R=[k.b, v.b], c0=c0,
                              subs=list(range(max(0, vv), 4)), Efn=efn)
                    if vv >= 0:
                        tl['diag'] = vv
                    tiles.append(tl)
                def epi(accA=accA, accB=accB, views=views, o=ost[set_i], cq=cq, h=h, at=at):
                    for j in range(4):
                        ab = accA if j < 2 else accB
                        v_ = views[j][0]
                        self.act(d2[:, j:j + 1], v_[:, 128:129], AF.Abs, [ab.b], [d2.b])
                    self.tt('dve', d2[:], d2[:], at[:, 1, 4 * cq:4 * cq + 4], ALU.max, [d2.b, at.b], [d2.b])
                    self.op_('dve', lambda e: e.reciprocal(out=d2[:], in_=d2[:]), reads=[d2.b], writes=[d2.b])
                    for j in range(4):
                        ab = accA if j < 2 else accB
                        v_ = views[j][0]
                        self.ts('dve', o[:, j, :], v_[:, 0:128], d2[:, j:j + 1], ALU.mult, [ab.b, d2.b], [o.b])
                    self.dma('sp', self.Y[cq * 512:(cq + 1) * 512, 512 + h * 128:512 + (h + 1) * 128]
                             .rearrange("(j p) c -> p j c", p=128), o[:], [o.b], [self.Y.b])
                self.attn_block(q[:, cq * 512:(cq + 1) * 512], [q.b], 512, tiles, 1.0, views, [accA.b, accB.b],
                                mode='mul', epilogue=epi)
        self.attn_flush()

    def dsa_attn(self, es):
        T, NT = self.T, self.NT
        KSEL = min(256, T // 4)
        ikT = self.sb(es, 'ikT', [32, T], BF16)
        ckT = self.sb(es, 'ckT', [64, T], BF16)
        cv = self.sb(es, 'cv', [128, NT, 65], BF16)
        self.dma('sp', ikT[:], self.idx_kT[:, :], [self.idx_kT.b], [ikT.b])
        self.dma('sp', ckT[:], self.dsa_kT[:, :], [self.dsa_kT.b], [ckT.b])
        self.dma('sp', cv[:], self.dsa_v[:, :, :], [self.dsa_v.b], [cv.b])
        iq = [self.sb(es, f'iq{i}', [32, 8, 128], BF16) for i in range(2)]
        iw = [self.sb(es, f'iw{i}', [128, 8], F32) for i in range(2)]
        cq_ = [self.sb(es, f'cq{i}', [64, 2, 512], BF16) for i in range(2)]
        score2 = [self.sb(es, f'dscore{i}', [128, T], F32) for i in range(3)]
        thrA2 = [self.sb(es, f'dthrA{i}', [128, 1], F32) for i in range(2)]
        work = self.sb(es, 'dwork', [128, T], F32)
        negm = [self.sb(es, f'dnegm{i}', [128, T], BF16) for i in range(2)]
        rl = [self.sb(es, f'drl{i}', [128, 512], F32) for i in range(3)]
        m8 = self.sb(es, 'dm8', [128, 8], F32)
        thr = self.sb(es, 'dthr', [128, 1], F32)
        rec = self.sb(es, 'drec', [128, 4], F32)
        ost = [self.sb(es, f'dost{i}', [128, 4, 64], F32) for i in range(2)]
        rlc_ = [0]
        blk_ = [0]
        NBIS = 34
        junk = self.sb(es, 'djunk', [128, T], BF16)
        amax = self.sb(es, 'damax', [128, 1], F32)
        w0 = self.sb(es, 'dw0', [128, 1], F32)
        nHh = self.sb(es, 'dnHh', [128, 40], F32)
        nmid = [self.sb(es, f'dnmid{i}', [128, 1], F32) for i in range(2)]
        Ssum = self.sb(es, 'dS', [128, 1], F32)
        tsg = self.sb(es, 'dtsg', [128, 1], F32)

        def stage_a1(qt):
            rlc = rlc_[0]
            sl = qt % 2
            tsl = slice(qt * 128, (qt + 1) * 128)
            q_i, w_i = iq[sl], iw[sl]
            score = score2[qt % 3]
            self.dma('sp', q_i[:], self.idx_qT[:, :, tsl].rearrange("h d t -> d h t"), [self.idx_qT.b], [q_i.b])
            self.dma('sp', w_i[:], self.idx_w[tsl, :], [self.idx_w.b], [w_i.b])
            ncols = (qt + 1) * 128
            for c in range((ncols + 511) // 512):
                c0 = c * 512
                wd = min(512, ncols - c0)
                for h in range(8):
                    bi = self.st_rr % len(self.st_banks)
                    self.st_rr += 1
                    bank, bb = self.st_banks[bi]
                    self.mm(bank[:, 0:wd], q_i[:, h, :], ikT[:, c0:c0 + wd], True, True, [q_i.b, ikT.b], [bb])
                    r = rl[rlc % 3]
                    rlc += 1
                    self.act(r[:, 0:wd], bank[:, 0:wd], AF.Relu, [bb], [r.b])
                    if h == 0:
                        self.ts('dve', score[:, c0:c0 + wd], r[:, 0:wd], w_i[:, 0:1], ALU.mult, [r.b, w_i.b], [score.b])
                    else:
                        self.stt(score[:, c0:c0 + wd], r[:, 0:wd], w_i[:, h:h + 1], score[:, c0:c0 + wd],
                                 ALU.mult, ALU.add, [r.b, w_i.b, score.b], [score.b])
            rlc_[0] = rlc

        def is_act_tile(qt):
            return ((qt + 1) * 128 > KSEL) and (qt % 4 != 0) and ('dsa_nobis' not in self.dbg)

        def stage_a2_finish(qt):
            sl = qt % 2
            nm = negm[sl]
            score = score2[qt % 3]
            ncols = (qt + 1) * 128
            th = thrA2[qt % 2] if is_act_tile(qt) else thr
            self.ts('dve', nm[:, 0:ncols], score[:, 0:ncols], th[:, 0:1], ALU.is_lt, [score.b, th.b], [nm.b],
                    s2=-BIG, op1=ALU.mult)

        def stage_a2(qt):
            sl = qt % 2
            tsl = slice(qt * 128, (qt + 1) * 128)
            q_c, nm = cq_[sl], negm[sl]
            score = score2[qt % 3]
            ncols = (qt + 1) * 128
            for half in range(2):
                self.dma('sp', q_c[:, half, :], self.dsa_qT[qt, :, half * 4:(half + 1) * 4, :].rearrange("d h t -> d (h t)"),
                         [self.dsa_qT.b], [q_c.b])
            use_act = is_act_tile(qt)
            if use_act:
                self.op_('dve', lambda e, ncols=ncols: e.tensor_reduce(out=amax[:], in_=score[:, 0:ncols], axis=AX.X,
                                                                      op=ALU.max, apply_absolute_value=True),
                         reads=[score.b], writes=[amax.b])
                self.ts('dve', w0[:], amax[:], 2.0, ALU.mult, [amax.b], [w0.b], s2=2.0, op1=ALU.add)
                self.ts('dve', nHh[:], self.pw[:], w0[:, 0:1], ALU.mult, [self.pw.b, w0.b], [nHh.b])
            self.tt('dve', score[:, tsl], score[:, tsl], self.trinegf[:], ALU.add, [score.b, self.trinegf.b], [score.b])
            if use_act:
                cconst = float(0.5 - (2 * KSEL - ncols - 1))
                self.memset('pool', nmid[0][:], 0.0, [nmid[0].b])
                for j in range(NBIS):
                    cur, nxt = nmid[j % 2], nmid[(j + 1) % 2]
                    self.act(junk[:, 0:ncols], score[:, 0:ncols], AF.Sign, [score.b, cur.b], [junk.b, Ssum.b],
                             bias=cur[:, 0:1], scale=1.0, accum=Ssum[:, 0:1])
                    self.act(tsg[:], Ssum[:], AF.Sign, [Ssum.b], [tsg.b], bias=cconst, scale=1.0)
                    self.act(nxt[:], tsg[:], AF.Identity, [tsg.b, cur.b, nHh.b], [nxt.b],
                             bias=cur[:, 0:1], scale=nHh[:, j:j + 1])
                fin = nmid[NBIS % 2]
                thrA = thrA2[qt % 2]
                self.act(thrA[:], fin[:], AF.Identity, [fin.b, nHh.b], [thrA.b], bias=nHh[:, NBIS - 1:NBIS], scale=-1.0)
                return
            elif ncols > KSEL and 'dsa_notopk' not in self.dbg:
                self.cp('pool', work[:, 0:ncols], score[:, 0:ncols], [score.b], [work.b])
                nr = KSEL // 8
                for r_ in range(nr):
                    self.op_('dve', lambda e, ncols=ncols: e.max(out=m8[:], in_=work[:, 0:ncols]),
                             reads=[work.b], writes=[m8.b])
                    if r_ < nr - 1:
                        self.op_('dve', lambda e, ncols=ncols: e.match_replace(
                            out=work[:, 0:ncols], in_to_replace=m8[:], in_values=work[:, 0:ncols], imm_value=-3e38),
                            reads=[work.b, m8.b], writes=[work.b])
                self.ts('dve', thr[:], m8[:, 7:8], -1e29, ALU.max, [m8.b], [thr.b])
            else:
                self.memset('dve', thr[:], -1e29, [thr.b])
            stage_a2_finish(qt)

        def stage_b(qt):
            blk = blk_[0]
            sl = qt % 2
            tsl = slice(qt * 128, (qt + 1) * 128)
            q_c, nm = cq_[sl], negm[sl]
            for half in range(2):
                set_i = blk % 2
                blk += 1
                accb = self.ps[3 + set_i]
                views = {j: (accb[:, j * 65:(j + 1) * 65], 0) for j in range(4)}
                tiles = [dict(kT=ckT[:, kt * 128:(kt + 1) * 128], V=cv[:, kt, :], R=[ckT.b, cv.b], c0=0,
                              subs=[0, 1, 2, 3],
                              masks=[(nm[:, kt * 128:(kt + 1) * 128], self.i4[:], 0, 512, [nm.b, self.i4.b])])
                         for kt in range(qt + 1)]
                def epi(accb=accb, o=ost[set_i], tsl=tsl, half=half):
                    av = accb[:, 0:260].rearrange("p (j c) -> p j c", j=4)
                    self.op_('dve', lambda e, av=av: e.reciprocal(out=rec[:], in_=av[:, :, 64]), reads=[accb.b], writes=[rec.b])
                    self.tt('dve', o[:], av[:, :, 0:64], rec[:].unsqueeze(2).to_broadcast([128, 4, 64]), ALU.mult,
                            [accb.b, rec.b], [o.b])
                    self.dma('sp', self.Y[tsl, half * 256:(half + 1) * 256], o[:].rearrange("p j c -> p (j c)"),
                             [o.b], [self.Y.b])
                self.attn_block(q_c[:, half, :], [q_c.b], 512, tiles, 0.125, views, [accb.b], epilogue=epi)
            blk_[0] = blk

        stage_a1(0)
        if NT > 1:
            stage_a1(1)
        stage_a2(0)
        for qt in range(NT):
            if qt + 2 < NT:
                stage_a1(qt + 2)
            if qt + 1 < NT:
                stage_a2(qt + 1)
            if is_act_tile(qt):
                stage_a2_finish(qt)
            stage_b(qt)
        self.attn_flush()


def host_consts(T):
    bf = ml_dtypes.bfloat16
    nb = T // 64
    ncp = max(1, T // 2048) * 128
    n_cmp = (T - 32) // 16 + 1
    c = {}
    c['c_ident'] = np.eye(128, dtype=np.float32).astype(bf)
    c['c_identf'] = np.eye(128, dtype=np.float32)
    c['c_i4'] = np.tile(np.eye(128, dtype=np.float32), (1, 4)).astype(bf)
    i8 = np.zeros((128, 512), np.float32)
    for p in range(128):
        for h in range(8):
            i8[p, h * 64 + p % 64] = 1.0
    c['c_i8x2'] = i8.astype(bf)
    t = np.arange(128)[:, None]
    s = np.arange(128)[None, :]
    c['c_tri_neg'] = np.where(s > t, -BIG, 0.0).astype(np.float32).astype(bf)
    c['c_edge_neg'] = np.where(s <= t, -BIG, 0.0).astype(np.float32).astype(bf)
    c['c_tri01T'] = np.where(t <= s, 1.0, 0.0).astype(np.float32).astype(bf)
    c['c_trinegf'] = np.where(s > t, -1e30, 0.0).astype(np.float32)
    inv = (10000.0 ** (-np.arange(32, dtype=np.float32) / 32)).astype(np.float32)
    c['c_invf'] = np.tile(inv[None, :], (128, 1)).astype(np.float32)
    c['c_pw'] = np.tile((-(2.0 ** -(np.arange(40, dtype=np.float64) + 2)))[None, :], (128, 1)).astype(np.float32)
    tt = np.arange(T)[:, None]
    n = np.arange(ncp)[None, :]
    cm = np.where((16 * n + 31 <= tt) & (n < n_cmp), 0.0, -BIG)
    c['c_cmpneg'] = cm.astype(np.float32).astype(bf)
    j = np.arange(nb)[None, :]
    cur = tt // 64
    forced = np.where(j == cur, 3e4, np.where(j == cur - 1, 2e4, np.where(j == 0, 1e4, 0.0)))
    forced = np.where(j * 64 <= tt, forced, -1e30)
    c['c_forced'] = forced.astype(np.float32)
    ni = np.arange(ncp)[:, None]
    ov = ((16 * ni <= j * 64 + 63) & (16 * ni + 31 >= j * 64) & (ni < n_cmp))
    c['c_overlap'] = ov.astype(np.float32).astype(bf)
    return c


_CACHE = {}


def make_in_maps(inputs, T, ncores):
    NT = T // 128
    consts = host_consts(T)
    wnames = ["ln_g", "mem_norm_g", "mem_w_kv", "mem_q_norm_g", "mem_k_norm_g", "w_out", "even_w_in",
              "mla_q_lat_g", "mla_kv_lat_g", "mla_w_uq", "mla_w_ukv", "mla_q_norm_g", "mla_k_norm_g",
              "nsa_q_norm_g", "nsa_k_norm_g", "nsa_cmp_w1", "nsa_cmp_w2", "odd_w_in", "dsa_q_norm_g",
              "dsa_k_norm_g", "mlstm_conv_b", "mlstm_i_bias", "mlstm_f_bias", "mlstm_h_norm_g"]
    shared = {k: np.ascontiguousarray(np.asarray(inputs[k], dtype=np.float32)) for k in wnames}
    shared["nsa_cmp_posT"] = np.ascontiguousarray(np.transpose(np.asarray(inputs["nsa_cmp_pos"], np.float32), (0, 1, 3, 2)))
    shared["mlstm_conv_wT"] = np.ascontiguousarray(np.transpose(np.asarray(inputs["mlstm_conv_w"], np.float32), (0, 2, 1)))
    shared.update(consts)
    maps = []
    for c in range(ncores):
        m = dict(shared)
        m["x"] = np.ascontiguousarray(np.asarray(inputs["x"][c, :T], np.float32))
        m["mem"] = np.ascontiguousarray(np.asarray(inputs["mem"][c], np.float32))
        pos = np.asarray(inputs["positions"][c, :T]).astype(np.int32)
        m["pos_t"] = np.ascontiguousarray(pos.reshape(NT, 128).T)
        maps.append(m)
    return maps


def kernel(**inputs):
    T = 4096
    key = ('full', T)
    if key not in _CACHE:
        _CACHE[key] = Builder(T, [0, 1, 2, 3]).build()
    nc = _CACHE[key]
    maps = make_in_maps(inputs, T, 8)
    res = run_bass_kernel_spmd(nc, maps, core_ids=list(range(8)))
    out = np.stack([np.asarray(r["out"], dtype=np.float32) for r in res.results], axis=0)
    return out
```

```python
import math
import numpy as np
import ml_dtypes
from contextlib import ExitStack
import concourse.bass as bass
import concourse.mybir as mybir
from concourse.bass_utils import run_bass_kernel_spmd

F32 = mybir.dt.float32
BF16 = mybir.dt.bfloat16
I32 = mybir.dt.int32
AF = mybir.ActivationFunctionType
ALU = mybir.AluOpType
AX = mybir.AxisListType

D = 1024
BIG = 30000.0
EPS = 1e-6
EVEN_COLS = 3256
ODD_COLS = 4016
ENGS = ('pe', 'act', 'dve', 'pool', 'sp')
EPOCH = 16000
NDQ = 8


class Buf:
    __slots__ = ('name', 'w', 'r')

    def __init__(self, name=''):
        self.name = name
        self.w = None
        self.r = {}


class Sched:
    def __init__(self, nc, es):
        self.nc = nc
        self.es = es
        self.prog = {e: [] for e in ENGS}
        self.esem = {e: [] for e in ENGS}
        self.cnt = {e: 0 for e in ENGS}
        self.seen = {e: {} for e in ENGS}
        self.dq = ('sp', 'pool', 'act')
        self.dsem = {q: [es.enter_context(nc.semaphore(f'D{q}{i}')) for i in range(NDQ)] for q in self.dq}
        self.dcnt = {q: 0 for q in self.dq}
        self.ninst = 0

    def _semobj(self, key):
        if key[0] == 'E':
            return self.esem[key[1]][key[2]]
        return self.dsem[key[1]][key[2]]

    def op(self, e, fn, reads=(), writes=(), dma=False):
        deps = {}
        for b in reads:
            if b.w is not None:
                k, v = b.w
                if deps.get(k, 0) < v:
                    deps[k] = v
        for b in writes:
            if b.w is not None:
                k, v = b.w
                if deps.get(k, 0) < v:
                    deps[k] = v
            for k, v in b.r.items():
                if deps.get(k, 0) < v:
                    deps[k] = v
        waits = []
        seen = self.seen[e]
        for k, v in deps.items():
            if e == 'pe' and k[0] == 'E' and k[1] == 'pe':
                continue
            if seen.get(k, 0) >= v:
                continue
            seen[k] = v
            waits.append((self._semobj(k), v))
        if dma:
            j = self.dcnt[e]
            self.dcnt[e] += 1
            slot = j % NDQ
            val = 16 * (j // NDQ + 1)
            key = ('D', e, slot)
            if val > 16 and seen.get(key, 0) < val - 16:
                seen[key] = val - 16
                waits.append((self.dsem[e][slot], val - 16))
            sem = self.dsem[e][slot]
            inc = 16
        else:
            c = self.cnt[e]
            ep = c // EPOCH
            if ep >= len(self.esem[e]):
                self.esem[e].append(self.es.enter_context(self.nc.semaphore(f'S{e}{ep}')))
            self.cnt[e] += 1
            key = ('E', e, ep)
            val = c % EPOCH + 1
            sem = self.esem[e][ep]
            inc = 1
        ev = (key, val)
        self.ninst += 1

        def thunk(eng, waits=waits, fn=fn, sem=sem, inc=inc):
            for s, v in waits:
                eng.wait_ge(s, v)
            fn(eng).then_inc(sem, inc)
        self.prog[e].append(thunk)
        for b in reads:
            if b.r.get(key, 0) < val:
                b.r[key] = val
        for b in writes:
            b.w = ev
            b.r = {}
        return ev

    def barrier(self):
        evs = []
        for e in ENGS:
            c = self.cnt[e]
            if c > 0:
                ep = (c - 1) // EPOCH
                evs.append((('E', e, ep), (c - 1) % EPOCH + 1))
        for q in self.dq:
            n = self.dcnt[q]
            for slot in range(min(n, NDQ)):
                cntslot = (n - 1 - slot) // NDQ + 1
                evs.append((('D', q, slot), 16 * cntslot))
        for e in ENGS:
            waits = []
            for k, v in evs:
                if k[0] == 'E' and k[1] == e:
                    continue
                if self.seen[e].get(k, 0) >= v:
                    continue
                self.seen[e][k] = v
                waits.append((self._semobj(k), v))

            def thunk(eng, waits=waits):
                for s, v in waits:
                    eng.wait_ge(s, v)
            self.prog[e].append(thunk)

    def emit(self):
        nc = self.nc
        with nc.Block() as block:
            @block.tensor
            def _(eng):
                for t in self.prog['pe']:
                    t(eng)

            @block.scalar
            def _(eng):
                for t in self.prog['act']:
                    t(eng)

            @block.vector
            def _(eng):
                for t in self.prog['dve']:
                    t(eng)

            @block.gpsimd
            def _(eng):
                for t in self.prog['pool']:
                    t(eng)

            @block.sync
            def _(eng):
                for t in self.prog['sp']:
                    t(eng)


class Tl:
    __slots__ = ('t', 'b')

    def __init__(self, t, name):
        self.t = t
        self.b = Buf(name)

    def __getitem__(self, k):
        return self.t[k]


class Builder:
    def __init__(self, T, layers, dbg=()):
        self.T = T
        self.NT = T // 128
        self.layers = list(layers)
        self.dbg = set(dbg)
        self.nc = bass.Bass("TRN2", target_bir_lowering=False)
        self.uid = 0
        self.rec = None
        self.dbg_out = {}

    def din(self, name, shape, dt=F32):
        return self.nc.dram_tensor(name, list(shape), dt, kind="ExternalInput").ap()

    def dscr(self, name, shape, dt, nbuf=1):
        kind = "ExternalOutput" if name in self.dbg else "Internal"
        t = self.nc.dram_tensor(name, list(shape), dt, kind=kind).ap()
        tl = Tl(t, name)
        if nbuf > 1:
            tl.b = [Buf(f'{name}{i}') for i in range(nbuf)]
        return tl

    def sb(self, es, name, shape, dt):
        self.uid += 1
        t = es.enter_context(self.nc.sbuf_tensor(f'{name}_{self.uid}', list(shape), dt))
        return Tl(t, name)

    def op_(self, e, fn, reads=(), writes=(), dma=False):
        if self.rec is not None:
            self.rec.append((e, fn, tuple(reads), tuple(writes), dma))
        else:
            self.S.op(e, fn, reads=reads, writes=writes, dma=dma)

    def chains_begin(self, names):
        self._chains = {k: [] for k in names}

    def chain(self, name, scr=None):
        self.rec = self._chains[name]
        self.scr = scr if scr is not None else self.scr0

    def chains_emit(self):
        self.rec = None
        self.scr = self.scr0
        lists = [l for l in self._chains.values() if l]
        idx = [0] * len(lists)
        left = sum(len(l) for l in lists)
        while left:
            for i, l in enumerate(lists):
                if idx[i] < len(l):
                    e, fn, R, W, dma = l[idx[i]]
                    idx[i] += 1
                    left -= 1
                    self.S.op(e, fn, reads=R, writes=W, dma=dma)

    def new_scr(self, es, tag, w):
        sc = {}
        for k in ('sq', 'tmp', 'ra', 'rb'):
            sc[k] = self.sb(es, f'sc_{k}_{tag}', [128, w], F32)
        for k in ('ssq', 'ln', 'rs'):
            sc[k] = self.sb(es, f'sc_{k}_{tag}', [128, 16], F32)
        return sc

    def mm(self, out, lhsT, rhs, start, stop, R, W):
        self.op_('pe', lambda e: e.matmul(out, lhsT=lhsT, rhs=rhs, start=start, stop=stop,
                                           skip_group_check=True), reads=R, writes=W)

    def tr(self, out, in_, ident, R, W):
        self.op_('pe', lambda e: e.transpose(out=out, in_=in_, identity=ident), reads=R, writes=W)

    def act(self, out, in_, func, R, W, bias=None, scale=None, accum=None):
        kw = {}
        if bias is not None:
            kw['bias'] = bias
        if scale is not None:
            kw['scale'] = scale
        if accum is not None:
            kw['accum_out'] = accum
        self.op_('act', lambda e: e.activation(out=out, in_=in_, func=func, **kw), reads=R, writes=W)

    def tt(self, eng, out, in0, in1, op, R, W):
        self.op_(eng, lambda e: e.tensor_tensor(out=out, in0=in0, in1=in1, op=op), reads=R, writes=W)

    def ts(self, eng, out, in0, s1, op0, R, W, s2=None, op1=None, accum=None):
        kw = {}
        if op1 is not None:
            kw['op1'] = op1
        if accum is not None:
            kw['accum_out'] = accum
        self.op_(eng, lambda e: e.tensor_scalar(out=out, in0=in0, scalar1=s1, scalar2=s2, op0=op0, **kw),
                  reads=R, writes=W)

    def stt(self, out, in0, scalar, in1, op0, op1, R, W):
        self.op_('dve', lambda e: e.scalar_tensor_tensor(out=out, in0=in0, scalar=scalar, in1=in1,
                                                         op0=op0, op1=op1), reads=R, writes=W)

    def cp(self, eng, out, in_, R, W):
        if eng == 'act':
            self.op_('act', lambda e: e.copy(out=out, in_=in_), reads=R, writes=W)
        else:
            self.op_(eng, lambda e: e.tensor_copy(out=out, in_=in_), reads=R, writes=W)

    def red(self, out, in_, op, R, W):
        self.op_('dve', lambda e: e.tensor_reduce(out=out, in_=in_, axis=AX.X, op=op), reads=R, writes=W)

    def memset(self, eng, ap, val, W):
        self.op_(eng, lambda e: e.memset(ap, val), writes=W)

    def dma(self, q, out, in_, R, W, **kw):
        self.op_(q, lambda e: e.dma_start(out=out, in_=in_, **kw), reads=R, writes=W, dma=True)

    def bc_load(self, es, name, row_ap, d):
        t = self.sb(es, name, [128, d], F32)
        self.dma('sp', t[:], row_ap.to_broadcast([128, d]), [], [t.b])
        return t

    def wload(self, w, k, src, c0, c1):
        c = c0
        while c < c1:
            ce = min(c1, c + 2048)
            self.dma('pool', w[:, k, c:ce], src[:, c:ce], [], [w.b])
            c = ce

    def rstd(self, ssq, H, d, R):
        ln, rs = self.scr['ln'], self.scr['rs']
        self.act(ln[:, 0:H], ssq, AF.Ln, R, [ln.b], bias=self.eps_c[:, 0:1], scale=1.0 / d)
        self.act(rs[:, 0:H], ln[:, 0:H], AF.Exp, [ln.b], [rs.b], scale=-0.5)
        return rs[:, 0:H]

    def rmsn(self, src, H, d, g, dst, R, W):
        sq, ssq, tmp = self.scr['sq'], self.scr['ssq'], self.scr['tmp']
        sqv = sq[:, 0:H * d].rearrange("p (h c) -> p h c", h=H)
        self.tt('pool', sqv, src, src, ALU.mult, R, [sq.b])
        self.red(ssq[:, 0:H], sqv, ALU.add, [sq.b], [ssq.b])
        rs = self.rstd(ssq[:, 0:H], H, d, [ssq.b])
        tv = tmp[:, 0:H * d].rearrange("p (h c) -> p h c", h=H)
        self.tt('dve', tv, src, rs.unsqueeze(2).to_broadcast([128, H, d]), ALU.mult,
                list(R) + [self.scr['rs'].b], [tmp.b])
        self.tt('pool', dst, tv, g[:].unsqueeze(1).to_broadcast([128, H, d]), ALU.mult,
                [tmp.b, g.b], W)

    def rope(self, src, H, d2, n, dst, R, W):
        tab = self.rope32 if d2 == 32 else self.rope16
        cosv = tab[:, n, 0:d2]
        sinv = tab[:, n, d2:2 * d2]
        A, Bm = self.scr['ra'], self.scr['rb']
        s4 = src.rearrange("p h (two c) -> p h two c", two=2)
        d4 = dst.rearrange("p h (two c) -> p h two c", two=2)
        Av = A[:, 0:H * 2 * d2].rearrange("p (h two c) -> p h two c", h=H, two=2)
        Bv = Bm[:, 0:H * 2 * d2].rearrange("p (h two c) -> p h two c", h=H, two=2)
        cb = cosv.unsqueeze(1).unsqueeze(1).to_broadcast([128, H, 2, d2])
        sbv = sinv.unsqueeze(1).unsqueeze(1).to_broadcast([128, H, 2, d2])
        self.tt('dve', Av, s4, cb, ALU.mult, list(R) + [tab.b], [A.b])
        self.tt('pool', Bv, s4, sbv, ALU.mult, list(R) + [tab.b], [Bm.b])
        self.tt('dve', d4[:, :, 0, :], Av[:, :, 0, :], Bv[:, :, 1, :], ALU.subtract, [A.b, Bm.b], W)
        self.tt('pool', d4[:, :, 1, :], Bv[:, :, 0, :], Av[:, :, 1, :], ALU.add, [A.b, Bm.b], W)

    def attn_block(self, rhs_q, q_R, N, tiles, scale, acc_views, acc_bufs, mode='exp', LA=2, epilogue=None):
        started = set()
        nt = len(tiles)
        last_use = {}
        for i, tl in enumerate(tiles):
            for j in tl['subs']:
                last_use[j] = i
        Atiles = {}

        def emit_s(i):
            tl = tiles[i]
            bi = self.st_rr % len(self.st_banks)
            self.st_rr += 1
            bank, bb = self.st_banks[bi]
            c0 = tl['c0']
            masks = tl.get('masks', [])
            if mode != 'exp':
                E, eR = tl['Efn']()
            self.mm(bank[:, c0:N], tl['kT'], rhs_q[:, c0:N], True, len(masks) == 0, list(q_R) + tl['R'], [bb])
            for mi, (ml, mr, off, ncols, mR) in enumerate(masks):
                self.mm(bank[:, off:off + ncols], ml, mr, False, mi == len(masks) - 1, mR, [bb])
            ai = self.at_rr % len(self.at_tiles)
            self.at_rr += 1
            A = self.at_tiles[ai]
            Atiles[i] = A
            if mode == 'exp':
                self.act(A[:, c0:N], bank[:, c0:N], AF.Exp, [bb], [A.b], scale=scale)
            else:
                self.tt('dve', A[:, c0:N], bank[:, c0:N], E[:, c0:N], ALU.mult, [bb] + eR, [A.b])
                if tl.get('diag') is not None:
                    dj = tl['diag']
                    self.tt('pool', A[:, dj * 128:(dj + 1) * 128], A[:, dj * 128:(dj + 1) * 128],
                            self.tri01T[:], ALU.mult, [A.b, self.tri01T.b], [A.b])

        def emit_pv(i):
            tl = tiles[i]
            A = Atiles.pop(i)
            for j in tl['subs']:
                view, bk = acc_views[j]
                st = bk not in started
                started.add(bk)
                self.mm(view, A[:, j * 128:(j + 1) * 128], tl['V'], st, last_use[j] == i,
                        [A.b] + tl['R'], [acc_bufs[bk]])

        for step in range(nt):
            emit_s(step)
            self.pend.append(('pv', (lambda i=step: emit_pv(i))))
            self.npv += 1
            self._drain(LA)
        if epilogue is None:
            self.attn_flush()
            return None
        self.epi_id += 1
        eid = self.epi_id
        self.pend.append(('epi', epilogue, eid))
        self._drain(LA)
        return eid

    def _drain(self, limit):
        q = self.pend
        while q and (q[0][0] == 'epi' or self.npv > limit):
            it = q.pop(0)
            if it[0] == 'pv':
                self.npv -= 1
                it[1]()
            else:
                it[1]()
                self.epi_done.add(it[2])

    def attn_flush(self):
        self._drain(-1)

    def attn_sync(self, eid):
        while eid is not None and eid not in self.epi_done:
            q = self.pend
            it = q.pop(0)
            if it[0] == 'pv':
                self.npv -= 1
                it[1]()
            else:
                it[1]()
                self.epi_done.add(it[2])

    def build(self):
        nc = self.nc
        T, NT = self.T, self.NT
        with ExitStack() as es:
            self.S = S = Sched(nc, es)
            self._decl_inputs()
            self._decl_scratch()
            self.ps = []
            for i in range(8):
                t = es.enter_context(nc.psum_tensor(f"psb{i}", [128, 512], F32))
                self.ps.append(Tl(t, f'ps{i}'))
            self._consts(es)
            self._prep(es)
            S.barrier()
            xin = Tl(self.x_in, 'xin')
            xin.b = [Buf('xin')] * 1
            cur = xin
            for idx, L in enumerate(self.layers):
                last = idx == len(self.layers) - 1
                dst = self.out_t if last else self.xs[idx % 2]
                with ExitStack() as les:
                    if L % 2 == 0:
                        self.layer_even(les, L, cur, dst)
                    else:
                        self.layer_odd(les, L, cur, dst)
                S.barrier()
                cur = dst
            S.barrier()
            S.emit()
        return nc

    def _decl_inputs(self):
        T, NT = self.T, self.NT
        d = self.din
        self.x_in = d("x", [T, D])
        self.mem = d("mem", [256, D])
        self.pos_t = d("pos_t", [128, NT], I32)
        self.w = {}
        spec = dict(
            ln_g=[4, D], mem_norm_g=[4, D], mem_w_kv=[4, D, 512], mem_q_norm_g=[4, 64], mem_k_norm_g=[4, 64],
            w_out=[4, 1280, D], even_w_in=[2, D, EVEN_COLS], mla_q_lat_g=[2, 256], mla_kv_lat_g=[2, 128],
            mla_w_uq=[2, 256, 768], mla_w_ukv=[2, 128, 1024], mla_q_norm_g=[2, 96], mla_k_norm_g=[2, 96],
            nsa_q_norm_g=[2, 64], nsa_k_norm_g=[2, 3, 64], nsa_cmp_posT=[2, 2, 64, 32],
            nsa_cmp_w1=[2, 2, 2048, 64], nsa_cmp_w2=[2, 2, 64, 64], odd_w_in=[2, D, ODD_COLS],
            dsa_q_norm_g=[2, 64], dsa_k_norm_g=[2, 64], mlstm_conv_wT=[2, 512, 4], mlstm_conv_b=[2, 512],
            mlstm_i_bias=[2, 4], mlstm_f_bias=[2, 4], mlstm_h_norm_g=[2, 128])
        for k, shp in spec.items():
            self.w[k] = d(k, shp)
        ncp = self.ncmp_pad = max(1, T // 2048) * 128
        nb = self.n_blk = T // 64
        self.c = dict(
            ident=d("c_ident", [128, 128], BF16), identf=d("c_identf", [128, 128], F32),
            i4=d("c_i4", [128, 512], BF16), i8x2=d("c_i8x2", [128, 512], BF16),
            tri_neg=d("c_tri_neg", [128, 128], BF16), edge_neg=d("c_edge_neg", [128, 128], BF16),
            tri01T=d("c_tri01T", [128, 128], BF16), trinegf=d("c_trinegf", [128, 128], F32),
            invf=d("c_invf", [128, 32]), pw=d("c_pw", [128, 40]), cmpneg=d("c_cmpneg", [T, ncp], BF16),
            forced=d("c_forced", [T, nb]), overlap=d("c_overlap", [ncp, nb], BF16))

    def _decl_scratch(self):
        T, NT = self.T, self.NT
        s = self.dscr
        self.out_t = Tl(self.nc.dram_tensor("out", [T, D], F32, kind="ExternalOutput").ap(), 'out')
        self.out_t.b = [Buf(f'out{i}') for i in range(NT)]
        self.xs = [s("xs0", [T, D], F32, NT), s("xs1", [T, D], F32, NT)]
        self.rope_d = s("rope_d", [T, 64], F32)
        self.Y = s("Y", [T, 2304], F32)
        self.G = s("G", [T, 1792], BF16, NT)
        self.SG = s("SG", [T, 32], F32, NT)
        self.mla_qT = s("mla_qT", [8, 96, T], BF16)
        self.mla_kT = s("mla_kT", [8, 96, T], BF16)
        self.mla_v = s("mla_v", [8, 128, NT, 65], BF16)
        self.nsa_qT = s("nsa_qT", [2, NT, 64, 4, 128], BF16)
        self.nsa_kT = s("nsa_kT", [8, 64, T], BF16)
        self.nsa_v = s("nsa_v", [4, 128, NT, 65], BF16)
        self.mem_qT = s("mem_qT", [4, 64, T], BF16)
        self.dsa_qT = s("dsa_qT", [NT, 64, 8, 128], BF16)
        self.dsa_kT = s("dsa_kT", [64, T], BF16)
        self.dsa_v = s("dsa_v", [128, NT, 65], BF16)
        self.idx_qT = s("idx_qT", [8, 32, T], BF16)
        self.idx_kT = s("idx_kT", [32, T], BF16)
        self.idx_w = s("idx_w", [T, 8], F32)
        self.ml_raw = s("ml_raw", [512, T], F32)
        self.ml_if = s("ml_if", [8, T], F32)
        self.ml_qkT = s("ml_qkT", [512, T], BF16)
        self.ml_v = s("ml_v", [4, 128, NT, 129], BF16)
        self.ml_g = s("ml_g", [12, T], F32)

    def _consts(self, es):
        c = self.c
        def ld(name, shape, dt):
            t = self.sb(es, name, shape, dt)
            self.dma('sp', t[:], c[name], [], [t.b])
            return t
        self.ident = ld('ident', [128, 128], BF16)
        self.identf = ld('identf', [128, 128], F32)
        self.i4 = ld('i4', [128, 512], BF16)
        self.i8x2 = ld('i8x2', [128, 512], BF16)
        self.tri_neg = ld('tri_neg', [128, 128], BF16)
        self.edge_neg = ld('edge_neg', [128, 128], BF16)
        self.tri01T = ld('tri01T', [128, 128], BF16)
        self.trinegf = ld('trinegf', [128, 128], F32)
        self.invf = ld('invf', [128, 32], F32)
        self.pw = ld('pw', [128, 40], F32)
        self.eps_c = self.sb(es, 'eps_c', [128, 1], F32)
        self.memset('dve', self.eps_c[:], EPS, [self.eps_c.b])
        self.one_c = self.sb(es, 'one_c', [128, 1], F32)
        self.memset('dve', self.one_c[:], 1.0, [self.one_c.b])
        self.rope32 = self.sb(es, 'rope32', [128, self.NT, 64], F32)
        self.rope16 = self.sb(es, 'rope16', [128, self.NT, 32], F32)
        self.sc_sq = self.sb(es, 'sc_sq', [128, 512], F32)
        self.sc_tmp = self.sb(es, 'sc_tmp', [128, 512], F32)
        self.sc_ra = self.sb(es, 'sc_ra', [128, 512], F32)
        self.sc_rb = self.sb(es, 'sc_rb', [128, 512], F32)
        self.sc_ssq = self.sb(es, 'sc_ssq', [128, 16], F32)
        self.sc_ln = self.sb(es, 'sc_ln', [128, 16], F32)
        self.sc_rs = self.sb(es, 'sc_rs', [128, 16], F32)
        self.scr0 = dict(sq=self.sc_sq, tmp=self.sc_tmp, ra=self.sc_ra, rb=self.sc_rb, ssq=self.sc_ssq,
                         ln=self.sc_ln, rs=self.sc_rs)
        self.scr = self.scr0

    def _prep(self, es0):
        NT = self.NT
        PI = math.pi
        with ExitStack() as es:
            pi_t = self.sb(es, 'pos_i', [128, NT], I32)
            self.dma('sp', pi_t[:], self.pos_t, [], [pi_t.b])
            pf = self.sb(es, 'pos_f', [128, NT], F32)
            self.cp('dve', pf[:], pi_t[:], [pi_t.b], [pf.b])
            ang = self.sb(es, 'ang', [128, NT, 32], F32)
            self.tt('dve', ang[:], pf[:].unsqueeze(2).to_broadcast([128, NT, 32]),
                    self.invf[:].unsqueeze(1).to_broadcast([128, NT, 32]), ALU.mult,
                    [pf.b, self.invf.b], [ang.b])
            kf = self.sb(es, 'kf', [128, NT, 32], F32)
            ki = self.sb(es, 'ki', [128, NT, 32], I32)
            r = self.sb(es, 'r', [128, NT, 32], F32)
            m = self.sb(es, 'm', [128, NT, 32], F32)

            def wrap(buf):
                self.ts('dve', m[:], buf[:], PI, ALU.is_gt, [buf.b], [m.b], s2=-2 * PI, op1=ALU.mult)
                self.tt('dve', buf[:], buf[:], m[:], ALU.add, [buf.b, m.b], [buf.b])
                self.ts('dve', m[:], buf[:], -PI, ALU.is_lt, [buf.b], [m.b], s2=2 * PI, op1=ALU.mult)
                self.tt('dve', buf[:], buf[:], m[:], ALU.add, [buf.b, m.b], [buf.b])
                self.ts('dve', buf[:], buf[:], PI, ALU.min, [buf.b], [buf.b], s2=-PI, op1=ALU.max)
            self.ts('dve', kf[:], ang[:], 1.0 / (2 * PI), ALU.mult, [ang.b], [kf.b])
            self.cp('dve', ki[:], kf[:], [kf.b], [ki.b])
            self.cp('dve', kf[:], ki[:], [ki.b], [kf.b])
            C1 = 6.28125
            C2 = 2 * PI - C1
            self.stt(r[:], kf[:], -C1, ang[:], ALU.mult, ALU.add, [kf.b, ang.b], [r.b])
            self.stt(r[:], kf[:], -C2, r[:], ALU.mult, ALU.add, [kf.b, r.b], [r.b])
            wrap(r)
            self.act(self.rope32[:, :, 32:64], r[:], AF.Sin, [r.b], [self.rope32.b])
            self.ts('dve', r[:], r[:], PI / 2, ALU.add, [r.b], [r.b])
            wrap(r)
            self.act(self.rope32[:, :, 0:32], r[:], AF.Sin, [r.b], [self.rope32.b])
            self.cp('dve', self.rope16[:, :, 0:16], self.rope32[:, :, 0:32:2], [self.rope32.b], [self.rope16.b])
            self.cp('dve', self.rope16[:, :, 16:32], self.rope32[:, :, 32:64:2], [self.rope32.b], [self.rope16.b])
            self.dma('sp', self.rope_d[:].rearrange("(n p) c -> p n c", p=128), self.rope32[:],
                     [self.rope32.b], [self.rope_d.b])
            self.S.barrier()

    def load_win(self, es, L, w_in, ncols):
        win = self.sb(es, 'win', [128, 8, ncols], BF16)
        for k in range(8):
            self.wload(win, k, w_in[k * 128:(k + 1) * 128, :], 0, ncols)
        lng = self.bc_load(es, 'lng', self.w['ln_g'][L:L + 1, :], D)
        return win, lng

    def load_wout(self, es, L):
        wout = self.sb(es, 'wout', [128, 10, D], BF16)
        for k in range(10):
            self.wload(wout, k, self.w['w_out'][L, k * 128:(k + 1) * 128, :], 0, D)
        return wout

    def mem_kv(self, es, L):
        w = self.w
        kT = self.sb(es, 'memkT', [64, 4, 256], BF16)
        V = self.sb(es, 'memV', [128, 2, 4, 65], BF16)
        self.memset('pool', V[:], 1.0, [V.b])
        with ExitStack() as s2:
            wkv = self.sb(s2, 'wkv', [128, 8, 512], BF16)
            for k in range(8):
                self.wload(wkv, k, w['mem_w_kv'][L, k * 128:(k + 1) * 128, :], 0, 512)
            mg = self.bc_load(s2, 'mg', w['mem_norm_g'][L:L + 1, :], D)
            kg = self.bc_load(s2, 'kg', w['mem_k_norm_g'][L:L + 1, :], 64)
            mt = self.sb(s2, 'mt', [128, D], F32)
            junk = self.sb(s2, 'junk', [128, D], BF16)
            mh = self.sb(s2, 'mh', [128, D], BF16)
            mhT = self.sb(s2, 'mhT', [128, 8, 128], BF16)
            kv = self.sb(s2, 'kv', [128, 512], F32)
            kn = self.sb(s2, 'kn', [128, 256], BF16)
            ss = self.sb(s2, 'ss', [128, 1], F32)
            for i in range(2):
                self.dma('sp', mt[:], self.mem[i * 128:(i + 1) * 128, :], [], [mt.b])
                self.act(junk[:], mt[:], AF.Square, [mt.b], [junk.b, ss.b], accum=ss[:])
                rs = self.rstd(ss[:, 0:1], 1, D, [ss.b])
                self.stt(mh[:], mt[:], rs, mg[:], ALU.mult, ALU.mult, [mt.b, self.scr['rs'].b, mg.b], [mh.b])
                pT = self.ps[2]
                pTb = pT[:].bitcast(BF16)
                for k in range(8):
                    self.tr(pTb[:, k * 128:(k + 1) * 128], mh[:, k * 128:(k + 1) * 128], self.ident[:],
                            [mh.b, self.ident.b], [pT.b])
                self.cp('act', mhT[:].rearrange("p k c -> p (k c)"), pTb[:, 0:1024], [pT.b], [mhT.b])
                pU = self.ps[0]
                for k in range(8):
                    self.mm(pU[:, 0:512], mhT[:, k, :], wkv[:, k, :], k == 0, k == 7, [mhT.b, wkv.b], [pU.b])
                self.cp('act', kv[:], pU[:, 0:512], [pU.b], [kv.b])
                self.rmsn(kv[:, 0:256].rearrange("p (h c) -> p h c", h=4), 4, 64, kg,
                          kn[:].rearrange("p (h c) -> p h c", h=4), [kv.b], [kn.b])
                self.cp('dve', V[:, i, :, 0:64], kv[:, 256:512].rearrange("p (h c) -> p h c", h=4), [kv.b], [V.b])
                pK = self.ps[3]
                pKb = pK[:].bitcast(BF16)
                for h in range(4):
                    self.tr(pKb[0:64, h * 128:(h + 1) * 128], kn[:, h * 64:(h + 1) * 64], self.ident[:],
                            [kn.b, self.ident.b], [pK.b])
                self.cp('act', kT[:, :, i * 128:(i + 1) * 128],
                        pKb[0:64, 0:512].rearrange("p (h c) -> p h c", h=4), [pK.b], [kT.b])
            self.S.barrier()
        return kT, V

    def p1_front(self, n, x_src, xt, ht, hT, u, ss, junk, lng, win, ncols):
        sl = n % 2
        x_t, h_t, hT_t, u_t = xt[sl], ht[sl], hT[sl], u[sl]
        xb = x_src.b[n] if len(x_src.b) > 1 else x_src.b[0]
        self.dma('sp', x_t[:], x_src[n * 128:(n + 1) * 128, :], [xb], [x_t.b])
        self.act(junk[:], x_t[:], AF.Square, [x_t.b], [junk.b, ss.b], accum=ss[:])
        rs = self.rstd(ss[:, 0:1], 1, D, [ss.b])
        self.stt(h_t[:], x_t[:], rs, lng[:], ALU.mult, ALU.mult, [x_t.b, self.scr['rs'].b, lng.b], [h_t.b])
        pT = self.ps[2]
        pTb = pT[:].bitcast(BF16)
        for k in range(8):
            self.tr(pTb[:, k * 128:(k + 1) * 128], h_t[:, k * 128:(k + 1) * 128], self.ident[:],
                    [h_t.b, self.ident.b], [pT.b])
        self.cp('act', hT_t[:].rearrange("p k c -> p (k c)"), pTb[:, 0:1024], [pT.b], [hT_t.b])
        nchunk = (ncols + 511) // 512
        for c in range(nchunk):
            c0 = c * 512
            wd = min(512, ncols - c0)
            pU = self.ps[c % 2]
            for k in range(8):
                self.mm(pU[:, 0:wd], hT_t[:, k, :], win[:, k, c0:c0 + wd], k == 0, k == 7,
                        [hT_t.b, win.b], [pU.b])
            self.cp('act' if c % 2 == 0 else 'dve', u_t[:, c0:c0 + wd], pU[:, 0:wd], [pU.b], [u_t.b])
        return u_t

    def transposes_out(self, srcs, rows, stage, dst_ap, dst_b, pidx):
        pT = self.ps[pidx]
        pTb = pT[:].bitcast(BF16)
        k = len(srcs)
        for i, (ap, R) in enumerate(srcs):
            self.tr(pTb[0:rows, i * 128:(i + 1) * 128], ap, self.ident[:], list(R) + [self.ident.b], [pT.b])
        self.cp('act', stage[0:rows, 0:k, :], pTb[0:rows, 0:k * 128].rearrange("p (k c) -> p k c", k=k),
                [pT.b], [stage.b])
        self.dma('sp', dst_ap, stage[0:rows, 0:k, :], [stage.b], dst_b)

    def p3(self, es, L, x_src, x_dst, wout, mixfn):
        NT = self.NT
        xt = [self.sb(es, f'p3x{i}', [128, D], F32) for i in range(2)]
        mixT = [self.sb(es, f'p3mT{i}', [128, 10, 128], BF16) for i in range(2)]
        xo = [self.sb(es, f'p3o{i}', [128, D], F32) for i in range(2)]
        for n in range(NT):
            sl = n % 2
            mix = mixfn(n)
            xb = x_src.b[n] if len(x_src.b) > 1 else x_src.b[0]
            self.dma('sp', xt[sl][:], x_src[n * 128:(n + 1) * 128, :], [xb], [xt[sl].b])
            for half in range(2):
                pT = self.ps[2 + half]
                pTb = pT[:].bitcast(BF16)
                for k in range(5):
                    kk = half * 5 + k
                    self.tr(pTb[:, k * 128:(k + 1) * 128], mix[:, kk * 128:(kk + 1) * 128], self.ident[:],
                            [mix.b, self.ident.b], [pT.b])
                self.cp('act', mixT[sl][:, half * 5:half * 5 + 5, :].rearrange("p k c -> p (k c)"),
                        pTb[:, 0:640], [pT.b], [mixT[sl].b])
            for c in range(2):
                pU = self.ps[c]
                for k in range(10):
                    self.mm(pU[:, 0:512], mixT[sl][:, k, :], wout[:, k, c * 512:(c + 1) * 512], k == 0, k == 9,
                            [mixT[sl].b, wout.b], [pU.b])
                self.tt('dve', xo[sl][:, c * 512:(c + 1) * 512], pU[:, 0:512], xt[sl][:, c * 512:(c + 1) * 512],
                        ALU.add, [pU.b, xt[sl].b], [xo[sl].b])
            self.dma('sp', x_dst[n * 128:(n + 1) * 128, :], xo[sl][:], [xo[sl].b], [x_dst.b[n]])

    def attn_setup(self, es, dvp_two_banks=False):
        self.st_banks = [(self.ps[i], self.ps[i].b) for i in range(3)]
        self.st_rr = 0
        self.at_tiles = [self.sb(es, f'At{i}', [128, 512], BF16) for i in range(3)]
        self.at_rr = 0
        self.pend = []
        self.npv = 0
        self.epi_id = 0
        self.epi_done = set()

    def mem_attn(self, es, memkT, memV):
        T = self.T
        qT = [self.sb(es, f'mqT{i}', [64, T], BF16) for i in range(2)]
        ost = [self.sb(es, f'most{i}', [128, 4, 64], F32) for i in range(2)]
        rec = self.sb(es, 'mrec', [128, 4], F32)
        blk = 0
        for h in range(4):
            q = qT[h % 2]
            self.dma('sp', q[:], self.mem_qT[h], [self.mem_qT.b], [q.b])
            for cq in range(T // 512):
                set_i = blk % 2
                blk += 1
                accb = self.ps[3 + set_i]
                views = {j: (accb[:, j * 65:(j + 1) * 65], 0) for j in range(4)}
                tiles = [dict(kT=memkT[:, h, kt * 128:(kt + 1) * 128], V=memV[:, kt, h, :],
                              R=[memkT.b, memV.b], c0=0, subs=[0, 1, 2, 3]) for kt in range(2)]
                def epi(accb=accb, o=ost[set_i], cq=cq, h=h):
                    av = accb[:, 0:260].rearrange("p (j c) -> p j c", j=4)
                    self.op_('dve', lambda e, av=av: e.reciprocal(out=rec[:], in_=av[:, :, 64]), reads=[accb.b],
                             writes=[rec.b])
                    self.tt('dve', o[:], av[:, :, 0:64], rec[:].unsqueeze(2).to_broadcast([128, 4, 64]), ALU.mult,
                            [accb.b, rec.b], [o.b])
                    self.dma('sp', self.Y[cq * 512:(cq + 1) * 512, 1024 + h * 64:1024 + (h + 1) * 64]
                             .rearrange("(j p) c -> p j c", p=128), o[:], [o.b], [self.Y.b])
                self.attn_block(q[:, cq * 512:(cq + 1) * 512], [q.b], 512, tiles, 0.125, views, [accb.b], epilogue=epi)
        self.attn_flush()

    def layer_even(self, es, L, x_src, x_dst):
        li = L // 2
        w = self.w
        T, NT = self.T, self.NT
        S = self.S
        memkT, memV = self.mem_kv(es, L)
        with ExitStack() as p1:
            win, lng = self.load_win(p1, L, w['even_w_in'][li], EVEN_COLS)
            wuq = self.sb(p1, 'wuq', [128, 2, 768], BF16)
            for k in range(2):
                self.wload(wuq, k, w['mla_w_uq'][li, k * 128:(k + 1) * 128, :], 0, 768)
            wukv = self.sb(p1, 'wukv', [128, 1, 1024], BF16)
            self.wload(wukv, 0, w['mla_w_ukv'][li], 0, 1024)
            g_ql = self.bc_load(p1, 'g_ql', w['mla_q_lat_g'][li:li + 1, :], 256)
            g_kvl = self.bc_load(p1, 'g_kvl', w['mla_kv_lat_g'][li:li + 1, :], 128)
            g_qn = self.bc_load(p1, 'g_qn', w['mla_q_norm_g'][li:li + 1, 0:64], 64)
            g_qp = self.bc_load(p1, 'g_qp', w['mla_q_norm_g'][li:li + 1, 64:96], 32)
            g_kn = self.bc_load(p1, 'g_kn', w['mla_k_norm_g'][li:li + 1, 0:64], 64)
            g_kp = self.bc_load(p1, 'g_kp', w['mla_k_norm_g'][li:li + 1, 64:96], 32)
            g_bq = self.bc_load(p1, 'g_bq', w['nsa_q_norm_g'][li:li + 1, :], 64)
            g_ks = self.bc_load(p1, 'g_ks', w['nsa_k_norm_g'][li, 1:2, :], 64)
            g_kw = self.bc_load(p1, 'g_kw', w['nsa_k_norm_g'][li, 2:3, :], 64)
            g_mq = self.bc_load(p1, 'g_mq', w['mem_q_norm_g'][L:L + 1, :], 64)
            xt = [self.sb(p1, f'xt{i}', [128, D], F32) for i in range(2)]
            ht = [self.sb(p1, f'ht{i}', [128, D], BF16) for i in range(2)]
            hT = [self.sb(p1, f'hT{i}', [128, 8, 128], BF16) for i in range(2)]
            u = [self.sb(p1, f'u{i}', [128, EVEN_COLS], F32) for i in range(2)]
            ss = self.sb(p1, 'ss', [128, 1], F32)
            junk = self.sb(p1, 'junk', [128, D], BF16)
            latn = self.sb(p1, 'latn', [128, 384], BF16)
            latT = self.sb(p1, 'latT', [128, 3, 128], BF16)
            qsb = self.sb(p1, 'qsb', [128, 768], F32)
            kvsb = self.sb(p1, 'kvsb', [128, 1024], F32)
            qpe = self.sb(p1, 'qpe', [128, 256], F32)
            kpe = self.sb(p1, 'kpe', [128, 32], F32)
            kpeb = self.sb(p1, 'kpeb', [128, 32], BF16)
            qf = self.sb(p1, 'qf', [128, 8, 96], BF16)
            kfm = self.sb(p1, 'kfm', [128, 8, 96], BF16)
            vaug = self.sb(p1, 'vaug', [128, 8, 65], BF16)
            self.memset('pool', vaug[:], 1.0, [vaug.b])
            bqn = self.sb(p1, 'bqn', [128, 512], F32)
            bqf = self.sb(p1, 'bqf', [128, 512], BF16)
            kn2 = self.sb(p1, 'kn2', [128, 128], F32)
            kmisc = self.sb(p1, 'kmisc', [128, 8, 64], BF16)
            nv = self.sb(p1, 'nv', [128, 4, 65], BF16)
            self.memset('pool', nv[:], 1.0, [nv.b])
            mqf = self.sb(p1, 'mqf', [128, 256], BF16)
            gt = self.sb(p1, 'gt', [128, 1280], BF16)
            sg = self.sb(p1, 'sg', [128, 24], F32)
            stq = self.sb(p1, 'stq', [96, 8, 128], BF16)
            stk = self.sb(p1, 'stk', [96, 8, 128], BF16)
            stb = self.sb(p1, 'stb', [64, 8, 128], BF16)
            stm = self.sb(p1, 'stm', [64, 8, 128], BF16)
            stmq = self.sb(p1, 'stmq', [64, 4, 128], BF16)
            scrA = self.new_scr(p1, 'A', 512)
            scrB = self.new_scr(p1, 'B', 512)
            scrC = self.new_scr(p1, 'C', 128)
            for n in range(NT):
                ut = self.p1_front(n, x_src, xt, ht, hT, u, ss, junk, lng, win, EVEN_COLS)
                U = lambda a, b_: ut[:, a:b_]
                ub = [ut.b]
                tsl = slice(n * 128, (n + 1) * 128)
                self.chains_begin(['A', 'B', 'C', 'M', 'G'])
                self.chain('G')
                self.act(gt[:, 0:512], U(416, 928), AF.Silu, ub, [gt.b])
                self.act(gt[:, 512:1024], U(2232, 2744), AF.Silu, ub, [gt.b])
                self.act(gt[:, 1024:1280], U(3000, 3256), AF.Silu, ub, [gt.b])
                self.dma('sp', self.G[tsl, 0:1280], gt[:], [gt.b], [self.G.b[n]])
                self.act(sg[:], U(2208, 2232), AF.Sigmoid, ub, [sg.b])
                self.dma('sp', self.SG[tsl, 0:24], sg[:], [sg.b], [self.SG.b[n]])
                self.chain('A', scrA)
                self.rmsn(U(0, 256).rearrange("p (h c) -> p h c", h=1), 1, 256, g_ql,
                          latn[:, 0:256].rearrange("p (h c) -> p h c", h=1), ub, [latn.b])
                self.rmsn(U(256, 384).rearrange("p (h c) -> p h c", h=1), 1, 128, g_kvl,
                          latn[:, 256:384].rearrange("p (h c) -> p h c", h=1), ub, [latn.b])
                pT = self.ps[3]
                pTb = pT[:].bitcast(BF16)
                for k in range(3):
                    self.tr(pTb[:, k * 128:(k + 1) * 128], latn[:, k * 128:(k + 1) * 128], self.ident[:],
                            [latn.b, self.ident.b], [pT.b])
                self.cp('act', latT[:].rearrange("p k c -> p (k c)"), pTb[:, 0:384], [pT.b], [latT.b])
                pQ, pQ2, pK, pK2 = self.ps[3], self.ps[4], self.ps[5], self.ps[4]
                for k in range(2):
                    self.mm(pQ[:, 0:512], latT[:, k, :], wuq[:, k, 0:512], k == 0, k == 1, [latT.b, wuq.b], [pQ.b])
                for k in range(2):
                    self.mm(pQ2[:, 0:256], latT[:, k, :], wuq[:, k, 512:768], k == 0, k == 1, [latT.b, wuq.b], [pQ2.b])
                self.cp('act', qsb[:, 0:512], pQ[:, 0:512], [pQ.b], [qsb.b])
                self.cp('dve', qsb[:, 512:768], pQ2[:, 0:256], [pQ2.b], [qsb.b])
                self.mm(pK[:, 0:512], latT[:, 2, :], wukv[:, 0, 0:512], True, True, [latT.b, wukv.b], [pK.b])
                self.mm(pK2[:, 0:512], latT[:, 2, :], wukv[:, 0, 512:1024], True, True, [latT.b, wukv.b], [pK2.b])
                self.cp('act', kvsb[:, 0:512], pK[:, 0:512], [pK.b], [kvsb.b])
                self.cp('dve', kvsb[:, 512:1024], pK2[:, 0:512], [pK2.b], [kvsb.b])
                q3 = qsb[:].rearrange("p (h c) -> p h c", h=8)
                kv3 = kvsb[:].rearrange("p (h c) -> p h c", h=8)
                self.rmsn(q3[:, :, 0:64], 8, 64, g_qn, qf[:, :, 0:64], [qsb.b], [qf.b])
                qpe3 = qpe[:].rearrange("p (h c) -> p h c", h=8)
                self.rmsn(q3[:, :, 64:96], 8, 32, g_qp, qpe3, [qsb.b], [qpe.b])
                self.rope(qpe3, 8, 16, n, qf[:, :, 64:96], [qpe.b], [qf.b])
                self.rmsn(kv3[:, :, 0:64], 8, 64, g_kn, kfm[:, :, 0:64], [kvsb.b], [kfm.b])
                self.cp('dve', vaug[:, :, 0:64], kv3[:, :, 64:128], [kvsb.b], [vaug.b])
                kpe3 = kpe[:].rearrange("p (h c) -> p h c", h=1)
                self.rmsn(U(384, 416).rearrange("p (h c) -> p h c", h=1), 1, 32, g_kp, kpe3, ub, [kpe.b])
                self.rope(kpe3, 1, 16, n, kpeb[:].rearrange("p (h c) -> p h c", h=1), [kpe.b], [kpeb.b])
                self.cp('pool', kfm[:, :, 64:96], kpeb[:].unsqueeze(1).to_broadcast([128, 8, 32]), [kpeb.b], [kfm.b])
                self.transposes_out([(qf[:, h, :], [qf.b]) for h in range(8)], 96, stq,
                                    self.mla_qT[:, :, tsl].rearrange("h d t -> d h t"), [self.mla_qT.b], 3)
                self.transposes_out([(kfm[:, h, :], [kfm.b]) for h in range(8)], 96, stk,
                                    self.mla_kT[:, :, tsl].rearrange("h d t -> d h t"), [self.mla_kT.b], 5)
                self.dma('sp', self.mla_v[:, :, n, :].rearrange("h p c -> p h c"), vaug[:], [vaug.b], [self.mla_v.b])
                self.chain('B', scrB)
                bq3 = bqn[:].rearrange("p (h c) -> p h c", h=8)
                self.rmsn(U(928, 1440).rearrange("p (h c) -> p h c", h=8), 8, 64, g_bq, bq3, ub, [bqn.b])
                self.rope(bq3, 8, 32, n, bqf[:].rearrange("p (h c) -> p h c", h=8), [bqn.b], [bqf.b])
                self.chain('C', scrC)
                k23 = kn2[:].rearrange("p (h c) -> p h c", h=2)
                self.rmsn(U(1696, 1824).rearrange("p (h c) -> p h c", h=2), 2, 64, g_ks, k23, ub, [kn2.b])
                self.rope(k23, 2, 32, n, kmisc[:, 0:2, :], [kn2.b], [kmisc.b])
                self.rmsn(U(1952, 2080).rearrange("p (h c) -> p h c", h=2), 2, 64, g_kw, k23, ub, [kn2.b])
                self.rope(k23, 2, 32, n, kmisc[:, 2:4, :], [kn2.b], [kmisc.b])
                self.cp('pool', kmisc[:, 4:8, :], U(1440, 1696).rearrange("p (h c) -> p h c", h=4), ub, [kmisc.b])
                self.cp('dve', nv[:, 0:2, 0:64], U(1824, 1952).rearrange("p (h c) -> p h c", h=2), ub, [nv.b])
                self.cp('dve', nv[:, 2:4, 0:64], U(2080, 2208).rearrange("p (h c) -> p h c", h=2), ub, [nv.b])
                self.chain('B', scrB)
                self.transposes_out([(bqf[:, h * 64:(h + 1) * 64], [bqf.b]) for h in range(8)], 64, stb,
                                    self.nsa_qT[:, n].rearrange("g d r t -> d g r t"), [self.nsa_qT.b], 6)
                self.chain('C', scrC)
                self.transposes_out([(kmisc[:, i, :], [kmisc.b]) for i in range(8)], 64, stm,
                                    self.nsa_kT[:, :, tsl].rearrange("k d t -> d k t"), [self.nsa_kT.b], 7)
                self.dma('sp', self.nsa_v[:, :, n, :].rearrange("k p c -> p k c"), nv[:], [nv.b], [self.nsa_v.b])
                self.chain('B', scrB)
                self.rmsn(U(2744, 3000).rearrange("p (h c) -> p h c", h=4), 4, 64, g_mq,
                          mqf[:].rearrange("p (h c) -> p h c", h=4), ub, [mqf.b])
                self.transposes_out([(mqf[:, h * 64:(h + 1) * 64], [mqf.b]) for h in range(4)], 64, stmq,
                                    self.mem_qT[:, :, tsl].rearrange("h d t -> d h t"), [self.mem_qT.b], 6)
                self.chains_emit()
            S.barrier()
        kcmpT = self.sb(es, 'kcmpT', [64, 2, self.ncmp_pad], BF16)
        vcmp = self.sb(es, 'vcmp', [128, 2, self.ncmp_pad // 128, 129], BF16)
        self.nsa_compress(li, kcmpT, vcmp)
        S.barrier()
        with ExitStack() as pa:
            self.attn_setup(pa)
            self.mla_attn(pa)
            S.barrier()
        with ExitStack() as pa:
            self.attn_setup(pa)
            self.mem_attn(pa, memkT, memV)
            S.barrier()
        with ExitStack() as pa:
            self.attn_setup(pa)
            self.nsa_attn(pa, kcmpT, vcmp)
            S.barrier()
        with ExitStack() as p3:
            wout = self.load_wout(p3, L)
            yt = [self.sb(p3, f'yt{i}', [128, 2304], F32) for i in range(2)]
            gtt = [self.sb(p3, f'gtt{i}', [128, 1280], BF16) for i in range(2)]
            sgt = [self.sb(p3, f'sgt{i}', [128, 24], F32) for i in range(2)]
            mix = [self.sb(p3, f'mix{i}', [128, 1280], BF16) for i in range(2)]
            yb = self.sb(p3, 'yb', [128, 512], F32)
            yb2 = self.sb(p3, 'yb2', [128, 512], F32)

            def mixfn(n):
                sl = n % 2
                y, g, s_, m = yt[sl], gtt[sl], sgt[sl], mix[sl]
                tsl = slice(n * 128, (n + 1) * 128)
                self.dma('sp', y[:], self.Y[tsl, :], [self.Y.b], [y.b])
                self.dma('sp', g[:], self.G[tsl, 0:1280], [self.G.b[n]], [g.b])
                self.dma('sp', s_[:], self.SG[tsl, 0:24], [self.SG.b[n]], [s_.b])
                self.tt('dve', m[:, 0:512], y[:, 0:512], g[:, 0:512], ALU.mult, [y.b, g.b], [m.b])
                self.tt('pool', m[:, 1024:1280], y[:, 1024:1280], g[:, 1024:1280], ALU.mult, [y.b, g.b], [m.b])
                s3 = s_[:].rearrange("p (h c) -> p h c", c=3)
                y3 = lambda a: y[:, a:a + 512].rearrange("p (h c) -> p h c", h=8)
                b3 = yb[:].rearrange("p (h c) -> p h c", h=8)
                b23 = yb2[:].rearrange("p (h c) -> p h c", h=8)
                self.tt('dve', b3, y3(1280), s3[:, :, 0:1].to_broadcast([128, 8, 64]), ALU.mult, [y.b, s_.b], [yb.b])
                self.tt('pool', b23, y3(512), s3[:, :, 1:2].to_broadcast([128, 8, 64]), ALU.mult, [y.b, s_.b], [yb2.b])
                self.tt('dve', b3, b3, b23, ALU.add, [yb.b, yb2.b], [yb.b])
                self.tt('pool', b23, y3(1792), s3[:, :, 2:3].to_broadcast([128, 8, 64]), ALU.mult, [y.b, s_.b], [yb2.b])
                self.tt('dve', b3, b3, b23, ALU.add, [yb.b, yb2.b], [yb.b])
                self.tt('dve', m[:, 512:1024], yb[:], g[:, 512:1024], ALU.mult, [yb.b, g.b], [m.b])
                return m
            self.p3(p3, L, x_src, x_dst, wout, mixfn)
            S.barrier()

    def mla_attn(self, es):
        T, NT = self.T, self.NT
        qT = [self.sb(es, f'aqT{i}', [96, T], BF16) for i in range(2)]
        kT = [self.sb(es, f'akT{i}', [96, T], BF16) for i in range(2)]
        V = [self.sb(es, f'aV{i}', [128, NT, 65], BF16) for i in range(2)]
        ost = [self.sb(es, f'aost{i}', [128, 4, 64], F32) for i in range(2)]
        rec = self.sb(es, 'arec', [128, 4], F32)
        scale = 96 ** -0.5
        blk = 0
        for h in range(8):
            q, k, v = qT[h % 2], kT[h % 2], V[h % 2]
            self.dma('sp', q[:], self.mla_qT[h], [self.mla_qT.b], [q.b])
            self.dma('sp', k[:], self.mla_kT[h], [self.mla_kT.b], [k.b])
            self.dma('sp', v[:], self.mla_v[h], [self.mla_v.b], [v.b])
            for cq in range(T // 512):
                set_i = blk % 2
                blk += 1
                accb = self.ps[3 + set_i]
                views = {j: (accb[:, j * 65:(j + 1) * 65], 0) for j in range(4)}
                tiles = []
                for kt in range(4 * cq + 4):
                    vv = kt - 4 * cq
                    tl = dict(kT=k[:, kt * 128:(kt + 1) * 128], V=v[:, kt, :], R=[k.b, v.b],
                              c0=max(0, vv) * 128, subs=list(range(max(0, vv), 4)))
                    if vv >= 0:
                        tl['masks'] = [(self.tri_neg[:], self.ident[:], vv * 128, 128,
                                        [self.tri_neg.b, self.ident.b])]
                    tiles.append(tl)
                def epi(accb=accb, o=ost[set_i], cq=cq, h=h):
                    av = accb[:, 0:260].rearrange("p (j c) -> p j c", j=4)
                    self.op_('dve', lambda e, av=av: e.reciprocal(out=rec[:], in_=av[:, :, 64]), reads=[accb.b],
                             writes=[rec.b])
                    self.tt('dve', o[:], av[:, :, 0:64], rec[:].unsqueeze(2).to_broadcast([128, 4, 64]), ALU.mult,
                            [accb.b, rec.b], [o.b])
                    self.dma('sp', self.Y[cq * 512:(cq + 1) * 512, h * 64:(h + 1) * 64]
                             .rearrange("(j p) c -> p j c", p=128), o[:], [o.b], [self.Y.b])
                self.attn_block(q[:, cq * 512:(cq + 1) * 512], [q.b], 512, tiles, scale, views, [accb.b], epilogue=epi)
        self.attn_flush()

    def nsa_compress(self, li, kcmpT, vcmp):
        w = self.w
        T = self.T
        n_cmp = (T - 32) // 16 + 1
        ncp = self.ncmp_pad
        nct = ncp // 128
        self.memset('pool', kcmpT[:], 0.0, [kcmpT.b])
        self.memset('pool', vcmp[:], 0.0, [vcmp.b])
        with ExitStack() as es:
            w1 = self.sb(es, 'cw1', [64, 2, 32, 64], BF16)
            w2 = self.sb(es, 'cw2', [64, 2, 64], BF16)
            peT = self.sb(es, 'cpeT', [64, 2, 32], BF16)
            for kv in range(2):
                self.dma('pool', w1[:, kv], w['nsa_cmp_w1'][li, kv].rearrange("(l d) o -> d l o", d=64), [], [w1.b])
                self.dma('pool', w2[:, kv], w['nsa_cmp_w2'][li, kv], [], [w2.b])
                self.dma('pool', peT[:, kv], w['nsa_cmp_posT'][li, kv], [], [peT.b])
            g_kc = self.bc_load(es, 'g_kc', w['nsa_k_norm_g'][li, 0:1, :], 64)
            ovl = self.sb(es, 'ovl', [128, nct, self.n_blk], BF16)
            self.dma('sp', ovl[:], self.c['overlap'].rearrange("(k p) j -> p k j", p=128), [], [ovl.b])
            xT = [self.sb(es, f'cxT{i}', [64, T], BF16) for i in range(2)]
            bias = self.sb(es, 'cbias', [64, 1], F32)
            hid = self.sb(es, 'chid', [64, ncp], BF16)
            self.memset('pool', hid[:], 0.0, [hid.b])
            ctm = self.sb(es, 'ctm', [128, 64], F32)
            ctn = self.sb(es, 'ctn', [128, 64], F32)
            ctb = self.sb(es, 'ctb', [128, 64], BF16)
            rp = self.sb(es, 'crp', [128, 64], F32)
            it = 0
            for kv in range(2):
                for g in range(2):
                    x = xT[it % 2]
                    it += 1
                    self.dma('sp', x[:], self.nsa_kT[4 + kv * 2 + g], [self.nsa_kT.b], [x.b])
                    pH = self.ps[it % 2]
                    for l in range(32):
                        self.mm(pH[0:64, 0:n_cmp], w1[:, kv, l, :], x[:, l:l + 16 * (n_cmp - 1) + 1:16],
                                l == 0, False, [w1.b, x.b], [pH.b])
                        self.mm(pH[0:64, 511:512], w1[:, kv, l, :], peT[:, kv, l:l + 1], False, l == 31,
                                [w1.b, peT.b], [pH.b])
                    self.cp('dve', bias[:], pH[0:64, 511:512], [pH.b], [bias.b])
                    self.act(hid[:, 0:n_cmp], pH[0:64, 0:n_cmp], AF.Silu, [pH.b, bias.b], [hid.b], bias=bias[:, 0:1])
                    for kt in range(nct):
                        pO = self.ps[2 + kt % 2]
                        self.mm(pO[:, 0:64], hid[:, kt * 128:(kt + 1) * 128], w2[:, kv, :], True, True,
                                [hid.b, w2.b], [pO.b])
                        if kv == 1:
                            self.cp('act', vcmp[:, g, kt, 0:64], pO[:, 0:64], [pO.b], [vcmp.b])
                        else:
                            self.cp('act', ctm[:], pO[:, 0:64], [pO.b], [ctm.b])
                            self.rmsn(ctm[:].rearrange("p (h c) -> p h c", h=1), 1, 64, g_kc,
                                      ctn[:].rearrange("p (h c) -> p h c", h=1), [ctm.b], [ctn.b])
                            nrow = min(128, n_cmp - kt * 128)
                            r0 = 31 + 16 * 128 * kt
                            self.memset('dve', rp[:], 0.0, [rp.b])
                            self.dma('sp', rp[0:nrow, :], self.rope_d[r0:r0 + 16 * (nrow - 1) + 1:16, :],
                                     [self.rope_d.b], [rp.b])
                            A, Bm = self.sc_ra, self.sc_rb
                            c2 = ctn[:].rearrange("p (two c) -> p two c", two=2)
                            o2 = ctb[:].rearrange("p (two c) -> p two c", two=2)
                            Av = A[:, 0:64].rearrange("p (two c) -> p two c", two=2)
                            Bv = Bm[:, 0:64].rearrange("p (two c) -> p two c", two=2)
                            self.tt('dve', Av, c2, rp[:, 0:32].unsqueeze(1).to_broadcast([128, 2, 32]), ALU.mult,
                                    [ctn.b, rp.b], [A.b])
                            self.tt('dve', Bv, c2, rp[:, 32:64].unsqueeze(1).to_broadcast([128, 2, 32]), ALU.mult,
                                    [ctn.b, rp.b], [Bm.b])
                            self.tt('dve', o2[:, 0, :], Av[:, 0, :], Bv[:, 1, :], ALU.subtract, [A.b, Bm.b], [ctb.b])
                            self.tt('dve', o2[:, 1, :], Bv[:, 0, :], Av[:, 1, :], ALU.add, [A.b, Bm.b], [ctb.b])
                            pT = self.ps[4]
                            pTb = pT[:].bitcast(BF16)
                            self.tr(pTb[0:64, 0:128], ctb[:], self.ident[:], [ctb.b, self.ident.b], [pT.b])
                            self.cp('act', kcmpT[:, g, kt * 128:(kt + 1) * 128], pTb[0:64, 0:128], [pT.b], [kcmpT.b])
            for g in range(2):
                for kt in range(nct):
                    self.memset('pool', vcmp[:, g, kt, 64:65], 1.0, [vcmp.b])
                    self.cp('pool', vcmp[:, g, kt, 65:65 + self.n_blk], ovl[:, kt, :], [ovl.b], [vcmp.b])
            self.S.barrier()

    def nsa_attn(self, es, kcmpT, vcmp):
        T, NT = self.T, self.NT
        nb = self.n_blk
        nct = self.ncmp_pad // 128
        dvc = 65 + nb
        ksT = self.sb(es, 'ksT', [64, T], BF16)
        kwT = self.sb(es, 'kwT', [64, T], BF16)
        vs = self.sb(es, 'vs', [128, NT, 65], BF16)
        vw = self.sb(es, 'vw', [128, NT, 65], BF16)
        qt_ = [self.sb(es, f'nq{i}', [64, 512], BF16) for i in range(2)]
        cneg = [self.sb(es, f'cneg{i}', [128, self.ncmp_pad], BF16) for i in range(2)]
        forced = [self.sb(es, f'forced{i}', [128, nb], F32) for i in range(2)]
        negm = [self.sb(es, f'negm{i}', [128, T], BF16) for i in range(2)]
        rec = self.sb(es, 'nrec', [128, 4], F32)
        score = self.sb(es, 'nscore', [128, nb], F32)
        work = self.sb(es, 'nwork', [128, nb], F32)
        m8 = self.sb(es, 'nm8', [128, 8], F32)
        thr = self.sb(es, 'nthr', [128, 1], F32)
        nsel = self.sb(es, 'nsel', [128, nb], BF16)
        ost = [self.sb(es, f'nost{i}', [128, 3, 4, 64], F32) for i in range(2)]
        accA, accB, accS, accW = self.ps[3], self.ps[4], self.ps[5], self.ps[6]
        cmp_eid = {}

        def stage1(g, qt, sl):
            q, cn, fo, nm, o = qt_[sl], cneg[sl], forced[sl], negm[sl], ost[sl]
            tsl = slice(qt * 128, (qt + 1) * 128)
            self.dma('sp', q[:], self.nsa_qT[g, qt].rearrange("d r t -> d (r t)"), [self.nsa_qT.b], [q.b])
            self.dma('sp', cn[:], self.c['cmpneg'][tsl, :], [], [cn.b])
            self.dma('sp', fo[:], self.c['forced'][tsl, :], [], [fo.b])
            views = {j: ((accA if j < 2 else accB)[:, (j % 2) * dvc:(j % 2 + 1) * dvc], j // 2) for j in range(4)}
            tiles = []
            for kt in range(nct):
                if 16 * 128 * kt + 31 > qt * 128 + 127:
                    continue
                tiles.append(dict(kT=kcmpT[:, g, kt * 128:(kt + 1) * 128], V=vcmp[:, g, kt, 0:dvc],
                                  R=[kcmpT.b, vcmp.b], c0=0, subs=[0, 1, 2, 3],
                                  masks=[(cn[:, kt * 128:(kt + 1) * 128], self.i4[:], 0, 512, [cn.b, self.i4.b])]))
            if not tiles:
                tiles.append(dict(kT=kcmpT[:, g, 0:128], V=vcmp[:, g, 0, 0:dvc], R=[kcmpT.b, vcmp.b], c0=0,
                                  subs=[0, 1, 2, 3],
                                  masks=[(cn[:, 0:128], self.i4[:], 0, 512, [cn.b, self.i4.b])]))
            cmp_tiles = tiles
            viewsW = {j: (accW[:, j * 65:(j + 1) * 65], 0) for j in range(4)}
            tiles = []
            for kt in range(max(0, qt - 4), qt + 1):
                tl = dict(kT=kwT[:, kt * 128:(kt + 1) * 128], V=vw[:, kt, :], R=[kwT.b, vw.b], c0=0,
                          subs=[0, 1, 2, 3], masks=[])
                if kt == qt:
                    tl['masks'].append((self.tri_neg[:], self.i4[:], 0, 512, [self.tri_neg.b, self.i4.b]))
                if kt == qt - 4:
                    tl['masks'].append((self.edge_neg[:], self.i4[:], 0, 512, [self.edge_neg.b, self.i4.b]))
                tiles.append(tl)
            win_tiles = tiles

            def epi_cmp():
                for j in range(4):
                    ab = accA if j < 2 else accB
                    v_ = views[j][0]
                    self.ts('dve', rec[:, j:j + 1], v_[:, 64:65], 1e-30, ALU.max, [ab.b], [rec.b])
                self.op_('dve', lambda e: e.reciprocal(out=rec[:], in_=rec[:]), reads=[rec.b], writes=[rec.b])
                for j in range(4):
                    ab = accA if j < 2 else accB
                    v_ = views[j][0]
                    self.ts('dve', o[:, 0, j, :], v_[:, 0:64], rec[:, j:j + 1], ALU.mult, [ab.b, rec.b], [o.b])
                    if j == 0:
                        self.ts('dve', score[:], v_[:, 65:65 + nb], rec[:, 0:1], ALU.mult, [ab.b, rec.b], [score.b])
                    else:
                        self.stt(score[:], v_[:, 65:65 + nb], rec[:, j:j + 1], score[:], ALU.mult, ALU.add,
                                 [ab.b, rec.b, score.b], [score.b])
                self.tt('dve', score[:], score[:], fo[:], ALU.add, [score.b, fo.b], [score.b])
                self.op_('dve', lambda e: e.max(out=m8[:], in_=score[:]), reads=[score.b], writes=[m8.b])
                self.op_('dve', lambda e: e.match_replace(out=work[:], in_to_replace=m8[:], in_values=score[:],
                                                           imm_value=-3e38), reads=[score.b, m8.b], writes=[work.b])
                self.op_('dve', lambda e: e.max(out=m8[:], in_=work[:]), reads=[work.b], writes=[m8.b])
                self.ts('dve', thr[:], m8[:, 7:8], -1e29, ALU.max, [m8.b], [thr.b])
                self.ts('dve', nsel[:], score[:], thr[:, 0:1], ALU.is_lt, [score.b, thr.b], [nsel.b], s2=-BIG, op1=ALU.mult)
                nblk_need = (qt + 1) * 2
                self.cp('pool', nm[:, 0:nblk_need * 64].rearrange("p (j c) -> p j c", c=64),
                        nsel[:, 0:nblk_need].unsqueeze(2).to_broadcast([128, nblk_need, 64]), [nsel.b], [nm.b])
                self.tt('pool', nm[:, tsl], nm[:, tsl], self.tri_neg[:], ALU.add, [nm.b, self.tri_neg.b], [nm.b])

            def epi_win():
                av = accW[:, 0:260].rearrange("p (j c) -> p j c", j=4)
                self.op_('dve', lambda e, av=av: e.reciprocal(out=rec[:], in_=av[:, :, 64]), reads=[accW.b], writes=[rec.b])
                self.tt('dve', o[:, 2], av[:, :, 0:64], rec[:].unsqueeze(2).to_broadcast([128, 4, 64]), ALU.mult,
                        [accW.b, rec.b], [o.b])

            eid = self.attn_block(q[:], [q.b], 512, cmp_tiles, 0.125, views, [accA.b, accB.b], epilogue=epi_cmp)
            self.attn_block(q[:], [q.b], 512, win_tiles, 0.125, viewsW, [accW.b], epilogue=epi_win)
            cmp_eid[(g, qt)] = eid

        def stage2(g, qt, sl):
            q, nm, o = qt_[sl], negm[sl], ost[sl]
            tsl = slice(qt * 128, (qt + 1) * 128)
            self.attn_sync(cmp_eid[(g, qt)])
            views = {j: (accS[:, j * 65:(j + 1) * 65], 0) for j in range(4)}
            tiles = [dict(kT=ksT[:, kt * 128:(kt + 1) * 128], V=vs[:, kt, :], R=[ksT.b, vs.b], c0=0,
                          subs=[0, 1, 2, 3],
                          masks=[(nm[:, kt * 128:(kt + 1) * 128], self.i4[:], 0, 512, [nm.b, self.i4.b])])
                     for kt in range(qt + 1)]
            def epi_sel():
                av = accS[:, 0:260].rearrange("p (j c) -> p j c", j=4)
                self.op_('dve', lambda e, av=av: e.reciprocal(out=rec[:], in_=av[:, :, 64]), reads=[accS.b], writes=[rec.b])
                self.tt('dve', o[:, 1], av[:, :, 0:64], rec[:].unsqueeze(2).to_broadcast([128, 4, 64]), ALU.mult,
                        [accS.b, rec.b], [o.b])
                for bi, base in enumerate((1280, 512, 1792)):
                    self.dma('sp', self.Y[tsl, base + g * 256:base + (g + 1) * 256],
                             o[:, bi].rearrange("p r c -> p (r c)"), [o.b], [self.Y.b])
            self.attn_block(q[:], [q.b], 512, tiles, 0.125, views, [accS.b], epilogue=epi_sel)

        for g in range(2):
            self.dma('sp', ksT[:], self.nsa_kT[0 + g], [self.nsa_kT.b], [ksT.b])
            self.dma('sp', kwT[:], self.nsa_kT[2 + g], [self.nsa_kT.b], [kwT.b])
            self.dma('sp', vs[:], self.nsa_v[0 + g], [self.nsa_v.b], [vs.b])
            self.dma('sp', vw[:], self.nsa_v[2 + g], [self.nsa_v.b], [vw.b])
            stage1(g, 0, 0)
            for qt in range(NT):
                if qt + 1 < NT:
                    stage1(g, qt + 1, (qt + 1) % 2)
                stage2(g, qt, qt % 2)
            self.attn_flush()

    def layer_odd(self, es, L, x_src, x_dst):
        li = L // 2
        w = self.w
        T, NT = self.T, self.NT
        S = self.S
        memkT, memV = self.mem_kv(es, L)
        with ExitStack() as p1:
            win, lng = self.load_win(p1, L, w['odd_w_in'][li], ODD_COLS)
            g_cq = self.bc_load(p1, 'g_cq', w['dsa_q_norm_g'][li:li + 1, :], 64)
            g_ck = self.bc_load(p1, 'g_ck', w['dsa_k_norm_g'][li:li + 1, :], 64)
            g_mq = self.bc_load(p1, 'g_mq', w['mem_q_norm_g'][L:L + 1, :], 64)
            xt = [self.sb(p1, f'xt{i}', [128, D], F32) for i in range(2)]
            ht = [self.sb(p1, f'ht{i}', [128, D], BF16) for i in range(2)]
            hT = [self.sb(p1, f'hT{i}', [128, 8, 128], BF16) for i in range(2)]
            u = [self.sb(p1, f'u{i}', [128, ODD_COLS], F32) for i in range(2)]
            ss = self.sb(p1, 'ss', [128, 1], F32)
            junk = self.sb(p1, 'junk', [128, D], BF16)
            gt = self.sb(p1, 'gt', [128, 1792], BF16)
            cqn = self.sb(p1, 'cqn', [128, 512], F32)
            cqf = self.sb(p1, 'cqf', [128, 512], BF16)
            ckn = self.sb(p1, 'ckn', [128, 64], F32)
            ckf = self.sb(p1, 'ckf', [128, 64], BF16)
            cva = self.sb(p1, 'cva', [128, 65], BF16)
            self.memset('pool', cva[:], 1.0, [cva.b])
            iqf = self.sb(p1, 'iqf', [128, 256], BF16)
            ikf = self.sb(p1, 'ikf', [128, 32], BF16)
            iwt = self.sb(p1, 'iwt', [128, 8], F32)
            mlv = self.sb(p1, 'mlv', [128, 4, 129], BF16)
            self.memset('pool', mlv[:], 1.0, [mlv.b])
            mqf = self.sb(p1, 'mqf', [128, 256], BF16)
            stq = self.sb(p1, 'stq', [64, 8, 128], BF16)
            stk = self.sb(p1, 'stk', [64, 1, 128], BF16)
            sti = self.sb(p1, 'sti', [32, 8, 128], BF16)
            stik = self.sb(p1, 'stik', [32, 1, 128], BF16)
            stmq = self.sb(p1, 'stmq', [64, 4, 128], BF16)
            strw = self.sb(p1, 'strw', [128, 4, 128], F32)
            stif = self.sb(p1, 'stif', [8, 128], F32)
            scrA = self.new_scr(p1, 'A', 512)
            scrB = self.new_scr(p1, 'B', 256)
            for n in range(NT):
                ut = self.p1_front(n, x_src, xt, ht, hT, u, ss, junk, lng, win, ODD_COLS)
                U = lambda a, b_: ut[:, a:b_]
                ub = [ut.b]
                tsl = slice(n * 128, (n + 1) * 128)
                self.chains_begin(['A', 'B', 'L', 'M', 'G'])
                self.chain('G')
                self.act(gt[:, 0:512], U(936, 1448), AF.Silu, ub, [gt.b])
                self.act(gt[:, 512:1024], U(2992, 3504), AF.Silu, ub, [gt.b])
                self.act(gt[:, 1024:1280], U(3760, 4016), AF.Silu, ub, [gt.b])
                self.act(gt[:, 1280:1792], U(2480, 2992), AF.Sigmoid, ub, [gt.b])
                self.dma('sp', self.G[tsl, :], gt[:], [gt.b], [self.G.b[n]])
                self.chain('A', scrA)
                cq3 = cqn[:].rearrange("p (h c) -> p h c", h=8)
                self.rmsn(U(0, 512).rearrange("p (h c) -> p h c", h=8), 8, 64, g_cq, cq3, ub, [cqn.b])
                self.rope(cq3, 8, 32, n, cqf[:].rearrange("p (h c) -> p h c", h=8), [cqn.b], [cqf.b])
                ck3 = ckn[:].rearrange("p (h c) -> p h c", h=1)
                self.rmsn(U(512, 576).rearrange("p (h c) -> p h c", h=1), 1, 64, g_ck, ck3, ub, [ckn.b])
                self.rope(ck3, 1, 32, n, ckf[:].rearrange("p (h c) -> p h c", h=1), [ckn.b], [ckf.b])
                self.cp('dve', cva[:, 0:64], U(576, 640), ub, [cva.b])
                self.dma('sp', self.dsa_v[:, n, :], cva[:], [cva.b], [self.dsa_v.b])
                pT = self.ps[4]
                pTb = pT[:].bitcast(BF16)
                for h in range(8):
                    self.tr(pTb[0:64, h * 128:(h + 1) * 128], cqf[:, h * 64:(h + 1) * 64], self.ident[:],
                            [cqf.b, self.ident.b], [pT.b])
                self.cp('act', stq[:], pTb[0:64, 0:1024].rearrange("p (k c) -> p k c", k=8), [pT.b], [stq.b])
                self.dma('sp', self.dsa_qT[n], stq[:], [stq.b], [self.dsa_qT.b])
                self.transposes_out([(ckf[:], [ckf.b])], 64, stk,
                                    self.dsa_kT[:, tsl].rearrange("d (k t) -> d k t", k=1), [self.dsa_kT.b], 5)
                self.chain('B', scrB)
                self.rope(U(640, 896).rearrange("p (h c) -> p h c", h=8), 8, 16, n,
                          iqf[:].rearrange("p (h c) -> p h c", h=8), ub, [iqf.b])
                self.rope(U(896, 928).rearrange("p (h c) -> p h c", h=1), 1, 16, n,
                          ikf[:].rearrange("p (h c) -> p h c", h=1), ub, [ikf.b])
                self.ts('dve', iwt[:], U(928, 936), 8 ** -0.5, ALU.mult, ub, [iwt.b])
                self.dma('sp', self.idx_w[tsl, :], iwt[:], [iwt.b], [self.idx_w.b])
                self.transposes_out([(iqf[:, h * 32:(h + 1) * 32], [iqf.b]) for h in range(8)], 32, sti,
                                    self.idx_qT[:, :, tsl].rearrange("h d t -> d h t"), [self.idx_qT.b], 6)
                self.transposes_out([(ikf[:], [ikf.b])], 32, stik,
                                    self.idx_kT[:, tsl].rearrange("d (k t) -> d k t", k=1), [self.idx_kT.b], 7)
                self.chain('L')
                pR = self.ps[3]
                for k in range(4):
                    self.tr(pR[:, k * 128:(k + 1) * 128], U(1448 + k * 128, 1448 + (k + 1) * 128), self.identf[:],
                            ub + [self.identf.b], [pR.b])
                self.cp('act', strw[:].rearrange("p k c -> p (k c)"), pR[:, 0:512], [pR.b], [strw.b])
                self.dma('sp', self.ml_raw[:, tsl].rearrange("(k p) t -> p k t", p=128), strw[:], [strw.b],
                         [self.ml_raw.b])
                pI = self.ps[3]
                self.tr(pI[0:8, 0:128], U(2472, 2480), self.identf[:], ub + [self.identf.b], [pI.b])
                self.cp('act', stif[:], pI[0:8, 0:128], [pI.b], [stif.b])
                self.dma('sp', self.ml_if[:, tsl], stif[:], [stif.b], [self.ml_if.b])
                self.cp('dve', mlv[:, :, 0:128], U(1960, 2472).rearrange("p (h c) -> p h c", h=4), ub, [mlv.b])
                self.dma('sp', self.ml_v[:, :, n, :].rearrange("h p c -> p h c"), mlv[:], [mlv.b], [self.ml_v.b])
                self.chain('B', scrB)
                self.rmsn(U(3504, 3760).rearrange("p (h c) -> p h c", h=4), 4, 64, g_mq,
                          mqf[:].rearrange("p (h c) -> p h c", h=4), ub, [mqf.b])
                self.transposes_out([(mqf[:, h * 64:(h + 1) * 64], [mqf.b]) for h in range(4)], 64, stmq,
                                    self.mem_qT[:, :, tsl].rearrange("h d t -> d h t"), [self.mem_qT.b], 6)
                self.chains_emit()
            S.barrier()
        if 'stop_p1' in self.dbg:
            return
        self.mlstm_pre(li)
        S.barrier()
        if 'stop_pre' in self.dbg:
            return
        with ExitStack() as pa:
            self.attn_setup(pa)
            self.mlstm_attn(pa)
            S.barrier()
        if 'stop_ml' in self.dbg:
            return
        with ExitStack() as pa:
            self.attn_setup(pa)
            self.mem_attn(pa, memkT, memV)
            S.barrier()
        if 'stop_mem' in self.dbg:
            return
        with ExitStack() as pa:
            self.attn_setup(pa)
            self.dsa_attn(pa)
            S.barrier()
        if 'stop_dsa' in self.dbg:
            return
        with ExitStack() as p3:
            wout = self.load_wout(p3, L)
            g_h = self.bc_load(p3, 'g_h', w['mlstm_h_norm_g'][li:li + 1, :], 128)
            yt = [self.sb(p3, f'yt{i}', [128, 1280], F32) for i in range(2)]
            gtt = [self.sb(p3, f'gtt{i}', [128, 1792], BF16) for i in range(2)]
            mix = [self.sb(p3, f'mix{i}', [128, 1280], BF16) for i in range(2)]
            hn = self.sb(p3, 'hn', [128, 512], F32)

            def mixfn(n):
                sl = n % 2
                y, g, m = yt[sl], gtt[sl], mix[sl]
                tsl = slice(n * 128, (n + 1) * 128)
                self.dma('sp', y[:], self.Y[tsl, 0:1280], [self.Y.b], [y.b])
                self.dma('sp', g[:], self.G[tsl, :], [self.G.b[n]], [g.b])
                self.tt('dve', m[:, 0:512], y[:, 0:512], g[:, 0:512], ALU.mult, [y.b, g.b], [m.b])
                self.tt('pool', m[:, 1024:1280], y[:, 1024:1280], g[:, 1024:1280], ALU.mult, [y.b, g.b], [m.b])
                self.rmsn(y[:, 512:1024].rearrange("p (h c) -> p h c", h=4), 4, 128, g_h,
                          hn[:].rearrange("p (h c) -> p h c", h=4), [y.b], [hn.b])
                self.tt('dve', hn[:], hn[:], g[:, 1280:1792], ALU.mult, [hn.b, g.b], [hn.b])
                self.tt('dve', m[:, 512:1024], hn[:], g[:, 512:1024], ALU.mult, [hn.b, g.b], [m.b])
                return m
            self.p3(p3, L, x_src, x_dst, wout, mixfn)
            S.barrier()

    def mlstm_pre(self, li):
        w = self.w
        T = self.T
        with ExitStack() as es:
            xp = [self.sb(es, f'xp{i}', [128, T + 3], F32) for i in range(2)]
            y = self.sb(es, 'cy', [128, T], F32)
            yo = [self.sb(es, f'cyo{i}', [128, T], BF16) for i in range(2)]
            wc = self.sb(es, 'cwc', [128, 4, 4], F32)
            bc = self.sb(es, 'cbc', [128, 4], F32)
            for ck in range(4):
                self.dma('sp', wc[:, ck, :], w['mlstm_conv_wT'][li, ck * 128:(ck + 1) * 128, :], [], [wc.b])
                self.dma('sp', bc[:, ck:ck + 1], w['mlstm_conv_b'][li, ck * 128:(ck + 1) * 128].unsqueeze(1), [], [bc.b])
            for ck in range(4):
                x = xp[ck % 2]
                o = yo[ck % 2]
                self.memset('pool', x[:, 0:3], 0.0, [x.b])
                self.dma('sp', x[:, 3:T + 3], self.ml_raw[ck * 128:(ck + 1) * 128, :], [self.ml_raw.b], [x.b])
                self.ts('dve', y[:], x[:, 0:T], wc[:, ck, 0:1], ALU.mult, [x.b, wc.b, bc.b], [y.b],
                        s2=bc[:, ck:ck + 1], op1=ALU.add)
                for j in range(1, 4):
                    self.stt(y[:], x[:, j:j + T], wc[:, ck, j:j + 1], y[:], ALU.mult, ALU.add, [x.b, wc.b, y.b], [y.b])
                self.act(o[:], y[:], AF.Silu, [y.b], [o.b])
                self.dma('sp', self.ml_qkT[ck * 128:(ck + 1) * 128, :], o[:], [o.b], [self.ml_qkT.b])
            self.S.barrier()
        with ExitStack() as es:
            ig = self.sb(es, 'ig', [4, T], F32)
            fg = self.sb(es, 'fg', [4, T], F32)
            cs = self.sb(es, 'cs', [4, T], F32)
            a = self.sb(es, 'ga', [4, T], F32)
            Mt = self.sb(es, 'gM', [4, T], F32)
            ones = self.sb(es, 'gones', [4, T], F32)
            ib = self.sb(es, 'gib', [4, 1], F32)
            fb = self.sb(es, 'gfb', [4, 1], F32)
            self.memset('pool', ones[:], 1.0, [ones.b])
            self.dma('sp', ig[:], self.ml_if[0:4, :], [self.ml_if.b], [ig.b])
            self.dma('sp', fg[:], self.ml_if[4:8, :], [self.ml_if.b], [fg.b])
            self.dma('sp', ib[:], w['mlstm_i_bias'][li].unsqueeze(1), [], [ib.b])
            self.dma('sp', fb[:], w['mlstm_f_bias'][li].unsqueeze(1), [], [fb.b])
            self.ts('dve', fb[:], fb[:], -1.0, ALU.mult, [fb.b], [fb.b])
            self.act(fg[:], fg[:], AF.Exp, [fg.b, fb.b], [fg.b], bias=fb[:, 0:1], scale=-1.0)
            self.act(fg[:], fg[:], AF.Ln, [fg.b], [fg.b], bias=self.one_c[0:4, 0:1], scale=1.0)
            self.op_('dve', lambda e: e.tensor_tensor_scan(out=cs[:], data0=ones[:], data1=fg[:], initial=0.0,
                                                             op0=ALU.mult, op1=ALU.add),
                      reads=[ones.b, fg.b], writes=[cs.b])
            self.stt(a[:], ig[:], ib[:, 0:1], cs[:], ALU.add, ALU.add, [ig.b, ib.b, cs.b], [a.b])
            self.op_('dve', lambda e: e.tensor_tensor_scan(out=Mt[:], data0=a[:], data1=a[:], initial=0.0,
                                                             op0=ALU.max, op1=ALU.max),
                      reads=[a.b], writes=[Mt.b])
            self.tt('dve', cs[:], cs[:], Mt[:], ALU.subtract, [cs.b, Mt.b], [cs.b])
            self.act(cs[:], cs[:], AF.Exp, [cs.b], [cs.b])
            self.ts('dve', Mt[:], Mt[:], -1.0, ALU.mult, [Mt.b], [Mt.b])
            self.dma('sp', self.ml_g[0:4, :], a[:], [a.b], [self.ml_g.b])
            self.dma('sp', self.ml_g[4:8, :], Mt[:], [Mt.b], [self.ml_g.b])
            self.dma('sp', self.ml_g[8:12, :], cs[:], [cs.b], [self.ml_g.b])
            self.S.barrier()

    def mlstm_attn(self, es):
        T, NT = self.T, self.NT
        qT = [self.sb(es, f'lqT{i}', [64, T], BF16) for i in range(2)]
        kT = [self.sb(es, f'lkT{i}', [64, T], BF16) for i in range(2)]
        V = [self.sb(es, f'lV{i}', [128, NT, 129], BF16) for i in range(2)]
        nM = [self.sb(es, f'lnM{i}', [128, T], F32) for i in range(2)]
        ant = self.sb(es, 'lant', [NT, 2, 128], F32)
        atm = [self.sb(es, f'latm{i}', [128, 2, NT], F32) for i in range(2)]
        Et = [self.sb(es, f'lEt{i}', [128, 512], F32) for i in range(3)]
        ost = [self.sb(es, f'lost{i}', [128, 4, 128], F32) for i in range(2)]
        d2 = self.sb(es, 'ld2', [128, 4], F32)
        ecnt = [0]
        blk = 0
        LN8 = math.log(0.125)
        for h in range(4):
            q, k, v, nm, at = qT[h % 2], kT[h % 2], V[h % 2], nM[h % 2], atm[h % 2]
            self.dma('sp', q[:], self.ml_qkT[h * 64:(h + 1) * 64, :], [self.ml_qkT.b], [q.b])
            self.dma('sp', k[:], self.ml_qkT[256 + h * 64:256 + (h + 1) * 64, :], [self.ml_qkT.b], [k.b])
            self.dma('sp', v[:], self.ml_v[h], [self.ml_v.b], [v.b])
            self.dma('sp', nm[:], self.ml_g[4 + h:5 + h, :].to_broadcast([128, T]), [self.ml_g.b], [nm.b])
            self.dma('sp', ant[:, 0, :], self.ml_g[h, :].rearrange("(n p) -> n p", p=128), [self.ml_g.b], [ant.b])
            self.dma('sp', ant[:, 1, :], self.ml_g[8 + h, :].rearrange("(n p) -> n p", p=128), [self.ml_g.b], [ant.b])
            pA = self.ps[7]
            for i in range(2):
                self.tr(pA[:, i * NT:(i + 1) * NT], ant[:, i, :], self.identf[0:NT, 0:NT], [ant.b, self.identf.b], [pA.b])
            self.cp('act', at[:].rearrange("p a n -> p (a n)"), pA[:, 0:2 * NT], [pA.b], [at.b])
            self.ts('dve', at[:, 0, :], at[:, 0, :], LN8, ALU.add, [at.b], [at.b])
            for cq in range(T // 512):
                set_i = blk % 2
                blk += 1
                accA, accB = self.ps[3 + 2 * set_i], self.ps[4 + 2 * set_i]
                views = {j: ((accA if j < 2 else accB)[:, (j % 2) * 129:(j % 2 + 1) * 129], j // 2) for j in range(4)}
                tiles = []
                for kt in range(4 * cq + 4):
                    vv = kt - 4 * cq
                    c0 = max(0, vv) * 128

                    def efn(kt=kt, c0=c0, cq=cq, nm=nm, at=at):
                        E = Et[ecnt[0] % 3]
                        ecnt[0] += 1
                        self.act(E[:, c0:512], nm[:, cq * 512 + c0:(cq + 1) * 512], AF.Exp, [nm.b, at.b], [E.b],
                                 bias=at[:, 0, kt:kt + 1], scale=1.0)
                        return E, [E.b]
                    tl = dict(kT=k[:, kt * 128:(kt + 1) * 128], V=v[:, kt, :], R=[k.b, v.b], c0=c0,
                              subs=list(range(max(0, vv), 4)), Efn=efn)
                    if vv >= 0:
                        tl['diag'] = vv
                    tiles.append(tl)
                def epi(accA=accA, accB=accB, views=views, o=ost[set_i], cq=cq, h=h, at=at):
                    for j in range(4):
                        ab = accA if j < 2 else accB
                        v_ = views[j][0]
                        self.act(d2[:, j:j + 1], v_[:, 128:129], AF.Abs, [ab.b], [d2.b])
                    self.tt('dve', d2[:], d2[:], at[:, 1, 4 * cq:4 * cq + 4], ALU.max, [d2.b, at.b], [d2.b])
                    self.op_('dve', lambda e: e.reciprocal(out=d2[:], in_=d2[:]), reads=[d2.b], writes=[d2.b])
                    for j in range(4):
                        ab = accA if j < 2 else accB
                        v_ = views[j][0]
                        self.ts('dve', o[:, j, :], v_[:, 0:128], d2[:, j:j + 1], ALU.mult, [ab.b, d2.b], [o.b])
                    self.dma('sp', self.Y[cq * 512:(cq + 1) * 512, 512 + h * 128:512 + (h + 1) * 128]
                             .rearrange("(j p) c -> p j c", p=128), o[:], [o.b], [self.Y.b])
                self.attn_block(q[:, cq * 512:(cq + 1) * 512], [q.b], 512, tiles, 1.0, views, [accA.b, accB.b],
                                mode='mul', epilogue=epi)
        self.attn_flush()

    def dsa_attn(self, es):
        T, NT = self.T, self.NT
        KSEL = min(256, T // 4)
        ikT = self.sb(es, 'ikT', [32, T], BF16)
        ckT = self.sb(es, 'ckT', [64, T], BF16)
        cv = self.sb(es, 'cv', [128, NT, 65], BF16)
        self.dma('sp', ikT[:], self.idx_kT[:, :], [self.idx_kT.b], [ikT.b])
        self.dma('sp', ckT[:], self.dsa_kT[:, :], [self.dsa_kT.b], [ckT.b])
        self.dma('sp', cv[:], self.dsa_v[:, :, :], [self.dsa_v.b], [cv.b])
        iq = [self.sb(es, f'iq{i}', [32, 8, 128], BF16) for i in range(2)]
        iw = [self.sb(es, f'iw{i}', [128, 8], F32) for i in range(2)]
        cq_ = [self.sb(es, f'cq{i}', [64, 2, 512], BF16) for i in range(2)]
        score2 = [self.sb(es, f'dscore{i}', [128, T], F32) for i in range(3)]
        thrA2 = [self.sb(es, f'dthrA{i}', [128, 1], F32) for i in range(2)]
        work = self.sb(es, 'dwork', [128, T], F32)
        negm = [self.sb(es, f'dnegm{i}', [128, T], BF16) for i in range(2)]
        rl = [self.sb(es, f'drl{i}', [128, 512], F32) for i in range(3)]
        m8 = self.sb(es, 'dm8', [128, 8], F32)
        thr = self.sb(es, 'dthr', [128, 1], F32)
        rec = self.sb(es, 'drec', [128, 4], F32)
        ost = [self.sb(es, f'dost{i}', [128, 4, 64], F32) for i in range(2)]
        rlc_ = [0]
        blk_ = [0]
        NBIS = 34
        junk = self.sb(es, 'djunk', [128, T], BF16)
        amax = self.sb(es, 'damax', [128, 1], F32)
        w0 = self.sb(es, 'dw0', [128, 1], F32)
        nHh = self.sb(es, 'dnHh', [128, 40], F32)
        nmid = [self.sb(es, f'dnmid{i}', [128, 1], F32) for i in range(2)]
        Ssum = self.sb(es, 'dS', [128, 1], F32)
        tsg = self.sb(es, 'dtsg', [128, 1], F32)

        def stage_a1(qt):
            rlc = rlc_[0]
            sl = qt % 2
            tsl = slice(qt * 128, (qt + 1) * 128)
            q_i, w_i = iq[sl], iw[sl]
            score = score2[qt % 3]
            self.dma('sp', q_i[:], self.idx_qT[:, :, tsl].rearrange("h d t -> d h t"), [self.idx_qT.b], [q_i.b])
            self.dma('sp', w_i[:], self.idx_w[tsl, :], [self.idx_w.b], [w_i.b])
            ncols = (qt + 1) * 128
            for c in range((ncols + 511) // 512):
                c0 = c * 512
                wd = min(512, ncols - c0)
                for h in range(8):
                    bi = self.st_rr % len(self.st_banks)
                    self.st_rr += 1
                    bank, bb = self.st_banks[bi]
                    self.mm(bank[:, 0:wd], q_i[:, h, :], ikT[:, c0:c0 + wd], True, True, [q_i.b, ikT.b], [bb])
                    r = rl[rlc % 3]
                    rlc += 1
                    self.act(r[:, 0:wd], bank[:, 0:wd], AF.Relu, [bb], [r.b])
                    if h == 0:
                        self.ts('dve', score[:, c0:c0 + wd], r[:, 0:wd], w_i[:, 0:1], ALU.mult, [r.b, w_i.b], [score.b])
                    else:
                        self.stt(score[:, c0:c0 + wd], r[:, 0:wd], w_i[:, h:h + 1], score[:, c0:c0 + wd],
                                 ALU.mult, ALU.add, [r.b, w_i.b, score.b], [score.b])
            rlc_[0] = rlc

        def is_act_tile(qt):
            return ((qt + 1) * 128 > KSEL) and ('dsa_nobis' not in self.dbg)

        def stage_a2_finish(qt):
            sl = qt % 2
            nm = negm[sl]
            score = score2[qt % 3]
            ncols = (qt + 1) * 128
            th = thrA2[qt % 2] if is_act_tile(qt) else thr
            self.ts('dve', nm[:, 0:ncols], score[:, 0:ncols], th[:, 0:1], ALU.is_lt, [score.b, th.b], [nm.b],
                    s2=-BIG, op1=ALU.mult)

        def stage_a2(qt):
            sl = qt % 2
            tsl = slice(qt * 128, (qt + 1) * 128)
            q_c, nm = cq_[sl], negm[sl]
            score = score2[qt % 3]
            ncols = (qt + 1) * 128
            for half in range(2):
                self.dma('sp', q_c[:, half, :], self.dsa_qT[qt, :, half * 4:(half + 1) * 4, :].rearrange("d h t -> d (h t)"),
                         [self.dsa_qT.b], [q_c.b])
            use_act = is_act_tile(qt)
            if use_act:
                self.op_('dve', lambda e, ncols=ncols: e.tensor_reduce(out=amax[:], in_=score[:, 0:ncols], axis=AX.X,
                                                                      op=ALU.max, apply_absolute_value=True),
                         reads=[score.b], writes=[amax.b])
                self.ts('dve', w0[:], amax[:], 2.0, ALU.mult, [amax.b], [w0.b], s2=2.0, op1=ALU.add)
                self.ts('dve', nHh[:], self.pw[:], w0[:, 0:1], ALU.mult, [self.pw.b, w0.b], [nHh.b])
            self.tt('dve', score[:, tsl], score[:, tsl], self.trinegf[:], ALU.add, [score.b, self.trinegf.b], [score.b])
            if use_act:
                cconst = float(0.5 - (2 * KSEL - ncols - 1))
                self.memset('pool', nmid[0][:], 0.0, [nmid[0].b])
                for j in range(NBIS):
                    cur, nxt = nmid[j % 2], nmid[(j + 1) % 2]
                    self.act(junk[:, 0:ncols], score[:, 0:ncols], AF.Sign, [score.b, cur.b], [junk.b, Ssum.b],
                             bias=cur[:, 0:1], scale=1.0, accum=Ssum[:, 0:1])
                    self.act(tsg[:], Ssum[:], AF.Sign, [Ssum.b], [tsg.b], bias=cconst, scale=1.0)
                    self.act(nxt[:], tsg[:], AF.Identity, [tsg.b, cur.b, nHh.b], [nxt.b],
                             bias=cur[:, 0:1], scale=nHh[:, j:j + 1])
                fin = nmid[NBIS % 2]
                thrA = thrA2[qt % 2]
                self.act(thrA[:], fin[:], AF.Identity, [fin.b, nHh.b], [thrA.b], bias=nHh[:, NBIS - 1:NBIS], scale=-1.0)
                return
            elif ncols > KSEL and 'dsa_notopk' not in self.dbg:
                self.cp('pool', work[:, 0:ncols], score[:, 0:ncols], [score.b], [work.b])
                nr = KSEL // 8
                for r_ in range(nr):
                    self.op_('dve', lambda e, ncols=ncols: e.max(out=m8[:], in_=work[:, 0:ncols]),
                             reads=[work.b], writes=[m8.b])
                    if r_ < nr - 1:
                        self.op_('dve', lambda e, ncols=ncols: e.match_replace(
                            out=work[:, 0:ncols], in_to_replace=m8[:], in_values=work[:, 0:ncols], imm_value=-3e38),
                            reads=[work.b, m8.b], writes=[work.b])
                self.ts('dve', thr[:], m8[:, 7:8], -1e29, ALU.max, [m8.b], [thr.b])
            else:
                self.memset('dve', thr[:], -1e29, [thr.b])
            stage_a2_finish(qt)

        def stage_b(qt):
            blk = blk_[0]
            sl = qt % 2
            tsl = slice(qt * 128, (qt + 1) * 128)
            q_c, nm = cq_[sl], negm[sl]
            for half in range(2):
                set_i = blk % 2
                blk += 1
                accb = self.ps[3 + set_i]
                views = {j: (accb[:, j * 65:(j + 1) * 65], 0) for j in range(4)}
                tiles = [dict(kT=ckT[:, kt * 128:(kt + 1) * 128], V=cv[:, kt, :], R=[ckT.b, cv.b], c0=0,
                              subs=[0, 1, 2, 3],
                              masks=[(nm[:, kt * 128:(kt + 1) * 128], self.i4[:], 0, 512, [nm.b, self.i4.b])])
                         for kt in range(qt + 1)]
                def epi(accb=accb, o=ost[set_i], tsl=tsl, half=half):
                    av = accb[:, 0:260].rearrange("p (j c) -> p j c", j=4)
                    self.op_('dve', lambda e, av=av: e.reciprocal(out=rec[:], in_=av[:, :, 64]), reads=[accb.b], writes=[rec.b])
                    self.tt('dve', o[:], av[:, :, 0:64], rec[:].unsqueeze(2).to_broadcast([128, 4, 64]), ALU.mult,
                            [accb.b, rec.b], [o.b])
                    self.dma('sp', self.Y[tsl, half * 256:(half + 1) * 256], o[:].rearrange("p j c -> p (j c)"),
                             [o.b], [self.Y.b])
                self.attn_block(q_c[:, half, :], [q_c.b], 512, tiles, 0.125, views, [accb.b], epilogue=epi)
            blk_[0] = blk

        stage_a1(0)
        if NT > 1:
            stage_a1(1)
        stage_a2(0)
        for qt in range(NT):
            if qt + 2 < NT:
                stage_a1(qt + 2)
            if qt + 1 < NT:
                stage_a2(qt + 1)
            if is_act_tile(qt):
                stage_a2_finish(qt)
            stage_b(qt)
        self.attn_flush()


def host_consts(T):
    bf = ml_dtypes.bfloat16
    nb = T // 64
    ncp = max(1, T // 2048) * 128
    n_cmp = (T - 32) // 16 + 1
    c = {}
    c['c_ident'] = np.eye(128, dtype=np.float32).astype(bf)
    c['c_identf'] = np.eye(128, dtype=np.float32)
    c['c_i4'] = np.tile(np.eye(128, dtype=np.float32), (1, 4)).astype(bf)
    i8 = np.zeros((128, 512), np.float32)
    for p in range(128):
        for h in range(8):
            i8[p, h * 64 + p % 64] = 1.0
    c['c_i8x2'] = i8.astype(bf)
    t = np.arange(128)[:, None]
    s = np.arange(128)[None, :]
    c['c_tri_neg'] = np.where(s > t, -BIG, 0.0).astype(np.float32).astype(bf)
    c['c_edge_neg'] = np.where(s <= t, -BIG, 0.0).astype(np.float32).astype(bf)
    c['c_tri01T'] = np.where(t <= s, 1.0, 0.0).astype(np.float32).astype(bf)
    c['c_trinegf'] = np.where(s > t, -1e30, 0.0).astype(np.float32)
    inv = (10000.0 ** (-np.arange(32, dtype=np.float32) / 32)).astype(np.float32)
    c['c_invf'] = np.tile(inv[None, :], (128, 1)).astype(np.float32)
    c['c_pw'] = np.tile((-(2.0 ** -(np.arange(40, dtype=np.float64) + 2)))[None, :], (128, 1)).astype(np.float32)
    tt = np.arange(T)[:, None]
    n = np.arange(ncp)[None, :]
    cm = np.where((16 * n + 31 <= tt) & (n < n_cmp), 0.0, -BIG)
    c['c_cmpneg'] = cm.astype(np.float32).astype(bf)
    j = np.arange(nb)[None, :]
    cur = tt // 64
    forced = np.where(j == cur, 3e4, np.where(j == cur - 1, 2e4, np.where(j == 0, 1e4, 0.0)))
    forced = np.where(j * 64 <= tt, forced, -1e30)
    c['c_forced'] = forced.astype(np.float32)
    ni = np.arange(ncp)[:, None]
    ov = ((16 * ni <= j * 64 + 63) & (16 * ni + 31 >= j * 64) & (ni < n_cmp))
    c['c_overlap'] = ov.astype(np.float32).astype(bf)
    return c


_CACHE = {}


def make_in_maps(inputs, T, ncores):
    NT = T // 128
    consts = host_consts(T)
    wnames = ["ln_g", "mem_norm_g", "mem_w_kv", "mem_q_norm_g", "mem_k_norm_g", "w_out", "even_w_in",
              "mla_q_lat_g", "mla_kv_lat_g", "mla_w_uq", "mla_w_ukv", "mla_q_norm_g", "mla_k_norm_g",
              "nsa_q_norm_g", "nsa_k_norm_g", "nsa_cmp_w1", "nsa_cmp_w2", "odd_w_in", "dsa_q_norm_g",
              "dsa_k_norm_g", "mlstm_conv_b", "mlstm_i_bias", "mlstm_f_bias", "mlstm_h_norm_g"]
    shared = {k: np.ascontiguousarray(np.asarray(inputs[k], dtype=np.float32)) for k in wnames}
    shared["nsa_cmp_posT"] = np.ascontiguousarray(np.transpose(np.asarray(inputs["nsa_cmp_pos"], np.float32), (0, 1, 3, 2)))
    shared["mlstm_conv_wT"] = np.ascontiguousarray(np.transpose(np.asarray(inputs["mlstm_conv_w"], np.float32), (0, 2, 1)))
    shared.update(consts)
    maps = []
    for c in range(ncores):
        m = dict(shared)
        m["x"] = np.ascontiguousarray(np.asarray(inputs["x"][c, :T], np.float32))
        m["mem"] = np.ascontiguousarray(np.asarray(inputs["mem"][c], np.float32))
        pos = np.asarray(inputs["positions"][c, :T]).astype(np.int32)
        m["pos_t"] = np.ascontiguousarray(pos.reshape(NT, 128).T)
        maps.append(m)
    return maps


def kernel(**inputs):
    T = 4096
    key = ('full', T)
    if key not in _CACHE:
        _CACHE[key] = Builder(T, [0, 1, 2, 3]).build()
    nc = _CACHE[key]
    maps = make_in_maps(inputs, T, 8)
    res = run_bass_kernel_spmd(nc, maps, core_ids=list(range(8)))
    out = np.stack([np.asarray(r["out"], dtype=np.float32) for r in res.results], axis=0)
    return out
```

```python
import math
import numpy as np
import ml_dtypes
from contextlib import ExitStack
import concourse.bass as bass
import concourse.mybir as mybir
from concourse.bass_utils import run_bass_kernel_spmd

F32 = mybir.dt.float32
BF16 = mybir.dt.bfloat16
I32 = mybir.dt.int32
AF = mybir.ActivationFunctionType
ALU = mybir.AluOpType
AX = mybir.AxisListType

D = 1024
BIG = 30000.0
EPS = 1e-6
EVEN_COLS = 3256
ODD_COLS = 4016
ENGS = ('pe', 'act', 'dve', 'pool', 'sp')
EPOCH = 16000
NDQ = 8


class Buf:
    __slots__ = ('name', 'w', 'r')

    def __init__(self, name=''):
        self.name = name
        self.w = None
        self.r = {}


class Sched:
    def __init__(self, nc, es):
        self.nc = nc
        self.es = es
        self.prog = {e: [] for e in ENGS}
        self.esem = {e: [] for e in ENGS}
        self.cnt = {e: 0 for e in ENGS}
        self.seen = {e: {} for e in ENGS}
        self.dq = ('sp', 'pool', 'act')
        self.dsem = {q: [es.enter_context(nc.semaphore(f'D{q}{i}')) for i in range(NDQ)] for q in self.dq}
        self.dcnt = {q: 0 for q in self.dq}
        self.ninst = 0

    def _semobj(self, key):
        if key[0] == 'E':
            return self.esem[key[1]][key[2]]
        return self.dsem[key[1]][key[2]]

    def op(self, e, fn, reads=(), writes=(), dma=False):
        deps = {}
        for b in reads:
            if b.w is not None:
                k, v = b.w
                if deps.get(k, 0) < v:
                    deps[k] = v
        for b in writes:
            if b.w is not None:
                k, v = b.w
                if deps.get(k, 0) < v:
                    deps[k] = v
            for k, v in b.r.items():
                if deps.get(k, 0) < v:
                    deps[k] = v
        waits = []
        seen = self.seen[e]
        for k, v in deps.items():
            if e == 'pe' and k[0] == 'E' and k[1] == 'pe':
                continue
            if seen.get(k, 0) >= v:
                continue
            seen[k] = v
            waits.append((self._semobj(k), v))
        if dma:
            j = self.dcnt[e]
            self.dcnt[e] += 1
            slot = j % NDQ
            val = 16 * (j // NDQ + 1)
            key = ('D', e, slot)
            if val > 16 and seen.get(key, 0) < val - 16:
                seen[key] = val - 16
                waits.append((self.dsem[e][slot], val - 16))
            sem = self.dsem[e][slot]
            inc = 16
        else:
            c = self.cnt[e]
            ep = c // EPOCH
            if ep >= len(self.esem[e]):
                self.esem[e].append(self.es.enter_context(self.nc.semaphore(f'S{e}{ep}')))
            self.cnt[e] += 1
            key = ('E', e, ep)
            val = c % EPOCH + 1
            sem = self.esem[e][ep]
            inc = 1
        ev = (key, val)
        self.ninst += 1

        def thunk(eng, waits=waits, fn=fn, sem=sem, inc=inc):
            for s, v in waits:
                eng.wait_ge(s, v)
            fn(eng).then_inc(sem, inc)
        self.prog[e].append(thunk)
        for b in reads:
            if b.r.get(key, 0) < val:
                b.r[key] = val
        for b in writes:
            b.w = ev
            b.r = {}
        return ev

    def barrier(self):
        evs = []
        for e in ENGS:
            c = self.cnt[e]
            if c > 0:
                ep = (c - 1) // EPOCH
                evs.append((('E', e, ep), (c - 1) % EPOCH + 1))
        for q in self.dq:
            n = self.dcnt[q]
            for slot in range(min(n, NDQ)):
                cntslot = (n - 1 - slot) // NDQ + 1
                evs.append((('D', q, slot), 16 * cntslot))
        for e in ENGS:
            waits = []
            for k, v in evs:
                if k[0] == 'E' and k[1] == e:
                    continue
                if self.seen[e].get(k, 0) >= v:
                    continue
                self.seen[e][k] = v
                waits.append((self._semobj(k), v))

            def thunk(eng, waits=waits):
                for s, v in waits:
                    eng.wait_ge(s, v)
            self.prog[e].append(thunk)

    def emit(self):
        nc = self.nc
        with nc.Block() as block:
            @block.tensor
            def _(eng):
                for t in self.prog['pe']:
                    t(eng)

            @block.scalar
            def _(eng):
                for t in self.prog['act']:
                    t(eng)

            @block.vector
            def _(eng):
                for t in self.prog['dve']:
                    t(eng)

            @block.gpsimd
            def _(eng):
                for t in self.prog['pool']:
                    t(eng)

            @block.sync
            def _(eng):
                for t in self.prog['sp']:
                    t(eng)


class Tl:
    __slots__ = ('t', 'b')

    def __init__(self, t, name):
        self.t = t
        self.b = Buf(name)

    def __getitem__(self, k):
        return self.t[k]


class Builder:
    def __init__(self, T, layers, dbg=()):
        self.T = T
        self.NT = T // 128
        self.layers = list(layers)
        self.dbg = set(dbg)
        self.nc = bass.Bass("TRN2", target_bir_lowering=False)
        self.uid = 0
        self.rec = None
        self.dbg_out = {}

    def din(self, name, shape, dt=F32):
        return self.nc.dram_tensor(name, list(shape), dt, kind="ExternalInput").ap()

    def dscr(self, name, shape, dt, nbuf=1):
        kind = "ExternalOutput" if name in self.dbg else "Internal"
        t = self.nc.dram_tensor(name, list(shape), dt, kind=kind).ap()
        tl = Tl(t, name)
        if nbuf > 1:
            tl.b = [Buf(f'{name}{i}') for i in range(nbuf)]
        return tl

    def sb(self, es, name, shape, dt):
        self.uid += 1
        t = es.enter_context(self.nc.sbuf_tensor(f'{name}_{self.uid}', list(shape), dt))
        return Tl(t, name)

    def op_(self, e, fn, reads=(), writes=(), dma=False):
        if self.rec is not None:
            self.rec.append((e, fn, tuple(reads), tuple(writes), dma))
        else:
            self.S.op(e, fn, reads=reads, writes=writes, dma=dma)

    def chains_begin(self, names):
        self._chains = {k: [] for k in names}

    def chain(self, name, scr=None):
        self.rec = self._chains[name]
        self.scr = scr if scr is not None else self.scr0

    def chains_emit(self):
        self.rec = None
        self.scr = self.scr0
        lists = [l for l in self._chains.values() if l]
        idx = [0] * len(lists)
        left = sum(len(l) for l in lists)
        while left:
            for i, l in enumerate(lists):
                if idx[i] < len(l):
                    e, fn, R, W, dma = l[idx[i]]
                    idx[i] += 1
                    left -= 1
                    self.S.op(e, fn, reads=R, writes=W, dma=dma)

    def new_scr(self, es, tag, w):
        sc = {}
        for k in ('sq', 'tmp', 'ra', 'rb'):
            sc[k] = self.sb(es, f'sc_{k}_{tag}', [128, w], F32)
        for k in ('ssq', 'ln', 'rs'):
            sc[k] = self.sb(es, f'sc_{k}_{tag}', [128, 16], F32)
        return sc

    def mm(self, out, lhsT, rhs, start, stop, R, W):
        self.op_('pe', lambda e: e.matmul(out, lhsT=lhsT, rhs=rhs, start=start, stop=stop,
                                           skip_group_check=True), reads=R, writes=W)

    def tr(self, out, in_, ident, R, W):
        self.op_('pe', lambda e: e.transpose(out=out, in_=in_, identity=ident), reads=R, writes=W)

    def act(self, out, in_, func, R, W, bias=None, scale=None, accum=None):
        kw = {}
        if bias is not None:
            kw['bias'] = bias
        if scale is not None:
            kw['scale'] = scale
        if accum is not None:
            kw['accum_out'] = accum
        self.op_('act', lambda e: e.activation(out=out, in_=in_, func=func, **kw), reads=R, writes=W)

    def tt(self, eng, out, in0, in1, op, R, W):
        self.op_(eng, lambda e: e.tensor_tensor(out=out, in0=in0, in1=in1, op=op), reads=R, writes=W)

    def ts(self, eng, out, in0, s1, op0, R, W, s2=None, op1=None, accum=None):
        kw = {}
        if op1 is not None:
            kw['op1'] = op1
        if accum is not None:
            kw['accum_out'] = accum
        self.op_(eng, lambda e: e.tensor_scalar(out=out, in0=in0, scalar1=s1, scalar2=s2, op0=op0, **kw),
                  reads=R, writes=W)

    def stt(self, out, in0, scalar, in1, op0, op1, R, W):
        self.op_('dve', lambda e: e.scalar_tensor_tensor(out=out, in0=in0, scalar=scalar, in1=in1,
                                                         op0=op0, op1=op1), reads=R, writes=W)

    def cp(self, eng, out, in_, R, W):
        if eng == 'act':
            self.op_('act', lambda e: e.copy(out=out, in_=in_), reads=R, writes=W)
        else:
            self.op_(eng, lambda e: e.tensor_copy(out=out, in_=in_), reads=R, writes=W)

    def red(self, out, in_, op, R, W):
        self.op_('dve', lambda e: e.tensor_reduce(out=out, in_=in_, axis=AX.X, op=op), reads=R, writes=W)

    def memset(self, eng, ap, val, W):
        self.op_(eng, lambda e: e.memset(ap, val), writes=W)

    def dma(self, q, out, in_, R, W, **kw):
        self.op_(q, lambda e: e.dma_start(out=out, in_=in_, **kw), reads=R, writes=W, dma=True)

    def bc_load(self, es, name, row_ap, d):
        t = self.sb(es, name, [128, d], F32)
        self.dma('sp', t[:], row_ap.to_broadcast([128, d]), [], [t.b])
        return t

    def wload(self, w, k, src, c0, c1):
        c = c0
        while c < c1:
            ce = min(c1, c + 2048)
            self.dma('pool', w[:, k, c:ce], src[:, c:ce], [], [w.b])
            c = ce

    def rstd(self, ssq, H, d, R):
        ln, rs = self.scr['ln'], self.scr['rs']
        self.act(ln[:, 0:H], ssq, AF.Ln, R, [ln.b], bias=self.eps_c[:, 0:1], scale=1.0 / d)
        self.act(rs[:, 0:H], ln[:, 0:H], AF.Exp, [ln.b], [rs.b], scale=-0.5)
        return rs[:, 0:H]

    def rmsn(self, src, H, d, g, dst, R, W):
        sq, ssq, tmp = self.scr['sq'], self.scr['ssq'], self.scr['tmp']
        sqv = sq[:, 0:H * d].rearrange("p (h c) -> p h c", h=H)
        self.tt('pool', sqv, src, src, ALU.mult, R, [sq.b])
        self.red(ssq[:, 0:H], sqv, ALU.add, [sq.b], [ssq.b])
        rs = self.rstd(ssq[:, 0:H], H, d, [ssq.b])
        tv = tmp[:, 0:H * d].rearrange("p (h c) -> p h c", h=H)
        self.tt('dve', tv, src, rs.unsqueeze(2).to_broadcast([128, H, d]), ALU.mult,
                list(R) + [self.scr['rs'].b], [tmp.b])
        self.tt('pool', dst, tv, g[:].unsqueeze(1).to_broadcast([128, H, d]), ALU.mult,
                [tmp.b, g.b], W)

    def rope(self, src, H, d2, n, dst, R, W):
        tab = self.rope32 if d2 == 32 else self.rope16
        cosv = tab[:, n, 0:d2]
        sinv = tab[:, n, d2:2 * d2]
        A, Bm = self.scr['ra'], self.scr['rb']
        s4 = src.rearrange("p h (two c) -> p h two c", two=2)
        d4 = dst.rearrange("p h (two c) -> p h two c", two=2)
        Av = A[:, 0:H * 2 * d2].rearrange("p (h two c) -> p h two c", h=H, two=2)
        Bv = Bm[:, 0:H * 2 * d2].rearrange("p (h two c) -> p h two c", h=H, two=2)
        cb = cosv.unsqueeze(1).unsqueeze(1).to_broadcast([128, H, 2, d2])
        sbv = sinv.unsqueeze(1).unsqueeze(1).to_broadcast([128, H, 2, d2])
        self.tt('dve', Av, s4, cb, ALU.mult, list(R) + [tab.b], [A.b])
        self.tt('pool', Bv, s4, sbv, ALU.mult, list(R) + [tab.b], [Bm.b])
        self.tt('dve', d4[:, :, 0, :], Av[:, :, 0, :], Bv[:, :, 1, :], ALU.subtract, [A.b, Bm.b], W)
        self.tt('pool', d4[:, :, 1, :], Bv[:, :, 0, :], Av[:, :, 1, :], ALU.add, [A.b, Bm.b], W)

    def attn_block(self, rhs_q, q_R, N, tiles, scale, acc_views, acc_bufs, mode='exp', LA=2, epilogue=None):
        started = set()
        nt = len(tiles)
        last_use = {}
        for i, tl in enumerate(tiles):
            for j in tl['subs']:
                last_use[j] = i
        Atiles = {}

        def emit_s(i):
            tl = tiles[i]
            bi = self.st_rr % len(self.st_banks)
            self.st_rr += 1
            bank, bb = self.st_banks[bi]
            c0 = tl['c0']
            masks = tl.get('masks', [])
            if mode != 'exp':
                E, eR = tl['Efn']()
            self.mm(bank[:, c0:N], tl['kT'], rhs_q[:, c0:N], True, len(masks) == 0, list(q_R) + tl['R'], [bb])
            for mi, (ml, mr, off, ncols, mR) in enumerate(masks):
                self.mm(bank[:, off:off + ncols], ml, mr, False, mi == len(masks) - 1, mR, [bb])
            ai = self.at_rr % len(self.at_tiles)
            self.at_rr += 1
            A = self.at_tiles[ai]
            Atiles[i] = A
            if mode == 'exp':
                self.act(A[:, c0:N], bank[:, c0:N], AF.Exp, [bb], [A.b], scale=scale)
            else:
                self.tt('dve', A[:, c0:N], bank[:, c0:N], E[:, c0:N], ALU.mult, [bb] + eR, [A.b])
                if tl.get('diag') is not None:
                    dj = tl['diag']
                    self.tt('pool', A[:, dj * 128:(dj + 1) * 128], A[:, dj * 128:(dj + 1) * 128],
                            self.tri01T[:], ALU.mult, [A.b, self.tri01T.b], [A.b])

        def emit_pv(i):
            tl = tiles[i]
            A = Atiles.pop(i)
            for j in tl['subs']:
                view, bk = acc_views[j]
                st = bk not in started
                started.add(bk)
                self.mm(view, A[:, j * 128:(j + 1) * 128], tl['V'], st, last_use[j] == i,
                        [A.b] + tl['R'], [acc_bufs[bk]])

        for step in range(nt):
            emit_s(step)
            self.pend.append(('pv', (lambda i=step: emit_pv(i))))
            self.npv += 1
            self._drain(LA)
        if epilogue is None:
            self.attn_flush()
            return None
        self.epi_id += 1
        eid = self.epi_id
        self.pend.append(('epi', epilogue, eid))
        self._drain(LA)
        return eid

    def _drain(self, limit):
        q = self.pend
        while q and (q[0][0] == 'epi' or self.npv > limit):
            it = q.pop(0)
            if it[0] == 'pv':
                self.npv -= 1
                it[1]()
            else:
                it[1]()
                self.epi_done.add(it[2])

    def attn_flush(self):
        self._drain(-1)

    def attn_sync(self, eid):
        while eid is not None and eid not in self.epi_done:
            q = self.pend
            it = q.pop(0)
            if it[0] == 'pv':
                self.npv -= 1
                it[1]()
            else:
                it[1]()
                self.epi_done.add(it[2])

    def build(self):
        nc = self.nc
        T, NT = self.T, self.NT
        with ExitStack() as es:
            self.S = S = Sched(nc, es)
            self._decl_inputs()
            self._decl_scratch()
            self.ps = []
            for i in range(8):
                t = es.enter_context(nc.psum_tensor(f"psb{i}", [128, 512], F32))
                self.ps.append(Tl(t, f'ps{i}'))
            self._consts(es)
            self._prep(es)
            S.barrier()
            xin = Tl(self.x_in, 'xin')
            xin.b = [Buf('xin')] * 1
            cur = xin
            for idx, L in enumerate(self.layers):
                last = idx == len(self.layers) - 1
                dst = self.out_t if last else self.xs[idx % 2]
                with ExitStack() as les:
                    if L % 2 == 0:
                        self.layer_even(les, L, cur, dst)
                    else:
                        self.layer_odd(les, L, cur, dst)
                S.barrier()
                cur = dst
            S.barrier()
            S.emit()
        return nc

    def _decl_inputs(self):
        T, NT = self.T, self.NT
        d = self.din
        self.x_in = d("x", [T, D])
        self.mem = d("mem", [256, D])
        self.pos_t = d("pos_t", [128, NT], I32)
        self.w = {}
        spec = dict(
            ln_g=[4, D], mem_norm_g=[4, D], mem_w_kv=[4, D, 512], mem_q_norm_g=[4, 64], mem_k_norm_g=[4, 64],
            w_out=[4, 1280, D], even_w_in=[2, D, EVEN_COLS], mla_q_lat_g=[2, 256], mla_kv_lat_g=[2, 128],
            mla_w_uq=[2, 256, 768], mla_w_ukv=[2, 128, 1024], mla_q_norm_g=[2, 96], mla_k_norm_g=[2, 96],
            nsa_q_norm_g=[2, 64], nsa_k_norm_g=[2, 3, 64], nsa_cmp_posT=[2, 2, 64, 32],
            nsa_cmp_w1=[2, 2, 2048, 64], nsa_cmp_w2=[2, 2, 64, 64], odd_w_in=[2, D, ODD_COLS],
            dsa_q_norm_g=[2, 64], dsa_k_norm_g=[2, 64], mlstm_conv_wT=[2, 512, 4], mlstm_conv_b=[2, 512],
            mlstm_i_bias=[2, 4], mlstm_f_bias=[2, 4], mlstm_h_norm_g=[2, 128])
        for k, shp in spec.items():
            self.w[k] = d(k, shp)
        ncp = self.ncmp_pad = max(1, T // 2048) * 128
        nb = self.n_blk = T // 64
        self.c = dict(
            ident=d("c_ident", [128, 128], BF16), identf=d("c_identf", [128, 128], F32),
            i4=d("c_i4", [128, 512], BF16), i8x2=d("c_i8x2", [128, 512], BF16),
            tri_neg=d("c_tri_neg", [128, 128], BF16), edge_neg=d("c_edge_neg", [128, 128], BF16),
            tri01T=d("c_tri01T", [128, 128], BF16), trinegf=d("c_trinegf", [128, 128], F32),
            invf=d("c_invf", [128, 32]), pw=d("c_pw", [128, 40]), cmpneg=d("c_cmpneg", [T, ncp], BF16),
            forced=d("c_forced", [T, nb]), overlap=d("c_overlap", [ncp, nb], BF16))

    def _decl_scratch(self):
        T, NT = self.T, self.NT
        s = self.dscr
        self.out_t = Tl(self.nc.dram_tensor("out", [T, D], F32, kind="ExternalOutput").ap(), 'out')
        self.out_t.b = [Buf(f'out{i}') for i in range(NT)]
        self.xs = [s("xs0", [T, D], F32, NT), s("xs1", [T, D], F32, NT)]
        self.rope_d = s("rope_d", [T, 64], F32)
        self.Y = s("Y", [T, 2304], F32)
        self.G = s("G", [T, 1792], BF16, NT)
        self.SG = s("SG", [T, 32], F32, NT)
        self.mla_qT = s("mla_qT", [8, 96, T], BF16)
        self.mla_kT = s("mla_kT", [8, 96, T], BF16)
        self.mla_v = s("mla_v", [8, 128, NT, 65], BF16)
        self.nsa_qT = s("nsa_qT", [2, NT, 64, 4, 128], BF16)
        self.nsa_kT = s("nsa_kT", [8, 64, T], BF16)
        self.nsa_v = s("nsa_v", [4, 128, NT, 65], BF16)
        self.mem_qT = s("mem_qT", [4, 64, T], BF16)
        self.dsa_qT = s("dsa_qT", [NT, 64, 8, 128], BF16)
        self.dsa_kT = s("dsa_kT", [64, T], BF16)
        self.dsa_v = s("dsa_v", [128, NT, 65], BF16)
        self.idx_qT = s("idx_qT", [8, 32, T], BF16)
        self.idx_kT = s("idx_kT", [32, T], BF16)
        self.idx_w = s("idx_w", [T, 8], F32)
        self.ml_raw = s("ml_raw", [512, T], F32)
        self.ml_if = s("ml_if", [8, T], F32)
        self.ml_qkT = s("ml_qkT", [512, T], BF16)
        self.ml_v = s("ml_v", [4, 128, NT, 129], BF16)
        self.ml_g = s("ml_g", [12, T], F32)

    def _consts(self, es):
        c = self.c
        def ld(name, shape, dt):
            t = self.sb(es, name, shape, dt)
            self.dma('sp', t[:], c[name], [], [t.b])
            return t
        self.ident = ld('ident', [128, 128], BF16)
        self.identf = ld('identf', [128, 128], F32)
        self.i4 = ld('i4', [128, 512], BF16)
        self.i8x2 = ld('i8x2', [128, 512], BF16)
        self.tri_neg = ld('tri_neg', [128, 128], BF16)
        self.edge_neg = ld('edge_neg', [128, 128], BF16)
        self.tri01T = ld('tri01T', [128, 128], BF16)
        self.trinegf = ld('trinegf', [128, 128], F32)
        self.invf = ld('invf', [128, 32], F32)
        self.pw = ld('pw', [128, 40], F32)
        self.eps_c = self.sb(es, 'eps_c', [128, 1], F32)
        self.memset('dve', self.eps_c[:], EPS, [self.eps_c.b])
        self.one_c = self.sb(es, 'one_c', [128, 1], F32)
        self.memset('dve', self.one_c[:], 1.0, [self.one_c.b])
        self.rope32 = self.sb(es, 'rope32', [128, self.NT, 64], F32)
        self.rope16 = self.sb(es, 'rope16', [128, self.NT, 32], F32)
        self.sc_sq = self.sb(es, 'sc_sq', [128, 512], F32)
        self.sc_tmp = self.sb(es, 'sc_tmp', [128, 512], F32)
        self.sc_ra = self.sb(es, 'sc_ra', [128, 512], F32)
        self.sc_rb = self.sb(es, 'sc_rb', [128, 512], F32)
        self.sc_ssq = self.sb(es, 'sc_ssq', [128, 16], F32)
        self.sc_ln = self.sb(es, 'sc_ln', [128, 16], F32)
        self.sc_rs = self.sb(es, 'sc_rs', [128, 16], F32)
        self.scr0 = dict(sq=self.sc_sq, tmp=self.sc_tmp, ra=self.sc_ra, rb=self.sc_rb, ssq=self.sc_ssq,
                         ln=self.sc_ln, rs=self.sc_rs)
        self.scr = self.scr0

    def _prep(self, es0):
        NT = self.NT
        PI = math.pi
        with ExitStack() as es:
            pi_t = self.sb(es, 'pos_i', [128, NT], I32)
            self.dma('sp', pi_t[:], self.pos_t, [], [pi_t.b])
            pf = self.sb(es, 'pos_f', [128, NT], F32)
            self.cp('dve', pf[:], pi_t[:], [pi_t.b], [pf.b])
            ang = self.sb(es, 'ang', [128, NT, 32], F32)
            self.tt('dve', ang[:], pf[:].unsqueeze(2).to_broadcast([128, NT, 32]),
                    self.invf[:].unsqueeze(1).to_broadcast([128, NT, 32]), ALU.mult,
                    [pf.b, self.invf.b], [ang.b])
            kf = self.sb(es, 'kf', [128, NT, 32], F32)
            ki = self.sb(es, 'ki', [128, NT, 32], I32)
            r = self.sb(es, 'r', [128, NT, 32], F32)
            m = self.sb(es, 'm', [128, NT, 32], F32)

            def wrap(buf):
                self.ts('dve', m[:], buf[:], PI, ALU.is_gt, [buf.b], [m.b], s2=-2 * PI, op1=ALU.mult)
                self.tt('dve', buf[:], buf[:], m[:], ALU.add, [buf.b, m.b], [buf.b])
                self.ts('dve', m[:], buf[:], -PI, ALU.is_lt, [buf.b], [m.b], s2=2 * PI, op1=ALU.mult)
                self.tt('dve', buf[:], buf[:], m[:], ALU.add, [buf.b, m.b], [buf.b])
                self.ts('dve', buf[:], buf[:], PI, ALU.min, [buf.b], [buf.b], s2=-PI, op1=ALU.max)
            self.ts('dve', kf[:], ang[:], 1.0 / (2 * PI), ALU.mult, [ang.b], [kf.b])
            self.cp('dve', ki[:], kf[:], [kf.b], [ki.b])
            self.cp('dve', kf[:], ki[:], [ki.b], [kf.b])
            C1 = 6.28125
            C2 = 2 * PI - C1
            self.stt(r[:], kf[:], -C1, ang[:], ALU.mult, ALU.add, [kf.b, ang.b], [r.b])
            self.stt(r[:], kf[:], -C2, r[:], ALU.mult, ALU.add, [kf.b, r.b], [r.b])
            wrap(r)
            self.act(self.rope32[:, :, 32:64], r[:], AF.Sin, [r.b], [self.rope32.b])
            self.ts('dve', r[:], r[:], PI / 2, ALU.add, [r.b], [r.b])
            wrap(r)
            self.act(self.rope32[:, :, 0:32], r[:], AF.Sin, [r.b], [self.rope32.b])
            self.cp('dve', self.rope16[:, :, 0:16], self.rope32[:, :, 0:32:2], [self.rope32.b], [self.rope16.b])
            self.cp('dve', self.rope16[:, :, 16:32], self.rope32[:, :, 32:64:2], [self.rope32.b], [self.rope16.b])
            self.dma('sp', self.rope_d[:].rearrange("(n p) c -> p n c", p=128), self.rope32[:],
                     [self.rope32.b], [self.rope_d.b])
            self.S.barrier()

    def load_win(self, es, L, w_in, ncols):
        win = self.sb(es, 'win', [128, 8, ncols], BF16)
        for k in range(8):
            self.wload(win, k, w_in[k * 128:(k + 1) * 128, :], 0, ncols)
        lng = self.bc_load(es, 'lng', self.w['ln_g'][L:L + 1, :], D)
        return win, lng

    def load_wout(self, es, L):
        wout = self.sb(es, 'wout', [128, 10, D], BF16)
        for k in range(10):
            self.wload(wout, k, self.w['w_out'][L, k * 128:(k + 1) * 128, :], 0, D)
        return wout

    def mem_kv(self, es, L):
        w = self.w
        kT = self.sb(es, 'memkT', [64, 4, 256], BF16)
        V = self.sb(es, 'memV', [128, 2, 4, 65], BF16)
        self.memset('pool', V[:], 1.0, [V.b])
        with ExitStack() as s2:
            wkv = self.sb(s2, 'wkv', [128, 8, 512], BF16)
            for k in range(8):
                self.wload(wkv, k, w['mem_w_kv'][L, k * 128:(k + 1) * 128, :], 0, 512)
            mg = self.bc_load(s2, 'mg', w['mem_norm_g'][L:L + 1, :], D)
            kg = self.bc_load(s2, 'kg', w['mem_k_norm_g'][L:L + 1, :], 64)
            mt = self.sb(s2, 'mt', [128, D], F32)
            junk = self.sb(s2, 'junk', [128, D], BF16)
            mh = self.sb(s2, 'mh', [128, D], BF16)
            mhT = self.sb(s2, 'mhT', [128, 8, 128], BF16)
            kv = self.sb(s2, 'kv', [128, 512], F32)
            kn = self.sb(s2, 'kn', [128, 256], BF16)
            ss = self.sb(s2, 'ss', [128, 1], F32)
            for i in range(2):
                self.dma('sp', mt[:], self.mem[i * 128:(i + 1) * 128, :], [], [mt.b])
                self.act(junk[:], mt[:], AF.Square, [mt.b], [junk.b, ss.b], accum=ss[:])
                rs = self.rstd(ss[:, 0:1], 1, D, [ss.b])
                self.stt(mh[:], mt[:], rs, mg[:], ALU.mult, ALU.mult, [mt.b, self.scr['rs'].b, mg.b], [mh.b])
                pT = self.ps[2]
                pTb = pT[:].bitcast(BF16)
                for k in range(8):
                    self.tr(pTb[:, k * 128:(k + 1) * 128], mh[:, k * 128:(k + 1) * 128], self.ident[:],
                            [mh.b, self.ident.b], [pT.b])
                self.cp('act', mhT[:].rearrange("p k c -> p (k c)"), pTb[:, 0:1024], [pT.b], [mhT.b])
                pU = self.ps[0]
                for k in range(8):
                    self.mm(pU[:, 0:512], mhT[:, k, :], wkv[:, k, :], k == 0, k == 7, [mhT.b, wkv.b], [pU.b])
                self.cp('act', kv[:], pU[:, 0:512], [pU.b], [kv.b])
                self.rmsn(kv[:, 0:256].rearrange("p (h c) -> p h c", h=4), 4, 64, kg,
                          kn[:].rearrange("p (h c) -> p h c", h=4), [kv.b], [kn.b])
                self.cp('dve', V[:, i, :, 0:64], kv[:, 256:512].rearrange("p (h c) -> p h c", h=4), [kv.b], [V.b])
                pK = self.ps[3]
                pKb = pK[:].bitcast(BF16)
                for h in range(4):
                    self.tr(pKb[0:64, h * 128:(h + 1) * 128], kn[:, h * 64:(h + 1) * 64], self.ident[:],
                            [kn.b, self.ident.b], [pK.b])
                self.cp('act', kT[:, :, i * 128:(i + 1) * 128],
                        pKb[0:64, 0:512].rearrange("p (h c) -> p h c", h=4), [pK.b], [kT.b])
            self.S.barrier()
        return kT, V

    def p1_front(self, n, x_src, xt, ht, hT, u, ss, junk, lng, win, ncols):
        sl = n % 2
        x_t, h_t, hT_t, u_t = xt[sl], ht[sl], hT[sl], u[sl]
        xb = x_src.b[n] if len(x_src.b) > 1 else x_src.b[0]
        self.dma('sp', x_t[:], x_src[n * 128:(n + 1) * 128, :], [xb], [x_t.b])
        self.act(junk[:], x_t[:], AF.Square, [x_t.b], [junk.b, ss.b], accum=ss[:])
        rs = self.rstd(ss[:, 0:1], 1, D, [ss.b])
        self.stt(h_t[:], x_t[:], rs, lng[:], ALU.mult, ALU.mult, [x_t.b, self.scr['rs'].b, lng.b], [h_t.b])
        pT = self.ps[2]
        pTb = pT[:].bitcast(BF16)
        for k in range(8):
            self.tr(pTb[:, k * 128:(k + 1) * 128], h_t[:, k * 128:(k + 1) * 128], self.ident[:],
                    [h_t.b, self.ident.b], [pT.b])
        self.cp('act', hT_t[:].rearrange("p k c -> p (k c)"), pTb[:, 0:1024], [pT.b], [hT_t.b])
        nchunk = (ncols + 511) // 512
        for c in range(nchunk):
            c0 = c * 512
            wd = min(512, ncols - c0)
            pU = self.ps[c % 2]
            for k in range(8):
                self.mm(pU[:, 0:wd], hT_t[:, k, :], win[:, k, c0:c0 + wd], k == 0, k == 7,
                        [hT_t.b, win.b], [pU.b])
            self.cp('act' if c % 2 == 0 else 'dve', u_t[:, c0:c0 + wd], pU[:, 0:wd], [pU.b], [u_t.b])
        return u_t

    def transposes_out(self, srcs, rows, stage, dst_ap, dst_b, pidx):
        pT = self.ps[pidx]
        pTb = pT[:].bitcast(BF16)
        k = len(srcs)
        for i, (ap, R) in enumerate(srcs):
            self.tr(pTb[0:rows, i * 128:(i + 1) * 128], ap, self.ident[:], list(R) + [self.ident.b], [pT.b])
        self.cp('act', stage[0:rows, 0:k, :], pTb[0:rows, 0:k * 128].rearrange("p (k c) -> p k c", k=k),
                [pT.b], [stage.b])
        self.dma('sp', dst_ap, stage[0:rows, 0:k, :], [stage.b], dst_b)

    def p3(self, es, L, x_src, x_dst, wout, mixfn):
        NT = self.NT
        xt = [self.sb(es, f'p3x{i}', [128, D], F32) for i in range(2)]
        mixT = [self.sb(es, f'p3mT{i}', [128, 10, 128], BF16) for i in range(2)]
        xo = [self.sb(es, f'p3o{i}', [128, D], F32) for i in range(2)]
        for n in range(NT):
            sl = n % 2
            mix = mixfn(n)
            xb = x_src.b[n] if len(x_src.b) > 1 else x_src.b[0]
            self.dma('sp', xt[sl][:], x_src[n * 128:(n + 1) * 128, :], [xb], [xt[sl].b])
            for half in range(2):
                pT = self.ps[2 + half]
                pTb = pT[:].bitcast(BF16)
                for k in range(5):
                    kk = half * 5 + k
                    self.tr(pTb[:, k * 128:(k + 1) * 128], mix[:, kk * 128:(kk + 1) * 128], self.ident[:],
                            [mix.b, self.ident.b], [pT.b])
                self.cp('act', mixT[sl][:, half * 5:half * 5 + 5, :].rearrange("p k c -> p (k c)"),
                        pTb[:, 0:640], [pT.b], [mixT[sl].b])
            for c in range(2):
                pU = self.ps[c]
                for k in range(10):
                    self.mm(pU[:, 0:512], mixT[sl][:, k, :], wout[:, k, c * 512:(c + 1) * 512], k == 0, k == 9,
                            [mixT[sl].b, wout.b], [pU.b])
                self.tt('dve', xo[sl][:, c * 512:(c + 1) * 512], pU[:, 0:512], xt[sl][:, c * 512:(c + 1) * 512],
                        ALU.add, [pU.b, xt[sl].b], [xo[sl].b])
            self.dma('sp', x_dst[n * 128:(n + 1) * 128, :], xo[sl][:], [xo[sl].b], [x_dst.b[n]])

    def attn_setup(self, es, dvp_two_banks=False):
        self.st_banks = [(self.ps[i], self.ps[i].b) for i in range(3)]
        self.st_rr = 0
        self.at_tiles = [self.sb(es, f'At{i}', [128, 512], BF16) for i in range(3)]
        self.at_rr = 0
        self.pend = []
        self.npv = 0
        self.epi_id = 0
        self.epi_done = set()

    def mem_attn(self, es, memkT, memV):
        T = self.T
        qT = [self.sb(es, f'mqT{i}', [64, T], BF16) for i in range(2)]
        ost = [self.sb(es, f'most{i}', [128, 4, 64], F32) for i in range(2)]
        rec = self.sb(es, 'mrec', [128, 4], F32)
        blk = 0
        for h in range(4):
            q = qT[h % 2]
            self.dma('sp', q[:], self.mem_qT[h], [self.mem_qT.b], [q.b])
            for cq in range(T // 512):
                set_i = blk % 2
                blk += 1
                accb = self.ps[3 + set_i]
                views = {j: (accb[:, j * 65:(j + 1) * 65], 0) for j in range(4)}
                tiles = [dict(kT=memkT[:, h, kt * 128:(kt + 1) * 128], V=memV[:, kt, h, :],
                              R=[memkT.b, memV.b], c0=0, subs=[0, 1, 2, 3]) for kt in range(2)]
                def epi(accb=accb, o=ost[set_i], cq=cq, h=h):
                    av = accb[:, 0:260].rearrange("p (j c) -> p j c", j=4)
                    self.op_('dve', lambda e, av=av: e.reciprocal(out=rec[:], in_=av[:, :, 64]), reads=[accb.b],
                             writes=[rec.b])
                    self.tt('dve', o[:], av[:, :, 0:64], rec[:].unsqueeze(2).to_broadcast([128, 4, 64]), ALU.mult,
                            [accb.b, rec.b], [o.b])
                    self.dma('sp', self.Y[cq * 512:(cq + 1) * 512, 1024 + h * 64:1024 + (h + 1) * 64]
                             .rearrange("(j p) c -> p j c", p=128), o[:], [o.b], [self.Y.b])
                self.attn_block(q[:, cq * 512:(cq + 1) * 512], [q.b], 512, tiles, 0.125, views, [accb.b], epilogue=epi)
        self.attn_flush()

    def layer_even(self, es, L, x_src, x_dst):
        li = L // 2
        w = self.w
        T, NT = self.T, self.NT
        S = self.S
        memkT, memV = self.mem_kv(es, L)
        with ExitStack() as p1:
            win, lng = self.load_win(p1, L, w['even_w_in'][li], EVEN_COLS)
            wuq = self.sb(p1, 'wuq', [128, 2, 768], BF16)
            for k in range(2):
                self.wload(wuq, k, w['mla_w_uq'][li, k * 128:(k + 1) * 128, :], 0, 768)
            wukv = self.sb(p1, 'wukv', [128, 1, 1024], BF16)
            self.wload(wukv, 0, w['mla_w_ukv'][li], 0, 1024)
            g_ql = self.bc_load(p1, 'g_ql', w['mla_q_lat_g'][li:li + 1, :], 256)
            g_kvl = self.bc_load(p1, 'g_kvl', w['mla_kv_lat_g'][li:li + 1, :], 128)
            g_qn = self.bc_load(p1, 'g_qn', w['mla_q_norm_g'][li:li + 1, 0:64], 64)
            g_qp = self.bc_load(p1, 'g_qp', w['mla_q_norm_g'][li:li + 1, 64:96], 32)
            g_kn = self.bc_load(p1, 'g_kn', w['mla_k_norm_g'][li:li + 1, 0:64], 64)
            g_kp = self.bc_load(p1, 'g_kp', w['mla_k_norm_g'][li:li + 1, 64:96], 32)
            g_bq = self.bc_load(p1, 'g_bq', w['nsa_q_norm_g'][li:li + 1, :], 64)
            g_ks = self.bc_load(p1, 'g_ks', w['nsa_k_norm_g'][li, 1:2, :], 64)
            g_kw = self.bc_load(p1, 'g_kw', w['nsa_k_norm_g'][li, 2:3, :], 64)
            g_mq = self.bc_load(p1, 'g_mq', w['mem_q_norm_g'][L:L + 1, :], 64)
            xt = [self.sb(p1, f'xt{i}', [128, D], F32) for i in range(2)]
            ht = [self.sb(p1, f'ht{i}', [128, D], BF16) for i in range(2)]
            hT = [self.sb(p1, f'hT{i}', [128, 8, 128], BF16) for i in range(2)]
            u = [self.sb(p1, f'u{i}', [128, EVEN_COLS], F32) for i in range(2)]
            ss = self.sb(p1, 'ss', [128, 1], F32)
            junk = self.sb(p1, 'junk', [128, D], BF16)
            latn = self.sb(p1, 'latn', [128, 384], BF16)
            latT = self.sb(p1, 'latT', [128, 3, 128], BF16)
            qsb = self.sb(p1, 'qsb', [128, 768], F32)
            kvsb = self.sb(p1, 'kvsb', [128, 1024], F32)
            qpe = self.sb(p1, 'qpe', [128, 256], F32)
            kpe = self.sb(p1, 'kpe', [128, 32], F32)
            kpeb = self.sb(p1, 'kpeb', [128, 32], BF16)
            qf = self.sb(p1, 'qf', [128, 8, 96], BF16)
            kfm = self.sb(p1, 'kfm', [128, 8, 96], BF16)
            vaug = self.sb(p1, 'vaug', [128, 8, 65], BF16)
            self.memset('pool', vaug[:], 1.0, [vaug.b])
            bqn = self.sb(p1, 'bqn', [128, 512], F32)
            bqf = self.sb(p1, 'bqf', [128, 512], BF16)
            kn2 = self.sb(p1, 'kn2', [128, 128], F32)
            kmisc = self.sb(p1, 'kmisc', [128, 8, 64], BF16)
            nv = self.sb(p1, 'nv', [128, 4, 65], BF16)
            self.memset('pool', nv[:], 1.0, [nv.b])
            mqf = self.sb(p1, 'mqf', [128, 256], BF16)
            gt = self.sb(p1, 'gt', [128, 1280], BF16)
            sg = self.sb(p1, 'sg', [128, 24], F32)
            stq = self.sb(p1, 'stq', [96, 8, 128], BF16)
            stk = self.sb(p1, 'stk', [96, 8, 128], BF16)
            stb = self.sb(p1, 'stb', [64, 8, 128], BF16)
            stm = self.sb(p1, 'stm', [64, 8, 128], BF16)
            stmq = self.sb(p1, 'stmq', [64, 4, 128], BF16)
            scrA = self.new_scr(p1, 'A', 512)
            scrB = self.new_scr(p1, 'B', 512)
            scrC = self.new_scr(p1, 'C', 128)
            for n in range(NT):
                ut = self.p1_front(n, x_src, xt, ht, hT, u, ss, junk, lng, win, EVEN_COLS)
                U = lambda a, b_: ut[:, a:b_]
                ub = [ut.b]
                tsl = slice(n * 128, (n + 1) * 128)
                self.chains_begin(['A', 'B', 'C', 'M', 'G'])
                self.chain('G')
                self.act(gt[:, 0:512], U(416, 928), AF.Silu, ub, [gt.b])
                self.act(gt[:, 512:1024], U(2232, 2744), AF.Silu, ub, [gt.b])
                self.act(gt[:, 1024:1280], U(3000, 3256), AF.Silu, ub, [gt.b])
                self.dma('sp', self.G[tsl, 0:1280], gt[:], [gt.b], [self.G.b[n]])
                self.act(sg[:], U(2208, 2232), AF.Sigmoid, ub, [sg.b])
                self.dma('sp', self.SG[tsl, 0:24], sg[:], [sg.b], [self.SG.b[n]])
                self.chain('A', scrA)
                self.rmsn(U(0, 256).rearrange("p (h c) -> p h c", h=1), 1, 256, g_ql,
                          latn[:, 0:256].rearrange("p (h c) -> p h c", h=1), ub, [latn.b])
                self.rmsn(U(256, 384).rearrange("p (h c) -> p h c", h=1), 1, 128, g_kvl,
                          latn[:, 256:384].rearrange("p (h c) -> p h c", h=1), ub, [latn.b])
                pT = self.ps[3]
                pTb = pT[:].bitcast(BF16)
                for k in range(3):
                    self.tr(pTb[:, k * 128:(k + 1) * 128], latn[:, k * 128:(k + 1) * 128], self.ident[:],
                            [latn.b, self.ident.b], [pT.b])
                self.cp('act', latT[:].rearrange("p k c -> p (k c)"), pTb[:, 0:384], [pT.b], [latT.b])
                pQ, pQ2, pK, pK2 = self.ps[3], self.ps[4], self.ps[5], self.ps[4]
                for k in range(2):
                    self.mm(pQ[:, 0:512], latT[:, k, :], wuq[:, k, 0:512], k == 0, k == 1, [latT.b, wuq.b], [pQ.b])
                for k in range(2):
                    self.mm(pQ2[:, 0:256], latT[:, k, :], wuq[:, k, 512:768], k == 0, k == 1, [latT.b, wuq.b], [pQ2.b])
                self.cp('act', qsb[:, 0:512], pQ[:, 0:512], [pQ.b], [qsb.b])
                self.cp('dve', qsb[:, 512:768], pQ2[:, 0:256], [pQ2.b], [qsb.b])
                self.mm(pK[:, 0:512], latT[:, 2, :], wukv[:, 0, 0:512], True, True, [latT.b, wukv.b], [pK.b])
                self.mm(pK2[:, 0:512], latT[:, 2, :], wukv[:, 0, 512:1024], True, True, [latT.b, wukv.b], [pK2.b])
                self.cp('act', kvsb[:, 0:512], pK[:, 0:512], [pK.b], [kvsb.b])
                self.cp('dve', kvsb[:, 512:1024], pK2[:, 0:512], [pK2.b], [kvsb.b])
                q3 = qsb[:].rearrange("p (h c) -> p h c", h=8)
                kv3 = kvsb[:].rearrange("p (h c) -> p h c", h=8)
                self.rmsn(q3[:, :, 0:64], 8, 64, g_qn, qf[:, :, 0:64], [qsb.b], [qf.b])
                qpe3 = qpe[:].rearrange("p (h c) -> p h c", h=8)
                self.rmsn(q3[:, :, 64:96], 8, 32, g_qp, qpe3, [qsb.b], [qpe.b])
                self.rope(qpe3, 8, 16, n, qf[:, :, 64:96], [qpe.b], [qf.b])
                self.rmsn(kv3[:, :, 0:64], 8, 64, g_kn, kfm[:, :, 0:64], [kvsb.b], [kfm.b])
                self.cp('dve', vaug[:, :, 0:64], kv3[:, :, 64:128], [kvsb.b], [vaug.b])
                kpe3 = kpe[:].rearrange("p (h c) -> p h c", h=1)
                self.rmsn(U(384, 416).rearrange("p (h c) -> p h c", h=1), 1, 32, g_kp, kpe3, ub, [kpe.b])
                self.rope(kpe3, 1, 16, n, kpeb[:].rearrange("p (h c) -> p h c", h=1), [kpe.b], [kpeb.b])
                self.cp('pool', kfm[:, :, 64:96], kpeb[:].unsqueeze(1).to_broadcast([128, 8, 32]), [kpeb.b], [kfm.b])
                self.transposes_out([(qf[:, h, :], [qf.b]) for h in range(8)], 96, stq,
                                    self.mla_qT[:, :, tsl].rearrange("h d t -> d h t"), [self.mla_qT.b], 3)
                self.transposes_out([(kfm[:, h, :], [kfm.b]) for h in range(8)], 96, stk,
                                    self.mla_kT[:, :, tsl].rearrange("h d t -> d h t"), [self.mla_kT.b], 5)
                self.dma('sp', self.mla_v[:, :, n, :].rearrange("h p c -> p h c"), vaug[:], [vaug.b], [self.mla_v.b])
                self.chain('B', scrB)
                bq3 = bqn[:].rearrange("p (h c) -> p h c", h=8)
                self.rmsn(U(928, 1440).rearrange("p (h c) -> p h c", h=8), 8, 64, g_bq, bq3, ub, [bqn.b])
                self.rope(bq3, 8, 32, n, bqf[:].rearrange("p (h c) -> p h c", h=8), [bqn.b], [bqf.b])
                self.chain('C', scrC)
                k23 = kn2[:].rearrange("p (h c) -> p h c", h=2)
                self.rmsn(U(1696, 1824).rearrange("p (h c) -> p h c", h=2), 2, 64, g_ks, k23, ub, [kn2.b])
                self.rope(k23, 2, 32, n, kmisc[:, 0:2, :], [kn2.b], [kmisc.b])
                self.rmsn(U(1952, 2080).rearrange("p (h c) -> p h c", h=2), 2, 64, g_kw, k23, ub, [kn2.b])
                self.rope(k23, 2, 32, n, kmisc[:, 2:4, :], [kn2.b], [kmisc.b])
                self.cp('pool', kmisc[:, 4:8, :], U(1440, 1696).rearrange("p (h c) -> p h c", h=4), ub, [kmisc.b])
                self.cp('dve', nv[:, 0:2, 0:64], U(1824, 1952).rearrange("p (h c) -> p h c", h=2), ub, [nv.b])
                self.cp('dve', nv[:, 2:4, 0:64], U(2080, 2208).rearrange("p (h c) -> p h c", h=2), ub, [nv.b])
                self.chain('B', scrB)
                self.transposes_out([(bqf[:, h * 64:(h + 1) * 64], [bqf.b]) for h in range(8)], 64, stb,
                                    self.nsa_qT[:, n].rearrange("g d r t -> d g r t"), [self.nsa_qT.b], 6)
                self.chain('C', scrC)
                self.transposes_out([(kmisc[:, i, :], [kmisc.b]) for i in range(8)], 64, stm,
                                    self.nsa_kT[:, :, tsl].rearrange("k d t -> d k t"), [self.nsa_kT.b], 7)
                self.dma('sp', self.nsa_v[:, :, n, :].rearrange("k p c -> p k c"), nv[:], [nv.b], [self.nsa_v.b])
                self.chain('B', scrB)
                self.rmsn(U(2744, 3000).rearrange("p (h c) -> p h c", h=4), 4, 64, g_mq,
                          mqf[:].rearrange("p (h c) -> p h c", h=4), ub, [mqf.b])
                self.transposes_out([(mqf[:, h * 64:(h + 1) * 64], [mqf.b]) for h in range(4)], 64, stmq,
                                    self.mem_qT[:, :, tsl].rearrange("h d t -> d h t"), [self.mem_qT.b], 6)
                self.chains_emit()
            S.barrier()
        kcmpT = self.sb(es, 'kcmpT', [64, 2, self.ncmp_pad], BF16)
        vcmp = self.sb(es, 'vcmp', [128, 2, self.ncmp_pad // 128, 129], BF16)
        self.nsa_compress(li, kcmpT, vcmp)
        S.barrier()
        with ExitStack() as pa:
            self.attn_setup(pa)
            self.mla_attn(pa)
            S.barrier()
        with ExitStack() as pa:
            self.attn_setup(pa)
            self.mem_attn(pa, memkT, memV)
            S.barrier()
        with ExitStack() as pa:
            self.attn_setup(pa)
            self.nsa_attn(pa, kcmpT, vcmp)
            S.barrier()
        with ExitStack() as p3:
            wout = self.load_wout(p3, L)
            yt = [self.sb(p3, f'yt{i}', [128, 2304], F32) for i in range(2)]
            gtt = [self.sb(p3, f'gtt{i}', [128, 1280], BF16) for i in range(2)]
            sgt = [self.sb(p3, f'sgt{i}', [128, 24], F32) for i in range(2)]
            mix = [self.sb(p3, f'mix{i}', [128, 1280], BF16) for i in range(2)]
            yb = self.sb(p3, 'yb', [128, 512], F32)
            yb2 = self.sb(p3, 'yb2', [128, 512], F32)

            def mixfn(n):
                sl = n % 2
                y, g, s_, m = yt[sl], gtt[sl], sgt[sl], mix[sl]
                tsl = slice(n * 128, (n + 1) * 128)
                self.dma('sp', y[:], self.Y[tsl, :], [self.Y.b], [y.b])
                self.dma('sp', g[:], self.G[tsl, 0:1280], [self.G.b[n]], [g.b])
                self.dma('sp', s_[:], self.SG[tsl, 0:24], [self.SG.b[n]], [s_.b])
                self.tt('dve', m[:, 0:512], y[:, 0:512], g[:, 0:512], ALU.mult, [y.b, g.b], [m.b])
                self.tt('pool', m[:, 1024:1280], y[:, 1024:1280], g[:, 1024:1280], ALU.mult, [y.b, g.b], [m.b])
                s3 = s_[:].rearrange("p (h c) -> p h c", c=3)
                y3 = lambda a: y[:, a:a + 512].rearrange("p (h c) -> p h c", h=8)
                b3 = yb[:].rearrange("p (h c) -> p h c", h=8)
                b23 = yb2[:].rearrange("p (h c) -> p h c", h=8)
                self.tt('dve', b3, y3(1280), s3[:, :, 0:1].to_broadcast([128, 8, 64]), ALU.mult, [y.b, s_.b], [yb.b])
                self.tt('pool', b23, y3(512), s3[:, :, 1:2].to_broadcast([128, 8, 64]), ALU.mult, [y.b, s_.b], [yb2.b])
                self.tt('dve', b3, b3, b23, ALU.add, [yb.b, yb2.b], [yb.b])
                self.tt('pool', b23, y3(1792), s3[:, :, 2:3].to_broadcast([128, 8, 64]), ALU.mult, [y.b, s_.b], [yb2.b])
                self.tt('dve', b3, b3, b23, ALU.add, [yb.b, yb2.b], [yb.b])
                self.tt('dve', m[:, 512:1024], yb[:], g[:, 512:1024], ALU.mult, [yb.b, g.b], [m.b])
                return m
            self.p3(p3, L, x_src, x_dst, wout, mixfn)
            S.barrier()

    def mla_attn(self, es):
        T, NT = self.T, self.NT
        qT = [self.sb(es, f'aqT{i}', [96, T], BF16) for i in range(2)]
        kT = [self.sb(es, f'akT{i}', [96, T], BF16) for i in range(2)]
        V = [self.sb(es, f'aV{i}', [128, NT, 65], BF16) for i in range(2)]
        ost = [self.sb(es, f'aost{i}', [128, 4, 64], F32) for i in range(2)]
        rec = self.sb(es, 'arec', [128, 4], F32)
        scale = 96 ** -0.5
        blk = 0
        for h in range(8):
            q, k, v = qT[h % 2], kT[h % 2], V[h % 2]
            self.dma('sp', q[:], self.mla_qT[h], [self.mla_qT.b], [q.b])
            self.dma('sp', k[:], self.mla_kT[h], [self.mla_kT.b], [k.b])
            self.dma('sp', v[:], self.mla_v[h], [self.mla_v.b], [v.b])
            for cq in range(T // 512):
                set_i = blk % 2
                blk += 1
                accb = self.ps[3 + set_i]
                views = {j: (accb[:, j * 65:(j + 1) * 65], 0) for j in range(4)}
                tiles = []
                for kt in range(4 * cq + 4):
                    vv = kt - 4 * cq
                    tl = dict(kT=k[:, kt * 128:(kt + 1) * 128], V=v[:, kt, :], R=[k.b, v.b],
                              c0=max(0, vv) * 128, subs=list(range(max(0, vv), 4)))
                    if vv >= 0:
                        tl['masks'] = [(self.tri_neg[:], self.ident[:], vv * 128, 128,
                                        [self.tri_neg.b, self.ident.b])]
                    tiles.append(tl)
                def epi(accb=accb, o=ost[set_i], cq=cq, h=h):
                    av = accb[:, 0:260].rearrange("p (j c) -> p j c", j=4)
                    self.op_('dve', lambda e, av=av: e.reciprocal(out=rec[:], in_=av[:, :, 64]), reads=[accb.b],
                             writes=[rec.b])
                    self.tt('dve', o[:], av[:, :, 0:64], rec[:].unsqueeze(2).to_broadcast([128, 4, 64]), ALU.mult,
                            [accb.b, rec.b], [o.b])
                    self.dma('sp', self.Y[cq * 512:(cq + 1) * 512, h * 64:(h + 1) * 64]
                             .rearrange("(j p) c -> p j c", p=128), o[:], [o.b], [self.Y.b])
                self.attn_block(q[:, cq * 512:(cq + 1) * 512], [q.b], 512, tiles, scale, views, [accb.b], epilogue=epi)
        self.attn_flush()

    def nsa_compress(self, li, kcmpT, vcmp):
        w = self.w
        T = self.T
        n_cmp = (T - 32) // 16 + 1
        ncp = self.ncmp_pad
        nct = ncp // 128
        self.memset('pool', kcmpT[:], 0.0, [kcmpT.b])
        self.memset('pool', vcmp[:], 0.0, [vcmp.b])
        with ExitStack() as es:
            w1 = self.sb(es, 'cw1', [64, 2, 32, 64], BF16)
            w2 = self.sb(es, 'cw2', [64, 2, 64], BF16)
            peT = self.sb(es, 'cpeT', [64, 2, 32], BF16)
            for kv in range(2):
                self.dma('pool', w1[:, kv], w['nsa_cmp_w1'][li, kv].rearrange("(l d) o -> d l o", d=64), [], [w1.b])
                self.dma('pool', w2[:, kv], w['nsa_cmp_w2'][li, kv], [], [w2.b])
                self.dma('pool', peT[:, kv], w['nsa_cmp_posT'][li, kv], [], [peT.b])
            g_kc = self.bc_load(es, 'g_kc', w['nsa_k_norm_g'][li, 0:1, :], 64)
            ovl = self.sb(es, 'ovl', [128, nct, self.n_blk], BF16)
            self.dma('sp', ovl[:], self.c['overlap'].rearrange("(k p) j -> p k j", p=128), [], [ovl.b])
            xT = [self.sb(es, f'cxT{i}', [64, T], BF16) for i in range(2)]
            bias = self.sb(es, 'cbias', [64, 1], F32)
            hid = self.sb(es, 'chid', [64, ncp], BF16)
            self.memset('pool', hid[:], 0.0, [hid.b])
            ctm = self.sb(es, 'ctm', [128, 64], F32)
            ctn = self.sb(es, 'ctn', [128, 64], F32)
            ctb = self.sb(es, 'ctb', [128, 64], BF16)
            rp = self.sb(es, 'crp', [128, 64], F32)
            it = 0
            for kv in range(2):
                for g in range(2):
                    x = xT[it % 2]
                    it += 1
                    self.dma('sp', x[:], self.nsa_kT[4 + kv * 2 + g], [self.nsa_kT.b], [x.b])
                    pH = self.ps[it % 2]
                    for l in range(32):
                        self.mm(pH[0:64, 0:n_cmp], w1[:, kv, l, :], x[:, l:l + 16 * (n_cmp - 1) + 1:16],
                                l == 0, False, [w1.b, x.b], [pH.b])
                        self.mm(pH[0:64, 511:512], w1[:, kv, l, :], peT[:, kv, l:l + 1], False, l == 31,
                                [w1.b, peT.b], [pH.b])
                    self.cp('dve', bias[:], pH[0:64, 511:512], [pH.b], [bias.b])
                    self.act(hid[:, 0:n_cmp], pH[0:64, 0:n_cmp], AF.Silu, [pH.b, bias.b], [hid.b], bias=bias[:, 0:1])
                    for kt in range(nct):
                        pO = self.ps[2 + kt % 2]
                        self.mm(pO[:, 0:64], hid[:, kt * 128:(kt + 1) * 128], w2[:, kv, :], True, True,
                                [hid.b, w2.b], [pO.b])
                        if kv == 1:
                            self.cp('act', vcmp[:, g, kt, 0:64], pO[:, 0:64], [pO.b], [vcmp.b])
                        else:
                            self.cp('act', ctm[:], pO[:, 0:64], [pO.b], [ctm.b])
                            self.rmsn(ctm[:].rearrange("p (h c) -> p h c", h=1), 1, 64, g_kc,
                                      ctn[:].rearrange("p (h c) -> p h c", h=1), [ctm.b], [ctn.b])
                            nrow = min(128, n_cmp - kt * 128)
                            r0 = 31 + 16 * 128 * kt
                            self.memset('dve', rp[:], 0.0, [rp.b])
                            self.dma('sp', rp[0:nrow, :], self.rope_d[r0:r0 + 16 * (nrow - 1) + 1:16, :],
                                     [self.rope_d.b], [rp.b])
                            A, Bm = self.sc_ra, self.sc_rb
                            c2 = ctn[:].rearrange("p (two c) -> p two c", two=2)
                            o2 = ctb[:].rearrange("p (two c) -> p two c", two=2)
                            Av = A[:, 0:64].rearrange("p (two c) -> p two c", two=2)
                            Bv = Bm[:, 0:64].rearrange("p (two c) -> p two c", two=2)
                            self.tt('dve', Av, c2, rp[:, 0:32].unsqueeze(1).to_broadcast([128, 2, 32]), ALU.mult,
                                    [ctn.b, rp.b], [A.b])
                            self.tt('dve', Bv, c2, rp[:, 32:64].unsqueeze(1).to_broadcast([128, 2, 32]), ALU.mult,
                                    [ctn.b, rp.b], [Bm.b])
                            self.tt('dve', o2[:, 0, :], Av[:, 0, :], Bv[:, 1, :], ALU.subtract, [A.b, Bm.b], [ctb.b])
                            self.tt('dve', o2[:, 1, :], Bv[:, 0, :], Av[:, 1, :], ALU.add, [A.b, Bm.b], [ctb.b])
                            pT = self.ps[4]
                            pTb = pT[:].bitcast(BF16)
                            self.tr(pTb[0:64, 0:128], ctb[:], self.ident[:], [ctb.b, self.ident.b], [pT.b])
                            self.cp('act', kcmpT[:, g, kt * 128:(kt + 1) * 128], pTb[0:64, 0:128], [pT.b], [kcmpT.b])
            for g in range(2):
                for kt in range(nct):
                    self.memset('pool', vcmp[:, g, kt, 64:65], 1.0, [vcmp.b])
                    self.cp('pool', vcmp[:, g, kt, 65:65 + self.n_blk], ovl[:, kt, :], [ovl.b], [vcmp.b])
            self.S.barrier()

    def nsa_attn(self, es, kcmpT, vcmp):
        T, NT = self.T, self.NT
        nb = self.n_blk
        nct = self.ncmp_pad // 128
        dvc = 65 + nb
        ksT = self.sb(es, 'ksT', [64, T], BF16)
        kwT = self.sb(es, 'kwT', [64, T], BF16)
        vs = self.sb(es, 'vs', [128, NT, 65], BF16)
        vw = self.sb(es, 'vw', [128, NT, 65], BF16)
        qt_ = [self.sb(es, f'nq{i}', [64, 512], BF16) for i in range(2)]
        cneg = [self.sb(es, f'cneg{i}', [128, self.ncmp_pad], BF16) for i in range(2)]
        forced = [self.sb(es, f'forced{i}', [128, nb], F32) for i in range(2)]
        negm = [self.sb(es, f'negm{i}', [128, T], BF16) for i in range(2)]
        rec = self.sb(es, 'nrec', [128, 4], F32)
        score = self.sb(es, 'nscore', [128, nb], F32)
        work = self.sb(es, 'nwork', [128, nb], F32)
        m8 = self.sb(es, 'nm8', [128, 8], F32)
        thr = self.sb(es, 'nthr', [128, 1], F32)
        nsel = self.sb(es, 'nsel', [128, nb], BF16)
        ost = [self.sb(es, f'nost{i}', [128, 3, 4, 64], F32) for i in range(2)]
        accA, accB, accS, accW = self.ps[3], self.ps[4], self.ps[5], self.ps[6]
        cmp_eid = {}

        def stage1(g, qt, sl):
            q, cn, fo, nm, o = qt_[sl], cneg[sl], forced[sl], negm[sl], ost[sl]
            tsl = slice(qt * 128, (qt + 1) * 128)
            self.dma('sp', q[:], self.nsa_qT[g, qt].rearrange("d r t -> d (r t)"), [self.nsa_qT.b], [q.b])
            self.dma('sp', cn[:], self.c['cmpneg'][tsl, :], [], [cn.b])
            self.dma('sp', fo[:], self.c['forced'][tsl, :], [], [fo.b])
            views = {j: ((accA if j < 2 else accB)[:, (j % 2) * dvc:(j % 2 + 1) * dvc], j // 2) for j in range(4)}
            tiles = []
            for kt in range(nct):
                if 16 * 128 * kt + 31 > qt * 128 + 127:
                    continue
                tiles.append(dict(kT=kcmpT[:, g, kt * 128:(kt + 1) * 128], V=vcmp[:, g, kt, 0:dvc],
                                  R=[kcmpT.b, vcmp.b], c0=0, subs=[0, 1, 2, 3],
                                  masks=[(cn[:, kt * 128:(kt + 1) * 128], self.i4[:], 0, 512, [cn.b, self.i4.b])]))
            if not tiles:
                tiles.append(dict(kT=kcmpT[:, g, 0:128], V=vcmp[:, g, 0, 0:dvc], R=[kcmpT.b, vcmp.b], c0=0,
                                  subs=[0, 1, 2, 3],
                                  masks=[(cn[:, 0:128], self.i4[:], 0, 512, [cn.b, self.i4.b])]))
            cmp_tiles = tiles
            viewsW = {j: (accW[:, j * 65:(j + 1) * 65], 0) for j in range(4)}
            tiles = []
            for kt in range(max(0, qt - 4), qt + 1):
                tl = dict(kT=kwT[:, kt * 128:(kt + 1) * 128], V=vw[:, kt, :], R=[kwT.b, vw.b], c0=0,
                          subs=[0, 1, 2, 3], masks=[])
                if kt == qt:
                    tl['masks'].append((self.tri_neg[:], self.i4[:], 0, 512, [self.tri_neg.b, self.i4.b]))
                if kt == qt - 4:
                    tl['masks'].append((self.edge_neg[:], self.i4[:], 0, 512, [self.edge_neg.b, self.i4.b]))
                tiles.append(tl)
            win_tiles = tiles

            def epi_cmp():
                for j in range(4):
                    ab = accA if j < 2 else accB
                    v_ = views[j][0]
                    self.ts('dve', rec[:, j:j + 1], v_[:, 64:65], 1e-30, ALU.max, [ab.b], [rec.b])
                self.op_('dve', lambda e: e.reciprocal(out=rec[:], in_=rec[:]), reads=[rec.b], writes=[rec.b])
                for j in range(4):
                    ab = accA if j < 2 else accB
                    v_ = views[j][0]
                    self.ts('dve', o[:, 0, j, :], v_[:, 0:64], rec[:, j:j + 1], ALU.mult, [ab.b, rec.b], [o.b])
                    if j == 0:
                        self.ts('dve', score[:], v_[:, 65:65 + nb], rec[:, 0:1], ALU.mult, [ab.b, rec.b], [score.b])
                    else:
                        self.stt(score[:], v_[:, 65:65 + nb], rec[:, j:j + 1], score[:], ALU.mult, ALU.add,
                                 [ab.b, rec.b, score.b], [score.b])
                self.tt('dve', score[:], score[:], fo[:], ALU.add, [score.b, fo.b], [score.b])
                self.op_('dve', lambda e: e.max(out=m8[:], in_=score[:]), reads=[score.b], writes=[m8.b])
                self.op_('dve', lambda e: e.match_replace(out=work[:], in_to_replace=m8[:], in_values=score[:],
                                                           imm_value=-3e38), reads=[score.b, m8.b], writes=[work.b])
                self.op_('dve', lambda e: e.max(out=m8[:], in_=work[:]), reads=[work.b], writes=[m8.b])
                self.ts('dve', thr[:], m8[:, 7:8], -1e29, ALU.max, [m8.b], [thr.b])
                self.ts('dve', nsel[:], score[:], thr[:, 0:1], ALU.is_lt, [score.b, thr.b], [nsel.b], s2=-BIG, op1=ALU.mult)
                nblk_need = (qt + 1) * 2
                self.cp('pool', nm[:, 0:nblk_need * 64].rearrange("p (j c) -> p j c", c=64),
                        nsel[:, 0:nblk_need].unsqueeze(2).to_broadcast([128, nblk_need, 64]), [nsel.b], [nm.b])
                self.tt('pool', nm[:, tsl], nm[:, tsl], self.tri_neg[:], ALU.add, [nm.b, self.tri_neg.b], [nm.b])

            def epi_win():
                av = accW[:, 0:260].rearrange("p (j c) -> p j c", j=4)
                self.op_('dve', lambda e, av=av: e.reciprocal(out=rec[:], in_=av[:, :, 64]), reads=[accW.b], writes=[rec.b])
                self.tt('dve', o[:, 2], av[:, :, 0:64], rec[:].unsqueeze(2).to_broadcast([128, 4, 64]), ALU.mult,
                        [accW.b, rec.b], [o.b])

            eid = self.attn_block(q[:], [q.b], 512, cmp_tiles, 0.125, views, [accA.b, accB.b], epilogue=epi_cmp)
            self.attn_block(q[:], [q.b], 512, win_tiles, 0.125, viewsW, [accW.b], epilogue=epi_win)
            cmp_eid[(g, qt)] = eid

        def stage2(g, qt, sl):
            q, nm, o = qt_[sl], negm[sl], ost[sl]
            tsl = slice(qt * 128, (qt + 1) * 128)
            self.attn_sync(cmp_eid[(g, qt)])
            views = {j: (accS[:, j * 65:(j + 1) * 65], 0) for j in range(4)}
            tiles = [dict(kT=ksT[:, kt * 128:(kt + 1) * 128], V=vs[:, kt, :], R=[ksT.b, vs.b], c0=0,
                          subs=[0, 1, 2, 3],
                          masks=[(nm[:, kt * 128:(kt + 1) * 128], self.i4[:], 0, 512, [nm.b, self.i4.b])])
                     for kt in range(qt + 1)]
            def epi_sel():
                av = accS[:, 0:260].rearrange("p (j c) -> p j c", j=4)
                self.op_('dve', lambda e, av=av: e.reciprocal(out=rec[:], in_=av[:, :, 64]), reads=[accS.b], writes=[rec.b])
                self.tt('dve', o[:, 1], av[:, :, 0:64], rec[:].unsqueeze(2).to_broadcast([128, 4, 64]), ALU.mult,
                        [accS.b, rec.b], [o.b])
                for bi, base in enumerate((1280, 512, 1792)):
                    self.dma('sp', self.Y[tsl, base + g * 256:base + (g + 1) * 256],
                             o[:, bi].rearrange("p r c -> p (r c)"), [o.b], [self.Y.b])
            self.attn_block(q[:], [q.b], 512, tiles, 0.125, views, [accS.b], epilogue=epi_sel)

        for g in range(2):
            self.dma('sp', ksT[:], self.nsa_kT[0 + g], [self.nsa_kT.b], [ksT.b])
            self.dma('sp', kwT[:], self.nsa_kT[2 + g], [self.nsa_kT.b], [kwT.b])
            self.dma('sp', vs[:], self.nsa_v[0 + g], [self.nsa_v.b], [vs.b])
            self.dma('sp', vw[:], self.nsa_v[2 + g], [self.nsa_v.b], [vw.b])
            stage1(g, 0, 0)
            for qt in range(NT):
                if qt + 1 < NT:
                    stage1(g, qt + 1, (qt + 1) % 2)
                stage2(g, qt, qt % 2)
            self.attn_flush()

    def layer_odd(self, es, L, x_src, x_dst):
        li = L // 2
        w = self.w
        T, NT = self.T, self.NT
        S = self.S
        memkT, memV = self.mem_kv(es, L)
        with ExitStack() as p1:
            win, lng = self.load_win(p1, L, w['odd_w_in'][li], ODD_COLS)
            g_cq = self.bc_load(p1, 'g_cq', w['dsa_q_norm_g'][li:li + 1, :], 64)
            g_ck = self.bc_load(p1, 'g_ck', w['dsa_k_norm_g'][li:li + 1, :], 64)
            g_mq = self.bc_load(p1, 'g_mq', w['mem_q_norm_g'][L:L + 1, :], 64)
            xt = [self.sb(p1, f'xt{i}', [128, D], F32) for i in range(2)]
            ht = [self.sb(p1, f'ht{i}', [128, D], BF16) for i in range(2)]
            hT = [self.sb(p1, f'hT{i}', [128, 8, 128], BF16) for i in range(2)]
            u = [self.sb(p1, f'u{i}', [128, ODD_COLS], F32) for i in range(2)]
            ss = self.sb(p1, 'ss', [128, 1], F32)
            junk = self.sb(p1, 'junk', [128, D], BF16)
            gt = self.sb(p1, 'gt', [128, 1792], BF16)
            cqn = self.sb(p1, 'cqn', [128, 512], F32)
            cqf = self.sb(p1, 'cqf', [128, 512], BF16)
            ckn = self.sb(p1, 'ckn', [128, 64], F32)
            ckf = self.sb(p1, 'ckf', [128, 64], BF16)
            cva = self.sb(p1, 'cva', [128, 65], BF16)
            self.memset('pool', cva[:], 1.0, [cva.b])
            iqf = self.sb(p1, 'iqf', [128, 256], BF16)
            ikf = self.sb(p1, 'ikf', [128, 32], BF16)
            iwt = self.sb(p1, 'iwt', [128, 8], F32)
            mlv = self.sb(p1, 'mlv', [128, 4, 129], BF16)
            self.memset('pool', mlv[:], 1.0, [mlv.b])
            mqf = self.sb(p1, 'mqf', [128, 256], BF16)
            stq = self.sb(p1, 'stq', [64, 8, 128], BF16)
            stk = self.sb(p1, 'stk', [64, 1, 128], BF16)
            sti = self.sb(p1, 'sti', [32, 8, 128], BF16)
            stik = self.sb(p1, 'stik', [32, 1, 128], BF16)
            stmq = self.sb(p1, 'stmq', [64, 4, 128], BF16)
            strw = self.sb(p1, 'strw', [128, 4, 128], F32)
            stif = self.sb(p1, 'stif', [8, 128], F32)
            scrA = self.new_scr(p1, 'A', 512)
            scrB = self.new_scr(p1, 'B', 256)
            for n in range(NT):
                ut = self.p1_front(n, x_src, xt, ht, hT, u, ss, junk, lng, win, ODD_COLS)
                U = lambda a, b_: ut[:, a:b_]
                ub = [ut.b]
                tsl = slice(n * 128, (n + 1) * 128)
                self.chains_begin(['A', 'B', 'L', 'M', 'G'])
                self.chain('G')
                self.act(gt[:, 0:512], U(936, 1448), AF.Silu, ub, [gt.b])
                self.act(gt[:, 512:1024], U(2992, 3504), AF.Silu, ub, [gt.b])
                self.act(gt[:, 1024:1280], U(3760, 4016), AF.Silu, ub, [gt.b])
                self.act(gt[:, 1280:1792], U(2480, 2992), AF.Sigmoid, ub, [gt.b])
                self.dma('sp', self.G[tsl, :], gt[:], [gt.b], [self.G.b[n]])
                self.chain('A', scrA)
                cq3 = cqn[:].rearrange("p (h c) -> p h c", h=8)
                self.rmsn(U(0, 512).rearrange("p (h c) -> p h c", h=8), 8, 64, g_cq, cq3, ub, [cqn.b])
                self.rope(cq3, 8, 32, n, cqf[:].rearrange("p (h c) -> p h c", h=8), [cqn.b], [cqf.b])
                ck3 = ckn[:].rearrange("p (h c) -> p h c", h=1)
                self.rmsn(U(512, 576).rearrange("p (h c) -> p h c", h=1), 1, 64, g_ck, ck3, ub, [ckn.b])
                self.rope(ck3, 1, 32, n, ckf[:].rearrange("p (h c) -> p h c", h=1), [ckn.b], [ckf.b])
                self.cp('dve', cva[:, 0:64], U(576, 640), ub, [cva.b])
                self.dma('sp', self.dsa_v[:, n, :], cva[:], [cva.b], [self.dsa_v.b])
                pT = self.ps[4]
                pTb = pT[:].bitcast(BF16)
                for h in range(8):
                    self.tr(pTb[0:64, h * 128:(h + 1) * 128], cqf[:, h * 64:(h + 1) * 64], self.ident[:],
                            [cqf.b, self.ident.b], [pT.b])
                self.cp('act', stq[:], pTb[0:64, 0:1024].rearrange("p (k c) -> p k c", k=8), [pT.b], [stq.b])
                self.dma('sp', self.dsa_qT[n], stq[:], [stq.b], [self.dsa_qT.b])
                self.transposes_out([(ckf[:], [ckf.b])], 64, stk,
                                    self.dsa_kT[:, tsl].rearrange("d (k t) -> d k t", k=1), [self.dsa_kT.b], 5)
                self.chain('B', scrB)
                self.rope(U(640, 896).rearrange("p (h c) -> p h c", h=8), 8, 16, n,
                          iqf[:].rearrange("p (h c) -> p h c", h=8), ub, [iqf.b])
                self.rope(U(896, 928).rearrange("p (h c) -> p h c", h=1), 1, 16, n,
                          ikf[:].rearrange("p (h c) -> p h c", h=1), ub, [ikf.b])
                self.ts('dve', iwt[:], U(928, 936), 8 ** -0.5, ALU.mult, ub, [iwt.b])
                self.dma('sp', self.idx_w[tsl, :], iwt[:], [iwt.b], [self.idx_w.b])
                self.transposes_out([(iqf[:, h * 32:(h + 1) * 32], [iqf.b]) for h in range(8)], 32, sti,
                                    self.idx_qT[:, :, tsl].rearrange("h d t -> d h t"), [self.idx_qT.b], 6)
                self.transposes_out([(ikf[:], [ikf.b])], 32, stik,
                                    self.idx_kT[:, tsl].rearrange("d (k t) -> d k t", k=1), [self.idx_kT.b], 7)
                self.chain('L')
                pR = self.ps[3]
                for k in range(4):
                    self.tr(pR[:, k * 128:(k + 1) * 128], U(1448 + k * 128, 1448 + (k + 1) * 128), self.identf[:],
                            ub + [self.identf.b], [pR.b])
                self.cp('act', strw[:].rearrange("p k c -> p (k c)"), pR[:, 0:512], [pR.b], [strw.b])
                self.dma('sp', self.ml_raw[:, tsl].rearrange("(k p) t -> p k t", p=128), strw[:], [strw.b],
                         [self.ml_raw.b])
                pI = self.ps[3]
                self.tr(pI[0:8, 0:128], U(2472, 2480), self.identf[:], ub + [self.identf.b], [pI.b])
                self.cp('act', stif[:], pI[0:8, 0:128], [pI.b], [stif.b])
                self.dma('sp', self.ml_if[:, tsl], stif[:], [stif.b], [self.ml_if.b])
                self.cp('dve', mlv[:, :, 0:128], U(1960, 2472).rearrange("p (h c) -> p h c", h=4), ub, [mlv.b])
                self.dma('sp', self.ml_v[:, :, n, :].rearrange("h p c -> p h c"), mlv[:], [mlv.b], [self.ml_v.b])
                self.chain('B', scrB)
                self.rmsn(U(3504, 3760).rearrange("p (h c) -> p h c", h=4), 4, 64, g_mq,
                          mqf[:].rearrange("p (h c) -> p h c", h=4), ub, [mqf.b])
                self.transposes_out([(mqf[:, h * 64:(h + 1) * 64], [mqf.b]) for h in range(4)], 64, stmq,
                                    self.mem_qT[:, :, tsl].rearrange("h d t -> d h t"), [self.mem_qT.b], 6)
                self.chains_emit()
            S.barrier()
        if 'stop_p1' in self.dbg:
            return
        self.mlstm_pre(li)
        S.barrier()
        if 'stop_pre' in self.dbg:
            return
        with ExitStack() as pa:
            self.attn_setup(pa)
            self.mlstm_attn(pa)
            S.barrier()
        if 'stop_ml' in self.dbg:
            return
        with ExitStack() as pa:
            self.attn_setup(pa)
            self.mem_attn(pa, memkT, memV)
            S.barrier()
        if 'stop_mem' in self.dbg:
            return
        with ExitStack() as pa:
            self.attn_setup(pa)
            self.dsa_attn(pa)
            S.barrier()
        if 'stop_dsa' in self.dbg:
            return
        with ExitStack() as p3:
            wout = self.load_wout(p3, L)
            g_h = self.bc_load(p3, 'g_h', w['mlstm_h_norm_g'][li:li + 1, :], 128)
            yt = [self.sb(p3, f'yt{i}', [128, 1280], F32) for i in range(2)]
            gtt = [self.sb(p3, f'gtt{i}', [128, 1792], BF16) for i in range(2)]
            mix = [self.sb(p3, f'mix{i}', [128, 1280], BF16) for i in range(2)]
            hn = self.sb(p3, 'hn', [128, 512], F32)

            def mixfn(n):
                sl = n % 2
                y, g, m = yt[sl], gtt[sl], mix[sl]
                tsl = slice(n * 128, (n + 1) * 128)
                self.dma('sp', y[:], self.Y[tsl, 0:1280], [self.Y.b], [y.b])
                self.dma('sp', g[:], self.G[tsl, :], [self.G.b[n]], [g.b])
                self.tt('dve', m[:, 0:512], y[:, 0:512], g[:, 0:512], ALU.mult, [y.b, g.b], [m.b])
                self.tt('pool', m[:, 1024:1280], y[:, 1024:1280], g[:, 1024:1280], ALU.mult, [y.b, g.b], [m.b])
                self.rmsn(y[:, 512:1024].rearrange("p (h c) -> p h c", h=4), 4, 128, g_h,
                          hn[:].rearrange("p (h c) -> p h c", h=4), [y.b], [hn.b])
                self.tt('dve', hn[:], hn[:], g[:, 1280:1792], ALU.mult, [hn.b, g.b], [hn.b])
                self.tt('dve', m[:, 512:1024], hn[:], g[:, 512:1024], ALU.mult, [hn.b, g.b], [m.b])
                return m
            self.p3(p3, L, x_src, x_dst, wout, mixfn)
            S.barrier()

    def mlstm_pre(self, li):
        w = self.w
        T = self.T
        with ExitStack() as es:
            xp = [self.sb(es, f'xp{i}', [128, T + 3], F32) for i in range(2)]
            y = self.sb(es, 'cy', [128, T], F32)
            yo = [self.sb(es, f'cyo{i}', [128, T], BF16) for i in range(2)]
            wc = self.sb(es, 'cwc', [128, 4, 4], F32)
            bc = self.sb(es, 'cbc', [128, 4], F32)
            for ck in range(4):
                self.dma('sp', wc[:, ck, :], w['mlstm_conv_wT'][li, ck * 128:(ck + 1) * 128, :], [], [wc.b])
                self.dma('sp', bc[:, ck:ck + 1], w['mlstm_conv_b'][li, ck * 128:(ck + 1) * 128].unsqueeze(1), [], [bc.b])
            for ck in range(4):
                x = xp[ck % 2]
                o = yo[ck % 2]
                self.memset('pool', x[:, 0:3], 0.0, [x.b])
                self.dma('sp', x[:, 3:T + 3], self.ml_raw[ck * 128:(ck + 1) * 128, :], [self.ml_raw.b], [x.b])
                self.ts('dve', y[:], x[:, 0:T], wc[:, ck, 0:1], ALU.mult, [x.b, wc.b, bc.b], [y.b],
                        s2=bc[:, ck:ck + 1], op1=ALU.add)
                for j in range(1, 4):
                    self.stt(y[:], x[:, j:j + T], wc[:, ck, j:j + 1], y[:], ALU.mult, ALU.add, [x.b, wc.b, y.b], [y.b])
                self.act(o[:], y[:], AF.Silu, [y.b], [o.b])
                self.dma('sp', self.ml_qkT[ck * 128:(ck + 1) * 128, :], o[:], [o.b], [self.ml_qkT.b])
            self.S.barrier()
        with ExitStack() as es:
            ig = self.sb(es, 'ig', [4, T], F32)
            fg = self.sb(es, 'fg', [4, T], F32)
            cs = self.sb(es, 'cs', [4, T], F32)
            a = self.sb(es, 'ga', [4, T], F32)
            Mt = self.sb(es, 'gM', [4, T], F32)
            ones = self.sb(es, 'gones', [4, T], F32)
            ib = self.sb(es, 'gib', [4, 1], F32)
            fb = self.sb(es, 'gfb', [4, 1], F32)
            self.memset('pool', ones[:], 1.0, [ones.b])
            self.dma('sp', ig[:], self.ml_if[0:4, :], [self.ml_if.b], [ig.b])
            self.dma('sp', fg[:], self.ml_if[4:8, :], [self.ml_if.b], [fg.b])
            self.dma('sp', ib[:], w['mlstm_i_bias'][li].unsqueeze(1), [], [ib.b])
            self.dma('sp', fb[:], w['mlstm_f_bias'][li].unsqueeze(1), [], [fb.b])
            self.ts('dve', fb[:], fb[:], -1.0, ALU.mult, [fb.b], [fb.b])
            self.act(fg[:], fg[:], AF.Exp, [fg.b, fb.b], [fg.b], bias=fb[:, 0:1], scale=-1.0)
            self.act(fg[:], fg[:], AF.Ln, [fg.b], [fg.b], bias=self.one_c[0:4, 0:1], scale=1.0)
            self.op_('dve', lambda e: e.tensor_tensor_scan(out=cs[:], data0=ones[:], data1=fg[:], initial=0.0,
                                                             op0=ALU.mult, op1=ALU.add),
                      reads=[ones.b, fg.b], writes=[cs.b])
            self.stt(a[:], ig[:], ib[:, 0:1], cs[:], ALU.add, ALU.add, [ig.b, ib.b, cs.b], [a.b])
            self.op_('dve', lambda e: e.tensor_tensor_scan(out=Mt[:], data0=a[:], data1=a[:], initial=0.0,
                                                             op0=ALU.max, op1=ALU.max),
                      reads=[a.b], writes=[Mt.b])
            self.tt('dve', cs[:], cs[:], Mt[:], ALU.subtract, [cs.b, Mt.b], [cs.b])
            self.act(cs[:], cs[:], AF.Exp, [cs.b], [cs.b])
            self.ts('dve', Mt[:], Mt[:], -1.0, ALU.mult, [Mt.b], [Mt.b])
            self.dma('sp', self.ml_g[0:4, :], a[:], [a.b], [self.ml_g.b])
            self.dma('sp', self.ml_g[4:8, :], Mt[:], [Mt.b], [self.ml_g.b])
            self.dma('sp', self.ml_g[8:12, :], cs[:], [cs.b], [self.ml_g.b])
            self.S.barrier()

    def mlstm_attn(self, es):
        T, NT = self.T, self.NT
        qT = [self.sb(es, f'lqT{i}', [64, T], BF16) for i in range(2)]
        kT = [self.sb(es, f'lkT{i}', [64, T], BF16) for i in range(2)]
        V = [self.sb(es, f'lV{i}', [128, NT, 129], BF16) for i in range(2)]
        nM = [self.sb(es, f'lnM{i}', [128, T], F32) for i in range(2)]
        ant = self.sb(es, 'lant', [NT, 2, 128], F32)
        atm = [self.sb(es, f'latm{i}', [128, 2, NT], F32) for i in range(2)]
        Et = [self.sb(es, f'lEt{i}', [128, 512], F32) for i in range(3)]
        ost = [self.sb(es, f'lost{i}', [128, 4, 128], F32) for i in range(2)]
        d2 = self.sb(es, 'ld2', [128, 4], F32)
        ecnt = [0]
        blk = 0
        LN8 = math.log(0.125)
        for h in range(4):
            q, k, v, nm, at = qT[h % 2], kT[h % 2], V[h % 2], nM[h % 2], atm[h % 2]
            self.dma('sp', q[:], self.ml_qkT[h * 64:(h + 1) * 64, :], [self.ml_qkT.b], [q.b])
            self.dma('sp', k[:], self.ml_qkT[256 + h * 64:256 + (h + 1) * 64, :], [self.ml_qkT.b], [k.b])
            self.dma('sp', v[:], self.ml_v[h], [self.ml_v.b], [v.b])
            self.dma('sp', nm[:], self.ml_g[4 + h:5 + h, :].to_broadcast([128, T]), [self.ml_g.b], [nm.b])
            self.dma('sp', ant[:, 0, :], self.ml_g[h, :].rearrange("(n p) -> n p", p=128), [self.ml_g.b], [ant.b])
            self.dma('sp', ant[:, 1, :], self.ml_g[8 + h, :].rearrange("(n p) -> n p", p=128), [self.ml_g.b], [ant.b])
            pA = self.ps[7]
            for i in range(2):
                self.tr(pA[:, i * NT:(i + 1) * NT], ant[:, i, :], self.identf[0:NT, 0:NT], [ant.b, self.identf.b], [pA.b])
            self.cp('act', at[:].rearrange("p a n -> p (a n)"), pA[:, 0:2 * NT], [pA.b], [at.b])
            self.ts('dve', at[:, 0, :], at[:, 0, :], LN8, ALU.add, [at.b], [at.b])
            for cq in range(T // 512):
                set_i = blk % 2
                blk += 1
                accA, accB = self.ps[3 + 2 * set_i], self.ps[4 + 2 * set_i]
                views = {j: ((accA if j < 2 else accB)[:, (j % 2) * 129:(j % 2 + 1) * 129], j // 2) for j in range(4)}
                tiles = []
                for kt in range(4 * cq + 4):
                    vv = kt - 4 * cq
                    c0 = max(0, vv) * 128

                    def efn(kt=kt, c0=c0, cq=cq, nm=nm, at=at):
                        E = Et[ecnt[0] % 3]
                        ecnt[0] += 1
                        self.act(E[:, c0:512], nm[:, cq * 512 + c0:(cq + 1) * 512], AF.Exp, [nm.b, at.b], [E.b],
                                 bias=at[:, 0, kt:kt + 1], scale=1.0)
                        return E, [E.b]
                    tl = dict(kT=k[:, kt * 128:(kt + 1) * 128], V=v[:, kt, :], R=[k.b, v.b], c0=c0,
                              subs=list(range(max(0, vv), 4)), Efn=efn)
                    if vv >= 0:
                        tl['diag'] = vv
                    tiles.append(tl)
                def epi(accA=accA, accB=accB, views=views, o=ost[set_i], cq=cq, h=h, at=at):
                    for j in range(4):
                        ab = accA if j < 2 else accB
                        v_ = views[j][0]
                        self.act(d2[:, j:j + 1], v_[:, 128:129], AF.Abs, [ab.b], [d2.b])
                    self.tt('dve', d2[:], d2[:], at[:, 1, 4 * cq:4 * cq + 4], ALU.max, [d2.b, at.b], [d2.b])
                    self.op_('dve', lambda e: e.reciprocal(out=d2[:], in_=d2[:]), reads=[d2.b], writes=[d2.b])
                    for j in range(4):
                        ab = accA if j < 2 else accB
                        v_ = views[j][0]
                        self.ts('dve', o[:, j, :], v_[:, 0:128], d2[:, j:j + 1], ALU.mult, [ab.b, d2.b], [o.b])
                    self.dma('sp', self.Y[cq * 512:(cq + 1) * 512, 512 + h * 128:512 + (h + 1) * 128]
                             .rearrange("(j p) c -> p j c", p=128), o[:], [o.b], [self.Y.b])
                self.attn_block(q[:, cq * 512:(cq + 1) * 512], [q.b], 512, tiles, 1.0, views, [accA.b, accB.b],
                                mode='mul', epilogue=epi)
        self.attn_flush()

    def dsa_attn(self, es):
        T, NT = self.T, self.NT
        KSEL = min(256, T // 4)
        ikT = self.sb(es, 'ikT', [32, T], BF16)
        ckT = self.sb(es, 'ckT', [64, T], BF16)
        cv = self.sb(es, 'cv', [128, NT, 65], BF16)
        self.dma('sp', ikT[:], self.idx_kT[:, :], [self.idx_kT.b], [ikT.b])
        self.dma('sp', ckT[:], self.dsa_kT[:, :], [self.dsa_kT.b], [ckT.b])
        self.dma('sp', cv[:], self.dsa_v[:, :, :], [self.dsa_v.b], [cv.b])
        iq = [self.sb(es, f'iq{i}', [32, 8, 128], BF16) for i in range(2)]
        iw = [self.sb(es, f'iw{i}', [128, 8], F32) for i in range(2)]
        cq_ = [self.sb(es, f'cq{i}', [64, 2, 512], BF16) for i in range(2)]
        score2 = [self.sb(es, f'dscore{i}', [128, T], F32) for i in range(3)]
        thrA2 = [self.sb(es, f'dthrA{i}', [128, 1], F32) for i in range(2)]
        work = self.sb(es, 'dwork', [128, T], F32)
        negm = [self.sb(es, f'dnegm{i}', [128, T], BF16) for i in range(2)]
        rl = [self.sb(es, f'drl{i}', [128, 512], F32) for i in range(3)]
        m8 = self.sb(es, 'dm8', [128, 8], F32)
        thr = self.sb(es, 'dthr', [128, 1], F32)
        rec = self.sb(es, 'drec', [128, 4], F32)
        ost = [self.sb(es, f'dost{i}', [128, 4, 64], F32) for i in range(2)]
        rlc_ = [0]
        blk_ = [0]
        NBIS = 24
        junk = self.sb(es, 'djunk', [128, T], BF16)
        amax = self.sb(es, 'damax', [128, 1], F32)
        w0 = self.sb(es, 'dw0', [128, 1], F32)
        nHh = self.sb(es, 'dnHh', [128, 40], F32)
        nmid = [self.sb(es, f'dnmid{i}', [128, 1], F32) for i in range(2)]
        Ssum = self.sb(es, 'dS', [128, 1], F32)
        tsg = self.sb(es, 'dtsg', [128, 1], F32)

        def stage_a1(qt):
            rlc = rlc_[0]
            sl = qt % 2
            tsl = slice(qt * 128, (qt + 1) * 128)
            q_i, w_i = iq[sl], iw[sl]
            score = score2[qt % 3]
            self.dma('sp', q_i[:], self.idx_qT[:, :, tsl].rearrange("h d t -> d h t"), [self.idx_qT.b], [q_i.b])
            self.dma('sp', w_i[:], self.idx_w[tsl, :], [self.idx_w.b], [w_i.b])
            ncols = (qt + 1) * 128
            for c in range((ncols + 511) // 512):
                c0 = c * 512
                wd = min(512, ncols - c0)
                for h in range(8):
                    bi = self.st_rr % len(self.st_banks)
                    self.st_rr += 1
                    bank, bb = self.st_banks[bi]
                    self.mm(bank[:, 0:wd], q_i[:, h, :], ikT[:, c0:c0 + wd], True, True, [q_i.b, ikT.b], [bb])
                    if h == 0:
                        self.ts('dve', score[:, c0:c0 + wd], bank[:, 0:wd], 0.0, ALU.max, [bb, w_i.b], [score.b],
                                s2=w_i[:, 0:1], op1=ALU.mult)
                    else:
                        r = rl[rlc % 3]
                        rlc += 1
                        self.ts('dve', r[:, 0:wd], bank[:, 0:wd], 0.0, ALU.max, [bb, w_i.b], [r.b],
                                s2=w_i[:, h:h + 1], op1=ALU.mult)
                        self.tt('dve', score[:, c0:c0 + wd], score[:, c0:c0 + wd], r[:, 0:wd], ALU.add,
                                [score.b, r.b], [score.b])
            rlc_[0] = rlc

        def is_act_tile(qt):
            return ((qt + 1) * 128 > KSEL) and ('dsa_nobis' not in self.dbg)

        def stage_a2_finish(qt):
            sl = qt % 2
            nm = negm[sl]
            score = score2[qt % 3]
            ncols = (qt + 1) * 128
            th = thrA2[qt % 2] if is_act_tile(qt) else thr
            self.ts('dve', nm[:, 0:ncols], score[:, 0:ncols], th[:, 0:1], ALU.is_lt, [score.b, th.b], [nm.b],
                    s2=-BIG, op1=ALU.mult)

        def stage_a2(qt):
            sl = qt % 2
            tsl = slice(qt * 128, (qt + 1) * 128)
            q_c, nm = cq_[sl], negm[sl]
            score = score2[qt % 3]
            ncols = (qt + 1) * 128
            for half in range(2):
                self.dma('sp', q_c[:, half, :], self.dsa_qT[qt, :, half * 4:(half + 1) * 4, :].rearrange("d h t -> d (h t)"),
                         [self.dsa_qT.b], [q_c.b])
            use_act = is_act_tile(qt)
            if use_act:
                self.op_('dve', lambda e, ncols=ncols: e.tensor_reduce(out=amax[:], in_=score[:, 0:ncols], axis=AX.X,
                                                                      op=ALU.max, apply_absolute_value=True),
                         reads=[score.b], writes=[amax.b])
                self.ts('dve', w0[:], amax[:], 2.0, ALU.mult, [amax.b], [w0.b], s2=2.0, op1=ALU.add)
                self.ts('dve', nHh[:], self.pw[:], w0[:, 0:1], ALU.mult, [self.pw.b, w0.b], [nHh.b])
            self.tt('dve', score[:, tsl], score[:, tsl], self.trinegf[:], ALU.add, [score.b, self.trinegf.b], [score.b])
            if use_act:
                cconst = float(0.5 - (2 * KSEL - ncols - 1))
                self.memset('pool', nmid[0][:], 0.0, [nmid[0].b])
                for j in range(NBIS):
                    cur, nxt = nmid[j % 2], nmid[(j + 1) % 2]
                    self.act(junk[:, 0:ncols], score[:, 0:ncols], AF.Sign, [score.b, cur.b], [junk.b, Ssum.b],
                             bias=cur[:, 0:1], scale=1.0, accum=Ssum[:, 0:1])
                    self.act(tsg[:], Ssum[:], AF.Sign, [Ssum.b], [tsg.b], bias=cconst, scale=1.0)
                    self.act(nxt[:], tsg[:], AF.Identity, [tsg.b, cur.b, nHh.b], [nxt.b],
                             bias=cur[:, 0:1], scale=nHh[:, j:j + 1])
                fin = nmid[NBIS % 2]
                thrA = thrA2[qt % 2]
                self.act(thrA[:], fin[:], AF.Identity, [fin.b, nHh.b], [thrA.b], bias=nHh[:, NBIS - 1:NBIS], scale=-1.0)
                return
            elif ncols > KSEL and 'dsa_notopk' not in self.dbg:
                self.cp('pool', work[:, 0:ncols], score[:, 0:ncols], [score.b], [work.b])
                nr = KSEL // 8
                for r_ in range(nr):
                    self.op_('dve', lambda e, ncols=ncols: e.max(out=m8[:], in_=work[:, 0:ncols]),
                             reads=[work.b], writes=[m8.b])
                    if r_ < nr - 1:
                        self.op_('dve', lambda e, ncols=ncols: e.match_replace(
                            out=work[:, 0:ncols], in_to_replace=m8[:], in_values=work[:, 0:ncols], imm_value=-3e38),
                            reads=[work.b, m8.b], writes=[work.b])
                self.ts('dve', thr[:], m8[:, 7:8], -1e29, ALU.max, [m8.b], [thr.b])
            else:
                self.memset('dve', thr[:], -1e29, [thr.b])
            stage_a2_finish(qt)

        def stage_b(qt):
            blk = blk_[0]
            sl = qt % 2
            tsl = slice(qt * 128, (qt + 1) * 128)
            q_c, nm = cq_[sl], negm[sl]
            for half in range(2):
                set_i = blk % 2
                blk += 1
                accb = self.ps[3 + set_i]
                views = {j: (accb[:, j * 65:(j + 1) * 65], 0) for j in range(4)}
                tiles = [dict(kT=ckT[:, kt * 128:(kt + 1) * 128], V=cv[:, kt, :], R=[ckT.b, cv.b], c0=0,
                              subs=[0, 1, 2, 3],
                              masks=[(nm[:, kt * 128:(kt + 1) * 128], self.i4[:], 0, 512, [nm.b, self.i4.b])])
                         for kt in range(qt + 1)]
                def epi(accb=accb, o=ost[set_i], tsl=tsl, half=half):
                    av = accb[:, 0:260].rearrange("p (j c) -> p j c", j=4)
                    self.op_('dve', lambda e, av=av: e.reciprocal(out=rec[:], in_=av[:, :, 64]), reads=[accb.b], writes=[rec.b])
                    self.tt('dve', o[:], av[:, :, 0:64], rec[:].unsqueeze(2).to_broadcast([128, 4, 64]), ALU.mult,
                            [accb.b, rec.b], [o.b])
                    self.dma('sp', self.Y[tsl, half * 256:(half + 1) * 256], o[:].rearrange("p j c -> p (j c)"),
                             [o.b], [self.Y.b])
                self.attn_block(q_c[:, half, :], [q_c.b], 512, tiles, 0.125, views, [accb.b], epilogue=epi)
            blk_[0] = blk

        stage_a1(0)
        if NT > 1:
            stage_a1(1)
        stage_a2(0)
        for qt in range(NT):
            if qt + 2 < NT:
                stage_a1(qt + 2)
            if qt + 1 < NT:
                stage_a2(qt + 1)
            if is_act_tile(qt):
                stage_a2_finish(qt)
            stage_b(qt)
        self.attn_flush()


def host_consts(T):
    bf = ml_dtypes.bfloat16
    nb = T // 64
    ncp = max(1, T // 2048) * 128
    n_cmp = (T - 32) // 16 + 1
    c = {}
    c['c_ident'] = np.eye(128, dtype=np.float32).astype(bf)
    c['c_identf'] = np.eye(128, dtype=np.float32)
    c['c_i4'] = np.tile(np.eye(128, dtype=np.float32), (1, 4)).astype(bf)
    i8 = np.zeros((128, 512), np.float32)
    for p in range(128):
        for h in range(8):
            i8[p, h * 64 + p % 64] = 1.0
    c['c_i8x2'] = i8.astype(bf)
    t = np.arange(128)[:, None]
    s = np.arange(128)[None, :]
    c['c_tri_neg'] = np.where(s > t, -BIG, 0.0).astype(np.float32).astype(bf)
    c['c_edge_neg'] = np.where(s <= t, -BIG, 0.0).astype(np.float32).astype(bf)
    c['c_tri01T'] = np.where(t <= s, 1.0, 0.0).astype(np.float32).astype(bf)
    c['c_trinegf'] = np.where(s > t, -1e30, 0.0).astype(np.float32)
    inv = (10000.0 ** (-np.arange(32, dtype=np.float32) / 32)).astype(np.float32)
    c['c_invf'] = np.tile(inv[None, :], (128, 1)).astype(np.float32)
    c['c_pw'] = np.tile((-(2.0 ** -(np.arange(40, dtype=np.float64) + 2)))[None, :], (128, 1)).astype(np.float32)
    tt = np.arange(T)[:, None]
    n = np.arange(ncp)[None, :]
    cm = np.where((16 * n + 31 <= tt) & (n < n_cmp), 0.0, -BIG)
    c['c_cmpneg'] = cm.astype(np.float32).astype(bf)
    j = np.arange(nb)[None, :]
    cur = tt // 64
    forced = np.where(j == cur, 3e4, np.where(j == cur - 1, 2e4, np.where(j == 0, 1e4, 0.0)))
    forced = np.where(j * 64 <= tt, forced, -1e30)
    c['c_forced'] = forced.astype(np.float32)
    ni = np.arange(ncp)[:, None]
    ov = ((16 * ni <= j * 64 + 63) & (16 * ni + 31 >= j * 64) & (ni < n_cmp))
    c['c_overlap'] = ov.astype(np.float32).astype(bf)
    return c


_CACHE = {}


def make_in_maps(inputs, T, ncores):
    NT = T // 128
    consts = host_consts(T)
    wnames = ["ln_g", "mem_norm_g", "mem_w_kv", "mem_q_norm_g", "mem_k_norm_g", "w_out", "even_w_in",
              "mla_q_lat_g", "mla_kv_lat_g", "mla_w_uq", "mla_w_ukv", "mla_q_norm_g", "mla_k_norm_g",
              "nsa_q_norm_g", "nsa_k_norm_g", "nsa_cmp_w1", "nsa_cmp_w2", "odd_w_in", "dsa_q_norm_g",
              "dsa_k_norm_g", "mlstm_conv_b", "mlstm_i_bias", "mlstm_f_bias", "mlstm_h_norm_g"]
    shared = {k: np.ascontiguousarray(np.asarray(inputs[k], dtype=np.float32)) for k in wnames}
    shared["nsa_cmp_posT"] = np.ascontiguousarray(np.transpose(np.asarray(inputs["nsa_cmp_pos"], np.float32), (0, 1, 3, 2)))
    shared["mlstm_conv_wT"] = np.ascontiguousarray(np.transpose(np.asarray(inputs["mlstm_conv_w"], np.float32), (0, 2, 1)))
    shared.update(consts)
    maps = []
    for c in range(ncores):
        m = dict(shared)
        m["x"] = np.ascontiguousarray(np.asarray(inputs["x"][c, :T], np.float32))
        m["mem"] = np.ascontiguousarray(np.asarray(inputs["mem"][c], np.float32))
        pos = np.asarray(inputs["positions"][c, :T]).astype(np.int32)
        m["pos_t"] = np.ascontiguousarray(pos.reshape(NT, 128).T)
        maps.append(m)
    return maps


def kernel(**inputs):
    T = 4096
    key = ('full', T)
    if key not in _CACHE:
        _CACHE[key] = Builder(T, [0, 1, 2, 3]).build()
    nc = _CACHE[key]
    maps = make_in_maps(inputs, T, 8)
    res = run_bass_kernel_spmd(nc, maps, core_ids=list(range(8)))
    out = np.stack([np.asarray(r["out"], dtype=np.float32) for r in res.results], axis=0)
    return out
```

```python
import math
import numpy as np
import ml_dtypes
from contextlib import ExitStack
import concourse.bass as bass
import concourse.mybir as mybir
from concourse.bass_utils import run_bass_kernel_spmd

F32 = mybir.dt.float32
BF16 = mybir.dt.bfloat16
I32 = mybir.dt.int32
AF = mybir.ActivationFunctionType
ALU = mybir.AluOpType
AX = mybir.AxisListType

D = 1024
BIG = 30000.0
EPS = 1e-6
EVEN_COLS = 3256
ODD_COLS = 4016
ENGS = ('pe', 'act', 'dve', 'pool', 'sp')
EPOCH = 16000
NDQ = 8


class Buf:
    __slots__ = ('name', 'w', 'r')

    def __init__(self, name=''):
        self.name = name
        self.w = None
        self.r = {}


class Sched:
    def __init__(self, nc, es):
        self.nc = nc
        self.es = es
        self.prog = {e: [] for e in ENGS}
        self.esem = {e: [] for e in ENGS}
        self.cnt = {e: 0 for e in ENGS}
        self.seen = {e: {} for e in ENGS}
        self.dq = ('sp', 'pool', 'act')
        self.dsem = {q: [es.enter_context(nc.semaphore(f'D{q}{i}')) for i in range(NDQ)] for q in self.dq}
        self.dcnt = {q: 0 for q in self.dq}
        self.ninst = 0

    def _semobj(self, key):
        if key[0] == 'E':
            return self.esem[key[1]][key[2]]
        return self.dsem[key[1]][key[2]]

    def op(self, e, fn, reads=(), writes=(), dma=False):
        deps = {}
        for b in reads:
            if b.w is not None:
                k, v = b.w
                if deps.get(k, 0) < v:
                    deps[k] = v
        for b in writes:
            if b.w is not None:
                k, v = b.w
                if deps.get(k, 0) < v:
                    deps[k] = v
            for k, v in b.r.items():
                if deps.get(k, 0) < v:
                    deps[k] = v
        waits = []
        seen = self.seen[e]
        for k, v in deps.items():
            if e == 'pe' and k[0] == 'E' and k[1] == 'pe':
                continue
            if seen.get(k, 0) >= v:
                continue
            seen[k] = v
            waits.append((self._semobj(k), v))
        if dma:
            j = self.dcnt[e]
            self.dcnt[e] += 1
            slot = j % NDQ
            val = 16 * (j // NDQ + 1)
            key = ('D', e, slot)
            if val > 16 and seen.get(key, 0) < val - 16:
                seen[key] = val - 16
                waits.append((self.dsem[e][slot], val - 16))
            sem = self.dsem[e][slot]
            inc = 16
        else:
            c = self.cnt[e]
            ep = c // EPOCH
            if ep >= len(self.esem[e]):
                self.esem[e].append(self.es.enter_context(self.nc.semaphore(f'S{e}{ep}')))
            self.cnt[e] += 1
            key = ('E', e, ep)
            val = c % EPOCH + 1
            sem = self.esem[e][ep]
            inc = 1
        ev = (key, val)
        self.ninst += 1

        def thunk(eng, waits=waits, fn=fn, sem=sem, inc=inc):
            for s, v in waits:
                eng.wait_ge(s, v)
            fn(eng).then_inc(sem, inc)
        self.prog[e].append(thunk)
        for b in reads:
            if b.r.get(key, 0) < val:
                b.r[key] = val
        for b in writes:
            b.w = ev
            b.r = {}
        return ev

    def barrier(self):
        evs = []
        for e in ENGS:
            c = self.cnt[e]
            if c > 0:
                ep = (c - 1) // EPOCH
                evs.append((('E', e, ep), (c - 1) % EPOCH + 1))
        for q in self.dq:
            n = self.dcnt[q]
            for slot in range(min(n, NDQ)):
                cntslot = (n - 1 - slot) // NDQ + 1
                evs.append((('D', q, slot), 16 * cntslot))
        for e in ENGS:
            waits = []
            for k, v in evs:
                if k[0] == 'E' and k[1] == e:
                    continue
                if self.seen[e].get(k, 0) >= v:
                    continue
                self.seen[e][k] = v
                waits.append((self._semobj(k), v))

            def thunk(eng, waits=waits):
                for s, v in waits:
                    eng.wait_ge(s, v)
            self.prog[e].append(thunk)

    def emit(self):
        nc = self.nc
        with nc.Block() as block:
            @block.tensor
            def _(eng):
                for t in self.prog['pe']:
                    t(eng)

            @block.scalar
            def _(eng):
                for t in self.prog['act']:
                    t(eng)

            @block.vector
            def _(eng):
                for t in self.prog['dve']:
                    t(eng)

            @block.gpsimd
            def _(eng):
                for t in self.prog['pool']:
                    t(eng)

            @block.sync
            def _(eng):
                for t in self.prog['sp']:
                    t(eng)


class Tl:
    __slots__ = ('t', 'b')

    def __init__(self, t, name):
        self.t = t
        self.b = Buf(name)

    def __getitem__(self, k):
        return self.t[k]


class Builder:
    def __init__(self, T, layers, dbg=()):
        self.T = T
        self.NT = T // 128
        self.layers = list(layers)
        self.dbg = set(dbg)
        self.nc = bass.Bass("TRN2", target_bir_lowering=False)
        self.uid = 0
        self.rec = None
        self.dbg_out = {}

    def din(self, name, shape, dt=F32):
        return self.nc.dram_tensor(name, list(shape), dt, kind="ExternalInput").ap()

    def dscr(self, name, shape, dt, nbuf=1):
        kind = "ExternalOutput" if name in self.dbg else "Internal"
        t = self.nc.dram_tensor(name, list(shape), dt, kind=kind).ap()
        tl = Tl(t, name)
        if nbuf > 1:
            tl.b = [Buf(f'{name}{i}') for i in range(nbuf)]
        return tl

    def sb(self, es, name, shape, dt):
        self.uid += 1
        t = es.enter_context(self.nc.sbuf_tensor(f'{name}_{self.uid}', list(shape), dt))
        return Tl(t, name)

    def op_(self, e, fn, reads=(), writes=(), dma=False):
        if self.rec is not None:
            self.rec.append((e, fn, tuple(reads), tuple(writes), dma))
        else:
            self.S.op(e, fn, reads=reads, writes=writes, dma=dma)

    def chains_begin(self, names):
        self._chains = {k: [] for k in names}

    def chain(self, name, scr=None):
        self.rec = self._chains[name]
        self.scr = scr if scr is not None else self.scr0

    def chains_emit(self):
        self.rec = None
        self.scr = self.scr0
        lists = [l for l in self._chains.values() if l]
        idx = [0] * len(lists)
        left = sum(len(l) for l in lists)
        while left:
            for i, l in enumerate(lists):
                if idx[i] < len(l):
                    e, fn, R, W, dma = l[idx[i]]
                    idx[i] += 1
                    left -= 1
                    self.S.op(e, fn, reads=R, writes=W, dma=dma)

    def new_scr(self, es, tag, w):
        sc = {}
        for k in ('sq', 'tmp', 'ra', 'rb'):
            sc[k] = self.sb(es, f'sc_{k}_{tag}', [128, w], F32)
        for k in ('ssq', 'ln', 'rs'):
            sc[k] = self.sb(es, f'sc_{k}_{tag}', [128, 16], F32)
        return sc

    def mm(self, out, lhsT, rhs, start, stop, R, W):
        self.op_('pe', lambda e: e.matmul(out, lhsT=lhsT, rhs=rhs, start=start, stop=stop,
                                           skip_group_check=True), reads=R, writes=W)

    def tr(self, out, in_, ident, R, W):
        self.op_('pe', lambda e: e.transpose(out=out, in_=in_, identity=ident), reads=R, writes=W)

    def act(self, out, in_, func, R, W, bias=None, scale=None, accum=None):
        kw = {}
        if bias is not None:
            kw['bias'] = bias
        if scale is not None:
            kw['scale'] = scale
        if accum is not None:
            kw['accum_out'] = accum
        self.op_('act', lambda e: e.activation(out=out, in_=in_, func=func, **kw), reads=R, writes=W)

    def tt(self, eng, out, in0, in1, op, R, W):
        self.op_(eng, lambda e: e.tensor_tensor(out=out, in0=in0, in1=in1, op=op), reads=R, writes=W)

    def ts(self, eng, out, in0, s1, op0, R, W, s2=None, op1=None, accum=None):
        kw = {}
        if op1 is not None:
            kw['op1'] = op1
        if accum is not None:
            kw['accum_out'] = accum
        self.op_(eng, lambda e: e.tensor_scalar(out=out, in0=in0, scalar1=s1, scalar2=s2, op0=op0, **kw),
                  reads=R, writes=W)

    def stt(self, out, in0, scalar, in1, op0, op1, R, W):
        self.op_('dve', lambda e: e.scalar_tensor_tensor(out=out, in0=in0, scalar=scalar, in1=in1,
                                                         op0=op0, op1=op1), reads=R, writes=W)

    def cp(self, eng, out, in_, R, W):
        if eng == 'act':
            self.op_('act', lambda e: e.copy(out=out, in_=in_), reads=R, writes=W)
        else:
            self.op_(eng, lambda e: e.tensor_copy(out=out, in_=in_), reads=R, writes=W)

    def red(self, out, in_, op, R, W):
        self.op_('dve', lambda e: e.tensor_reduce(out=out, in_=in_, axis=AX.X, op=op), reads=R, writes=W)

    def memset(self, eng, ap, val, W):
        self.op_(eng, lambda e: e.memset(ap, val), writes=W)

    def dma(self, q, out, in_, R, W, **kw):
        self.op_(q, lambda e: e.dma_start(out=out, in_=in_, **kw), reads=R, writes=W, dma=True)

    def bc_load(self, es, name, row_ap, d):
        t = self.sb(es, name, [128, d], F32)
        self.dma('sp', t[:], row_ap.to_broadcast([128, d]), [], [t.b])
        return t

    def wload(self, w, k, src, c0, c1):
        c = c0
        while c < c1:
            ce = min(c1, c + 2048)
            self.dma('pool', w[:, k, c:ce], src[:, c:ce], [], [w.b])
            c = ce

    def rstd(self, ssq, H, d, R):
        ln, rs = self.scr['ln'], self.scr['rs']
        self.act(ln[:, 0:H], ssq, AF.Ln, R, [ln.b], bias=self.eps_c[:, 0:1], scale=1.0 / d)
        self.act(rs[:, 0:H], ln[:, 0:H], AF.Exp, [ln.b], [rs.b], scale=-0.5)
        return rs[:, 0:H]

    def rmsn(self, src, H, d, g, dst, R, W):
        sq, ssq, tmp = self.scr['sq'], self.scr['ssq'], self.scr['tmp']
        sqv = sq[:, 0:H * d].rearrange("p (h c) -> p h c", h=H)
        self.tt('pool', sqv, src, src, ALU.mult, R, [sq.b])
        self.red(ssq[:, 0:H], sqv, ALU.add, [sq.b], [ssq.b])
        rs = self.rstd(ssq[:, 0:H], H, d, [ssq.b])
        tv = tmp[:, 0:H * d].rearrange("p (h c) -> p h c", h=H)
        self.tt('dve', tv, src, rs.unsqueeze(2).to_broadcast([128, H, d]), ALU.mult,
                list(R) + [self.scr['rs'].b], [tmp.b])
        self.tt('pool', dst, tv, g[:].unsqueeze(1).to_broadcast([128, H, d]), ALU.mult,
                [tmp.b, g.b], W)

    def rope(self, src, H, d2, n, dst, R, W):
        tab = self.rope32 if d2 == 32 else self.rope16
        cosv = tab[:, n, 0:d2]
        sinv = tab[:, n, d2:2 * d2]
        A, Bm = self.scr['ra'], self.scr['rb']
        s4 = src.rearrange("p h (two c) -> p h two c", two=2)
        d4 = dst.rearrange("p h (two c) -> p h two c", two=2)
        Av = A[:, 0:H * 2 * d2].rearrange("p (h two c) -> p h two c", h=H, two=2)
        Bv = Bm[:, 0:H * 2 * d2].rearrange("p (h two c) -> p h two c", h=H, two=2)
        cb = cosv.unsqueeze(1).unsqueeze(1).to_broadcast([128, H, 2, d2])
        sbv = sinv.unsqueeze(1).unsqueeze(1).to_broadcast([128, H, 2, d2])
        self.tt('dve', Av, s4, cb, ALU.mult, list(R) + [tab.b], [A.b])
        self.tt('pool', Bv, s4, sbv, ALU.mult, list(R) + [tab.b], [Bm.b])
        self.tt('dve', d4[:, :, 0, :], Av[:, :, 0, :], Bv[:, :, 1, :], ALU.subtract, [A.b, Bm.b], W)
        self.tt('pool', d4[:, :, 1, :], Bv[:, :, 0, :], Av[:, :, 1, :], ALU.add, [A.b, Bm.b], W)

    def attn_block(self, rhs_q, q_R, N, tiles, scale, acc_views, acc_bufs, mode='exp', LA=2, epilogue=None):
        started = set()
        nt = len(tiles)
        last_use = {}
        for i, tl in enumerate(tiles):
            for j in tl['subs']:
                last_use[j] = i
        Atiles = {}

        def emit_s(i):
            tl = tiles[i]
            bi = self.st_rr % len(self.st_banks)
            self.st_rr += 1
            bank, bb = self.st_banks[bi]
            c0 = tl['c0']
            masks = tl.get('masks', [])
            if mode != 'exp':
                E, eR = tl['Efn']()
            self.mm(bank[:, c0:N], tl['kT'], rhs_q[:, c0:N], True, len(masks) == 0, list(q_R) + tl['R'], [bb])
            for mi, (ml, mr, off, ncols, mR) in enumerate(masks):
                self.mm(bank[:, off:off + ncols], ml, mr, False, mi == len(masks) - 1, mR, [bb])
            ai = self.at_rr % len(self.at_tiles)
            self.at_rr += 1
            A = self.at_tiles[ai]
            Atiles[i] = A
            if mode == 'exp':
                self.act(A[:, c0:N], bank[:, c0:N], AF.Exp, [bb], [A.b], scale=scale)
            else:
                self.tt('dve', A[:, c0:N], bank[:, c0:N], E[:, c0:N], ALU.mult, [bb] + eR, [A.b])
                if tl.get('diag') is not None:
                    dj = tl['diag']
                    self.tt('pool', A[:, dj * 128:(dj + 1) * 128], A[:, dj * 128:(dj + 1) * 128],
                            self.tri01T[:], ALU.mult, [A.b, self.tri01T.b], [A.b])

        def emit_pv(i):
            tl = tiles[i]
            A = Atiles.pop(i)
            for j in tl['subs']:
                view, bk = acc_views[j]
                st = bk not in started
                started.add(bk)
                self.mm(view, A[:, j * 128:(j + 1) * 128], tl['V'], st, last_use[j] == i,
                        [A.b] + tl['R'], [acc_bufs[bk]])

        for step in range(nt):
            emit_s(step)
            self.pend.append(('pv', (lambda i=step: emit_pv(i))))
            self.npv += 1
            self._drain(LA)
        if epilogue is None:
            self.attn_flush()
            return None
        self.epi_id += 1
        eid = self.epi_id
        self.pend.append(('epi', epilogue, eid))
        self._drain(LA)
        return eid

    def _drain(self, limit):
        q = self.pend
        while q and (q[0][0] == 'epi' or self.npv > limit):
            it = q.pop(0)
            if it[0] == 'pv':
                self.npv -= 1
                it[1]()
            else:
                it[1]()
                self.epi_done.add(it[2])

    def attn_flush(self):
        self._drain(-1)

    def attn_sync(self, eid):
        while eid is not None and eid not in self.epi_done:
            q = self.pend
            it = q.pop(0)
            if it[0] == 'pv':
                self.npv -= 1
                it[1]()
            else:
                it[1]()
                self.epi_done.add(it[2])

    def build(self):
        nc = self.nc
        T, NT = self.T, self.NT
        with ExitStack() as es:
            self.S = S = Sched(nc, es)
            self._decl_inputs()
            self._decl_scratch()
            self.ps = []
            for i in range(8):
                t = es.enter_context(nc.psum_tensor(f"psb{i}", [128, 512], F32))
                self.ps.append(Tl(t, f'ps{i}'))
            self._consts(es)
            self._prep(es)
            S.barrier()
            xin = Tl(self.x_in, 'xin')
            xin.b = [Buf('xin')] * 1
            cur = xin
            for idx, L in enumerate(self.layers):
                last = idx == len(self.layers) - 1
                dst = self.out_t if last else self.xs[idx % 2]
                with ExitStack() as les:
                    if L % 2 == 0:
                        self.layer_even(les, L, cur, dst)
                    else:
                        self.layer_odd(les, L, cur, dst)
                S.barrier()
                cur = dst
            S.barrier()
            S.emit()
        return nc

    def _decl_inputs(self):
        T, NT = self.T, self.NT
        d = self.din
        self.x_in = d("x", [T, D])
        self.mem = d("mem", [256, D])
        self.pos_t = d("pos_t", [128, NT], I32)
        self.w = {}
        spec = dict(
            ln_g=[4, D], mem_norm_g=[4, D], mem_w_kv=[4, D, 512], mem_q_norm_g=[4, 64], mem_k_norm_g=[4, 64],
            w_out=[4, 1280, D], even_w_in=[2, D, EVEN_COLS], mla_q_lat_g=[2, 256], mla_kv_lat_g=[2, 128],
            mla_w_uq=[2, 256, 768], mla_w_ukv=[2, 128, 1024], mla_q_norm_g=[2, 96], mla_k_norm_g=[2, 96],
            nsa_q_norm_g=[2, 64], nsa_k_norm_g=[2, 3, 64], nsa_cmp_posT=[2, 2, 64, 32],
            nsa_cmp_w1=[2, 2, 2048, 64], nsa_cmp_w2=[2, 2, 64, 64], odd_w_in=[2, D, ODD_COLS],
            dsa_q_norm_g=[2, 64], dsa_k_norm_g=[2, 64], mlstm_conv_wT=[2, 512, 4], mlstm_conv_b=[2, 512],
            mlstm_i_bias=[2, 4], mlstm_f_bias=[2, 4], mlstm_h_norm_g=[2, 128])
        for k, shp in spec.items():
            self.w[k] = d(k, shp)
        ncp = self.ncmp_pad = max(1, T // 2048) * 128
        nb = self.n_blk = T // 64
        self.c = dict(
            ident=d("c_ident", [128, 128], BF16), identf=d("c_identf", [128, 128], F32),
            i4=d("c_i4", [128, 512], BF16), i8x2=d("c_i8x2", [128, 512], BF16),
            tri_neg=d("c_tri_neg", [128, 128], BF16), edge_neg=d("c_edge_neg", [128, 128], BF16),
            tri01T=d("c_tri01T", [128, 128], BF16), trinegf=d("c_trinegf", [128, 128], F32),
            invf=d("c_invf", [128, 32]), pw=d("c_pw", [128, 40]), cmpneg=d("c_cmpneg", [T, ncp], BF16),
            forced=d("c_forced", [T, nb]), overlap=d("c_overlap", [ncp, nb], BF16))

    def _decl_scratch(self):
        T, NT = self.T, self.NT
        s = self.dscr
        self.out_t = Tl(self.nc.dram_tensor("out", [T, D], F32, kind="ExternalOutput").ap(), 'out')
        self.out_t.b = [Buf(f'out{i}') for i in range(NT)]
        self.xs = [s("xs0", [T, D], F32, NT), s("xs1", [T, D], F32, NT)]
        self.rope_d = s("rope_d", [T, 64], F32)
        self.Y = s("Y", [T, 2304], F32)
        self.G = s("G", [T, 1792], BF16, NT)
        self.SG = s("SG", [T, 32], F32, NT)
        self.mla_qT = s("mla_qT", [8, 96, T], BF16)
        self.mla_kT = s("mla_kT", [8, 96, T], BF16)
        self.mla_v = s("mla_v", [8, 128, NT, 65], BF16)
        self.nsa_qT = s("nsa_qT", [2, NT, 64, 4, 128], BF16)
        self.nsa_kT = s("nsa_kT", [8, 64, T], BF16)
        self.nsa_v = s("nsa_v", [4, 128, NT, 65], BF16)
        self.mem_qT = s("mem_qT", [4, 64, T], BF16)
        self.dsa_qT = s("dsa_qT", [NT, 64, 8, 128], BF16)
        self.dsa_kT = s("dsa_kT", [64, T], BF16)
        self.dsa_v = s("dsa_v", [128, NT, 65], BF16)
        self.idx_qT = s("idx_qT", [8, 32, T], BF16)
        self.idx_kT = s("idx_kT", [32, T], BF16)
        self.idx_w = s("idx_w", [T, 8], F32)
        self.ml_raw = s("ml_raw", [512, T], F32)
        self.ml_if = s("ml_if", [8, T], F32)
        self.ml_qkT = s("ml_qkT", [512, T], BF16)
        self.ml_v = s("ml_v", [4, 128, NT, 129], BF16)
        self.ml_g = s("ml_g", [12, T], F32)

    def _consts(self, es):
        c = self.c
        def ld(name, shape, dt):
            t = self.sb(es, name, shape, dt)
            self.dma('sp', t[:], c[name], [], [t.b])
            return t
        self.ident = ld('ident', [128, 128], BF16)
        self.identf = ld('identf', [128, 128], F32)
        self.i4 = ld('i4', [128, 512], BF16)
        self.i8x2 = ld('i8x2', [128, 512], BF16)
        self.tri_neg = ld('tri_neg', [128, 128], BF16)
        self.edge_neg = ld('edge_neg', [128, 128], BF16)
        self.tri01T = ld('tri01T', [128, 128], BF16)
        self.trinegf = ld('trinegf', [128, 128], F32)
        self.invf = ld('invf', [128, 32], F32)
        self.pw = ld('pw', [128, 40], F32)
        self.eps_c = self.sb(es, 'eps_c', [128, 1], F32)
        self.memset('dve', self.eps_c[:], EPS, [self.eps_c.b])
        self.one_c = self.sb(es, 'one_c', [128, 1], F32)
        self.memset('dve', self.one_c[:], 1.0, [self.one_c.b])
        self.rope32 = self.sb(es, 'rope32', [128, self.NT, 64], F32)
        self.rope16 = self.sb(es, 'rope16', [128, self.NT, 32], F32)
        self.sc_sq = self.sb(es, 'sc_sq', [128, 512], F32)
        self.sc_tmp = self.sb(es, 'sc_tmp', [128, 512], F32)
        self.sc_ra = self.sb(es, 'sc_ra', [128, 512], F32)
        self.sc_rb = self.sb(es, 'sc_rb', [128, 512], F32)
        self.sc_ssq = self.sb(es, 'sc_ssq', [128, 16], F32)
        self.sc_ln = self.sb(es, 'sc_ln', [128, 16], F32)
        self.sc_rs = self.sb(es, 'sc_rs', [128, 16], F32)
        self.scr0 = dict(sq=self.sc_sq, tmp=self.sc_tmp, ra=self.sc_ra, rb=self.sc_rb, ssq=self.sc_ssq,
                         ln=self.sc_ln, rs=self.sc_rs)
        self.scr = self.scr0

    def _prep(self, es0):
        NT = self.NT
        PI = math.pi
        with ExitStack() as es:
            pi_t = self.sb(es, 'pos_i', [128, NT], I32)
            self.dma('sp', pi_t[:], self.pos_t, [], [pi_t.b])
            pf = self.sb(es, 'pos_f', [128, NT], F32)
            self.cp('dve', pf[:], pi_t[:], [pi_t.b], [pf.b])
            ang = self.sb(es, 'ang', [128, NT, 32], F32)
            self.tt('dve', ang[:], pf[:].unsqueeze(2).to_broadcast([128, NT, 32]),
                    self.invf[:].unsqueeze(1).to_broadcast([128, NT, 32]), ALU.mult,
                    [pf.b, self.invf.b], [ang.b])
            kf = self.sb(es, 'kf', [128, NT, 32], F32)
            ki = self.sb(es, 'ki', [128, NT, 32], I32)
            r = self.sb(es, 'r', [128, NT, 32], F32)
            m = self.sb(es, 'm', [128, NT, 32], F32)

            def wrap(buf):
                self.ts('dve', m[:], buf[:], PI, ALU.is_gt, [buf.b], [m.b], s2=-2 * PI, op1=ALU.mult)
                self.tt('dve', buf[:], buf[:], m[:], ALU.add, [buf.b, m.b], [buf.b])
                self.ts('dve', m[:], buf[:], -PI, ALU.is_lt, [buf.b], [m.b], s2=2 * PI, op1=ALU.mult)
                self.tt('dve', buf[:], buf[:], m[:], ALU.add, [buf.b, m.b], [buf.b])
                self.ts('dve', buf[:], buf[:], PI, ALU.min, [buf.b], [buf.b], s2=-PI, op1=ALU.max)
            self.ts('dve', kf[:], ang[:], 1.0 / (2 * PI), ALU.mult, [ang.b], [kf.b])
            self.cp('dve', ki[:], kf[:], [kf.b], [ki.b])
            self.cp('dve', kf[:], ki[:], [ki.b], [kf.b])
            C1 = 6.28125
            C2 = 2 * PI - C1
            self.stt(r[:], kf[:], -C1, ang[:], ALU.mult, ALU.add, [kf.b, ang.b], [r.b])
            self.stt(r[:], kf[:], -C2, r[:], ALU.mult, ALU.add, [kf.b, r.b], [r.b])
            wrap(r)
            self.act(self.rope32[:, :, 32:64], r[:], AF.Sin, [r.b], [self.rope32.b])
            self.ts('dve', r[:], r[:], PI / 2, ALU.add, [r.b], [r.b])
            wrap(r)
            self.act(self.rope32[:, :, 0:32], r[:], AF.Sin, [r.b], [self.rope32.b])
            self.cp('dve', self.rope16[:, :, 0:16], self.rope32[:, :, 0:32:2], [self.rope32.b], [self.rope16.b])
            self.cp('dve', self.rope16[:, :, 16:32], self.rope32[:, :, 32:64:2], [self.rope32.b], [self.rope16.b])
            self.dma('sp', self.rope_d[:].rearrange("(n p) c -> p n c", p=128), self.rope32[:],
                     [self.rope32.b], [self.rope_d.b])
            self.S.barrier()

    def load_win(self, es, L, w_in, ncols):
        win = self.sb(es, 'win', [128, 8, ncols], BF16)
        for k in range(8):
            self.wload(win, k, w_in[k * 128:(k + 1) * 128, :], 0, ncols)
        lng = self.bc_load(es, 'lng', self.w['ln_g'][L:L + 1, :], D)
        return win, lng

    def load_wout(self, es, L):
        wout = self.sb(es, 'wout', [128, 10, D], BF16)
        for k in range(10):
            self.wload(wout, k, self.w['w_out'][L, k * 128:(k + 1) * 128, :], 0, D)
        return wout

    def mem_kv(self, es, L):
        w = self.w
        kT = self.sb(es, 'memkT', [64, 4, 256], BF16)
        V = self.sb(es, 'memV', [128, 2, 4, 65], BF16)
        self.memset('pool', V[:], 1.0, [V.b])
        with ExitStack() as s2:
            wkv = self.sb(s2, 'wkv', [128, 8, 512], BF16)
            for k in range(8):
                self.wload(wkv, k, w['mem_w_kv'][L, k * 128:(k + 1) * 128, :], 0, 512)
            mg = self.bc_load(s2, 'mg', w['mem_norm_g'][L:L + 1, :], D)
            kg = self.bc_load(s2, 'kg', w['mem_k_norm_g'][L:L + 1, :], 64)
            mt = self.sb(s2, 'mt', [128, D], F32)
            junk = self.sb(s2, 'junk', [128, D], BF16)
            mh = self.sb(s2, 'mh', [128, D], BF16)
            mhT = self.sb(s2, 'mhT', [128, 8, 128], BF16)
            kv = self.sb(s2, 'kv', [128, 512], F32)
            kn = self.sb(s2, 'kn', [128, 256], BF16)
            ss = self.sb(s2, 'ss', [128, 1], F32)
            for i in range(2):
                self.dma('sp', mt[:], self.mem[i * 128:(i + 1) * 128, :], [], [mt.b])
                self.act(junk[:], mt[:], AF.Square, [mt.b], [junk.b, ss.b], accum=ss[:])
                rs = self.rstd(ss[:, 0:1], 1, D, [ss.b])
                self.stt(mh[:], mt[:], rs, mg[:], ALU.mult, ALU.mult, [mt.b, self.scr['rs'].b, mg.b], [mh.b])
                pT = self.ps[2]
                pTb = pT[:].bitcast(BF16)
                for k in range(8):
                    self.tr(pTb[:, k * 128:(k + 1) * 128], mh[:, k * 128:(k + 1) * 128], self.ident[:],
                            [mh.b, self.ident.b], [pT.b])
                self.cp('act', mhT[:].rearrange("p k c -> p (k c)"), pTb[:, 0:1024], [pT.b], [mhT.b])
                pU = self.ps[0]
                for k in range(8):
                    self.mm(pU[:, 0:512], mhT[:, k, :], wkv[:, k, :], k == 0, k == 7, [mhT.b, wkv.b], [pU.b])
                self.cp('act', kv[:], pU[:, 0:512], [pU.b], [kv.b])
                self.rmsn(kv[:, 0:256].rearrange("p (h c) -> p h c", h=4), 4, 64, kg,
                          kn[:].rearrange("p (h c) -> p h c", h=4), [kv.b], [kn.b])
                self.cp('dve', V[:, i, :, 0:64], kv[:, 256:512].rearrange("p (h c) -> p h c", h=4), [kv.b], [V.b])
                pK = self.ps[3]
                pKb = pK[:].bitcast(BF16)
                for h in range(4):
                    self.tr(pKb[0:64, h * 128:(h + 1) * 128], kn[:, h * 64:(h + 1) * 64], self.ident[:],
                            [kn.b, self.ident.b], [pK.b])
                self.cp('act', kT[:, :, i * 128:(i + 1) * 128],
                        pKb[0:64, 0:512].rearrange("p (h c) -> p h c", h=4), [pK.b], [kT.b])
            self.S.barrier()
        return kT, V

    def p1_front(self, n, x_src, xt, ht, hT, u, ss, junk, lng, win, ncols):
        sl = n % 2
        x_t, h_t, hT_t, u_t = xt[sl], ht[sl], hT[sl], u[sl]
        xb = x_src.b[n] if len(x_src.b) > 1 else x_src.b[0]
        self.dma('sp', x_t[:], x_src[n * 128:(n + 1) * 128, :], [xb], [x_t.b])
        self.act(junk[:], x_t[:], AF.Square, [x_t.b], [junk.b, ss.b], accum=ss[:])
        rs = self.rstd(ss[:, 0:1], 1, D, [ss.b])
        self.stt(h_t[:], x_t[:], rs, lng[:], ALU.mult, ALU.mult, [x_t.b, self.scr['rs'].b, lng.b], [h_t.b])
        pT = self.ps[2]
        pTb = pT[:].bitcast(BF16)
        for k in range(8):
            self.tr(pTb[:, k * 128:(k + 1) * 128], h_t[:, k * 128:(k + 1) * 128], self.ident[:],
                    [h_t.b, self.ident.b], [pT.b])
        self.cp('act', hT_t[:].rearrange("p k c -> p (k c)"), pTb[:, 0:1024], [pT.b], [hT_t.b])
        nchunk = (ncols + 511) // 512
        for c in range(nchunk):
            c0 = c * 512
            wd = min(512, ncols - c0)
            pU = self.ps[c % 2]
            for k in range(8):
                self.mm(pU[:, 0:wd], hT_t[:, k, :], win[:, k, c0:c0 + wd], k == 0, k == 7,
                        [hT_t.b, win.b], [pU.b])
            self.cp('act' if c % 2 == 0 else 'dve', u_t[:, c0:c0 + wd], pU[:, 0:wd], [pU.b], [u_t.b])
        return u_t

    def transposes_out(self, srcs, rows, stage, dst_ap, dst_b, pidx):
        pT = self.ps[pidx]
        pTb = pT[:].bitcast(BF16)
        k = len(srcs)
        for i, (ap, R) in enumerate(srcs):
            self.tr(pTb[0:rows, i * 128:(i + 1) * 128], ap, self.ident[:], list(R) + [self.ident.b], [pT.b])
        self.cp('act', stage[0:rows, 0:k, :], pTb[0:rows, 0:k * 128].rearrange("p (k c) -> p k c", k=k),
                [pT.b], [stage.b])
        self.dma('sp', dst_ap, stage[0:rows, 0:k, :], [stage.b], dst_b)

    def p3(self, es, L, x_src, x_dst, wout, mixfn):
        NT = self.NT
        xt = [self.sb(es, f'p3x{i}', [128, D], F32) for i in range(2)]
        mixT = [self.sb(es, f'p3mT{i}', [128, 10, 128], BF16) for i in range(2)]
        xo = [self.sb(es, f'p3o{i}', [128, D], F32) for i in range(2)]
        for n in range(NT):
            sl = n % 2
            mix = mixfn(n)
            xb = x_src.b[n] if len(x_src.b) > 1 else x_src.b[0]
            self.dma('sp', xt[sl][:], x_src[n * 128:(n + 1) * 128, :], [xb], [xt[sl].b])
            for half in range(2):
                pT = self.ps[2 + half]
                pTb = pT[:].bitcast(BF16)
                for k in range(5):
                    kk = half * 5 + k
                    self.tr(pTb[:, k * 128:(k + 1) * 128], mix[:, kk * 128:(kk + 1) * 128], self.ident[:],
                            [mix.b, self.ident.b], [pT.b])
                self.cp('act', mixT[sl][:, half * 5:half * 5 + 5, :].rearrange("p k c -> p (k c)"),
                        pTb[:, 0:640], [pT.b], [mixT[sl].b])
            for c in range(2):
                pU = self.ps[c]
                for k in range(10):
                    self.mm(pU[:, 0:512], mixT[sl][:, k, :], wout[:, k, c * 512:(c + 1) * 512], k == 0, k == 9,
                            [mixT[sl].b, wout.b], [pU.b])
                self.tt('dve', xo[sl][:, c * 512:(c + 1) * 512], pU[:, 0:512], xt[sl][:, c * 512:(c + 1) * 512],
                        ALU.add, [pU.b, xt[sl].b], [xo[sl].b])
            self.dma('sp', x_dst[n * 128:(n + 1) * 128, :], xo[sl][:], [xo[sl].b], [x_dst.b[n]])

    def attn_setup(self, es, dvp_two_banks=False):
        self.st_banks = [(self.ps[i], self.ps[i].b) for i in range(3)]
        self.st_rr = 0
        self.at_tiles = [self.sb(es, f'At{i}', [128, 512], BF16) for i in range(3)]
        self.at_rr = 0
        self.pend = []
        self.npv = 0
        self.epi_id = 0
        self.epi_done = set()

    def mem_attn(self, es, memkT, memV):
        T = self.T
        qT = [self.sb(es, f'mqT{i}', [64, T], BF16) for i in range(2)]
        ost = [self.sb(es, f'most{i}', [128, 4, 64], F32) for i in range(2)]
        rec = self.sb(es, 'mrec', [128, 4], F32)
        blk = 0
        for h in range(4):
            q = qT[h % 2]
            self.dma('sp', q[:], self.mem_qT[h], [self.mem_qT.b], [q.b])
            for cq in range(T // 512):
                set_i = blk % 2
                blk += 1
                accb = self.ps[3 + set_i]
                views = {j: (accb[:, j * 65:(j + 1) * 65], 0) for j in range(4)}
                tiles = [dict(kT=memkT[:, h, kt * 128:(kt + 1) * 128], V=memV[:, kt, h, :],
                              R=[memkT.b, memV.b], c0=0, subs=[0, 1, 2, 3]) for kt in range(2)]
                def epi(accb=accb, o=ost[set_i], cq=cq, h=h):
                    av = accb[:, 0:260].rearrange("p (j c) -> p j c", j=4)
                    self.op_('dve', lambda e, av=av: e.reciprocal(out=rec[:], in_=av[:, :, 64]), reads=[accb.b],
                             writes=[rec.b])
                    self.tt('dve', o[:], av[:, :, 0:64], rec[:].unsqueeze(2).to_broadcast([128, 4, 64]), ALU.mult,
                            [accb.b, rec.b], [o.b])
                    self.dma('sp', self.Y[cq * 512:(cq + 1) * 512, 1024 + h * 64:1024 + (h + 1) * 64]
                             .rearrange("(j p) c -> p j c", p=128), o[:], [o.b], [self.Y.b])
                self.attn_block(q[:, cq * 512:(cq + 1) * 512], [q.b], 512, tiles, 0.125, views, [accb.b], epilogue=epi)
        self.attn_flush()

    def layer_even(self, es, L, x_src, x_dst):
        li = L // 2
        w = self.w
        T, NT = self.T, self.NT
        S = self.S
        memkT, memV = self.mem_kv(es, L)
        with ExitStack() as p1:
            win, lng = self.load_win(p1, L, w['even_w_in'][li], EVEN_COLS)
            wuq = self.sb(p1, 'wuq', [128, 2, 768], BF16)
            for k in range(2):
                self.wload(wuq, k, w['mla_w_uq'][li, k * 128:(k + 1) * 128, :], 0, 768)
            wukv = self.sb(p1, 'wukv', [128, 1, 1024], BF16)
            self.wload(wukv, 0, w['mla_w_ukv'][li], 0, 1024)
            g_ql = self.bc_load(p1, 'g_ql', w['mla_q_lat_g'][li:li + 1, :], 256)
            g_kvl = self.bc_load(p1, 'g_kvl', w['mla_kv_lat_g'][li:li + 1, :], 128)
            g_qn = self.bc_load(p1, 'g_qn', w['mla_q_norm_g'][li:li + 1, 0:64], 64)
            g_qp = self.bc_load(p1, 'g_qp', w['mla_q_norm_g'][li:li + 1, 64:96], 32)
            g_kn = self.bc_load(p1, 'g_kn', w['mla_k_norm_g'][li:li + 1, 0:64], 64)
            g_kp = self.bc_load(p1, 'g_kp', w['mla_k_norm_g'][li:li + 1, 64:96], 32)
            g_bq = self.bc_load(p1, 'g_bq', w['nsa_q_norm_g'][li:li + 1, :], 64)
            g_ks = self.bc_load(p1, 'g_ks', w['nsa_k_norm_g'][li, 1:2, :], 64)
            g_kw = self.bc_load(p1, 'g_kw', w['nsa_k_norm_g'][li, 2:3, :], 64)
            g_mq = self.bc_load(p1, 'g_mq', w['mem_q_norm_g'][L:L + 1, :], 64)
            xt = [self.sb(p1, f'xt{i}', [128, D], F32) for i in range(2)]
            ht = [self.sb(p1, f'ht{i}', [128, D], BF16) for i in range(2)]
            hT = [self.sb(p1, f'hT{i}', [128, 8, 128], BF16) for i in range(2)]
            u = [self.sb(p1, f'u{i}', [128, EVEN_COLS], F32) for i in range(2)]
            ss = self.sb(p1, 'ss', [128, 1], F32)
            junk = self.sb(p1, 'junk', [128, D], BF16)
            latn = self.sb(p1, 'latn', [128, 384], BF16)
            latT = self.sb(p1, 'latT', [128, 3, 128], BF16)
            qsb = self.sb(p1, 'qsb', [128, 768], F32)
            kvsb = self.sb(p1, 'kvsb', [128, 1024], F32)
            qpe = self.sb(p1, 'qpe', [128, 256], F32)
            kpe = self.sb(p1, 'kpe', [128, 32], F32)
            kpeb = self.sb(p1, 'kpeb', [128, 32], BF16)
            qf = self.sb(p1, 'qf', [128, 8, 96], BF16)
            kfm = self.sb(p1, 'kfm', [128, 8, 96], BF16)
            vaug = self.sb(p1, 'vaug', [128, 8, 65], BF16)
            self.memset('pool', vaug[:], 1.0, [vaug.b])
            bqn = self.sb(p1, 'bqn', [128, 512], F32)
            bqf = self.sb(p1, 'bqf', [128, 512], BF16)
            kn2 = self.sb(p1, 'kn2', [128, 128], F32)
            kmisc = self.sb(p1, 'kmisc', [128, 8, 64], BF16)
            nv = self.sb(p1, 'nv', [128, 4, 65], BF16)
            self.memset('pool', nv[:], 1.0, [nv.b])
            mqf = self.sb(p1, 'mqf', [128, 256], BF16)
            gt = self.sb(p1, 'gt', [128, 1280], BF16)
            sg = self.sb(p1, 'sg', [128, 24], F32)
            stq = self.sb(p1, 'stq', [96, 8, 128], BF16)
            stk = self.sb(p1, 'stk', [96, 8, 128], BF16)
            stb = self.sb(p1, 'stb', [64, 8, 128], BF16)
            stm = self.sb(p1, 'stm', [64, 8, 128], BF16)
            stmq = self.sb(p1, 'stmq', [64, 4, 128], BF16)
            scrA = self.new_scr(p1, 'A', 512)
            scrB = self.new_scr(p1, 'B', 512)
            scrC = self.new_scr(p1, 'C', 128)
            ut_next = self.p1_front(0, x_src, xt, ht, hT, u, ss, junk, lng, win, EVEN_COLS)
            for n in range(NT):
                ut = ut_next
                U = lambda a, b_, ut=ut: ut[:, a:b_]
                ub = [ut.b]
                tsl = slice(n * 128, (n + 1) * 128)
                self.chains_begin(['A', 'F', 'B', 'C', 'M', 'G'])
                if n + 1 < NT:
                    self.chain('F')
                    ut_next = self.p1_front(n + 1, x_src, xt, ht, hT, u, ss, junk, lng, win, EVEN_COLS)
                self.chain('G')
                self.act(gt[:, 0:512], U(416, 928), AF.Silu, ub, [gt.b])
                self.act(gt[:, 512:1024], U(2232, 2744), AF.Silu, ub, [gt.b])
                self.act(gt[:, 1024:1280], U(3000, 3256), AF.Silu, ub, [gt.b])
                self.dma('sp', self.G[tsl, 0:1280], gt[:], [gt.b], [self.G.b[n]])
                self.act(sg[:], U(2208, 2232), AF.Sigmoid, ub, [sg.b])
                self.dma('sp', self.SG[tsl, 0:24], sg[:], [sg.b], [self.SG.b[n]])
                self.chain('A', scrA)
                self.rmsn(U(0, 256).rearrange("p (h c) -> p h c", h=1), 1, 256, g_ql,
                          latn[:, 0:256].rearrange("p (h c) -> p h c", h=1), ub, [latn.b])
                self.rmsn(U(256, 384).rearrange("p (h c) -> p h c", h=1), 1, 128, g_kvl,
                          latn[:, 256:384].rearrange("p (h c) -> p h c", h=1), ub, [latn.b])
                pT = self.ps[3]
                pTb = pT[:].bitcast(BF16)
                for k in range(3):
                    self.tr(pTb[:, k * 128:(k + 1) * 128], latn[:, k * 128:(k + 1) * 128], self.ident[:],
                            [latn.b, self.ident.b], [pT.b])
                self.cp('act', latT[:].rearrange("p k c -> p (k c)"), pTb[:, 0:384], [pT.b], [latT.b])
                pQ, pQ2, pK, pK2 = self.ps[3], self.ps[4], self.ps[5], self.ps[4]
                for k in range(2):
                    self.mm(pQ[:, 0:512], latT[:, k, :], wuq[:, k, 0:512], k == 0, k == 1, [latT.b, wuq.b], [pQ.b])
                for k in range(2):
                    self.mm(pQ2[:, 0:256], latT[:, k, :], wuq[:, k, 512:768], k == 0, k == 1, [latT.b, wuq.b], [pQ2.b])
                self.cp('act', qsb[:, 0:512], pQ[:, 0:512], [pQ.b], [qsb.b])
                self.cp('dve', qsb[:, 512:768], pQ2[:, 0:256], [pQ2.b], [qsb.b])
                self.mm(pK[:, 0:512], latT[:, 2, :], wukv[:, 0, 0:512], True, True, [latT.b, wukv.b], [pK.b])
                self.mm(pK2[:, 0:512], latT[:, 2, :], wukv[:, 0, 512:1024], True, True, [latT.b, wukv.b], [pK2.b])
                self.cp('act', kvsb[:, 0:512], pK[:, 0:512], [pK.b], [kvsb.b])
                self.cp('dve', kvsb[:, 512:1024], pK2[:, 0:512], [pK2.b], [kvsb.b])
                q3 = qsb[:].rearrange("p (h c) -> p h c", h=8)
                kv3 = kvsb[:].rearrange("p (h c) -> p h c", h=8)
                self.rmsn(q3[:, :, 0:64], 8, 64, g_qn, qf[:, :, 0:64], [qsb.b], [qf.b])
                qpe3 = qpe[:].rearrange("p (h c) -> p h c", h=8)
                self.rmsn(q3[:, :, 64:96], 8, 32, g_qp, qpe3, [qsb.b], [qpe.b])
                self.rope(qpe3, 8, 16, n, qf[:, :, 64:96], [qpe.b], [qf.b])
                self.rmsn(kv3[:, :, 0:64], 8, 64, g_kn, kfm[:, :, 0:64], [kvsb.b], [kfm.b])
                self.cp('dve', vaug[:, :, 0:64], kv3[:, :, 64:128], [kvsb.b], [vaug.b])
                kpe3 = kpe[:].rearrange("p (h c) -> p h c", h=1)
                self.rmsn(U(384, 416).rearrange("p (h c) -> p h c", h=1), 1, 32, g_kp, kpe3, ub, [kpe.b])
                self.rope(kpe3, 1, 16, n, kpeb[:].rearrange("p (h c) -> p h c", h=1), [kpe.b], [kpeb.b])
                self.cp('pool', kfm[:, :, 64:96], kpeb[:].unsqueeze(1).to_broadcast([128, 8, 32]), [kpeb.b], [kfm.b])
                self.transposes_out([(qf[:, h, :], [qf.b]) for h in range(8)], 96, stq,
                                    self.mla_qT[:, :, tsl].rearrange("h d t -> d h t"), [self.mla_qT.b], 3)
                self.transposes_out([(kfm[:, h, :], [kfm.b]) for h in range(8)], 96, stk,
                                    self.mla_kT[:, :, tsl].rearrange("h d t -> d h t"), [self.mla_kT.b], 5)
                self.dma('sp', self.mla_v[:, :, n, :].rearrange("h p c -> p h c"), vaug[:], [vaug.b], [self.mla_v.b])
                self.chain('B', scrB)
                bq3 = bqn[:].rearrange("p (h c) -> p h c", h=8)
                self.rmsn(U(928, 1440).rearrange("p (h c) -> p h c", h=8), 8, 64, g_bq, bq3, ub, [bqn.b])
                self.rope(bq3, 8, 32, n, bqf[:].rearrange("p (h c) -> p h c", h=8), [bqn.b], [bqf.b])
                self.chain('C', scrC)
                k23 = kn2[:].rearrange("p (h c) -> p h c", h=2)
                self.rmsn(U(1696, 1824).rearrange("p (h c) -> p h c", h=2), 2, 64, g_ks, k23, ub, [kn2.b])
                self.rope(k23, 2, 32, n, kmisc[:, 0:2, :], [kn2.b], [kmisc.b])
                self.rmsn(U(1952, 2080).rearrange("p (h c) -> p h c", h=2), 2, 64, g_kw, k23, ub, [kn2.b])
                self.rope(k23, 2, 32, n, kmisc[:, 2:4, :], [kn2.b], [kmisc.b])
                self.cp('pool', kmisc[:, 4:8, :], U(1440, 1696).rearrange("p (h c) -> p h c", h=4), ub, [kmisc.b])
                self.cp('dve', nv[:, 0:2, 0:64], U(1824, 1952).rearrange("p (h c) -> p h c", h=2), ub, [nv.b])
                self.cp('dve', nv[:, 2:4, 0:64], U(2080, 2208).rearrange("p (h c) -> p h c", h=2), ub, [nv.b])
                self.chain('B', scrB)
                self.transposes_out([(bqf[:, h * 64:(h + 1) * 64], [bqf.b]) for h in range(8)], 64, stb,
                                    self.nsa_qT[:, n].rearrange("g d r t -> d g r t"), [self.nsa_qT.b], 6)
                self.chain('C', scrC)
                self.transposes_out([(kmisc[:, i, :], [kmisc.b]) for i in range(8)], 64, stm,
                                    self.nsa_kT[:, :, tsl].rearrange("k d t -> d k t"), [self.nsa_kT.b], 7)
                self.dma('sp', self.nsa_v[:, :, n, :].rearrange("k p c -> p k c"), nv[:], [nv.b], [self.nsa_v.b])
                self.chain('B', scrB)
                self.rmsn(U(2744, 3000).rearrange("p (h c) -> p h c", h=4), 4, 64, g_mq,
                          mqf[:].rearrange("p (h c) -> p h c", h=4), ub, [mqf.b])
                self.transposes_out([(mqf[:, h * 64:(h + 1) * 64], [mqf.b]) for h in range(4)], 64, stmq,
                                    self.mem_qT[:, :, tsl].rearrange("h d t -> d h t"), [self.mem_qT.b], 6)
                self.chains_emit()
            S.barrier()
        kcmpT = self.sb(es, 'kcmpT', [64, 2, self.ncmp_pad], BF16)
        vcmp = self.sb(es, 'vcmp', [128, 2, self.ncmp_pad // 128, 129], BF16)
        self.nsa_compress(li, kcmpT, vcmp)
        S.barrier()
        with ExitStack() as pa:
            self.attn_setup(pa)
            self.mla_attn(pa)
            S.barrier()
        with ExitStack() as pa:
            self.attn_setup(pa)
            self.mem_attn(pa, memkT, memV)
            S.barrier()
        with ExitStack() as pa:
            self.attn_setup(pa)
            self.nsa_attn(pa, kcmpT, vcmp)
            S.barrier()
        with ExitStack() as p3:
            wout = self.load_wout(p3, L)
            yt = [self.sb(p3, f'yt{i}', [128, 2304], F32) for i in range(2)]
            gtt = [self.sb(p3, f'gtt{i}', [128, 1280], BF16) for i in range(2)]
            sgt = [self.sb(p3, f'sgt{i}', [128, 24], F32) for i in range(2)]
            mix = [self.sb(p3, f'mix{i}', [128, 1280], BF16) for i in range(2)]
            yb = self.sb(p3, 'yb', [128, 512], F32)
            yb2 = self.sb(p3, 'yb2', [128, 512], F32)

            def mixfn(n):
                sl = n % 2
                y, g, s_, m = yt[sl], gtt[sl], sgt[sl], mix[sl]
                tsl = slice(n * 128, (n + 1) * 128)
                self.dma('sp', y[:], self.Y[tsl, :], [self.Y.b], [y.b])
                self.dma('sp', g[:], self.G[tsl, 0:1280], [self.G.b[n]], [g.b])
                self.dma('sp', s_[:], self.SG[tsl, 0:24], [self.SG.b[n]], [s_.b])
                self.tt('dve', m[:, 0:512], y[:, 0:512], g[:, 0:512], ALU.mult, [y.b, g.b], [m.b])
                self.tt('pool', m[:, 1024:1280], y[:, 1024:1280], g[:, 1024:1280], ALU.mult, [y.b, g.b], [m.b])
                s3 = s_[:].rearrange("p (h c) -> p h c", c=3)
                y3 = lambda a: y[:, a:a + 512].rearrange("p (h c) -> p h c", h=8)
                b3 = yb[:].rearrange("p (h c) -> p h c", h=8)
                b23 = yb2[:].rearrange("p (h c) -> p h c", h=8)
                self.tt('dve', b3, y3(1280), s3[:, :, 0:1].to_broadcast([128, 8, 64]), ALU.mult, [y.b, s_.b], [yb.b])
                self.tt('pool', b23, y3(512), s3[:, :, 1:2].to_broadcast([128, 8, 64]), ALU.mult, [y.b, s_.b], [yb2.b])
                self.tt('dve', b3, b3, b23, ALU.add, [yb.b, yb2.b], [yb.b])
                self.tt('pool', b23, y3(1792), s3[:, :, 2:3].to_broadcast([128, 8, 64]), ALU.mult, [y.b, s_.b], [yb2.b])
                self.tt('dve', b3, b3, b23, ALU.add, [yb.b, yb2.b], [yb.b])
                self.tt('dve', m[:, 512:1024], yb[:], g[:, 512:1024], ALU.mult, [yb.b, g.b], [m.b])
                return m
            self.p3(p3, L, x_src, x_dst, wout, mixfn)
            S.barrier()

    def mla_attn(self, es):
        T, NT = self.T, self.NT
        qT = [self.sb(es, f'aqT{i}', [96, T], BF16) for i in range(2)]
        kT = [self.sb(es, f'akT{i}', [96, T], BF16) for i in range(2)]
        V = [self.sb(es, f'aV{i}', [128, NT, 65], BF16) for i in range(2)]
        ost = [self.sb(es, f'aost{i}', [128, 4, 64], F32) for i in range(2)]
        rec = self.sb(es, 'arec', [128, 4], F32)
        scale = 96 ** -0.5
        blk = 0
        for h in range(8):
            q, k, v = qT[h % 2], kT[h % 2], V[h % 2]
            self.dma('sp', q[:], self.mla_qT[h], [self.mla_qT.b], [q.b])
            self.dma('sp', k[:], self.mla_kT[h], [self.mla_kT.b], [k.b])
            self.dma('sp', v[:], self.mla_v[h], [self.mla_v.b], [v.b])
            for cq in range(T // 512):
                set_i = blk % 2
                blk += 1
                accb = self.ps[3 + set_i]
                views = {j: (accb[:, j * 65:(j + 1) * 65], 0) for j in range(4)}
                tiles = []
                for kt in range(4 * cq + 4):
                    vv = kt - 4 * cq
                    tl = dict(kT=k[:, kt * 128:(kt + 1) * 128], V=v[:, kt, :], R=[k.b, v.b],
                              c0=max(0, vv) * 128, subs=list(range(max(0, vv), 4)))
                    if vv >= 0:
                        tl['masks'] = [(self.tri_neg[:], self.ident[:], vv * 128, 128,
                                        [self.tri_neg.b, self.ident.b])]
                    tiles.append(tl)
                def epi(accb=accb, o=ost[set_i], cq=cq, h=h):
                    av = accb[:, 0:260].rearrange("p (j c) -> p j c", j=4)
                    self.op_('dve', lambda e, av=av: e.reciprocal(out=rec[:], in_=av[:, :, 64]), reads=[accb.b],
                             writes=[rec.b])
                    self.tt('dve', o[:], av[:, :, 0:64], rec[:].unsqueeze(2).to_broadcast([128, 4, 64]), ALU.mult,
                            [accb.b, rec.b], [o.b])
                    self.dma('sp', self.Y[cq * 512:(cq + 1) * 512, h * 64:(h + 1) * 64]
                             .rearrange("(j p) c -> p j c", p=128), o[:], [o.b], [self.Y.b])
                self.attn_block(q[:, cq * 512:(cq + 1) * 512], [q.b], 512, tiles, scale, views, [accb.b], epilogue=epi)
        self.attn_flush()

    def nsa_compress(self, li, kcmpT, vcmp):
        w = self.w
        T = self.T
        n_cmp = (T - 32) // 16 + 1
        ncp = self.ncmp_pad
        nct = ncp // 128
        self.memset('pool', kcmpT[:], 0.0, [kcmpT.b])
        self.memset('pool', vcmp[:], 0.0, [vcmp.b])
        with ExitStack() as es:
            w1 = self.sb(es, 'cw1', [64, 2, 32, 64], BF16)
            w2 = self.sb(es, 'cw2', [64, 2, 64], BF16)
            peT = self.sb(es, 'cpeT', [64, 2, 32], BF16)
            for kv in range(2):
                self.dma('pool', w1[:, kv], w['nsa_cmp_w1'][li, kv].rearrange("(l d) o -> d l o", d=64), [], [w1.b])
                self.dma('pool', w2[:, kv], w['nsa_cmp_w2'][li, kv], [], [w2.b])
                self.dma('pool', peT[:, kv], w['nsa_cmp_posT'][li, kv], [], [peT.b])
            g_kc = self.bc_load(es, 'g_kc', w['nsa_k_norm_g'][li, 0:1, :], 64)
            ovl = self.sb(es, 'ovl', [128, nct, self.n_blk], BF16)
            self.dma('sp', ovl[:], self.c['overlap'].rearrange("(k p) j -> p k j", p=128), [], [ovl.b])
            xT = [self.sb(es, f'cxT{i}', [64, T], BF16) for i in range(2)]
            bias = self.sb(es, 'cbias', [64, 1], F32)
            hid = self.sb(es, 'chid', [64, ncp], BF16)
            self.memset('pool', hid[:], 0.0, [hid.b])
            ctm = self.sb(es, 'ctm', [128, 64], F32)
            ctn = self.sb(es, 'ctn', [128, 64], F32)
            ctb = self.sb(es, 'ctb', [128, 64], BF16)
            rp = self.sb(es, 'crp', [128, 64], F32)
            it = 0
            for kv in range(2):
                for g in range(2):
                    x = xT[it % 2]
                    it += 1
                    self.dma('sp', x[:], self.nsa_kT[4 + kv * 2 + g], [self.nsa_kT.b], [x.b])
                    pH = self.ps[it % 2]
                    for l in range(32):
                        self.mm(pH[0:64, 0:n_cmp], w1[:, kv, l, :], x[:, l:l + 16 * (n_cmp - 1) + 1:16],
                                l == 0, False, [w1.b, x.b], [pH.b])
                        self.mm(pH[0:64, 511:512], w1[:, kv, l, :], peT[:, kv, l:l + 1], False, l == 31,
                                [w1.b, peT.b], [pH.b])
                    self.cp('dve', bias[:], pH[0:64, 511:512], [pH.b], [bias.b])
                    self.act(hid[:, 0:n_cmp], pH[0:64, 0:n_cmp], AF.Silu, [pH.b, bias.b], [hid.b], bias=bias[:, 0:1])
                    for kt in range(nct):
                        pO = self.ps[2 + kt % 2]
                        self.mm(pO[:, 0:64], hid[:, kt * 128:(kt + 1) * 128], w2[:, kv, :], True, True,
                                [hid.b, w2.b], [pO.b])
                        if kv == 1:
                            self.cp('act', vcmp[:, g, kt, 0:64], pO[:, 0:64], [pO.b], [vcmp.b])
                        else:
                            self.cp('act', ctm[:], pO[:, 0:64], [pO.b], [ctm.b])
                            self.rmsn(ctm[:].rearrange("p (h c) -> p h c", h=1), 1, 64, g_kc,
                                      ctn[:].rearrange("p (h c) -> p h c", h=1), [ctm.b], [ctn.b])
                            nrow = min(128, n_cmp - kt * 128)
                            r0 = 31 + 16 * 128 * kt
                            self.memset('dve', rp[:], 0.0, [rp.b])
                            self.dma('sp', rp[0:nrow, :], self.rope_d[r0:r0 + 16 * (nrow - 1) + 1:16, :],
                                     [self.rope_d.b], [rp.b])
                            A, Bm = self.sc_ra, self.sc_rb
                            c2 = ctn[:].rearrange("p (two c) -> p two c", two=2)
                            o2 = ctb[:].rearrange("p (two c) -> p two c", two=2)
                            Av = A[:, 0:64].rearrange("p (two c) -> p two c", two=2)
                            Bv = Bm[:, 0:64].rearrange("p (two c) -> p two c", two=2)
                            self.tt('dve', Av, c2, rp[:, 0:32].unsqueeze(1).to_broadcast([128, 2, 32]), ALU.mult,
                                    [ctn.b, rp.b], [A.b])
                            self.tt('dve', Bv, c2, rp[:, 32:64].unsqueeze(1).to_broadcast([128, 2, 32]), ALU.mult,
                                    [ctn.b, rp.b], [Bm.b])
                            self.tt('dve', o2[:, 0, :], Av[:, 0, :], Bv[:, 1, :], ALU.subtract, [A.b, Bm.b], [ctb.b])
                            self.tt('dve', o2[:, 1, :], Bv[:, 0, :], Av[:, 1, :], ALU.add, [A.b, Bm.b], [ctb.b])
                            pT = self.ps[4]
                            pTb = pT[:].bitcast(BF16)
                            self.tr(pTb[0:64, 0:128], ctb[:], self.ident[:], [ctb.b, self.ident.b], [pT.b])
                            self.cp('act', kcmpT[:, g, kt * 128:(kt + 1) * 128], pTb[0:64, 0:128], [pT.b], [kcmpT.b])
            for g in range(2):
                for kt in range(nct):
                    self.memset('pool', vcmp[:, g, kt, 64:65], 1.0, [vcmp.b])
                    self.cp('pool', vcmp[:, g, kt, 65:65 + self.n_blk], ovl[:, kt, :], [ovl.b], [vcmp.b])
            self.S.barrier()

    def nsa_attn(self, es, kcmpT, vcmp):
        T, NT = self.T, self.NT
        nb = self.n_blk
        nct = self.ncmp_pad // 128
        dvc = 65 + nb
        ksT = self.sb(es, 'ksT', [64, T], BF16)
        kwT = self.sb(es, 'kwT', [64, T], BF16)
        vs = self.sb(es, 'vs', [128, NT, 65], BF16)
        vw = self.sb(es, 'vw', [128, NT, 65], BF16)
        qt_ = [self.sb(es, f'nq{i}', [64, 512], BF16) for i in range(2)]
        cneg = [self.sb(es, f'cneg{i}', [128, self.ncmp_pad], BF16) for i in range(2)]
        forced = [self.sb(es, f'forced{i}', [128, nb], F32) for i in range(2)]
        negm = [self.sb(es, f'negm{i}', [128, T], BF16) for i in range(2)]
        rec = self.sb(es, 'nrec', [128, 4], F32)
        score = self.sb(es, 'nscore', [128, nb], F32)
        work = self.sb(es, 'nwork', [128, nb], F32)
        m8 = self.sb(es, 'nm8', [128, 8], F32)
        thr = self.sb(es, 'nthr', [128, 1], F32)
        nsel = self.sb(es, 'nsel', [128, nb], BF16)
        ost = [self.sb(es, f'nost{i}', [128, 3, 4, 64], F32) for i in range(2)]
        accA, accB, accS, accW = self.ps[3], self.ps[4], self.ps[5], self.ps[6]
        cmp_eid = {}

        def stage1(g, qt, sl):
            q, cn, fo, nm, o = qt_[sl], cneg[sl], forced[sl], negm[sl], ost[sl]
            tsl = slice(qt * 128, (qt + 1) * 128)
            self.dma('sp', q[:], self.nsa_qT[g, qt].rearrange("d r t -> d (r t)"), [self.nsa_qT.b], [q.b])
            self.dma('sp', cn[:], self.c['cmpneg'][tsl, :], [], [cn.b])
            self.dma('sp', fo[:], self.c['forced'][tsl, :], [], [fo.b])
            views = {j: ((accA if j < 2 else accB)[:, (j % 2) * dvc:(j % 2 + 1) * dvc], j // 2) for j in range(4)}
            tiles = []
            for kt in range(nct):
                if 16 * 128 * kt + 31 > qt * 128 + 127:
                    continue
                tiles.append(dict(kT=kcmpT[:, g, kt * 128:(kt + 1) * 128], V=vcmp[:, g, kt, 0:dvc],
                                  R=[kcmpT.b, vcmp.b], c0=0, subs=[0, 1, 2, 3],
                                  masks=[(cn[:, kt * 128:(kt + 1) * 128], self.i4[:], 0, 512, [cn.b, self.i4.b])]))
            if not tiles:
                tiles.append(dict(kT=kcmpT[:, g, 0:128], V=vcmp[:, g, 0, 0:dvc], R=[kcmpT.b, vcmp.b], c0=0,
                                  subs=[0, 1, 2, 3],
                                  masks=[(cn[:, 0:128], self.i4[:], 0, 512, [cn.b, self.i4.b])]))
            cmp_tiles = tiles
            viewsW = {j: (accW[:, j * 65:(j + 1) * 65], 0) for j in range(4)}
            tiles = []
            for kt in range(max(0, qt - 4), qt + 1):
                tl = dict(kT=kwT[:, kt * 128:(kt + 1) * 128], V=vw[:, kt, :], R=[kwT.b, vw.b], c0=0,
                          subs=[0, 1, 2, 3], masks=[])
                if kt == qt:
                    tl['masks'].append((self.tri_neg[:], self.i4[:], 0, 512, [self.tri_neg.b, self.i4.b]))
                if kt == qt - 4:
                    tl['masks'].append((self.edge_neg[:], self.i4[:], 0, 512, [self.edge_neg.b, self.i4.b]))
                tiles.append(tl)
            win_tiles = tiles

            def epi_cmp():
                for j in range(4):
                    ab = accA if j < 2 else accB
                    v_ = views[j][0]
                    self.ts('dve', rec[:, j:j + 1], v_[:, 64:65], 1e-30, ALU.max, [ab.b], [rec.b])
                self.op_('dve', lambda e: e.reciprocal(out=rec[:], in_=rec[:]), reads=[rec.b], writes=[rec.b])
                for j in range(4):
                    ab = accA if j < 2 else accB
                    v_ = views[j][0]
                    self.ts('dve', o[:, 0, j, :], v_[:, 0:64], rec[:, j:j + 1], ALU.mult, [ab.b, rec.b], [o.b])
                    if j == 0:
                        self.ts('dve', score[:], v_[:, 65:65 + nb], rec[:, 0:1], ALU.mult, [ab.b, rec.b], [score.b])
                    else:
                        self.stt(score[:], v_[:, 65:65 + nb], rec[:, j:j + 1], score[:], ALU.mult, ALU.add,
                                 [ab.b, rec.b, score.b], [score.b])
                self.tt('dve', score[:], score[:], fo[:], ALU.add, [score.b, fo.b], [score.b])
                self.op_('dve', lambda e: e.max(out=m8[:], in_=score[:]), reads=[score.b], writes=[m8.b])
                self.op_('dve', lambda e: e.match_replace(out=work[:], in_to_replace=m8[:], in_values=score[:],
                                                           imm_value=-3e38), reads=[score.b, m8.b], writes=[work.b])
                self.op_('dve', lambda e: e.max(out=m8[:], in_=work[:]), reads=[work.b], writes=[m8.b])
                self.ts('dve', thr[:], m8[:, 7:8], -1e29, ALU.max, [m8.b], [thr.b])
                self.ts('dve', nsel[:], score[:], thr[:, 0:1], ALU.is_lt, [score.b, thr.b], [nsel.b], s2=-BIG, op1=ALU.mult)
                nblk_need = (qt + 1) * 2
                self.cp('pool', nm[:, 0:nblk_need * 64].rearrange("p (j c) -> p j c", c=64),
                        nsel[:, 0:nblk_need].unsqueeze(2).to_broadcast([128, nblk_need, 64]), [nsel.b], [nm.b])
                self.tt('pool', nm[:, tsl], nm[:, tsl], self.tri_neg[:], ALU.add, [nm.b, self.tri_neg.b], [nm.b])

            def epi_win():
                av = accW[:, 0:260].rearrange("p (j c) -> p j c", j=4)
                self.op_('dve', lambda e, av=av: e.reciprocal(out=rec[:], in_=av[:, :, 64]), reads=[accW.b], writes=[rec.b])
                self.tt('dve', o[:, 2], av[:, :, 0:64], rec[:].unsqueeze(2).to_broadcast([128, 4, 64]), ALU.mult,
                        [accW.b, rec.b], [o.b])

            eid = self.attn_block(q[:], [q.b], 512, cmp_tiles, 0.125, views, [accA.b, accB.b], epilogue=epi_cmp)
            self.attn_block(q[:], [q.b], 512, win_tiles, 0.125, viewsW, [accW.b], epilogue=epi_win)
            cmp_eid[(g, qt)] = eid

        def stage2(g, qt, sl):
            q, nm, o = qt_[sl], negm[sl], ost[sl]
            tsl = slice(qt * 128, (qt + 1) * 128)
            self.attn_sync(cmp_eid[(g, qt)])
            views = {j: (accS[:, j * 65:(j + 1) * 65], 0) for j in range(4)}
            tiles = [dict(kT=ksT[:, kt * 128:(kt + 1) * 128], V=vs[:, kt, :], R=[ksT.b, vs.b], c0=0,
                          subs=[0, 1, 2, 3],
                          masks=[(nm[:, kt * 128:(kt + 1) * 128], self.i4[:], 0, 512, [nm.b, self.i4.b])])
                     for kt in range(qt + 1)]
            def epi_sel():
                av = accS[:, 0:260].rearrange("p (j c) -> p j c", j=4)
                self.op_('dve', lambda e, av=av: e.reciprocal(out=rec[:], in_=av[:, :, 64]), reads=[accS.b], writes=[rec.b])
                self.tt('dve', o[:, 1], av[:, :, 0:64], rec[:].unsqueeze(2).to_broadcast([128, 4, 64]), ALU.mult,
                        [accS.b, rec.b], [o.b])
                for bi, base in enumerate((1280, 512, 1792)):
                    self.dma('sp', self.Y[tsl, base + g * 256:base + (g + 1) * 256],
                             o[:, bi].rearrange("p r c -> p (r c)"), [o.b], [self.Y.b])
            self.attn_block(q[:], [q.b], 512, tiles, 0.125, views, [accS.b], epilogue=epi_sel)

        for g in range(2):
            self.dma('sp', ksT[:], self.nsa_kT[0 + g], [self.nsa_kT.b], [ksT.b])
            self.dma('sp', kwT[:], self.nsa_kT[2 + g], [self.nsa_kT.b], [kwT.b])
            self.dma('sp', vs[:], self.nsa_v[0 + g], [self.nsa_v.b], [vs.b])
            self.dma('sp', vw[:], self.nsa_v[2 + g], [self.nsa_v.b], [vw.b])
            stage1(g, 0, 0)
            for qt in range(NT):
                if qt + 1 < NT:
                    stage1(g, qt + 1, (qt + 1) % 2)
                stage2(g, qt, qt % 2)
            self.attn_flush()

    def layer_odd(self, es, L, x_src, x_dst):
        li = L // 2
        w = self.w
        T, NT = self.T, self.NT
        S = self.S
        memkT, memV = self.mem_kv(es, L)
        with ExitStack() as p1:
            win, lng = self.load_win(p1, L, w['odd_w_in'][li], ODD_COLS)
            g_cq = self.bc_load(p1, 'g_cq', w['dsa_q_norm_g'][li:li + 1, :], 64)
            g_ck = self.bc_load(p1, 'g_ck', w['dsa_k_norm_g'][li:li + 1, :], 64)
            g_mq = self.bc_load(p1, 'g_mq', w['mem_q_norm_g'][L:L + 1, :], 64)
            xt = [self.sb(p1, f'xt{i}', [128, D], F32) for i in range(2)]
            ht = [self.sb(p1, f'ht{i}', [128, D], BF16) for i in range(2)]
            hT = [self.sb(p1, f'hT{i}', [128, 8, 128], BF16) for i in range(2)]
            u = [self.sb(p1, f'u{i}', [128, ODD_COLS], F32) for i in range(2)]
            ss = self.sb(p1, 'ss', [128, 1], F32)
            junk = self.sb(p1, 'junk', [128, D], BF16)
            gt = self.sb(p1, 'gt', [128, 1792], BF16)
            cqn = self.sb(p1, 'cqn', [128, 512], F32)
            cqf = self.sb(p1, 'cqf', [128, 512], BF16)
            ckn = self.sb(p1, 'ckn', [128, 64], F32)
            ckf = self.sb(p1, 'ckf', [128, 64], BF16)
            cva = self.sb(p1, 'cva', [128, 65], BF16)
            self.memset('pool', cva[:], 1.0, [cva.b])
            iqf = self.sb(p1, 'iqf', [128, 256], BF16)
            ikf = self.sb(p1, 'ikf', [128, 32], BF16)
            iwt = self.sb(p1, 'iwt', [128, 8], F32)
            mlv = self.sb(p1, 'mlv', [128, 4, 129], BF16)
            self.memset('pool', mlv[:], 1.0, [mlv.b])
            mqf = self.sb(p1, 'mqf', [128, 256], BF16)
            stq = self.sb(p1, 'stq', [64, 8, 128], BF16)
            stk = self.sb(p1, 'stk', [64, 1, 128], BF16)
            sti = self.sb(p1, 'sti', [32, 8, 128], BF16)
            stik = self.sb(p1, 'stik', [32, 1, 128], BF16)
            stmq = self.sb(p1, 'stmq', [64, 4, 128], BF16)
            strw = self.sb(p1, 'strw', [128, 4, 128], F32)
            stif = self.sb(p1, 'stif', [8, 128], F32)
            scrA = self.new_scr(p1, 'A', 512)
            scrB = self.new_scr(p1, 'B', 256)
            ut_next = self.p1_front(0, x_src, xt, ht, hT, u, ss, junk, lng, win, ODD_COLS)
            for n in range(NT):
                ut = ut_next
                U = lambda a, b_, ut=ut: ut[:, a:b_]
                ub = [ut.b]
                tsl = slice(n * 128, (n + 1) * 128)
                self.chains_begin(['A', 'F', 'B', 'L', 'M', 'G'])
                if n + 1 < NT:
                    self.chain('F')
                    ut_next = self.p1_front(n + 1, x_src, xt, ht, hT, u, ss, junk, lng, win, ODD_COLS)
                self.chain('G')
                self.act(gt[:, 0:512], U(936, 1448), AF.Silu, ub, [gt.b])
                self.act(gt[:, 512:1024], U(2992, 3504), AF.Silu, ub, [gt.b])
                self.act(gt[:, 1024:1280], U(3760, 4016), AF.Silu, ub, [gt.b])
                self.act(gt[:, 1280:1792], U(2480, 2992), AF.Sigmoid, ub, [gt.b])
                self.dma('sp', self.G[tsl, :], gt[:], [gt.b], [self.G.b[n]])
                self.chain('A', scrA)
                cq3 = cqn[:].rearrange("p (h c) -> p h c", h=8)
                self.rmsn(U(0, 512).rearrange("p (h c) -> p h c", h=8), 8, 64, g_cq, cq3, ub, [cqn.b])
                self.rope(cq3, 8, 32, n, cqf[:].rearrange("p (h c) -> p h c", h=8), [cqn.b], [cqf.b])
                ck3 = ckn[:].rearrange("p (h c) -> p h c", h=1)
                self.rmsn(U(512, 576).rearrange("p (h c) -> p h c", h=1), 1, 64, g_ck, ck3, ub, [ckn.b])
                self.rope(ck3, 1, 32, n, ckf[:].rearrange("p (h c) -> p h c", h=1), [ckn.b], [ckf.b])
                self.cp('dve', cva[:, 0:64], U(576, 640), ub, [cva.b])
                self.dma('sp', self.dsa_v[:, n, :], cva[:], [cva.b], [self.dsa_v.b])
                pT = self.ps[4]
                pTb = pT[:].bitcast(BF16)
                for h in range(8):
                    self.tr(pTb[0:64, h * 128:(h + 1) * 128], cqf[:, h * 64:(h + 1) * 64], self.ident[:],
                            [cqf.b, self.ident.b], [pT.b])
                self.cp('act', stq[:], pTb[0:64, 0:1024].rearrange("p (k c) -> p k c", k=8), [pT.b], [stq.b])
                self.dma('sp', self.dsa_qT[n], stq[:], [stq.b], [self.dsa_qT.b])
                self.transposes_out([(ckf[:], [ckf.b])], 64, stk,
                                    self.dsa_kT[:, tsl].rearrange("d (k t) -> d k t", k=1), [self.dsa_kT.b], 5)
                self.chain('B', scrB)
                self.rope(U(640, 896).rearrange("p (h c) -> p h c", h=8), 8, 16, n,
                          iqf[:].rearrange("p (h c) -> p h c", h=8), ub, [iqf.b])
                self.rope(U(896, 928).rearrange("p (h c) -> p h c", h=1), 1, 16, n,
                          ikf[:].rearrange("p (h c) -> p h c", h=1), ub, [ikf.b])
                self.ts('dve', iwt[:], U(928, 936), 8 ** -0.5, ALU.mult, ub, [iwt.b])
                self.dma('sp', self.idx_w[tsl, :], iwt[:], [iwt.b], [self.idx_w.b])
                self.transposes_out([(iqf[:, h * 32:(h + 1) * 32], [iqf.b]) for h in range(8)], 32, sti,
                                    self.idx_qT[:, :, tsl].rearrange("h d t -> d h t"), [self.idx_qT.b], 6)
                self.transposes_out([(ikf[:], [ikf.b])], 32, stik,
                                    self.idx_kT[:, tsl].rearrange("d (k t) -> d k t", k=1), [self.idx_kT.b], 7)
                self.chain('L')
                pR = self.ps[3]
                for k in range(4):
                    self.tr(pR[:, k * 128:(k + 1) * 128], U(1448 + k * 128, 1448 + (k + 1) * 128), self.identf[:],
                            ub + [self.identf.b], [pR.b])
                self.cp('act', strw[:].rearrange("p k c -> p (k c)"), pR[:, 0:512], [pR.b], [strw.b])
                self.dma('sp', self.ml_raw[:, tsl].rearrange("(k p) t -> p k t", p=128), strw[:], [strw.b],
                         [self.ml_raw.b])
                pI = self.ps[3]
                self.tr(pI[0:8, 0:128], U(2472, 2480), self.identf[:], ub + [self.identf.b], [pI.b])
                self.cp('act', stif[:], pI[0:8, 0:128], [pI.b], [stif.b])
                self.dma('sp', self.ml_if[:, tsl], stif[:], [stif.b], [self.ml_if.b])
                self.cp('dve', mlv[:, :, 0:128], U(1960, 2472).rearrange("p (h c) -> p h c", h=4), ub, [mlv.b])
                self.dma('sp', self.ml_v[:, :, n, :].rearrange("h p c -> p h c"), mlv[:], [mlv.b], [self.ml_v.b])
                self.chain('B', scrB)
                self.rmsn(U(3504, 3760).rearrange("p (h c) -> p h c", h=4), 4, 64, g_mq,
                          mqf[:].rearrange("p (h c) -> p h c", h=4), ub, [mqf.b])
                self.transposes_out([(mqf[:, h * 64:(h + 1) * 64], [mqf.b]) for h in range(4)], 64, stmq,
                                    self.mem_qT[:, :, tsl].rearrange("h d t -> d h t"), [self.mem_qT.b], 6)
                self.chains_emit()
            S.barrier()
        if 'stop_p1' in self.dbg:
            return
        self.mlstm_pre(li)
        S.barrier()
        if 'stop_pre' in self.dbg:
            return
        with ExitStack() as pa:
            self.attn_setup(pa)
            self.mlstm_attn(pa)
            S.barrier()
        if 'stop_ml' in self.dbg:
            return
        with ExitStack() as pa:
            self.attn_setup(pa)
            self.mem_attn(pa, memkT, memV)
            S.barrier()
        if 'stop_mem' in self.dbg:
            return
        with ExitStack() as pa:
            self.attn_setup(pa)
            self.dsa_attn(pa)
            S.barrier()
        if 'stop_dsa' in self.dbg:
            return
        with ExitStack() as p3:
            wout = self.load_wout(p3, L)
            g_h = self.bc_load(p3, 'g_h', w['mlstm_h_norm_g'][li:li + 1, :], 128)
            yt = [self.sb(p3, f'yt{i}', [128, 1280], F32) for i in range(2)]
            gtt = [self.sb(p3, f'gtt{i}', [128, 1792], BF16) for i in range(2)]
            mix = [self.sb(p3, f'mix{i}', [128, 1280], BF16) for i in range(2)]
            hn = self.sb(p3, 'hn', [128, 512], F32)

            def mixfn(n):
                sl = n % 2
                y, g, m = yt[sl], gtt[sl], mix[sl]
                tsl = slice(n * 128, (n + 1) * 128)
                self.dma('sp', y[:], self.Y[tsl, 0:1280], [self.Y.b], [y.b])
                self.dma('sp', g[:], self.G[tsl, :], [self.G.b[n]], [g.b])
                self.tt('dve', m[:, 0:512], y[:, 0:512], g[:, 0:512], ALU.mult, [y.b, g.b], [m.b])
                self.tt('pool', m[:, 1024:1280], y[:, 1024:1280], g[:, 1024:1280], ALU.mult, [y.b, g.b], [m.b])
                self.rmsn(y[:, 512:1024].rearrange("p (h c) -> p h c", h=4), 4, 128, g_h,
                          hn[:].rearrange("p (h c) -> p h c", h=4), [y.b], [hn.b])
                self.tt('dve', hn[:], hn[:], g[:, 1280:1792], ALU.mult, [hn.b, g.b], [hn.b])
                self.tt('dve', m[:, 512:1024], hn[:], g[:, 512:1024], ALU.mult, [hn.b, g.b], [m.b])
                return m
            self.p3(p3, L, x_src, x_dst, wout, mixfn)
            S.barrier()

    def mlstm_pre(self, li):
        w = self.w
        T = self.T
        with ExitStack() as es:
            xp = [self.sb(es, f'xp{i}', [128, T + 3], F32) for i in range(2)]
            y = self.sb(es, 'cy', [128, T], F32)
            yo = [self.sb(es, f'cyo{i}', [128, T], BF16) for i in range(2)]
            wc = self.sb(es, 'cwc', [128, 4, 4], F32)
            bc = self.sb(es, 'cbc', [128, 4], F32)
            for ck in range(4):
                self.dma('sp', wc[:, ck, :], w['mlstm_conv_wT'][li, ck * 128:(ck + 1) * 128, :], [], [wc.b])
                self.dma('sp', bc[:, ck:ck + 1], w['mlstm_conv_b'][li, ck * 128:(ck + 1) * 128].unsqueeze(1), [], [bc.b])
            for ck in range(4):
                x = xp[ck % 2]
                o = yo[ck % 2]
                self.memset('pool', x[:, 0:3], 0.0, [x.b])
                self.dma('sp', x[:, 3:T + 3], self.ml_raw[ck * 128:(ck + 1) * 128, :], [self.ml_raw.b], [x.b])
                self.ts('dve', y[:], x[:, 0:T], wc[:, ck, 0:1], ALU.mult, [x.b, wc.b, bc.b], [y.b],
                        s2=bc[:, ck:ck + 1], op1=ALU.add)
                for j in range(1, 4):
                    self.stt(y[:], x[:, j:j + T], wc[:, ck, j:j + 1], y[:], ALU.mult, ALU.add, [x.b, wc.b, y.b], [y.b])
                self.act(o[:], y[:], AF.Silu, [y.b], [o.b])
                self.dma('sp', self.ml_qkT[ck * 128:(ck + 1) * 128, :], o[:], [o.b], [self.ml_qkT.b])
            self.S.barrier()
        with ExitStack() as es:
            ig = self.sb(es, 'ig', [4, T], F32)
            fg = self.sb(es, 'fg', [4, T], F32)
            cs = self.sb(es, 'cs', [4, T], F32)
            a = self.sb(es, 'ga', [4, T], F32)
            Mt = self.sb(es, 'gM', [4, T], F32)
            ones = self.sb(es, 'gones', [4, T], F32)
            ib = self.sb(es, 'gib', [4, 1], F32)
            fb = self.sb(es, 'gfb', [4, 1], F32)
            self.memset('pool', ones[:], 1.0, [ones.b])
            self.dma('sp', ig[:], self.ml_if[0:4, :], [self.ml_if.b], [ig.b])
            self.dma('sp', fg[:], self.ml_if[4:8, :], [self.ml_if.b], [fg.b])
            self.dma('sp', ib[:], w['mlstm_i_bias'][li].unsqueeze(1), [], [ib.b])
            self.dma('sp', fb[:], w['mlstm_f_bias'][li].unsqueeze(1), [], [fb.b])
            self.ts('dve', fb[:], fb[:], -1.0, ALU.mult, [fb.b], [fb.b])
            self.act(fg[:], fg[:], AF.Exp, [fg.b, fb.b], [fg.b], bias=fb[:, 0:1], scale=-1.0)
            self.act(fg[:], fg[:], AF.Ln, [fg.b], [fg.b], bias=self.one_c[0:4, 0:1], scale=1.0)
            self.op_('dve', lambda e: e.tensor_tensor_scan(out=cs[:], data0=ones[:], data1=fg[:], initial=0.0,
                                                             op0=ALU.mult, op1=ALU.add),
                      reads=[ones.b, fg.b], writes=[cs.b])
            self.stt(a[:], ig[:], ib[:, 0:1], cs[:], ALU.add, ALU.add, [ig.b, ib.b, cs.b], [a.b])
            self.op_('dve', lambda e: e.tensor_tensor_scan(out=Mt[:], data0=a[:], data1=a[:], initial=0.0,
                                                             op0=ALU.max, op1=ALU.max),
                      reads=[a.b], writes=[Mt.b])
            self.tt('dve', cs[:], cs[:], Mt[:], ALU.subtract, [cs.b, Mt.b], [cs.b])
            self.act(cs[:], cs[:], AF.Exp, [cs.b], [cs.b])
            self.ts('dve', Mt[:], Mt[:], -1.0, ALU.mult, [Mt.b], [Mt.b])
            self.dma('sp', self.ml_g[0:4, :], a[:], [a.b], [self.ml_g.b])
            self.dma('sp', self.ml_g[4:8, :], Mt[:], [Mt.b], [self.ml_g.b])
            self.dma('sp', self.ml_g[8:12, :], cs[:], [cs.b], [self.ml_g.b])
            self.S.barrier()

    def mlstm_attn(self, es):
        T, NT = self.T, self.NT
        qT = [self.sb(es, f'lqT{i}', [64, T], BF16) for i in range(2)]
        kT = [self.sb(es, f'lkT{i}', [64, T], BF16) for i in range(2)]
        V = [self.sb(es, f'lV{i}', [128, NT, 129], BF16) for i in range(2)]
        nM = [self.sb(es, f'lnM{i}', [128, T], F32) for i in range(2)]
        ant = self.sb(es, 'lant', [NT, 2, 128], F32)
        atm = [self.sb(es, f'latm{i}', [128, 2, NT], F32) for i in range(2)]
        Et = [self.sb(es, f'lEt{i}', [128, 512], F32) for i in range(3)]
        ost = [self.sb(es, f'lost{i}', [128, 4, 128], F32) for i in range(2)]
        d2 = self.sb(es, 'ld2', [128, 4], F32)
        ecnt = [0]
        blk = 0
        LN8 = math.log(0.125)
        for h in range(4):
            q, k, v, nm, at = qT[h % 2], kT[h % 2], V[h % 2], nM[h % 2], atm[h % 2]
            self.dma('sp', q[:], self.ml_qkT[h * 64:(h + 1) * 64, :], [self.ml_qkT.b], [q.b])
            self.dma('sp', k[:], self.ml_qkT[256 + h * 64:256 + (h + 1) * 64, :], [self.ml_qkT.b], [k.b])
            self.dma('sp', v[:], self.ml_v[h], [self.ml_v.b], [v.b])
            self.dma('sp', nm[:], self.ml_g[4 + h:5 + h, :].to_broadcast([128, T]), [self.ml_g.b], [nm.b])
            self.dma('sp', ant[:, 0, :], self.ml_g[h, :].rearrange("(n p) -> n p", p=128), [self.ml_g.b], [ant.b])
            self.dma('sp', ant[:, 1, :], self.ml_g[8 + h, :].rearrange("(n p) -> n p", p=128), [self.ml_g.b], [ant.b])
            pA = self.ps[7]
            for i in range(2):
                self.tr(pA[:, i * NT:(i + 1) * NT], ant[:, i, :], self.identf[0:NT, 0:NT], [ant.b, self.identf.b], [pA.b])
            self.cp('act', at[:].rearrange("p a n -> p (a n)"), pA[:, 0:2 * NT], [pA.b], [at.b])
            self.ts('dve', at[:, 0, :], at[:, 0, :], LN8, ALU.add, [at.b], [at.b])
            for cq in range(T // 512):
                set_i = blk % 2
                blk += 1
                accA, accB = self.ps[3 + 2 * set_i], self.ps[4 + 2 * set_i]
                views = {j: ((accA if j < 2 else accB)[:, (j % 2) * 129:(j % 2 + 1) * 129], j // 2) for j in range(4)}
                tiles = []
                for kt in range(4 * cq + 4):
                    vv = kt - 4 * cq
                    c0 = max(0, vv) * 128

                    def efn(kt=kt, c0=c0, cq=cq, nm=nm, at=at):
                        E = Et[ecnt[0] % 3]
                        ecnt[0] += 1
                        self.act(E[:, c0:512], nm[:, cq * 512 + c0:(cq + 1) * 512], AF.Exp, [nm.b, at.b], [E.b],
                                 bias=at[:, 0, kt:kt + 1], scale=1.0)
                        return E, [E.b]
                    tl = dict(kT=k[:, kt * 128:(kt + 1) * 128], V=v[:, kt, :], R=[k.b, v.b], c0=c0,
                              subs=list(range(max(0, vv), 4)), Efn=efn)
                    if vv >= 0:
                        tl['diag'] = vv
                    tiles.append(tl)
                def epi(accA=accA, accB=accB, views=views, o=ost[set_i], cq=cq, h=h, at=at):
                    for j in range(4):
                        ab = accA if j < 2 else accB
                        v_ = views[j][0]
                        self.act(d2[:, j:j + 1], v_[:, 128:129], AF.Abs, [ab.b], [d2.b])
                    self.tt('dve', d2[:], d2[:], at[:, 1, 4 * cq:4 * cq + 4], ALU.max, [d2.b, at.b], [d2.b])
                    self.op_('dve', lambda e: e.reciprocal(out=d2[:], in_=d2[:]), reads=[d2.b], writes=[d2.b])
                    for j in range(4):
                        ab = accA if j < 2 else accB
                        v_ = views[j][0]
                        self.ts('dve', o[:, j, :], v_[:, 0:128], d2[:, j:j + 1], ALU.mult, [ab.b, d2.b], [o.b])
                    self.dma('sp', self.Y[cq * 512:(cq + 1) * 512, 512 + h * 128:512 + (h + 1) * 128]
                             .rearrange("(j p) c -> p j c", p=128), o[:], [o.b], [self.Y.b])
                self.attn_block(q[:, cq * 512:(cq + 1) * 512], [q.b], 512, tiles, 1.0, views, [accA.b, accB.b],
                                mode='mul', epilogue=epi)
        self.attn_flush()

    def dsa_attn(self, es):
        T, NT = self.T, self.NT
        KSEL = min(256, T // 4)
        ikT = self.sb(es, 'ikT', [32, T], BF16)
        ckT = self.sb(es, 'ckT', [64, T], BF16)
        cv = self.sb(es, 'cv', [128, NT, 65], BF16)
        self.dma('sp', ikT[:], self.idx_kT[:, :], [self.idx_kT.b], [ikT.b])
        self.dma('sp', ckT[:], self.dsa_kT[:, :], [self.dsa_kT.b], [ckT.b])
        self.dma('sp', cv[:], self.dsa_v[:, :, :], [self.dsa_v.b], [cv.b])
        iq = [self.sb(es, f'iq{i}', [32, 8, 128], BF16) for i in range(2)]
        iw = [self.sb(es, f'iw{i}', [128, 8], F32) for i in range(2)]
        cq_ = [self.sb(es, f'cq{i}', [64, 2, 512], BF16) for i in range(2)]
        score2 = [self.sb(es, f'dscore{i}', [128, T], F32) for i in range(3)]
        thrA2 = [self.sb(es, f'dthrA{i}', [128, 1], F32) for i in range(2)]
        work = self.sb(es, 'dwork', [128, T], F32)
        negm = [self.sb(es, f'dnegm{i}', [128, T], BF16) for i in range(2)]
        rl = [self.sb(es, f'drl{i}', [128, 512], F32) for i in range(3)]
        m8 = self.sb(es, 'dm8', [128, 8], F32)
        thr = self.sb(es, 'dthr', [128, 1], F32)
        rec = self.sb(es, 'drec', [128, 4], F32)
        ost = [self.sb(es, f'dost{i}', [128, 4, 64], F32) for i in range(2)]
        rlc_ = [0]
        blk_ = [0]
        NBIS = 20
        junk = self.sb(es, 'djunk', [128, T], BF16)
        amax = self.sb(es, 'damax', [128, 1], F32)
        w0 = self.sb(es, 'dw0', [128, 1], F32)
        nHh = self.sb(es, 'dnHh', [128, 40], F32)
        nmid = [self.sb(es, f'dnmid{i}', [128, 1], F32) for i in range(2)]
        Ssum = self.sb(es, 'dS', [128, 1], F32)
        tsg = self.sb(es, 'dtsg', [128, 1], F32)

        def stage_a1(qt):
            rlc = rlc_[0]
            sl = qt % 2
            tsl = slice(qt * 128, (qt + 1) * 128)
            q_i, w_i = iq[sl], iw[sl]
            score = score2[qt % 3]
            self.dma('sp', q_i[:], self.idx_qT[:, :, tsl].rearrange("h d t -> d h t"), [self.idx_qT.b], [q_i.b])
            self.dma('sp', w_i[:], self.idx_w[tsl, :], [self.idx_w.b], [w_i.b])
            ncols = (qt + 1) * 128
            for c in range((ncols + 511) // 512):
                c0 = c * 512
                wd = min(512, ncols - c0)
                for h in range(8):
                    bi = self.st_rr % len(self.st_banks)
                    self.st_rr += 1
                    bank, bb = self.st_banks[bi]
                    self.mm(bank[:, 0:wd], q_i[:, h, :], ikT[:, c0:c0 + wd], True, True, [q_i.b, ikT.b], [bb])
                    if h == 0:
                        self.ts('dve', score[:, c0:c0 + wd], bank[:, 0:wd], 0.0, ALU.max, [bb, w_i.b], [score.b],
                                s2=w_i[:, 0:1], op1=ALU.mult)
                    else:
                        r = rl[rlc % 3]
                        rlc += 1
                        self.ts('dve', r[:, 0:wd], bank[:, 0:wd], 0.0, ALU.max, [bb, w_i.b], [r.b],
                                s2=w_i[:, h:h + 1], op1=ALU.mult)
                        self.tt('dve', score[:, c0:c0 + wd], score[:, c0:c0 + wd], r[:, 0:wd], ALU.add,
                                [score.b, r.b], [score.b])
            rlc_[0] = rlc

        def is_act_tile(qt):
            return ((qt + 1) * 128 > KSEL) and ('dsa_nobis' not in self.dbg)

        def stage_a2_finish(qt):
            sl = qt % 2
            nm = negm[sl]
            score = score2[qt % 3]
            ncols = (qt + 1) * 128
            th = thrA2[qt % 2] if is_act_tile(qt) else thr
            self.ts('dve', nm[:, 0:ncols], score[:, 0:ncols], th[:, 0:1], ALU.is_lt, [score.b, th.b], [nm.b],
                    s2=-BIG, op1=ALU.mult)

        def stage_a2(qt):
            sl = qt % 2
            tsl = slice(qt * 128, (qt + 1) * 128)
            q_c, nm = cq_[sl], negm[sl]
            score = score2[qt % 3]
            ncols = (qt + 1) * 128
            for half in range(2):
                self.dma('sp', q_c[:, half, :], self.dsa_qT[qt, :, half * 4:(half + 1) * 4, :].rearrange("d h t -> d (h t)"),
                         [self.dsa_qT.b], [q_c.b])
            use_act = is_act_tile(qt)
            if use_act:
                self.op_('dve', lambda e, ncols=ncols: e.tensor_reduce(out=amax[:], in_=score[:, 0:ncols], axis=AX.X,
                                                                      op=ALU.max, apply_absolute_value=True),
                         reads=[score.b], writes=[amax.b])
                self.ts('dve', w0[:], amax[:], 2.0, ALU.mult, [amax.b], [w0.b], s2=2.0, op1=ALU.add)
                self.ts('dve', nHh[:], self.pw[:], w0[:, 0:1], ALU.mult, [self.pw.b, w0.b], [nHh.b])
            self.tt('dve', score[:, tsl], score[:, tsl], self.trinegf[:], ALU.add, [score.b, self.trinegf.b], [score.b])
            if use_act:
                cconst = float(0.5 - (2 * KSEL - ncols - 1))
                self.memset('pool', nmid[0][:], 0.0, [nmid[0].b])
                for j in range(NBIS):
                    cur, nxt = nmid[j % 2], nmid[(j + 1) % 2]
                    self.act(junk[:, 0:ncols], score[:, 0:ncols], AF.Sign, [score.b, cur.b], [junk.b, Ssum.b],
                             bias=cur[:, 0:1], scale=1.0, accum=Ssum[:, 0:1])
                    self.act(tsg[:], Ssum[:], AF.Sign, [Ssum.b], [tsg.b], bias=cconst, scale=1.0)
                    self.act(nxt[:], tsg[:], AF.Identity, [tsg.b, cur.b, nHh.b], [nxt.b],
                             bias=cur[:, 0:1], scale=nHh[:, j:j + 1])
                fin = nmid[NBIS % 2]
                thrA = thrA2[qt % 2]
                self.act(thrA[:], fin[:], AF.Identity, [fin.b, nHh.b], [thrA.b], bias=nHh[:, NBIS - 1:NBIS], scale=-1.0)
                return
            elif ncols > KSEL and 'dsa_notopk' not in self.dbg:
                self.cp('pool', work[:, 0:ncols], score[:, 0:ncols], [score.b], [work.b])
                nr = KSEL // 8
                for r_ in range(nr):
                    self.op_('dve', lambda e, ncols=ncols: e.max(out=m8[:], in_=work[:, 0:ncols]),
                             reads=[work.b], writes=[m8.b])
                    if r_ < nr - 1:
                        self.op_('dve', lambda e, ncols=ncols: e.match_replace(
                            out=work[:, 0:ncols], in_to_replace=m8[:], in_values=work[:, 0:ncols], imm_value=-3e38),
                            reads=[work.b, m8.b], writes=[work.b])
                self.ts('dve', thr[:], m8[:, 7:8], -1e29, ALU.max, [m8.b], [thr.b])
            else:
                self.memset('dve', thr[:], -1e29, [thr.b])
            stage_a2_finish(qt)

        def stage_b(qt):
            blk = blk_[0]
            sl = qt % 2
            tsl = slice(qt * 128, (qt + 1) * 128)
            q_c, nm = cq_[sl], negm[sl]
            for half in range(2):
                set_i = blk % 2
                blk += 1
                accb = self.ps[3 + set_i]
                views = {j: (accb[:, j * 65:(j + 1) * 65], 0) for j in range(4)}
                tiles = [dict(kT=ckT[:, kt * 128:(kt + 1) * 128], V=cv[:, kt, :], R=[ckT.b, cv.b], c0=0,
                              subs=[0, 1, 2, 3],
                              masks=[(nm[:, kt * 128:(kt + 1) * 128], self.i4[:], 0, 512, [nm.b, self.i4.b])])
                         for kt in range(qt + 1)]
                def epi(accb=accb, o=ost[set_i], tsl=tsl, half=half):
                    av = accb[:, 0:260].rearrange("p (j c) -> p j c", j=4)
                    self.op_('dve', lambda e, av=av: e.reciprocal(out=rec[:], in_=av[:, :, 64]), reads=[accb.b], writes=[rec.b])
                    self.tt('dve', o[:], av[:, :, 0:64], rec[:].unsqueeze(2).to_broadcast([128, 4, 64]), ALU.mult,
                            [accb.b, rec.b], [o.b])
                    self.dma('sp', self.Y[tsl, half * 256:(half + 1) * 256], o[:].rearrange("p j c -> p (j c)"),
                             [o.b], [self.Y.b])
                self.attn_block(q_c[:, half, :], [q_c.b], 512, tiles, 0.125, views, [accb.b], epilogue=epi)
            blk_[0] = blk

        stage_a1(0)
        if NT > 1:
            stage_a1(1)
        stage_a2(0)
        for qt in range(NT):
            if qt + 2 < NT:
                stage_a1(qt + 2)
            if qt + 1 < NT:
                stage_a2(qt + 1)
            if is_act_tile(qt):
                stage_a2_finish(qt)
            stage_b(qt)
        self.attn_flush()


def host_consts(T):
    bf = ml_dtypes.bfloat16
    nb = T // 64
    ncp = max(1, T // 2048) * 128
    n_cmp = (T - 32) // 16 + 1
    c = {}
    c['c_ident'] = np.eye(128, dtype=np.float32).astype(bf)
    c['c_identf'] = np.eye(128, dtype=np.float32)
    c['c_i4'] = np.tile(np.eye(128, dtype=np.float32), (1, 4)).astype(bf)
    i8 = np.zeros((128, 512), np.float32)
    for p in range(128):
        for h in range(8):
            i8[p, h * 64 + p % 64] = 1.0
    c['c_i8x2'] = i8.astype(bf)
    t = np.arange(128)[:, None]
    s = np.arange(128)[None, :]
    c['c_tri_neg'] = np.where(s > t, -BIG, 0.0).astype(np.float32).astype(bf)
    c['c_edge_neg'] = np.where(s <= t, -BIG, 0.0).astype(np.float32).astype(bf)
    c['c_tri01T'] = np.where(t <= s, 1.0, 0.0).astype(np.float32).astype(bf)
    c['c_trinegf'] = np.where(s > t, -1e30, 0.0).astype(np.float32)
    inv = (10000.0 ** (-np.arange(32, dtype=np.float32) / 32)).astype(np.float32)
    c['c_invf'] = np.tile(inv[None, :], (128, 1)).astype(np.float32)
    c['c_pw'] = np.tile((-(2.0 ** -(np.arange(40, dtype=np.float64) + 2)))[None, :], (128, 1)).astype(np.float32)
    tt = np.arange(T)[:, None]
    n = np.arange(ncp)[None, :]
    cm = np.where((16 * n + 31 <= tt) & (n < n_cmp), 0.0, -BIG)
    c['c_cmpneg'] = cm.astype(np.float32).astype(bf)
    j = np.arange(nb)[None, :]
    cur = tt // 64
    forced = np.where(j == cur, 3e4, np.where(j == cur - 1, 2e4, np.where(j == 0, 1e4, 0.0)))
    forced = np.where(j * 64 <= tt, forced, -1e30)
    c['c_forced'] = forced.astype(np.float32)
    ni = np.arange(ncp)[:, None]
    ov = ((16 * ni <= j * 64 + 63) & (16 * ni + 31 >= j * 64) & (ni < n_cmp))
    c['c_overlap'] = ov.astype(np.float32).astype(bf)
    return c


_CACHE = {}


def make_in_maps(inputs, T, ncores):
    NT = T // 128
    consts = host_consts(T)
    wnames = ["ln_g", "mem_norm_g", "mem_w_kv", "mem_q_norm_g", "mem_k_norm_g", "w_out", "even_w_in",
              "mla_q_lat_g", "mla_kv_lat_g", "mla_w_uq", "mla_w_ukv", "mla_q_norm_g", "mla_k_norm_g",
              "nsa_q_norm_g", "nsa_k_norm_g", "nsa_cmp_w1", "nsa_cmp_w2", "odd_w_in", "dsa_q_norm_g",
              "dsa_k_norm_g", "mlstm_conv_b", "mlstm_i_bias", "mlstm_f_bias", "mlstm_h_norm_g"]
    shared = {k: np.ascontiguousarray(np.asarray(inputs[k], dtype=np.float32)) for k in wnames}
    shared["nsa_cmp_posT"] = np.ascontiguousarray(np.transpose(np.asarray(inputs["nsa_cmp_pos"], np.float32), (0, 1, 3, 2)))
    shared["mlstm_conv_wT"] = np.ascontiguousarray(np.transpose(np.asarray(inputs["mlstm_conv_w"], np.float32), (0, 2, 1)))
    shared.update(consts)
    maps = []
    for c in range(ncores):
        m = dict(shared)
        m["x"] = np.ascontiguousarray(np.asarray(inputs["x"][c, :T], np.float32))
        m["mem"] = np.ascontiguousarray(np.asarray(inputs["mem"][c], np.float32))
        pos = np.asarray(inputs["positions"][c, :T]).astype(np.int32)
        m["pos_t"] = np.ascontiguousarray(pos.reshape(NT, 128).T)
        maps.append(m)
    return maps


def kernel(**inputs):
    T = 4096
    key = ('full', T)
    if key not in _CACHE:
        _CACHE[key] = Builder(T, [0, 1, 2, 3]).build()
    nc = _CACHE[key]
    maps = make_in_maps(inputs, T, 8)
    res = run_bass_kernel_spmd(nc, maps, core_ids=list(range(8)))
    out = np.stack([np.asarray(r["out"], dtype=np.float32) for r in res.results], axis=0)
    return out
```

```python
import math
import numpy as np
import ml_dtypes
from contextlib import ExitStack
import concourse.bass as bass
import concourse.mybir as mybir
from concourse.bass_utils import run_bass_kernel_spmd

F32 = mybir.dt.float32
BF16 = mybir.dt.bfloat16
I32 = mybir.dt.int32
AF = mybir.ActivationFunctionType
ALU = mybir.AluOpType
AX = mybir.AxisListType

D = 1024
BIG = 30000.0
EPS = 1e-6
EVEN_COLS = 3256
ODD_COLS = 4016
ENGS = ('pe', 'act', 'dve', 'pool', 'sp')
EPOCH = 16000
NDQ = 8


class Buf:
    __slots__ = ('name', 'w', 'r')

    def __init__(self, name=''):
        self.name = name
        self.w = None
        self.r = {}


class Sched:
    def __init__(self, nc, es):
        self.nc = nc
        self.es = es
        self.prog = {e: [] for e in ENGS}
        self.esem = {e: [] for e in ENGS}
        self.cnt = {e: 0 for e in ENGS}
        self.seen = {e: {} for e in ENGS}
        self.dq = ('sp', 'pool', 'act')
        self.dsem = {q: [es.enter_context(nc.semaphore(f'D{q}{i}')) for i in range(NDQ)] for q in self.dq}
        self.dcnt = {q: 0 for q in self.dq}
        self.ninst = 0

    def _semobj(self, key):
        if key[0] == 'E':
            return self.esem[key[1]][key[2]]
        return self.dsem[key[1]][key[2]]

    def op(self, e, fn, reads=(), writes=(), dma=False):
        deps = {}
        for b in reads:
            if b.w is not None:
                k, v = b.w
                if deps.get(k, 0) < v:
                    deps[k] = v
        for b in writes:
            if b.w is not None:
                k, v = b.w
                if deps.get(k, 0) < v:
                    deps[k] = v
            for k, v in b.r.items():
                if deps.get(k, 0) < v:
                    deps[k] = v
        waits = []
        seen = self.seen[e]
        for k, v in deps.items():
            if e == 'pe' and k[0] == 'E' and k[1] == 'pe':
                continue
            if seen.get(k, 0) >= v:
                continue
            seen[k] = v
            waits.append((self._semobj(k), v))
        if dma:
            j = self.dcnt[e]
            self.dcnt[e] += 1
            slot = j % NDQ
            val = 16 * (j // NDQ + 1)
            key = ('D', e, slot)
            if val > 16 and seen.get(key, 0) < val - 16:
                seen[key] = val - 16
                waits.append((self.dsem[e][slot], val - 16))
            sem = self.dsem[e][slot]
            inc = 16
        else:
            c = self.cnt[e]
            ep = c // EPOCH
            if ep >= len(self.esem[e]):
                self.esem[e].append(self.es.enter_context(self.nc.semaphore(f'S{e}{ep}')))
            self.cnt[e] += 1
            key = ('E', e, ep)
            val = c % EPOCH + 1
            sem = self.esem[e][ep]
            inc = 1
        ev = (key, val)
        self.ninst += 1

        def thunk(eng, waits=waits, fn=fn, sem=sem, inc=inc):
            for s, v in waits:
                eng.wait_ge(s, v)
            fn(eng).then_inc(sem, inc)
        self.prog[e].append(thunk)
        for b in reads:
            if b.r.get(key, 0) < val:
                b.r[key] = val
        for b in writes:
            b.w = ev
            b.r = {}
        return ev

    def barrier(self):
        evs = []
        for e in ENGS:
            c = self.cnt[e]
            if c > 0:
                ep = (c - 1) // EPOCH
                evs.append((('E', e, ep), (c - 1) % EPOCH + 1))
        for q in self.dq:
            n = self.dcnt[q]
            for slot in range(min(n, NDQ)):
                cntslot = (n - 1 - slot) // NDQ + 1
                evs.append((('D', q, slot), 16 * cntslot))
        for e in ENGS:
            waits = []
            for k, v in evs:
                if k[0] == 'E' and k[1] == e:
                    continue
                if self.seen[e].get(k, 0) >= v:
                    continue
                self.seen[e][k] = v
                waits.append((self._semobj(k), v))

            def thunk(eng, waits=waits):
                for s, v in waits:
                    eng.wait_ge(s, v)
            self.prog[e].append(thunk)

    def emit(self):
        nc = self.nc
        with nc.Block() as block:
            @block.tensor
            def _(eng):
                for t in self.prog['pe']:
                    t(eng)

            @block.scalar
            def _(eng):
                for t in self.prog['act']:
                    t(eng)

            @block.vector
            def _(eng):
                for t in self.prog['dve']:
                    t(eng)

            @block.gpsimd
            def _(eng):
                for t in self.prog['pool']:
                    t(eng)

            @block.sync
            def _(eng):
                for t in self.prog['sp']:
                    t(eng)


class Tl:
    __slots__ = ('t', 'b')

    def __init__(self, t, name):
        self.t = t
        self.b = Buf(name)

    def __getitem__(self, k):
        return self.t[k]


class Builder:
    def __init__(self, T, layers, dbg=()):
        self.T = T
        self.NT = T // 128
        self.layers = list(layers)
        self.dbg = set(dbg)
        self.nc = bass.Bass("TRN2", target_bir_lowering=False)
        self.uid = 0
        self.rec = None
        self.dbg_out = {}

    def din(self, name, shape, dt=F32):
        return self.nc.dram_tensor(name, list(shape), dt, kind="ExternalInput").ap()

    def dscr(self, name, shape, dt, nbuf=1):
        kind = "ExternalOutput" if name in self.dbg else "Internal"
        t = self.nc.dram_tensor(name, list(shape), dt, kind=kind).ap()
        tl = Tl(t, name)
        if nbuf > 1:
            tl.b = [Buf(f'{name}{i}') for i in range(nbuf)]
        return tl

    def sb(self, es, name, shape, dt):
        self.uid += 1
        t = es.enter_context(self.nc.sbuf_tensor(f'{name}_{self.uid}', list(shape), dt))
        return Tl(t, name)

    def op_(self, e, fn, reads=(), writes=(), dma=False):
        if self.rec is not None:
            self.rec.append((e, fn, tuple(reads), tuple(writes), dma))
        else:
            self.S.op(e, fn, reads=reads, writes=writes, dma=dma)

    def chains_begin(self, names):
        self._chains = {k: [] for k in names}

    def chain(self, name, scr=None):
        self.rec = self._chains[name]
        self.scr = scr if scr is not None else self.scr0

    def chains_emit(self):
        self.rec = None
        self.scr = self.scr0
        lists = [l for l in self._chains.values() if l]
        idx = [0] * len(lists)
        left = sum(len(l) for l in lists)
        while left:
            for i, l in enumerate(lists):
                if idx[i] < len(l):
                    e, fn, R, W, dma = l[idx[i]]
                    idx[i] += 1
                    left -= 1
                    self.S.op(e, fn, reads=R, writes=W, dma=dma)

    def new_scr(self, es, tag, w):
        sc = {}
        for k in ('sq', 'tmp', 'ra', 'rb'):
            sc[k] = self.sb(es, f'sc_{k}_{tag}', [128, w], F32)
        for k in ('ssq', 'ln', 'rs'):
            sc[k] = self.sb(es, f'sc_{k}_{tag}', [128, 16], F32)
        return sc

    def mm(self, out, lhsT, rhs, start, stop, R, W):
        self.op_('pe', lambda e: e.matmul(out, lhsT=lhsT, rhs=rhs, start=start, stop=stop,
                                           skip_group_check=True), reads=R, writes=W)

    def tr(self, out, in_, ident, R, W):
        self.op_('pe', lambda e: e.transpose(out=out, in_=in_, identity=ident), reads=R, writes=W)

    def act(self, out, in_, func, R, W, bias=None, scale=None, accum=None):
        kw = {}
        if bias is not None:
            kw['bias'] = bias
        if scale is not None:
            kw['scale'] = scale
        if accum is not None:
            kw['accum_out'] = accum
        self.op_('act', lambda e: e.activation(out=out, in_=in_, func=func, **kw), reads=R, writes=W)

    def tt(self, eng, out, in0, in1, op, R, W):
        self.op_(eng, lambda e: e.tensor_tensor(out=out, in0=in0, in1=in1, op=op), reads=R, writes=W)

    def ts(self, eng, out, in0, s1, op0, R, W, s2=None, op1=None, accum=None):
        kw = {}
        if op1 is not None:
            kw['op1'] = op1
        if accum is not None:
            kw['accum_out'] = accum
        self.op_(eng, lambda e: e.tensor_scalar(out=out, in0=in0, scalar1=s1, scalar2=s2, op0=op0, **kw),
                  reads=R, writes=W)

    def stt(self, out, in0, scalar, in1, op0, op1, R, W):
        self.op_('dve', lambda e: e.scalar_tensor_tensor(out=out, in0=in0, scalar=scalar, in1=in1,
                                                         op0=op0, op1=op1), reads=R, writes=W)

    def cp(self, eng, out, in_, R, W):
        if eng == 'act':
            self.op_('act', lambda e: e.copy(out=out, in_=in_), reads=R, writes=W)
        else:
            self.op_(eng, lambda e: e.tensor_copy(out=out, in_=in_), reads=R, writes=W)

    def red(self, out, in_, op, R, W):
        self.op_('dve', lambda e: e.tensor_reduce(out=out, in_=in_, axis=AX.X, op=op), reads=R, writes=W)

    def memset(self, eng, ap, val, W):
        self.op_(eng, lambda e: e.memset(ap, val), writes=W)

    def dma(self, q, out, in_, R, W, **kw):
        self.op_(q, lambda e: e.dma_start(out=out, in_=in_, **kw), reads=R, writes=W, dma=True)

    def bc_load(self, es, name, row_ap, d):
        t = self.sb(es, name, [128, d], F32)
        self.dma('sp', t[:], row_ap.to_broadcast([128, d]), [], [t.b])
        return t

    def wload(self, w, k, src, c0, c1):
        c = c0
        while c < c1:
            ce = min(c1, c + 2048)
            self.dma('pool', w[:, k, c:ce], src[:, c:ce], [], [w.b])
            c = ce

    def rstd(self, ssq, H, d, R):
        ln, rs = self.scr['ln'], self.scr['rs']
        self.act(ln[:, 0:H], ssq, AF.Ln, R, [ln.b], bias=self.eps_c[:, 0:1], scale=1.0 / d)
        self.act(rs[:, 0:H], ln[:, 0:H], AF.Exp, [ln.b], [rs.b], scale=-0.5)
        return rs[:, 0:H]

    def rmsn(self, src, H, d, g, dst, R, W):
        sq, ssq, tmp = self.scr['sq'], self.scr['ssq'], self.scr['tmp']
        sqv = sq[:, 0:H * d].rearrange("p (h c) -> p h c", h=H)
        self.tt('pool', sqv, src, src, ALU.mult, R, [sq.b])
        self.red(ssq[:, 0:H], sqv, ALU.add, [sq.b], [ssq.b])
        rs = self.rstd(ssq[:, 0:H], H, d, [ssq.b])
        tv = tmp[:, 0:H * d].rearrange("p (h c) -> p h c", h=H)
        self.tt('dve', tv, src, rs.unsqueeze(2).to_broadcast([128, H, d]), ALU.mult,
                list(R) + [self.scr['rs'].b], [tmp.b])
        self.tt('pool', dst, tv, g[:].unsqueeze(1).to_broadcast([128, H, d]), ALU.mult,
                [tmp.b, g.b], W)

    def rope(self, src, H, d2, n, dst, R, W):
        tab = self.rope32 if d2 == 32 else self.rope16
        cosv = tab[:, n, 0:d2]
        sinv = tab[:, n, d2:2 * d2]
        A, Bm = self.scr['ra'], self.scr['rb']
        s4 = src.rearrange("p h (two c) -> p h two c", two=2)
        d4 = dst.rearrange("p h (two c) -> p h two c", two=2)
        Av = A[:, 0:H * 2 * d2].rearrange("p (h two c) -> p h two c", h=H, two=2)
        Bv = Bm[:, 0:H * 2 * d2].rearrange("p (h two c) -> p h two c", h=H, two=2)
        cb = cosv.unsqueeze(1).unsqueeze(1).to_broadcast([128, H, 2, d2])
        sbv = sinv.unsqueeze(1).unsqueeze(1).to_broadcast([128, H, 2, d2])
        self.tt('dve', Av, s4, cb, ALU.mult, list(R) + [tab.b], [A.b])
        self.tt('pool', Bv, s4, sbv, ALU.mult, list(R) + [tab.b], [Bm.b])
        self.tt('dve', d4[:, :, 0, :], Av[:, :, 0, :], Bv[:, :, 1, :], ALU.subtract, [A.b, Bm.b], W)
        self.tt('pool', d4[:, :, 1, :], Bv[:, :, 0, :], Av[:, :, 1, :], ALU.add, [A.b, Bm.b], W)

    def attn_block(self, rhs_q, q_R, N, tiles, scale, acc_views, acc_bufs, mode='exp', LA=2, epilogue=None):
        started = set()
        nt = len(tiles)
        last_use = {}
        for i, tl in enumerate(tiles):
            for j in tl['subs']:
                last_use[j] = i
        Atiles = {}

        def emit_s(i):
            tl = tiles[i]
            bi = self.st_rr % len(self.st_banks)
            self.st_rr += 1
            bank, bb = self.st_banks[bi]
            c0 = tl['c0']
            masks = tl.get('masks', [])
            if mode != 'exp':
                E, eR = tl['Efn']()
            self.mm(bank[:, c0:N], tl['kT'], rhs_q[:, c0:N], True, len(masks) == 0, list(q_R) + tl['R'], [bb])
            for mi, (ml, mr, off, ncols, mR) in enumerate(masks):
                self.mm(bank[:, off:off + ncols], ml, mr, False, mi == len(masks) - 1, mR, [bb])
            ai = self.at_rr % len(self.at_tiles)
            self.at_rr += 1
            A = self.at_tiles[ai]
            Atiles[i] = A
            if mode == 'exp':
                self.act(A[:, c0:N], bank[:, c0:N], AF.Exp, [bb], [A.b], scale=scale)
            else:
                self.tt('dve', A[:, c0:N], bank[:, c0:N], E[:, c0:N], ALU.mult, [bb] + eR, [A.b])
                if tl.get('diag') is not None:
                    dj = tl['diag']
                    self.tt('pool', A[:, dj * 128:(dj + 1) * 128], A[:, dj * 128:(dj + 1) * 128],
                            self.tri01T[:], ALU.mult, [A.b, self.tri01T.b], [A.b])

        def emit_pv(i):
            tl = tiles[i]
            A = Atiles.pop(i)
            for j in tl['subs']:
                view, bk = acc_views[j]
                st = bk not in started
                started.add(bk)
                self.mm(view, A[:, j * 128:(j + 1) * 128], tl['V'], st, last_use[j] == i,
                        [A.b] + tl['R'], [acc_bufs[bk]])

        for step in range(nt):
            emit_s(step)
            self.pend.append(('pv', (lambda i=step: emit_pv(i))))
            self.npv += 1
            self._drain(LA)
        if epilogue is None:
            self.attn_flush()
            return None
        self.epi_id += 1
        eid = self.epi_id
        self.pend.append(('epi', epilogue, eid))
        self._drain(LA)
        return eid

    def _drain(self, limit):
        q = self.pend
        while q and (q[0][0] == 'epi' or self.npv > limit):
            it = q.pop(0)
            if it[0] == 'pv':
                self.npv -= 1
                it[1]()
            else:
                it[1]()
                self.epi_done.add(it[2])

    def attn_flush(self):
        self._drain(-1)

    def attn_sync(self, eid):
        while eid is not None and eid not in self.epi_done:
            q = self.pend
            it = q.pop(0)
            if it[0] == 'pv':
                self.npv -= 1
                it[1]()
            else:
                it[1]()
                self.epi_done.add(it[2])

    def build(self):
        nc = self.nc
        T, NT = self.T, self.NT
        with ExitStack() as es:
            self.S = S = Sched(nc, es)
            self._decl_inputs()
            self._decl_scratch()
            self.ps = []
            for i in range(8):
                t = es.enter_context(nc.psum_tensor(f"psb{i}", [128, 512], F32))
                self.ps.append(Tl(t, f'ps{i}'))
            self._consts(es)
            self._prep(es)
            S.barrier()
            xin = Tl(self.x_in, 'xin')
            xin.b = [Buf('xin')] * 1
            cur = xin
            for idx, L in enumerate(self.layers):
                last = idx == len(self.layers) - 1
                dst = self.out_t if last else self.xs[idx % 2]
                with ExitStack() as les:
                    if L % 2 == 0:
                        self.layer_even(les, L, cur, dst)
                    else:
                        self.layer_odd(les, L, cur, dst)
                S.barrier()
                cur = dst
            S.barrier()
            S.emit()
        return nc

    def _decl_inputs(self):
        T, NT = self.T, self.NT
        d = self.din
        self.x_in = d("x", [T, D])
        self.mem = d("mem", [256, D])
        self.pos_t = d("pos_t", [128, NT], I32)
        self.w = {}
        spec = dict(
            ln_g=[4, D], mem_norm_g=[4, D], mem_w_kv=[4, D, 512], mem_q_norm_g=[4, 64], mem_k_norm_g=[4, 64],
            w_out=[4, 1280, D], even_w_in=[2, D, EVEN_COLS], mla_q_lat_g=[2, 256], mla_kv_lat_g=[2, 128],
            mla_w_uq=[2, 256, 768], mla_w_ukv=[2, 128, 1024], mla_q_norm_g=[2, 96], mla_k_norm_g=[2, 96],
            nsa_q_norm_g=[2, 64], nsa_k_norm_g=[2, 3, 64], nsa_cmp_posT=[2, 2, 64, 32],
            nsa_cmp_w1=[2, 2, 2048, 64], nsa_cmp_w2=[2, 2, 64, 64], odd_w_in=[2, D, ODD_COLS],
            dsa_q_norm_g=[2, 64], dsa_k_norm_g=[2, 64], mlstm_conv_wT=[2, 512, 4], mlstm_conv_b=[2, 512],
            mlstm_i_bias=[2, 4], mlstm_f_bias=[2, 4], mlstm_h_norm_g=[2, 128])
        for k, shp in spec.items():
            self.w[k] = d(k, shp)
        ncp = self.ncmp_pad = max(1, T // 2048) * 128
        nb = self.n_blk = T // 64
        self.c = dict(
            ident=d("c_ident", [128, 128], BF16), identf=d("c_identf", [128, 128], F32),
            i4=d("c_i4", [128, 512], BF16), i8x2=d("c_i8x2", [128, 512], BF16),
            tri_neg=d("c_tri_neg", [128, 128], BF16), edge_neg=d("c_edge_neg", [128, 128], BF16),
            tri01T=d("c_tri01T", [128, 128], BF16), trinegf=d("c_trinegf", [128, 128], F32),
            invf=d("c_invf", [128, 32]), pw=d("c_pw", [128, 40]), cmpneg=d("c_cmpneg", [T, ncp], BF16),
            forced=d("c_forced", [T, nb]), overlap=d("c_overlap", [ncp, nb], BF16))

    def _decl_scratch(self):
        T, NT = self.T, self.NT
        s = self.dscr
        self.out_t = Tl(self.nc.dram_tensor("out", [T, D], F32, kind="ExternalOutput").ap(), 'out')
        self.out_t.b = [Buf(f'out{i}') for i in range(NT)]
        self.xs = [s("xs0", [T, D], F32, NT), s("xs1", [T, D], F32, NT)]
        self.rope_d = s("rope_d", [T, 64], F32)
        self.Y = s("Y", [T, 2304], F32)
        self.G = s("G", [T, 1792], BF16, NT)
        self.SG = s("SG", [T, 32], F32, NT)
        self.mla_qT = s("mla_qT", [8, 96, T], BF16)
        self.mla_kT = s("mla_kT", [8, 96, T], BF16)
        self.mla_v = s("mla_v", [8, 128, NT, 65], BF16)
        self.nsa_qT = s("nsa_qT", [2, NT, 64, 4, 128], BF16)
        self.nsa_kT = s("nsa_kT", [8, 64, T], BF16)
        self.nsa_v = s("nsa_v", [4, 128, NT, 65], BF16)
        self.mem_qT = s("mem_qT", [4, 64, T], BF16)
        self.dsa_qT = s("dsa_qT", [NT, 64, 8, 128], BF16)
        self.dsa_kT = s("dsa_kT", [64, T], BF16)
        self.dsa_v = s("dsa_v", [128, NT, 65], BF16)
        self.idx_qT = s("idx_qT", [8, 32, T], BF16)
        self.idx_kT = s("idx_kT", [32, T], BF16)
        self.idx_w = s("idx_w", [T, 8], F32)
        self.ml_raw = s("ml_raw", [512, T], F32)
        self.ml_if = s("ml_if", [8, T], F32)
        self.ml_qkT = s("ml_qkT", [512, T], BF16)
        self.ml_v = s("ml_v", [4, 128, NT, 129], BF16)
        self.ml_g = s("ml_g", [12, T], F32)

    def _consts(self, es):
        c = self.c
        def ld(name, shape, dt):
            t = self.sb(es, name, shape, dt)
            self.dma('sp', t[:], c[name], [], [t.b])
            return t
        self.ident = ld('ident', [128, 128], BF16)
        self.identf = ld('identf', [128, 128], F32)
        self.i4 = ld('i4', [128, 512], BF16)
        self.i8x2 = ld('i8x2', [128, 512], BF16)
        self.tri_neg = ld('tri_neg', [128, 128], BF16)
        self.edge_neg = ld('edge_neg', [128, 128], BF16)
        self.tri01T = ld('tri01T', [128, 128], BF16)
        self.trinegf = ld('trinegf', [128, 128], F32)
        self.invf = ld('invf', [128, 32], F32)
        self.pw = ld('pw', [128, 40], F32)
        self.eps_c = self.sb(es, 'eps_c', [128, 1], F32)
        self.memset('dve', self.eps_c[:], EPS, [self.eps_c.b])
        self.one_c = self.sb(es, 'one_c', [128, 1], F32)
        self.memset('dve', self.one_c[:], 1.0, [self.one_c.b])
        self.rope32 = self.sb(es, 'rope32', [128, self.NT, 64], F32)
        self.rope16 = self.sb(es, 'rope16', [128, self.NT, 32], F32)
        self.sc_sq = self.sb(es, 'sc_sq', [128, 512], F32)
        self.sc_tmp = self.sb(es, 'sc_tmp', [128, 512], F32)
        self.sc_ra = self.sb(es, 'sc_ra', [128, 512], F32)
        self.sc_rb = self.sb(es, 'sc_rb', [128, 512], F32)
        self.sc_ssq = self.sb(es, 'sc_ssq', [128, 16], F32)
        self.sc_ln = self.sb(es, 'sc_ln', [128, 16], F32)
        self.sc_rs = self.sb(es, 'sc_rs', [128, 16], F32)
        self.scr0 = dict(sq=self.sc_sq, tmp=self.sc_tmp, ra=self.sc_ra, rb=self.sc_rb, ssq=self.sc_ssq,
                         ln=self.sc_ln, rs=self.sc_rs)
        self.scr = self.scr0

    def _prep(self, es0):
        NT = self.NT
        PI = math.pi
        with ExitStack() as es:
            pi_t = self.sb(es, 'pos_i', [128, NT], I32)
            self.dma('sp', pi_t[:], self.pos_t, [], [pi_t.b])
            pf = self.sb(es, 'pos_f', [128, NT], F32)
            self.cp('dve', pf[:], pi_t[:], [pi_t.b], [pf.b])
            ang = self.sb(es, 'ang', [128, NT, 32], F32)
            self.tt('dve', ang[:], pf[:].unsqueeze(2).to_broadcast([128, NT, 32]),
                    self.invf[:].unsqueeze(1).to_broadcast([128, NT, 32]), ALU.mult,
                    [pf.b, self.invf.b], [ang.b])
            kf = self.sb(es, 'kf', [128, NT, 32], F32)
            ki = self.sb(es, 'ki', [128, NT, 32], I32)
            r = self.sb(es, 'r', [128, NT, 32], F32)
            m = self.sb(es, 'm', [128, NT, 32], F32)

            def wrap(buf):
                self.ts('dve', m[:], buf[:], PI, ALU.is_gt, [buf.b], [m.b], s2=-2 * PI, op1=ALU.mult)
                self.tt('dve', buf[:], buf[:], m[:], ALU.add, [buf.b, m.b], [buf.b])
                self.ts('dve', m[:], buf[:], -PI, ALU.is_lt, [buf.b], [m.b], s2=2 * PI, op1=ALU.mult)
                self.tt('dve', buf[:], buf[:], m[:], ALU.add, [buf.b, m.b], [buf.b])
                self.ts('dve', buf[:], buf[:], PI, ALU.min, [buf.b], [buf.b], s2=-PI, op1=ALU.max)
            self.ts('dve', kf[:], ang[:], 1.0 / (2 * PI), ALU.mult, [ang.b], [kf.b])
            self.cp('dve', ki[:], kf[:], [kf.b], [ki.b])
            self.cp('dve', kf[:], ki[:], [ki.b], [kf.b])
            C1 = 6.28125
            C2 = 2 * PI - C1
            self.stt(r[:], kf[:], -C1, ang[:], ALU.mult, ALU.add, [kf.b, ang.b], [r.b])
            self.stt(r[:], kf[:], -C2, r[:], ALU.mult, ALU.add, [kf.b, r.b], [r.b])
            wrap(r)
            self.act(self.rope32[:, :, 32:64], r[:], AF.Sin, [r.b], [self.rope32.b])
            self.ts('dve', r[:], r[:], PI / 2, ALU.add, [r.b], [r.b])
            wrap(r)
            self.act(self.rope32[:, :, 0:32], r[:], AF.Sin, [r.b], [self.rope32.b])
            self.cp('dve', self.rope16[:, :, 0:16], self.rope32[:, :, 0:32:2], [self.rope32.b], [self.rope16.b])
            self.cp('dve', self.rope16[:, :, 16:32], self.rope32[:, :, 32:64:2], [self.rope32.b], [self.rope16.b])
            self.dma('sp', self.rope_d[:].rearrange("(n p) c -> p n c", p=128), self.rope32[:],
                     [self.rope32.b], [self.rope_d.b])
            self.S.barrier()

    def load_win(self, es, L, w_in, ncols):
        win = self.sb(es, 'win', [128, 8, ncols], BF16)
        for k in range(8):
            self.wload(win, k, w_in[k * 128:(k + 1) * 128, :], 0, ncols)
        lng = self.bc_load(es, 'lng', self.w['ln_g'][L:L + 1, :], D)
        return win, lng

    def load_wout(self, es, L):
        wout = self.sb(es, 'wout', [128, 10, D], BF16)
        for k in range(10):
            self.wload(wout, k, self.w['w_out'][L, k * 128:(k + 1) * 128, :], 0, D)
        return wout

    def mem_kv(self, es, L):
        w = self.w
        kT = self.sb(es, 'memkT', [64, 4, 256], BF16)
        V = self.sb(es, 'memV', [128, 2, 4, 65], BF16)
        self.memset('pool', V[:], 1.0, [V.b])
        with ExitStack() as s2:
            wkv = self.sb(s2, 'wkv', [128, 8, 512], BF16)
            for k in range(8):
                self.wload(wkv, k, w['mem_w_kv'][L, k * 128:(k + 1) * 128, :], 0, 512)
            mg = self.bc_load(s2, 'mg', w['mem_norm_g'][L:L + 1, :], D)
            kg = self.bc_load(s2, 'kg', w['mem_k_norm_g'][L:L + 1, :], 64)
            mt = self.sb(s2, 'mt', [128, D], F32)
            junk = self.sb(s2, 'junk', [128, D], BF16)
            mh = self.sb(s2, 'mh', [128, D], BF16)
            mhT = self.sb(s2, 'mhT', [128, 8, 128], BF16)
            kv = self.sb(s2, 'kv', [128, 512], F32)
            kn = self.sb(s2, 'kn', [128, 256], BF16)
            ss = self.sb(s2, 'ss', [128, 1], F32)
            for i in range(2):
                self.dma('sp', mt[:], self.mem[i * 128:(i + 1) * 128, :], [], [mt.b])
                self.act(junk[:], mt[:], AF.Square, [mt.b], [junk.b, ss.b], accum=ss[:])
                rs = self.rstd(ss[:, 0:1], 1, D, [ss.b])
                self.stt(mh[:], mt[:], rs, mg[:], ALU.mult, ALU.mult, [mt.b, self.scr['rs'].b, mg.b], [mh.b])
                pT = self.ps[2]
                pTb = pT[:].bitcast(BF16)
                for k in range(8):
                    self.tr(pTb[:, k * 128:(k + 1) * 128], mh[:, k * 128:(k + 1) * 128], self.ident[:],
                            [mh.b, self.ident.b], [pT.b])
                self.cp('act', mhT[:].rearrange("p k c -> p (k c)"), pTb[:, 0:1024], [pT.b], [mhT.b])
                pU = self.ps[0]
                for k in range(8):
                    self.mm(pU[:, 0:512], mhT[:, k, :], wkv[:, k, :], k == 0, k == 7, [mhT.b, wkv.b], [pU.b])
                self.cp('act', kv[:], pU[:, 0:512], [pU.b], [kv.b])
                self.rmsn(kv[:, 0:256].rearrange("p (h c) -> p h c", h=4), 4, 64, kg,
                          kn[:].rearrange("p (h c) -> p h c", h=4), [kv.b], [kn.b])
                self.cp('dve', V[:, i, :, 0:64], kv[:, 256:512].rearrange("p (h c) -> p h c", h=4), [kv.b], [V.b])
                pK = self.ps[3]
                pKb = pK[:].bitcast(BF16)
                for h in range(4):
                    self.tr(pKb[0:64, h * 128:(h + 1) * 128], kn[:, h * 64:(h + 1) * 64], self.ident[:],
                            [kn.b, self.ident.b], [pK.b])
                self.cp('act', kT[:, :, i * 128:(i + 1) * 128],
                        pKb[0:64, 0:512].rearrange("p (h c) -> p h c", h=4), [pK.b], [kT.b])
            self.S.barrier()
        return kT, V

    def p1_front(self, n, x_src, xt, ht, hT, u, ss, junk, lng, win, ncols):
        sl = n % 2
        x_t, h_t, hT_t, u_t = xt[sl], ht[sl], hT[sl], u[sl]
        xb = x_src.b[n] if len(x_src.b) > 1 else x_src.b[0]
        self.dma('sp', x_t[:], x_src[n * 128:(n + 1) * 128, :], [xb], [x_t.b])
        self.act(junk[:], x_t[:], AF.Square, [x_t.b], [junk.b, ss.b], accum=ss[:])
        rs = self.rstd(ss[:, 0:1], 1, D, [ss.b])
        self.stt(h_t[:], x_t[:], rs, lng[:], ALU.mult, ALU.mult, [x_t.b, self.scr['rs'].b, lng.b], [h_t.b])
        pT = self.ps[2]
        pTb = pT[:].bitcast(BF16)
        for k in range(8):
            self.tr(pTb[:, k * 128:(k + 1) * 128], h_t[:, k * 128:(k + 1) * 128], self.ident[:],
                    [h_t.b, self.ident.b], [pT.b])
        self.cp('act', hT_t[:].rearrange("p k c -> p (k c)"), pTb[:, 0:1024], [pT.b], [hT_t.b])
        nchunk = (ncols + 511) // 512
        for c in range(nchunk):
            c0 = c * 512
            wd = min(512, ncols - c0)
            pU = self.ps[c % 2]
            for k in range(8):
                self.mm(pU[:, 0:wd], hT_t[:, k, :], win[:, k, c0:c0 + wd], k == 0, k == 7,
                        [hT_t.b, win.b], [pU.b])
            self.cp('act' if c % 2 == 0 else 'dve', u_t[:, c0:c0 + wd], pU[:, 0:wd], [pU.b], [u_t.b])
        return u_t

    def transposes_out(self, srcs, rows, stage, dst_ap, dst_b, pidx):
        pT = self.ps[pidx]
        pTb = pT[:].bitcast(BF16)
        k = len(srcs)
        for i, (ap, R) in enumerate(srcs):
            self.tr(pTb[0:rows, i * 128:(i + 1) * 128], ap, self.ident[:], list(R) + [self.ident.b], [pT.b])
        self.cp('act', stage[0:rows, 0:k, :], pTb[0:rows, 0:k * 128].rearrange("p (k c) -> p k c", k=k),
                [pT.b], [stage.b])
        self.dma('sp', dst_ap, stage[0:rows, 0:k, :], [stage.b], dst_b)

    def p3(self, es, L, x_src, x_dst, wout, mixfn):
        NT = self.NT
        xt = [self.sb(es, f'p3x{i}', [128, D], F32) for i in range(2)]
        mixT = [self.sb(es, f'p3mT{i}', [128, 10, 128], BF16) for i in range(2)]
        xo = [self.sb(es, f'p3o{i}', [128, D], F32) for i in range(2)]

        def one(n):
            sl = n % 2
            pb = 4 * sl
            mix = mixfn(n)
            xb = x_src.b[n] if len(x_src.b) > 1 else x_src.b[0]
            self.dma('sp', xt[sl][:], x_src[n * 128:(n + 1) * 128, :], [xb], [xt[sl].b])
            for half in range(2):
                pT = self.ps[pb + 2 + half]
                pTb = pT[:].bitcast(BF16)
                for k in range(5):
                    kk = half * 5 + k
                    self.tr(pTb[:, k * 128:(k + 1) * 128], mix[:, kk * 128:(kk + 1) * 128], self.ident[:],
                            [mix.b, self.ident.b], [pT.b])
                self.cp('act', mixT[sl][:, half * 5:half * 5 + 5, :].rearrange("p k c -> p (k c)"),
                        pTb[:, 0:640], [pT.b], [mixT[sl].b])
            for c in range(2):
                pU = self.ps[pb + c]
                for k in range(10):
                    self.mm(pU[:, 0:512], mixT[sl][:, k, :], wout[:, k, c * 512:(c + 1) * 512], k == 0, k == 9,
                            [mixT[sl].b, wout.b], [pU.b])
                self.tt('dve', xo[sl][:, c * 512:(c + 1) * 512], pU[:, 0:512], xt[sl][:, c * 512:(c + 1) * 512],
                        ALU.add, [pU.b, xt[sl].b], [xo[sl].b])
            self.dma('sp', x_dst[n * 128:(n + 1) * 128, :], xo[sl][:], [xo[sl].b], [x_dst.b[n]])

        for n0 in range(0, NT, 2):
            self.chains_begin(['P0', 'P1'])
            self.chain('P0')
            one(n0)
            if n0 + 1 < NT:
                self.chain('P1')
                one(n0 + 1)
            self.chains_emit()

    def attn_setup(self, es, dvp_two_banks=False):
        self.st_banks = [(self.ps[i], self.ps[i].b) for i in range(3)]
        self.st_rr = 0
        self.at_tiles = [self.sb(es, f'At{i}', [128, 512], BF16) for i in range(3)]
        self.at_rr = 0
        self.pend = []
        self.npv = 0
        self.epi_id = 0
        self.epi_done = set()

    def mem_attn(self, es, memkT, memV):
        T = self.T
        qT = [self.sb(es, f'mqT{i}', [64, T], BF16) for i in range(2)]
        ost = [self.sb(es, f'most{i}', [128, 4, 64], F32) for i in range(2)]
        rec = self.sb(es, 'mrec', [128, 4], F32)
        blk = 0
        for h in range(4):
            q = qT[h % 2]
            self.dma('sp', q[:], self.mem_qT[h], [self.mem_qT.b], [q.b])
            for cq in range(T // 512):
                set_i = blk % 2
                blk += 1
                accb = self.ps[3 + set_i]
                views = {j: (accb[:, j * 65:(j + 1) * 65], 0) for j in range(4)}
                tiles = [dict(kT=memkT[:, h, kt * 128:(kt + 1) * 128], V=memV[:, kt, h, :],
                              R=[memkT.b, memV.b], c0=0, subs=[0, 1, 2, 3]) for kt in range(2)]
                def epi(accb=accb, o=ost[set_i], cq=cq, h=h):
                    av = accb[:, 0:260].rearrange("p (j c) -> p j c", j=4)
                    self.op_('dve', lambda e, av=av: e.reciprocal(out=rec[:], in_=av[:, :, 64]), reads=[accb.b],
                             writes=[rec.b])
                    self.tt('dve', o[:], av[:, :, 0:64], rec[:].unsqueeze(2).to_broadcast([128, 4, 64]), ALU.mult,
                            [accb.b, rec.b], [o.b])
                    self.dma('sp', self.Y[cq * 512:(cq + 1) * 512, 1024 + h * 64:1024 + (h + 1) * 64]
                             .rearrange("(j p) c -> p j c", p=128), o[:], [o.b], [self.Y.b])
                self.attn_block(q[:, cq * 512:(cq + 1) * 512], [q.b], 512, tiles, 0.125, views, [accb.b], epilogue=epi)
        self.attn_flush()

    def layer_even(self, es, L, x_src, x_dst):
        li = L // 2
        w = self.w
        T, NT = self.T, self.NT
        S = self.S
        memkT, memV = self.mem_kv(es, L)
        with ExitStack() as p1:
            win, lng = self.load_win(p1, L, w['even_w_in'][li], EVEN_COLS)
            wuq = self.sb(p1, 'wuq', [128, 2, 768], BF16)
            for k in range(2):
                self.wload(wuq, k, w['mla_w_uq'][li, k * 128:(k + 1) * 128, :], 0, 768)
            wukv = self.sb(p1, 'wukv', [128, 1, 1024], BF16)
            self.wload(wukv, 0, w['mla_w_ukv'][li], 0, 1024)
            g_ql = self.bc_load(p1, 'g_ql', w['mla_q_lat_g'][li:li + 1, :], 256)
            g_kvl = self.bc_load(p1, 'g_kvl', w['mla_kv_lat_g'][li:li + 1, :], 128)
            g_qn = self.bc_load(p1, 'g_qn', w['mla_q_norm_g'][li:li + 1, 0:64], 64)
            g_qp = self.bc_load(p1, 'g_qp', w['mla_q_norm_g'][li:li + 1, 64:96], 32)
            g_kn = self.bc_load(p1, 'g_kn', w['mla_k_norm_g'][li:li + 1, 0:64], 64)
            g_kp = self.bc_load(p1, 'g_kp', w['mla_k_norm_g'][li:li + 1, 64:96], 32)
            g_bq = self.bc_load(p1, 'g_bq', w['nsa_q_norm_g'][li:li + 1, :], 64)
            g_ks = self.bc_load(p1, 'g_ks', w['nsa_k_norm_g'][li, 1:2, :], 64)
            g_kw = self.bc_load(p1, 'g_kw', w['nsa_k_norm_g'][li, 2:3, :], 64)
            g_mq = self.bc_load(p1, 'g_mq', w['mem_q_norm_g'][L:L + 1, :], 64)
            xt = [self.sb(p1, f'xt{i}', [128, D], F32) for i in range(2)]
            ht = [self.sb(p1, f'ht{i}', [128, D], BF16) for i in range(2)]
            hT = [self.sb(p1, f'hT{i}', [128, 8, 128], BF16) for i in range(2)]
            u = [self.sb(p1, f'u{i}', [128, EVEN_COLS], F32) for i in range(2)]
            ss = self.sb(p1, 'ss', [128, 1], F32)
            junk = self.sb(p1, 'junk', [128, D], BF16)
            latn = self.sb(p1, 'latn', [128, 384], BF16)
            latT = self.sb(p1, 'latT', [128, 3, 128], BF16)
            qsb = self.sb(p1, 'qsb', [128, 768], F32)
            kvsb = self.sb(p1, 'kvsb', [128, 1024], F32)
            qpe = self.sb(p1, 'qpe', [128, 256], F32)
            kpe = self.sb(p1, 'kpe', [128, 32], F32)
            kpeb = self.sb(p1, 'kpeb', [128, 32], BF16)
            qf = self.sb(p1, 'qf', [128, 8, 96], BF16)
            kfm = self.sb(p1, 'kfm', [128, 8, 96], BF16)
            vaug = self.sb(p1, 'vaug', [128, 8, 65], BF16)
            self.memset('pool', vaug[:], 1.0, [vaug.b])
            bqn = self.sb(p1, 'bqn', [128, 512], F32)
            bqf = self.sb(p1, 'bqf', [128, 512], BF16)
            kn2 = self.sb(p1, 'kn2', [128, 128], F32)
            kmisc = self.sb(p1, 'kmisc', [128, 8, 64], BF16)
            nv = self.sb(p1, 'nv', [128, 4, 65], BF16)
            self.memset('pool', nv[:], 1.0, [nv.b])
            mqf = self.sb(p1, 'mqf', [128, 256], BF16)
            gt = self.sb(p1, 'gt', [128, 1280], BF16)
            sg = self.sb(p1, 'sg', [128, 24], F32)
            stq = self.sb(p1, 'stq', [96, 8, 128], BF16)
            stk = self.sb(p1, 'stk', [96, 8, 128], BF16)
            stb = self.sb(p1, 'stb', [64, 8, 128], BF16)
            stm = self.sb(p1, 'stm', [64, 8, 128], BF16)
            stmq = self.sb(p1, 'stmq', [64, 4, 128], BF16)
            scrA = self.new_scr(p1, 'A', 512)
            scrB = self.new_scr(p1, 'B', 512)
            scrC = self.new_scr(p1, 'C', 128)
            ut_next = self.p1_front(0, x_src, xt, ht, hT, u, ss, junk, lng, win, EVEN_COLS)
            for n in range(NT):
                ut = ut_next
                U = lambda a, b_, ut=ut: ut[:, a:b_]
                ub = [ut.b]
                tsl = slice(n * 128, (n + 1) * 128)
                self.chains_begin(['A', 'F', 'B', 'C', 'M', 'G'])
                if n + 1 < NT:
                    self.chain('F')
                    ut_next = self.p1_front(n + 1, x_src, xt, ht, hT, u, ss, junk, lng, win, EVEN_COLS)
                self.chain('G')
                self.act(gt[:, 0:512], U(416, 928), AF.Silu, ub, [gt.b])
                self.act(gt[:, 512:1024], U(2232, 2744), AF.Silu, ub, [gt.b])
                self.act(gt[:, 1024:1280], U(3000, 3256), AF.Silu, ub, [gt.b])
                self.dma('sp', self.G[tsl, 0:1280], gt[:], [gt.b], [self.G.b[n]])
                self.act(sg[:], U(2208, 2232), AF.Sigmoid, ub, [sg.b])
                self.dma('sp', self.SG[tsl, 0:24], sg[:], [sg.b], [self.SG.b[n]])
                self.chain('A', scrA)
                self.rmsn(U(0, 256).rearrange("p (h c) -> p h c", h=1), 1, 256, g_ql,
                          latn[:, 0:256].rearrange("p (h c) -> p h c", h=1), ub, [latn.b])
                self.rmsn(U(256, 384).rearrange("p (h c) -> p h c", h=1), 1, 128, g_kvl,
                          latn[:, 256:384].rearrange("p (h c) -> p h c", h=1), ub, [latn.b])
                pT = self.ps[3]
                pTb = pT[:].bitcast(BF16)
                for k in range(3):
                    self.tr(pTb[:, k * 128:(k + 1) * 128], latn[:, k * 128:(k + 1) * 128], self.ident[:],
                            [latn.b, self.ident.b], [pT.b])
                self.cp('act', latT[:].rearrange("p k c -> p (k c)"), pTb[:, 0:384], [pT.b], [latT.b])
                pQ, pQ2, pK, pK2 = self.ps[3], self.ps[4], self.ps[5], self.ps[4]
                for k in range(2):
                    self.mm(pQ[:, 0:512], latT[:, k, :], wuq[:, k, 0:512], k == 0, k == 1, [latT.b, wuq.b], [pQ.b])
                for k in range(2):
                    self.mm(pQ2[:, 0:256], latT[:, k, :], wuq[:, k, 512:768], k == 0, k == 1, [latT.b, wuq.b], [pQ2.b])
                self.cp('act', qsb[:, 0:512], pQ[:, 0:512], [pQ.b], [qsb.b])
                self.cp('dve', qsb[:, 512:768], pQ2[:, 0:256], [pQ2.b], [qsb.b])
                self.mm(pK[:, 0:512], latT[:, 2, :], wukv[:, 0, 0:512], True, True, [latT.b, wukv.b], [pK.b])
                self.mm(pK2[:, 0:512], latT[:, 2, :], wukv[:, 0, 512:1024], True, True, [latT.b, wukv.b], [pK2.b])
                self.cp('act', kvsb[:, 0:512], pK[:, 0:512], [pK.b], [kvsb.b])
                self.cp('dve', kvsb[:, 512:1024], pK2[:, 0:512], [pK2.b], [kvsb.b])
                q3 = qsb[:].rearrange("p (h c) -> p h c", h=8)
                kv3 = kvsb[:].rearrange("p (h c) -> p h c", h=8)
                self.rmsn(q3[:, :, 0:64], 8, 64, g_qn, qf[:, :, 0:64], [qsb.b], [qf.b])
                qpe3 = qpe[:].rearrange("p (h c) -> p h c", h=8)
                self.rmsn(q3[:, :, 64:96], 8, 32, g_qp, qpe3, [qsb.b], [qpe.b])
                self.rope(qpe3, 8, 16, n, qf[:, :, 64:96], [qpe.b], [qf.b])
                self.rmsn(kv3[:, :, 0:64], 8, 64, g_kn, kfm[:, :, 0:64], [kvsb.b], [kfm.b])
                self.cp('dve', vaug[:, :, 0:64], kv3[:, :, 64:128], [kvsb.b], [vaug.b])
                kpe3 = kpe[:].rearrange("p (h c) -> p h c", h=1)
                self.rmsn(U(384, 416).rearrange("p (h c) -> p h c", h=1), 1, 32, g_kp, kpe3, ub, [kpe.b])
                self.rope(kpe3, 1, 16, n, kpeb[:].rearrange("p (h c) -> p h c", h=1), [kpe.b], [kpeb.b])
                self.cp('pool', kfm[:, :, 64:96], kpeb[:].unsqueeze(1).to_broadcast([128, 8, 32]), [kpeb.b], [kfm.b])
                self.transposes_out([(qf[:, h, :], [qf.b]) for h in range(8)], 96, stq,
                                    self.mla_qT[:, :, tsl].rearrange("h d t -> d h t"), [self.mla_qT.b], 3)
                self.transposes_out([(kfm[:, h, :], [kfm.b]) for h in range(8)], 96, stk,
                                    self.mla_kT[:, :, tsl].rearrange("h d t -> d h t"), [self.mla_kT.b], 5)
                self.dma('sp', self.mla_v[:, :, n, :].rearrange("h p c -> p h c"), vaug[:], [vaug.b], [self.mla_v.b])
                self.chain('B', scrB)
                bq3 = bqn[:].rearrange("p (h c) -> p h c", h=8)
                self.rmsn(U(928, 1440).rearrange("p (h c) -> p h c", h=8), 8, 64, g_bq, bq3, ub, [bqn.b])
                self.rope(bq3, 8, 32, n, bqf[:].rearrange("p (h c) -> p h c", h=8), [bqn.b], [bqf.b])
                self.chain('C', scrC)
                k23 = kn2[:].rearrange("p (h c) -> p h c", h=2)
                self.rmsn(U(1696, 1824).rearrange("p (h c) -> p h c", h=2), 2, 64, g_ks, k23, ub, [kn2.b])
                self.rope(k23, 2, 32, n, kmisc[:, 0:2, :], [kn2.b], [kmisc.b])
                self.rmsn(U(1952, 2080).rearrange("p (h c) -> p h c", h=2), 2, 64, g_kw, k23, ub, [kn2.b])
                self.rope(k23, 2, 32, n, kmisc[:, 2:4, :], [kn2.b], [kmisc.b])
                self.cp('pool', kmisc[:, 4:8, :], U(1440, 1696).rearrange("p (h c) -> p h c", h=4), ub, [kmisc.b])
                self.cp('dve', nv[:, 0:2, 0:64], U(1824, 1952).rearrange("p (h c) -> p h c", h=2), ub, [nv.b])
                self.cp('dve', nv[:, 2:4, 0:64], U(2080, 2208).rearrange("p (h c) -> p h c", h=2), ub, [nv.b])
                self.chain('B', scrB)
                self.transposes_out([(bqf[:, h * 64:(h + 1) * 64], [bqf.b]) for h in range(8)], 64, stb,
                                    self.nsa_qT[:, n].rearrange("g d r t -> d g r t"), [self.nsa_qT.b], 6)
                self.chain('C', scrC)
                self.transposes_out([(kmisc[:, i, :], [kmisc.b]) for i in range(8)], 64, stm,
                                    self.nsa_kT[:, :, tsl].rearrange("k d t -> d k t"), [self.nsa_kT.b], 7)
                self.dma('sp', self.nsa_v[:, :, n, :].rearrange("k p c -> p k c"), nv[:], [nv.b], [self.nsa_v.b])
                self.chain('B', scrB)
                self.rmsn(U(2744, 3000).rearrange("p (h c) -> p h c", h=4), 4, 64, g_mq,
                          mqf[:].rearrange("p (h c) -> p h c", h=4), ub, [mqf.b])
                self.transposes_out([(mqf[:, h * 64:(h + 1) * 64], [mqf.b]) for h in range(4)], 64, stmq,
                                    self.mem_qT[:, :, tsl].rearrange("h d t -> d h t"), [self.mem_qT.b], 6)
                self.chains_emit()
            S.barrier()
        kcmpT = self.sb(es, 'kcmpT', [64, 2, self.ncmp_pad], BF16)
        vcmp = self.sb(es, 'vcmp', [128, 2, self.ncmp_pad // 128, 129], BF16)
        self.nsa_compress(li, kcmpT, vcmp)
        S.barrier()
        with ExitStack() as pa:
            self.attn_setup(pa)
            self.mla_attn(pa)
            S.barrier()
        with ExitStack() as pa:
            self.attn_setup(pa)
            self.mem_attn(pa, memkT, memV)
            S.barrier()
        with ExitStack() as pa:
            self.attn_setup(pa)
            self.nsa_attn(pa, kcmpT, vcmp)
            S.barrier()
        with ExitStack() as p3:
            wout = self.load_wout(p3, L)
            yt = [self.sb(p3, f'yt{i}', [128, 2304], F32) for i in range(2)]
            gtt = [self.sb(p3, f'gtt{i}', [128, 1280], BF16) for i in range(2)]
            sgt = [self.sb(p3, f'sgt{i}', [128, 24], F32) for i in range(2)]
            mix = [self.sb(p3, f'mix{i}', [128, 1280], BF16) for i in range(2)]
            ybs = [self.sb(p3, f'yb{i}', [128, 512], F32) for i in range(2)]
            yb2s = [self.sb(p3, f'yb2{i}', [128, 512], F32) for i in range(2)]

            def mixfn(n):
                sl = n % 2
                y, g, s_, m = yt[sl], gtt[sl], sgt[sl], mix[sl]
                yb, yb2 = ybs[sl], yb2s[sl]
                tsl = slice(n * 128, (n + 1) * 128)
                self.dma('sp', y[:], self.Y[tsl, :], [self.Y.b], [y.b])
                self.dma('sp', g[:], self.G[tsl, 0:1280], [self.G.b[n]], [g.b])
                self.dma('sp', s_[:], self.SG[tsl, 0:24], [self.SG.b[n]], [s_.b])
                self.tt('dve', m[:, 0:512], y[:, 0:512], g[:, 0:512], ALU.mult, [y.b, g.b], [m.b])
                self.tt('pool', m[:, 1024:1280], y[:, 1024:1280], g[:, 1024:1280], ALU.mult, [y.b, g.b], [m.b])
                s3 = s_[:].rearrange("p (h c) -> p h c", c=3)
                y3 = lambda a: y[:, a:a + 512].rearrange("p (h c) -> p h c", h=8)
                b3 = yb[:].rearrange("p (h c) -> p h c", h=8)
                b23 = yb2[:].rearrange("p (h c) -> p h c", h=8)
                self.tt('dve', b3, y3(1280), s3[:, :, 0:1].to_broadcast([128, 8, 64]), ALU.mult, [y.b, s_.b], [yb.b])
                self.tt('pool', b23, y3(512), s3[:, :, 1:2].to_broadcast([128, 8, 64]), ALU.mult, [y.b, s_.b], [yb2.b])
                self.tt('dve', b3, b3, b23, ALU.add, [yb.b, yb2.b], [yb.b])
                self.tt('pool', b23, y3(1792), s3[:, :, 2:3].to_broadcast([128, 8, 64]), ALU.mult, [y.b, s_.b], [yb2.b])
                self.tt('dve', b3, b3, b23, ALU.add, [yb.b, yb2.b], [yb.b])
                self.tt('dve', m[:, 512:1024], yb[:], g[:, 512:1024], ALU.mult, [yb.b, g.b], [m.b])
                return m
            self.p3(p3, L, x_src, x_dst, wout, mixfn)
            S.barrier()

    def mla_attn(self, es):
        T, NT = self.T, self.NT
        qT = [self.sb(es, f'aqT{i}', [96, T], BF16) for i in range(2)]
        kT = [self.sb(es, f'akT{i}', [96, T], BF16) for i in range(2)]
        V = [self.sb(es, f'aV{i}', [128, NT, 65], BF16) for i in range(2)]
        ost = [self.sb(es, f'aost{i}', [128, 4, 64], F32) for i in range(2)]
        rec = self.sb(es, 'arec', [128, 4], F32)
        scale = 96 ** -0.5
        blk = 0
        for h in range(8):
            q, k, v = qT[h % 2], kT[h % 2], V[h % 2]
            self.dma('sp', q[:], self.mla_qT[h], [self.mla_qT.b], [q.b])
            self.dma('sp', k[:], self.mla_kT[h], [self.mla_kT.b], [k.b])
            self.dma('sp', v[:], self.mla_v[h], [self.mla_v.b], [v.b])
            for cq in range(T // 512):
                set_i = blk % 2
                blk += 1
                accb = self.ps[3 + set_i]
                views = {j: (accb[:, j * 65:(j + 1) * 65], 0) for j in range(4)}
                tiles = []
                for kt in range(4 * cq + 4):
                    vv = kt - 4 * cq
                    tl = dict(kT=k[:, kt * 128:(kt + 1) * 128], V=v[:, kt, :], R=[k.b, v.b],
                              c0=max(0, vv) * 128, subs=list(range(max(0, vv), 4)))
                    if vv >= 0:
                        tl['masks'] = [(self.tri_neg[:], self.ident[:], vv * 128, 128,
                                        [self.tri_neg.b, self.ident.b])]
                    tiles.append(tl)
                def epi(accb=accb, o=ost[set_i], cq=cq, h=h):
                    av = accb[:, 0:260].rearrange("p (j c) -> p j c", j=4)
                    self.op_('dve', lambda e, av=av: e.reciprocal(out=rec[:], in_=av[:, :, 64]), reads=[accb.b],
                             writes=[rec.b])
                    self.tt('dve', o[:], av[:, :, 0:64], rec[:].unsqueeze(2).to_broadcast([128, 4, 64]), ALU.mult,
                            [accb.b, rec.b], [o.b])
                    self.dma('sp', self.Y[cq * 512:(cq + 1) * 512, h * 64:(h + 1) * 64]
                             .rearrange("(j p) c -> p j c", p=128), o[:], [o.b], [self.Y.b])
                self.attn_block(q[:, cq * 512:(cq + 1) * 512], [q.b], 512, tiles, scale, views, [accb.b], epilogue=epi)
        self.attn_flush()

    def nsa_compress(self, li, kcmpT, vcmp):
        w = self.w
        T = self.T
        n_cmp = (T - 32) // 16 + 1
        ncp = self.ncmp_pad
        nct = ncp // 128
        self.memset('pool', kcmpT[:], 0.0, [kcmpT.b])
        self.memset('pool', vcmp[:], 0.0, [vcmp.b])
        with ExitStack() as es:
            w1 = self.sb(es, 'cw1', [64, 2, 32, 64], BF16)
            w2 = self.sb(es, 'cw2', [64, 2, 64], BF16)
            peT = self.sb(es, 'cpeT', [64, 2, 32], BF16)
            for kv in range(2):
                self.dma('pool', w1[:, kv], w['nsa_cmp_w1'][li, kv].rearrange("(l d) o -> d l o", d=64), [], [w1.b])
                self.dma('pool', w2[:, kv], w['nsa_cmp_w2'][li, kv], [], [w2.b])
                self.dma('pool', peT[:, kv], w['nsa_cmp_posT'][li, kv], [], [peT.b])
            g_kc = self.bc_load(es, 'g_kc', w['nsa_k_norm_g'][li, 0:1, :], 64)
            ovl = self.sb(es, 'ovl', [128, nct, self.n_blk], BF16)
            self.dma('sp', ovl[:], self.c['overlap'].rearrange("(k p) j -> p k j", p=128), [], [ovl.b])
            xT = [self.sb(es, f'cxT{i}', [64, T], BF16) for i in range(2)]
            bias = self.sb(es, 'cbias', [64, 1], F32)
            hid = self.sb(es, 'chid', [64, ncp], BF16)
            self.memset('pool', hid[:], 0.0, [hid.b])
            ctm = self.sb(es, 'ctm', [128, 64], F32)
            ctn = self.sb(es, 'ctn', [128, 64], F32)
            ctb = self.sb(es, 'ctb', [128, 64], BF16)
            rp = self.sb(es, 'crp', [128, 64], F32)
            it = 0
            for kv in range(2):
                for g in range(2):
                    x = xT[it % 2]
                    it += 1
                    self.dma('sp', x[:], self.nsa_kT[4 + kv * 2 + g], [self.nsa_kT.b], [x.b])
                    pH = self.ps[it % 2]
                    for l in range(32):
                        self.mm(pH[0:64, 0:n_cmp], w1[:, kv, l, :], x[:, l:l + 16 * (n_cmp - 1) + 1:16],
                                l == 0, False, [w1.b, x.b], [pH.b])
                        self.mm(pH[0:64, 511:512], w1[:, kv, l, :], peT[:, kv, l:l + 1], False, l == 31,
                                [w1.b, peT.b], [pH.b])
                    self.cp('dve', bias[:], pH[0:64, 511:512], [pH.b], [bias.b])
                    self.act(hid[:, 0:n_cmp], pH[0:64, 0:n_cmp], AF.Silu, [pH.b, bias.b], [hid.b], bias=bias[:, 0:1])
                    for kt in range(nct):
                        pO = self.ps[2 + kt % 2]
                        self.mm(pO[:, 0:64], hid[:, kt * 128:(kt + 1) * 128], w2[:, kv, :], True, True,
                                [hid.b, w2.b], [pO.b])
                        if kv == 1:
                            self.cp('act', vcmp[:, g, kt, 0:64], pO[:, 0:64], [pO.b], [vcmp.b])
                        else:
                            self.cp('act', ctm[:], pO[:, 0:64], [pO.b], [ctm.b])
                            self.rmsn(ctm[:].rearrange("p (h c) -> p h c", h=1), 1, 64, g_kc,
                                      ctn[:].rearrange("p (h c) -> p h c", h=1), [ctm.b], [ctn.b])
                            nrow = min(128, n_cmp - kt * 128)
                            r0 = 31 + 16 * 128 * kt
                            self.memset('dve', rp[:], 0.0, [rp.b])
                            self.dma('sp', rp[0:nrow, :], self.rope_d[r0:r0 + 16 * (nrow - 1) + 1:16, :],
                                     [self.rope_d.b], [rp.b])
                            A, Bm = self.sc_ra, self.sc_rb
                            c2 = ctn[:].rearrange("p (two c) -> p two c", two=2)
                            o2 = ctb[:].rearrange("p (two c) -> p two c", two=2)
                            Av = A[:, 0:64].rearrange("p (two c) -> p two c", two=2)
                            Bv = Bm[:, 0:64].rearrange("p (two c) -> p two c", two=2)
                            self.tt('dve', Av, c2, rp[:, 0:32].unsqueeze(1).to_broadcast([128, 2, 32]), ALU.mult,
                                    [ctn.b, rp.b], [A.b])
                            self.tt('dve', Bv, c2, rp[:, 32:64].unsqueeze(1).to_broadcast([128, 2, 32]), ALU.mult,
                                    [ctn.b, rp.b], [Bm.b])
                            self.tt('dve', o2[:, 0, :], Av[:, 0, :], Bv[:, 1, :], ALU.subtract, [A.b, Bm.b], [ctb.b])
                            self.tt('dve', o2[:, 1, :], Bv[:, 0, :], Av[:, 1, :], ALU.add, [A.b, Bm.b], [ctb.b])
                            pT = self.ps[4]
                            pTb = pT[:].bitcast(BF16)
                            self.tr(pTb[0:64, 0:128], ctb[:], self.ident[:], [ctb.b, self.ident.b], [pT.b])
                            self.cp('act', kcmpT[:, g, kt * 128:(kt + 1) * 128], pTb[0:64, 0:128], [pT.b], [kcmpT.b])
            for g in range(2):
                for kt in range(nct):
                    self.memset('pool', vcmp[:, g, kt, 64:65], 1.0, [vcmp.b])
                    self.cp('pool', vcmp[:, g, kt, 65:65 + self.n_blk], ovl[:, kt, :], [ovl.b], [vcmp.b])
            self.S.barrier()

    def nsa_attn(self, es, kcmpT, vcmp):
        T, NT = self.T, self.NT
        nb = self.n_blk
        nct = self.ncmp_pad // 128
        dvc = 65 + nb
        ksT = self.sb(es, 'ksT', [64, T], BF16)
        kwT = self.sb(es, 'kwT', [64, T], BF16)
        vs = self.sb(es, 'vs', [128, NT, 65], BF16)
        vw = self.sb(es, 'vw', [128, NT, 65], BF16)
        qt_ = [self.sb(es, f'nq{i}', [64, 512], BF16) for i in range(2)]
        cneg = [self.sb(es, f'cneg{i}', [128, self.ncmp_pad], BF16) for i in range(2)]
        forced = [self.sb(es, f'forced{i}', [128, nb], F32) for i in range(2)]
        negm = [self.sb(es, f'negm{i}', [128, T], BF16) for i in range(2)]
        rec = self.sb(es, 'nrec', [128, 4], F32)
        score = self.sb(es, 'nscore', [128, nb], F32)
        work = self.sb(es, 'nwork', [128, nb], F32)
        m8 = self.sb(es, 'nm8', [128, 8], F32)
        thr = self.sb(es, 'nthr', [128, 1], F32)
        nsel = self.sb(es, 'nsel', [128, nb], BF16)
        ost = [self.sb(es, f'nost{i}', [128, 3, 4, 64], F32) for i in range(2)]
        accA, accB, accS, accW = self.ps[3], self.ps[4], self.ps[5], self.ps[6]
        cmp_eid = {}

        def stage1(g, qt, sl):
            q, cn, fo, nm, o = qt_[sl], cneg[sl], forced[sl], negm[sl], ost[sl]
            tsl = slice(qt * 128, (qt + 1) * 128)
            self.dma('sp', q[:], self.nsa_qT[g, qt].rearrange("d r t -> d (r t)"), [self.nsa_qT.b], [q.b])
            self.dma('sp', cn[:], self.c['cmpneg'][tsl, :], [], [cn.b])
            self.dma('sp', fo[:], self.c['forced'][tsl, :], [], [fo.b])
            views = {j: ((accA if j < 2 else accB)[:, (j % 2) * dvc:(j % 2 + 1) * dvc], j // 2) for j in range(4)}
            tiles = []
            for kt in range(nct):
                if 16 * 128 * kt + 31 > qt * 128 + 127:
                    continue
                tiles.append(dict(kT=kcmpT[:, g, kt * 128:(kt + 1) * 128], V=vcmp[:, g, kt, 0:dvc],
                                  R=[kcmpT.b, vcmp.b], c0=0, subs=[0, 1, 2, 3],
                                  masks=[(cn[:, kt * 128:(kt + 1) * 128], self.i4[:], 0, 512, [cn.b, self.i4.b])]))
            if not tiles:
                tiles.append(dict(kT=kcmpT[:, g, 0:128], V=vcmp[:, g, 0, 0:dvc], R=[kcmpT.b, vcmp.b], c0=0,
                                  subs=[0, 1, 2, 3],
                                  masks=[(cn[:, 0:128], self.i4[:], 0, 512, [cn.b, self.i4.b])]))
            cmp_tiles = tiles
            viewsW = {j: (accW[:, j * 65:(j + 1) * 65], 0) for j in range(4)}
            tiles = []
            for kt in range(max(0, qt - 4), qt + 1):
                tl = dict(kT=kwT[:, kt * 128:(kt + 1) * 128], V=vw[:, kt, :], R=[kwT.b, vw.b], c0=0,
                          subs=[0, 1, 2, 3], masks=[])
                if kt == qt:
                    tl['masks'].append((self.tri_neg[:], self.i4[:], 0, 512, [self.tri_neg.b, self.i4.b]))
                if kt == qt - 4:
                    tl['masks'].append((self.edge_neg[:], self.i4[:], 0, 512, [self.edge_neg.b, self.i4.b]))
                tiles.append(tl)
            win_tiles = tiles

            def epi_cmp():
                for j in range(4):
                    ab = accA if j < 2 else accB
                    v_ = views[j][0]
                    self.ts('dve', rec[:, j:j + 1], v_[:, 64:65], 1e-30, ALU.max, [ab.b], [rec.b])
                self.op_('dve', lambda e: e.reciprocal(out=rec[:], in_=rec[:]), reads=[rec.b], writes=[rec.b])
                for j in range(4):
                    ab = accA if j < 2 else accB
                    v_ = views[j][0]
                    self.ts('dve', o[:, 0, j, :], v_[:, 0:64], rec[:, j:j + 1], ALU.mult, [ab.b, rec.b], [o.b])
                    if j == 0:
                        self.ts('dve', score[:], v_[:, 65:65 + nb], rec[:, 0:1], ALU.mult, [ab.b, rec.b], [score.b])
                    else:
                        self.stt(score[:], v_[:, 65:65 + nb], rec[:, j:j + 1], score[:], ALU.mult, ALU.add,
                                 [ab.b, rec.b, score.b], [score.b])
                self.tt('dve', score[:], score[:], fo[:], ALU.add, [score.b, fo.b], [score.b])
                self.op_('dve', lambda e: e.max(out=m8[:], in_=score[:]), reads=[score.b], writes=[m8.b])
                self.op_('dve', lambda e: e.match_replace(out=work[:], in_to_replace=m8[:], in_values=score[:],
                                                           imm_value=-3e38), reads=[score.b, m8.b], writes=[work.b])
                self.op_('dve', lambda e: e.max(out=m8[:], in_=work[:]), reads=[work.b], writes=[m8.b])
                self.ts('dve', thr[:], m8[:, 7:8], -1e29, ALU.max, [m8.b], [thr.b])
                self.ts('dve', nsel[:], score[:], thr[:, 0:1], ALU.is_lt, [score.b, thr.b], [nsel.b], s2=-BIG, op1=ALU.mult)
                nblk_need = (qt + 1) * 2
                self.cp('pool', nm[:, 0:nblk_need * 64].rearrange("p (j c) -> p j c", c=64),
                        nsel[:, 0:nblk_need].unsqueeze(2).to_broadcast([128, nblk_need, 64]), [nsel.b], [nm.b])
                self.tt('pool', nm[:, tsl], nm[:, tsl], self.tri_neg[:], ALU.add, [nm.b, self.tri_neg.b], [nm.b])

            def epi_win():
                av = accW[:, 0:260].rearrange("p (j c) -> p j c", j=4)
                self.op_('dve', lambda e, av=av: e.reciprocal(out=rec[:], in_=av[:, :, 64]), reads=[accW.b], writes=[rec.b])
                self.tt('dve', o[:, 2], av[:, :, 0:64], rec[:].unsqueeze(2).to_broadcast([128, 4, 64]), ALU.mult,
                        [accW.b, rec.b], [o.b])

            eid = self.attn_block(q[:], [q.b], 512, cmp_tiles, 0.125, views, [accA.b, accB.b], epilogue=epi_cmp)
            self.attn_block(q[:], [q.b], 512, win_tiles, 0.125, viewsW, [accW.b], epilogue=epi_win)
            cmp_eid[(g, qt)] = eid

        def stage2(g, qt, sl):
            q, nm, o = qt_[sl], negm[sl], ost[sl]
            tsl = slice(qt * 128, (qt + 1) * 128)
            self.attn_sync(cmp_eid[(g, qt)])
            views = {j: (accS[:, j * 65:(j + 1) * 65], 0) for j in range(4)}
            tiles = [dict(kT=ksT[:, kt * 128:(kt + 1) * 128], V=vs[:, kt, :], R=[ksT.b, vs.b], c0=0,
                          subs=[0, 1, 2, 3],
                          masks=[(nm[:, kt * 128:(kt + 1) * 128], self.i4[:], 0, 512, [nm.b, self.i4.b])])
                     for kt in range(qt + 1)]
            def epi_sel():
                av = accS[:, 0:260].rearrange("p (j c) -> p j c", j=4)
                self.op_('dve', lambda e, av=av: e.reciprocal(out=rec[:], in_=av[:, :, 64]), reads=[accS.b], writes=[rec.b])
                self.tt('dve', o[:, 1], av[:, :, 0:64], rec[:].unsqueeze(2).to_broadcast([128, 4, 64]), ALU.mult,
                        [accS.b, rec.b], [o.b])
                for bi, base in enumerate((1280, 512, 1792)):
                    self.dma('sp', self.Y[tsl, base + g * 256:base + (g + 1) * 256],
                             o[:, bi].rearrange("p r c -> p (r c)"), [o.b], [self.Y.b])
            self.attn_block(q[:], [q.b], 512, tiles, 0.125, views, [accS.b], epilogue=epi_sel)

        for g in range(2):
            self.dma('sp', ksT[:], self.nsa_kT[0 + g], [self.nsa_kT.b], [ksT.b])
            self.dma('sp', kwT[:], self.nsa_kT[2 + g], [self.nsa_kT.b], [kwT.b])
            self.dma('sp', vs[:], self.nsa_v[0 + g], [self.nsa_v.b], [vs.b])
            self.dma('sp', vw[:], self.nsa_v[2 + g], [self.nsa_v.b], [vw.b])
            stage1(g, 0, 0)
            for qt in range(NT):
                if qt + 1 < NT:
                    stage1(g, qt + 1, (qt + 1) % 2)
                stage2(g, qt, qt % 2)
            self.attn_flush()

    def layer_odd(self, es, L, x_src, x_dst):
        li = L // 2
        w = self.w
        T, NT = self.T, self.NT
        S = self.S
        memkT, memV = self.mem_kv(es, L)
        with ExitStack() as p1:
            win, lng = self.load_win(p1, L, w['odd_w_in'][li], ODD_COLS)
            g_cq = self.bc_load(p1, 'g_cq', w['dsa_q_norm_g'][li:li + 1, :], 64)
            g_ck = self.bc_load(p1, 'g_ck', w['dsa_k_norm_g'][li:li + 1, :], 64)
            g_mq = self.bc_load(p1, 'g_mq', w['mem_q_norm_g'][L:L + 1, :], 64)
            xt = [self.sb(p1, f'xt{i}', [128, D], F32) for i in range(2)]
            ht = [self.sb(p1, f'ht{i}', [128, D], BF16) for i in range(2)]
            hT = [self.sb(p1, f'hT{i}', [128, 8, 128], BF16) for i in range(2)]
            u = [self.sb(p1, f'u{i}', [128, ODD_COLS], F32) for i in range(2)]
            ss = self.sb(p1, 'ss', [128, 1], F32)
            junk = self.sb(p1, 'junk', [128, D], BF16)
            gt = self.sb(p1, 'gt', [128, 1792], BF16)
            cqn = self.sb(p1, 'cqn', [128, 512], F32)
            cqf = self.sb(p1, 'cqf', [128, 512], BF16)
            ckn = self.sb(p1, 'ckn', [128, 64], F32)
            ckf = self.sb(p1, 'ckf', [128, 64], BF16)
            cva = self.sb(p1, 'cva', [128, 65], BF16)
            self.memset('pool', cva[:], 1.0, [cva.b])
            iqf = self.sb(p1, 'iqf', [128, 256], BF16)
            ikf = self.sb(p1, 'ikf', [128, 32], BF16)
            iwt = self.sb(p1, 'iwt', [128, 8], F32)
            mlv = self.sb(p1, 'mlv', [128, 4, 129], BF16)
            self.memset('pool', mlv[:], 1.0, [mlv.b])
            mqf = self.sb(p1, 'mqf', [128, 256], BF16)
            stq = self.sb(p1, 'stq', [64, 8, 128], BF16)
            stk = self.sb(p1, 'stk', [64, 1, 128], BF16)
            sti = self.sb(p1, 'sti', [32, 8, 128], BF16)
            stik = self.sb(p1, 'stik', [32, 1, 128], BF16)
            stmq = self.sb(p1, 'stmq', [64, 4, 128], BF16)
            strw = self.sb(p1, 'strw', [128, 4, 128], F32)
            stif = self.sb(p1, 'stif', [8, 128], F32)
            scrA = self.new_scr(p1, 'A', 512)
            scrB = self.new_scr(p1, 'B', 256)
            ut_next = self.p1_front(0, x_src, xt, ht, hT, u, ss, junk, lng, win, ODD_COLS)
            for n in range(NT):
                ut = ut_next
                U = lambda a, b_, ut=ut: ut[:, a:b_]
                ub = [ut.b]
                tsl = slice(n * 128, (n + 1) * 128)
                self.chains_begin(['A', 'F', 'B', 'L', 'M', 'G'])
                if n + 1 < NT:
                    self.chain('F')
                    ut_next = self.p1_front(n + 1, x_src, xt, ht, hT, u, ss, junk, lng, win, ODD_COLS)
                self.chain('G')
                self.act(gt[:, 0:512], U(936, 1448), AF.Silu, ub, [gt.b])
                self.act(gt[:, 512:1024], U(2992, 3504), AF.Silu, ub, [gt.b])
                self.act(gt[:, 1024:1280], U(3760, 4016), AF.Silu, ub, [gt.b])
                self.act(gt[:, 1280:1792], U(2480, 2992), AF.Sigmoid, ub, [gt.b])
                self.dma('sp', self.G[tsl, :], gt[:], [gt.b], [self.G.b[n]])
                self.chain('A', scrA)
                cq3 = cqn[:].rearrange("p (h c) -> p h c", h=8)
                self.rmsn(U(0, 512).rearrange("p (h c) -> p h c", h=8), 8, 64, g_cq, cq3, ub, [cqn.b])
                self.rope(cq3, 8, 32, n, cqf[:].rearrange("p (h c) -> p h c", h=8), [cqn.b], [cqf.b])
                ck3 = ckn[:].rearrange("p (h c) -> p h c", h=1)
                self.rmsn(U(512, 576).rearrange("p (h c) -> p h c", h=1), 1, 64, g_ck, ck3, ub, [ckn.b])
                self.rope(ck3, 1, 32, n, ckf[:].rearrange("p (h c) -> p h c", h=1), [ckn.b], [ckf.b])
                self.cp('dve', cva[:, 0:64], U(576, 640), ub, [cva.b])
                self.dma('sp', self.dsa_v[:, n, :], cva[:], [cva.b], [self.dsa_v.b])
                pT = self.ps[4]
                pTb = pT[:].bitcast(BF16)
                for h in range(8):
                    self.tr(pTb[0:64, h * 128:(h + 1) * 128], cqf[:, h * 64:(h + 1) * 64], self.ident[:],
                            [cqf.b, self.ident.b], [pT.b])
                self.cp('act', stq[:], pTb[0:64, 0:1024].rearrange("p (k c) -> p k c", k=8), [pT.b], [stq.b])
                self.dma('sp', self.dsa_qT[n], stq[:], [stq.b], [self.dsa_qT.b])
                self.transposes_out([(ckf[:], [ckf.b])], 64, stk,
                                    self.dsa_kT[:, tsl].rearrange("d (k t) -> d k t", k=1), [self.dsa_kT.b], 5)
                self.chain('B', scrB)
                self.rope(U(640, 896).rearrange("p (h c) -> p h c", h=8), 8, 16, n,
                          iqf[:].rearrange("p (h c) -> p h c", h=8), ub, [iqf.b])
                self.rope(U(896, 928).rearrange("p (h c) -> p h c", h=1), 1, 16, n,
                          ikf[:].rearrange("p (h c) -> p h c", h=1), ub, [ikf.b])
                self.ts('dve', iwt[:], U(928, 936), 8 ** -0.5, ALU.mult, ub, [iwt.b])
                self.dma('sp', self.idx_w[tsl, :], iwt[:], [iwt.b], [self.idx_w.b])
                self.transposes_out([(iqf[:, h * 32:(h + 1) * 32], [iqf.b]) for h in range(8)], 32, sti,
                                    self.idx_qT[:, :, tsl].rearrange("h d t -> d h t"), [self.idx_qT.b], 6)
                self.transposes_out([(ikf[:], [ikf.b])], 32, stik,
                                    self.idx_kT[:, tsl].rearrange("d (k t) -> d k t", k=1), [self.idx_kT.b], 7)
                self.chain('L')
                pR = self.ps[3]
                for k in range(4):
                    self.tr(pR[:, k * 128:(k + 1) * 128], U(1448 + k * 128, 1448 + (k + 1) * 128), self.identf[:],
                            ub + [self.identf.b], [pR.b])
                self.cp('act', strw[:].rearrange("p k c -> p (k c)"), pR[:, 0:512], [pR.b], [strw.b])
                self.dma('sp', self.ml_raw[:, tsl].rearrange("(k p) t -> p k t", p=128), strw[:], [strw.b],
                         [self.ml_raw.b])
                pI = self.ps[3]
                self.tr(pI[0:8, 0:128], U(2472, 2480), self.identf[:], ub + [self.identf.b], [pI.b])
                self.cp('act', stif[:], pI[0:8, 0:128], [pI.b], [stif.b])
                self.dma('sp', self.ml_if[:, tsl], stif[:], [stif.b], [self.ml_if.b])
                self.cp('dve', mlv[:, :, 0:128], U(1960, 2472).rearrange("p (h c) -> p h c", h=4), ub, [mlv.b])
                self.dma('sp', self.ml_v[:, :, n, :].rearrange("h p c -> p h c"), mlv[:], [mlv.b], [self.ml_v.b])
                self.chain('B', scrB)
                self.rmsn(U(3504, 3760).rearrange("p (h c) -> p h c", h=4), 4, 64, g_mq,
                          mqf[:].rearrange("p (h c) -> p h c", h=4), ub, [mqf.b])
                self.transposes_out([(mqf[:, h * 64:(h + 1) * 64], [mqf.b]) for h in range(4)], 64, stmq,
                                    self.mem_qT[:, :, tsl].rearrange("h d t -> d h t"), [self.mem_qT.b], 6)
                self.chains_emit()
            S.barrier()
        if 'stop_p1' in self.dbg:
            return
        self.mlstm_pre(li)
        S.barrier()
        if 'stop_pre' in self.dbg:
            return
        with ExitStack() as pa:
            self.attn_setup(pa)
            self.mlstm_attn(pa)
            S.barrier()
        if 'stop_ml' in self.dbg:
            return
        with ExitStack() as pa:
            self.attn_setup(pa)
            self.mem_attn(pa, memkT, memV)
            S.barrier()
        if 'stop_mem' in self.dbg:
            return
        with ExitStack() as pa:
            self.attn_setup(pa)
            self.dsa_attn(pa)
            S.barrier()
        if 'stop_dsa' in self.dbg:
            return
        with ExitStack() as p3:
            wout = self.load_wout(p3, L)
            g_h = self.bc_load(p3, 'g_h', w['mlstm_h_norm_g'][li:li + 1, :], 128)
            yt = [self.sb(p3, f'yt{i}', [128, 1280], F32) for i in range(2)]
            gtt = [self.sb(p3, f'gtt{i}', [128, 1792], BF16) for i in range(2)]
            mix = [self.sb(p3, f'mix{i}', [128, 1280], BF16) for i in range(2)]
            hns = [self.sb(p3, f'hn{i}', [128, 512], F32) for i in range(2)]
            scrP = [self.new_scr(p3, f'P{i}', 512) for i in range(2)]

            def mixfn(n):
                sl = n % 2
                y, g, m = yt[sl], gtt[sl], mix[sl]
                hn = hns[sl]
                self.scr = scrP[sl]
                tsl = slice(n * 128, (n + 1) * 128)
                self.dma('sp', y[:], self.Y[tsl, 0:1280], [self.Y.b], [y.b])
                self.dma('sp', g[:], self.G[tsl, :], [self.G.b[n]], [g.b])
                self.tt('dve', m[:, 0:512], y[:, 0:512], g[:, 0:512], ALU.mult, [y.b, g.b], [m.b])
                self.tt('pool', m[:, 1024:1280], y[:, 1024:1280], g[:, 1024:1280], ALU.mult, [y.b, g.b], [m.b])
                self.rmsn(y[:, 512:1024].rearrange("p (h c) -> p h c", h=4), 4, 128, g_h,
                          hn[:].rearrange("p (h c) -> p h c", h=4), [y.b], [hn.b])
                self.tt('dve', hn[:], hn[:], g[:, 1280:1792], ALU.mult, [hn.b, g.b], [hn.b])
                self.tt('dve', m[:, 512:1024], hn[:], g[:, 512:1024], ALU.mult, [hn.b, g.b], [m.b])
                return m
            self.p3(p3, L, x_src, x_dst, wout, mixfn)
            S.barrier()

    def mlstm_pre(self, li):
        w = self.w
        T = self.T
        with ExitStack() as es:
            xp = [self.sb(es, f'xp{i}', [128, T + 3], F32) for i in range(2)]
            y = self.sb(es, 'cy', [128, T], F32)
            yo = [self.sb(es, f'cyo{i}', [128, T], BF16) for i in range(2)]
            wc = self.sb(es, 'cwc', [128, 4, 4], F32)
            bc = self.sb(es, 'cbc', [128, 4], F32)
            for ck in range(4):
                self.dma('sp', wc[:, ck, :], w['mlstm_conv_wT'][li, ck * 128:(ck + 1) * 128, :], [], [wc.b])
                self.dma('sp', bc[:, ck:ck + 1], w['mlstm_conv_b'][li, ck * 128:(ck + 1) * 128].unsqueeze(1), [], [bc.b])
            for ck in range(4):
                x = xp[ck % 2]
                o = yo[ck % 2]
                self.memset('pool', x[:, 0:3], 0.0, [x.b])
                self.dma('sp', x[:, 3:T + 3], self.ml_raw[ck * 128:(ck + 1) * 128, :], [self.ml_raw.b], [x.b])
                self.ts('dve', y[:], x[:, 0:T], wc[:, ck, 0:1], ALU.mult, [x.b, wc.b, bc.b], [y.b],
                        s2=bc[:, ck:ck + 1], op1=ALU.add)
                for j in range(1, 4):
                    self.stt(y[:], x[:, j:j + T], wc[:, ck, j:j + 1], y[:], ALU.mult, ALU.add, [x.b, wc.b, y.b], [y.b])
                self.act(o[:], y[:], AF.Silu, [y.b], [o.b])
                self.dma('sp', self.ml_qkT[ck * 128:(ck + 1) * 128, :], o[:], [o.b], [self.ml_qkT.b])
            self.S.barrier()
        with ExitStack() as es:
            ig = self.sb(es, 'ig', [4, T], F32)
            fg = self.sb(es, 'fg', [4, T], F32)
            cs = self.sb(es, 'cs', [4, T], F32)
            a = self.sb(es, 'ga', [4, T], F32)
            Mt = self.sb(es, 'gM', [4, T], F32)
            ones = self.sb(es, 'gones', [4, T], F32)
            ib = self.sb(es, 'gib', [4, 1], F32)
            fb = self.sb(es, 'gfb', [4, 1], F32)
            self.memset('pool', ones[:], 1.0, [ones.b])
            self.dma('sp', ig[:], self.ml_if[0:4, :], [self.ml_if.b], [ig.b])
            self.dma('sp', fg[:], self.ml_if[4:8, :], [self.ml_if.b], [fg.b])
            self.dma('sp', ib[:], w['mlstm_i_bias'][li].unsqueeze(1), [], [ib.b])
            self.dma('sp', fb[:], w['mlstm_f_bias'][li].unsqueeze(1), [], [fb.b])
            self.ts('dve', fb[:], fb[:], -1.0, ALU.mult, [fb.b], [fb.b])
            self.act(fg[:], fg[:], AF.Exp, [fg.b, fb.b], [fg.b], bias=fb[:, 0:1], scale=-1.0)
            self.act(fg[:], fg[:], AF.Ln, [fg.b], [fg.b], bias=self.one_c[0:4, 0:1], scale=1.0)
            self.op_('dve', lambda e: e.tensor_tensor_scan(out=cs[:], data0=ones[:], data1=fg[:], initial=0.0,
                                                             op0=ALU.mult, op1=ALU.add),
                      reads=[ones.b, fg.b], writes=[cs.b])
            self.stt(a[:], ig[:], ib[:, 0:1], cs[:], ALU.add, ALU.add, [ig.b, ib.b, cs.b], [a.b])
            self.op_('dve', lambda e: e.tensor_tensor_scan(out=Mt[:], data0=a[:], data1=a[:], initial=0.0,
                                                             op0=ALU.max, op1=ALU.max),
                      reads=[a.b], writes=[Mt.b])
            self.tt('dve', cs[:], cs[:], Mt[:], ALU.subtract, [cs.b, Mt.b], [cs.b])
            self.act(cs[:], cs[:], AF.Exp, [cs.b], [cs.b])
            self.ts('dve', Mt[:], Mt[:], -1.0, ALU.mult, [Mt.b], [Mt.b])
            self.dma('sp', self.ml_g[0:4, :], a[:], [a.b], [self.ml_g.b])
            self.dma('sp', self.ml_g[4:8, :], Mt[:], [Mt.b], [self.ml_g.b])
            self.dma('sp', self.ml_g[8:12, :], cs[:], [cs.b], [self.ml_g.b])
            self.S.barrier()

    def mlstm_attn(self, es):
        T, NT = self.T, self.NT
        qT = [self.sb(es, f'lqT{i}', [64, T], BF16) for i in range(2)]
        kT = [self.sb(es, f'lkT{i}', [64, T], BF16) for i in range(2)]
        V = [self.sb(es, f'lV{i}', [128, NT, 129], BF16) for i in range(2)]
        nM = [self.sb(es, f'lnM{i}', [128, T], F32) for i in range(2)]
        ant = self.sb(es, 'lant', [NT, 2, 128], F32)
        atm = [self.sb(es, f'latm{i}', [128, 2, NT], F32) for i in range(2)]
        Et = [self.sb(es, f'lEt{i}', [128, 512], F32) for i in range(3)]
        ost = [self.sb(es, f'lost{i}', [128, 4, 128], F32) for i in range(2)]
        d2 = self.sb(es, 'ld2', [128, 4], F32)
        ecnt = [0]
        blk = 0
        LN8 = math.log(0.125)
        for h in range(4):
            q, k, v, nm, at = qT[h % 2], kT[h % 2], V[h % 2], nM[h % 2], atm[h % 2]
            self.dma('sp', q[:], self.ml_qkT[h * 64:(h + 1) * 64, :], [self.ml_qkT.b], [q.b])
            self.dma('sp', k[:], self.ml_qkT[256 + h * 64:256 + (h + 1) * 64, :], [self.ml_qkT.b], [k.b])
            self.dma('sp', v[:], self.ml_v[h], [self.ml_v.b], [v.b])
            self.dma('sp', nm[:], self.ml_g[4 + h:5 + h, :].to_broadcast([128, T]), [self.ml_g.b], [nm.b])
            self.dma('sp', ant[:, 0, :], self.ml_g[h, :].rearrange("(n p) -> n p", p=128), [self.ml_g.b], [ant.b])
            self.dma('sp', ant[:, 1, :], self.ml_g[8 + h, :].rearrange("(n p) -> n p", p=128), [self.ml_g.b], [ant.b])
            pA = self.ps[7]
            for i in range(2):
                self.tr(pA[:, i * NT:(i + 1) * NT], ant[:, i, :], self.identf[0:NT, 0:NT], [ant.b, self.identf.b], [pA.b])
            self.cp('act', at[:].rearrange("p a n -> p (a n)"), pA[:, 0:2 * NT], [pA.b], [at.b])
            self.ts('dve', at[:, 0, :], at[:, 0, :], LN8, ALU.add, [at.b], [at.b])
            for cq in range(T // 512):
                set_i = blk % 2
                blk += 1
                accA, accB = self.ps[3 + 2 * set_i], self.ps[4 + 2 * set_i]
                views = {j: ((accA if j < 2 else accB)[:, (j % 2) * 129:(j % 2 + 1) * 129], j // 2) for j in range(4)}
                tiles = []
                for kt in range(4 * cq + 4):
                    vv = kt - 4 * cq
                    c0 = max(0, vv) * 128

                    def efn(kt=kt, c0=c0, cq=cq, nm=nm, at=at):
                        E = Et[ecnt[0] % 3]
                        ecnt[0] += 1
                        self.act(E[:, c0:512], nm[:, cq * 512 + c0:(cq + 1) * 512], AF.Exp, [nm.b, at.b], [E.b],
                                 bias=at[:, 0, kt:kt + 1], scale=1.0)
                        return E, [E.b]
                    tl = dict(kT=k[:, kt * 128:(kt + 1) * 128], V=v[:, kt, :], R=[k.b, v.b], c0=c0,
                              subs=list(range(max(0, vv), 4)), Efn=efn)
                    if vv >= 0:
                        tl['diag'] = vv
                    tiles.append(tl)
                def epi(accA=accA, accB=accB, views=views, o=ost[set_i], cq=cq, h=h, at=at):
                    for j in range(4):
                        ab = accA if j < 2 else accB
                        v_ = views[j][0]
                        self.act(d2[:, j:j + 1], v_[:, 128:129], AF.Abs, [ab.b], [d2.b])
                    self.tt('dve', d2[:], d2[:], at[:, 1, 4 * cq:4 * cq + 4], ALU.max, [d2.b, at.b], [d2.b])
                    self.op_('dve', lambda e: e.reciprocal(out=d2[:], in_=d2[:]), reads=[d2.b], writes=[d2.b])
                    for j in range(4):
                        ab = accA if j < 2 else accB
                        v_ = views[j][0]
                        self.ts('dve', o[:, j, :], v_[:, 0:128], d2[:, j:j + 1], ALU.mult, [ab.b, d2.b], [o.b])
                    self.dma('sp', self.Y[cq * 512:(cq + 1) * 512, 512 + h * 128:512 + (h + 1) * 128]
                             .rearrange("(j p) c -> p j c", p=128), o[:], [o.b], [self.Y.b])
                self.attn_block(q[:, cq * 512:(cq + 1) * 512], [q.b], 512, tiles, 1.0, views, [accA.b, accB.b],
                                mode='mul', epilogue=epi)
        self.attn_flush()

    def dsa_attn(self, es):
        T, NT = self.T, self.NT
        KSEL = min(256, T // 4)
        ikT = self.sb(es, 'ikT', [32, T], BF16)
        ckT = self.sb(es, 'ckT', [64, T], BF16)
        cv = self.sb(es, 'cv', [128, NT, 65], BF16)
        self.dma('sp', ikT[:], self.idx_kT[:, :], [self.idx_kT.b], [ikT.b])
        self.dma('sp', ckT[:], self.dsa_kT[:, :], [self.dsa_kT.b], [ckT.b])
        self.dma('sp', cv[:], self.dsa_v[:, :, :], [self.dsa_v.b], [cv.b])
        iq = [self.sb(es, f'iq{i}', [32, 8, 128], BF16) for i in range(2)]
        iw = [self.sb(es, f'iw{i}', [128, 8], F32) for i in range(2)]
        cq_ = [self.sb(es, f'cq{i}', [64, 2, 512], BF16) for i in range(2)]
        score2 = [self.sb(es, f'dscore{i}', [128, T], F32) for i in range(3)]
        thrA2 = [self.sb(es, f'dthrA{i}', [128, 1], F32) for i in range(2)]
        work = self.sb(es, 'dwork', [128, T], F32)
        negm = [self.sb(es, f'dnegm{i}', [128, T], BF16) for i in range(2)]
        rl = [self.sb(es, f'drl{i}', [128, 512], F32) for i in range(3)]
        m8 = self.sb(es, 'dm8', [128, 8], F32)
        thr = self.sb(es, 'dthr', [128, 1], F32)
        rec = self.sb(es, 'drec', [128, 4], F32)
        ost = [self.sb(es, f'dost{i}', [128, 4, 64], F32) for i in range(2)]
        rlc_ = [0]
        blk_ = [0]
        NBIS = 20
        junk = self.sb(es, 'djunk', [128, T], BF16)
        amax = self.sb(es, 'damax', [128, 1], F32)
        w0 = self.sb(es, 'dw0', [128, 1], F32)
        nHh = self.sb(es, 'dnHh', [128, 40], F32)
        nmid = [self.sb(es, f'dnmid{i}', [128, 1], F32) for i in range(2)]
        Ssum = self.sb(es, 'dS', [128, 1], F32)
        tsg = self.sb(es, 'dtsg', [128, 1], F32)

        def stage_a1(qt):
            rlc = rlc_[0]
            sl = qt % 2
            tsl = slice(qt * 128, (qt + 1) * 128)
            q_i, w_i = iq[sl], iw[sl]
            score = score2[qt % 3]
            self.dma('sp', q_i[:], self.idx_qT[:, :, tsl].rearrange("h d t -> d h t"), [self.idx_qT.b], [q_i.b])
            self.dma('sp', w_i[:], self.idx_w[tsl, :], [self.idx_w.b], [w_i.b])
            ncols = (qt + 1) * 128
            for c in range((ncols + 511) // 512):
                c0 = c * 512
                wd = min(512, ncols - c0)
                for h in range(8):
                    bi = self.st_rr % len(self.st_banks)
                    self.st_rr += 1
                    bank, bb = self.st_banks[bi]
                    self.mm(bank[:, 0:wd], q_i[:, h, :], ikT[:, c0:c0 + wd], True, True, [q_i.b, ikT.b], [bb])
                    if h == 0:
                        self.ts('dve', score[:, c0:c0 + wd], bank[:, 0:wd], 0.0, ALU.max, [bb, w_i.b], [score.b],
                                s2=w_i[:, 0:1], op1=ALU.mult)
                    else:
                        r = rl[rlc % 3]
                        rlc += 1
                        self.ts('dve', r[:, 0:wd], bank[:, 0:wd], 0.0, ALU.max, [bb, w_i.b], [r.b],
                                s2=w_i[:, h:h + 1], op1=ALU.mult)
                        self.tt('dve', score[:, c0:c0 + wd], score[:, c0:c0 + wd], r[:, 0:wd], ALU.add,
                                [score.b, r.b], [score.b])
            rlc_[0] = rlc

        def is_act_tile(qt):
            return ((qt + 1) * 128 > KSEL) and ('dsa_nobis' not in self.dbg)

        def stage_a2_finish(qt):
            sl = qt % 2
            nm = negm[sl]
            score = score2[qt % 3]
            ncols = (qt + 1) * 128
            th = thrA2[qt % 2] if is_act_tile(qt) else thr
            self.ts('dve', nm[:, 0:ncols], score[:, 0:ncols], th[:, 0:1], ALU.is_lt, [score.b, th.b], [nm.b],
                    s2=-BIG, op1=ALU.mult)

        def stage_a2(qt):
            sl = qt % 2
            tsl = slice(qt * 128, (qt + 1) * 128)
            q_c, nm = cq_[sl], negm[sl]
            score = score2[qt % 3]
            ncols = (qt + 1) * 128
            for half in range(2):
                self.dma('sp', q_c[:, half, :], self.dsa_qT[qt, :, half * 4:(half + 1) * 4, :].rearrange("d h t -> d (h t)"),
                         [self.dsa_qT.b], [q_c.b])
            use_act = is_act_tile(qt)
            if use_act:
                self.op_('dve', lambda e, ncols=ncols: e.tensor_reduce(out=amax[:], in_=score[:, 0:ncols], axis=AX.X,
                                                                      op=ALU.max, apply_absolute_value=True),
                         reads=[score.b], writes=[amax.b])
                self.ts('dve', w0[:], amax[:], 2.0, ALU.mult, [amax.b], [w0.b], s2=2.0, op1=ALU.add)
                self.ts('dve', nHh[:], self.pw[:], w0[:, 0:1], ALU.mult, [self.pw.b, w0.b], [nHh.b])
            self.tt('dve', score[:, tsl], score[:, tsl], self.trinegf[:], ALU.add, [score.b, self.trinegf.b], [score.b])
            if use_act:
                cconst = float(0.5 - (2 * KSEL - ncols - 1))
                self.memset('pool', nmid[0][:], 0.0, [nmid[0].b])
                for j in range(NBIS):
                    cur, nxt = nmid[j % 2], nmid[(j + 1) % 2]
                    self.act(junk[:, 0:ncols], score[:, 0:ncols], AF.Sign, [score.b, cur.b], [junk.b, Ssum.b],
                             bias=cur[:, 0:1], scale=1.0, accum=Ssum[:, 0:1])
                    self.act(tsg[:], Ssum[:], AF.Sign, [Ssum.b], [tsg.b], bias=cconst, scale=1.0)
                    self.act(nxt[:], tsg[:], AF.Identity, [tsg.b, cur.b, nHh.b], [nxt.b],
                             bias=cur[:, 0:1], scale=nHh[:, j:j + 1])
                fin = nmid[NBIS % 2]
                thrA = thrA2[qt % 2]
                self.act(thrA[:], fin[:], AF.Identity, [fin.b, nHh.b], [thrA.b], bias=nHh[:, NBIS - 1:NBIS], scale=-1.0)
                return
            elif ncols > KSEL and 'dsa_notopk' not in self.dbg:
                self.cp('pool', work[:, 0:ncols], score[:, 0:ncols], [score.b], [work.b])
                nr = KSEL // 8
                for r_ in range(nr):
                    self.op_('dve', lambda e, ncols=ncols: e.max(out=m8[:], in_=work[:, 0:ncols]),
                             reads=[work.b], writes=[m8.b])
                    if r_ < nr - 1:
                        self.op_('dve', lambda e, ncols=ncols: e.match_replace(
                            out=work[:, 0:ncols], in_to_replace=m8[:], in_values=work[:, 0:ncols], imm_value=-3e38),
                            reads=[work.b, m8.b], writes=[work.b])
                self.ts('dve', thr[:], m8[:, 7:8], -1e29, ALU.max, [m8.b], [thr.b])
            else:
                self.memset('dve', thr[:], -1e29, [thr.b])
            stage_a2_finish(qt)

        def stage_b(qt):
            blk = blk_[0]
            sl = qt % 2
            tsl = slice(qt * 128, (qt + 1) * 128)
            q_c, nm = cq_[sl], negm[sl]
            for half in range(2):
                set_i = blk % 2
                blk += 1
                accb = self.ps[3 + set_i]
                views = {j: (accb[:, j * 65:(j + 1) * 65], 0) for j in range(4)}
                tiles = [dict(kT=ckT[:, kt * 128:(kt + 1) * 128], V=cv[:, kt, :], R=[ckT.b, cv.b], c0=0,
                              subs=[0, 1, 2, 3],
                              masks=[(nm[:, kt * 128:(kt + 1) * 128], self.i4[:], 0, 512, [nm.b, self.i4.b])])
                         for kt in range(qt + 1)]
                def epi(accb=accb, o=ost[set_i], tsl=tsl, half=half):
                    av = accb[:, 0:260].rearrange("p (j c) -> p j c", j=4)
                    self.op_('dve', lambda e, av=av: e.reciprocal(out=rec[:], in_=av[:, :, 64]), reads=[accb.b], writes=[rec.b])
                    self.tt('dve', o[:], av[:, :, 0:64], rec[:].unsqueeze(2).to_broadcast([128, 4, 64]), ALU.mult,
                            [accb.b, rec.b], [o.b])
                    self.dma('sp', self.Y[tsl, half * 256:(half + 1) * 256], o[:].rearrange("p j c -> p (j c)"),
                             [o.b], [self.Y.b])
                self.attn_block(q_c[:, half, :], [q_c.b], 512, tiles, 0.125, views, [accb.b], epilogue=epi)
            blk_[0] = blk

        stage_a1(0)
        if NT > 1:
            stage_a1(1)
        stage_a2(0)
        for qt in range(NT):
            if qt + 2 < NT:
                stage_a1(qt + 2)
            if qt + 1 < NT:
                stage_a2(qt + 1)
            if is_act_tile(qt):
                stage_a2_finish(qt)
            stage_b(qt)
        self.attn_flush()


def host_consts(T):
    bf = ml_dtypes.bfloat16
    nb = T // 64
    ncp = max(1, T // 2048) * 128
    n_cmp = (T - 32) // 16 + 1
    c = {}
    c['c_ident'] = np.eye(128, dtype=np.float32).astype(bf)
    c['c_identf'] = np.eye(128, dtype=np.float32)
    c['c_i4'] = np.tile(np.eye(128, dtype=np.float32), (1, 4)).astype(bf)
    i8 = np.zeros((128, 512), np.float32)
    for p in range(128):
        for h in range(8):
            i8[p, h * 64 + p % 64] = 1.0
    c['c_i8x2'] = i8.astype(bf)
    t = np.arange(128)[:, None]
    s = np.arange(128)[None, :]
    c['c_tri_neg'] = np.where(s > t, -BIG, 0.0).astype(np.float32).astype(bf)
    c['c_edge_neg'] = np.where(s <= t, -BIG, 0.0).astype(np.float32).astype(bf)
    c['c_tri01T'] = np.where(t <= s, 1.0, 0.0).astype(np.float32).astype(bf)
    c['c_trinegf'] = np.where(s > t, -1e30, 0.0).astype(np.float32)
    inv = (10000.0 ** (-np.arange(32, dtype=np.float32) / 32)).astype(np.float32)
    c['c_invf'] = np.tile(inv[None, :], (128, 1)).astype(np.float32)
    c['c_pw'] = np.tile((-(2.0 ** -(np.arange(40, dtype=np.float64) + 2)))[None, :], (128, 1)).astype(np.float32)
    tt = np.arange(T)[:, None]
    n = np.arange(ncp)[None, :]
    cm = np.where((16 * n + 31 <= tt) & (n < n_cmp), 0.0, -BIG)
    c['c_cmpneg'] = cm.astype(np.float32).astype(bf)
    j = np.arange(nb)[None, :]
    cur = tt // 64
    forced = np.where(j == cur, 3e4, np.where(j == cur - 1, 2e4, np.where(j == 0, 1e4, 0.0)))
    forced = np.where(j * 64 <= tt, forced, -1e30)
    c['c_forced'] = forced.astype(np.float32)
    ni = np.arange(ncp)[:, None]
    ov = ((16 * ni <= j * 64 + 63) & (16 * ni + 31 >= j * 64) & (ni < n_cmp))
    c['c_overlap'] = ov.astype(np.float32).astype(bf)
    return c


_CACHE = {}


def make_in_maps(inputs, T, ncores):
    NT = T // 128
    consts = host_consts(T)
    wnames = ["ln_g", "mem_norm_g", "mem_w_kv", "mem_q_norm_g", "mem_k_norm_g", "w_out", "even_w_in",
              "mla_q_lat_g", "mla_kv_lat_g", "mla_w_uq", "mla_w_ukv", "mla_q_norm_g", "mla_k_norm_g",
              "nsa_q_norm_g", "nsa_k_norm_g", "nsa_cmp_w1", "nsa_cmp_w2", "odd_w_in", "dsa_q_norm_g",
              "dsa_k_norm_g", "mlstm_conv_b", "mlstm_i_bias", "mlstm_f_bias", "mlstm_h_norm_g"]
    shared = {k: np.ascontiguousarray(np.asarray(inputs[k], dtype=np.float32)) for k in wnames}
    shared["nsa_cmp_posT"] = np.ascontiguousarray(np.transpose(np.asarray(inputs["nsa_cmp_pos"], np.float32), (0, 1, 3, 2)))
    shared["mlstm_conv_wT"] = np.ascontiguousarray(np.transpose(np.asarray(inputs["mlstm_conv_w"], np.float32), (0, 2, 1)))
    shared.update(consts)
    maps = []
    for c in range(ncores):
        m = dict(shared)
        m["x"] = np.ascontiguousarray(np.asarray(inputs["x"][c, :T], np.float32))
        m["mem"] = np.ascontiguousarray(np.asarray(inputs["mem"][c], np.float32))
        pos = np.asarray(inputs["positions"][c, :T]).astype(np.int32)
        m["pos_t"] = np.ascontiguousarray(pos.reshape(NT, 128).T)
        maps.append(m)
    return maps


def kernel(**inputs):
    T = 4096
    key = ('full', T)
    if key not in _CACHE:
        _CACHE[key] = Builder(T, [0, 1, 2, 3]).build()
    nc = _CACHE[key]
    maps = make_in_maps(inputs, T, 8)
    res = run_bass_kernel_spmd(nc, maps, core_ids=list(range(8)))
    out = np.stack([np.asarray(r["out"], dtype=np.float32) for r in res.results], axis=0)
    return out
```

```python
import math
import numpy as np
import ml_dtypes
from contextlib import ExitStack
import concourse.bass as bass
import concourse.mybir as mybir
from concourse.bass_utils import run_bass_kernel_spmd

F32 = mybir.dt.float32
BF16 = mybir.dt.bfloat16
I32 = mybir.dt.int32
AF = mybir.ActivationFunctionType
ALU = mybir.AluOpType
AX = mybir.AxisListType

D = 1024
BIG = 30000.0
EPS = 1e-6
EVEN_COLS = 3256
ODD_COLS = 4016
ENGS = ('pe', 'act', 'dve', 'pool', 'sp')
EPOCH = 16000
NDQ = 8


class Buf:
    __slots__ = ('name', 'w', 'r')

    def __init__(self, name=''):
        self.name = name
        self.w = None
        self.r = {}


class Sched:
    def __init__(self, nc, es):
        self.nc = nc
        self.es = es
        self.prog = {e: [] for e in ENGS}
        self.esem = {e: [] for e in ENGS}
        self.cnt = {e: 0 for e in ENGS}
        self.seen = {e: {} for e in ENGS}
        self.dq = ('sp', 'pool', 'act')
        self.dsem = {q: [es.enter_context(nc.semaphore(f'D{q}{i}')) for i in range(NDQ)] for q in self.dq}
        self.dcnt = {q: 0 for q in self.dq}
        self.ninst = 0

    def _semobj(self, key):
        if key[0] == 'E':
            return self.esem[key[1]][key[2]]
        return self.dsem[key[1]][key[2]]

    def op(self, e, fn, reads=(), writes=(), dma=False):
        deps = {}
        for b in reads:
            if b.w is not None:
                k, v = b.w
                if deps.get(k, 0) < v:
                    deps[k] = v
        for b in writes:
            if b.w is not None:
                k, v = b.w
                if deps.get(k, 0) < v:
                    deps[k] = v
            for k, v in b.r.items():
                if deps.get(k, 0) < v:
                    deps[k] = v
        waits = []
        seen = self.seen[e]
        for k, v in deps.items():
            if e == 'pe' and k[0] == 'E' and k[1] == 'pe':
                continue
            if seen.get(k, 0) >= v:
                continue
            seen[k] = v
            waits.append((self._semobj(k), v))
        if dma:
            j = self.dcnt[e]
            self.dcnt[e] += 1
            slot = j % NDQ
            val = 16 * (j // NDQ + 1)
            key = ('D', e, slot)
            if val > 16 and seen.get(key, 0) < val - 16:
                seen[key] = val - 16
                waits.append((self.dsem[e][slot], val - 16))
            sem = self.dsem[e][slot]
            inc = 16
        else:
            c = self.cnt[e]
            ep = c // EPOCH
            if ep >= len(self.esem[e]):
                self.esem[e].append(self.es.enter_context(self.nc.semaphore(f'S{e}{ep}')))
            self.cnt[e] += 1
            key = ('E', e, ep)
            val = c % EPOCH + 1
            sem = self.esem[e][ep]
            inc = 1
        ev = (key, val)
        self.ninst += 1

        def thunk(eng, waits=waits, fn=fn, sem=sem, inc=inc):
            for s, v in waits:
                eng.wait_ge(s, v)
            fn(eng).then_inc(sem, inc)
        self.prog[e].append(thunk)
        for b in reads:
            if b.r.get(key, 0) < val:
                b.r[key] = val
        for b in writes:
            b.w = ev
            b.r = {}
        return ev

    def barrier(self):
        evs = []
        for e in ENGS:
            c = self.cnt[e]
            if c > 0:
                ep = (c - 1) // EPOCH
                evs.append((('E', e, ep), (c - 1) % EPOCH + 1))
        for q in self.dq:
            n = self.dcnt[q]
            for slot in range(min(n, NDQ)):
                cntslot = (n - 1 - slot) // NDQ + 1
                evs.append((('D', q, slot), 16 * cntslot))
        for e in ENGS:
            waits = []
            for k, v in evs:
                if k[0] == 'E' and k[1] == e:
                    continue
                if self.seen[e].get(k, 0) >= v:
                    continue
                self.seen[e][k] = v
                waits.append((self._semobj(k), v))

            def thunk(eng, waits=waits):
                for s, v in waits:
                    eng.wait_ge(s, v)
            self.prog[e].append(thunk)

    def emit(self):
        nc = self.nc
        with nc.Block() as block:
            @block.tensor
            def _(eng):
                for t in self.prog['pe']:
                    t(eng)

            @block.scalar
            def _(eng):
                for t in self.prog['act']:
                    t(eng)

            @block.vector
            def _(eng):
                for t in self.prog['dve']:
                    t(eng)

            @block.gpsimd
            def _(eng):
                for t in self.prog['pool']:
                    t(eng)

            @block.sync
            def _(eng):
                for t in self.prog['sp']:
                    t(eng)


class Tl:
    __slots__ = ('t', 'b')

    def __init__(self, t, name):
        self.t = t
        self.b = Buf(name)

    def __getitem__(self, k):
        return self.t[k]


class Builder:
    def __init__(self, T, layers, dbg=()):
        self.T = T
        self.NT = T // 128
        self.layers = list(layers)
        self.dbg = set(dbg)
        self.nc = bass.Bass("TRN2", target_bir_lowering=False)
        self.uid = 0
        self.rec = None
        self.dbg_out = {}

    def din(self, name, shape, dt=F32):
        return self.nc.dram_tensor(name, list(shape), dt, kind="ExternalInput").ap()

    def dscr(self, name, shape, dt, nbuf=1):
        kind = "ExternalOutput" if name in self.dbg else "Internal"
        t = self.nc.dram_tensor(name, list(shape), dt, kind=kind).ap()
        tl = Tl(t, name)
        if nbuf > 1:
            tl.b = [Buf(f'{name}{i}') for i in range(nbuf)]
        return tl

    def sb(self, es, name, shape, dt):
        self.uid += 1
        t = es.enter_context(self.nc.sbuf_tensor(f'{name}_{self.uid}', list(shape), dt))
        return Tl(t, name)

    def op_(self, e, fn, reads=(), writes=(), dma=False):
        if self.rec is not None:
            self.rec.append((e, fn, tuple(reads), tuple(writes), dma))
        else:
            self.S.op(e, fn, reads=reads, writes=writes, dma=dma)

    def chains_begin(self, names):
        self._chains = {k: [] for k in names}

    def chain(self, name, scr=None):
        self.rec = self._chains[name]
        self.scr = scr if scr is not None else self.scr0

    def chains_emit(self):
        self.rec = None
        self.scr = self.scr0
        lists = [l for l in self._chains.values() if l]
        idx = [0] * len(lists)
        left = sum(len(l) for l in lists)
        while left:
            for i, l in enumerate(lists):
                if idx[i] < len(l):
                    e, fn, R, W, dma = l[idx[i]]
                    idx[i] += 1
                    left -= 1
                    self.S.op(e, fn, reads=R, writes=W, dma=dma)

    def new_scr(self, es, tag, w):
        sc = {}
        for k in ('sq', 'tmp', 'ra', 'rb'):
            sc[k] = self.sb(es, f'sc_{k}_{tag}', [128, w], F32)
        for k in ('ssq', 'ln', 'rs'):
            sc[k] = self.sb(es, f'sc_{k}_{tag}', [128, 16], F32)
        return sc

    def mm(self, out, lhsT, rhs, start, stop, R, W):
        self.op_('pe', lambda e: e.matmul(out, lhsT=lhsT, rhs=rhs, start=start, stop=stop,
                                           skip_group_check=True), reads=R, writes=W)

    def tr(self, out, in_, ident, R, W):
        self.op_('pe', lambda e: e.transpose(out=out, in_=in_, identity=ident), reads=R, writes=W)

    def act(self, out, in_, func, R, W, bias=None, scale=None, accum=None):
        kw = {}
        if bias is not None:
            kw['bias'] = bias
        if scale is not None:
            kw['scale'] = scale
        if accum is not None:
            kw['accum_out'] = accum
        self.op_('act', lambda e: e.activation(out=out, in_=in_, func=func, **kw), reads=R, writes=W)

    def tt(self, eng, out, in0, in1, op, R, W):
        self.op_(eng, lambda e: e.tensor_tensor(out=out, in0=in0, in1=in1, op=op), reads=R, writes=W)

    def ts(self, eng, out, in0, s1, op0, R, W, s2=None, op1=None, accum=None):
        kw = {}
        if op1 is not None:
            kw['op1'] = op1
        if accum is not None:
            kw['accum_out'] = accum
        self.op_(eng, lambda e: e.tensor_scalar(out=out, in0=in0, scalar1=s1, scalar2=s2, op0=op0, **kw),
                  reads=R, writes=W)

    def stt(self, out, in0, scalar, in1, op0, op1, R, W):
        self.op_('dve', lambda e: e.scalar_tensor_tensor(out=out, in0=in0, scalar=scalar, in1=in1,
                                                         op0=op0, op1=op1), reads=R, writes=W)

    def cp(self, eng, out, in_, R, W):
        if eng == 'act':
            self.op_('act', lambda e: e.copy(out=out, in_=in_), reads=R, writes=W)
        else:
            self.op_(eng, lambda e: e.tensor_copy(out=out, in_=in_), reads=R, writes=W)

    def red(self, out, in_, op, R, W):
        self.op_('dve', lambda e: e.tensor_reduce(out=out, in_=in_, axis=AX.X, op=op), reads=R, writes=W)

    def memset(self, eng, ap, val, W):
        self.op_(eng, lambda e: e.memset(ap, val), writes=W)

    def dma(self, q, out, in_, R, W, **kw):
        self.op_(q, lambda e: e.dma_start(out=out, in_=in_, **kw), reads=R, writes=W, dma=True)

    def bc_load(self, es, name, row_ap, d):
        t = self.sb(es, name, [128, d], F32)
        self.dma('sp', t[:], row_ap.to_broadcast([128, d]), [], [t.b])
        return t

    def wload(self, w, k, src, c0, c1):
        c = c0
        while c < c1:
            ce = min(c1, c + 2048)
            self.dma('pool', w[:, k, c:ce], src[:, c:ce], [], [w.b])
            c = ce

    def rstd(self, ssq, H, d, R):
        ln, rs = self.scr['ln'], self.scr['rs']
        self.act(ln[:, 0:H], ssq, AF.Ln, R, [ln.b], bias=self.eps_c[:, 0:1], scale=1.0 / d)
        self.act(rs[:, 0:H], ln[:, 0:H], AF.Exp, [ln.b], [rs.b], scale=-0.5)
        return rs[:, 0:H]

    def rmsn(self, src, H, d, g, dst, R, W):
        sq, ssq, tmp = self.scr['sq'], self.scr['ssq'], self.scr['tmp']
        sqv = sq[:, 0:H * d].rearrange("p (h c) -> p h c", h=H)
        self.tt('pool', sqv, src, src, ALU.mult, R, [sq.b])
        self.red(ssq[:, 0:H], sqv, ALU.add, [sq.b], [ssq.b])
        rs = self.rstd(ssq[:, 0:H], H, d, [ssq.b])
        tv = tmp[:, 0:H * d].rearrange("p (h c) -> p h c", h=H)
        self.tt('dve', tv, src, rs.unsqueeze(2).to_broadcast([128, H, d]), ALU.mult,
                list(R) + [self.scr['rs'].b], [tmp.b])
        self.tt('pool', dst, tv, g[:].unsqueeze(1).to_broadcast([128, H, d]), ALU.mult,
                [tmp.b, g.b], W)

    def rope(self, src, H, d2, n, dst, R, W):
        tab = self.rope32 if d2 == 32 else self.rope16
        cosv = tab[:, n, 0:d2]
        sinv = tab[:, n, d2:2 * d2]
        A, Bm = self.scr['ra'], self.scr['rb']
        s4 = src.rearrange("p h (two c) -> p h two c", two=2)
        d4 = dst.rearrange("p h (two c) -> p h two c", two=2)
        Av = A[:, 0:H * 2 * d2].rearrange("p (h two c) -> p h two c", h=H, two=2)
        Bv = Bm[:, 0:H * 2 * d2].rearrange("p (h two c) -> p h two c", h=H, two=2)
        cb = cosv.unsqueeze(1).unsqueeze(1).to_broadcast([128, H, 2, d2])
        sbv = sinv.unsqueeze(1).unsqueeze(1).to_broadcast([128, H, 2, d2])
        self.tt('dve', Av, s4, cb, ALU.mult, list(R) + [tab.b], [A.b])
        self.tt('pool', Bv, s4, sbv, ALU.mult, list(R) + [tab.b], [Bm.b])
        self.tt('dve', d4[:, :, 0, :], Av[:, :, 0, :], Bv[:, :, 1, :], ALU.subtract, [A.b, Bm.b], W)
        self.tt('pool', d4[:, :, 1, :], Bv[:, :, 0, :], Av[:, :, 1, :], ALU.add, [A.b, Bm.b], W)

    def attn_block(self, rhs_q, q_R, N, tiles, scale, acc_views, acc_bufs, mode='exp', LA=2, epilogue=None):
        started = set()
        nt = len(tiles)
        last_use = {}
        for i, tl in enumerate(tiles):
            for j in tl['subs']:
                last_use[j] = i
        Atiles = {}

        def emit_s(i):
            tl = tiles[i]
            bi = self.st_rr % len(self.st_banks)
            self.st_rr += 1
            bank, bb = self.st_banks[bi]
            c0 = tl['c0']
            masks = tl.get('masks', [])
            if mode != 'exp':
                E, eR = tl['Efn']()
            self.mm(bank[:, c0:N], tl['kT'], rhs_q[:, c0:N], True, len(masks) == 0, list(q_R) + tl['R'], [bb])
            for mi, (ml, mr, off, ncols, mR) in enumerate(masks):
                self.mm(bank[:, off:off + ncols], ml, mr, False, mi == len(masks) - 1, mR, [bb])
            ai = self.at_rr % len(self.at_tiles)
            self.at_rr += 1
            A = self.at_tiles[ai]
            Atiles[i] = A
            if mode == 'exp':
                self.act(A[:, c0:N], bank[:, c0:N], AF.Exp, [bb], [A.b], scale=scale)
            else:
                self.tt('dve', A[:, c0:N], bank[:, c0:N], E[:, c0:N], ALU.mult, [bb] + eR, [A.b])
                if tl.get('diag') is not None:
                    dj = tl['diag']
                    self.tt('pool', A[:, dj * 128:(dj + 1) * 128], A[:, dj * 128:(dj + 1) * 128],
                            self.tri01T[:], ALU.mult, [A.b, self.tri01T.b], [A.b])

        def emit_pv(i):
            tl = tiles[i]
            A = Atiles.pop(i)
            for j in tl['subs']:
                view, bk = acc_views[j]
                st = bk not in started
                started.add(bk)
                self.mm(view, A[:, j * 128:(j + 1) * 128], tl['V'], st, last_use[j] == i,
                        [A.b] + tl['R'], [acc_bufs[bk]])

        for step in range(nt):
            emit_s(step)
            self.pend.append(('pv', (lambda i=step: emit_pv(i))))
            self.npv += 1
            self._drain(LA)
        if epilogue is None:
            self.attn_flush()
            return None
        self.epi_id += 1
        eid = self.epi_id
        self.pend.append(('epi', epilogue, eid))
        self._drain(LA)
        return eid

    def _drain(self, limit):
        q = self.pend
        while q and (q[0][0] == 'epi' or self.npv > limit):
            it = q.pop(0)
            if it[0] == 'pv':
                self.npv -= 1
                it[1]()
            else:
                it[1]()
                self.epi_done.add(it[2])

    def attn_flush(self):
        self._drain(-1)

    def attn_sync(self, eid):
        while eid is not None and eid not in self.epi_done:
            q = self.pend
            it = q.pop(0)
            if it[0] == 'pv':
                self.npv -= 1
                it[1]()
            else:
                it[1]()
                self.epi_done.add(it[2])

    def build(self):
        nc = self.nc
        T, NT = self.T, self.NT
        with ExitStack() as es:
            self.S = S = Sched(nc, es)
            self._decl_inputs()
            self._decl_scratch()
            self.ps = []
            for i in range(8):
                t = es.enter_context(nc.psum_tensor(f"psb{i}", [128, 512], F32))
                self.ps.append(Tl(t, f'ps{i}'))
            self._consts(es)
            self._prep(es)
            S.barrier()
            xin = Tl(self.x_in, 'xin')
            xin.b = [Buf('xin')] * 1
            cur = xin
            for idx, L in enumerate(self.layers):
                last = idx == len(self.layers) - 1
                dst = self.out_t if last else self.xs[idx % 2]
                with ExitStack() as les:
                    if L % 2 == 0:
                        self.layer_even(les, L, cur, dst)
                    else:
                        self.layer_odd(les, L, cur, dst)
                S.barrier()
                cur = dst
            S.barrier()
            S.emit()
        return nc

    def _decl_inputs(self):
        T, NT = self.T, self.NT
        d = self.din
        self.x_in = d("x", [T, D])
        self.mem = d("mem", [256, D])
        self.pos_t = d("pos_t", [128, NT], I32)
        self.w = {}
        spec = dict(
            ln_g=[4, D], mem_norm_g=[4, D], mem_w_kv=[4, D, 512], mem_q_norm_g=[4, 64], mem_k_norm_g=[4, 64],
            w_out=[4, 1280, D], even_w_in=[2, D, EVEN_COLS], mla_q_lat_g=[2, 256], mla_kv_lat_g=[2, 128],
            mla_w_uq=[2, 256, 768], mla_w_ukv=[2, 128, 1024], mla_q_norm_g=[2, 96], mla_k_norm_g=[2, 96],
            nsa_q_norm_g=[2, 64], nsa_k_norm_g=[2, 3, 64], nsa_cmp_posT=[2, 2, 64, 32],
            nsa_cmp_w1=[2, 2, 2048, 64], nsa_cmp_w2=[2, 2, 64, 64], odd_w_in=[2, D, ODD_COLS],
            dsa_q_norm_g=[2, 64], dsa_k_norm_g=[2, 64], mlstm_conv_wT=[2, 512, 4], mlstm_conv_b=[2, 512],
            mlstm_i_bias=[2, 4], mlstm_f_bias=[2, 4], mlstm_h_norm_g=[2, 128])
        for k, shp in spec.items():
            self.w[k] = d(k, shp)
        ncp = self.ncmp_pad = max(1, T // 2048) * 128
        nb = self.n_blk = T // 64
        self.c = dict(
            ident=d("c_ident", [128, 128], BF16), identf=d("c_identf", [128, 128], F32),
            i4=d("c_i4", [128, 512], BF16), i8x2=d("c_i8x2", [128, 512], BF16),
            tri_neg=d("c_tri_neg", [128, 128], BF16), edge_neg=d("c_edge_neg", [128, 128], BF16),
            tri01T=d("c_tri01T", [128, 128], BF16), trinegf=d("c_trinegf", [128, 128], F32),
            invf=d("c_invf", [128, 32]), pw=d("c_pw", [128, 40]), cmpneg=d("c_cmpneg", [T, ncp], BF16),
            forced=d("c_forced", [T, nb]), overlap=d("c_overlap", [ncp, nb], BF16))

    def _decl_scratch(self):
        T, NT = self.T, self.NT
        s = self.dscr
        self.out_t = Tl(self.nc.dram_tensor("out", [T, D], F32, kind="ExternalOutput").ap(), 'out')
        self.out_t.b = [Buf(f'out{i}') for i in range(NT)]
        self.xs = [s("xs0", [T, D], F32, NT), s("xs1", [T, D], F32, NT)]
        self.rope_d = s("rope_d", [T, 64], F32)
        self.Y = s("Y", [T, 2304], F32)
        self.G = s("G", [T, 1792], BF16, NT)
        self.SG = s("SG", [T, 32], F32, NT)
        self.mla_qT = s("mla_qT", [8, 96, T], BF16)
        self.mla_kT = s("mla_kT", [8, 96, T], BF16)
        self.mla_v = s("mla_v", [8, 128, NT, 65], BF16)
        self.nsa_qT = s("nsa_qT", [2, NT, 64, 4, 128], BF16)
        self.nsa_kT = s("nsa_kT", [8, 64, T], BF16)
        self.nsa_v = s("nsa_v", [4, 128, NT, 65], BF16)
        self.mem_qT = s("mem_qT", [4, 64, T], BF16)
        self.dsa_qT = s("dsa_qT", [NT, 64, 8, 128], BF16)
        self.dsa_kT = s("dsa_kT", [64, T], BF16)
        self.dsa_v = s("dsa_v", [128, NT, 65], BF16)
        self.idx_qT = s("idx_qT", [8, 32, T], BF16)
        self.idx_kT = s("idx_kT", [32, T], BF16)
        self.idx_w = s("idx_w", [T, 8], F32)
        self.ml_raw = s("ml_raw", [512, T], F32)
        self.ml_if = s("ml_if", [8, T], F32)
        self.ml_qkT = s("ml_qkT", [512, T], BF16)
        self.ml_v = s("ml_v", [4, 128, NT, 129], BF16)
        self.ml_g = s("ml_g", [12, T], F32)

    def _consts(self, es):
        c = self.c
        def ld(name, shape, dt):
            t = self.sb(es, name, shape, dt)
            self.dma('sp', t[:], c[name], [], [t.b])
            return t
        self.ident = ld('ident', [128, 128], BF16)
        self.identf = ld('identf', [128, 128], F32)
        self.i4 = ld('i4', [128, 512], BF16)
        self.i8x2 = ld('i8x2', [128, 512], BF16)
        self.tri_neg = ld('tri_neg', [128, 128], BF16)
        self.edge_neg = ld('edge_neg', [128, 128], BF16)
        self.tri01T = ld('tri01T', [128, 128], BF16)
        self.trinegf = ld('trinegf', [128, 128], F32)
        self.invf = ld('invf', [128, 32], F32)
        self.pw = ld('pw', [128, 40], F32)
        self.eps_c = self.sb(es, 'eps_c', [128, 1], F32)
        self.memset('dve', self.eps_c[:], EPS, [self.eps_c.b])
        self.one_c = self.sb(es, 'one_c', [128, 1], F32)
        self.memset('dve', self.one_c[:], 1.0, [self.one_c.b])
        self.rope32 = self.sb(es, 'rope32', [128, self.NT, 64], F32)
        self.rope16 = self.sb(es, 'rope16', [128, self.NT, 32], F32)
        self.sc_sq = self.sb(es, 'sc_sq', [128, 512], F32)
        self.sc_tmp = self.sb(es, 'sc_tmp', [128, 512], F32)
        self.sc_ra = self.sb(es, 'sc_ra', [128, 512], F32)
        self.sc_rb = self.sb(es, 'sc_rb', [128, 512], F32)
        self.sc_ssq = self.sb(es, 'sc_ssq', [128, 16], F32)
        self.sc_ln = self.sb(es, 'sc_ln', [128, 16], F32)
        self.sc_rs = self.sb(es, 'sc_rs', [128, 16], F32)
        self.scr0 = dict(sq=self.sc_sq, tmp=self.sc_tmp, ra=self.sc_ra, rb=self.sc_rb, ssq=self.sc_ssq,
                         ln=self.sc_ln, rs=self.sc_rs)
        self.scr = self.scr0

    def _prep(self, es0):
        NT = self.NT
        PI = math.pi
        with ExitStack() as es:
            pi_t = self.sb(es, 'pos_i', [128, NT], I32)
            self.dma('sp', pi_t[:], self.pos_t, [], [pi_t.b])
            pf = self.sb(es, 'pos_f', [128, NT], F32)
            self.cp('dve', pf[:], pi_t[:], [pi_t.b], [pf.b])
            ang = self.sb(es, 'ang', [128, NT, 32], F32)
            self.tt('dve', ang[:], pf[:].unsqueeze(2).to_broadcast([128, NT, 32]),
                    self.invf[:].unsqueeze(1).to_broadcast([128, NT, 32]), ALU.mult,
                    [pf.b, self.invf.b], [ang.b])
            kf = self.sb(es, 'kf', [128, NT, 32], F32)
            ki = self.sb(es, 'ki', [128, NT, 32], I32)
            r = self.sb(es, 'r', [128, NT, 32], F32)
            m = self.sb(es, 'm', [128, NT, 32], F32)

            def wrap(buf):
                self.ts('dve', m[:], buf[:], PI, ALU.is_gt, [buf.b], [m.b], s2=-2 * PI, op1=ALU.mult)
                self.tt('dve', buf[:], buf[:], m[:], ALU.add, [buf.b, m.b], [buf.b])
                self.ts('dve', m[:], buf[:], -PI, ALU.is_lt, [buf.b], [m.b], s2=2 * PI, op1=ALU.mult)
                self.tt('dve', buf[:], buf[:], m[:], ALU.add, [buf.b, m.b], [buf.b])
                self.ts('dve', buf[:], buf[:], PI, ALU.min, [buf.b], [buf.b], s2=-PI, op1=ALU.max)
            self.ts('dve', kf[:], ang[:], 1.0 / (2 * PI), ALU.mult, [ang.b], [kf.b])
            self.cp('dve', ki[:], kf[:], [kf.b], [ki.b])
            self.cp('dve', kf[:], ki[:], [ki.b], [kf.b])
            C1 = 6.28125
            C2 = 2 * PI - C1
            self.stt(r[:], kf[:], -C1, ang[:], ALU.mult, ALU.add, [kf.b, ang.b], [r.b])
            self.stt(r[:], kf[:], -C2, r[:], ALU.mult, ALU.add, [kf.b, r.b], [r.b])
            wrap(r)
            self.act(self.rope32[:, :, 32:64], r[:], AF.Sin, [r.b], [self.rope32.b])
            self.ts('dve', r[:], r[:], PI / 2, ALU.add, [r.b], [r.b])
            wrap(r)
            self.act(self.rope32[:, :, 0:32], r[:], AF.Sin, [r.b], [self.rope32.b])
            self.cp('dve', self.rope16[:, :, 0:16], self.rope32[:, :, 0:32:2], [self.rope32.b], [self.rope16.b])
            self.cp('dve', self.rope16[:, :, 16:32], self.rope32[:, :, 32:64:2], [self.rope32.b], [self.rope16.b])
            self.dma('sp', self.rope_d[:].rearrange("(n p) c -> p n c", p=128), self.rope32[:],
                     [self.rope32.b], [self.rope_d.b])
            self.S.barrier()

    def load_win(self, es, L, w_in, ncols):
        win = self.sb(es, 'win', [128, 8, ncols], BF16)
        for k in range(8):
            self.wload(win, k, w_in[k * 128:(k + 1) * 128, :], 0, ncols)
        lng = self.bc_load(es, 'lng', self.w['ln_g'][L:L + 1, :], D)
        return win, lng

    def load_wout(self, es, L):
        wout = self.sb(es, 'wout', [128, 10, D], BF16)
        for k in range(10):
            self.wload(wout, k, self.w['w_out'][L, k * 128:(k + 1) * 128, :], 0, D)
        return wout

    def mem_kv(self, es, L):
        w = self.w
        kT = self.sb(es, 'memkT', [64, 4, 256], BF16)
        V = self.sb(es, 'memV', [128, 2, 4, 65], BF16)
        self.memset('pool', V[:], 1.0, [V.b])
        with ExitStack() as s2:
            wkv = self.sb(s2, 'wkv', [128, 8, 512], BF16)
            for k in range(8):
                self.wload(wkv, k, w['mem_w_kv'][L, k * 128:(k + 1) * 128, :], 0, 512)
            mg = self.bc_load(s2, 'mg', w['mem_norm_g'][L:L + 1, :], D)
            kg = self.bc_load(s2, 'kg', w['mem_k_norm_g'][L:L + 1, :], 64)
            mt = self.sb(s2, 'mt', [128, D], F32)
            junk = self.sb(s2, 'junk', [128, D], BF16)
            mh = self.sb(s2, 'mh', [128, D], BF16)
            mhT = self.sb(s2, 'mhT', [128, 8, 128], BF16)
            kv = self.sb(s2, 'kv', [128, 512], F32)
            kn = self.sb(s2, 'kn', [128, 256], BF16)
            ss = self.sb(s2, 'ss', [128, 1], F32)
            for i in range(2):
                self.dma('sp', mt[:], self.mem[i * 128:(i + 1) * 128, :], [], [mt.b])
                self.act(junk[:], mt[:], AF.Square, [mt.b], [junk.b, ss.b], accum=ss[:])
                rs = self.rstd(ss[:, 0:1], 1, D, [ss.b])
                self.stt(mh[:], mt[:], rs, mg[:], ALU.mult, ALU.mult, [mt.b, self.scr['rs'].b, mg.b], [mh.b])
                pT = self.ps[2]
                pTb = pT[:].bitcast(BF16)
                for k in range(8):
                    self.tr(pTb[:, k * 128:(k + 1) * 128], mh[:, k * 128:(k + 1) * 128], self.ident[:],
                            [mh.b, self.ident.b], [pT.b])
                self.cp('act', mhT[:].rearrange("p k c -> p (k c)"), pTb[:, 0:1024], [pT.b], [mhT.b])
                pU = self.ps[0]
                for k in range(8):
                    self.mm(pU[:, 0:512], mhT[:, k, :], wkv[:, k, :], k == 0, k == 7, [mhT.b, wkv.b], [pU.b])
                self.cp('act', kv[:], pU[:, 0:512], [pU.b], [kv.b])
                self.rmsn(kv[:, 0:256].rearrange("p (h c) -> p h c", h=4), 4, 64, kg,
                          kn[:].rearrange("p (h c) -> p h c", h=4), [kv.b], [kn.b])
                self.cp('dve', V[:, i, :, 0:64], kv[:, 256:512].rearrange("p (h c) -> p h c", h=4), [kv.b], [V.b])
                pK = self.ps[3]
                pKb = pK[:].bitcast(BF16)
                for h in range(4):
                    self.tr(pKb[0:64, h * 128:(h + 1) * 128], kn[:, h * 64:(h + 1) * 64], self.ident[:],
                            [kn.b, self.ident.b], [pK.b])
                self.cp('act', kT[:, :, i * 128:(i + 1) * 128],
                        pKb[0:64, 0:512].rearrange("p (h c) -> p h c", h=4), [pK.b], [kT.b])
            self.S.barrier()
        return kT, V

    def p1_front(self, n, x_src, xt, ht, hT, u, ss, junk, lng, win, ncols):
        sl = n % 2
        x_t, h_t, hT_t, u_t = xt[sl], ht[sl], hT[sl], u[sl]
        xb = x_src.b[n] if len(x_src.b) > 1 else x_src.b[0]
        self.dma('sp', x_t[:], x_src[n * 128:(n + 1) * 128, :], [xb], [x_t.b])
        self.act(junk[:], x_t[:], AF.Square, [x_t.b], [junk.b, ss.b], accum=ss[:])
        rs = self.rstd(ss[:, 0:1], 1, D, [ss.b])
        self.stt(h_t[:], x_t[:], rs, lng[:], ALU.mult, ALU.mult, [x_t.b, self.scr['rs'].b, lng.b], [h_t.b])
        pT = self.ps[2]
        pTb = pT[:].bitcast(BF16)
        for k in range(8):
            self.tr(pTb[:, k * 128:(k + 1) * 128], h_t[:, k * 128:(k + 1) * 128], self.ident[:],
                    [h_t.b, self.ident.b], [pT.b])
        self.cp('act', hT_t[:].rearrange("p k c -> p (k c)"), pTb[:, 0:1024], [pT.b], [hT_t.b])
        nchunk = (ncols + 511) // 512
        for c in range(nchunk):
            c0 = c * 512
            wd = min(512, ncols - c0)
            pU = self.ps[c % 2]
            for k in range(8):
                self.mm(pU[:, 0:wd], hT_t[:, k, :], win[:, k, c0:c0 + wd], k == 0, k == 7,
                        [hT_t.b, win.b], [pU.b])
            self.cp('act' if c % 2 == 0 else 'dve', u_t[:, c0:c0 + wd], pU[:, 0:wd], [pU.b], [u_t.b])
        return u_t

    def transposes_out(self, srcs, rows, stage, dst_ap, dst_b, pidx):
        pT = self.ps[pidx]
        pTb = pT[:].bitcast(BF16)
        k = len(srcs)
        for i, (ap, R) in enumerate(srcs):
            self.tr(pTb[0:rows, i * 128:(i + 1) * 128], ap, self.ident[:], list(R) + [self.ident.b], [pT.b])
        self.cp('act', stage[0:rows, 0:k, :], pTb[0:rows, 0:k * 128].rearrange("p (k c) -> p k c", k=k),
                [pT.b], [stage.b])
        self.dma('sp', dst_ap, stage[0:rows, 0:k, :], [stage.b], dst_b)

    def p3(self, es, L, x_src, x_dst, wout, mixfn):
        NT = self.NT
        xt = [self.sb(es, f'p3x{i}', [128, D], F32) for i in range(2)]
        mixT = [self.sb(es, f'p3mT{i}', [128, 10, 128], BF16) for i in range(2)]
        xo = [self.sb(es, f'p3o{i}', [128, D], F32) for i in range(2)]

        def one(n):
            sl = n % 2
            pb = 4 * sl
            mix = mixfn(n)
            xb = x_src.b[n] if len(x_src.b) > 1 else x_src.b[0]
            self.dma('sp', xt[sl][:], x_src[n * 128:(n + 1) * 128, :], [xb], [xt[sl].b])
            for half in range(2):
                pT = self.ps[pb + 2 + half]
                pTb = pT[:].bitcast(BF16)
                for k in range(5):
                    kk = half * 5 + k
                    self.tr(pTb[:, k * 128:(k + 1) * 128], mix[:, kk * 128:(kk + 1) * 128], self.ident[:],
                            [mix.b, self.ident.b], [pT.b])
                self.cp('act', mixT[sl][:, half * 5:half * 5 + 5, :].rearrange("p k c -> p (k c)"),
                        pTb[:, 0:640], [pT.b], [mixT[sl].b])
            for c in range(2):
                pU = self.ps[pb + c]
                for k in range(10):
                    self.mm(pU[:, 0:512], mixT[sl][:, k, :], wout[:, k, c * 512:(c + 1) * 512], k == 0, k == 9,
                            [mixT[sl].b, wout.b], [pU.b])
                self.tt('dve', xo[sl][:, c * 512:(c + 1) * 512], pU[:, 0:512], xt[sl][:, c * 512:(c + 1) * 512],
                        ALU.add, [pU.b, xt[sl].b], [xo[sl].b])
            self.dma('sp', x_dst[n * 128:(n + 1) * 128, :], xo[sl][:], [xo[sl].b], [x_dst.b[n]])

        for n0 in range(0, NT, 2):
            self.chains_begin(['P0', 'P1'])
            self.chain('P0')
            one(n0)
            if n0 + 1 < NT:
                self.chain('P1')
                one(n0 + 1)
            self.chains_emit()

    def attn_setup(self, es, dvp_two_banks=False):
        self.st_banks = [(self.ps[i], self.ps[i].b) for i in range(3)]
        self.st_rr = 0
        self.at_tiles = [self.sb(es, f'At{i}', [128, 512], BF16) for i in range(3)]
        self.at_rr = 0
        self.pend = []
        self.npv = 0
        self.epi_id = 0
        self.epi_done = set()

    def mem_attn(self, es, memkT, memV):
        T = self.T
        qT = [self.sb(es, f'mqT{i}', [64, T], BF16) for i in range(2)]
        ost = [self.sb(es, f'most{i}', [128, 4, 64], F32) for i in range(2)]
        rec = self.sb(es, 'mrec', [128, 4], F32)
        blk = 0
        for h in range(4):
            q = qT[h % 2]
            self.dma('sp', q[:], self.mem_qT[h], [self.mem_qT.b], [q.b])
            for cq in range(T // 512):
                set_i = blk % 2
                blk += 1
                accb = self.ps[3 + set_i]
                views = {j: (accb[:, j * 65:(j + 1) * 65], 0) for j in range(4)}
                tiles = [dict(kT=memkT[:, h, kt * 128:(kt + 1) * 128], V=memV[:, kt, h, :],
                              R=[memkT.b, memV.b], c0=0, subs=[0, 1, 2, 3]) for kt in range(2)]
                def epi(accb=accb, o=ost[set_i], cq=cq, h=h):
                    av = accb[:, 0:260].rearrange("p (j c) -> p j c", j=4)
                    self.op_('dve', lambda e, av=av: e.reciprocal(out=rec[:], in_=av[:, :, 64]), reads=[accb.b],
                             writes=[rec.b])
                    self.tt('dve', o[:], av[:, :, 0:64], rec[:].unsqueeze(2).to_broadcast([128, 4, 64]), ALU.mult,
                            [accb.b, rec.b], [o.b])
                    self.dma('sp', self.Y[cq * 512:(cq + 1) * 512, 1024 + h * 64:1024 + (h + 1) * 64]
                             .rearrange("(j p) c -> p j c", p=128), o[:], [o.b], [self.Y.b])
                self.attn_block(q[:, cq * 512:(cq + 1) * 512], [q.b], 512, tiles, 0.125, views, [accb.b], epilogue=epi)
        self.attn_flush()

    def layer_even(self, es, L, x_src, x_dst):
        li = L // 2
        w = self.w
        T, NT = self.T, self.NT
        S = self.S
        memkT, memV = self.mem_kv(es, L)
        with ExitStack() as p1:
            win, lng = self.load_win(p1, L, w['even_w_in'][li], EVEN_COLS)
            wuq = self.sb(p1, 'wuq', [128, 2, 768], BF16)
            for k in range(2):
                self.wload(wuq, k, w['mla_w_uq'][li, k * 128:(k + 1) * 128, :], 0, 768)
            wukv = self.sb(p1, 'wukv', [128, 1, 1024], BF16)
            self.wload(wukv, 0, w['mla_w_ukv'][li], 0, 1024)
            g_ql = self.bc_load(p1, 'g_ql', w['mla_q_lat_g'][li:li + 1, :], 256)
            g_kvl = self.bc_load(p1, 'g_kvl', w['mla_kv_lat_g'][li:li + 1, :], 128)
            g_qn = self.bc_load(p1, 'g_qn', w['mla_q_norm_g'][li:li + 1, 0:64], 64)
            g_qp = self.bc_load(p1, 'g_qp', w['mla_q_norm_g'][li:li + 1, 64:96], 32)
            g_kn = self.bc_load(p1, 'g_kn', w['mla_k_norm_g'][li:li + 1, 0:64], 64)
            g_kp = self.bc_load(p1, 'g_kp', w['mla_k_norm_g'][li:li + 1, 64:96], 32)
            g_bq = self.bc_load(p1, 'g_bq', w['nsa_q_norm_g'][li:li + 1, :], 64)
            g_ks = self.bc_load(p1, 'g_ks', w['nsa_k_norm_g'][li, 1:2, :], 64)
            g_kw = self.bc_load(p1, 'g_kw', w['nsa_k_norm_g'][li, 2:3, :], 64)
            g_mq = self.bc_load(p1, 'g_mq', w['mem_q_norm_g'][L:L + 1, :], 64)
            xt = [self.sb(p1, f'xt{i}', [128, D], F32) for i in range(2)]
            ht = [self.sb(p1, f'ht{i}', [128, D], BF16) for i in range(2)]
            hT = [self.sb(p1, f'hT{i}', [128, 8, 128], BF16) for i in range(2)]
            u = [self.sb(p1, f'u{i}', [128, EVEN_COLS], F32) for i in range(2)]
            ss = self.sb(p1, 'ss', [128, 1], F32)
            junk = self.sb(p1, 'junk', [128, D], BF16)
            latn = self.sb(p1, 'latn', [128, 384], BF16)
            latT = self.sb(p1, 'latT', [128, 3, 128], BF16)
            qsb = self.sb(p1, 'qsb', [128, 768], F32)
            kvsb = self.sb(p1, 'kvsb', [128, 1024], F32)
            qpe = self.sb(p1, 'qpe', [128, 256], F32)
            kpe = self.sb(p1, 'kpe', [128, 32], F32)
            kpeb = self.sb(p1, 'kpeb', [128, 32], BF16)
            qf = self.sb(p1, 'qf', [128, 8, 96], BF16)
            kfm = self.sb(p1, 'kfm', [128, 8, 96], BF16)
            vaug = self.sb(p1, 'vaug', [128, 8, 65], BF16)
            self.memset('pool', vaug[:], 1.0, [vaug.b])
            bqn = self.sb(p1, 'bqn', [128, 512], F32)
            bqf = self.sb(p1, 'bqf', [128, 512], BF16)
            kn2 = self.sb(p1, 'kn2', [128, 128], F32)
            kmisc = self.sb(p1, 'kmisc', [128, 8, 64], BF16)
            nv = self.sb(p1, 'nv', [128, 4, 65], BF16)
            self.memset('pool', nv[:], 1.0, [nv.b])
            mqf = self.sb(p1, 'mqf', [128, 256], BF16)
            gt = self.sb(p1, 'gt', [128, 1280], BF16)
            sg = self.sb(p1, 'sg', [128, 24], F32)
            stq = self.sb(p1, 'stq', [96, 8, 128], BF16)
            stk = self.sb(p1, 'stk', [96, 8, 128], BF16)
            stb = self.sb(p1, 'stb', [64, 8, 128], BF16)
            stm = self.sb(p1, 'stm', [64, 8, 128], BF16)
            stmq = self.sb(p1, 'stmq', [64, 4, 128], BF16)
            scrA = self.new_scr(p1, 'A', 512)
            scrB = self.new_scr(p1, 'B', 512)
            scrC = self.new_scr(p1, 'C', 128)
            ut_next = self.p1_front(0, x_src, xt, ht, hT, u, ss, junk, lng, win, EVEN_COLS)
            for n in range(NT):
                ut = ut_next
                U = lambda a, b_, ut=ut: ut[:, a:b_]
                ub = [ut.b]
                tsl = slice(n * 128, (n + 1) * 128)
                self.chains_begin(['A', 'F', 'B', 'C', 'M', 'G'])
                if n + 1 < NT:
                    self.chain('F')
                    ut_next = self.p1_front(n + 1, x_src, xt, ht, hT, u, ss, junk, lng, win, EVEN_COLS)
                self.chain('G')
                self.act(gt[:, 0:512], U(416, 928), AF.Silu, ub, [gt.b])
                self.act(gt[:, 512:1024], U(2232, 2744), AF.Silu, ub, [gt.b])
                self.act(gt[:, 1024:1280], U(3000, 3256), AF.Silu, ub, [gt.b])
                self.dma('sp', self.G[tsl, 0:1280], gt[:], [gt.b], [self.G.b[n]])
                self.act(sg[:], U(2208, 2232), AF.Sigmoid, ub, [sg.b])
                self.dma('sp', self.SG[tsl, 0:24], sg[:], [sg.b], [self.SG.b[n]])
                self.chain('A', scrA)
                self.rmsn(U(0, 256).rearrange("p (h c) -> p h c", h=1), 1, 256, g_ql,
                          latn[:, 0:256].rearrange("p (h c) -> p h c", h=1), ub, [latn.b])
                self.rmsn(U(256, 384).rearrange("p (h c) -> p h c", h=1), 1, 128, g_kvl,
                          latn[:, 256:384].rearrange("p (h c) -> p h c", h=1), ub, [latn.b])
                pT = self.ps[3]
                pTb = pT[:].bitcast(BF16)
                for k in range(3):
                    self.tr(pTb[:, k * 128:(k + 1) * 128], latn[:, k * 128:(k + 1) * 128], self.ident[:],
                            [latn.b, self.ident.b], [pT.b])
                self.cp('act', latT[:].rearrange("p k c -> p (k c)"), pTb[:, 0:384], [pT.b], [latT.b])
                pQ, pQ2, pK, pK2 = self.ps[3], self.ps[4], self.ps[5], self.ps[4]
                for k in range(2):
                    self.mm(pQ[:, 0:512], latT[:, k, :], wuq[:, k, 0:512], k == 0, k == 1, [latT.b, wuq.b], [pQ.b])
                for k in range(2):
                    self.mm(pQ2[:, 0:256], latT[:, k, :], wuq[:, k, 512:768], k == 0, k == 1, [latT.b, wuq.b], [pQ2.b])
                self.cp('act', qsb[:, 0:512], pQ[:, 0:512], [pQ.b], [qsb.b])
                self.cp('dve', qsb[:, 512:768], pQ2[:, 0:256], [pQ2.b], [qsb.b])
                self.mm(pK[:, 0:512], latT[:, 2, :], wukv[:, 0, 0:512], True, True, [latT.b, wukv.b], [pK.b])
                self.mm(pK2[:, 0:512], latT[:, 2, :], wukv[:, 0, 512:1024], True, True, [latT.b, wukv.b], [pK2.b])
                self.cp('act', kvsb[:, 0:512], pK[:, 0:512], [pK.b], [kvsb.b])
                self.cp('dve', kvsb[:, 512:1024], pK2[:, 0:512], [pK2.b], [kvsb.b])
                q3 = qsb[:].rearrange("p (h c) -> p h c", h=8)
                kv3 = kvsb[:].rearrange("p (h c) -> p h c", h=8)
                self.rmsn(q3[:, :, 0:64], 8, 64, g_qn, qf[:, :, 0:64], [qsb.b], [qf.b])
                qpe3 = qpe[:].rearrange("p (h c) -> p h c", h=8)
                self.rmsn(q3[:, :, 64:96], 8, 32, g_qp, qpe3, [qsb.b], [qpe.b])
                self.rope(qpe3, 8, 16, n, qf[:, :, 64:96], [qpe.b], [qf.b])
                self.rmsn(kv3[:, :, 0:64], 8, 64, g_kn, kfm[:, :, 0:64], [kvsb.b], [kfm.b])
                self.cp('dve', vaug[:, :, 0:64], kv3[:, :, 64:128], [kvsb.b], [vaug.b])
                kpe3 = kpe[:].rearrange("p (h c) -> p h c", h=1)
                self.rmsn(U(384, 416).rearrange("p (h c) -> p h c", h=1), 1, 32, g_kp, kpe3, ub, [kpe.b])
                self.rope(kpe3, 1, 16, n, kpeb[:].rearrange("p (h c) -> p h c", h=1), [kpe.b], [kpeb.b])
                self.cp('pool', kfm[:, :, 64:96], kpeb[:].unsqueeze(1).to_broadcast([128, 8, 32]), [kpeb.b], [kfm.b])
                self.transposes_out([(qf[:, h, :], [qf.b]) for h in range(8)], 96, stq,
                                    self.mla_qT[:, :, tsl].rearrange("h d t -> d h t"), [self.mla_qT.b], 3)
                self.transposes_out([(kfm[:, h, :], [kfm.b]) for h in range(8)], 96, stk,
                                    self.mla_kT[:, :, tsl].rearrange("h d t -> d h t"), [self.mla_kT.b], 5)
                self.dma('sp', self.mla_v[:, :, n, :].rearrange("h p c -> p h c"), vaug[:], [vaug.b], [self.mla_v.b])
                self.chain('B', scrB)
                bq3 = bqn[:].rearrange("p (h c) -> p h c", h=8)
                self.rmsn(U(928, 1440).rearrange("p (h c) -> p h c", h=8), 8, 64, g_bq, bq3, ub, [bqn.b])
                self.rope(bq3, 8, 32, n, bqf[:].rearrange("p (h c) -> p h c", h=8), [bqn.b], [bqf.b])
                self.chain('C', scrC)
                k23 = kn2[:].rearrange("p (h c) -> p h c", h=2)
                self.rmsn(U(1696, 1824).rearrange("p (h c) -> p h c", h=2), 2, 64, g_ks, k23, ub, [kn2.b])
                self.rope(k23, 2, 32, n, kmisc[:, 0:2, :], [kn2.b], [kmisc.b])
                self.rmsn(U(1952, 2080).rearrange("p (h c) -> p h c", h=2), 2, 64, g_kw, k23, ub, [kn2.b])
                self.rope(k23, 2, 32, n, kmisc[:, 2:4, :], [kn2.b], [kmisc.b])
                self.cp('pool', kmisc[:, 4:8, :], U(1440, 1696).rearrange("p (h c) -> p h c", h=4), ub, [kmisc.b])
                self.cp('dve', nv[:, 0:2, 0:64], U(1824, 1952).rearrange("p (h c) -> p h c", h=2), ub, [nv.b])
                self.cp('dve', nv[:, 2:4, 0:64], U(2080, 2208).rearrange("p (h c) -> p h c", h=2), ub, [nv.b])
                self.chain('B', scrB)
                self.transposes_out([(bqf[:, h * 64:(h + 1) * 64], [bqf.b]) for h in range(8)], 64, stb,
                                    self.nsa_qT[:, n].rearrange("g d r t -> d g r t"), [self.nsa_qT.b], 6)
                self.chain('C', scrC)
                self.transposes_out([(kmisc[:, i, :], [kmisc.b]) for i in range(8)], 64, stm,
                                    self.nsa_kT[:, :, tsl].rearrange("k d t -> d k t"), [self.nsa_kT.b], 7)
                self.dma('sp', self.nsa_v[:, :, n, :].rearrange("k p c -> p k c"), nv[:], [nv.b], [self.nsa_v.b])
                self.chain('B', scrB)
                self.rmsn(U(2744, 3000).rearrange("p (h c) -> p h c", h=4), 4, 64, g_mq,
                          mqf[:].rearrange("p (h c) -> p h c", h=4), ub, [mqf.b])
                self.transposes_out([(mqf[:, h * 64:(h + 1) * 64], [mqf.b]) for h in range(4)], 64, stmq,
                                    self.mem_qT[:, :, tsl].rearrange("h d t -> d h t"), [self.mem_qT.b], 6)
                self.chains_emit()
            S.barrier()
        kcmpT = self.sb(es, 'kcmpT', [64, 2, self.ncmp_pad], BF16)
        vcmp = self.sb(es, 'vcmp', [128, 2, self.ncmp_pad // 128, 129], BF16)
        self.nsa_compress(li, kcmpT, vcmp)
        S.barrier()
        with ExitStack() as pa:
            self.attn_setup(pa)
            self.mla_attn(pa)
            S.barrier()
        with ExitStack() as pa:
            self.attn_setup(pa)
            self.mem_attn(pa, memkT, memV)
            S.barrier()
        with ExitStack() as pa:
            self.attn_setup(pa)
            self.nsa_attn(pa, kcmpT, vcmp)
            S.barrier()
        with ExitStack() as p3:
            wout = self.load_wout(p3, L)
            yt = [self.sb(p3, f'yt{i}', [128, 2304], F32) for i in range(2)]
            gtt = [self.sb(p3, f'gtt{i}', [128, 1280], BF16) for i in range(2)]
            sgt = [self.sb(p3, f'sgt{i}', [128, 24], F32) for i in range(2)]
            mix = [self.sb(p3, f'mix{i}', [128, 1280], BF16) for i in range(2)]
            ybs = [self.sb(p3, f'yb{i}', [128, 512], F32) for i in range(2)]
            yb2s = [self.sb(p3, f'yb2{i}', [128, 512], F32) for i in range(2)]

            def mixfn(n):
                sl = n % 2
                y, g, s_, m = yt[sl], gtt[sl], sgt[sl], mix[sl]
                yb, yb2 = ybs[sl], yb2s[sl]
                tsl = slice(n * 128, (n + 1) * 128)
                self.dma('sp', y[:], self.Y[tsl, :], [self.Y.b], [y.b])
                self.dma('sp', g[:], self.G[tsl, 0:1280], [self.G.b[n]], [g.b])
                self.dma('sp', s_[:], self.SG[tsl, 0:24], [self.SG.b[n]], [s_.b])
                self.tt('dve', m[:, 0:512], y[:, 0:512], g[:, 0:512], ALU.mult, [y.b, g.b], [m.b])
                self.tt('pool', m[:, 1024:1280], y[:, 1024:1280], g[:, 1024:1280], ALU.mult, [y.b, g.b], [m.b])
                s3 = s_[:].rearrange("p (h c) -> p h c", c=3)
                y3 = lambda a: y[:, a:a + 512].rearrange("p (h c) -> p h c", h=8)
                b3 = yb[:].rearrange("p (h c) -> p h c", h=8)
                b23 = yb2[:].rearrange("p (h c) -> p h c", h=8)
                self.tt('dve', b3, y3(1280), s3[:, :, 0:1].to_broadcast([128, 8, 64]), ALU.mult, [y.b, s_.b], [yb.b])
                self.tt('pool', b23, y3(512), s3[:, :, 1:2].to_broadcast([128, 8, 64]), ALU.mult, [y.b, s_.b], [yb2.b])
                self.tt('dve', b3, b3, b23, ALU.add, [yb.b, yb2.b], [yb.b])
                self.tt('pool', b23, y3(1792), s3[:, :, 2:3].to_broadcast([128, 8, 64]), ALU.mult, [y.b, s_.b], [yb2.b])
                self.tt('dve', b3, b3, b23, ALU.add, [yb.b, yb2.b], [yb.b])
                self.tt('dve', m[:, 512:1024], yb[:], g[:, 512:1024], ALU.mult, [yb.b, g.b], [m.b])
                return m
            self.p3(p3, L, x_src, x_dst, wout, mixfn)
            S.barrier()

    def mla_attn(self, es):
        T, NT = self.T, self.NT
        qT = [self.sb(es, f'aqT{i}', [96, T], BF16) for i in range(2)]
        kT = [self.sb(es, f'akT{i}', [96, T], BF16) for i in range(2)]
        V = [self.sb(es, f'aV{i}', [128, NT, 65], BF16) for i in range(2)]
        ost = [self.sb(es, f'aost{i}', [128, 4, 64], F32) for i in range(2)]
        rec = self.sb(es, 'arec', [128, 4], F32)
        scale = 96 ** -0.5
        blk = 0
        for h in range(8):
            q, k, v = qT[h % 2], kT[h % 2], V[h % 2]
            self.dma('sp', q[:], self.mla_qT[h], [self.mla_qT.b], [q.b])
            self.dma('sp', k[:], self.mla_kT[h], [self.mla_kT.b], [k.b])
            self.dma('sp', v[:], self.mla_v[h], [self.mla_v.b], [v.b])
            for cq in range(T // 512):
                set_i = blk % 2
                blk += 1
                accb = self.ps[3 + set_i]
                views = {j: (accb[:, j * 65:(j + 1) * 65], 0) for j in range(4)}
                tiles = []
                for kt in range(4 * cq + 4):
                    vv = kt - 4 * cq
                    tl = dict(kT=k[:, kt * 128:(kt + 1) * 128], V=v[:, kt, :], R=[k.b, v.b],
                              c0=max(0, vv) * 128, subs=list(range(max(0, vv), 4)))
                    if vv >= 0:
                        tl['masks'] = [(self.tri_neg[:], self.ident[:], vv * 128, 128,
                                        [self.tri_neg.b, self.ident.b])]
                    tiles.append(tl)
                def epi(accb=accb, o=ost[set_i], cq=cq, h=h):
                    av = accb[:, 0:260].rearrange("p (j c) -> p j c", j=4)
                    self.op_('dve', lambda e, av=av: e.reciprocal(out=rec[:], in_=av[:, :, 64]), reads=[accb.b],
                             writes=[rec.b])
                    self.tt('dve', o[:], av[:, :, 0:64], rec[:].unsqueeze(2).to_broadcast([128, 4, 64]), ALU.mult,
                            [accb.b, rec.b], [o.b])
                    self.dma('sp', self.Y[cq * 512:(cq + 1) * 512, h * 64:(h + 1) * 64]
                             .rearrange("(j p) c -> p j c", p=128), o[:], [o.b], [self.Y.b])
                self.attn_block(q[:, cq * 512:(cq + 1) * 512], [q.b], 512, tiles, scale, views, [accb.b], epilogue=epi)
        self.attn_flush()

    def nsa_compress(self, li, kcmpT, vcmp):
        w = self.w
        T = self.T
        n_cmp = (T - 32) // 16 + 1
        ncp = self.ncmp_pad
        nct = ncp // 128
        self.memset('pool', kcmpT[:], 0.0, [kcmpT.b])
        self.memset('pool', vcmp[:], 0.0, [vcmp.b])
        with ExitStack() as es:
            w1 = self.sb(es, 'cw1', [64, 2, 32, 64], BF16)
            w2 = self.sb(es, 'cw2', [64, 2, 64], BF16)
            peT = self.sb(es, 'cpeT', [64, 2, 32], BF16)
            for kv in range(2):
                self.dma('pool', w1[:, kv], w['nsa_cmp_w1'][li, kv].rearrange("(l d) o -> d l o", d=64), [], [w1.b])
                self.dma('pool', w2[:, kv], w['nsa_cmp_w2'][li, kv], [], [w2.b])
                self.dma('pool', peT[:, kv], w['nsa_cmp_posT'][li, kv], [], [peT.b])
            g_kc = self.bc_load(es, 'g_kc', w['nsa_k_norm_g'][li, 0:1, :], 64)
            ovl = self.sb(es, 'ovl', [128, nct, self.n_blk], BF16)
            self.dma('sp', ovl[:], self.c['overlap'].rearrange("(k p) j -> p k j", p=128), [], [ovl.b])
            xT = [self.sb(es, f'cxT{i}', [64, T], BF16) for i in range(2)]
            bias = self.sb(es, 'cbias', [64, 1], F32)
            hid = self.sb(es, 'chid', [64, ncp], BF16)
            self.memset('pool', hid[:], 0.0, [hid.b])
            ctm = self.sb(es, 'ctm', [128, 64], F32)
            ctn = self.sb(es, 'ctn', [128, 64], F32)
            ctb = self.sb(es, 'ctb', [128, 64], BF16)
            rp = self.sb(es, 'crp', [128, 64], F32)
            it = 0
            for kv in range(2):
                for g in range(2):
                    x = xT[it % 2]
                    it += 1
                    self.dma('sp', x[:], self.nsa_kT[4 + kv * 2 + g], [self.nsa_kT.b], [x.b])
                    pH = self.ps[it % 2]
                    for l in range(32):
                        self.mm(pH[0:64, 0:n_cmp], w1[:, kv, l, :], x[:, l:l + 16 * (n_cmp - 1) + 1:16],
                                l == 0, False, [w1.b, x.b], [pH.b])
                        self.mm(pH[0:64, 511:512], w1[:, kv, l, :], peT[:, kv, l:l + 1], False, l == 31,
                                [w1.b, peT.b], [pH.b])
                    self.cp('dve', bias[:], pH[0:64, 511:512], [pH.b], [bias.b])
                    self.act(hid[:, 0:n_cmp], pH[0:64, 0:n_cmp], AF.Silu, [pH.b, bias.b], [hid.b], bias=bias[:, 0:1])
                    for kt in range(nct):
                        pO = self.ps[2 + kt % 2]
                        self.mm(pO[:, 0:64], hid[:, kt * 128:(kt + 1) * 128], w2[:, kv, :], True, True,
                                [hid.b, w2.b], [pO.b])
                        if kv == 1:
                            self.cp('act', vcmp[:, g, kt, 0:64], pO[:, 0:64], [pO.b], [vcmp.b])
                        else:
                            self.cp('act', ctm[:], pO[:, 0:64], [pO.b], [ctm.b])
                            self.rmsn(ctm[:].rearrange("p (h c) -> p h c", h=1), 1, 64, g_kc,
                                      ctn[:].rearrange("p (h c) -> p h c", h=1), [ctm.b], [ctn.b])
                            nrow = min(128, n_cmp - kt * 128)
                            r0 = 31 + 16 * 128 * kt
                            self.memset('dve', rp[:], 0.0, [rp.b])
                            self.dma('sp', rp[0:nrow, :], self.rope_d[r0:r0 + 16 * (nrow - 1) + 1:16, :],
                                     [self.rope_d.b], [rp.b])
                            A, Bm = self.sc_ra, self.sc_rb
                            c2 = ctn[:].rearrange("p (two c) -> p two c", two=2)
                            o2 = ctb[:].rearrange("p (two c) -> p two c", two=2)
                            Av = A[:, 0:64].rearrange("p (two c) -> p two c", two=2)
                            Bv = Bm[:, 0:64].rearrange("p (two c) -> p two c", two=2)
                            self.tt('dve', Av, c2, rp[:, 0:32].unsqueeze(1).to_broadcast([128, 2, 32]), ALU.mult,
                                    [ctn.b, rp.b], [A.b])
                            self.tt('dve', Bv, c2, rp[:, 32:64].unsqueeze(1).to_broadcast([128, 2, 32]), ALU.mult,
                                    [ctn.b, rp.b], [Bm.b])
                            self.tt('dve', o2[:, 0, :], Av[:, 0, :], Bv[:, 1, :], ALU.subtract, [A.b, Bm.b], [ctb.b])
                            self.tt('dve', o2[:, 1, :], Bv[:, 0, :], Av[:, 1, :], ALU.add, [A.b, Bm.b], [ctb.b])
                            pT = self.ps[4]
                            pTb = pT[:].bitcast(BF16)
                            self.tr(pTb[0:64, 0:128], ctb[:], self.ident[:], [ctb.b, self.ident.b], [pT.b])
                            self.cp('act', kcmpT[:, g, kt * 128:(kt + 1) * 128], pTb[0:64, 0:128], [pT.b], [kcmpT.b])
            for g in range(2):
                for kt in range(nct):
                    self.memset('pool', vcmp[:, g, kt, 64:65], 1.0, [vcmp.b])
                    self.cp('pool', vcmp[:, g, kt, 65:65 + self.n_blk], ovl[:, kt, :], [ovl.b], [vcmp.b])
            self.S.barrier()

    def nsa_attn(self, es, kcmpT, vcmp):
        T, NT = self.T, self.NT
        nb = self.n_blk
        nct = self.ncmp_pad // 128
        dvc = 65 + nb
        ksT = self.sb(es, 'ksT', [64, T], BF16)
        kwT = self.sb(es, 'kwT', [64, T], BF16)
        vs = self.sb(es, 'vs', [128, NT, 65], BF16)
        vw = self.sb(es, 'vw', [128, NT, 65], BF16)
        qt_ = [self.sb(es, f'nq{i}', [64, 512], BF16) for i in range(2)]
        cneg = [self.sb(es, f'cneg{i}', [128, self.ncmp_pad], BF16) for i in range(2)]
        forced = [self.sb(es, f'forced{i}', [128, nb], F32) for i in range(2)]
        negm = [self.sb(es, f'negm{i}', [128, T], BF16) for i in range(2)]
        rec = self.sb(es, 'nrec', [128, 4], F32)
        score = self.sb(es, 'nscore', [128, nb], F32)
        work = self.sb(es, 'nwork', [128, nb], F32)
        m8 = self.sb(es, 'nm8', [128, 8], F32)
        thr = self.sb(es, 'nthr', [128, 1], F32)
        nsel = self.sb(es, 'nsel', [128, nb], BF16)
        ost = [self.sb(es, f'nost{i}', [128, 3, 4, 64], F32) for i in range(2)]
        accA, accB, accS, accW = self.ps[3], self.ps[4], self.ps[5], self.ps[6]
        cmp_eid = {}

        def stage1(g, qt, sl):
            q, cn, fo, nm, o = qt_[sl], cneg[sl], forced[sl], negm[sl], ost[sl]
            tsl = slice(qt * 128, (qt + 1) * 128)
            self.dma('sp', q[:], self.nsa_qT[g, qt].rearrange("d r t -> d (r t)"), [self.nsa_qT.b], [q.b])
            self.dma('sp', cn[:], self.c['cmpneg'][tsl, :], [], [cn.b])
            self.dma('sp', fo[:], self.c['forced'][tsl, :], [], [fo.b])
            views = {j: ((accA if j < 2 else accB)[:, (j % 2) * dvc:(j % 2 + 1) * dvc], j // 2) for j in range(4)}
            tiles = []
            for kt in range(nct):
                if 16 * 128 * kt + 31 > qt * 128 + 127:
                    continue
                tiles.append(dict(kT=kcmpT[:, g, kt * 128:(kt + 1) * 128], V=vcmp[:, g, kt, 0:dvc],
                                  R=[kcmpT.b, vcmp.b], c0=0, subs=[0, 1, 2, 3],
                                  masks=[(cn[:, kt * 128:(kt + 1) * 128], self.i4[:], 0, 512, [cn.b, self.i4.b])]))
            if not tiles:
                tiles.append(dict(kT=kcmpT[:, g, 0:128], V=vcmp[:, g, 0, 0:dvc], R=[kcmpT.b, vcmp.b], c0=0,
                                  subs=[0, 1, 2, 3],
                                  masks=[(cn[:, 0:128], self.i4[:], 0, 512, [cn.b, self.i4.b])]))
            cmp_tiles = tiles
            viewsW = {j: (accW[:, j * 65:(j + 1) * 65], 0) for j in range(4)}
            tiles = []
            for kt in range(max(0, qt - 4), qt + 1):
                tl = dict(kT=kwT[:, kt * 128:(kt + 1) * 128], V=vw[:, kt, :], R=[kwT.b, vw.b], c0=0,
                          subs=[0, 1, 2, 3], masks=[])
                if kt == qt:
                    tl['masks'].append((self.tri_neg[:], self.i4[:], 0, 512, [self.tri_neg.b, self.i4.b]))
                if kt == qt - 4:
                    tl['masks'].append((self.edge_neg[:], self.i4[:], 0, 512, [self.edge_neg.b, self.i4.b]))
                tiles.append(tl)
            win_tiles = tiles

            def epi_cmp():
                for j in range(4):
                    ab = accA if j < 2 else accB
                    v_ = views[j][0]
                    self.ts('dve', rec[:, j:j + 1], v_[:, 64:65], 1e-30, ALU.max, [ab.b], [rec.b])
                self.op_('dve', lambda e: e.reciprocal(out=rec[:], in_=rec[:]), reads=[rec.b], writes=[rec.b])
                for j in range(4):
                    ab = accA if j < 2 else accB
                    v_ = views[j][0]
                    self.ts('dve', o[:, 0, j, :], v_[:, 0:64], rec[:, j:j + 1], ALU.mult, [ab.b, rec.b], [o.b])
                    if j == 0:
                        self.ts('dve', score[:], v_[:, 65:65 + nb], rec[:, 0:1], ALU.mult, [ab.b, rec.b], [score.b])
                    else:
                        self.stt(score[:], v_[:, 65:65 + nb], rec[:, j:j + 1], score[:], ALU.mult, ALU.add,
                                 [ab.b, rec.b, score.b], [score.b])
                self.tt('dve', score[:], score[:], fo[:], ALU.add, [score.b, fo.b], [score.b])
                self.op_('dve', lambda e: e.max(out=m8[:], in_=score[:]), reads=[score.b], writes=[m8.b])
                self.op_('dve', lambda e: e.match_replace(out=work[:], in_to_replace=m8[:], in_values=score[:],
                                                           imm_value=-3e38), reads=[score.b, m8.b], writes=[work.b])
                self.op_('dve', lambda e: e.max(out=m8[:], in_=work[:]), reads=[work.b], writes=[m8.b])
                self.ts('dve', thr[:], m8[:, 7:8], -1e29, ALU.max, [m8.b], [thr.b])
                self.ts('dve', nsel[:], score[:], thr[:, 0:1], ALU.is_lt, [score.b, thr.b], [nsel.b], s2=-BIG, op1=ALU.mult)
                nblk_need = (qt + 1) * 2
                self.cp('pool', nm[:, 0:nblk_need * 64].rearrange("p (j c) -> p j c", c=64),
                        nsel[:, 0:nblk_need].unsqueeze(2).to_broadcast([128, nblk_need, 64]), [nsel.b], [nm.b])
                self.tt('pool', nm[:, tsl], nm[:, tsl], self.tri_neg[:], ALU.add, [nm.b, self.tri_neg.b], [nm.b])

            def epi_win():
                av = accW[:, 0:260].rearrange("p (j c) -> p j c", j=4)
                self.op_('dve', lambda e, av=av: e.reciprocal(out=rec[:], in_=av[:, :, 64]), reads=[accW.b], writes=[rec.b])
                self.tt('dve', o[:, 2], av[:, :, 0:64], rec[:].unsqueeze(2).to_broadcast([128, 4, 64]), ALU.mult,
                        [accW.b, rec.b], [o.b])

            eid = self.attn_block(q[:], [q.b], 512, cmp_tiles, 0.125, views, [accA.b, accB.b], epilogue=epi_cmp)
            self.attn_block(q[:], [q.b], 512, win_tiles, 0.125, viewsW, [accW.b], epilogue=epi_win)
            cmp_eid[(g, qt)] = eid

        def stage2(g, qt, sl):
            q, nm, o = qt_[sl], negm[sl], ost[sl]
            tsl = slice(qt * 128, (qt + 1) * 128)
            self.attn_sync(cmp_eid[(g, qt)])
            views = {j: (accS[:, j * 65:(j + 1) * 65], 0) for j in range(4)}
            tiles = [dict(kT=ksT[:, kt * 128:(kt + 1) * 128], V=vs[:, kt, :], R=[ksT.b, vs.b], c0=0,
                          subs=[0, 1, 2, 3],
                          masks=[(nm[:, kt * 128:(kt + 1) * 128], self.i4[:], 0, 512, [nm.b, self.i4.b])])
                     for kt in range(qt + 1)]
            def epi_sel():
                av = accS[:, 0:260].rearrange("p (j c) -> p j c", j=4)
                self.op_('dve', lambda e, av=av: e.reciprocal(out=rec[:], in_=av[:, :, 64]), reads=[accS.b], writes=[rec.b])
                self.tt('dve', o[:, 1], av[:, :, 0:64], rec[:].unsqueeze(2).to_broadcast([128, 4, 64]), ALU.mult,
                        [accS.b, rec.b], [o.b])
                for bi, base in enumerate((1280, 512, 1792)):
                    self.dma('sp', self.Y[tsl, base + g * 256:base + (g + 1) * 256],
                             o[:, bi].rearrange("p r c -> p (r c)"), [o.b], [self.Y.b])
            self.attn_block(q[:], [q.b], 512, tiles, 0.125, views, [accS.b], epilogue=epi_sel)

        for g in range(2):
            self.dma('sp', ksT[:], self.nsa_kT[0 + g], [self.nsa_kT.b], [ksT.b])
            self.dma('sp', kwT[:], self.nsa_kT[2 + g], [self.nsa_kT.b], [kwT.b])
            self.dma('sp', vs[:], self.nsa_v[0 + g], [self.nsa_v.b], [vs.b])
            self.dma('sp', vw[:], self.nsa_v[2 + g], [self.nsa_v.b], [vw.b])
            stage1(g, 0, 0)
            for qt in range(NT):
                if qt + 1 < NT:
                    stage1(g, qt + 1, (qt + 1) % 2)
                stage2(g, qt, qt % 2)
            self.attn_flush()

    def layer_odd(self, es, L, x_src, x_dst):
        li = L // 2
        w = self.w
        T, NT = self.T, self.NT
        S = self.S
        memkT, memV = self.mem_kv(es, L)
        with ExitStack() as p1:
            win, lng = self.load_win(p1, L, w['odd_w_in'][li], ODD_COLS)
            g_cq = self.bc_load(p1, 'g_cq', w['dsa_q_norm_g'][li:li + 1, :], 64)
            g_ck = self.bc_load(p1, 'g_ck', w['dsa_k_norm_g'][li:li + 1, :], 64)
            g_mq = self.bc_load(p1, 'g_mq', w['mem_q_norm_g'][L:L + 1, :], 64)
            xt = [self.sb(p1, f'xt{i}', [128, D], F32) for i in range(2)]
            ht = [self.sb(p1, f'ht{i}', [128, D], BF16) for i in range(2)]
            hT = [self.sb(p1, f'hT{i}', [128, 8, 128], BF16) for i in range(2)]
            u = [self.sb(p1, f'u{i}', [128, ODD_COLS], F32) for i in range(2)]
            ss = self.sb(p1, 'ss', [128, 1], F32)
            junk = self.sb(p1, 'junk', [128, D], BF16)
            gt = self.sb(p1, 'gt', [128, 1792], BF16)
            cqn = self.sb(p1, 'cqn', [128, 512], F32)
            cqf = self.sb(p1, 'cqf', [128, 512], BF16)
            ckn = self.sb(p1, 'ckn', [128, 64], F32)
            ckf = self.sb(p1, 'ckf', [128, 64], BF16)
            cva = self.sb(p1, 'cva', [128, 65], BF16)
            self.memset('pool', cva[:], 1.0, [cva.b])
            iqf = self.sb(p1, 'iqf', [128, 256], BF16)
            ikf = self.sb(p1, 'ikf', [128, 32], BF16)
            iwt = self.sb(p1, 'iwt', [128, 8], F32)
            mlv = self.sb(p1, 'mlv', [128, 4, 129], BF16)
            self.memset('pool', mlv[:], 1.0, [mlv.b])
            mqf = self.sb(p1, 'mqf', [128, 256], BF16)
            stq = self.sb(p1, 'stq', [64, 8, 128], BF16)
            stk = self.sb(p1, 'stk', [64, 1, 128], BF16)
            sti = self.sb(p1, 'sti', [32, 8, 128], BF16)
            stik = self.sb(p1, 'stik', [32, 1, 128], BF16)
            stmq = self.sb(p1, 'stmq', [64, 4, 128], BF16)
            strw = self.sb(p1, 'strw', [128, 4, 128], F32)
            stif = self.sb(p1, 'stif', [8, 128], F32)
            scrA = self.new_scr(p1, 'A', 512)
            scrB = self.new_scr(p1, 'B', 256)
            ut_next = self.p1_front(0, x_src, xt, ht, hT, u, ss, junk, lng, win, ODD_COLS)
            for n in range(NT):
                ut = ut_next
                U = lambda a, b_, ut=ut: ut[:, a:b_]
                ub = [ut.b]
                tsl = slice(n * 128, (n + 1) * 128)
                self.chains_begin(['A', 'F', 'B', 'L', 'M', 'G'])
                if n + 1 < NT:
                    self.chain('F')
                    ut_next = self.p1_front(n + 1, x_src, xt, ht, hT, u, ss, junk, lng, win, ODD_COLS)
                self.chain('G')
                self.act(gt[:, 0:512], U(936, 1448), AF.Silu, ub, [gt.b])
                self.act(gt[:, 512:1024], U(2992, 3504), AF.Silu, ub, [gt.b])
                self.act(gt[:, 1024:1280], U(3760, 4016), AF.Silu, ub, [gt.b])
                self.act(gt[:, 1280:1792], U(2480, 2992), AF.Sigmoid, ub, [gt.b])
                self.dma('sp', self.G[tsl, :], gt[:], [gt.b], [self.G.b[n]])
                self.chain('A', scrA)
                cq3 = cqn[:].rearrange("p (h c) -> p h c", h=8)
                self.rmsn(U(0, 512).rearrange("p (h c) -> p h c", h=8), 8, 64, g_cq, cq3, ub, [cqn.b])
                self.rope(cq3, 8, 32, n, cqf[:].rearrange("p (h c) -> p h c", h=8), [cqn.b], [cqf.b])
                ck3 = ckn[:].rearrange("p (h c) -> p h c", h=1)
                self.rmsn(U(512, 576).rearrange("p (h c) -> p h c", h=1), 1, 64, g_ck, ck3, ub, [ckn.b])
                self.rope(ck3, 1, 32, n, ckf[:].rearrange("p (h c) -> p h c", h=1), [ckn.b], [ckf.b])
                self.cp('dve', cva[:, 0:64], U(576, 640), ub, [cva.b])
                self.dma('sp', self.dsa_v[:, n, :], cva[:], [cva.b], [self.dsa_v.b])
                pT = self.ps[4]
                pTb = pT[:].bitcast(BF16)
                for h in range(8):
                    self.tr(pTb[0:64, h * 128:(h + 1) * 128], cqf[:, h * 64:(h + 1) * 64], self.ident[:],
                            [cqf.b, self.ident.b], [pT.b])
                self.cp('act', stq[:], pTb[0:64, 0:1024].rearrange("p (k c) -> p k c", k=8), [pT.b], [stq.b])
                self.dma('sp', self.dsa_qT[n], stq[:], [stq.b], [self.dsa_qT.b])
                self.transposes_out([(ckf[:], [ckf.b])], 64, stk,
                                    self.dsa_kT[:, tsl].rearrange("d (k t) -> d k t", k=1), [self.dsa_kT.b], 5)
                self.chain('B', scrB)
                self.rope(U(640, 896).rearrange("p (h c) -> p h c", h=8), 8, 16, n,
                          iqf[:].rearrange("p (h c) -> p h c", h=8), ub, [iqf.b])
                self.rope(U(896, 928).rearrange("p (h c) -> p h c", h=1), 1, 16, n,
                          ikf[:].rearrange("p (h c) -> p h c", h=1), ub, [ikf.b])
                self.ts('dve', iwt[:], U(928, 936), 8 ** -0.5, ALU.mult, ub, [iwt.b])
                self.dma('sp', self.idx_w[tsl, :], iwt[:], [iwt.b], [self.idx_w.b])
                self.transposes_out([(iqf[:, h * 32:(h + 1) * 32], [iqf.b]) for h in range(8)], 32, sti,
                                    self.idx_qT[:, :, tsl].rearrange("h d t -> d h t"), [self.idx_qT.b], 6)
                self.transposes_out([(ikf[:], [ikf.b])], 32, stik,
                                    self.idx_kT[:, tsl].rearrange("d (k t) -> d k t", k=1), [self.idx_kT.b], 7)
                self.chain('L')
                pR = self.ps[3]
                for k in range(4):
                    self.tr(pR[:, k * 128:(k + 1) * 128], U(1448 + k * 128, 1448 + (k + 1) * 128), self.identf[:],
                            ub + [self.identf.b], [pR.b])
                self.cp('act', strw[:].rearrange("p k c -> p (k c)"), pR[:, 0:512], [pR.b], [strw.b])
                self.dma('sp', self.ml_raw[:, tsl].rearrange("(k p) t -> p k t", p=128), strw[:], [strw.b],
                         [self.ml_raw.b])
                pI = self.ps[3]
                self.tr(pI[0:8, 0:128], U(2472, 2480), self.identf[:], ub + [self.identf.b], [pI.b])
                self.cp('act', stif[:], pI[0:8, 0:128], [pI.b], [stif.b])
                self.dma('sp', self.ml_if[:, tsl], stif[:], [stif.b], [self.ml_if.b])
                self.cp('dve', mlv[:, :, 0:128], U(1960, 2472).rearrange("p (h c) -> p h c", h=4), ub, [mlv.b])
                self.dma('sp', self.ml_v[:, :, n, :].rearrange("h p c -> p h c"), mlv[:], [mlv.b], [self.ml_v.b])
                self.chain('B', scrB)
                self.rmsn(U(3504, 3760).rearrange("p (h c) -> p h c", h=4), 4, 64, g_mq,
                          mqf[:].rearrange("p (h c) -> p h c", h=4), ub, [mqf.b])
                self.transposes_out([(mqf[:, h * 64:(h + 1) * 64], [mqf.b]) for h in range(4)], 64, stmq,
                                    self.mem_qT[:, :, tsl].rearrange("h d t -> d h t"), [self.mem_qT.b], 6)
                self.chains_emit()
            S.barrier()
        if 'stop_p1' in self.dbg:
            return
        self.mlstm_pre(li)
        S.barrier()
        if 'stop_pre' in self.dbg:
            return
        with ExitStack() as pa:
            self.attn_setup(pa)
            self.mlstm_attn(pa)
            S.barrier()
        if 'stop_ml' in self.dbg:
            return
        with ExitStack() as pa:
            self.attn_setup(pa)
            self.mem_attn(pa, memkT, memV)
            S.barrier()
        if 'stop_mem' in self.dbg:
            return
        with ExitStack() as pa:
            self.attn_setup(pa)
            self.dsa_attn(pa)
            S.barrier()
        if 'stop_dsa' in self.dbg:
            return
        with ExitStack() as p3:
            wout = self.load_wout(p3, L)
            g_h = self.bc_load(p3, 'g_h', w['mlstm_h_norm_g'][li:li + 1, :], 128)
            yt = [self.sb(p3, f'yt{i}', [128, 1280], F32) for i in range(2)]
            gtt = [self.sb(p3, f'gtt{i}', [128, 1792], BF16) for i in range(2)]
            mix = [self.sb(p3, f'mix{i}', [128, 1280], BF16) for i in range(2)]
            hns = [self.sb(p3, f'hn{i}', [128, 512], F32) for i in range(2)]
            scrP = [self.new_scr(p3, f'P{i}', 512) for i in range(2)]

            def mixfn(n):
                sl = n % 2
                y, g, m = yt[sl], gtt[sl], mix[sl]
                hn = hns[sl]
                self.scr = scrP[sl]
                tsl = slice(n * 128, (n + 1) * 128)
                self.dma('sp', y[:], self.Y[tsl, 0:1280], [self.Y.b], [y.b])
                self.dma('sp', g[:], self.G[tsl, :], [self.G.b[n]], [g.b])
                self.tt('dve', m[:, 0:512], y[:, 0:512], g[:, 0:512], ALU.mult, [y.b, g.b], [m.b])
                self.tt('pool', m[:, 1024:1280], y[:, 1024:1280], g[:, 1024:1280], ALU.mult, [y.b, g.b], [m.b])
                self.rmsn(y[:, 512:1024].rearrange("p (h c) -> p h c", h=4), 4, 128, g_h,
                          hn[:].rearrange("p (h c) -> p h c", h=4), [y.b], [hn.b])
                self.tt('dve', hn[:], hn[:], g[:, 1280:1792], ALU.mult, [hn.b, g.b], [hn.b])
                self.tt('dve', m[:, 512:1024], hn[:], g[:, 512:1024], ALU.mult, [hn.b, g.b], [m.b])
                return m
            self.p3(p3, L, x_src, x_dst, wout, mixfn)
            S.barrier()

    def mlstm_pre(self, li):
        w = self.w
        T = self.T
        with ExitStack() as es:
            xp = [self.sb(es, f'xp{i}', [128, T + 3], F32) for i in range(2)]
            y = self.sb(es, 'cy', [128, T], F32)
            yo = [self.sb(es, f'cyo{i}', [128, T], BF16) for i in range(2)]
            wc = self.sb(es, 'cwc', [128, 4, 4], F32)
            bc = self.sb(es, 'cbc', [128, 4], F32)
            for ck in range(4):
                self.dma('sp', wc[:, ck, :], w['mlstm_conv_wT'][li, ck * 128:(ck + 1) * 128, :], [], [wc.b])
                self.dma('sp', bc[:, ck:ck + 1], w['mlstm_conv_b'][li, ck * 128:(ck + 1) * 128].unsqueeze(1), [], [bc.b])
            for ck in range(4):
                x = xp[ck % 2]
                o = yo[ck % 2]
                self.memset('pool', x[:, 0:3], 0.0, [x.b])
                self.dma('sp', x[:, 3:T + 3], self.ml_raw[ck * 128:(ck + 1) * 128, :], [self.ml_raw.b], [x.b])
                self.ts('dve', y[:], x[:, 0:T], wc[:, ck, 0:1], ALU.mult, [x.b, wc.b, bc.b], [y.b],
                        s2=bc[:, ck:ck + 1], op1=ALU.add)
                for j in range(1, 4):
                    self.stt(y[:], x[:, j:j + T], wc[:, ck, j:j + 1], y[:], ALU.mult, ALU.add, [x.b, wc.b, y.b], [y.b])
                self.act(o[:], y[:], AF.Silu, [y.b], [o.b])
                self.dma('sp', self.ml_qkT[ck * 128:(ck + 1) * 128, :], o[:], [o.b], [self.ml_qkT.b])
            self.S.barrier()
        with ExitStack() as es:
            ig = self.sb(es, 'ig', [4, T], F32)
            fg = self.sb(es, 'fg', [4, T], F32)
            cs = self.sb(es, 'cs', [4, T], F32)
            a = self.sb(es, 'ga', [4, T], F32)
            Mt = self.sb(es, 'gM', [4, T], F32)
            ones = self.sb(es, 'gones', [4, T], F32)
            ib = self.sb(es, 'gib', [4, 1], F32)
            fb = self.sb(es, 'gfb', [4, 1], F32)
            self.memset('pool', ones[:], 1.0, [ones.b])
            self.dma('sp', ig[:], self.ml_if[0:4, :], [self.ml_if.b], [ig.b])
            self.dma('sp', fg[:], self.ml_if[4:8, :], [self.ml_if.b], [fg.b])
            self.dma('sp', ib[:], w['mlstm_i_bias'][li].unsqueeze(1), [], [ib.b])
            self.dma('sp', fb[:], w['mlstm_f_bias'][li].unsqueeze(1), [], [fb.b])
            self.ts('dve', fb[:], fb[:], -1.0, ALU.mult, [fb.b], [fb.b])
            self.act(fg[:], fg[:], AF.Exp, [fg.b, fb.b], [fg.b], bias=fb[:, 0:1], scale=-1.0)
            self.act(fg[:], fg[:], AF.Ln, [fg.b], [fg.b], bias=self.one_c[0:4, 0:1], scale=1.0)
            self.op_('dve', lambda e: e.tensor_tensor_scan(out=cs[:], data0=ones[:], data1=fg[:], initial=0.0,
                                                             op0=ALU.mult, op1=ALU.add),
                      reads=[ones.b, fg.b], writes=[cs.b])
            self.stt(a[:], ig[:], ib[:, 0:1], cs[:], ALU.add, ALU.add, [ig.b, ib.b, cs.b], [a.b])
            self.op_('dve', lambda e: e.tensor_tensor_scan(out=Mt[:], data0=a[:], data1=a[:], initial=0.0,
                                                             op0=ALU.max, op1=ALU.max),
                      reads=[a.b], writes=[Mt.b])
            self.tt('dve', cs[:], cs[:], Mt[:], ALU.subtract, [cs.b, Mt.b], [cs.b])
            self.act(cs[:], cs[:], AF.Exp, [cs.b], [cs.b])
            self.ts('dve', Mt[:], Mt[:], -1.0, ALU.mult, [Mt.b], [Mt.b])
            self.dma('sp', self.ml_g[0:4, :], a[:], [a.b], [self.ml_g.b])
            self.dma('sp', self.ml_g[4:8, :], Mt[:], [Mt.b], [self.ml_g.b])
            self.dma('sp', self.ml_g[8:12, :], cs[:], [cs.b], [self.ml_g.b])
            self.S.barrier()

    def mlstm_attn(self, es):
        T, NT = self.T, self.NT
        qT = [self.sb(es, f'lqT{i}', [64, T], BF16) for i in range(2)]
        kT = [self.sb(es, f'lkT{i}', [64, T], BF16) for i in range(2)]
        V = [self.sb(es, f'lV{i}', [128, NT, 129], BF16) for i in range(2)]
        nM = [self.sb(es, f'lnM{i}', [128, T], F32) for i in range(2)]
        ant = self.sb(es, 'lant', [NT, 2, 128], F32)
        atm = [self.sb(es, f'latm{i}', [128, 2, NT], F32) for i in range(2)]
        Et = [self.sb(es, f'lEt{i}', [128, 512], F32) for i in range(3)]
        ost = [self.sb(es, f'lost{i}', [128, 4, 128], F32) for i in range(2)]
        d2 = self.sb(es, 'ld2', [128, 4], F32)
        ecnt = [0]
        blk = 0
        LN8 = math.log(0.125)
        for h in range(4):
            q, k, v, nm, at = qT[h % 2], kT[h % 2], V[h % 2], nM[h % 2], atm[h % 2]
            self.dma('sp', q[:], self.ml_qkT[h * 64:(h + 1) * 64, :], [self.ml_qkT.b], [q.b])
            self.dma('sp', k[:], self.ml_qkT[256 + h * 64:256 + (h + 1) * 64, :], [self.ml_qkT.b], [k.b])
            self.dma('sp', v[:], self.ml_v[h], [self.ml_v.b], [v.b])
            self.dma('sp', nm[:], self.ml_g[4 + h:5 + h, :].to_broadcast([128, T]), [self.ml_g.b], [nm.b])
            self.dma('sp', ant[:, 0, :], self.ml_g[h, :].rearrange("(n p) -> n p", p=128), [self.ml_g.b], [ant.b])
            self.dma('sp', ant[:, 1, :], self.ml_g[8 + h, :].rearrange("(n p) -> n p", p=128), [self.ml_g.b], [ant.b])
            pA = self.ps[7]
            for i in range(2):
                self.tr(pA[:, i * NT:(i + 1) * NT], ant[:, i, :], self.identf[0:NT, 0:NT], [ant.b, self.identf.b], [pA.b])
            self.cp('act', at[:].rearrange("p a n -> p (a n)"), pA[:, 0:2 * NT], [pA.b], [at.b])
            self.ts('dve', at[:, 0, :], at[:, 0, :], LN8, ALU.add, [at.b], [at.b])
            for cq in range(T // 512):
                set_i = blk % 2
                blk += 1
                accA, accB = self.ps[3 + 2 * set_i], self.ps[4 + 2 * set_i]
                views = {j: ((accA if j < 2 else accB)[:, (j % 2) * 129:(j % 2 + 1) * 129], j // 2) for j in range(4)}
                tiles = []
                for kt in range(4 * cq + 4):
                    vv = kt - 4 * cq
                    c0 = max(0, vv) * 128

                    def efn(kt=kt, c0=c0, cq=cq, nm=nm, at=at):
                        E = Et[ecnt[0] % 3]
                        ecnt[0] += 1
                        self.act(E[:, c0:512], nm[:, cq * 512 + c0:(cq + 1) * 512], AF.Exp, [nm.b, at.b], [E.b],
                                 bias=at[:, 0, kt:kt + 1], scale=1.0)
                        return E, [E.b]
                    tl = dict(kT=k[:, kt * 128:(kt + 1) * 128], V=v[:, kt, :], R=[k.b, v.b], c0=c0,
                              subs=list(range(max(0, vv), 4)), Efn=efn)
                    if vv >= 0:
                        tl['diag'] = vv
                    tiles.append(tl)
                def epi(accA=accA, accB=accB, views=views, o=ost[set_i], cq=cq, h=h, at=at):
                    for j in range(4):
                        ab = accA if j < 2 else accB
                        v_ = views[j][0]
                        self.act(d2[:, j:j + 1], v_[:, 128:129], AF.Abs, [ab.b], [d2.b])
                    self.tt('dve', d2[:], d2[:], at[:, 1, 4 * cq:4 * cq + 4], ALU.max, [d2.b, at.b], [d2.b])
                    self.op_('dve', lambda e: e.reciprocal(out=d2[:], in_=d2[:]), reads=[d2.b], writes=[d2.b])
                    for j in range(4):
                        ab = accA if j < 2 else accB
                        v_ = views[j][0]
                        self.ts('dve', o[:, j, :], v_[:, 0:128], d2[:, j:j + 1], ALU.mult, [ab.b, d2.b], [o.b])
                    self.dma('sp', self.Y[cq * 512:(cq + 1) * 512, 512 + h * 128:512 + (h + 1) * 128]
                             .rearrange("(j p) c -> p j c", p=128), o[:], [o.b], [self.Y.b])
                self.attn_block(q[:, cq * 512:(cq + 1) * 512], [q.b], 512, tiles, 1.0, views, [accA.b, accB.b],
                                mode='mul', epilogue=epi)
        self.attn_flush()

    def dsa_attn(self, es):
        T, NT = self.T, self.NT
        KSEL = min(256, T // 4)
        ikT = self.sb(es, 'ikT', [32, T], BF16)
        ckT = self.sb(es, 'ckT', [64, T], BF16)
        cv = self.sb(es, 'cv', [128, NT, 65], BF16)
        self.dma('sp', ikT[:], self.idx_kT[:, :], [self.idx_kT.b], [ikT.b])
        self.dma('sp', ckT[:], self.dsa_kT[:, :], [self.dsa_kT.b], [ckT.b])
        self.dma('sp', cv[:], self.dsa_v[:, :, :], [self.dsa_v.b], [cv.b])
        iq = [self.sb(es, f'iq{i}', [32, 8, 128], BF16) for i in range(2)]
        iw = [self.sb(es, f'iw{i}', [128, 8], F32) for i in range(2)]
        cq_ = [self.sb(es, f'cq{i}', [64, 2, 512], BF16) for i in range(2)]
        score2 = [self.sb(es, f'dscore{i}', [128, T], F32) for i in range(3)]
        thrA2 = [self.sb(es, f'dthrA{i}', [128, 1], F32) for i in range(2)]
        work = self.sb(es, 'dwork', [128, T], F32)
        negm = [self.sb(es, f'dnegm{i}', [128, T], BF16) for i in range(2)]
        rl = [self.sb(es, f'drl{i}', [128, 512], F32) for i in range(3)]
        m8 = self.sb(es, 'dm8', [128, 8], F32)
        thr = self.sb(es, 'dthr', [128, 1], F32)
        rec = self.sb(es, 'drec', [128, 4], F32)
        ost = [self.sb(es, f'dost{i}', [128, 4, 64], F32) for i in range(2)]
        rlc_ = [0]
        blk_ = [0]
        NBIS = 20
        junk = self.sb(es, 'djunk', [128, T], BF16)
        amax = self.sb(es, 'damax', [128, 1], F32)
        w0 = self.sb(es, 'dw0', [128, 1], F32)
        nHh = self.sb(es, 'dnHh', [128, 40], F32)
        nmid = [self.sb(es, f'dnmid{i}', [128, 1], F32) for i in range(2)]
        Ssum = self.sb(es, 'dS', [128, 1], F32)
        tsg = self.sb(es, 'dtsg', [128, 1], F32)

        def stage_a1(qt):
            rlc = rlc_[0]
            sl = qt % 2
            tsl = slice(qt * 128, (qt + 1) * 128)
            q_i, w_i = iq[sl], iw[sl]
            score = score2[qt % 3]
            self.dma('sp', q_i[:], self.idx_qT[:, :, tsl].rearrange("h d t -> d h t"), [self.idx_qT.b], [q_i.b])
            self.dma('sp', w_i[:], self.idx_w[tsl, :], [self.idx_w.b], [w_i.b])
            ncols = (qt + 1) * 128
            for c in range((ncols + 511) // 512):
                c0 = c * 512
                wd = min(512, ncols - c0)
                for h in range(8):
                    bi = self.st_rr % len(self.st_banks)
                    self.st_rr += 1
                    bank, bb = self.st_banks[bi]
                    self.mm(bank[:, 0:wd], q_i[:, h, :], ikT[:, c0:c0 + wd], True, True, [q_i.b, ikT.b], [bb])
                    if h == 0:
                        self.ts('dve', score[:, c0:c0 + wd], bank[:, 0:wd], 0.0, ALU.max, [bb, w_i.b], [score.b],
                                s2=w_i[:, 0:1], op1=ALU.mult)
                    else:
                        r = rl[rlc % 3]
                        rlc += 1
                        self.ts('dve', r[:, 0:wd], bank[:, 0:wd], 0.0, ALU.max, [bb, w_i.b], [r.b],
                                s2=w_i[:, h:h + 1], op1=ALU.mult)
                        self.tt('dve', score[:, c0:c0 + wd], score[:, c0:c0 + wd], r[:, 0:wd], ALU.add,
                                [score.b, r.b], [score.b])
            rlc_[0] = rlc

        def is_act_tile(qt):
            return ((qt + 1) * 128 > KSEL) and ('dsa_nobis' not in self.dbg)

        def stage_a2_finish(qt):
            sl = qt % 2
            nm = negm[sl]
            score = score2[qt % 3]
            ncols = (qt + 1) * 128
            th = thrA2[qt % 2] if is_act_tile(qt) else thr
            self.ts('dve', nm[:, 0:ncols], score[:, 0:ncols], th[:, 0:1], ALU.is_lt, [score.b, th.b], [nm.b],
                    s2=-BIG, op1=ALU.mult)

        def stage_a2(qt):
            sl = qt % 2
            tsl = slice(qt * 128, (qt + 1) * 128)
            q_c, nm = cq_[sl], negm[sl]
            score = score2[qt % 3]
            ncols = (qt + 1) * 128
            for half in range(2):
                self.dma('sp', q_c[:, half, :], self.dsa_qT[qt, :, half * 4:(half + 1) * 4, :].rearrange("d h t -> d (h t)"),
                         [self.dsa_qT.b], [q_c.b])
            use_act = is_act_tile(qt)
            if use_act:
                self.op_('dve', lambda e, ncols=ncols: e.tensor_reduce(out=amax[:], in_=score[:, 0:ncols], axis=AX.X,
                                                                      op=ALU.max, apply_absolute_value=True),
                         reads=[score.b], writes=[amax.b])
                self.ts('dve', w0[:], amax[:], 2.0, ALU.mult, [amax.b], [w0.b], s2=2.0, op1=ALU.add)
                self.ts('dve', nHh[:], self.pw[:], w0[:, 0:1], ALU.mult, [self.pw.b, w0.b], [nHh.b])
            self.tt('dve', score[:, tsl], score[:, tsl], self.trinegf[:], ALU.add, [score.b, self.trinegf.b], [score.b])
            if use_act:
                cconst = float(0.5 - (2 * KSEL - ncols - 1))
                self.memset('pool', nmid[0][:], 0.0, [nmid[0].b])
                for j in range(NBIS):
                    cur, nxt = nmid[j % 2], nmid[(j + 1) % 2]
                    self.act(junk[:, 0:ncols], score[:, 0:ncols], AF.Sign, [score.b, cur.b], [junk.b, Ssum.b],
                             bias=cur[:, 0:1], scale=1.0, accum=Ssum[:, 0:1])
                    self.act(tsg[:], Ssum[:], AF.Sign, [Ssum.b], [tsg.b], bias=cconst, scale=1.0)
                    self.act(nxt[:], tsg[:], AF.Identity, [tsg.b, cur.b, nHh.b], [nxt.b],
                             bias=cur[:, 0:1], scale=nHh[:, j:j + 1])
                fin = nmid[NBIS % 2]
                thrA = thrA2[qt % 2]
                self.act(thrA[:], fin[:], AF.Identity, [fin.b, nHh.b], [thrA.b], bias=nHh[:, NBIS - 1:NBIS], scale=-1.0)
                return
            elif ncols > KSEL and 'dsa_notopk' not in self.dbg:
                self.cp('pool', work[:, 0:ncols], score[:, 0:ncols], [score.b], [work.b])
                nr = KSEL // 8
                for r_ in range(nr):
                    self.op_('dve', lambda e, ncols=ncols: e.max(out=m8[:], in_=work[:, 0:ncols]),
                             reads=[work.b], writes=[m8.b])
                    if r_ < nr - 1:
                        self.op_('dve', lambda e, ncols=ncols: e.match_replace(
                            out=work[:, 0:ncols], in_to_replace=m8[:], in_values=work[:, 0:ncols], imm_value=-3e38),
                            reads=[work.b, m8.b], writes=[work.b])
                self.ts('dve', thr[:], m8[:, 7:8], -1e29, ALU.max, [m8.b], [thr.b])
            else:
                self.memset('dve', thr[:], -1e29, [thr.b])
            stage_a2_finish(qt)

        def stage_b(qt):
            blk = blk_[0]
            sl = qt % 2
            tsl = slice(qt * 128, (qt + 1) * 128)
            q_c, nm = cq_[sl], negm[sl]
            for half in range(2):
                set_i = blk % 2
                blk += 1
                accb = self.ps[3 + set_i]
                views = {j: (accb[:, j * 65:(j + 1) * 65], 0) for j in range(4)}
                tiles = [dict(kT=ckT[:, kt * 128:(kt + 1) * 128], V=cv[:, kt, :], R=[ckT.b, cv.b], c0=0,
                              subs=[0, 1, 2, 3],
                              masks=[(nm[:, kt * 128:(kt + 1) * 128], self.i4[:], 0, 512, [nm.b, self.i4.b])])
                         for kt in range(qt + 1)]
                def epi(accb=accb, o=ost[set_i], tsl=tsl, half=half):
                    av = accb[:, 0:260].rearrange("p (j c) -> p j c", j=4)
                    self.op_('dve', lambda e, av=av: e.reciprocal(out=rec[:], in_=av[:, :, 64]), reads=[accb.b], writes=[rec.b])
                    self.tt('dve', o[:], av[:, :, 0:64], rec[:].unsqueeze(2).to_broadcast([128, 4, 64]), ALU.mult,
                            [accb.b, rec.b], [o.b])
                    self.dma('sp', self.Y[tsl, half * 256:(half + 1) * 256], o[:].rearrange("p j c -> p (j c)"),
                             [o.b], [self.Y.b])
                self.attn_block(q_c[:, half, :], [q_c.b], 512, tiles, 0.125, views, [accb.b], epilogue=epi)
            blk_[0] = blk

        stage_a1(0)
        if NT > 1:
            stage_a1(1)
        stage_a2(0)
        for qt in range(NT):
            if qt + 2 < NT:
                stage_a1(qt + 2)
            self.chains_begin(['X', 'Y'])
            if qt + 1 < NT:
                self.chain('X')
                stage_a2(qt + 1)
            self.chain('Y')
            if is_act_tile(qt):
                stage_a2_finish(qt)
            stage_b(qt)
            self.chains_emit()
        self.attn_flush()


def host_consts(T):
    bf = ml_dtypes.bfloat16
    nb = T // 64
    ncp = max(1, T // 2048) * 128
    n_cmp = (T - 32) // 16 + 1
    c = {}
    c['c_ident'] = np.eye(128, dtype=np.float32).astype(bf)
    c['c_identf'] = np.eye(128, dtype=np.float32)
    c['c_i4'] = np.tile(np.eye(128, dtype=np.float32), (1, 4)).astype(bf)
    i8 = np.zeros((128, 512), np.float32)
    for p in range(128):
        for h in range(8):
            i8[p, h * 64 + p % 64] = 1.0
    c['c_i8x2'] = i8.astype(bf)
    t = np.arange(128)[:, None]
    s = np.arange(128)[None, :]
    c['c_tri_neg'] = np.where(s > t, -BIG, 0.0).astype(np.float32).astype(bf)
    c['c_edge_neg'] = np.where(s <= t, -BIG, 0.0).astype(np.float32).astype(bf)
    c['c_tri01T'] = np.where(t <= s, 1.0, 0.0).astype(np.float32).astype(bf)
    c['c_trinegf'] = np.where(s > t, -1e30, 0.0).astype(np.float32)
    inv = (10000.0 ** (-np.arange(32, dtype=np.float32) / 32)).astype(np.float32)
    c['c_invf'] = np.tile(inv[None, :], (128, 1)).astype(np.float32)
    c['c_pw'] = np.tile((-(2.0 ** -(np.arange(40, dtype=np.float64) + 2)))[None, :], (128, 1)).astype(np.float32)
    tt = np.arange(T)[:, None]
    n = np.arange(ncp)[None, :]
    cm = np.where((16 * n + 31 <= tt) & (n < n_cmp), 0.0, -BIG)
    c['c_cmpneg'] = cm.astype(np.float32).astype(bf)
    j = np.arange(nb)[None, :]
    cur = tt // 64
    forced = np.where(j == cur, 3e4, np.where(j == cur - 1, 2e4, np.where(j == 0, 1e4, 0.0)))
    forced = np.where(j * 64 <= tt, forced, -1e30)
    c['c_forced'] = forced.astype(np.float32)
    ni = np.arange(ncp)[:, None]
    ov = ((16 * ni <= j * 64 + 63) & (16 * ni + 31 >= j * 64) & (ni < n_cmp))
    c['c_overlap'] = ov.astype(np.float32).astype(bf)
    return c


_CACHE = {}


def make_in_maps(inputs, T, ncores):
    NT = T // 128
    consts = host_consts(T)
    wnames = ["ln_g", "mem_norm_g", "mem_w_kv", "mem_q_norm_g", "mem_k_norm_g", "w_out", "even_w_in",
              "mla_q_lat_g", "mla_kv_lat_g", "mla_w_uq", "mla_w_ukv", "mla_q_norm_g", "mla_k_norm_g",
              "nsa_q_norm_g", "nsa_k_norm_g", "nsa_cmp_w1", "nsa_cmp_w2", "odd_w_in", "dsa_q_norm_g",
              "dsa_k_norm_g", "mlstm_conv_b", "mlstm_i_bias", "mlstm_f_bias", "mlstm_h_norm_g"]
    shared = {k: np.ascontiguousarray(np.asarray(inputs[k], dtype=np.float32)) for k in wnames}
    shared["nsa_cmp_posT"] = np.ascontiguousarray(np.transpose(np.asarray(inputs["nsa_cmp_pos"], np.float32), (0, 1, 3, 2)))
    shared["mlstm_conv_wT"] = np.ascontiguousarray(np.transpose(np.asarray(inputs["mlstm_conv_w"], np.float32), (0, 2, 1)))
    shared.update(consts)
    maps = []
    for c in range(ncores):
        m = dict(shared)
        m["x"] = np.ascontiguousarray(np.asarray(inputs["x"][c, :T], np.float32))
        m["mem"] = np.ascontiguousarray(np.asarray(inputs["mem"][c], np.float32))
        pos = np.asarray(inputs["positions"][c, :T]).astype(np.int32)
        m["pos_t"] = np.ascontiguousarray(pos.reshape(NT, 128).T)
        maps.append(m)
    return maps


def kernel(**inputs):
    T = 4096
    key = ('full', T)
    if key not in _CACHE:
        _CACHE[key] = Builder(T, [0, 1, 2, 3]).build()
    nc = _CACHE[key]
    maps = make_in_maps(inputs, T, 8)
    res = run_bass_kernel_spmd(nc, maps, core_ids=list(range(8)))
    out = np.stack([np.asarray(r["out"], dtype=np.float32) for r in res.results], axis=0)
    return out
```

```python
import math
import numpy as np
import ml_dtypes
from contextlib import ExitStack
import concourse.bass as bass
import concourse.mybir as mybir
from concourse.bass_utils import run_bass_kernel_spmd

F32 = mybir.dt.float32
BF16 = mybir.dt.bfloat16
I32 = mybir.dt.int32
AF = mybir.ActivationFunctionType
ALU = mybir.AluOpType
AX = mybir.AxisListType

D = 1024
BIG = 30000.0
EPS = 1e-6
EVEN_COLS = 3256
ODD_COLS = 4016
ENGS = ('pe', 'act', 'dve', 'pool', 'sp')
EPOCH = 16000
NDQ = 8


class Buf:
    __slots__ = ('name', 'w', 'r')

    def __init__(self, name=''):
        self.name = name
        self.w = None
        self.r = {}


class Sched:
    def __init__(self, nc, es):
        self.nc = nc
        self.es = es
        self.prog = {e: [] for e in ENGS}
        self.esem = {e: [] for e in ENGS}
        self.cnt = {e: 0 for e in ENGS}
        self.seen = {e: {} for e in ENGS}
        self.dq = ('sp', 'pool', 'act')
        self.dsem = {q: [es.enter_context(nc.semaphore(f'D{q}{i}')) for i in range(NDQ)] for q in self.dq}
        self.dcnt = {q: 0 for q in self.dq}
        self.ninst = 0

    def _semobj(self, key):
        if key[0] == 'E':
            return self.esem[key[1]][key[2]]
        return self.dsem[key[1]][key[2]]

    def op(self, e, fn, reads=(), writes=(), dma=False):
        deps = {}
        for b in reads:
            if b.w is not None:
                k, v = b.w
                if deps.get(k, 0) < v:
                    deps[k] = v
        for b in writes:
            if b.w is not None:
                k, v = b.w
                if deps.get(k, 0) < v:
                    deps[k] = v
            for k, v in b.r.items():
                if deps.get(k, 0) < v:
                    deps[k] = v
        waits = []
        seen = self.seen[e]
        for k, v in deps.items():
            if e == 'pe' and k[0] == 'E' and k[1] == 'pe':
                continue
            if seen.get(k, 0) >= v:
                continue
            seen[k] = v
            waits.append((self._semobj(k), v))
        if dma:
            j = self.dcnt[e]
            self.dcnt[e] += 1
            slot = j % NDQ
            val = 16 * (j // NDQ + 1)
            key = ('D', e, slot)
            if val > 16 and seen.get(key, 0) < val - 16:
                seen[key] = val - 16
                waits.append((self.dsem[e][slot], val - 16))
            sem = self.dsem[e][slot]
            inc = 16
        else:
            c = self.cnt[e]
            ep = c // EPOCH
            if ep >= len(self.esem[e]):
                self.esem[e].append(self.es.enter_context(self.nc.semaphore(f'S{e}{ep}')))
            self.cnt[e] += 1
            key = ('E', e, ep)
            val = c % EPOCH + 1
            sem = self.esem[e][ep]
            inc = 1
        ev = (key, val)
        self.ninst += 1

        def thunk(eng, waits=waits, fn=fn, sem=sem, inc=inc):
            for s, v in waits:
                eng.wait_ge(s, v)
            fn(eng).then_inc(sem, inc)
        self.prog[e].append(thunk)
        for b in reads:
            if b.r.get(key, 0) < val:
                b.r[key] = val
        for b in writes:
            b.w = ev
            b.r = {}
        return ev

    def barrier(self):
        evs = []
        for e in ENGS:
            c = self.cnt[e]
            if c > 0:
                ep = (c - 1) // EPOCH
                evs.append((('E', e, ep), (c - 1) % EPOCH + 1))
        for q in self.dq:
            n = self.dcnt[q]
            for slot in range(min(n, NDQ)):
                cntslot = (n - 1 - slot) // NDQ + 1
                evs.append((('D', q, slot), 16 * cntslot))
        for e in ENGS:
            waits = []
            for k, v in evs:
                if k[0] == 'E' and k[1] == e:
                    continue
                if self.seen[e].get(k, 0) >= v:
                    continue
                self.seen[e][k] = v
                waits.append((self._semobj(k), v))

            def thunk(eng, waits=waits):
                for s, v in waits:
                    eng.wait_ge(s, v)
            self.prog[e].append(thunk)

    def emit(self):
        nc = self.nc
        with nc.Block() as block:
            @block.tensor
            def _(eng):
                for t in self.prog['pe']:
                    t(eng)

            @block.scalar
            def _(eng):
                for t in self.prog['act']:
                    t(eng)

            @block.vector
            def _(eng):
                for t in self.prog['dve']:
                    t(eng)

            @block.gpsimd
            def _(eng):
                for t in self.prog['pool']:
                    t(eng)

            @block.sync
            def _(eng):
                for t in self.prog['sp']:
                    t(eng)


class Tl:
    __slots__ = ('t', 'b')

    def __init__(self, t, name):
        self.t = t
        self.b = Buf(name)

    def __getitem__(self, k):
        return self.t[k]


class Builder:
    def __init__(self, T, layers, dbg=()):
        self.T = T
        self.NT = T // 128
        self.layers = list(layers)
        self.dbg = set(dbg)
        self.nc = bass.Bass("TRN2", target_bir_lowering=False)
        self.uid = 0
        self.rec = None
        self.dbg_out = {}

    def din(self, name, shape, dt=F32):
        return self.nc.dram_tensor(name, list(shape), dt, kind="ExternalInput").ap()

    def dscr(self, name, shape, dt, nbuf=1):
        kind = "ExternalOutput" if name in self.dbg else "Internal"
        t = self.nc.dram_tensor(name, list(shape), dt, kind=kind).ap()
        tl = Tl(t, name)
        if nbuf > 1:
            tl.b = [Buf(f'{name}{i}') for i in range(nbuf)]
        return tl

    def sb(self, es, name, shape, dt):
        self.uid += 1
        t = es.enter_context(self.nc.sbuf_tensor(f'{name}_{self.uid}', list(shape), dt))
        return Tl(t, name)

    def op_(self, e, fn, reads=(), writes=(), dma=False):
        if self.rec is not None:
            self.rec.append((e, fn, tuple(reads), tuple(writes), dma))
        else:
            self.S.op(e, fn, reads=reads, writes=writes, dma=dma)

    def chains_begin(self, names):
        self._chains = {k: [] for k in names}

    def chain(self, name, scr=None):
        self.rec = self._chains[name]
        self.scr = scr if scr is not None else self.scr0

    def chains_emit(self, proportional=False):
        self.rec = None
        self.scr = self.scr0
        lists = [l for l in self._chains.values() if l]
        idx = [0] * len(lists)
        left = sum(len(l) for l in lists)
        while proportional and left:
            i = min((k for k in range(len(lists)) if idx[k] < len(lists[k])), key=lambda k: idx[k] / len(lists[k]))
            e, fn, R, W, dma = lists[i][idx[i]]
            idx[i] += 1
            left -= 1
            self.S.op(e, fn, reads=R, writes=W, dma=dma)
        while left:
            for i, l in enumerate(lists):
                if idx[i] < len(l):
                    e, fn, R, W, dma = l[idx[i]]
                    idx[i] += 1
                    left -= 1
                    self.S.op(e, fn, reads=R, writes=W, dma=dma)

    def new_scr(self, es, tag, w):
        sc = {}
        for k in ('sq', 'tmp', 'ra', 'rb'):
            sc[k] = self.sb(es, f'sc_{k}_{tag}', [128, w], F32)
        for k in ('ssq', 'ln', 'rs'):
            sc[k] = self.sb(es, f'sc_{k}_{tag}', [128, 16], F32)
        return sc

    def mm(self, out, lhsT, rhs, start, stop, R, W):
        self.op_('pe', lambda e: e.matmul(out, lhsT=lhsT, rhs=rhs, start=start, stop=stop,
                                           skip_group_check=True), reads=R, writes=W)

    def tr(self, out, in_, ident, R, W):
        self.op_('pe', lambda e: e.transpose(out=out, in_=in_, identity=ident), reads=R, writes=W)

    def act(self, out, in_, func, R, W, bias=None, scale=None, accum=None):
        kw = {}
        if bias is not None:
            kw['bias'] = bias
        if scale is not None:
            kw['scale'] = scale
        if accum is not None:
            kw['accum_out'] = accum
        self.op_('act', lambda e: e.activation(out=out, in_=in_, func=func, **kw), reads=R, writes=W)

    def tt(self, eng, out, in0, in1, op, R, W):
        self.op_(eng, lambda e: e.tensor_tensor(out=out, in0=in0, in1=in1, op=op), reads=R, writes=W)

    def ts(self, eng, out, in0, s1, op0, R, W, s2=None, op1=None, accum=None):
        kw = {}
        if op1 is not None:
            kw['op1'] = op1
        if accum is not None:
            kw['accum_out'] = accum
        self.op_(eng, lambda e: e.tensor_scalar(out=out, in0=in0, scalar1=s1, scalar2=s2, op0=op0, **kw),
                  reads=R, writes=W)

    def stt(self, out, in0, scalar, in1, op0, op1, R, W):
        self.op_('dve', lambda e: e.scalar_tensor_tensor(out=out, in0=in0, scalar=scalar, in1=in1,
                                                         op0=op0, op1=op1), reads=R, writes=W)

    def cp(self, eng, out, in_, R, W):
        if eng == 'act':
            self.op_('act', lambda e: e.copy(out=out, in_=in_), reads=R, writes=W)
        else:
            self.op_(eng, lambda e: e.tensor_copy(out=out, in_=in_), reads=R, writes=W)

    def red(self, out, in_, op, R, W):
        self.op_('dve', lambda e: e.tensor_reduce(out=out, in_=in_, axis=AX.X, op=op), reads=R, writes=W)

    def memset(self, eng, ap, val, W):
        self.op_(eng, lambda e: e.memset(ap, val), writes=W)

    def dma(self, q, out, in_, R, W, **kw):
        self.op_(q, lambda e: e.dma_start(out=out, in_=in_, **kw), reads=R, writes=W, dma=True)

    def bc_load(self, es, name, row_ap, d):
        t = self.sb(es, name, [128, d], F32)
        self.dma('sp', t[:], row_ap.to_broadcast([128, d]), [], [t.b])
        return t

    def wload(self, w, k, src, c0, c1):
        c = c0
        while c < c1:
            ce = min(c1, c + 2048)
            self.dma('pool', w[:, k, c:ce], src[:, c:ce], [], [w.b])
            c = ce

    def rstd(self, ssq, H, d, R):
        ln, rs = self.scr['ln'], self.scr['rs']
        self.act(ln[:, 0:H], ssq, AF.Ln, R, [ln.b], bias=self.eps_c[:, 0:1], scale=1.0 / d)
        self.act(rs[:, 0:H], ln[:, 0:H], AF.Exp, [ln.b], [rs.b], scale=-0.5)
        return rs[:, 0:H]

    def rmsn(self, src, H, d, g, dst, R, W):
        sq, ssq, tmp = self.scr['sq'], self.scr['ssq'], self.scr['tmp']
        sqv = sq[:, 0:H * d].rearrange("p (h c) -> p h c", h=H)
        self.tt('pool', sqv, src, src, ALU.mult, R, [sq.b])
        self.red(ssq[:, 0:H], sqv, ALU.add, [sq.b], [ssq.b])
        rs = self.rstd(ssq[:, 0:H], H, d, [ssq.b])
        tv = tmp[:, 0:H * d].rearrange("p (h c) -> p h c", h=H)
        self.tt('dve', tv, src, rs.unsqueeze(2).to_broadcast([128, H, d]), ALU.mult,
                list(R) + [self.scr['rs'].b], [tmp.b])
        self.tt('pool', dst, tv, g[:].unsqueeze(1).to_broadcast([128, H, d]), ALU.mult,
                [tmp.b, g.b], W)

    def rope(self, src, H, d2, n, dst, R, W):
        tab = self.rope32 if d2 == 32 else self.rope16
        cosv = tab[:, n, 0:d2]
        sinv = tab[:, n, d2:2 * d2]
        A, Bm = self.scr['ra'], self.scr['rb']
        s4 = src.rearrange("p h (two c) -> p h two c", two=2)
        d4 = dst.rearrange("p h (two c) -> p h two c", two=2)
        Av = A[:, 0:H * 2 * d2].rearrange("p (h two c) -> p h two c", h=H, two=2)
        Bv = Bm[:, 0:H * 2 * d2].rearrange("p (h two c) -> p h two c", h=H, two=2)
        cb = cosv.unsqueeze(1).unsqueeze(1).to_broadcast([128, H, 2, d2])
        sbv = sinv.unsqueeze(1).unsqueeze(1).to_broadcast([128, H, 2, d2])
        self.tt('dve', Av, s4, cb, ALU.mult, list(R) + [tab.b], [A.b])
        self.tt('pool', Bv, s4, sbv, ALU.mult, list(R) + [tab.b], [Bm.b])
        self.tt('dve', d4[:, :, 0, :], Av[:, :, 0, :], Bv[:, :, 1, :], ALU.subtract, [A.b, Bm.b], W)
        self.tt('pool', d4[:, :, 1, :], Bv[:, :, 0, :], Av[:, :, 1, :], ALU.add, [A.b, Bm.b], W)

    def attn_block(self, rhs_q, q_R, N, tiles, scale, acc_views, acc_bufs, mode='exp', LA=2, epilogue=None):
        started = set()
        nt = len(tiles)
        last_use = {}
        for i, tl in enumerate(tiles):
            for j in tl['subs']:
                last_use[j] = i
        Atiles = {}

        def emit_s(i):
            tl = tiles[i]
            bi = self.st_rr % len(self.st_banks)
            self.st_rr += 1
            bank, bb = self.st_banks[bi]
            c0 = tl['c0']
            masks = tl.get('masks', [])
            if mode != 'exp':
                E, eR = tl['Efn']()
            self.mm(bank[:, c0:N], tl['kT'], rhs_q[:, c0:N], True, len(masks) == 0, list(q_R) + tl['R'], [bb])
            for mi, (ml, mr, off, ncols, mR) in enumerate(masks):
                self.mm(bank[:, off:off + ncols], ml, mr, False, mi == len(masks) - 1, mR, [bb])
            ai = self.at_rr % len(self.at_tiles)
            self.at_rr += 1
            A = self.at_tiles[ai]
            Atiles[i] = A
            if mode == 'exp':
                self.act(A[:, c0:N], bank[:, c0:N], AF.Exp, [bb], [A.b], scale=scale)
            else:
                self.tt('dve', A[:, c0:N], bank[:, c0:N], E[:, c0:N], ALU.mult, [bb] + eR, [A.b])
                if tl.get('diag') is not None:
                    dj = tl['diag']
                    self.tt('pool', A[:, dj * 128:(dj + 1) * 128], A[:, dj * 128:(dj + 1) * 128],
                            self.tri01T[:], ALU.mult, [A.b, self.tri01T.b], [A.b])

        def emit_pv(i):
            tl = tiles[i]
            A = Atiles.pop(i)
            for j in tl['subs']:
                view, bk = acc_views[j]
                st = bk not in started
                started.add(bk)
                self.mm(view, A[:, j * 128:(j + 1) * 128], tl['V'], st, last_use[j] == i,
                        [A.b] + tl['R'], [acc_bufs[bk]])

        for step in range(nt):
            emit_s(step)
            self.pend.append(('pv', (lambda i=step: emit_pv(i))))
            self.npv += 1
            self._drain(LA)
        if epilogue is None:
            self.attn_flush()
            return None
        self.epi_id += 1
        eid = self.epi_id
        self.pend.append(('epi', epilogue, eid))
        self._drain(LA)
        return eid

    def _drain(self, limit):
        q = self.pend
        while q and (q[0][0] == 'epi' or self.npv > limit):
            it = q.pop(0)
            if it[0] == 'pv':
                self.npv -= 1
                it[1]()
            else:
                it[1]()
                self.epi_done.add(it[2])

    def attn_flush(self):
        self._drain(-1)

    def attn_sync(self, eid):
        while eid is not None and eid not in self.epi_done:
            q = self.pend
            it = q.pop(0)
            if it[0] == 'pv':
                self.npv -= 1
                it[1]()
            else:
                it[1]()
                self.epi_done.add(it[2])

    def build(self):
        nc = self.nc
        T, NT = self.T, self.NT
        with ExitStack() as es:
            self.S = S = Sched(nc, es)
            self._decl_inputs()
            self._decl_scratch()
            self.ps = []
            for i in range(8):
                t = es.enter_context(nc.psum_tensor(f"psb{i}", [128, 512], F32))
                self.ps.append(Tl(t, f'ps{i}'))
            self._consts(es)
            self._prep(es)
            S.barrier()
            xin = Tl(self.x_in, 'xin')
            xin.b = [Buf('xin')] * 1
            cur = xin
            for idx, L in enumerate(self.layers):
                last = idx == len(self.layers) - 1
                dst = self.out_t if last else self.xs[idx % 2]
                with ExitStack() as les:
                    if L % 2 == 0:
                        self.layer_even(les, L, cur, dst)
                    else:
                        self.layer_odd(les, L, cur, dst)
                S.barrier()
                cur = dst
            S.barrier()
            S.emit()
        return nc

    def _decl_inputs(self):
        T, NT = self.T, self.NT
        d = self.din
        self.x_in = d("x", [T, D])
        self.mem = d("mem", [256, D])
        self.pos_t = d("pos_t", [128, NT], I32)
        self.w = {}
        spec = dict(
            ln_g=[4, D], mem_norm_g=[4, D], mem_w_kv=[4, D, 512], mem_q_norm_g=[4, 64], mem_k_norm_g=[4, 64],
            w_out=[4, 1280, D], even_w_in=[2, D, EVEN_COLS], mla_q_lat_g=[2, 256], mla_kv_lat_g=[2, 128],
            mla_w_uq=[2, 256, 768], mla_w_ukv=[2, 128, 1024], mla_q_norm_g=[2, 96], mla_k_norm_g=[2, 96],
            nsa_q_norm_g=[2, 64], nsa_k_norm_g=[2, 3, 64], nsa_cmp_posT=[2, 2, 64, 32],
            nsa_cmp_w1=[2, 2, 2048, 64], nsa_cmp_w2=[2, 2, 64, 64], odd_w_in=[2, D, ODD_COLS],
            dsa_q_norm_g=[2, 64], dsa_k_norm_g=[2, 64], mlstm_conv_wT=[2, 512, 4], mlstm_conv_b=[2, 512],
            mlstm_i_bias=[2, 4], mlstm_f_bias=[2, 4], mlstm_h_norm_g=[2, 128])
        for k, shp in spec.items():
            self.w[k] = d(k, shp)
        ncp = self.ncmp_pad = max(1, T // 2048) * 128
        nb = self.n_blk = T // 64
        self.c = dict(
            ident=d("c_ident", [128, 128], BF16), identf=d("c_identf", [128, 128], F32),
            i4=d("c_i4", [128, 512], BF16), i8x2=d("c_i8x2", [128, 512], BF16),
            tri_neg=d("c_tri_neg", [128, 128], BF16), edge_neg=d("c_edge_neg", [128, 128], BF16),
            tri01T=d("c_tri01T", [128, 128], BF16), trinegf=d("c_trinegf", [128, 128], F32),
            invf=d("c_invf", [128, 32]), pw=d("c_pw", [128, 40]), cmpneg=d("c_cmpneg", [T, ncp], BF16),
            forced=d("c_forced", [T, nb]), overlap=d("c_overlap", [ncp, nb], BF16))

    def _decl_scratch(self):
        T, NT = self.T, self.NT
        s = self.dscr
        self.out_t = Tl(self.nc.dram_tensor("out", [T, D], F32, kind="ExternalOutput").ap(), 'out')
        self.out_t.b = [Buf(f'out{i}') for i in range(NT)]
        self.xs = [s("xs0", [T, D], F32, NT), s("xs1", [T, D], F32, NT)]
        self.rope_d = s("rope_d", [T, 64], F32)
        self.Y = s("Y", [T, 2304], F32)
        self.G = s("G", [T, 1792], BF16, NT)
        self.SG = s("SG", [T, 32], F32, NT)
        self.mla_qT = s("mla_qT", [8, 96, T], BF16)
        self.mla_kT = s("mla_kT", [8, 96, T], BF16)
        self.mla_v = s("mla_v", [8, 128, NT, 65], BF16)
        self.nsa_qT = s("nsa_qT", [2, NT, 64, 4, 128], BF16)
        self.nsa_kT = s("nsa_kT", [8, 64, T], BF16)
        self.nsa_v = s("nsa_v", [4, 128, NT, 65], BF16)
        self.mem_qT = s("mem_qT", [4, 64, T], BF16)
        self.dsa_qT = s("dsa_qT", [NT, 64, 8, 128], BF16)
        self.dsa_kT = s("dsa_kT", [64, T], BF16)
        self.dsa_v = s("dsa_v", [128, NT, 65], BF16)
        self.idx_qT = s("idx_qT", [8, 32, T], BF16)
        self.idx_kT = s("idx_kT", [32, T], BF16)
        self.idx_w = s("idx_w", [T, 8], F32)
        self.ml_raw = s("ml_raw", [512, T], F32)
        self.ml_if = s("ml_if", [8, T], F32)
        self.ml_qkT = s("ml_qkT", [512, T], BF16)
        self.ml_v = s("ml_v", [4, 128, NT, 129], BF16)
        self.ml_g = s("ml_g", [12, T], F32)

    def _consts(self, es):
        c = self.c
        def ld(name, shape, dt):
            t = self.sb(es, name, shape, dt)
            self.dma('sp', t[:], c[name], [], [t.b])
            return t
        self.ident = ld('ident', [128, 128], BF16)
        self.identf = ld('identf', [128, 128], F32)
        self.i4 = ld('i4', [128, 512], BF16)
        self.i8x2 = ld('i8x2', [128, 512], BF16)
        self.tri_neg = ld('tri_neg', [128, 128], BF16)
        self.edge_neg = ld('edge_neg', [128, 128], BF16)
        self.tri01T = ld('tri01T', [128, 128], BF16)
        self.trinegf = ld('trinegf', [128, 128], F32)
        self.invf = ld('invf', [128, 32], F32)
        self.pw = ld('pw', [128, 40], F32)
        self.eps_c = self.sb(es, 'eps_c', [128, 1], F32)
        self.memset('dve', self.eps_c[:], EPS, [self.eps_c.b])
        self.one_c = self.sb(es, 'one_c', [128, 1], F32)
        self.memset('dve', self.one_c[:], 1.0, [self.one_c.b])
        self.rope32 = self.sb(es, 'rope32', [128, self.NT, 64], F32)
        self.rope16 = self.sb(es, 'rope16', [128, self.NT, 32], F32)
        self.sc_sq = self.sb(es, 'sc_sq', [128, 512], F32)
        self.sc_tmp = self.sb(es, 'sc_tmp', [128, 512], F32)
        self.sc_ra = self.sb(es, 'sc_ra', [128, 512], F32)
        self.sc_rb = self.sb(es, 'sc_rb', [128, 512], F32)
        self.sc_ssq = self.sb(es, 'sc_ssq', [128, 16], F32)
        self.sc_ln = self.sb(es, 'sc_ln', [128, 16], F32)
        self.sc_rs = self.sb(es, 'sc_rs', [128, 16], F32)
        self.scr0 = dict(sq=self.sc_sq, tmp=self.sc_tmp, ra=self.sc_ra, rb=self.sc_rb, ssq=self.sc_ssq,
                         ln=self.sc_ln, rs=self.sc_rs)
        self.scr = self.scr0

    def _prep(self, es0):
        NT = self.NT
        PI = math.pi
        with ExitStack() as es:
            pi_t = self.sb(es, 'pos_i', [128, NT], I32)
            self.dma('sp', pi_t[:], self.pos_t, [], [pi_t.b])
            pf = self.sb(es, 'pos_f', [128, NT], F32)
            self.cp('dve', pf[:], pi_t[:], [pi_t.b], [pf.b])
            ang = self.sb(es, 'ang', [128, NT, 32], F32)
            self.tt('dve', ang[:], pf[:].unsqueeze(2).to_broadcast([128, NT, 32]),
                    self.invf[:].unsqueeze(1).to_broadcast([128, NT, 32]), ALU.mult,
                    [pf.b, self.invf.b], [ang.b])
            kf = self.sb(es, 'kf', [128, NT, 32], F32)
            ki = self.sb(es, 'ki', [128, NT, 32], I32)
            r = self.sb(es, 'r', [128, NT, 32], F32)
            m = self.sb(es, 'm', [128, NT, 32], F32)

            def wrap(buf):
                self.ts('dve', m[:], buf[:], PI, ALU.is_gt, [buf.b], [m.b], s2=-2 * PI, op1=ALU.mult)
                self.tt('dve', buf[:], buf[:], m[:], ALU.add, [buf.b, m.b], [buf.b])
                self.ts('dve', m[:], buf[:], -PI, ALU.is_lt, [buf.b], [m.b], s2=2 * PI, op1=ALU.mult)
                self.tt('dve', buf[:], buf[:], m[:], ALU.add, [buf.b, m.b], [buf.b])
                self.ts('dve', buf[:], buf[:], PI, ALU.min, [buf.b], [buf.b], s2=-PI, op1=ALU.max)
            self.ts('dve', kf[:], ang[:], 1.0 / (2 * PI), ALU.mult, [ang.b], [kf.b])
            self.cp('dve', ki[:], kf[:], [kf.b], [ki.b])
            self.cp('dve', kf[:], ki[:], [ki.b], [kf.b])
            C1 = 6.28125
            C2 = 2 * PI - C1
            self.stt(r[:], kf[:], -C1, ang[:], ALU.mult, ALU.add, [kf.b, ang.b], [r.b])
            self.stt(r[:], kf[:], -C2, r[:], ALU.mult, ALU.add, [kf.b, r.b], [r.b])
            wrap(r)
            self.act(self.rope32[:, :, 32:64], r[:], AF.Sin, [r.b], [self.rope32.b])
            self.ts('dve', r[:], r[:], PI / 2, ALU.add, [r.b], [r.b])
            wrap(r)
            self.act(self.rope32[:, :, 0:32], r[:], AF.Sin, [r.b], [self.rope32.b])
            self.cp('dve', self.rope16[:, :, 0:16], self.rope32[:, :, 0:32:2], [self.rope32.b], [self.rope16.b])
            self.cp('dve', self.rope16[:, :, 16:32], self.rope32[:, :, 32:64:2], [self.rope32.b], [self.rope16.b])
            self.dma('sp', self.rope_d[:].rearrange("(n p) c -> p n c", p=128), self.rope32[:],
                     [self.rope32.b], [self.rope_d.b])
            self.S.barrier()

    def load_win(self, es, L, w_in, ncols):
        win = self.sb(es, 'win', [128, 8, ncols], BF16)
        for k in range(8):
            self.wload(win, k, w_in[k * 128:(k + 1) * 128, :], 0, ncols)
        lng = self.bc_load(es, 'lng', self.w['ln_g'][L:L + 1, :], D)
        return win, lng

    def load_wout(self, es, L):
        wout = self.sb(es, 'wout', [128, 10, D], BF16)
        for k in range(10):
            self.wload(wout, k, self.w['w_out'][L, k * 128:(k + 1) * 128, :], 0, D)
        return wout

    def mem_kv(self, es, L):
        w = self.w
        kT = self.sb(es, 'memkT', [64, 4, 256], BF16)
        V = self.sb(es, 'memV', [128, 2, 4, 65], BF16)
        self.memset('pool', V[:], 1.0, [V.b])
        with ExitStack() as s2:
            wkv = self.sb(s2, 'wkv', [128, 8, 512], BF16)
            for k in range(8):
                self.wload(wkv, k, w['mem_w_kv'][L, k * 128:(k + 1) * 128, :], 0, 512)
            mg = self.bc_load(s2, 'mg', w['mem_norm_g'][L:L + 1, :], D)
            kg = self.bc_load(s2, 'kg', w['mem_k_norm_g'][L:L + 1, :], 64)
            mt = self.sb(s2, 'mt', [128, D], F32)
            junk = self.sb(s2, 'junk', [128, D], BF16)
            mh = self.sb(s2, 'mh', [128, D], BF16)
            mhT = self.sb(s2, 'mhT', [128, 8, 128], BF16)
            kv = self.sb(s2, 'kv', [128, 512], F32)
            kn = self.sb(s2, 'kn', [128, 256], BF16)
            ss = self.sb(s2, 'ss', [128, 1], F32)
            for i in range(2):
                self.dma('sp', mt[:], self.mem[i * 128:(i + 1) * 128, :], [], [mt.b])
                self.act(junk[:], mt[:], AF.Square, [mt.b], [junk.b, ss.b], accum=ss[:])
                rs = self.rstd(ss[:, 0:1], 1, D, [ss.b])
                self.stt(mh[:], mt[:], rs, mg[:], ALU.mult, ALU.mult, [mt.b, self.scr['rs'].b, mg.b], [mh.b])
                pT = self.ps[2]
                pTb = pT[:].bitcast(BF16)
                for k in range(8):
                    self.tr(pTb[:, k * 128:(k + 1) * 128], mh[:, k * 128:(k + 1) * 128], self.ident[:],
                            [mh.b, self.ident.b], [pT.b])
                self.cp('act', mhT[:].rearrange("p k c -> p (k c)"), pTb[:, 0:1024], [pT.b], [mhT.b])
                pU = self.ps[0]
                for k in range(8):
                    self.mm(pU[:, 0:512], mhT[:, k, :], wkv[:, k, :], k == 0, k == 7, [mhT.b, wkv.b], [pU.b])
                self.cp('act', kv[:], pU[:, 0:512], [pU.b], [kv.b])
                self.rmsn(kv[:, 0:256].rearrange("p (h c) -> p h c", h=4), 4, 64, kg,
                          kn[:].rearrange("p (h c) -> p h c", h=4), [kv.b], [kn.b])
                self.cp('dve', V[:, i, :, 0:64], kv[:, 256:512].rearrange("p (h c) -> p h c", h=4), [kv.b], [V.b])
                pK = self.ps[3]
                pKb = pK[:].bitcast(BF16)
                for h in range(4):
                    self.tr(pKb[0:64, h * 128:(h + 1) * 128], kn[:, h * 64:(h + 1) * 64], self.ident[:],
                            [kn.b, self.ident.b], [pK.b])
                self.cp('act', kT[:, :, i * 128:(i + 1) * 128],
                        pKb[0:64, 0:512].rearrange("p (h c) -> p h c", h=4), [pK.b], [kT.b])
            self.S.barrier()
        return kT, V

    def p1_front(self, n, x_src, xt, ht, hT, u, ss, junk, lng, win, ncols):
        sl = n % 2
        x_t, h_t, hT_t, u_t = xt[sl], ht[sl], hT[sl], u[sl]
        xb = x_src.b[n] if len(x_src.b) > 1 else x_src.b[0]
        self.dma('sp', x_t[:], x_src[n * 128:(n + 1) * 128, :], [xb], [x_t.b])
        self.act(junk[:], x_t[:], AF.Square, [x_t.b], [junk.b, ss.b], accum=ss[:])
        rs = self.rstd(ss[:, 0:1], 1, D, [ss.b])
        self.stt(h_t[:], x_t[:], rs, lng[:], ALU.mult, ALU.mult, [x_t.b, self.scr['rs'].b, lng.b], [h_t.b])
        pT = self.ps[2]
        pTb = pT[:].bitcast(BF16)
        for k in range(8):
            self.tr(pTb[:, k * 128:(k + 1) * 128], h_t[:, k * 128:(k + 1) * 128], self.ident[:],
                    [h_t.b, self.ident.b], [pT.b])
        self.cp('act', hT_t[:].rearrange("p k c -> p (k c)"), pTb[:, 0:1024], [pT.b], [hT_t.b])
        nchunk = (ncols + 511) // 512
        for c in range(nchunk):
            c0 = c * 512
            wd = min(512, ncols - c0)
            pU = self.ps[c % 2]
            for k in range(8):
                self.mm(pU[:, 0:wd], hT_t[:, k, :], win[:, k, c0:c0 + wd], k == 0, k == 7,
                        [hT_t.b, win.b], [pU.b])
            self.cp('act' if c % 2 == 0 else 'dve', u_t[:, c0:c0 + wd], pU[:, 0:wd], [pU.b], [u_t.b])
        return u_t

    def transposes_out(self, srcs, rows, stage, dst_ap, dst_b, pidx):
        pT = self.ps[pidx]
        pTb = pT[:].bitcast(BF16)
        k = len(srcs)
        for i, (ap, R) in enumerate(srcs):
            self.tr(pTb[0:rows, i * 128:(i + 1) * 128], ap, self.ident[:], list(R) + [self.ident.b], [pT.b])
        self.cp('act', stage[0:rows, 0:k, :], pTb[0:rows, 0:k * 128].rearrange("p (k c) -> p k c", k=k),
                [pT.b], [stage.b])
        self.dma('sp', dst_ap, stage[0:rows, 0:k, :], [stage.b], dst_b)

    def p3(self, es, L, x_src, x_dst, wout, mixfn):
        NT = self.NT
        xt = [self.sb(es, f'p3x{i}', [128, D], F32) for i in range(2)]
        mixT = [self.sb(es, f'p3mT{i}', [128, 10, 128], BF16) for i in range(2)]
        xo = [self.sb(es, f'p3o{i}', [128, D], F32) for i in range(2)]

        def one(n):
            sl = n % 2
            pb = 4 * sl
            mix = mixfn(n)
            xb = x_src.b[n] if len(x_src.b) > 1 else x_src.b[0]
            self.dma('sp', xt[sl][:], x_src[n * 128:(n + 1) * 128, :], [xb], [xt[sl].b])
            for half in range(2):
                pT = self.ps[pb + 2 + half]
                pTb = pT[:].bitcast(BF16)
                for k in range(5):
                    kk = half * 5 + k
                    self.tr(pTb[:, k * 128:(k + 1) * 128], mix[:, kk * 128:(kk + 1) * 128], self.ident[:],
                            [mix.b, self.ident.b], [pT.b])
                self.cp('act', mixT[sl][:, half * 5:half * 5 + 5, :].rearrange("p k c -> p (k c)"),
                        pTb[:, 0:640], [pT.b], [mixT[sl].b])
            for c in range(2):
                pU = self.ps[pb + c]
                for k in range(10):
                    self.mm(pU[:, 0:512], mixT[sl][:, k, :], wout[:, k, c * 512:(c + 1) * 512], k == 0, k == 9,
                            [mixT[sl].b, wout.b], [pU.b])
                self.tt('dve', xo[sl][:, c * 512:(c + 1) * 512], pU[:, 0:512], xt[sl][:, c * 512:(c + 1) * 512],
                        ALU.add, [pU.b, xt[sl].b], [xo[sl].b])
            self.dma('sp', x_dst[n * 128:(n + 1) * 128, :], xo[sl][:], [xo[sl].b], [x_dst.b[n]])

        for n0 in range(0, NT, 2):
            self.chains_begin(['P0', 'P1'])
            self.chain('P0')
            one(n0)
            if n0 + 1 < NT:
                self.chain('P1')
                one(n0 + 1)
            self.chains_emit()

    def attn_setup(self, es, dvp_two_banks=False):
        self.st_banks = [(self.ps[i], self.ps[i].b) for i in range(3)]
        self.st_rr = 0
        self.at_tiles = [self.sb(es, f'At{i}', [128, 512], BF16) for i in range(3)]
        self.at_rr = 0
        self.pend = []
        self.npv = 0
        self.epi_id = 0
        self.epi_done = set()

    def mem_attn(self, es, memkT, memV):
        T = self.T
        qT = [self.sb(es, f'mqT{i}', [64, T], BF16) for i in range(2)]
        ost = [self.sb(es, f'most{i}', [128, 4, 64], F32) for i in range(2)]
        rec = self.sb(es, 'mrec', [128, 4], F32)
        blk = 0
        for h in range(4):
            q = qT[h % 2]
            self.dma('sp', q[:], self.mem_qT[h], [self.mem_qT.b], [q.b])
            for cq in range(T // 512):
                set_i = blk % 2
                blk += 1
                accb = self.ps[3 + set_i]
                views = {j: (accb[:, j * 65:(j + 1) * 65], 0) for j in range(4)}
                tiles = [dict(kT=memkT[:, h, kt * 128:(kt + 1) * 128], V=memV[:, kt, h, :],
                              R=[memkT.b, memV.b], c0=0, subs=[0, 1, 2, 3]) for kt in range(2)]
                def epi(accb=accb, o=ost[set_i], cq=cq, h=h):
                    av = accb[:, 0:260].rearrange("p (j c) -> p j c", j=4)
                    self.op_('dve', lambda e, av=av: e.reciprocal(out=rec[:], in_=av[:, :, 64]), reads=[accb.b],
                             writes=[rec.b])
                    self.tt('dve', o[:], av[:, :, 0:64], rec[:].unsqueeze(2).to_broadcast([128, 4, 64]), ALU.mult,
                            [accb.b, rec.b], [o.b])
                    self.dma('sp', self.Y[cq * 512:(cq + 1) * 512, 1024 + h * 64:1024 + (h + 1) * 64]
                             .rearrange("(j p) c -> p j c", p=128), o[:], [o.b], [self.Y.b])
                self.attn_block(q[:, cq * 512:(cq + 1) * 512], [q.b], 512, tiles, 0.125, views, [accb.b], epilogue=epi)
        self.attn_flush()

    def layer_even(self, es, L, x_src, x_dst):
        li = L // 2
        w = self.w
        T, NT = self.T, self.NT
        S = self.S
        memkT, memV = self.mem_kv(es, L)
        with ExitStack() as p1:
            win, lng = self.load_win(p1, L, w['even_w_in'][li], EVEN_COLS)
            wuq = self.sb(p1, 'wuq', [128, 2, 768], BF16)
            for k in range(2):
                self.wload(wuq, k, w['mla_w_uq'][li, k * 128:(k + 1) * 128, :], 0, 768)
            wukv = self.sb(p1, 'wukv', [128, 1, 1024], BF16)
            self.wload(wukv, 0, w['mla_w_ukv'][li], 0, 1024)
            g_ql = self.bc_load(p1, 'g_ql', w['mla_q_lat_g'][li:li + 1, :], 256)
            g_kvl = self.bc_load(p1, 'g_kvl', w['mla_kv_lat_g'][li:li + 1, :], 128)
            g_qn = self.bc_load(p1, 'g_qn', w['mla_q_norm_g'][li:li + 1, 0:64], 64)
            g_qp = self.bc_load(p1, 'g_qp', w['mla_q_norm_g'][li:li + 1, 64:96], 32)
            g_kn = self.bc_load(p1, 'g_kn', w['mla_k_norm_g'][li:li + 1, 0:64], 64)
            g_kp = self.bc_load(p1, 'g_kp', w['mla_k_norm_g'][li:li + 1, 64:96], 32)
            g_bq = self.bc_load(p1, 'g_bq', w['nsa_q_norm_g'][li:li + 1, :], 64)
            g_ks = self.bc_load(p1, 'g_ks', w['nsa_k_norm_g'][li, 1:2, :], 64)
            g_kw = self.bc_load(p1, 'g_kw', w['nsa_k_norm_g'][li, 2:3, :], 64)
            g_mq = self.bc_load(p1, 'g_mq', w['mem_q_norm_g'][L:L + 1, :], 64)
            xt = [self.sb(p1, f'xt{i}', [128, D], F32) for i in range(2)]
            ht = [self.sb(p1, f'ht{i}', [128, D], BF16) for i in range(2)]
            hT = [self.sb(p1, f'hT{i}', [128, 8, 128], BF16) for i in range(2)]
            u = [self.sb(p1, f'u{i}', [128, EVEN_COLS], F32) for i in range(2)]
            ss = self.sb(p1, 'ss', [128, 1], F32)
            junk = self.sb(p1, 'junk', [128, D], BF16)
            latn = self.sb(p1, 'latn', [128, 384], BF16)
            latT = self.sb(p1, 'latT', [128, 3, 128], BF16)
            qsb = self.sb(p1, 'qsb', [128, 768], F32)
            kvsb = self.sb(p1, 'kvsb', [128, 1024], F32)
            qpe = self.sb(p1, 'qpe', [128, 256], F32)
            kpe = self.sb(p1, 'kpe', [128, 32], F32)
            kpeb = self.sb(p1, 'kpeb', [128, 32], BF16)
            qf = self.sb(p1, 'qf', [128, 8, 96], BF16)
            kfm = self.sb(p1, 'kfm', [128, 8, 96], BF16)
            vaug = self.sb(p1, 'vaug', [128, 8, 65], BF16)
            self.memset('pool', vaug[:], 1.0, [vaug.b])
            bqn = self.sb(p1, 'bqn', [128, 512], F32)
            bqf = self.sb(p1, 'bqf', [128, 512], BF16)
            kn2 = self.sb(p1, 'kn2', [128, 128], F32)
            kmisc = self.sb(p1, 'kmisc', [128, 8, 64], BF16)
            nv = self.sb(p1, 'nv', [128, 4, 65], BF16)
            self.memset('pool', nv[:], 1.0, [nv.b])
            mqf = self.sb(p1, 'mqf', [128, 256], BF16)
            gt = self.sb(p1, 'gt', [128, 1280], BF16)
            sg = self.sb(p1, 'sg', [128, 24], F32)
            stq = self.sb(p1, 'stq', [96, 8, 128], BF16)
            stk = self.sb(p1, 'stk', [96, 8, 128], BF16)
            stb = self.sb(p1, 'stb', [64, 8, 128], BF16)
            stm = self.sb(p1, 'stm', [64, 8, 128], BF16)
            stmq = self.sb(p1, 'stmq', [64, 4, 128], BF16)
            scrA = self.new_scr(p1, 'A', 512)
            scrB = self.new_scr(p1, 'B', 512)
            scrC = self.new_scr(p1, 'C', 128)
            ut_next = self.p1_front(0, x_src, xt, ht, hT, u, ss, junk, lng, win, EVEN_COLS)
            for n in range(NT):
                ut = ut_next
                U = lambda a, b_, ut=ut: ut[:, a:b_]
                ub = [ut.b]
                tsl = slice(n * 128, (n + 1) * 128)
                self.chains_begin(['A', 'F', 'B', 'C', 'M', 'G'])
                if n + 1 < NT:
                    self.chain('F')
                    ut_next = self.p1_front(n + 1, x_src, xt, ht, hT, u, ss, junk, lng, win, EVEN_COLS)
                self.chain('G')
                self.act(gt[:, 0:512], U(416, 928), AF.Silu, ub, [gt.b])
                self.act(gt[:, 512:1024], U(2232, 2744), AF.Silu, ub, [gt.b])
                self.act(gt[:, 1024:1280], U(3000, 3256), AF.Silu, ub, [gt.b])
                self.dma('sp', self.G[tsl, 0:1280], gt[:], [gt.b], [self.G.b[n]])
                self.act(sg[:], U(2208, 2232), AF.Sigmoid, ub, [sg.b])
                self.dma('sp', self.SG[tsl, 0:24], sg[:], [sg.b], [self.SG.b[n]])
                self.chain('A', scrA)
                self.rmsn(U(0, 256).rearrange("p (h c) -> p h c", h=1), 1, 256, g_ql,
                          latn[:, 0:256].rearrange("p (h c) -> p h c", h=1), ub, [latn.b])
                self.rmsn(U(256, 384).rearrange("p (h c) -> p h c", h=1), 1, 128, g_kvl,
                          latn[:, 256:384].rearrange("p (h c) -> p h c", h=1), ub, [latn.b])
                pT = self.ps[3]
                pTb = pT[:].bitcast(BF16)
                for k in range(3):
                    self.tr(pTb[:, k * 128:(k + 1) * 128], latn[:, k * 128:(k + 1) * 128], self.ident[:],
                            [latn.b, self.ident.b], [pT.b])
                self.cp('act', latT[:].rearrange("p k c -> p (k c)"), pTb[:, 0:384], [pT.b], [latT.b])
                pQ, pQ2, pK, pK2 = self.ps[3], self.ps[4], self.ps[5], self.ps[4]
                for k in range(2):
                    self.mm(pQ[:, 0:512], latT[:, k, :], wuq[:, k, 0:512], k == 0, k == 1, [latT.b, wuq.b], [pQ.b])
                for k in range(2):
                    self.mm(pQ2[:, 0:256], latT[:, k, :], wuq[:, k, 512:768], k == 0, k == 1, [latT.b, wuq.b], [pQ2.b])
                self.cp('act', qsb[:, 0:512], pQ[:, 0:512], [pQ.b], [qsb.b])
                self.cp('dve', qsb[:, 512:768], pQ2[:, 0:256], [pQ2.b], [qsb.b])
                self.mm(pK[:, 0:512], latT[:, 2, :], wukv[:, 0, 0:512], True, True, [latT.b, wukv.b], [pK.b])
                self.mm(pK2[:, 0:512], latT[:, 2, :], wukv[:, 0, 512:1024], True, True, [latT.b, wukv.b], [pK2.b])
                self.cp('act', kvsb[:, 0:512], pK[:, 0:512], [pK.b], [kvsb.b])
                self.cp('dve', kvsb[:, 512:1024], pK2[:, 0:512], [pK2.b], [kvsb.b])
                q3 = qsb[:].rearrange("p (h c) -> p h c", h=8)
                kv3 = kvsb[:].rearrange("p (h c) -> p h c", h=8)
                self.rmsn(q3[:, :, 0:64], 8, 64, g_qn, qf[:, :, 0:64], [qsb.b], [qf.b])
                qpe3 = qpe[:].rearrange("p (h c) -> p h c", h=8)
                self.rmsn(q3[:, :, 64:96], 8, 32, g_qp, qpe3, [qsb.b], [qpe.b])
                self.rope(qpe3, 8, 16, n, qf[:, :, 64:96], [qpe.b], [qf.b])
                self.rmsn(kv3[:, :, 0:64], 8, 64, g_kn, kfm[:, :, 0:64], [kvsb.b], [kfm.b])
                self.cp('dve', vaug[:, :, 0:64], kv3[:, :, 64:128], [kvsb.b], [vaug.b])
                kpe3 = kpe[:].rearrange("p (h c) -> p h c", h=1)
                self.rmsn(U(384, 416).rearrange("p (h c) -> p h c", h=1), 1, 32, g_kp, kpe3, ub, [kpe.b])
                self.rope(kpe3, 1, 16, n, kpeb[:].rearrange("p (h c) -> p h c", h=1), [kpe.b], [kpeb.b])
                self.cp('pool', kfm[:, :, 64:96], kpeb[:].unsqueeze(1).to_broadcast([128, 8, 32]), [kpeb.b], [kfm.b])
                self.transposes_out([(qf[:, h, :], [qf.b]) for h in range(8)], 96, stq,
                                    self.mla_qT[:, :, tsl].rearrange("h d t -> d h t"), [self.mla_qT.b], 3)
                self.transposes_out([(kfm[:, h, :], [kfm.b]) for h in range(8)], 96, stk,
                                    self.mla_kT[:, :, tsl].rearrange("h d t -> d h t"), [self.mla_kT.b], 5)
                self.dma('sp', self.mla_v[:, :, n, :].rearrange("h p c -> p h c"), vaug[:], [vaug.b], [self.mla_v.b])
                self.chain('B', scrB)
                bq3 = bqn[:].rearrange("p (h c) -> p h c", h=8)
                self.rmsn(U(928, 1440).rearrange("p (h c) -> p h c", h=8), 8, 64, g_bq, bq3, ub, [bqn.b])
                self.rope(bq3, 8, 32, n, bqf[:].rearrange("p (h c) -> p h c", h=8), [bqn.b], [bqf.b])
                self.chain('C', scrC)
                k23 = kn2[:].rearrange("p (h c) -> p h c", h=2)
                self.rmsn(U(1696, 1824).rearrange("p (h c) -> p h c", h=2), 2, 64, g_ks, k23, ub, [kn2.b])
                self.rope(k23, 2, 32, n, kmisc[:, 0:2, :], [kn2.b], [kmisc.b])
                self.rmsn(U(1952, 2080).rearrange("p (h c) -> p h c", h=2), 2, 64, g_kw, k23, ub, [kn2.b])
                self.rope(k23, 2, 32, n, kmisc[:, 2:4, :], [kn2.b], [kmisc.b])
                self.cp('pool', kmisc[:, 4:8, :], U(1440, 1696).rearrange("p (h c) -> p h c", h=4), ub, [kmisc.b])
                self.cp('dve', nv[:, 0:2, 0:64], U(1824, 1952).rearrange("p (h c) -> p h c", h=2), ub, [nv.b])
                self.cp('dve', nv[:, 2:4, 0:64], U(2080, 2208).rearrange("p (h c) -> p h c", h=2), ub, [nv.b])
                self.chain('B', scrB)
                self.transposes_out([(bqf[:, h * 64:(h + 1) * 64], [bqf.b]) for h in range(8)], 64, stb,
                                    self.nsa_qT[:, n].rearrange("g d r t -> d g r t"), [self.nsa_qT.b], 6)
                self.chain('C', scrC)
                self.transposes_out([(kmisc[:, i, :], [kmisc.b]) for i in range(8)], 64, stm,
                                    self.nsa_kT[:, :, tsl].rearrange("k d t -> d k t"), [self.nsa_kT.b], 7)
                self.dma('sp', self.nsa_v[:, :, n, :].rearrange("k p c -> p k c"), nv[:], [nv.b], [self.nsa_v.b])
                self.chain('B', scrB)
                self.rmsn(U(2744, 3000).rearrange("p (h c) -> p h c", h=4), 4, 64, g_mq,
                          mqf[:].rearrange("p (h c) -> p h c", h=4), ub, [mqf.b])
                self.transposes_out([(mqf[:, h * 64:(h + 1) * 64], [mqf.b]) for h in range(4)], 64, stmq,
                                    self.mem_qT[:, :, tsl].rearrange("h d t -> d h t"), [self.mem_qT.b], 6)
                self.chains_emit()
            S.barrier()
        kcmpT = self.sb(es, 'kcmpT', [64, 2, self.ncmp_pad], BF16)
        vcmp = self.sb(es, 'vcmp', [128, 2, self.ncmp_pad // 128, 129], BF16)
        self.nsa_compress(li, kcmpT, vcmp)
        S.barrier()
        with ExitStack() as pa:
            self.attn_setup(pa)
            self.mla_attn(pa)
            S.barrier()
        with ExitStack() as pa:
            self.attn_setup(pa)
            self.mem_attn(pa, memkT, memV)
            S.barrier()
        with ExitStack() as pa:
            self.attn_setup(pa)
            self.nsa_attn(pa, kcmpT, vcmp)
            S.barrier()
        with ExitStack() as p3:
            wout = self.load_wout(p3, L)
            yt = [self.sb(p3, f'yt{i}', [128, 2304], F32) for i in range(2)]
            gtt = [self.sb(p3, f'gtt{i}', [128, 1280], BF16) for i in range(2)]
            sgt = [self.sb(p3, f'sgt{i}', [128, 24], F32) for i in range(2)]
            mix = [self.sb(p3, f'mix{i}', [128, 1280], BF16) for i in range(2)]
            ybs = [self.sb(p3, f'yb{i}', [128, 512], F32) for i in range(2)]
            yb2s = [self.sb(p3, f'yb2{i}', [128, 512], F32) for i in range(2)]

            def mixfn(n):
                sl = n % 2
                y, g, s_, m = yt[sl], gtt[sl], sgt[sl], mix[sl]
                yb, yb2 = ybs[sl], yb2s[sl]
                tsl = slice(n * 128, (n + 1) * 128)
                self.dma('sp', y[:], self.Y[tsl, :], [self.Y.b], [y.b])
                self.dma('sp', g[:], self.G[tsl, 0:1280], [self.G.b[n]], [g.b])
                self.dma('sp', s_[:], self.SG[tsl, 0:24], [self.SG.b[n]], [s_.b])
                self.tt('dve', m[:, 0:512], y[:, 0:512], g[:, 0:512], ALU.mult, [y.b, g.b], [m.b])
                self.tt('pool', m[:, 1024:1280], y[:, 1024:1280], g[:, 1024:1280], ALU.mult, [y.b, g.b], [m.b])
                s3 = s_[:].rearrange("p (h c) -> p h c", c=3)
                y3 = lambda a: y[:, a:a + 512].rearrange("p (h c) -> p h c", h=8)
                b3 = yb[:].rearrange("p (h c) -> p h c", h=8)
                b23 = yb2[:].rearrange("p (h c) -> p h c", h=8)
                self.tt('dve', b3, y3(1280), s3[:, :, 0:1].to_broadcast([128, 8, 64]), ALU.mult, [y.b, s_.b], [yb.b])
                self.tt('pool', b23, y3(512), s3[:, :, 1:2].to_broadcast([128, 8, 64]), ALU.mult, [y.b, s_.b], [yb2.b])
                self.tt('dve', b3, b3, b23, ALU.add, [yb.b, yb2.b], [yb.b])
                self.tt('pool', b23, y3(1792), s3[:, :, 2:3].to_broadcast([128, 8, 64]), ALU.mult, [y.b, s_.b], [yb2.b])
                self.tt('dve', b3, b3, b23, ALU.add, [yb.b, yb2.b], [yb.b])
                self.tt('dve', m[:, 512:1024], yb[:], g[:, 512:1024], ALU.mult, [yb.b, g.b], [m.b])
                return m
            self.p3(p3, L, x_src, x_dst, wout, mixfn)
            S.barrier()

    def mla_attn(self, es):
        T, NT = self.T, self.NT
        qT = [self.sb(es, f'aqT{i}', [96, T], BF16) for i in range(2)]
        kT = [self.sb(es, f'akT{i}', [96, T], BF16) for i in range(2)]
        V = [self.sb(es, f'aV{i}', [128, NT, 65], BF16) for i in range(2)]
        ost = [self.sb(es, f'aost{i}', [128, 4, 64], F32) for i in range(2)]
        rec = self.sb(es, 'arec', [128, 4], F32)
        scale = 96 ** -0.5
        blk = 0
        for h in range(8):
            q, k, v = qT[h % 2], kT[h % 2], V[h % 2]
            self.dma('sp', q[:], self.mla_qT[h], [self.mla_qT.b], [q.b])
            self.dma('sp', k[:], self.mla_kT[h], [self.mla_kT.b], [k.b])
            self.dma('sp', v[:], self.mla_v[h], [self.mla_v.b], [v.b])
            for cq in range(T // 512):
                set_i = blk % 2
                blk += 1
                accb = self.ps[3 + set_i]
                views = {j: (accb[:, j * 65:(j + 1) * 65], 0) for j in range(4)}
                tiles = []
                for kt in range(4 * cq + 4):
                    vv = kt - 4 * cq
                    tl = dict(kT=k[:, kt * 128:(kt + 1) * 128], V=v[:, kt, :], R=[k.b, v.b],
                              c0=max(0, vv) * 128, subs=list(range(max(0, vv), 4)))
                    if vv >= 0:
                        tl['masks'] = [(self.tri_neg[:], self.ident[:], vv * 128, 128,
                                        [self.tri_neg.b, self.ident.b])]
                    tiles.append(tl)
                def epi(accb=accb, o=ost[set_i], cq=cq, h=h):
                    av = accb[:, 0:260].rearrange("p (j c) -> p j c", j=4)
                    self.op_('dve', lambda e, av=av: e.reciprocal(out=rec[:], in_=av[:, :, 64]), reads=[accb.b],
                             writes=[rec.b])
                    self.tt('dve', o[:], av[:, :, 0:64], rec[:].unsqueeze(2).to_broadcast([128, 4, 64]), ALU.mult,
                            [accb.b, rec.b], [o.b])
                    self.dma('sp', self.Y[cq * 512:(cq + 1) * 512, h * 64:(h + 1) * 64]
                             .rearrange("(j p) c -> p j c", p=128), o[:], [o.b], [self.Y.b])
                self.attn_block(q[:, cq * 512:(cq + 1) * 512], [q.b], 512, tiles, scale, views, [accb.b], epilogue=epi)
        self.attn_flush()

    def nsa_compress(self, li, kcmpT, vcmp):
        w = self.w
        T = self.T
        n_cmp = (T - 32) // 16 + 1
        ncp = self.ncmp_pad
        nct = ncp // 128
        self.memset('pool', kcmpT[:], 0.0, [kcmpT.b])
        self.memset('pool', vcmp[:], 0.0, [vcmp.b])
        with ExitStack() as es:
            w1 = self.sb(es, 'cw1', [64, 2, 32, 64], BF16)
            w2 = self.sb(es, 'cw2', [64, 2, 64], BF16)
            peT = self.sb(es, 'cpeT', [64, 2, 32], BF16)
            for kv in range(2):
                self.dma('pool', w1[:, kv], w['nsa_cmp_w1'][li, kv].rearrange("(l d) o -> d l o", d=64), [], [w1.b])
                self.dma('pool', w2[:, kv], w['nsa_cmp_w2'][li, kv], [], [w2.b])
                self.dma('pool', peT[:, kv], w['nsa_cmp_posT'][li, kv], [], [peT.b])
            g_kc = self.bc_load(es, 'g_kc', w['nsa_k_norm_g'][li, 0:1, :], 64)
            ovl = self.sb(es, 'ovl', [128, nct, self.n_blk], BF16)
            self.dma('sp', ovl[:], self.c['overlap'].rearrange("(k p) j -> p k j", p=128), [], [ovl.b])
            xT = [self.sb(es, f'cxT{i}', [64, T], BF16) for i in range(2)]
            bias = self.sb(es, 'cbias', [64, 1], F32)
            hid = self.sb(es, 'chid', [64, ncp], BF16)
            self.memset('pool', hid[:], 0.0, [hid.b])
            ctm = self.sb(es, 'ctm', [128, 64], F32)
            ctn = self.sb(es, 'ctn', [128, 64], F32)
            ctb = self.sb(es, 'ctb', [128, 64], BF16)
            rp = self.sb(es, 'crp', [128, 64], F32)
            it = 0
            for kv in range(2):
                for g in range(2):
                    x = xT[it % 2]
                    it += 1
                    self.dma('sp', x[:], self.nsa_kT[4 + kv * 2 + g], [self.nsa_kT.b], [x.b])
                    pH = self.ps[it % 2]
                    for l in range(32):
                        self.mm(pH[0:64, 0:n_cmp], w1[:, kv, l, :], x[:, l:l + 16 * (n_cmp - 1) + 1:16],
                                l == 0, False, [w1.b, x.b], [pH.b])
                        self.mm(pH[0:64, 511:512], w1[:, kv, l, :], peT[:, kv, l:l + 1], False, l == 31,
                                [w1.b, peT.b], [pH.b])
                    self.cp('dve', bias[:], pH[0:64, 511:512], [pH.b], [bias.b])
                    self.act(hid[:, 0:n_cmp], pH[0:64, 0:n_cmp], AF.Silu, [pH.b, bias.b], [hid.b], bias=bias[:, 0:1])
                    for kt in range(nct):
                        pO = self.ps[2 + kt % 2]
                        self.mm(pO[:, 0:64], hid[:, kt * 128:(kt + 1) * 128], w2[:, kv, :], True, True,
                                [hid.b, w2.b], [pO.b])
                        if kv == 1:
                            self.cp('act', vcmp[:, g, kt, 0:64], pO[:, 0:64], [pO.b], [vcmp.b])
                        else:
                            self.cp('act', ctm[:], pO[:, 0:64], [pO.b], [ctm.b])
                            self.rmsn(ctm[:].rearrange("p (h c) -> p h c", h=1), 1, 64, g_kc,
                                      ctn[:].rearrange("p (h c) -> p h c", h=1), [ctm.b], [ctn.b])
                            nrow = min(128, n_cmp - kt * 128)
                            r0 = 31 + 16 * 128 * kt
                            self.memset('dve', rp[:], 0.0, [rp.b])
                            self.dma('sp', rp[0:nrow, :], self.rope_d[r0:r0 + 16 * (nrow - 1) + 1:16, :],
                                     [self.rope_d.b], [rp.b])
                            A, Bm = self.sc_ra, self.sc_rb
                            c2 = ctn[:].rearrange("p (two c) -> p two c", two=2)
                            o2 = ctb[:].rearrange("p (two c) -> p two c", two=2)
                            Av = A[:, 0:64].rearrange("p (two c) -> p two c", two=2)
                            Bv = Bm[:, 0:64].rearrange("p (two c) -> p two c", two=2)
                            self.tt('dve', Av, c2, rp[:, 0:32].unsqueeze(1).to_broadcast([128, 2, 32]), ALU.mult,
                                    [ctn.b, rp.b], [A.b])
                            self.tt('dve', Bv, c2, rp[:, 32:64].unsqueeze(1).to_broadcast([128, 2, 32]), ALU.mult,
                                    [ctn.b, rp.b], [Bm.b])
                            self.tt('dve', o2[:, 0, :], Av[:, 0, :], Bv[:, 1, :], ALU.subtract, [A.b, Bm.b], [ctb.b])
                            self.tt('dve', o2[:, 1, :], Bv[:, 0, :], Av[:, 1, :], ALU.add, [A.b, Bm.b], [ctb.b])
                            pT = self.ps[4]
                            pTb = pT[:].bitcast(BF16)
                            self.tr(pTb[0:64, 0:128], ctb[:], self.ident[:], [ctb.b, self.ident.b], [pT.b])
                            self.cp('act', kcmpT[:, g, kt * 128:(kt + 1) * 128], pTb[0:64, 0:128], [pT.b], [kcmpT.b])
            for g in range(2):
                for kt in range(nct):
                    self.memset('pool', vcmp[:, g, kt, 64:65], 1.0, [vcmp.b])
                    self.cp('pool', vcmp[:, g, kt, 65:65 + self.n_blk], ovl[:, kt, :], [ovl.b], [vcmp.b])
            self.S.barrier()

    def nsa_attn(self, es, kcmpT, vcmp):
        T, NT = self.T, self.NT
        nb = self.n_blk
        nct = self.ncmp_pad // 128
        dvc = 65 + nb
        ksT = self.sb(es, 'ksT', [64, T], BF16)
        kwT = self.sb(es, 'kwT', [64, T], BF16)
        vs = self.sb(es, 'vs', [128, NT, 65], BF16)
        vw = self.sb(es, 'vw', [128, NT, 65], BF16)
        qt_ = [self.sb(es, f'nq{i}', [64, 512], BF16) for i in range(2)]
        cneg = [self.sb(es, f'cneg{i}', [128, self.ncmp_pad], BF16) for i in range(2)]
        forced = [self.sb(es, f'forced{i}', [128, nb], F32) for i in range(2)]
        negm = [self.sb(es, f'negm{i}', [128, T], BF16) for i in range(2)]
        rec = self.sb(es, 'nrec', [128, 4], F32)
        score = self.sb(es, 'nscore', [128, nb], F32)
        work = self.sb(es, 'nwork', [128, nb], F32)
        m8 = self.sb(es, 'nm8', [128, 8], F32)
        thr = self.sb(es, 'nthr', [128, 1], F32)
        nsel = self.sb(es, 'nsel', [128, nb], BF16)
        ost = [self.sb(es, f'nost{i}', [128, 3, 4, 64], F32) for i in range(2)]
        accA, accB, accS, accW = self.ps[3], self.ps[4], self.ps[5], self.ps[6]
        cmp_eid = {}

        def stage1(g, qt, sl):
            q, cn, fo, nm, o = qt_[sl], cneg[sl], forced[sl], negm[sl], ost[sl]
            tsl = slice(qt * 128, (qt + 1) * 128)
            self.dma('sp', q[:], self.nsa_qT[g, qt].rearrange("d r t -> d (r t)"), [self.nsa_qT.b], [q.b])
            self.dma('sp', cn[:], self.c['cmpneg'][tsl, :], [], [cn.b])
            self.dma('sp', fo[:], self.c['forced'][tsl, :], [], [fo.b])
            views = {j: ((accA if j < 2 else accB)[:, (j % 2) * dvc:(j % 2 + 1) * dvc], j // 2) for j in range(4)}
            tiles = []
            for kt in range(nct):
                if 16 * 128 * kt + 31 > qt * 128 + 127:
                    continue
                tiles.append(dict(kT=kcmpT[:, g, kt * 128:(kt + 1) * 128], V=vcmp[:, g, kt, 0:dvc],
                                  R=[kcmpT.b, vcmp.b], c0=0, subs=[0, 1, 2, 3],
                                  masks=[(cn[:, kt * 128:(kt + 1) * 128], self.i4[:], 0, 512, [cn.b, self.i4.b])]))
            if not tiles:
                tiles.append(dict(kT=kcmpT[:, g, 0:128], V=vcmp[:, g, 0, 0:dvc], R=[kcmpT.b, vcmp.b], c0=0,
                                  subs=[0, 1, 2, 3],
                                  masks=[(cn[:, 0:128], self.i4[:], 0, 512, [cn.b, self.i4.b])]))
            cmp_tiles = tiles
            viewsW = {j: (accW[:, j * 65:(j + 1) * 65], 0) for j in range(4)}
            tiles = []
            for kt in range(max(0, qt - 4), qt + 1):
                tl = dict(kT=kwT[:, kt * 128:(kt + 1) * 128], V=vw[:, kt, :], R=[kwT.b, vw.b], c0=0,
                          subs=[0, 1, 2, 3], masks=[])
                if kt == qt:
                    tl['masks'].append((self.tri_neg[:], self.i4[:], 0, 512, [self.tri_neg.b, self.i4.b]))
                if kt == qt - 4:
                    tl['masks'].append((self.edge_neg[:], self.i4[:], 0, 512, [self.edge_neg.b, self.i4.b]))
                tiles.append(tl)
            win_tiles = tiles

            def epi_cmp():
                for j in range(4):
                    ab = accA if j < 2 else accB
                    v_ = views[j][0]
                    self.ts('dve', rec[:, j:j + 1], v_[:, 64:65], 1e-30, ALU.max, [ab.b], [rec.b])
                self.op_('dve', lambda e: e.reciprocal(out=rec[:], in_=rec[:]), reads=[rec.b], writes=[rec.b])
                for j in range(4):
                    ab = accA if j < 2 else accB
                    v_ = views[j][0]
                    self.ts('dve', o[:, 0, j, :], v_[:, 0:64], rec[:, j:j + 1], ALU.mult, [ab.b, rec.b], [o.b])
                    if j == 0:
                        self.ts('dve', score[:], v_[:, 65:65 + nb], rec[:, 0:1], ALU.mult, [ab.b, rec.b], [score.b])
                    else:
                        self.stt(score[:], v_[:, 65:65 + nb], rec[:, j:j + 1], score[:], ALU.mult, ALU.add,
                                 [ab.b, rec.b, score.b], [score.b])
                self.tt('dve', score[:], score[:], fo[:], ALU.add, [score.b, fo.b], [score.b])
                self.op_('dve', lambda e: e.max(out=m8[:], in_=score[:]), reads=[score.b], writes=[m8.b])
                self.op_('dve', lambda e: e.match_replace(out=work[:], in_to_replace=m8[:], in_values=score[:],
                                                           imm_value=-3e38), reads=[score.b, m8.b], writes=[work.b])
                self.op_('dve', lambda e: e.max(out=m8[:], in_=work[:]), reads=[work.b], writes=[m8.b])
                self.ts('dve', thr[:], m8[:, 7:8], -1e29, ALU.max, [m8.b], [thr.b])
                self.ts('dve', nsel[:], score[:], thr[:, 0:1], ALU.is_lt, [score.b, thr.b], [nsel.b], s2=-BIG, op1=ALU.mult)
                nblk_need = (qt + 1) * 2
                self.cp('pool', nm[:, 0:nblk_need * 64].rearrange("p (j c) -> p j c", c=64),
                        nsel[:, 0:nblk_need].unsqueeze(2).to_broadcast([128, nblk_need, 64]), [nsel.b], [nm.b])
                self.tt('pool', nm[:, tsl], nm[:, tsl], self.tri_neg[:], ALU.add, [nm.b, self.tri_neg.b], [nm.b])

            def epi_win():
                av = accW[:, 0:260].rearrange("p (j c) -> p j c", j=4)
                self.op_('dve', lambda e, av=av: e.reciprocal(out=rec[:], in_=av[:, :, 64]), reads=[accW.b], writes=[rec.b])
                self.tt('dve', o[:, 2], av[:, :, 0:64], rec[:].unsqueeze(2).to_broadcast([128, 4, 64]), ALU.mult,
                        [accW.b, rec.b], [o.b])

            eid = self.attn_block(q[:], [q.b], 512, cmp_tiles, 0.125, views, [accA.b, accB.b], epilogue=epi_cmp)
            self.attn_block(q[:], [q.b], 512, win_tiles, 0.125, viewsW, [accW.b], epilogue=epi_win)
            cmp_eid[(g, qt)] = eid

        def stage2(g, qt, sl):
            q, nm, o = qt_[sl], negm[sl], ost[sl]
            tsl = slice(qt * 128, (qt + 1) * 128)
            self.attn_sync(cmp_eid[(g, qt)])
            views = {j: (accS[:, j * 65:(j + 1) * 65], 0) for j in range(4)}
            tiles = [dict(kT=ksT[:, kt * 128:(kt + 1) * 128], V=vs[:, kt, :], R=[ksT.b, vs.b], c0=0,
                          subs=[0, 1, 2, 3],
                          masks=[(nm[:, kt * 128:(kt + 1) * 128], self.i4[:], 0, 512, [nm.b, self.i4.b])])
                     for kt in range(qt + 1)]
            def epi_sel():
                av = accS[:, 0:260].rearrange("p (j c) -> p j c", j=4)
                self.op_('dve', lambda e, av=av: e.reciprocal(out=rec[:], in_=av[:, :, 64]), reads=[accS.b], writes=[rec.b])
                self.tt('dve', o[:, 1], av[:, :, 0:64], rec[:].unsqueeze(2).to_broadcast([128, 4, 64]), ALU.mult,
                        [accS.b, rec.b], [o.b])
                for bi, base in enumerate((1280, 512, 1792)):
                    self.dma('sp', self.Y[tsl, base + g * 256:base + (g + 1) * 256],
                             o[:, bi].rearrange("p r c -> p (r c)"), [o.b], [self.Y.b])
            self.attn_block(q[:], [q.b], 512, tiles, 0.125, views, [accS.b], epilogue=epi_sel)

        for g in range(2):
            self.dma('sp', ksT[:], self.nsa_kT[0 + g], [self.nsa_kT.b], [ksT.b])
            self.dma('sp', kwT[:], self.nsa_kT[2 + g], [self.nsa_kT.b], [kwT.b])
            self.dma('sp', vs[:], self.nsa_v[0 + g], [self.nsa_v.b], [vs.b])
            self.dma('sp', vw[:], self.nsa_v[2 + g], [self.nsa_v.b], [vw.b])
            stage1(g, 0, 0)
            for qt in range(NT):
                if qt + 1 < NT:
                    stage1(g, qt + 1, (qt + 1) % 2)
                stage2(g, qt, qt % 2)
            self.attn_flush()

    def layer_odd(self, es, L, x_src, x_dst):
        li = L // 2
        w = self.w
        T, NT = self.T, self.NT
        S = self.S
        memkT, memV = self.mem_kv(es, L)
        with ExitStack() as p1:
            win, lng = self.load_win(p1, L, w['odd_w_in'][li], ODD_COLS)
            g_cq = self.bc_load(p1, 'g_cq', w['dsa_q_norm_g'][li:li + 1, :], 64)
            g_ck = self.bc_load(p1, 'g_ck', w['dsa_k_norm_g'][li:li + 1, :], 64)
            g_mq = self.bc_load(p1, 'g_mq', w['mem_q_norm_g'][L:L + 1, :], 64)
            xt = [self.sb(p1, f'xt{i}', [128, D], F32) for i in range(2)]
            ht = [self.sb(p1, f'ht{i}', [128, D], BF16) for i in range(2)]
            hT = [self.sb(p1, f'hT{i}', [128, 8, 128], BF16) for i in range(2)]
            u = [self.sb(p1, f'u{i}', [128, ODD_COLS], F32) for i in range(2)]
            ss = self.sb(p1, 'ss', [128, 1], F32)
            junk = self.sb(p1, 'junk', [128, D], BF16)
            gt = self.sb(p1, 'gt', [128, 1792], BF16)
            cqn = self.sb(p1, 'cqn', [128, 512], F32)
            cqf = self.sb(p1, 'cqf', [128, 512], BF16)
            ckn = self.sb(p1, 'ckn', [128, 64], F32)
            ckf = self.sb(p1, 'ckf', [128, 64], BF16)
            cva = self.sb(p1, 'cva', [128, 65], BF16)
            self.memset('pool', cva[:], 1.0, [cva.b])
            iqf = self.sb(p1, 'iqf', [128, 256], BF16)
            ikf = self.sb(p1, 'ikf', [128, 32], BF16)
            iwt = self.sb(p1, 'iwt', [128, 8], F32)
            mlv = self.sb(p1, 'mlv', [128, 4, 129], BF16)
            self.memset('pool', mlv[:], 1.0, [mlv.b])
            mqf = self.sb(p1, 'mqf', [128, 256], BF16)
            stq = self.sb(p1, 'stq', [64, 8, 128], BF16)
            stk = self.sb(p1, 'stk', [64, 1, 128], BF16)
            sti = self.sb(p1, 'sti', [32, 8, 128], BF16)
            stik = self.sb(p1, 'stik', [32, 1, 128], BF16)
            stmq = self.sb(p1, 'stmq', [64, 4, 128], BF16)
            strw = self.sb(p1, 'strw', [128, 4, 128], F32)
            stif = self.sb(p1, 'stif', [8, 128], F32)
            scrA = self.new_scr(p1, 'A', 512)
            scrB = self.new_scr(p1, 'B', 256)
            ut_next = self.p1_front(0, x_src, xt, ht, hT, u, ss, junk, lng, win, ODD_COLS)
            for n in range(NT):
                ut = ut_next
                U = lambda a, b_, ut=ut: ut[:, a:b_]
                ub = [ut.b]
                tsl = slice(n * 128, (n + 1) * 128)
                self.chains_begin(['A', 'F', 'B', 'L', 'M', 'G'])
                if n + 1 < NT:
                    self.chain('F')
                    ut_next = self.p1_front(n + 1, x_src, xt, ht, hT, u, ss, junk, lng, win, ODD_COLS)
                self.chain('G')
                self.act(gt[:, 0:512], U(936, 1448), AF.Silu, ub, [gt.b])
                self.act(gt[:, 512:1024], U(2992, 3504), AF.Silu, ub, [gt.b])
                self.act(gt[:, 1024:1280], U(3760, 4016), AF.Silu, ub, [gt.b])
                self.act(gt[:, 1280:1792], U(2480, 2992), AF.Sigmoid, ub, [gt.b])
                self.dma('sp', self.G[tsl, :], gt[:], [gt.b], [self.G.b[n]])
                self.chain('A', scrA)
                cq3 = cqn[:].rearrange("p (h c) -> p h c", h=8)
                self.rmsn(U(0, 512).rearrange("p (h c) -> p h c", h=8), 8, 64, g_cq, cq3, ub, [cqn.b])
                self.rope(cq3, 8, 32, n, cqf[:].rearrange("p (h c) -> p h c", h=8), [cqn.b], [cqf.b])
                ck3 = ckn[:].rearrange("p (h c) -> p h c", h=1)
                self.rmsn(U(512, 576).rearrange("p (h c) -> p h c", h=1), 1, 64, g_ck, ck3, ub, [ckn.b])
                self.rope(ck3, 1, 32, n, ckf[:].rearrange("p (h c) -> p h c", h=1), [ckn.b], [ckf.b])
                self.cp('dve', cva[:, 0:64], U(576, 640), ub, [cva.b])
                self.dma('sp', self.dsa_v[:, n, :], cva[:], [cva.b], [self.dsa_v.b])
                pT = self.ps[4]
                pTb = pT[:].bitcast(BF16)
                for h in range(8):
                    self.tr(pTb[0:64, h * 128:(h + 1) * 128], cqf[:, h * 64:(h + 1) * 64], self.ident[:],
                            [cqf.b, self.ident.b], [pT.b])
                self.cp('act', stq[:], pTb[0:64, 0:1024].rearrange("p (k c) -> p k c", k=8), [pT.b], [stq.b])
                self.dma('sp', self.dsa_qT[n], stq[:], [stq.b], [self.dsa_qT.b])
                self.transposes_out([(ckf[:], [ckf.b])], 64, stk,
                                    self.dsa_kT[:, tsl].rearrange("d (k t) -> d k t", k=1), [self.dsa_kT.b], 5)
                self.chain('B', scrB)
                self.rope(U(640, 896).rearrange("p (h c) -> p h c", h=8), 8, 16, n,
                          iqf[:].rearrange("p (h c) -> p h c", h=8), ub, [iqf.b])
                self.rope(U(896, 928).rearrange("p (h c) -> p h c", h=1), 1, 16, n,
                          ikf[:].rearrange("p (h c) -> p h c", h=1), ub, [ikf.b])
                self.ts('dve', iwt[:], U(928, 936), 8 ** -0.5, ALU.mult, ub, [iwt.b])
                self.dma('sp', self.idx_w[tsl, :], iwt[:], [iwt.b], [self.idx_w.b])
                self.transposes_out([(iqf[:, h * 32:(h + 1) * 32], [iqf.b]) for h in range(8)], 32, sti,
                                    self.idx_qT[:, :, tsl].rearrange("h d t -> d h t"), [self.idx_qT.b], 6)
                self.transposes_out([(ikf[:], [ikf.b])], 32, stik,
                                    self.idx_kT[:, tsl].rearrange("d (k t) -> d k t", k=1), [self.idx_kT.b], 7)
                self.chain('L')
                pR = self.ps[3]
                for k in range(4):
                    self.tr(pR[:, k * 128:(k + 1) * 128], U(1448 + k * 128, 1448 + (k + 1) * 128), self.identf[:],
                            ub + [self.identf.b], [pR.b])
                self.cp('act', strw[:].rearrange("p k c -> p (k c)"), pR[:, 0:512], [pR.b], [strw.b])
                self.dma('sp', self.ml_raw[:, tsl].rearrange("(k p) t -> p k t", p=128), strw[:], [strw.b],
                         [self.ml_raw.b])
                pI = self.ps[3]
                self.tr(pI[0:8, 0:128], U(2472, 2480), self.identf[:], ub + [self.identf.b], [pI.b])
                self.cp('act', stif[:], pI[0:8, 0:128], [pI.b], [stif.b])
                self.dma('sp', self.ml_if[:, tsl], stif[:], [stif.b], [self.ml_if.b])
                self.cp('dve', mlv[:, :, 0:128], U(1960, 2472).rearrange("p (h c) -> p h c", h=4), ub, [mlv.b])
                self.dma('sp', self.ml_v[:, :, n, :].rearrange("h p c -> p h c"), mlv[:], [mlv.b], [self.ml_v.b])
                self.chain('B', scrB)
                self.rmsn(U(3504, 3760).rearrange("p (h c) -> p h c", h=4), 4, 64, g_mq,
                          mqf[:].rearrange("p (h c) -> p h c", h=4), ub, [mqf.b])
                self.transposes_out([(mqf[:, h * 64:(h + 1) * 64], [mqf.b]) for h in range(4)], 64, stmq,
                                    self.mem_qT[:, :, tsl].rearrange("h d t -> d h t"), [self.mem_qT.b], 6)
                self.chains_emit()
            S.barrier()
        if 'stop_p1' in self.dbg:
            return
        self.mlstm_pre(li)
        S.barrier()
        if 'stop_pre' in self.dbg:
            return
        with ExitStack() as pa:
            self.attn_setup(pa)
            self.mlstm_attn(pa)
            S.barrier()
        if 'stop_ml' in self.dbg:
            return
        with ExitStack() as pa:
            self.attn_setup(pa)
            self.mem_attn(pa, memkT, memV)
            S.barrier()
        if 'stop_mem' in self.dbg:
            return
        with ExitStack() as pa:
            self.attn_setup(pa)
            self.dsa_attn(pa)
            S.barrier()
        if 'stop_dsa' in self.dbg:
            return
        with ExitStack() as p3:
            wout = self.load_wout(p3, L)
            g_h = self.bc_load(p3, 'g_h', w['mlstm_h_norm_g'][li:li + 1, :], 128)
            yt = [self.sb(p3, f'yt{i}', [128, 1280], F32) for i in range(2)]
            gtt = [self.sb(p3, f'gtt{i}', [128, 1792], BF16) for i in range(2)]
            mix = [self.sb(p3, f'mix{i}', [128, 1280], BF16) for i in range(2)]
            hns = [self.sb(p3, f'hn{i}', [128, 512], F32) for i in range(2)]
            scrP = [self.new_scr(p3, f'P{i}', 512) for i in range(2)]

            def mixfn(n):
                sl = n % 2
                y, g, m = yt[sl], gtt[sl], mix[sl]
                hn = hns[sl]
                self.scr = scrP[sl]
                tsl = slice(n * 128, (n + 1) * 128)
                self.dma('sp', y[:], self.Y[tsl, 0:1280], [self.Y.b], [y.b])
                self.dma('sp', g[:], self.G[tsl, :], [self.G.b[n]], [g.b])
                self.tt('dve', m[:, 0:512], y[:, 0:512], g[:, 0:512], ALU.mult, [y.b, g.b], [m.b])
                self.tt('pool', m[:, 1024:1280], y[:, 1024:1280], g[:, 1024:1280], ALU.mult, [y.b, g.b], [m.b])
                self.rmsn(y[:, 512:1024].rearrange("p (h c) -> p h c", h=4), 4, 128, g_h,
                          hn[:].rearrange("p (h c) -> p h c", h=4), [y.b], [hn.b])
                self.tt('dve', hn[:], hn[:], g[:, 1280:1792], ALU.mult, [hn.b, g.b], [hn.b])
                self.tt('dve', m[:, 512:1024], hn[:], g[:, 512:1024], ALU.mult, [hn.b, g.b], [m.b])
                return m
            self.p3(p3, L, x_src, x_dst, wout, mixfn)
            S.barrier()

    def mlstm_pre(self, li):
        w = self.w
        T = self.T
        with ExitStack() as es:
            xp = [self.sb(es, f'xp{i}', [128, T + 3], F32) for i in range(2)]
            y = self.sb(es, 'cy', [128, T], F32)
            yo = [self.sb(es, f'cyo{i}', [128, T], BF16) for i in range(2)]
            wc = self.sb(es, 'cwc', [128, 4, 4], F32)
            bc = self.sb(es, 'cbc', [128, 4], F32)
            for ck in range(4):
                self.dma('sp', wc[:, ck, :], w['mlstm_conv_wT'][li, ck * 128:(ck + 1) * 128, :], [], [wc.b])
                self.dma('sp', bc[:, ck:ck + 1], w['mlstm_conv_b'][li, ck * 128:(ck + 1) * 128].unsqueeze(1), [], [bc.b])
            for ck in range(4):
                x = xp[ck % 2]
                o = yo[ck % 2]
                self.memset('pool', x[:, 0:3], 0.0, [x.b])
                self.dma('sp', x[:, 3:T + 3], self.ml_raw[ck * 128:(ck + 1) * 128, :], [self.ml_raw.b], [x.b])
                self.ts('dve', y[:], x[:, 0:T], wc[:, ck, 0:1], ALU.mult, [x.b, wc.b, bc.b], [y.b],
                        s2=bc[:, ck:ck + 1], op1=ALU.add)
                for j in range(1, 4):
                    self.stt(y[:], x[:, j:j + T], wc[:, ck, j:j + 1], y[:], ALU.mult, ALU.add, [x.b, wc.b, y.b], [y.b])
                self.act(o[:], y[:], AF.Silu, [y.b], [o.b])
                self.dma('sp', self.ml_qkT[ck * 128:(ck + 1) * 128, :], o[:], [o.b], [self.ml_qkT.b])
            self.S.barrier()
        with ExitStack() as es:
            ig = self.sb(es, 'ig', [4, T], F32)
            fg = self.sb(es, 'fg', [4, T], F32)
            cs = self.sb(es, 'cs', [4, T], F32)
            a = self.sb(es, 'ga', [4, T], F32)
            Mt = self.sb(es, 'gM', [4, T], F32)
            ones = self.sb(es, 'gones', [4, T], F32)
            ib = self.sb(es, 'gib', [4, 1], F32)
            fb = self.sb(es, 'gfb', [4, 1], F32)
            self.memset('pool', ones[:], 1.0, [ones.b])
            self.dma('sp', ig[:], self.ml_if[0:4, :], [self.ml_if.b], [ig.b])
            self.dma('sp', fg[:], self.ml_if[4:8, :], [self.ml_if.b], [fg.b])
            self.dma('sp', ib[:], w['mlstm_i_bias'][li].unsqueeze(1), [], [ib.b])
            self.dma('sp', fb[:], w['mlstm_f_bias'][li].unsqueeze(1), [], [fb.b])
            self.ts('dve', fb[:], fb[:], -1.0, ALU.mult, [fb.b], [fb.b])
            self.act(fg[:], fg[:], AF.Exp, [fg.b, fb.b], [fg.b], bias=fb[:, 0:1], scale=-1.0)
            self.act(fg[:], fg[:], AF.Ln, [fg.b], [fg.b], bias=self.one_c[0:4, 0:1], scale=1.0)
            self.op_('dve', lambda e: e.tensor_tensor_scan(out=cs[:], data0=ones[:], data1=fg[:], initial=0.0,
                                                             op0=ALU.mult, op1=ALU.add),
                      reads=[ones.b, fg.b], writes=[cs.b])
            self.stt(a[:], ig[:], ib[:, 0:1], cs[:], ALU.add, ALU.add, [ig.b, ib.b, cs.b], [a.b])
            self.op_('dve', lambda e: e.tensor_tensor_scan(out=Mt[:], data0=a[:], data1=a[:], initial=0.0,
                                                             op0=ALU.max, op1=ALU.max),
                      reads=[a.b], writes=[Mt.b])
            self.tt('dve', cs[:], cs[:], Mt[:], ALU.subtract, [cs.b, Mt.b], [cs.b])
            self.act(cs[:], cs[:], AF.Exp, [cs.b], [cs.b])
            self.ts('dve', Mt[:], Mt[:], -1.0, ALU.mult, [Mt.b], [Mt.b])
            self.dma('sp', self.ml_g[0:4, :], a[:], [a.b], [self.ml_g.b])
            self.dma('sp', self.ml_g[4:8, :], Mt[:], [Mt.b], [self.ml_g.b])
            self.dma('sp', self.ml_g[8:12, :], cs[:], [cs.b], [self.ml_g.b])
            self.S.barrier()

    def mlstm_attn(self, es):
        T, NT = self.T, self.NT
        qT = [self.sb(es, f'lqT{i}', [64, T], BF16) for i in range(2)]
        kT = [self.sb(es, f'lkT{i}', [64, T], BF16) for i in range(2)]
        V = [self.sb(es, f'lV{i}', [128, NT, 129], BF16) for i in range(2)]
        nM = [self.sb(es, f'lnM{i}', [128, T], F32) for i in range(2)]
        ant = self.sb(es, 'lant', [NT, 2, 128], F32)
        atm = [self.sb(es, f'latm{i}', [128, 2, NT], F32) for i in range(2)]
        Et = [self.sb(es, f'lEt{i}', [128, 512], F32) for i in range(3)]
        ost = [self.sb(es, f'lost{i}', [128, 4, 128], F32) for i in range(2)]
        d2 = self.sb(es, 'ld2', [128, 4], F32)
        ecnt = [0]
        blk = 0
        LN8 = math.log(0.125)
        for h in range(4):
            q, k, v, nm, at = qT[h % 2], kT[h % 2], V[h % 2], nM[h % 2], atm[h % 2]
            self.dma('sp', q[:], self.ml_qkT[h * 64:(h + 1) * 64, :], [self.ml_qkT.b], [q.b])
            self.dma('sp', k[:], self.ml_qkT[256 + h * 64:256 + (h + 1) * 64, :], [self.ml_qkT.b], [k.b])
            self.dma('sp', v[:], self.ml_v[h], [self.ml_v.b], [v.b])
            self.dma('sp', nm[:], self.ml_g[4 + h:5 + h, :].to_broadcast([128, T]), [self.ml_g.b], [nm.b])
            self.dma('sp', ant[:, 0, :], self.ml_g[h, :].rearrange("(n p) -> n p", p=128), [self.ml_g.b], [ant.b])
            self.dma('sp', ant[:, 1, :], self.ml_g[8 + h, :].rearrange("(n p) -> n p", p=128), [self.ml_g.b], [ant.b])
            pA = self.ps[7]
            for i in range(2):
                self.tr(pA[:, i * NT:(i + 1) * NT], ant[:, i, :], self.identf[0:NT, 0:NT], [ant.b, self.identf.b], [pA.b])
            self.cp('act', at[:].rearrange("p a n -> p (a n)"), pA[:, 0:2 * NT], [pA.b], [at.b])
            self.ts('dve', at[:, 0, :], at[:, 0, :], LN8, ALU.add, [at.b], [at.b])
            for cq in range(T // 512):
                set_i = blk % 2
                blk += 1
                accA, accB = self.ps[3 + 2 * set_i], self.ps[4 + 2 * set_i]
                views = {j: ((accA if j < 2 else accB)[:, (j % 2) * 129:(j % 2 + 1) * 129], j // 2) for j in range(4)}
                tiles = []
                for kt in range(4 * cq + 4):
                    vv = kt - 4 * cq
                    c0 = max(0, vv) * 128

                    def efn(kt=kt, c0=c0, cq=cq, nm=nm, at=at):
                        E = Et[ecnt[0] % 3]
                        ecnt[0] += 1
                        self.act(E[:, c0:512], nm[:, cq * 512 + c0:(cq + 1) * 512], AF.Exp, [nm.b, at.b], [E.b],
                                 bias=at[:, 0, kt:kt + 1], scale=1.0)
                        return E, [E.b]
                    tl = dict(kT=k[:, kt * 128:(kt + 1) * 128], V=v[:, kt, :], R=[k.b, v.b], c0=c0,
                              subs=list(range(max(0, vv), 4)), Efn=efn)
                    if vv >= 0:
                        tl['diag'] = vv
                    tiles.append(tl)
                def epi(accA=accA, accB=accB, views=views, o=ost[set_i], cq=cq, h=h, at=at):
                    for j in range(4):
                        ab = accA if j < 2 else accB
                        v_ = views[j][0]
                        self.act(d2[:, j:j + 1], v_[:, 128:129], AF.Abs, [ab.b], [d2.b])
                    self.tt('dve', d2[:], d2[:], at[:, 1, 4 * cq:4 * cq + 4], ALU.max, [d2.b, at.b], [d2.b])
                    self.op_('dve', lambda e: e.reciprocal(out=d2[:], in_=d2[:]), reads=[d2.b], writes=[d2.b])
                    for j in range(4):
                        ab = accA if j < 2 else accB
                        v_ = views[j][0]
                        self.ts('dve', o[:, j, :], v_[:, 0:128], d2[:, j:j + 1], ALU.mult, [ab.b, d2.b], [o.b])
                    self.dma('sp', self.Y[cq * 512:(cq + 1) * 512, 512 + h * 128:512 + (h + 1) * 128]
                             .rearrange("(j p) c -> p j c", p=128), o[:], [o.b], [self.Y.b])
                self.attn_block(q[:, cq * 512:(cq + 1) * 512], [q.b], 512, tiles, 1.0, views, [accA.b, accB.b],
                                mode='mul', epilogue=epi)
        self.attn_flush()

    def dsa_attn(self, es):
        T, NT = self.T, self.NT
        KSEL = min(256, T // 4)
        ikT = self.sb(es, 'ikT', [32, T], BF16)
        ckT = self.sb(es, 'ckT', [64, T], BF16)
        cv = self.sb(es, 'cv', [128, NT, 65], BF16)
        self.dma('sp', ikT[:], self.idx_kT[:, :], [self.idx_kT.b], [ikT.b])
        self.dma('sp', ckT[:], self.dsa_kT[:, :], [self.dsa_kT.b], [ckT.b])
        self.dma('sp', cv[:], self.dsa_v[:, :, :], [self.dsa_v.b], [cv.b])
        iq = [self.sb(es, f'iq{i}', [32, 8, 128], BF16) for i in range(2)]
        iw = [self.sb(es, f'iw{i}', [128, 8], F32) for i in range(2)]
        cq_ = [self.sb(es, f'cq{i}', [64, 2, 512], BF16) for i in range(2)]
        score2 = [self.sb(es, f'dscore{i}', [128, T], F32) for i in range(3)]
        thrA2 = [self.sb(es, f'dthrA{i}', [128, 1], F32) for i in range(2)]
        work = self.sb(es, 'dwork', [128, T], F32)
        negm = [self.sb(es, f'dnegm{i}', [128, T], BF16) for i in range(2)]
        rl = [self.sb(es, f'drl{i}', [128, 512], F32) for i in range(3)]
        m8 = self.sb(es, 'dm8', [128, 8], F32)
        thr = self.sb(es, 'dthr', [128, 1], F32)
        rec = self.sb(es, 'drec', [128, 4], F32)
        ost = [self.sb(es, f'dost{i}', [128, 4, 64], F32) for i in range(2)]
        rlc_ = [0]
        blk_ = [0]
        NBIS = 20
        junk = self.sb(es, 'djunk', [128, T], BF16)
        amax = self.sb(es, 'damax', [128, 1], F32)
        w0 = self.sb(es, 'dw0', [128, 1], F32)
        nHh = self.sb(es, 'dnHh', [128, 40], F32)
        nmid = [self.sb(es, f'dnmid{i}', [128, 1], F32) for i in range(2)]
        Ssum = self.sb(es, 'dS', [128, 1], F32)
        tsg = self.sb(es, 'dtsg', [128, 1], F32)

        def stage_a1(qt):
            rlc = rlc_[0]
            sl = qt % 2
            tsl = slice(qt * 128, (qt + 1) * 128)
            q_i, w_i = iq[sl], iw[sl]
            score = score2[qt % 3]
            self.dma('sp', q_i[:], self.idx_qT[:, :, tsl].rearrange("h d t -> d h t"), [self.idx_qT.b], [q_i.b])
            self.dma('sp', w_i[:], self.idx_w[tsl, :], [self.idx_w.b], [w_i.b])
            ncols = (qt + 1) * 128
            for c in range((ncols + 511) // 512):
                c0 = c * 512
                wd = min(512, ncols - c0)
                for h in range(8):
                    bi = self.st_rr % len(self.st_banks)
                    self.st_rr += 1
                    bank, bb = self.st_banks[bi]
                    self.mm(bank[:, 0:wd], q_i[:, h, :], ikT[:, c0:c0 + wd], True, True, [q_i.b, ikT.b], [bb])
                    if h == 0:
                        self.ts('dve', score[:, c0:c0 + wd], bank[:, 0:wd], 0.0, ALU.max, [bb, w_i.b], [score.b],
                                s2=w_i[:, 0:1], op1=ALU.mult)
                    else:
                        r = rl[rlc % 3]
                        rlc += 1
                        self.ts('dve', r[:, 0:wd], bank[:, 0:wd], 0.0, ALU.max, [bb, w_i.b], [r.b],
                                s2=w_i[:, h:h + 1], op1=ALU.mult)
                        self.tt('dve', score[:, c0:c0 + wd], score[:, c0:c0 + wd], r[:, 0:wd], ALU.add,
                                [score.b, r.b], [score.b])
            rlc_[0] = rlc

        def is_act_tile(qt):
            return ((qt + 1) * 128 > KSEL) and ('dsa_nobis' not in self.dbg)

        def stage_a2_finish(qt):
            sl = qt % 2
            nm = negm[sl]
            score = score2[qt % 3]
            ncols = (qt + 1) * 128
            th = thrA2[qt % 2] if is_act_tile(qt) else thr
            self.ts('dve', nm[:, 0:ncols], score[:, 0:ncols], th[:, 0:1], ALU.is_lt, [score.b, th.b], [nm.b],
                    s2=-BIG, op1=ALU.mult)

        def stage_a2(qt):
            sl = qt % 2
            tsl = slice(qt * 128, (qt + 1) * 128)
            q_c, nm = cq_[sl], negm[sl]
            score = score2[qt % 3]
            ncols = (qt + 1) * 128
            for half in range(2):
                self.dma('sp', q_c[:, half, :], self.dsa_qT[qt, :, half * 4:(half + 1) * 4, :].rearrange("d h t -> d (h t)"),
                         [self.dsa_qT.b], [q_c.b])
            use_act = is_act_tile(qt)
            if use_act:
                self.op_('dve', lambda e, ncols=ncols: e.tensor_reduce(out=amax[:], in_=score[:, 0:ncols], axis=AX.X,
                                                                      op=ALU.max, apply_absolute_value=True),
                         reads=[score.b], writes=[amax.b])
                self.ts('dve', w0[:], amax[:], 2.0, ALU.mult, [amax.b], [w0.b], s2=2.0, op1=ALU.add)
                self.ts('dve', nHh[:], self.pw[:], w0[:, 0:1], ALU.mult, [self.pw.b, w0.b], [nHh.b])
            self.tt('dve', score[:, tsl], score[:, tsl], self.trinegf[:], ALU.add, [score.b, self.trinegf.b], [score.b])
            if use_act:
                cconst = float(0.5 - (2 * KSEL - ncols - 1))
                self.memset('pool', nmid[0][:], 0.0, [nmid[0].b])
                for j in range(NBIS):
                    cur, nxt = nmid[j % 2], nmid[(j + 1) % 2]
                    self.act(junk[:, 0:ncols], score[:, 0:ncols], AF.Sign, [score.b, cur.b], [junk.b, Ssum.b],
                             bias=cur[:, 0:1], scale=1.0, accum=Ssum[:, 0:1])
                    self.act(tsg[:], Ssum[:], AF.Sign, [Ssum.b], [tsg.b], bias=cconst, scale=1.0)
                    self.act(nxt[:], tsg[:], AF.Identity, [tsg.b, cur.b, nHh.b], [nxt.b],
                             bias=cur[:, 0:1], scale=nHh[:, j:j + 1])
                fin = nmid[NBIS % 2]
                thrA = thrA2[qt % 2]
                self.act(thrA[:], fin[:], AF.Identity, [fin.b, nHh.b], [thrA.b], bias=nHh[:, NBIS - 1:NBIS], scale=-1.0)
                return
            elif ncols > KSEL and 'dsa_notopk' not in self.dbg:
                self.cp('pool', work[:, 0:ncols], score[:, 0:ncols], [score.b], [work.b])
                nr = KSEL // 8
                for r_ in range(nr):
                    self.op_('dve', lambda e, ncols=ncols: e.max(out=m8[:], in_=work[:, 0:ncols]),
                             reads=[work.b], writes=[m8.b])
                    if r_ < nr - 1:
                        self.op_('dve', lambda e, ncols=ncols: e.match_replace(
                            out=work[:, 0:ncols], in_to_replace=m8[:], in_values=work[:, 0:ncols], imm_value=-3e38),
                            reads=[work.b, m8.b], writes=[work.b])
                self.ts('dve', thr[:], m8[:, 7:8], -1e29, ALU.max, [m8.b], [thr.b])
            else:
                self.memset('dve', thr[:], -1e29, [thr.b])
            stage_a2_finish(qt)

        def stage_b(qt):
            blk = blk_[0]
            sl = qt % 2
            tsl = slice(qt * 128, (qt + 1) * 128)
            q_c, nm = cq_[sl], negm[sl]
            for half in range(2):
                set_i = blk % 2
                blk += 1
                accb = self.ps[3 + set_i]
                views = {j: (accb[:, j * 65:(j + 1) * 65], 0) for j in range(4)}
                tiles = [dict(kT=ckT[:, kt * 128:(kt + 1) * 128], V=cv[:, kt, :], R=[ckT.b, cv.b], c0=0,
                              subs=[0, 1, 2, 3],
                              masks=[(nm[:, kt * 128:(kt + 1) * 128], self.i4[:], 0, 512, [nm.b, self.i4.b])])
                         for kt in range(qt + 1)]
                def epi(accb=accb, o=ost[set_i], tsl=tsl, half=half):
                    av = accb[:, 0:260].rearrange("p (j c) -> p j c", j=4)
                    self.op_('dve', lambda e, av=av: e.reciprocal(out=rec[:], in_=av[:, :, 64]), reads=[accb.b], writes=[rec.b])
                    self.tt('dve', o[:], av[:, :, 0:64], rec[:].unsqueeze(2).to_broadcast([128, 4, 64]), ALU.mult,
                            [accb.b, rec.b], [o.b])
                    self.dma('sp', self.Y[tsl, half * 256:(half + 1) * 256], o[:].rearrange("p j c -> p (j c)"),
                             [o.b], [self.Y.b])
                self.attn_block(q_c[:, half, :], [q_c.b], 512, tiles, 0.125, views, [accb.b], epilogue=epi)
            blk_[0] = blk

        stage_a1(0)
        if NT > 1:
            stage_a1(1)
        stage_a2(0)
        for qt in range(NT):
            if qt + 2 < NT:
                stage_a1(qt + 2)
            self.chains_begin(['X', 'Y'])
            if qt + 1 < NT:
                self.chain('X')
                stage_a2(qt + 1)
            self.chain('Y')
            if is_act_tile(qt):
                stage_a2_finish(qt)
            stage_b(qt)
            self.chains_emit(proportional=True)
        self.attn_flush()


def host_consts(T):
    bf = ml_dtypes.bfloat16
    nb = T // 64
    ncp = max(1, T // 2048) * 128
    n_cmp = (T - 32) // 16 + 1
    c = {}
    c['c_ident'] = np.eye(128, dtype=np.float32).astype(bf)
    c['c_identf'] = np.eye(128, dtype=np.float32)
    c['c_i4'] = np.tile(np.eye(128, dtype=np.float32), (1, 4)).astype(bf)
    i8 = np.zeros((128, 512), np.float32)
    for p in range(128):
        for h in range(8):
            i8[p, h * 64 + p % 64] = 1.0
    c['c_i8x2'] = i8.astype(bf)
    t = np.arange(128)[:, None]
    s = np.arange(128)[None, :]
    c['c_tri_neg'] = np.where(s > t, -BIG, 0.0).astype(np.float32).astype(bf)
    c['c_edge_neg'] = np.where(s <= t, -BIG, 0.0).astype(np.float32).astype(bf)
    c['c_tri01T'] = np.where(t <= s, 1.0, 0.0).astype(np.float32).astype(bf)
    c['c_trinegf'] = np.where(s > t, -1e30, 0.0).astype(np.float32)
    inv = (10000.0 ** (-np.arange(32, dtype=np.float32) / 32)).astype(np.float32)
    c['c_invf'] = np.tile(inv[None, :], (128, 1)).astype(np.float32)
    c['c_pw'] = np.tile((-(2.0 ** -(np.arange(40, dtype=np.float64) + 2)))[None, :], (128, 1)).astype(np.float32)
    tt = np.arange(T)[:, None]
    n = np.arange(ncp)[None, :]
    cm = np.where((16 * n + 31 <= tt) & (n < n_cmp), 0.0, -BIG)
    c['c_cmpneg'] = cm.astype(np.float32).astype(bf)
    j = np.arange(nb)[None, :]
    cur = tt // 64
    forced = np.where(j == cur, 3e4, np.where(j == cur - 1, 2e4, np.where(j == 0, 1e4, 0.0)))
    forced = np.where(j * 64 <= tt, forced, -1e30)
    c['c_forced'] = forced.astype(np.float32)
    ni = np.arange(ncp)[:, None]
    ov = ((16 * ni <= j * 64 + 63) & (16 * ni + 31 >= j * 64) & (ni < n_cmp))
    c['c_overlap'] = ov.astype(np.float32).astype(bf)
    return c


_CACHE = {}


def make_in_maps(inputs, T, ncores):
    NT = T // 128
    consts = host_consts(T)
    wnames = ["ln_g", "mem_norm_g", "mem_w_kv", "mem_q_norm_g", "mem_k_norm_g", "w_out", "even_w_in",
              "mla_q_lat_g", "mla_kv_lat_g", "mla_w_uq", "mla_w_ukv", "mla_q_norm_g", "mla_k_norm_g",
              "nsa_q_norm_g", "nsa_k_norm_g", "nsa_cmp_w1", "nsa_cmp_w2", "odd_w_in", "dsa_q_norm_g",
              "dsa_k_norm_g", "mlstm_conv_b", "mlstm_i_bias", "mlstm_f_bias", "mlstm_h_norm_g"]
    shared = {k: np.ascontiguousarray(np.asarray(inputs[k], dtype=np.float32)) for k in wnames}
    shared["nsa_cmp_posT"] = np.ascontiguousarray(np.transpose(np.asarray(inputs["nsa_cmp_pos"], np.float32), (0, 1, 3, 2)))
    shared["mlstm_conv_wT"] = np.ascontiguousarray(np.transpose(np.asarray(inputs["mlstm_conv_w"], np.float32), (0, 2, 1)))
    shared.update(consts)
    maps = []
    for c in range(ncores):
        m = dict(shared)
        m["x"] = np.ascontiguousarray(np.asarray(inputs["x"][c, :T], np.float32))
        m["mem"] = np.ascontiguousarray(np.asarray(inputs["mem"][c], np.float32))
        pos = np.asarray(inputs["positions"][c, :T]).astype(np.int32)
        m["pos_t"] = np.ascontiguousarray(pos.reshape(NT, 128).T)
        maps.append(m)
    return maps


def kernel(**inputs):
    T = 4096
    key = ('full', T)
    if key not in _CACHE:
        _CACHE[key] = Builder(T, [0, 1, 2, 3]).build()
    nc = _CACHE[key]
    maps = make_in_maps(inputs, T, 8)
    res = run_bass_kernel_spmd(nc, maps, core_ids=list(range(8)))
    out = np.stack([np.asarray(r["out"], dtype=np.float32) for r in res.results], axis=0)
    return out
```

```python
import math
import numpy as np
import ml_dtypes
from contextlib import ExitStack
import concourse.bass as bass
import concourse.mybir as mybir
from concourse.bass_utils import run_bass_kernel_spmd

F32 = mybir.dt.float32
BF16 = mybir.dt.bfloat16
I32 = mybir.dt.int32
AF = mybir.ActivationFunctionType
ALU = mybir.AluOpType
AX = mybir.AxisListType

D = 1024
BIG = 30000.0
EPS = 1e-6
EVEN_COLS = 3256
ODD_COLS = 4016
ENGS = ('pe', 'act', 'dve', 'pool', 'sp')
EPOCH = 16000
NDQ = 8


class Buf:
    __slots__ = ('name', 'w', 'r')

    def __init__(self, name=''):
        self.name = name
        self.w = None
        self.r = {}


class Sched:
    def __init__(self, nc, es):
        self.nc = nc
        self.es = es
        self.prog = {e: [] for e in ENGS}
        self.esem = {e: [] for e in ENGS}
        self.cnt = {e: 0 for e in ENGS}
        self.seen = {e: {} for e in ENGS}
        self.dq = ('sp', 'pool', 'act')
        self.dsem = {q: [es.enter_context(nc.semaphore(f'D{q}{i}')) for i in range(NDQ)] for q in self.dq}
        self.dcnt = {q: 0 for q in self.dq}
        self.ninst = 0

    def _semobj(self, key):
        if key[0] == 'E':
            return self.esem[key[1]][key[2]]
        return self.dsem[key[1]][key[2]]

    def op(self, e, fn, reads=(), writes=(), dma=False):
        deps = {}
        for b in reads:
            if b.w is not None:
                k, v = b.w
                if deps.get(k, 0) < v:
                    deps[k] = v
        for b in writes:
            if b.w is not None:
                k, v = b.w
                if deps.get(k, 0) < v:
                    deps[k] = v
            for k, v in b.r.items():
                if deps.get(k, 0) < v:
                    deps[k] = v
        waits = []
        seen = self.seen[e]
        for k, v in deps.items():
            if e == 'pe' and k[0] == 'E' and k[1] == 'pe':
                continue
            if seen.get(k, 0) >= v:
                continue
            seen[k] = v
            waits.append((self._semobj(k), v))
        if dma:
            j = self.dcnt[e]
            self.dcnt[e] += 1
            slot = j % NDQ
            val = 16 * (j // NDQ + 1)
            key = ('D', e, slot)
            if val > 16 and seen.get(key, 0) < val - 16:
                seen[key] = val - 16
                waits.append((self.dsem[e][slot], val - 16))
            sem = self.dsem[e][slot]
            inc = 16
        else:
            c = self.cnt[e]
            ep = c // EPOCH
            if ep >= len(self.esem[e]):
                self.esem[e].append(self.es.enter_context(self.nc.semaphore(f'S{e}{ep}')))
            self.cnt[e] += 1
            key = ('E', e, ep)
            val = c % EPOCH + 1
            sem = self.esem[e][ep]
            inc = 1
        ev = (key, val)
        self.ninst += 1

        def thunk(eng, waits=waits, fn=fn, sem=sem, inc=inc):
            for s, v in waits:
                eng.wait_ge(s, v)
            fn(eng).then_inc(sem, inc)
        self.prog[e].append(thunk)
        for b in reads:
            if b.r.get(key, 0) < val:
                b.r[key] = val
        for b in writes:
            b.w = ev
            b.r = {}
        return ev

    def barrier(self):
        evs = []
        for e in ENGS:
            c = self.cnt[e]
            if c > 0:
                ep = (c - 1) // EPOCH
                evs.append((('E', e, ep), (c - 1) % EPOCH + 1))
        for q in self.dq:
            n = self.dcnt[q]
            for slot in range(min(n, NDQ)):
                cntslot = (n - 1 - slot) // NDQ + 1
                evs.append((('D', q, slot), 16 * cntslot))
        for e in ENGS:
            waits = []
            for k, v in evs:
                if k[0] == 'E' and k[1] == e:
                    continue
                if self.seen[e].get(k, 0) >= v:
                    continue
                self.seen[e][k] = v
                waits.append((self._semobj(k), v))

            def thunk(eng, waits=waits):
                for s, v in waits:
                    eng.wait_ge(s, v)
            self.prog[e].append(thunk)

    def emit(self):
        nc = self.nc
        with nc.Block() as block:
            @block.tensor
            def _(eng):
                for t in self.prog['pe']:
                    t(eng)

            @block.scalar
            def _(eng):
                for t in self.prog['act']:
                    t(eng)

            @block.vector
            def _(eng):
                for t in self.prog['dve']:
                    t(eng)

            @block.gpsimd
            def _(eng):
                for t in self.prog['pool']:
                    t(eng)

            @block.sync
            def _(eng):
                for t in self.prog['sp']:
                    t(eng)


class Tl:
    __slots__ = ('t', 'b')

    def __init__(self, t, name):
        self.t = t
        self.b = Buf(name)

    def __getitem__(self, k):
        return self.t[k]


class Builder:
    def __init__(self, T, layers, dbg=()):
        self.T = T
        self.NT = T // 128
        self.layers = list(layers)
        self.dbg = set(dbg)
        self.nc = bass.Bass("TRN2", target_bir_lowering=False)
        self.uid = 0
        self.rec = None
        self.dbg_out = {}

    def din(self, name, shape, dt=F32):
        return self.nc.dram_tensor(name, list(shape), dt, kind="ExternalInput").ap()

    def dscr(self, name, shape, dt, nbuf=1):
        kind = "ExternalOutput" if name in self.dbg else "Internal"
        t = self.nc.dram_tensor(name, list(shape), dt, kind=kind).ap()
        tl = Tl(t, name)
        if nbuf > 1:
            tl.b = [Buf(f'{name}{i}') for i in range(nbuf)]
        return tl

    def sb(self, es, name, shape, dt):
        self.uid += 1
        t = es.enter_context(self.nc.sbuf_tensor(f'{name}_{self.uid}', list(shape), dt))
        return Tl(t, name)

    def op_(self, e, fn, reads=(), writes=(), dma=False):
        if self.rec is not None:
            self.rec.append((e, fn, tuple(reads), tuple(writes), dma))
        else:
            self.S.op(e, fn, reads=reads, writes=writes, dma=dma)

    def chains_begin(self, names):
        self._chains = {k: [] for k in names}

    def chain(self, name, scr=None):
        self.rec = self._chains[name]
        self.scr = scr if scr is not None else self.scr0

    def chains_emit(self, proportional=True):
        self.rec = None
        self.scr = self.scr0
        lists = [l for l in self._chains.values() if l]
        idx = [0] * len(lists)
        left = sum(len(l) for l in lists)
        while proportional and left:
            i = min((k for k in range(len(lists)) if idx[k] < len(lists[k])), key=lambda k: idx[k] / len(lists[k]))
            e, fn, R, W, dma = lists[i][idx[i]]
            idx[i] += 1
            left -= 1
            self.S.op(e, fn, reads=R, writes=W, dma=dma)
        while left:
            for i, l in enumerate(lists):
                if idx[i] < len(l):
                    e, fn, R, W, dma = l[idx[i]]
                    idx[i] += 1
                    left -= 1
                    self.S.op(e, fn, reads=R, writes=W, dma=dma)

    def new_scr(self, es, tag, w):
        sc = {}
        for k in ('sq', 'tmp', 'ra', 'rb'):
            sc[k] = self.sb(es, f'sc_{k}_{tag}', [128, w], F32)
        for k in ('ssq', 'ln', 'rs'):
            sc[k] = self.sb(es, f'sc_{k}_{tag}', [128, 16], F32)
        return sc

    def mm(self, out, lhsT, rhs, start, stop, R, W):
        self.op_('pe', lambda e: e.matmul(out, lhsT=lhsT, rhs=rhs, start=start, stop=stop,
                                           skip_group_check=True), reads=R, writes=W)

    def tr(self, out, in_, ident, R, W):
        self.op_('pe', lambda e: e.transpose(out=out, in_=in_, identity=ident), reads=R, writes=W)

    def act(self, out, in_, func, R, W, bias=None, scale=None, accum=None):
        kw = {}
        if bias is not None:
            kw['bias'] = bias
        if scale is not None:
            kw['scale'] = scale
        if accum is not None:
            kw['accum_out'] = accum
        self.op_('act', lambda e: e.activation(out=out, in_=in_, func=func, **kw), reads=R, writes=W)

    def tt(self, eng, out, in0, in1, op, R, W):
        self.op_(eng, lambda e: e.tensor_tensor(out=out, in0=in0, in1=in1, op=op), reads=R, writes=W)

    def ts(self, eng, out, in0, s1, op0, R, W, s2=None, op1=None, accum=None):
        kw = {}
        if op1 is not None:
            kw['op1'] = op1
        if accum is not None:
            kw['accum_out'] = accum
        self.op_(eng, lambda e: e.tensor_scalar(out=out, in0=in0, scalar1=s1, scalar2=s2, op0=op0, **kw),
                  reads=R, writes=W)

    def stt(self, out, in0, scalar, in1, op0, op1, R, W):
        self.op_('dve', lambda e: e.scalar_tensor_tensor(out=out, in0=in0, scalar=scalar, in1=in1,
                                                         op0=op0, op1=op1), reads=R, writes=W)

    def cp(self, eng, out, in_, R, W):
        if eng == 'act':
            self.op_('act', lambda e: e.copy(out=out, in_=in_), reads=R, writes=W)
        else:
            self.op_(eng, lambda e: e.tensor_copy(out=out, in_=in_), reads=R, writes=W)

    def red(self, out, in_, op, R, W):
        self.op_('dve', lambda e: e.tensor_reduce(out=out, in_=in_, axis=AX.X, op=op), reads=R, writes=W)

    def memset(self, eng, ap, val, W):
        self.op_(eng, lambda e: e.memset(ap, val), writes=W)

    def dma(self, q, out, in_, R, W, **kw):
        self.op_(q, lambda e: e.dma_start(out=out, in_=in_, **kw), reads=R, writes=W, dma=True)

    def bc_load(self, es, name, row_ap, d):
        t = self.sb(es, name, [128, d], F32)
        self.dma('sp', t[:], row_ap.to_broadcast([128, d]), [], [t.b])
        return t

    def wload(self, w, k, src, c0, c1):
        c = c0
        while c < c1:
            ce = min(c1, c + 2048)
            self.dma('pool', w[:, k, c:ce], src[:, c:ce], [], [w.b])
            c = ce

    def rstd(self, ssq, H, d, R):
        ln, rs = self.scr['ln'], self.scr['rs']
        self.act(ln[:, 0:H], ssq, AF.Ln, R, [ln.b], bias=self.eps_c[:, 0:1], scale=1.0 / d)
        self.act(rs[:, 0:H], ln[:, 0:H], AF.Exp, [ln.b], [rs.b], scale=-0.5)
        return rs[:, 0:H]

    def rmsn(self, src, H, d, g, dst, R, W):
        sq, ssq, tmp = self.scr['sq'], self.scr['ssq'], self.scr['tmp']
        sqv = sq[:, 0:H * d].rearrange("p (h c) -> p h c", h=H)
        self.tt('pool', sqv, src, src, ALU.mult, R, [sq.b])
        self.red(ssq[:, 0:H], sqv, ALU.add, [sq.b], [ssq.b])
        rs = self.rstd(ssq[:, 0:H], H, d, [ssq.b])
        tv = tmp[:, 0:H * d].rearrange("p (h c) -> p h c", h=H)
        self.tt('dve', tv, src, rs.unsqueeze(2).to_broadcast([128, H, d]), ALU.mult,
                list(R) + [self.scr['rs'].b], [tmp.b])
        self.tt('pool', dst, tv, g[:].unsqueeze(1).to_broadcast([128, H, d]), ALU.mult,
                [tmp.b, g.b], W)

    def rope(self, src, H, d2, n, dst, R, W):
        tab = self.rope32 if d2 == 32 else self.rope16
        cosv = tab[:, n, 0:d2]
        sinv = tab[:, n, d2:2 * d2]
        A, Bm = self.scr['ra'], self.scr['rb']
        s4 = src.rearrange("p h (two c) -> p h two c", two=2)
        d4 = dst.rearrange("p h (two c) -> p h two c", two=2)
        Av = A[:, 0:H * 2 * d2].rearrange("p (h two c) -> p h two c", h=H, two=2)
        Bv = Bm[:, 0:H * 2 * d2].rearrange("p (h two c) -> p h two c", h=H, two=2)
        cb = cosv.unsqueeze(1).unsqueeze(1).to_broadcast([128, H, 2, d2])
        sbv = sinv.unsqueeze(1).unsqueeze(1).to_broadcast([128, H, 2, d2])
        self.tt('dve', Av, s4, cb, ALU.mult, list(R) + [tab.b], [A.b])
        self.tt('pool', Bv, s4, sbv, ALU.mult, list(R) + [tab.b], [Bm.b])
        self.tt('dve', d4[:, :, 0, :], Av[:, :, 0, :], Bv[:, :, 1, :], ALU.subtract, [A.b, Bm.b], W)
        self.tt('pool', d4[:, :, 1, :], Bv[:, :, 0, :], Av[:, :, 1, :], ALU.add, [A.b, Bm.b], W)

    def attn_block(self, rhs_q, q_R, N, tiles, scale, acc_views, acc_bufs, mode='exp', LA=2, epilogue=None):
        started = set()
        nt = len(tiles)
        last_use = {}
        for i, tl in enumerate(tiles):
            for j in tl['subs']:
                last_use[j] = i
        Atiles = {}

        def emit_s(i):
            tl = tiles[i]
            bi = self.st_rr % len(self.st_banks)
            self.st_rr += 1
            bank, bb = self.st_banks[bi]
            c0 = tl['c0']
            masks = tl.get('masks', [])
            if mode != 'exp':
                E, eR = tl['Efn']()
            self.mm(bank[:, c0:N], tl['kT'], rhs_q[:, c0:N], True, len(masks) == 0, list(q_R) + tl['R'], [bb])
            for mi, (ml, mr, off, ncols, mR) in enumerate(masks):
                self.mm(bank[:, off:off + ncols], ml, mr, False, mi == len(masks) - 1, mR, [bb])
            ai = self.at_rr % len(self.at_tiles)
            self.at_rr += 1
            A = self.at_tiles[ai]
            Atiles[i] = A
            if mode == 'exp':
                self.act(A[:, c0:N], bank[:, c0:N], AF.Exp, [bb], [A.b], scale=scale)
            else:
                self.tt('dve', A[:, c0:N], bank[:, c0:N], E[:, c0:N], ALU.mult, [bb] + eR, [A.b])
                if tl.get('diag') is not None:
                    dj = tl['diag']
                    self.tt('pool', A[:, dj * 128:(dj + 1) * 128], A[:, dj * 128:(dj + 1) * 128],
                            self.tri01T[:], ALU.mult, [A.b, self.tri01T.b], [A.b])

        def emit_pv(i):
            tl = tiles[i]
            A = Atiles.pop(i)
            for j in tl['subs']:
                view, bk = acc_views[j]
                st = bk not in started
                started.add(bk)
                self.mm(view, A[:, j * 128:(j + 1) * 128], tl['V'], st, last_use[j] == i,
                        [A.b] + tl['R'], [acc_bufs[bk]])

        for step in range(nt):
            emit_s(step)
            self.pend.append(('pv', (lambda i=step: emit_pv(i))))
            self.npv += 1
            self._drain(LA)
        if epilogue is None:
            self.attn_flush()
            return None
        self.epi_id += 1
        eid = self.epi_id
        self.pend.append(('epi', epilogue, eid))
        self._drain(LA)
        return eid

    def _drain(self, limit):
        q = self.pend
        while q and (q[0][0] == 'epi' or self.npv > limit):
            it = q.pop(0)
            if it[0] == 'pv':
                self.npv -= 1
                it[1]()
            else:
                it[1]()
                self.epi_done.add(it[2])

    def attn_flush(self):
        self._drain(-1)

    def attn_sync(self, eid):
        while eid is not None and eid not in self.epi_done:
            q = self.pend
            it = q.pop(0)
            if it[0] == 'pv':
                self.npv -= 1
                it[1]()
            else:
                it[1]()
                self.epi_done.add(it[2])

    def build(self):
        nc = self.nc
        T, NT = self.T, self.NT
        with ExitStack() as es:
            self.S = S = Sched(nc, es)
            self._decl_inputs()
            self._decl_scratch()
            self.ps = []
            for i in range(8):
                t = es.enter_context(nc.psum_tensor(f"psb{i}", [128, 512], F32))
                self.ps.append(Tl(t, f'ps{i}'))
            self._consts(es)
            self._prep(es)
            S.barrier()
            xin = Tl(self.x_in, 'xin')
            xin.b = [Buf('xin')] * 1
            cur = xin
            for idx, L in enumerate(self.layers):
                last = idx == len(self.layers) - 1
                dst = self.out_t if last else self.xs[idx % 2]
                with ExitStack() as les:
                    if L % 2 == 0:
                        self.layer_even(les, L, cur, dst)
                    else:
                        self.layer_odd(les, L, cur, dst)
                S.barrier()
                cur = dst
            S.barrier()
            S.emit()
        return nc

    def _decl_inputs(self):
        T, NT = self.T, self.NT
        d = self.din
        self.x_in = d("x", [T, D])
        self.mem = d("mem", [256, D])
        self.pos_t = d("pos_t", [128, NT], I32)
        self.w = {}
        spec = dict(
            ln_g=[4, D], mem_norm_g=[4, D], mem_w_kv=[4, D, 512], mem_q_norm_g=[4, 64], mem_k_norm_g=[4, 64],
            w_out=[4, 1280, D], even_w_in=[2, D, EVEN_COLS], mla_q_lat_g=[2, 256], mla_kv_lat_g=[2, 128],
            mla_w_uq=[2, 256, 768], mla_w_ukv=[2, 128, 1024], mla_q_norm_g=[2, 96], mla_k_norm_g=[2, 96],
            nsa_q_norm_g=[2, 64], nsa_k_norm_g=[2, 3, 64], nsa_cmp_posT=[2, 2, 64, 32],
            nsa_cmp_w1=[2, 2, 2048, 64], nsa_cmp_w2=[2, 2, 64, 64], odd_w_in=[2, D, ODD_COLS],
            dsa_q_norm_g=[2, 64], dsa_k_norm_g=[2, 64], mlstm_conv_wT=[2, 512, 4], mlstm_conv_b=[2, 512],
            mlstm_i_bias=[2, 4], mlstm_f_bias=[2, 4], mlstm_h_norm_g=[2, 128])
        for k, shp in spec.items():
            self.w[k] = d(k, shp)
        ncp = self.ncmp_pad = max(1, T // 2048) * 128
        nb = self.n_blk = T // 64
        self.c = dict(
            ident=d("c_ident", [128, 128], BF16), identf=d("c_identf", [128, 128], F32),
            i4=d("c_i4", [128, 512], BF16), i8x2=d("c_i8x2", [128, 512], BF16),
            tri_neg=d("c_tri_neg", [128, 128], BF16), edge_neg=d("c_edge_neg", [128, 128], BF16),
            tri01T=d("c_tri01T", [128, 128], BF16), trinegf=d("c_trinegf", [128, 128], F32),
            invf=d("c_invf", [128, 32]), pw=d("c_pw", [128, 40]), cmpneg=d("c_cmpneg", [T, ncp], BF16),
            forced=d("c_forced", [T, nb]), overlap=d("c_overlap", [ncp, nb], BF16))

    def _decl_scratch(self):
        T, NT = self.T, self.NT
        s = self.dscr
        self.out_t = Tl(self.nc.dram_tensor("out", [T, D], F32, kind="ExternalOutput").ap(), 'out')
        self.out_t.b = [Buf(f'out{i}') for i in range(NT)]
        self.xs = [s("xs0", [T, D], F32, NT), s("xs1", [T, D], F32, NT)]
        self.rope_d = s("rope_d", [T, 64], F32)
        self.Y = s("Y", [T, 2304], F32)
        self.G = s("G", [T, 1792], BF16, NT)
        self.SG = s("SG", [T, 32], F32, NT)
        self.mla_qT = s("mla_qT", [8, 96, T], BF16)
        self.mla_kT = s("mla_kT", [8, 96, T], BF16)
        self.mla_v = s("mla_v", [8, 128, NT, 65], BF16)
        self.nsa_qT = s("nsa_qT", [2, NT, 64, 4, 128], BF16)
        self.nsa_kT = s("nsa_kT", [8, 64, T], BF16)
        self.nsa_v = s("nsa_v", [4, 128, NT, 65], BF16)
        self.mem_qT = s("mem_qT", [4, 64, T], BF16)
        self.dsa_qT = s("dsa_qT", [NT, 64, 8, 128], BF16)
        self.dsa_kT = s("dsa_kT", [64, T], BF16)
        self.dsa_v = s("dsa_v", [128, NT, 65], BF16)
        self.idx_qT = s("idx_qT", [8, 32, T], BF16)
        self.idx_kT = s("idx_kT", [32, T], BF16)
        self.idx_w = s("idx_w", [T, 8], F32)
        self.ml_raw = s("ml_raw", [512, T], F32)
        self.ml_if = s("ml_if", [8, T], F32)
        self.ml_qkT = s("ml_qkT", [512, T], BF16)
        self.ml_v = s("ml_v", [4, 128, NT, 129], BF16)
        self.ml_g = s("ml_g", [12, T], F32)

    def _consts(self, es):
        c = self.c
        def ld(name, shape, dt):
            t = self.sb(es, name, shape, dt)
            self.dma('sp', t[:], c[name], [], [t.b])
            return t
        self.ident = ld('ident', [128, 128], BF16)
        self.identf = ld('identf', [128, 128], F32)
        self.i4 = ld('i4', [128, 512], BF16)
        self.i8x2 = ld('i8x2', [128, 512], BF16)
        self.tri_neg = ld('tri_neg', [128, 128], BF16)
        self.edge_neg = ld('edge_neg', [128, 128], BF16)
        self.tri01T = ld('tri01T', [128, 128], BF16)
        self.trinegf = ld('trinegf', [128, 128], F32)
        self.invf = ld('invf', [128, 32], F32)
        self.pw = ld('pw', [128, 40], F32)
        self.eps_c = self.sb(es, 'eps_c', [128, 1], F32)
        self.memset('dve', self.eps_c[:], EPS, [self.eps_c.b])
        self.one_c = self.sb(es, 'one_c', [128, 1], F32)
        self.memset('dve', self.one_c[:], 1.0, [self.one_c.b])
        self.rope32 = self.sb(es, 'rope32', [128, self.NT, 64], F32)
        self.rope16 = self.sb(es, 'rope16', [128, self.NT, 32], F32)
        self.sc_sq = self.sb(es, 'sc_sq', [128, 512], F32)
        self.sc_tmp = self.sb(es, 'sc_tmp', [128, 512], F32)
        self.sc_ra = self.sb(es, 'sc_ra', [128, 512], F32)
        self.sc_rb = self.sb(es, 'sc_rb', [128, 512], F32)
        self.sc_ssq = self.sb(es, 'sc_ssq', [128, 16], F32)
        self.sc_ln = self.sb(es, 'sc_ln', [128, 16], F32)
        self.sc_rs = self.sb(es, 'sc_rs', [128, 16], F32)
        self.scr0 = dict(sq=self.sc_sq, tmp=self.sc_tmp, ra=self.sc_ra, rb=self.sc_rb, ssq=self.sc_ssq,
                         ln=self.sc_ln, rs=self.sc_rs)
        self.scr = self.scr0

    def _prep(self, es0):
        NT = self.NT
        PI = math.pi
        with ExitStack() as es:
            pi_t = self.sb(es, 'pos_i', [128, NT], I32)
            self.dma('sp', pi_t[:], self.pos_t, [], [pi_t.b])
            pf = self.sb(es, 'pos_f', [128, NT], F32)
            self.cp('dve', pf[:], pi_t[:], [pi_t.b], [pf.b])
            ang = self.sb(es, 'ang', [128, NT, 32], F32)
            self.tt('dve', ang[:], pf[:].unsqueeze(2).to_broadcast([128, NT, 32]),
                    self.invf[:].unsqueeze(1).to_broadcast([128, NT, 32]), ALU.mult,
                    [pf.b, self.invf.b], [ang.b])
            kf = self.sb(es, 'kf', [128, NT, 32], F32)
            ki = self.sb(es, 'ki', [128, NT, 32], I32)
            r = self.sb(es, 'r', [128, NT, 32], F32)
            m = self.sb(es, 'm', [128, NT, 32], F32)

            def wrap(buf):
                self.ts('dve', m[:], buf[:], PI, ALU.is_gt, [buf.b], [m.b], s2=-2 * PI, op1=ALU.mult)
                self.tt('dve', buf[:], buf[:], m[:], ALU.add, [buf.b, m.b], [buf.b])
                self.ts('dve', m[:], buf[:], -PI, ALU.is_lt, [buf.b], [m.b], s2=2 * PI, op1=ALU.mult)
                self.tt('dve', buf[:], buf[:], m[:], ALU.add, [buf.b, m.b], [buf.b])
                self.ts('dve', buf[:], buf[:], PI, ALU.min, [buf.b], [buf.b], s2=-PI, op1=ALU.max)
            self.ts('dve', kf[:], ang[:], 1.0 / (2 * PI), ALU.mult, [ang.b], [kf.b])
            self.cp('dve', ki[:], kf[:], [kf.b], [ki.b])
            self.cp('dve', kf[:], ki[:], [ki.b], [kf.b])
            C1 = 6.28125
            C2 = 2 * PI - C1
            self.stt(r[:], kf[:], -C1, ang[:], ALU.mult, ALU.add, [kf.b, ang.b], [r.b])
            self.stt(r[:], kf[:], -C2, r[:], ALU.mult, ALU.add, [kf.b, r.b], [r.b])
            wrap(r)
            self.act(self.rope32[:, :, 32:64], r[:], AF.Sin, [r.b], [self.rope32.b])
            self.ts('dve', r[:], r[:], PI / 2, ALU.add, [r.b], [r.b])
            wrap(r)
            self.act(self.rope32[:, :, 0:32], r[:], AF.Sin, [r.b], [self.rope32.b])
            self.cp('dve', self.rope16[:, :, 0:16], self.rope32[:, :, 0:32:2], [self.rope32.b], [self.rope16.b])
            self.cp('dve', self.rope16[:, :, 16:32], self.rope32[:, :, 32:64:2], [self.rope32.b], [self.rope16.b])
            self.dma('sp', self.rope_d[:].rearrange("(n p) c -> p n c", p=128), self.rope32[:],
                     [self.rope32.b], [self.rope_d.b])
            self.S.barrier()

    def load_win(self, es, L, w_in, ncols):
        win = self.sb(es, 'win', [128, 8, ncols], BF16)
        for k in range(8):
            self.wload(win, k, w_in[k * 128:(k + 1) * 128, :], 0, ncols)
        lng = self.bc_load(es, 'lng', self.w['ln_g'][L:L + 1, :], D)
        return win, lng

    def load_wout(self, es, L):
        wout = self.sb(es, 'wout', [128, 10, D], BF16)
        for k in range(10):
            self.wload(wout, k, self.w['w_out'][L, k * 128:(k + 1) * 128, :], 0, D)
        return wout

    def mem_kv(self, es, L):
        w = self.w
        kT = self.sb(es, 'memkT', [64, 4, 256], BF16)
        V = self.sb(es, 'memV', [128, 2, 4, 65], BF16)
        self.memset('pool', V[:], 1.0, [V.b])
        with ExitStack() as s2:
            wkv = self.sb(s2, 'wkv', [128, 8, 512], BF16)
            for k in range(8):
                self.wload(wkv, k, w['mem_w_kv'][L, k * 128:(k + 1) * 128, :], 0, 512)
            mg = self.bc_load(s2, 'mg', w['mem_norm_g'][L:L + 1, :], D)
            kg = self.bc_load(s2, 'kg', w['mem_k_norm_g'][L:L + 1, :], 64)
            mt = self.sb(s2, 'mt', [128, D], F32)
            junk = self.sb(s2, 'junk', [128, D], BF16)
            mh = self.sb(s2, 'mh', [128, D], BF16)
            mhT = self.sb(s2, 'mhT', [128, 8, 128], BF16)
            kv = self.sb(s2, 'kv', [128, 512], F32)
            kn = self.sb(s2, 'kn', [128, 256], BF16)
            ss = self.sb(s2, 'ss', [128, 1], F32)
            for i in range(2):
                self.dma('sp', mt[:], self.mem[i * 128:(i + 1) * 128, :], [], [mt.b])
                self.act(junk[:], mt[:], AF.Square, [mt.b], [junk.b, ss.b], accum=ss[:])
                rs = self.rstd(ss[:, 0:1], 1, D, [ss.b])
                self.stt(mh[:], mt[:], rs, mg[:], ALU.mult, ALU.mult, [mt.b, self.scr['rs'].b, mg.b], [mh.b])
                pT = self.ps[2]
                pTb = pT[:].bitcast(BF16)
                for k in range(8):
                    self.tr(pTb[:, k * 128:(k + 1) * 128], mh[:, k * 128:(k + 1) * 128], self.ident[:],
                            [mh.b, self.ident.b], [pT.b])
                self.cp('act', mhT[:].rearrange("p k c -> p (k c)"), pTb[:, 0:1024], [pT.b], [mhT.b])
                pU = self.ps[0]
                for k in range(8):
                    self.mm(pU[:, 0:512], mhT[:, k, :], wkv[:, k, :], k == 0, k == 7, [mhT.b, wkv.b], [pU.b])
                self.cp('act', kv[:], pU[:, 0:512], [pU.b], [kv.b])
                self.rmsn(kv[:, 0:256].rearrange("p (h c) -> p h c", h=4), 4, 64, kg,
                          kn[:].rearrange("p (h c) -> p h c", h=4), [kv.b], [kn.b])
                self.cp('dve', V[:, i, :, 0:64], kv[:, 256:512].rearrange("p (h c) -> p h c", h=4), [kv.b], [V.b])
                pK = self.ps[3]
                pKb = pK[:].bitcast(BF16)
                for h in range(4):
                    self.tr(pKb[0:64, h * 128:(h + 1) * 128], kn[:, h * 64:(h + 1) * 64], self.ident[:],
                            [kn.b, self.ident.b], [pK.b])
                self.cp('act', kT[:, :, i * 128:(i + 1) * 128],
                        pKb[0:64, 0:512].rearrange("p (h c) -> p h c", h=4), [pK.b], [kT.b])
            self.S.barrier()
        return kT, V

    def p1_front(self, n, x_src, xt, ht, hT, u, ss, junk, lng, win, ncols):
        sl = n % 2
        x_t, h_t, hT_t, u_t = xt[sl], ht[sl], hT[sl], u[sl]
        xb = x_src.b[n] if len(x_src.b) > 1 else x_src.b[0]
        self.dma('sp', x_t[:], x_src[n * 128:(n + 1) * 128, :], [xb], [x_t.b])
        self.act(junk[:], x_t[:], AF.Square, [x_t.b], [junk.b, ss.b], accum=ss[:])
        rs = self.rstd(ss[:, 0:1], 1, D, [ss.b])
        self.stt(h_t[:], x_t[:], rs, lng[:], ALU.mult, ALU.mult, [x_t.b, self.scr['rs'].b, lng.b], [h_t.b])
        pT = self.ps[2]
        pTb = pT[:].bitcast(BF16)
        for k in range(8):
            self.tr(pTb[:, k * 128:(k + 1) * 128], h_t[:, k * 128:(k + 1) * 128], self.ident[:],
                    [h_t.b, self.ident.b], [pT.b])
        self.cp('act', hT_t[:].rearrange("p k c -> p (k c)"), pTb[:, 0:1024], [pT.b], [hT_t.b])
        nchunk = (ncols + 511) // 512
        for c in range(nchunk):
            c0 = c * 512
            wd = min(512, ncols - c0)
            pU = self.ps[c % 2]
            for k in range(8):
                self.mm(pU[:, 0:wd], hT_t[:, k, :], win[:, k, c0:c0 + wd], k == 0, k == 7,
                        [hT_t.b, win.b], [pU.b])
            self.cp('act' if c % 2 == 0 else 'dve', u_t[:, c0:c0 + wd], pU[:, 0:wd], [pU.b], [u_t.b])
        return u_t

    def transposes_out(self, srcs, rows, stage, dst_ap, dst_b, pidx):
        pT = self.ps[pidx]
        pTb = pT[:].bitcast(BF16)
        k = len(srcs)
        for i, (ap, R) in enumerate(srcs):
            self.tr(pTb[0:rows, i * 128:(i + 1) * 128], ap, self.ident[:], list(R) + [self.ident.b], [pT.b])
        self.cp('act', stage[0:rows, 0:k, :], pTb[0:rows, 0:k * 128].rearrange("p (k c) -> p k c", k=k),
                [pT.b], [stage.b])
        self.dma('sp', dst_ap, stage[0:rows, 0:k, :], [stage.b], dst_b)

    def p3(self, es, L, x_src, x_dst, wout, mixfn):
        NT = self.NT
        xt = [self.sb(es, f'p3x{i}', [128, D], F32) for i in range(2)]
        mixT = [self.sb(es, f'p3mT{i}', [128, 10, 128], BF16) for i in range(2)]
        xo = [self.sb(es, f'p3o{i}', [128, D], F32) for i in range(2)]

        def one(n):
            sl = n % 2
            pb = 4 * sl
            mix = mixfn(n)
            xb = x_src.b[n] if len(x_src.b) > 1 else x_src.b[0]
            self.dma('sp', xt[sl][:], x_src[n * 128:(n + 1) * 128, :], [xb], [xt[sl].b])
            for half in range(2):
                pT = self.ps[pb + 2 + half]
                pTb = pT[:].bitcast(BF16)
                for k in range(5):
                    kk = half * 5 + k
                    self.tr(pTb[:, k * 128:(k + 1) * 128], mix[:, kk * 128:(kk + 1) * 128], self.ident[:],
                            [mix.b, self.ident.b], [pT.b])
                self.cp('act', mixT[sl][:, half * 5:half * 5 + 5, :].rearrange("p k c -> p (k c)"),
                        pTb[:, 0:640], [pT.b], [mixT[sl].b])
            for c in range(2):
                pU = self.ps[pb + c]
                for k in range(10):
                    self.mm(pU[:, 0:512], mixT[sl][:, k, :], wout[:, k, c * 512:(c + 1) * 512], k == 0, k == 9,
                            [mixT[sl].b, wout.b], [pU.b])
                self.tt('dve', xo[sl][:, c * 512:(c + 1) * 512], pU[:, 0:512], xt[sl][:, c * 512:(c + 1) * 512],
                        ALU.add, [pU.b, xt[sl].b], [xo[sl].b])
            self.dma('sp', x_dst[n * 128:(n + 1) * 128, :], xo[sl][:], [xo[sl].b], [x_dst.b[n]])

        for n0 in range(0, NT, 2):
            self.chains_begin(['P0', 'P1'])
            self.chain('P0')
            one(n0)
            if n0 + 1 < NT:
                self.chain('P1')
                one(n0 + 1)
            self.chains_emit()

    def attn_setup(self, es, dvp_two_banks=False):
        self.st_banks = [(self.ps[i], self.ps[i].b) for i in range(3)]
        self.st_rr = 0
        self.at_tiles = [self.sb(es, f'At{i}', [128, 512], BF16) for i in range(3)]
        self.at_rr = 0
        self.pend = []
        self.npv = 0
        self.epi_id = 0
        self.epi_done = set()

    def mem_attn(self, es, memkT, memV):
        T = self.T
        qT = [self.sb(es, f'mqT{i}', [64, T], BF16) for i in range(2)]
        ost = [self.sb(es, f'most{i}', [128, 4, 64], F32) for i in range(2)]
        rec = self.sb(es, 'mrec', [128, 4], F32)
        blk = 0
        for h in range(4):
            q = qT[h % 2]
            self.dma('sp', q[:], self.mem_qT[h], [self.mem_qT.b], [q.b])
            for cq in range(T // 512):
                set_i = blk % 2
                blk += 1
                accb = self.ps[3 + set_i]
                views = {j: (accb[:, j * 65:(j + 1) * 65], 0) for j in range(4)}
                tiles = [dict(kT=memkT[:, h, kt * 128:(kt + 1) * 128], V=memV[:, kt, h, :],
                              R=[memkT.b, memV.b], c0=0, subs=[0, 1, 2, 3]) for kt in range(2)]
                def epi(accb=accb, o=ost[set_i], cq=cq, h=h):
                    av = accb[:, 0:260].rearrange("p (j c) -> p j c", j=4)
                    self.op_('dve', lambda e, av=av: e.reciprocal(out=rec[:], in_=av[:, :, 64]), reads=[accb.b],
                             writes=[rec.b])
                    self.tt('dve', o[:], av[:, :, 0:64], rec[:].unsqueeze(2).to_broadcast([128, 4, 64]), ALU.mult,
                            [accb.b, rec.b], [o.b])
                    self.dma('sp', self.Y[cq * 512:(cq + 1) * 512, 1024 + h * 64:1024 + (h + 1) * 64]
                             .rearrange("(j p) c -> p j c", p=128), o[:], [o.b], [self.Y.b])
                self.attn_block(q[:, cq * 512:(cq + 1) * 512], [q.b], 512, tiles, 0.125, views, [accb.b], epilogue=epi)
        self.attn_flush()

    def layer_even(self, es, L, x_src, x_dst):
        li = L // 2
        w = self.w
        T, NT = self.T, self.NT
        S = self.S
        memkT, memV = self.mem_kv(es, L)
        with ExitStack() as p1:
            win, lng = self.load_win(p1, L, w['even_w_in'][li], EVEN_COLS)
            wuq = self.sb(p1, 'wuq', [128, 2, 768], BF16)
            for k in range(2):
                self.wload(wuq, k, w['mla_w_uq'][li, k * 128:(k + 1) * 128, :], 0, 768)
            wukv = self.sb(p1, 'wukv', [128, 1, 1024], BF16)
            self.wload(wukv, 0, w['mla_w_ukv'][li], 0, 1024)
            g_ql = self.bc_load(p1, 'g_ql', w['mla_q_lat_g'][li:li + 1, :], 256)
            g_kvl = self.bc_load(p1, 'g_kvl', w['mla_kv_lat_g'][li:li + 1, :], 128)
            g_qn = self.bc_load(p1, 'g_qn', w['mla_q_norm_g'][li:li + 1, 0:64], 64)
            g_qp = self.bc_load(p1, 'g_qp', w['mla_q_norm_g'][li:li + 1, 64:96], 32)
            g_kn = self.bc_load(p1, 'g_kn', w['mla_k_norm_g'][li:li + 1, 0:64], 64)
            g_kp = self.bc_load(p1, 'g_kp', w['mla_k_norm_g'][li:li + 1, 64:96], 32)
            g_bq = self.bc_load(p1, 'g_bq', w['nsa_q_norm_g'][li:li + 1, :], 64)
            g_ks = self.bc_load(p1, 'g_ks', w['nsa_k_norm_g'][li, 1:2, :], 64)
            g_kw = self.bc_load(p1, 'g_kw', w['nsa_k_norm_g'][li, 2:3, :], 64)
            g_mq = self.bc_load(p1, 'g_mq', w['mem_q_norm_g'][L:L + 1, :], 64)
            xt = [self.sb(p1, f'xt{i}', [128, D], F32) for i in range(2)]
            ht = [self.sb(p1, f'ht{i}', [128, D], BF16) for i in range(2)]
            hT = [self.sb(p1, f'hT{i}', [128, 8, 128], BF16) for i in range(2)]
            u = [self.sb(p1, f'u{i}', [128, EVEN_COLS], F32) for i in range(2)]
            ss = self.sb(p1, 'ss', [128, 1], F32)
            junk = self.sb(p1, 'junk', [128, D], BF16)
            latn = self.sb(p1, 'latn', [128, 384], BF16)
            latT = self.sb(p1, 'latT', [128, 3, 128], BF16)
            qsb = self.sb(p1, 'qsb', [128, 768], F32)
            kvsb = self.sb(p1, 'kvsb', [128, 1024], F32)
            qpe = self.sb(p1, 'qpe', [128, 256], F32)
            kpe = self.sb(p1, 'kpe', [128, 32], F32)
            kpeb = self.sb(p1, 'kpeb', [128, 32], BF16)
            qf = self.sb(p1, 'qf', [128, 8, 96], BF16)
            kfm = self.sb(p1, 'kfm', [128, 8, 96], BF16)
            vaug = self.sb(p1, 'vaug', [128, 8, 65], BF16)
            self.memset('pool', vaug[:], 1.0, [vaug.b])
            bqn = self.sb(p1, 'bqn', [128, 512], F32)
            bqf = self.sb(p1, 'bqf', [128, 512], BF16)
            kn2 = self.sb(p1, 'kn2', [128, 128], F32)
            kmisc = self.sb(p1, 'kmisc', [128, 8, 64], BF16)
            nv = self.sb(p1, 'nv', [128, 4, 65], BF16)
            self.memset('pool', nv[:], 1.0, [nv.b])
            mqf = self.sb(p1, 'mqf', [128, 256], BF16)
            gt = self.sb(p1, 'gt', [128, 1280], BF16)
            sg = self.sb(p1, 'sg', [128, 24], F32)
            stq = self.sb(p1, 'stq', [96, 8, 128], BF16)
            stk = self.sb(p1, 'stk', [96, 8, 128], BF16)
            stb = self.sb(p1, 'stb', [64, 8, 128], BF16)
            stm = self.sb(p1, 'stm', [64, 8, 128], BF16)
            stmq = self.sb(p1, 'stmq', [64, 4, 128], BF16)
            scrA = self.new_scr(p1, 'A', 512)
            scrB = self.new_scr(p1, 'B', 512)
            scrC = self.new_scr(p1, 'C', 128)
            ut_next = self.p1_front(0, x_src, xt, ht, hT, u, ss, junk, lng, win, EVEN_COLS)
            for n in range(NT):
                ut = ut_next
                U = lambda a, b_, ut=ut: ut[:, a:b_]
                ub = [ut.b]
                tsl = slice(n * 128, (n + 1) * 128)
                self.chains_begin(['A', 'F', 'B', 'C', 'M', 'G'])
                if n + 1 < NT:
                    self.chain('F')
                    ut_next = self.p1_front(n + 1, x_src, xt, ht, hT, u, ss, junk, lng, win, EVEN_COLS)
                self.chain('G')
                self.act(gt[:, 0:512], U(416, 928), AF.Silu, ub, [gt.b])
                self.act(gt[:, 512:1024], U(2232, 2744), AF.Silu, ub, [gt.b])
                self.act(gt[:, 1024:1280], U(3000, 3256), AF.Silu, ub, [gt.b])
                self.dma('sp', self.G[tsl, 0:1280], gt[:], [gt.b], [self.G.b[n]])
                self.act(sg[:], U(2208, 2232), AF.Sigmoid, ub, [sg.b])
                self.dma('sp', self.SG[tsl, 0:24], sg[:], [sg.b], [self.SG.b[n]])
                self.chain('A', scrA)
                self.rmsn(U(0, 256).rearrange("p (h c) -> p h c", h=1), 1, 256, g_ql,
                          latn[:, 0:256].rearrange("p (h c) -> p h c", h=1), ub, [latn.b])
                self.rmsn(U(256, 384).rearrange("p (h c) -> p h c", h=1), 1, 128, g_kvl,
                          latn[:, 256:384].rearrange("p (h c) -> p h c", h=1), ub, [latn.b])
                pT = self.ps[3]
                pTb = pT[:].bitcast(BF16)
                for k in range(3):
                    self.tr(pTb[:, k * 128:(k + 1) * 128], latn[:, k * 128:(k + 1) * 128], self.ident[:],
                            [latn.b, self.ident.b], [pT.b])
                self.cp('act', latT[:].rearrange("p k c -> p (k c)"), pTb[:, 0:384], [pT.b], [latT.b])
                pQ, pQ2, pK, pK2 = self.ps[3], self.ps[4], self.ps[5], self.ps[4]
                for k in range(2):
                    self.mm(pQ[:, 0:512], latT[:, k, :], wuq[:, k, 0:512], k == 0, k == 1, [latT.b, wuq.b], [pQ.b])
                for k in range(2):
                    self.mm(pQ2[:, 0:256], latT[:, k, :], wuq[:, k, 512:768], k == 0, k == 1, [latT.b, wuq.b], [pQ2.b])
                self.cp('act', qsb[:, 0:512], pQ[:, 0:512], [pQ.b], [qsb.b])
                self.cp('dve', qsb[:, 512:768], pQ2[:, 0:256], [pQ2.b], [qsb.b])
                self.mm(pK[:, 0:512], latT[:, 2, :], wukv[:, 0, 0:512], True, True, [latT.b, wukv.b], [pK.b])
                self.mm(pK2[:, 0:512], latT[:, 2, :], wukv[:, 0, 512:1024], True, True, [latT.b, wukv.b], [pK2.b])
                self.cp('act', kvsb[:, 0:512], pK[:, 0:512], [pK.b], [kvsb.b])
                self.cp('dve', kvsb[:, 512:1024], pK2[:, 0:512], [pK2.b], [kvsb.b])
                q3 = qsb[:].rearrange("p (h c) -> p h c", h=8)
                kv3 = kvsb[:].rearrange("p (h c) -> p h c", h=8)
                self.rmsn(q3[:, :, 0:64], 8, 64, g_qn, qf[:, :, 0:64], [qsb.b], [qf.b])
                qpe3 = qpe[:].rearrange("p (h c) -> p h c", h=8)
                self.rmsn(q3[:, :, 64:96], 8, 32, g_qp, qpe3, [qsb.b], [qpe.b])
                self.rope(qpe3, 8, 16, n, qf[:, :, 64:96], [qpe.b], [qf.b])
                self.rmsn(kv3[:, :, 0:64], 8, 64, g_kn, kfm[:, :, 0:64], [kvsb.b], [kfm.b])
                self.cp('dve', vaug[:, :, 0:64], kv3[:, :, 64:128], [kvsb.b], [vaug.b])
                kpe3 = kpe[:].rearrange("p (h c) -> p h c", h=1)
                self.rmsn(U(384, 416).rearrange("p (h c) -> p h c", h=1), 1, 32, g_kp, kpe3, ub, [kpe.b])
                self.rope(kpe3, 1, 16, n, kpeb[:].rearrange("p (h c) -> p h c", h=1), [kpe.b], [kpeb.b])
                self.cp('pool', kfm[:, :, 64:96], kpeb[:].unsqueeze(1).to_broadcast([128, 8, 32]), [kpeb.b], [kfm.b])
                self.transposes_out([(qf[:, h, :], [qf.b]) for h in range(8)], 96, stq,
                                    self.mla_qT[:, :, tsl].rearrange("h d t -> d h t"), [self.mla_qT.b], 3)
                self.transposes_out([(kfm[:, h, :], [kfm.b]) for h in range(8)], 96, stk,
                                    self.mla_kT[:, :, tsl].rearrange("h d t -> d h t"), [self.mla_kT.b], 5)
                self.dma('sp', self.mla_v[:, :, n, :].rearrange("h p c -> p h c"), vaug[:], [vaug.b], [self.mla_v.b])
                self.chain('B', scrB)
                bq3 = bqn[:].rearrange("p (h c) -> p h c", h=8)
                self.rmsn(U(928, 1440).rearrange("p (h c) -> p h c", h=8), 8, 64, g_bq, bq3, ub, [bqn.b])
                self.rope(bq3, 8, 32, n, bqf[:].rearrange("p (h c) -> p h c", h=8), [bqn.b], [bqf.b])
                self.chain('C', scrC)
                k23 = kn2[:].rearrange("p (h c) -> p h c", h=2)
                self.rmsn(U(1696, 1824).rearrange("p (h c) -> p h c", h=2), 2, 64, g_ks, k23, ub, [kn2.b])
                self.rope(k23, 2, 32, n, kmisc[:, 0:2, :], [kn2.b], [kmisc.b])
                self.rmsn(U(1952, 2080).rearrange("p (h c) -> p h c", h=2), 2, 64, g_kw, k23, ub, [kn2.b])
                self.rope(k23, 2, 32, n, kmisc[:, 2:4, :], [kn2.b], [kmisc.b])
                self.cp('pool', kmisc[:, 4:8, :], U(1440, 1696).rearrange("p (h c) -> p h c", h=4), ub, [kmisc.b])
                self.cp('dve', nv[:, 0:2, 0:64], U(1824, 1952).rearrange("p (h c) -> p h c", h=2), ub, [nv.b])
                self.cp('dve', nv[:, 2:4, 0:64], U(2080, 2208).rearrange("p (h c) -> p h c", h=2), ub, [nv.b])
                self.chain('B', scrB)
                self.transposes_out([(bqf[:, h * 64:(h + 1) * 64], [bqf.b]) for h in range(8)], 64, stb,
                                    self.nsa_qT[:, n].rearrange("g d r t -> d g r t"), [self.nsa_qT.b], 6)
                self.chain('C', scrC)
                self.transposes_out([(kmisc[:, i, :], [kmisc.b]) for i in range(8)], 64, stm,
                                    self.nsa_kT[:, :, tsl].rearrange("k d t -> d k t"), [self.nsa_kT.b], 7)
                self.dma('sp', self.nsa_v[:, :, n, :].rearrange("k p c -> p k c"), nv[:], [nv.b], [self.nsa_v.b])
                self.chain('B', scrB)
                self.rmsn(U(2744, 3000).rearrange("p (h c) -> p h c", h=4), 4, 64, g_mq,
                          mqf[:].rearrange("p (h c) -> p h c", h=4), ub, [mqf.b])
                self.transposes_out([(mqf[:, h * 64:(h + 1) * 64], [mqf.b]) for h in range(4)], 64, stmq,
                                    self.mem_qT[:, :, tsl].rearrange("h d t -> d h t"), [self.mem_qT.b], 6)
                self.chains_emit()
            S.barrier()
        kcmpT = self.sb(es, 'kcmpT', [64, 2, self.ncmp_pad], BF16)
        vcmp = self.sb(es, 'vcmp', [128, 2, self.ncmp_pad // 128, 129], BF16)
        self.nsa_compress(li, kcmpT, vcmp)
        S.barrier()
        with ExitStack() as pa:
            self.attn_setup(pa)
            self.mla_attn(pa)
            S.barrier()
        with ExitStack() as pa:
            self.attn_setup(pa)
            self.mem_attn(pa, memkT, memV)
            S.barrier()
        with ExitStack() as pa:
            self.attn_setup(pa)
            self.nsa_attn(pa, kcmpT, vcmp)
            S.barrier()
        with ExitStack() as p3:
            wout = self.load_wout(p3, L)
            yt = [self.sb(p3, f'yt{i}', [128, 2304], F32) for i in range(2)]
            gtt = [self.sb(p3, f'gtt{i}', [128, 1280], BF16) for i in range(2)]
            sgt = [self.sb(p3, f'sgt{i}', [128, 24], F32) for i in range(2)]
            mix = [self.sb(p3, f'mix{i}', [128, 1280], BF16) for i in range(2)]
            ybs = [self.sb(p3, f'yb{i}', [128, 512], F32) for i in range(2)]
            yb2s = [self.sb(p3, f'yb2{i}', [128, 512], F32) for i in range(2)]

            def mixfn(n):
                sl = n % 2
                y, g, s_, m = yt[sl], gtt[sl], sgt[sl], mix[sl]
                yb, yb2 = ybs[sl], yb2s[sl]
                tsl = slice(n * 128, (n + 1) * 128)
                self.dma('sp', y[:], self.Y[tsl, :], [self.Y.b], [y.b])
                self.dma('sp', g[:], self.G[tsl, 0:1280], [self.G.b[n]], [g.b])
                self.dma('sp', s_[:], self.SG[tsl, 0:24], [self.SG.b[n]], [s_.b])
                self.tt('dve', m[:, 0:512], y[:, 0:512], g[:, 0:512], ALU.mult, [y.b, g.b], [m.b])
                self.tt('pool', m[:, 1024:1280], y[:, 1024:1280], g[:, 1024:1280], ALU.mult, [y.b, g.b], [m.b])
                s3 = s_[:].rearrange("p (h c) -> p h c", c=3)
                y3 = lambda a: y[:, a:a + 512].rearrange("p (h c) -> p h c", h=8)
                b3 = yb[:].rearrange("p (h c) -> p h c", h=8)
                b23 = yb2[:].rearrange("p (h c) -> p h c", h=8)
                self.tt('dve', b3, y3(1280), s3[:, :, 0:1].to_broadcast([128, 8, 64]), ALU.mult, [y.b, s_.b], [yb.b])
                self.tt('pool', b23, y3(512), s3[:, :, 1:2].to_broadcast([128, 8, 64]), ALU.mult, [y.b, s_.b], [yb2.b])
                self.tt('dve', b3, b3, b23, ALU.add, [yb.b, yb2.b], [yb.b])
                self.tt('pool', b23, y3(1792), s3[:, :, 2:3].to_broadcast([128, 8, 64]), ALU.mult, [y.b, s_.b], [yb2.b])
                self.tt('dve', b3, b3, b23, ALU.add, [yb.b, yb2.b], [yb.b])
                self.tt('dve', m[:, 512:1024], yb[:], g[:, 512:1024], ALU.mult, [yb.b, g.b], [m.b])
                return m
            self.p3(p3, L, x_src, x_dst, wout, mixfn)
            S.barrier()

    def mla_attn(self, es):
        T, NT = self.T, self.NT
        qT = [self.sb(es, f'aqT{i}', [96, T], BF16) for i in range(2)]
        kT = [self.sb(es, f'akT{i}', [96, T], BF16) for i in range(2)]
        V = [self.sb(es, f'aV{i}', [128, NT, 65], BF16) for i in range(2)]
        ost = [self.sb(es, f'aost{i}', [128, 4, 64], F32) for i in range(2)]
        rec = self.sb(es, 'arec', [128, 4], F32)
        scale = 96 ** -0.5
        blk = 0
        for h in range(8):
            q, k, v = qT[h % 2], kT[h % 2], V[h % 2]
            self.dma('sp', q[:], self.mla_qT[h], [self.mla_qT.b], [q.b])
            self.dma('sp', k[:], self.mla_kT[h], [self.mla_kT.b], [k.b])
            self.dma('sp', v[:], self.mla_v[h], [self.mla_v.b], [v.b])
            for cq in range(T // 512):
                set_i = blk % 2
                blk += 1
                accb = self.ps[3 + set_i]
                views = {j: (accb[:, j * 65:(j + 1) * 65], 0) for j in range(4)}
                tiles = []
                for kt in range(4 * cq + 4):
                    vv = kt - 4 * cq
                    tl = dict(kT=k[:, kt * 128:(kt + 1) * 128], V=v[:, kt, :], R=[k.b, v.b],
                              c0=max(0, vv) * 128, subs=list(range(max(0, vv), 4)))
                    if vv >= 0:
                        tl['masks'] = [(self.tri_neg[:], self.ident[:], vv * 128, 128,
                                        [self.tri_neg.b, self.ident.b])]
                    tiles.append(tl)
                def epi(accb=accb, o=ost[set_i], cq=cq, h=h):
                    av = accb[:, 0:260].rearrange("p (j c) -> p j c", j=4)
                    self.op_('dve', lambda e, av=av: e.reciprocal(out=rec[:], in_=av[:, :, 64]), reads=[accb.b],
                             writes=[rec.b])
                    self.tt('dve', o[:], av[:, :, 0:64], rec[:].unsqueeze(2).to_broadcast([128, 4, 64]), ALU.mult,
                            [accb.b, rec.b], [o.b])
                    self.dma('sp', self.Y[cq * 512:(cq + 1) * 512, h * 64:(h + 1) * 64]
                             .rearrange("(j p) c -> p j c", p=128), o[:], [o.b], [self.Y.b])
                self.attn_block(q[:, cq * 512:(cq + 1) * 512], [q.b], 512, tiles, scale, views, [accb.b], epilogue=epi)
        self.attn_flush()

    def nsa_compress(self, li, kcmpT, vcmp):
        w = self.w
        T = self.T
        n_cmp = (T - 32) // 16 + 1
        ncp = self.ncmp_pad
        nct = ncp // 128
        self.memset('pool', kcmpT[:], 0.0, [kcmpT.b])
        self.memset('pool', vcmp[:], 0.0, [vcmp.b])
        with ExitStack() as es:
            w1 = self.sb(es, 'cw1', [64, 2, 32, 64], BF16)
            w2 = self.sb(es, 'cw2', [64, 2, 64], BF16)
            peT = self.sb(es, 'cpeT', [64, 2, 32], BF16)
            for kv in range(2):
                self.dma('pool', w1[:, kv], w['nsa_cmp_w1'][li, kv].rearrange("(l d) o -> d l o", d=64), [], [w1.b])
                self.dma('pool', w2[:, kv], w['nsa_cmp_w2'][li, kv], [], [w2.b])
                self.dma('pool', peT[:, kv], w['nsa_cmp_posT'][li, kv], [], [peT.b])
            g_kc = self.bc_load(es, 'g_kc', w['nsa_k_norm_g'][li, 0:1, :], 64)
            ovl = self.sb(es, 'ovl', [128, nct, self.n_blk], BF16)
            self.dma('sp', ovl[:], self.c['overlap'].rearrange("(k p) j -> p k j", p=128), [], [ovl.b])
            xT = [self.sb(es, f'cxT{i}', [64, T], BF16) for i in range(2)]
            bias = self.sb(es, 'cbias', [64, 1], F32)
            hid = self.sb(es, 'chid', [64, ncp], BF16)
            self.memset('pool', hid[:], 0.0, [hid.b])
            ctm = self.sb(es, 'ctm', [128, 64], F32)
            ctn = self.sb(es, 'ctn', [128, 64], F32)
            ctb = self.sb(es, 'ctb', [128, 64], BF16)
            rp = self.sb(es, 'crp', [128, 64], F32)
            it = 0
            for kv in range(2):
                for g in range(2):
                    x = xT[it % 2]
                    it += 1
                    self.dma('sp', x[:], self.nsa_kT[4 + kv * 2 + g], [self.nsa_kT.b], [x.b])
                    pH = self.ps[it % 2]
                    for l in range(32):
                        self.mm(pH[0:64, 0:n_cmp], w1[:, kv, l, :], x[:, l:l + 16 * (n_cmp - 1) + 1:16],
                                l == 0, False, [w1.b, x.b], [pH.b])
                        self.mm(pH[0:64, 511:512], w1[:, kv, l, :], peT[:, kv, l:l + 1], False, l == 31,
                                [w1.b, peT.b], [pH.b])
                    self.cp('dve', bias[:], pH[0:64, 511:512], [pH.b], [bias.b])
                    self.act(hid[:, 0:n_cmp], pH[0:64, 0:n_cmp], AF.Silu, [pH.b, bias.b], [hid.b], bias=bias[:, 0:1])
                    for kt in range(nct):
                        pO = self.ps[2 + kt % 2]
                        self.mm(pO[:, 0:64], hid[:, kt * 128:(kt + 1) * 128], w2[:, kv, :], True, True,
                                [hid.b, w2.b], [pO.b])
                        if kv == 1:
                            self.cp('act', vcmp[:, g, kt, 0:64], pO[:, 0:64], [pO.b], [vcmp.b])
                        else:
                            self.cp('act', ctm[:], pO[:, 0:64], [pO.b], [ctm.b])
                            self.rmsn(ctm[:].rearrange("p (h c) -> p h c", h=1), 1, 64, g_kc,
                                      ctn[:].rearrange("p (h c) -> p h c", h=1), [ctm.b], [ctn.b])
                            nrow = min(128, n_cmp - kt * 128)
                            r0 = 31 + 16 * 128 * kt
                            self.memset('dve', rp[:], 0.0, [rp.b])
                            self.dma('sp', rp[0:nrow, :], self.rope_d[r0:r0 + 16 * (nrow - 1) + 1:16, :],
                                     [self.rope_d.b], [rp.b])
                            A, Bm = self.sc_ra, self.sc_rb
                            c2 = ctn[:].rearrange("p (two c) -> p two c", two=2)
                            o2 = ctb[:].rearrange("p (two c) -> p two c", two=2)
                            Av = A[:, 0:64].rearrange("p (two c) -> p two c", two=2)
                            Bv = Bm[:, 0:64].rearrange("p (two c) -> p two c", two=2)
                            self.tt('dve', Av, c2, rp[:, 0:32].unsqueeze(1).to_broadcast([128, 2, 32]), ALU.mult,
                                    [ctn.b, rp.b], [A.b])
                            self.tt('dve', Bv, c2, rp[:, 32:64].unsqueeze(1).to_broadcast([128, 2, 32]), ALU.mult,
                                    [ctn.b, rp.b], [Bm.b])
                            self.tt('dve', o2[:, 0, :], Av[:, 0, :], Bv[:, 1, :], ALU.subtract, [A.b, Bm.b], [ctb.b])
                            self.tt('dve', o2[:, 1, :], Bv[:, 0, :], Av[:, 1, :], ALU.add, [A.b, Bm.b], [ctb.b])
                            pT = self.ps[4]
                            pTb = pT[:].bitcast(BF16)
                            self.tr(pTb[0:64, 0:128], ctb[:], self.ident[:], [ctb.b, self.ident.b], [pT.b])
                            self.cp('act', kcmpT[:, g, kt * 128:(kt + 1) * 128], pTb[0:64, 0:128], [pT.b], [kcmpT.b])
            for g in range(2):
                for kt in range(nct):
                    self.memset('pool', vcmp[:, g, kt, 64:65], 1.0, [vcmp.b])
                    self.cp('pool', vcmp[:, g, kt, 65:65 + self.n_blk], ovl[:, kt, :], [ovl.b], [vcmp.b])
            self.S.barrier()

    def nsa_attn(self, es, kcmpT, vcmp):
        T, NT = self.T, self.NT
        nb = self.n_blk
        nct = self.ncmp_pad // 128
        dvc = 65 + nb
        ksT = self.sb(es, 'ksT', [64, T], BF16)
        kwT = self.sb(es, 'kwT', [64, T], BF16)
        vs = self.sb(es, 'vs', [128, NT, 65], BF16)
        vw = self.sb(es, 'vw', [128, NT, 65], BF16)
        qt_ = [self.sb(es, f'nq{i}', [64, 512], BF16) for i in range(2)]
        cneg = [self.sb(es, f'cneg{i}', [128, self.ncmp_pad], BF16) for i in range(2)]
        forced = [self.sb(es, f'forced{i}', [128, nb], F32) for i in range(2)]
        negm = [self.sb(es, f'negm{i}', [128, T], BF16) for i in range(2)]
        rec = self.sb(es, 'nrec', [128, 4], F32)
        score = self.sb(es, 'nscore', [128, nb], F32)
        work = self.sb(es, 'nwork', [128, nb], F32)
        m8 = self.sb(es, 'nm8', [128, 8], F32)
        thr = self.sb(es, 'nthr', [128, 1], F32)
        nsel = self.sb(es, 'nsel', [128, nb], BF16)
        ost = [self.sb(es, f'nost{i}', [128, 3, 4, 64], F32) for i in range(2)]
        accA, accB, accS, accW = self.ps[3], self.ps[4], self.ps[5], self.ps[6]
        cmp_eid = {}

        def stage1(g, qt, sl):
            q, cn, fo, nm, o = qt_[sl], cneg[sl], forced[sl], negm[sl], ost[sl]
            tsl = slice(qt * 128, (qt + 1) * 128)
            self.dma('sp', q[:], self.nsa_qT[g, qt].rearrange("d r t -> d (r t)"), [self.nsa_qT.b], [q.b])
            self.dma('sp', cn[:], self.c['cmpneg'][tsl, :], [], [cn.b])
            self.dma('sp', fo[:], self.c['forced'][tsl, :], [], [fo.b])
            views = {j: ((accA if j < 2 else accB)[:, (j % 2) * dvc:(j % 2 + 1) * dvc], j // 2) for j in range(4)}
            tiles = []
            for kt in range(nct):
                if 16 * 128 * kt + 31 > qt * 128 + 127:
                    continue
                tiles.append(dict(kT=kcmpT[:, g, kt * 128:(kt + 1) * 128], V=vcmp[:, g, kt, 0:dvc],
                                  R=[kcmpT.b, vcmp.b], c0=0, subs=[0, 1, 2, 3],
                                  masks=[(cn[:, kt * 128:(kt + 1) * 128], self.i4[:], 0, 512, [cn.b, self.i4.b])]))
            if not tiles:
                tiles.append(dict(kT=kcmpT[:, g, 0:128], V=vcmp[:, g, 0, 0:dvc], R=[kcmpT.b, vcmp.b], c0=0,
                                  subs=[0, 1, 2, 3],
                                  masks=[(cn[:, 0:128], self.i4[:], 0, 512, [cn.b, self.i4.b])]))
            cmp_tiles = tiles
            viewsW = {j: (accW[:, j * 65:(j + 1) * 65], 0) for j in range(4)}
            tiles = []
            for kt in range(max(0, qt - 4), qt + 1):
                tl = dict(kT=kwT[:, kt * 128:(kt + 1) * 128], V=vw[:, kt, :], R=[kwT.b, vw.b], c0=0,
                          subs=[0, 1, 2, 3], masks=[])
                if kt == qt:
                    tl['masks'].append((self.tri_neg[:], self.i4[:], 0, 512, [self.tri_neg.b, self.i4.b]))
                if kt == qt - 4:
                    tl['masks'].append((self.edge_neg[:], self.i4[:], 0, 512, [self.edge_neg.b, self.i4.b]))
                tiles.append(tl)
            win_tiles = tiles

            def epi_cmp():
                for j in range(4):
                    ab = accA if j < 2 else accB
                    v_ = views[j][0]
                    self.ts('dve', rec[:, j:j + 1], v_[:, 64:65], 1e-30, ALU.max, [ab.b], [rec.b])
                self.op_('dve', lambda e: e.reciprocal(out=rec[:], in_=rec[:]), reads=[rec.b], writes=[rec.b])
                for j in range(4):
                    ab = accA if j < 2 else accB
                    v_ = views[j][0]
                    self.ts('dve', o[:, 0, j, :], v_[:, 0:64], rec[:, j:j + 1], ALU.mult, [ab.b, rec.b], [o.b])
                    if j == 0:
                        self.ts('dve', score[:], v_[:, 65:65 + nb], rec[:, 0:1], ALU.mult, [ab.b, rec.b], [score.b])
                    else:
                        self.stt(score[:], v_[:, 65:65 + nb], rec[:, j:j + 1], score[:], ALU.mult, ALU.add,
                                 [ab.b, rec.b, score.b], [score.b])
                self.tt('dve', score[:], score[:], fo[:], ALU.add, [score.b, fo.b], [score.b])
                self.op_('dve', lambda e: e.max(out=m8[:], in_=score[:]), reads=[score.b], writes=[m8.b])
                self.op_('dve', lambda e: e.match_replace(out=work[:], in_to_replace=m8[:], in_values=score[:],
                                                           imm_value=-3e38), reads=[score.b, m8.b], writes=[work.b])
                self.op_('dve', lambda e: e.max(out=m8[:], in_=work[:]), reads=[work.b], writes=[m8.b])
                self.ts('dve', thr[:], m8[:, 7:8], -1e29, ALU.max, [m8.b], [thr.b])
                self.ts('dve', nsel[:], score[:], thr[:, 0:1], ALU.is_lt, [score.b, thr.b], [nsel.b], s2=-BIG, op1=ALU.mult)
                nblk_need = (qt + 1) * 2
                self.cp('pool', nm[:, 0:nblk_need * 64].rearrange("p (j c) -> p j c", c=64),
                        nsel[:, 0:nblk_need].unsqueeze(2).to_broadcast([128, nblk_need, 64]), [nsel.b], [nm.b])
                self.tt('pool', nm[:, tsl], nm[:, tsl], self.tri_neg[:], ALU.add, [nm.b, self.tri_neg.b], [nm.b])

            def epi_win():
                av = accW[:, 0:260].rearrange("p (j c) -> p j c", j=4)
                self.op_('dve', lambda e, av=av: e.reciprocal(out=rec[:], in_=av[:, :, 64]), reads=[accW.b], writes=[rec.b])
                self.tt('dve', o[:, 2], av[:, :, 0:64], rec[:].unsqueeze(2).to_broadcast([128, 4, 64]), ALU.mult,
                        [accW.b, rec.b], [o.b])

            eid = self.attn_block(q[:], [q.b], 512, cmp_tiles, 0.125, views, [accA.b, accB.b], epilogue=epi_cmp)
            self.attn_block(q[:], [q.b], 512, win_tiles, 0.125, viewsW, [accW.b], epilogue=epi_win)
            cmp_eid[(g, qt)] = eid

        def stage2(g, qt, sl):
            q, nm, o = qt_[sl], negm[sl], ost[sl]
            tsl = slice(qt * 128, (qt + 1) * 128)
            self.attn_sync(cmp_eid[(g, qt)])
            views = {j: (accS[:, j * 65:(j + 1) * 65], 0) for j in range(4)}
            tiles = [dict(kT=ksT[:, kt * 128:(kt + 1) * 128], V=vs[:, kt, :], R=[ksT.b, vs.b], c0=0,
                          subs=[0, 1, 2, 3],
                          masks=[(nm[:, kt * 128:(kt + 1) * 128], self.i4[:], 0, 512, [nm.b, self.i4.b])])
                     for kt in range(qt + 1)]
            def epi_sel():
                av = accS[:, 0:260].rearrange("p (j c) -> p j c", j=4)
                self.op_('dve', lambda e, av=av: e.reciprocal(out=rec[:], in_=av[:, :, 64]), reads=[accS.b], writes=[rec.b])
                self.tt('dve', o[:, 1], av[:, :, 0:64], rec[:].unsqueeze(2).to_broadcast([128, 4, 64]), ALU.mult,
                        [accS.b, rec.b], [o.b])
                for bi, base in enumerate((1280, 512, 1792)):
                    self.dma('sp', self.Y[tsl, base + g * 256:base + (g + 1) * 256],
                             o[:, bi].rearrange("p r c -> p (r c)"), [o.b], [self.Y.b])
            self.attn_block(q[:], [q.b], 512, tiles, 0.125, views, [accS.b], epilogue=epi_sel)

        for g in range(2):
            self.dma('sp', ksT[:], self.nsa_kT[0 + g], [self.nsa_kT.b], [ksT.b])
            self.dma('sp', kwT[:], self.nsa_kT[2 + g], [self.nsa_kT.b], [kwT.b])
            self.dma('sp', vs[:], self.nsa_v[0 + g], [self.nsa_v.b], [vs.b])
            self.dma('sp', vw[:], self.nsa_v[2 + g], [self.nsa_v.b], [vw.b])
            stage1(g, 0, 0)
            for qt in range(NT):
                if qt + 1 < NT:
                    stage1(g, qt + 1, (qt + 1) % 2)
                stage2(g, qt, qt % 2)
            self.attn_flush()

    def layer_odd(self, es, L, x_src, x_dst):
        li = L // 2
        w = self.w
        T, NT = self.T, self.NT
        S = self.S
        memkT, memV = self.mem_kv(es, L)
        with ExitStack() as p1:
            win, lng = self.load_win(p1, L, w['odd_w_in'][li], ODD_COLS)
            g_cq = self.bc_load(p1, 'g_cq', w['dsa_q_norm_g'][li:li + 1, :], 64)
            g_ck = self.bc_load(p1, 'g_ck', w['dsa_k_norm_g'][li:li + 1, :], 64)
            g_mq = self.bc_load(p1, 'g_mq', w['mem_q_norm_g'][L:L + 1, :], 64)
            xt = [self.sb(p1, f'xt{i}', [128, D], F32) for i in range(2)]
            ht = [self.sb(p1, f'ht{i}', [128, D], BF16) for i in range(2)]
            hT = [self.sb(p1, f'hT{i}', [128, 8, 128], BF16) for i in range(2)]
            u = [self.sb(p1, f'u{i}', [128, ODD_COLS], F32) for i in range(2)]
            ss = self.sb(p1, 'ss', [128, 1], F32)
            junk = self.sb(p1, 'junk', [128, D], BF16)
            gt = self.sb(p1, 'gt', [128, 1792], BF16)
            cqn = self.sb(p1, 'cqn', [128, 512], F32)
            cqf = self.sb(p1, 'cqf', [128, 512], BF16)
            ckn = self.sb(p1, 'ckn', [128, 64], F32)
            ckf = self.sb(p1, 'ckf', [128, 64], BF16)
            cva = self.sb(p1, 'cva', [128, 65], BF16)
            self.memset('pool', cva[:], 1.0, [cva.b])
            iqf = self.sb(p1, 'iqf', [128, 256], BF16)
            ikf = self.sb(p1, 'ikf', [128, 32], BF16)
            iwt = self.sb(p1, 'iwt', [128, 8], F32)
            mlv = self.sb(p1, 'mlv', [128, 4, 129], BF16)
            self.memset('pool', mlv[:], 1.0, [mlv.b])
            mqf = self.sb(p1, 'mqf', [128, 256], BF16)
            stq = self.sb(p1, 'stq', [64, 8, 128], BF16)
            stk = self.sb(p1, 'stk', [64, 1, 128], BF16)
            sti = self.sb(p1, 'sti', [32, 8, 128], BF16)
            stik = self.sb(p1, 'stik', [32, 1, 128], BF16)
            stmq = self.sb(p1, 'stmq', [64, 4, 128], BF16)
            strw = self.sb(p1, 'strw', [128, 4, 128], F32)
            stif = self.sb(p1, 'stif', [8, 128], F32)
            scrA = self.new_scr(p1, 'A', 512)
            scrB = self.new_scr(p1, 'B', 256)
            ut_next = self.p1_front(0, x_src, xt, ht, hT, u, ss, junk, lng, win, ODD_COLS)
            for n in range(NT):
                ut = ut_next
                U = lambda a, b_, ut=ut: ut[:, a:b_]
                ub = [ut.b]
                tsl = slice(n * 128, (n + 1) * 128)
                self.chains_begin(['A', 'F', 'B', 'L', 'M', 'G'])
                if n + 1 < NT:
                    self.chain('F')
                    ut_next = self.p1_front(n + 1, x_src, xt, ht, hT, u, ss, junk, lng, win, ODD_COLS)
                self.chain('G')
                self.act(gt[:, 0:512], U(936, 1448), AF.Silu, ub, [gt.b])
                self.act(gt[:, 512:1024], U(2992, 3504), AF.Silu, ub, [gt.b])
                self.act(gt[:, 1024:1280], U(3760, 4016), AF.Silu, ub, [gt.b])
                self.act(gt[:, 1280:1792], U(2480, 2992), AF.Sigmoid, ub, [gt.b])
                self.dma('sp', self.G[tsl, :], gt[:], [gt.b], [self.G.b[n]])
                self.chain('A', scrA)
                cq3 = cqn[:].rearrange("p (h c) -> p h c", h=8)
                self.rmsn(U(0, 512).rearrange("p (h c) -> p h c", h=8), 8, 64, g_cq, cq3, ub, [cqn.b])
                self.rope(cq3, 8, 32, n, cqf[:].rearrange("p (h c) -> p h c", h=8), [cqn.b], [cqf.b])
                ck3 = ckn[:].rearrange("p (h c) -> p h c", h=1)
                self.rmsn(U(512, 576).rearrange("p (h c) -> p h c", h=1), 1, 64, g_ck, ck3, ub, [ckn.b])
                self.rope(ck3, 1, 32, n, ckf[:].rearrange("p (h c) -> p h c", h=1), [ckn.b], [ckf.b])
                self.cp('dve', cva[:, 0:64], U(576, 640), ub, [cva.b])
                self.dma('sp', self.dsa_v[:, n, :], cva[:], [cva.b], [self.dsa_v.b])
                pT = self.ps[4]
                pTb = pT[:].bitcast(BF16)
                for h in range(8):
                    self.tr(pTb[0:64, h * 128:(h + 1) * 128], cqf[:, h * 64:(h + 1) * 64], self.ident[:],
                            [cqf.b, self.ident.b], [pT.b])
                self.cp('act', stq[:], pTb[0:64, 0:1024].rearrange("p (k c) -> p k c", k=8), [pT.b], [stq.b])
                self.dma('sp', self.dsa_qT[n], stq[:], [stq.b], [self.dsa_qT.b])
                self.transposes_out([(ckf[:], [ckf.b])], 64, stk,
                                    self.dsa_kT[:, tsl].rearrange("d (k t) -> d k t", k=1), [self.dsa_kT.b], 5)
                self.chain('B', scrB)
                self.rope(U(640, 896).rearrange("p (h c) -> p h c", h=8), 8, 16, n,
                          iqf[:].rearrange("p (h c) -> p h c", h=8), ub, [iqf.b])
                self.rope(U(896, 928).rearrange("p (h c) -> p h c", h=1), 1, 16, n,
                          ikf[:].rearrange("p (h c) -> p h c", h=1), ub, [ikf.b])
                self.ts('dve', iwt[:], U(928, 936), 8 ** -0.5, ALU.mult, ub, [iwt.b])
                self.dma('sp', self.idx_w[tsl, :], iwt[:], [iwt.b], [self.idx_w.b])
                self.transposes_out([(iqf[:, h * 32:(h + 1) * 32], [iqf.b]) for h in range(8)], 32, sti,
                                    self.idx_qT[:, :, tsl].rearrange("h d t -> d h t"), [self.idx_qT.b], 6)
                self.transposes_out([(ikf[:], [ikf.b])], 32, stik,
                                    self.idx_kT[:, tsl].rearrange("d (k t) -> d k t", k=1), [self.idx_kT.b], 7)
                self.chain('L')
                pR = self.ps[3]
                for k in range(4):
                    self.tr(pR[:, k * 128:(k + 1) * 128], U(1448 + k * 128, 1448 + (k + 1) * 128), self.identf[:],
                            ub + [self.identf.b], [pR.b])
                self.cp('act', strw[:].rearrange("p k c -> p (k c)"), pR[:, 0:512], [pR.b], [strw.b])
                self.dma('sp', self.ml_raw[:, tsl].rearrange("(k p) t -> p k t", p=128), strw[:], [strw.b],
                         [self.ml_raw.b])
                pI = self.ps[3]
                self.tr(pI[0:8, 0:128], U(2472, 2480), self.identf[:], ub + [self.identf.b], [pI.b])
                self.cp('act', stif[:], pI[0:8, 0:128], [pI.b], [stif.b])
                self.dma('sp', self.ml_if[:, tsl], stif[:], [stif.b], [self.ml_if.b])
                self.cp('dve', mlv[:, :, 0:128], U(1960, 2472).rearrange("p (h c) -> p h c", h=4), ub, [mlv.b])
                self.dma('sp', self.ml_v[:, :, n, :].rearrange("h p c -> p h c"), mlv[:], [mlv.b], [self.ml_v.b])
                self.chain('B', scrB)
                self.rmsn(U(3504, 3760).rearrange("p (h c) -> p h c", h=4), 4, 64, g_mq,
                          mqf[:].rearrange("p (h c) -> p h c", h=4), ub, [mqf.b])
                self.transposes_out([(mqf[:, h * 64:(h + 1) * 64], [mqf.b]) for h in range(4)], 64, stmq,
                                    self.mem_qT[:, :, tsl].rearrange("h d t -> d h t"), [self.mem_qT.b], 6)
                self.chains_emit()
            S.barrier()
        if 'stop_p1' in self.dbg:
            return
        self.mlstm_pre(li)
        S.barrier()
        if 'stop_pre' in self.dbg:
            return
        with ExitStack() as pa:
            self.attn_setup(pa)
            self.mlstm_attn(pa)
            S.barrier()
        if 'stop_ml' in self.dbg:
            return
        with ExitStack() as pa:
            self.attn_setup(pa)
            self.mem_attn(pa, memkT, memV)
            S.barrier()
        if 'stop_mem' in self.dbg:
            return
        with ExitStack() as pa:
            self.attn_setup(pa)
            self.dsa_attn(pa)
            S.barrier()
        if 'stop_dsa' in self.dbg:
            return
        with ExitStack() as p3:
            wout = self.load_wout(p3, L)
            g_h = self.bc_load(p3, 'g_h', w['mlstm_h_norm_g'][li:li + 1, :], 128)
            yt = [self.sb(p3, f'yt{i}', [128, 1280], F32) for i in range(2)]
            gtt = [self.sb(p3, f'gtt{i}', [128, 1792], BF16) for i in range(2)]
            mix = [self.sb(p3, f'mix{i}', [128, 1280], BF16) for i in range(2)]
            hns = [self.sb(p3, f'hn{i}', [128, 512], F32) for i in range(2)]
            scrP = [self.new_scr(p3, f'P{i}', 512) for i in range(2)]

            def mixfn(n):
                sl = n % 2
                y, g, m = yt[sl], gtt[sl], mix[sl]
                hn = hns[sl]
                self.scr = scrP[sl]
                tsl = slice(n * 128, (n + 1) * 128)
                self.dma('sp', y[:], self.Y[tsl, 0:1280], [self.Y.b], [y.b])
                self.dma('sp', g[:], self.G[tsl, :], [self.G.b[n]], [g.b])
                self.tt('dve', m[:, 0:512], y[:, 0:512], g[:, 0:512], ALU.mult, [y.b, g.b], [m.b])
                self.tt('pool', m[:, 1024:1280], y[:, 1024:1280], g[:, 1024:1280], ALU.mult, [y.b, g.b], [m.b])
                self.rmsn(y[:, 512:1024].rearrange("p (h c) -> p h c", h=4), 4, 128, g_h,
                          hn[:].rearrange("p (h c) -> p h c", h=4), [y.b], [hn.b])
                self.tt('dve', hn[:], hn[:], g[:, 1280:1792], ALU.mult, [hn.b, g.b], [hn.b])
                self.tt('dve', m[:, 512:1024], hn[:], g[:, 512:1024], ALU.mult, [hn.b, g.b], [m.b])
                return m
            self.p3(p3, L, x_src, x_dst, wout, mixfn)
            S.barrier()

    def mlstm_pre(self, li):
        w = self.w
        T = self.T
        with ExitStack() as es:
            xp = [self.sb(es, f'xp{i}', [128, T + 3], F32) for i in range(2)]
            y = self.sb(es, 'cy', [128, T], F32)
            yo = [self.sb(es, f'cyo{i}', [128, T], BF16) for i in range(2)]
            wc = self.sb(es, 'cwc', [128, 4, 4], F32)
            bc = self.sb(es, 'cbc', [128, 4], F32)
            for ck in range(4):
                self.dma('sp', wc[:, ck, :], w['mlstm_conv_wT'][li, ck * 128:(ck + 1) * 128, :], [], [wc.b])
                self.dma('sp', bc[:, ck:ck + 1], w['mlstm_conv_b'][li, ck * 128:(ck + 1) * 128].unsqueeze(1), [], [bc.b])
            for ck in range(4):
                x = xp[ck % 2]
                o = yo[ck % 2]
                self.memset('pool', x[:, 0:3], 0.0, [x.b])
                self.dma('sp', x[:, 3:T + 3], self.ml_raw[ck * 128:(ck + 1) * 128, :], [self.ml_raw.b], [x.b])
                self.ts('dve', y[:], x[:, 0:T], wc[:, ck, 0:1], ALU.mult, [x.b, wc.b, bc.b], [y.b],
                        s2=bc[:, ck:ck + 1], op1=ALU.add)
                for j in range(1, 4):
                    self.stt(y[:], x[:, j:j + T], wc[:, ck, j:j + 1], y[:], ALU.mult, ALU.add, [x.b, wc.b, y.b], [y.b])
                self.act(o[:], y[:], AF.Silu, [y.b], [o.b])
                self.dma('sp', self.ml_qkT[ck * 128:(ck + 1) * 128, :], o[:], [o.b], [self.ml_qkT.b])
            self.S.barrier()
        with ExitStack() as es:
            ig = self.sb(es, 'ig', [4, T], F32)
            fg = self.sb(es, 'fg', [4, T], F32)
            cs = self.sb(es, 'cs', [4, T], F32)
            a = self.sb(es, 'ga', [4, T], F32)
            Mt = self.sb(es, 'gM', [4, T], F32)
            ones = self.sb(es, 'gones', [4, T], F32)
            ib = self.sb(es, 'gib', [4, 1], F32)
            fb = self.sb(es, 'gfb', [4, 1], F32)
            self.memset('pool', ones[:], 1.0, [ones.b])
            self.dma('sp', ig[:], self.ml_if[0:4, :], [self.ml_if.b], [ig.b])
            self.dma('sp', fg[:], self.ml_if[4:8, :], [self.ml_if.b], [fg.b])
            self.dma('sp', ib[:], w['mlstm_i_bias'][li].unsqueeze(1), [], [ib.b])
            self.dma('sp', fb[:], w['mlstm_f_bias'][li].unsqueeze(1), [], [fb.b])
            self.ts('dve', fb[:], fb[:], -1.0, ALU.mult, [fb.b], [fb.b])
            self.act(fg[:], fg[:], AF.Exp, [fg.b, fb.b], [fg.b], bias=fb[:, 0:1], scale=-1.0)
            self.act(fg[:], fg[:], AF.Ln, [fg.b], [fg.b], bias=self.one_c[0:4, 0:1], scale=1.0)
            self.op_('dve', lambda e: e.tensor_tensor_scan(out=cs[:], data0=ones[:], data1=fg[:], initial=0.0,
                                                             op0=ALU.mult, op1=ALU.add),
                      reads=[ones.b, fg.b], writes=[cs.b])
            self.stt(a[:], ig[:], ib[:, 0:1], cs[:], ALU.add, ALU.add, [ig.b, ib.b, cs.b], [a.b])
            self.op_('dve', lambda e: e.tensor_tensor_scan(out=Mt[:], data0=a[:], data1=a[:], initial=0.0,
                                                             op0=ALU.max, op1=ALU.max),
                      reads=[a.b], writes=[Mt.b])
            self.tt('dve', cs[:], cs[:], Mt[:], ALU.subtract, [cs.b, Mt.b], [cs.b])
            self.act(cs[:], cs[:], AF.Exp, [cs.b], [cs.b])
            self.ts('dve', Mt[:], Mt[:], -1.0, ALU.mult, [Mt.b], [Mt.b])
            self.dma('sp', self.ml_g[0:4, :], a[:], [a.b], [self.ml_g.b])
            self.dma('sp', self.ml_g[4:8, :], Mt[:], [Mt.b], [self.ml_g.b])
            self.dma('sp', self.ml_g[8:12, :], cs[:], [cs.b], [self.ml_g.b])
            self.S.barrier()

    def mlstm_attn(self, es):
        T, NT = self.T, self.NT
        qT = [self.sb(es, f'lqT{i}', [64, T], BF16) for i in range(2)]
        kT = [self.sb(es, f'lkT{i}', [64, T], BF16) for i in range(2)]
        V = [self.sb(es, f'lV{i}', [128, NT, 129], BF16) for i in range(2)]
        nM = [self.sb(es, f'lnM{i}', [128, T], F32) for i in range(2)]
        ant = self.sb(es, 'lant', [NT, 2, 128], F32)
        atm = [self.sb(es, f'latm{i}', [128, 2, NT], F32) for i in range(2)]
        Et = [self.sb(es, f'lEt{i}', [128, 512], F32) for i in range(3)]
        ost = [self.sb(es, f'lost{i}', [128, 4, 128], F32) for i in range(2)]
        d2 = self.sb(es, 'ld2', [128, 4], F32)
        ecnt = [0]
        blk = 0
        LN8 = math.log(0.125)
        for h in range(4):
            q, k, v, nm, at = qT[h % 2], kT[h % 2], V[h % 2], nM[h % 2], atm[h % 2]
            self.dma('sp', q[:], self.ml_qkT[h * 64:(h + 1) * 64, :], [self.ml_qkT.b], [q.b])
            self.dma('sp', k[:], self.ml_qkT[256 + h * 64:256 + (h + 1) * 64, :], [self.ml_qkT.b], [k.b])
            self.dma('sp', v[:], self.ml_v[h], [self.ml_v.b], [v.b])
            self.dma('sp', nm[:], self.ml_g[4 + h:5 + h, :].to_broadcast([128, T]), [self.ml_g.b], [nm.b])
            self.dma('sp', ant[:, 0, :], self.ml_g[h, :].rearrange("(n p) -> n p", p=128), [self.ml_g.b], [ant.b])
            self.dma('sp', ant[:, 1, :], self.ml_g[8 + h, :].rearrange("(n p) -> n p", p=128), [self.ml_g.b], [ant.b])
            pA = self.ps[7]
            for i in range(2):
                self.tr(pA[:, i * NT:(i + 1) * NT], ant[:, i, :], self.identf[0:NT, 0:NT], [ant.b, self.identf.b], [pA.b])
            self.cp('act', at[:].rearrange("p a n -> p (a n)"), pA[:, 0:2 * NT], [pA.b], [at.b])
            self.ts('dve', at[:, 0, :], at[:, 0, :], LN8, ALU.add, [at.b], [at.b])
            for cq in range(T // 512):
                set_i = blk % 2
                blk += 1
                accA, accB = self.ps[3 + 2 * set_i], self.ps[4 + 2 * set_i]
                views = {j: ((accA if j < 2 else accB)[:, (j % 2) * 129:(j % 2 + 1) * 129], j // 2) for j in range(4)}
                tiles = []
                for kt in range(4 * cq + 4):
                    vv = kt - 4 * cq
                    c0 = max(0, vv) * 128

                    def efn(kt=kt, c0=c0, cq=cq, nm=nm, at=at):
                        E = Et[ecnt[0] % 3]
                        ecnt[0] += 1
                        self.act(E[:, c0:512], nm[:, cq * 512 + c0:(cq + 1) * 512], AF.Exp, [nm.b, at.b], [E.b],
                                 bias=at[:, 0, kt:kt + 1], scale=1.0)
                        return E, [E.b]
                    tl = dict(kT=k[:, kt * 128:(kt + 1) * 128], V=v[:, kt, :], R=[k.b, v.b], c0=c0,
                              subs=list(range(max(0, vv), 4)), Efn=efn)
                    if vv >= 0:
                        tl['diag'] = vv
                    tiles.append(tl)
                def epi(accA=accA, accB=accB, views=views, o=ost[set_i], cq=cq, h=h, at=at):
                    for j in range(4):
                        ab = accA if j < 2 else accB
                        v_ = views[j][0]
                        self.act(d2[:, j:j + 1], v_[:, 128:129], AF.Abs, [ab.b], [d2.b])
                    self.tt('dve', d2[:], d2[:], at[:, 1, 4 * cq:4 * cq + 4], ALU.max, [d2.b, at.b], [d2.b])
                    self.op_('dve', lambda e: e.reciprocal(out=d2[:], in_=d2[:]), reads=[d2.b], writes=[d2.b])
                    for j in range(4):
                        ab = accA if j < 2 else accB
                        v_ = views[j][0]
                        self.ts('dve', o[:, j, :], v_[:, 0:128], d2[:, j:j + 1], ALU.mult, [ab.b, d2.b], [o.b])
                    self.dma('sp', self.Y[cq * 512:(cq + 1) * 512, 512 + h * 128:512 + (h + 1) * 128]
                             .rearrange("(j p) c -> p j c", p=128), o[:], [o.b], [self.Y.b])
                self.attn_block(q[:, cq * 512:(cq + 1) * 512], [q.b], 512, tiles, 1.0, views, [accA.b, accB.b],
                                mode='mul', epilogue=epi)
        self.attn_flush()

    def dsa_attn(self, es):
        T, NT = self.T, self.NT
        KSEL = min(256, T // 4)
        ikT = self.sb(es, 'ikT', [32, T], BF16)
        ckT = self.sb(es, 'ckT', [64, T], BF16)
        cv = self.sb(es, 'cv', [128, NT, 65], BF16)
        self.dma('sp', ikT[:], self.idx_kT[:, :], [self.idx_kT.b], [ikT.b])
        self.dma('sp', ckT[:], self.dsa_kT[:, :], [self.dsa_kT.b], [ckT.b])
        self.dma('sp', cv[:], self.dsa_v[:, :, :], [self.dsa_v.b], [cv.b])
        iq = [self.sb(es, f'iq{i}', [32, 8, 128], BF16) for i in range(2)]
        iw = [self.sb(es, f'iw{i}', [128, 8], F32) for i in range(2)]
        cq_ = [self.sb(es, f'cq{i}', [64, 2, 512], BF16) for i in range(2)]
        score2 = [self.sb(es, f'dscore{i}', [128, T], F32) for i in range(3)]
        thrA2 = [self.sb(es, f'dthrA{i}', [128, 1], F32) for i in range(2)]
        work = self.sb(es, 'dwork', [128, T], F32)
        negm = [self.sb(es, f'dnegm{i}', [128, T], BF16) for i in range(2)]
        rl = [self.sb(es, f'drl{i}', [128, 512], F32) for i in range(3)]
        m8 = self.sb(es, 'dm8', [128, 8], F32)
        thr = self.sb(es, 'dthr', [128, 1], F32)
        rec = self.sb(es, 'drec', [128, 4], F32)
        ost = [self.sb(es, f'dost{i}', [128, 4, 64], F32) for i in range(2)]
        rlc_ = [0]
        blk_ = [0]
        NBIS = 20
        junk = self.sb(es, 'djunk', [128, T], BF16)
        amax = self.sb(es, 'damax', [128, 1], F32)
        w0 = self.sb(es, 'dw0', [128, 1], F32)
        nHh = self.sb(es, 'dnHh', [128, 40], F32)
        nmid = [self.sb(es, f'dnmid{i}', [128, 1], F32) for i in range(2)]
        Ssum = self.sb(es, 'dS', [128, 1], F32)
        tsg = self.sb(es, 'dtsg', [128, 1], F32)

        def stage_a1(qt):
            rlc = rlc_[0]
            sl = qt % 2
            tsl = slice(qt * 128, (qt + 1) * 128)
            q_i, w_i = iq[sl], iw[sl]
            score = score2[qt % 3]
            self.dma('sp', q_i[:], self.idx_qT[:, :, tsl].rearrange("h d t -> d h t"), [self.idx_qT.b], [q_i.b])
            self.dma('sp', w_i[:], self.idx_w[tsl, :], [self.idx_w.b], [w_i.b])
            ncols = (qt + 1) * 128
            for c in range((ncols + 511) // 512):
                c0 = c * 512
                wd = min(512, ncols - c0)
                for h in range(8):
                    bi = self.st_rr % len(self.st_banks)
                    self.st_rr += 1
                    bank, bb = self.st_banks[bi]
                    self.mm(bank[:, 0:wd], q_i[:, h, :], ikT[:, c0:c0 + wd], True, True, [q_i.b, ikT.b], [bb])
                    if h == 0:
                        self.ts('dve', score[:, c0:c0 + wd], bank[:, 0:wd], 0.0, ALU.max, [bb, w_i.b], [score.b],
                                s2=w_i[:, 0:1], op1=ALU.mult)
                    else:
                        r = rl[rlc % 3]
                        rlc += 1
                        self.ts('dve', r[:, 0:wd], bank[:, 0:wd], 0.0, ALU.max, [bb, w_i.b], [r.b],
                                s2=w_i[:, h:h + 1], op1=ALU.mult)
                        self.tt('dve', score[:, c0:c0 + wd], score[:, c0:c0 + wd], r[:, 0:wd], ALU.add,
                                [score.b, r.b], [score.b])
            rlc_[0] = rlc

        def is_act_tile(qt):
            return ((qt + 1) * 128 > KSEL) and ('dsa_nobis' not in self.dbg)

        def stage_a2_finish(qt):
            sl = qt % 2
            nm = negm[sl]
            score = score2[qt % 3]
            ncols = (qt + 1) * 128
            th = thrA2[qt % 2] if is_act_tile(qt) else thr
            self.ts('dve', nm[:, 0:ncols], score[:, 0:ncols], th[:, 0:1], ALU.is_lt, [score.b, th.b], [nm.b],
                    s2=-BIG, op1=ALU.mult)

        def stage_a2(qt):
            sl = qt % 2
            tsl = slice(qt * 128, (qt + 1) * 128)
            q_c, nm = cq_[sl], negm[sl]
            score = score2[qt % 3]
            ncols = (qt + 1) * 128
            for half in range(2):
                self.dma('sp', q_c[:, half, :], self.dsa_qT[qt, :, half * 4:(half + 1) * 4, :].rearrange("d h t -> d (h t)"),
                         [self.dsa_qT.b], [q_c.b])
            use_act = is_act_tile(qt)
            if use_act:
                self.op_('dve', lambda e, ncols=ncols: e.tensor_reduce(out=amax[:], in_=score[:, 0:ncols], axis=AX.X,
                                                                      op=ALU.max, apply_absolute_value=True),
                         reads=[score.b], writes=[amax.b])
                self.ts('dve', w0[:], amax[:], 2.0, ALU.mult, [amax.b], [w0.b], s2=2.0, op1=ALU.add)
                self.ts('dve', nHh[:], self.pw[:], w0[:, 0:1], ALU.mult, [self.pw.b, w0.b], [nHh.b])
            self.tt('dve', score[:, tsl], score[:, tsl], self.trinegf[:], ALU.add, [score.b, self.trinegf.b], [score.b])
            if use_act:
                cconst = float(0.5 - (2 * KSEL - ncols - 1))
                self.memset('pool', nmid[0][:], 0.0, [nmid[0].b])
                for j in range(NBIS):
                    cur, nxt = nmid[j % 2], nmid[(j + 1) % 2]
                    self.act(junk[:, 0:ncols], score[:, 0:ncols], AF.Sign, [score.b, cur.b], [junk.b, Ssum.b],
                             bias=cur[:, 0:1], scale=1.0, accum=Ssum[:, 0:1])
                    self.act(tsg[:], Ssum[:], AF.Sign, [Ssum.b], [tsg.b], bias=cconst, scale=1.0)
                    self.act(nxt[:], tsg[:], AF.Identity, [tsg.b, cur.b, nHh.b], [nxt.b],
                             bias=cur[:, 0:1], scale=nHh[:, j:j + 1])
                fin = nmid[NBIS % 2]
                thrA = thrA2[qt % 2]
                self.act(thrA[:], fin[:], AF.Identity, [fin.b, nHh.b], [thrA.b], bias=nHh[:, NBIS - 1:NBIS], scale=-1.0)
                return
            elif ncols > KSEL and 'dsa_notopk' not in self.dbg:
                self.cp('pool', work[:, 0:ncols], score[:, 0:ncols], [score.b], [work.b])
                nr = KSEL // 8
                for r_ in range(nr):
                    self.op_('dve', lambda e, ncols=ncols: e.max(out=m8[:], in_=work[:, 0:ncols]),
                             reads=[work.b], writes=[m8.b])
                    if r_ < nr - 1:
                        self.op_('dve', lambda e, ncols=ncols: e.match_replace(
                            out=work[:, 0:ncols], in_to_replace=m8[:], in_values=work[:, 0:ncols], imm_value=-3e38),
                            reads=[work.b, m8.b], writes=[work.b])
                self.ts('dve', thr[:], m8[:, 7:8], -1e29, ALU.max, [m8.b], [thr.b])
            else:
                self.memset('dve', thr[:], -1e29, [thr.b])
            stage_a2_finish(qt)

        def stage_b(qt):
            blk = blk_[0]
            sl = qt % 2
            tsl = slice(qt * 128, (qt + 1) * 128)
            q_c, nm = cq_[sl], negm[sl]
            for half in range(2):
                set_i = blk % 2
                blk += 1
                accb = self.ps[3 + set_i]
                views = {j: (accb[:, j * 65:(j + 1) * 65], 0) for j in range(4)}
                tiles = [dict(kT=ckT[:, kt * 128:(kt + 1) * 128], V=cv[:, kt, :], R=[ckT.b, cv.b], c0=0,
                              subs=[0, 1, 2, 3],
                              masks=[(nm[:, kt * 128:(kt + 1) * 128], self.i4[:], 0, 512, [nm.b, self.i4.b])])
                         for kt in range(qt + 1)]
                def epi(accb=accb, o=ost[set_i], tsl=tsl, half=half):
                    av = accb[:, 0:260].rearrange("p (j c) -> p j c", j=4)
                    self.op_('dve', lambda e, av=av: e.reciprocal(out=rec[:], in_=av[:, :, 64]), reads=[accb.b], writes=[rec.b])
                    self.tt('dve', o[:], av[:, :, 0:64], rec[:].unsqueeze(2).to_broadcast([128, 4, 64]), ALU.mult,
                            [accb.b, rec.b], [o.b])
                    self.dma('sp', self.Y[tsl, half * 256:(half + 1) * 256], o[:].rearrange("p j c -> p (j c)"),
                             [o.b], [self.Y.b])
                self.attn_block(q_c[:, half, :], [q_c.b], 512, tiles, 0.125, views, [accb.b], epilogue=epi)
            blk_[0] = blk

        stage_a1(0)
        if NT > 1:
            stage_a1(1)
        stage_a2(0)
        for qt in range(NT):
            if qt + 2 < NT:
                stage_a1(qt + 2)
            self.chains_begin(['X', 'Y'])
            if qt + 1 < NT:
                self.chain('X')
                stage_a2(qt + 1)
            self.chain('Y')
            if is_act_tile(qt):
                stage_a2_finish(qt)
            stage_b(qt)
            self.chains_emit(proportional=True)
        self.attn_flush()


def host_consts(T):
    bf = ml_dtypes.bfloat16
    nb = T // 64
    ncp = max(1, T // 2048) * 128
    n_cmp = (T - 32) // 16 + 1
    c = {}
    c['c_ident'] = np.eye(128, dtype=np.float32).astype(bf)
    c['c_identf'] = np.eye(128, dtype=np.float32)
    c['c_i4'] = np.tile(np.eye(128, dtype=np.float32), (1, 4)).astype(bf)
    i8 = np.zeros((128, 512), np.float32)
    for p in range(128):
        for h in range(8):
            i8[p, h * 64 + p % 64] = 1.0
    c['c_i8x2'] = i8.astype(bf)
    t = np.arange(128)[:, None]
    s = np.arange(128)[None, :]
    c['c_tri_neg'] = np.where(s > t, -BIG, 0.0).astype(np.float32).astype(bf)
    c['c_edge_neg'] = np.where(s <= t, -BIG, 0.0).astype(np.float32).astype(bf)
    c['c_tri01T'] = np.where(t <= s, 1.0, 0.0).astype(np.float32).astype(bf)
    c['c_trinegf'] = np.where(s > t, -1e30, 0.0).astype(np.float32)
    inv = (10000.0 ** (-np.arange(32, dtype=np.float32) / 32)).astype(np.float32)
    c['c_invf'] = np.tile(inv[None, :], (128, 1)).astype(np.float32)
    c['c_pw'] = np.tile((-(2.0 ** -(np.arange(40, dtype=np.float64) + 2)))[None, :], (128, 1)).astype(np.float32)
    tt = np.arange(T)[:, None]
    n = np.arange(ncp)[None, :]
    cm = np.where((16 * n + 31 <= tt) & (n < n_cmp), 0.0, -BIG)
    c['c_cmpneg'] = cm.astype(np.float32).astype(bf)
    j = np.arange(nb)[None, :]
    cur = tt // 64
    forced = np.where(j == cur, 3e4, np.where(j == cur - 1, 2e4, np.where(j == 0, 1e4, 0.0)))
    forced = np.where(j * 64 <= tt, forced, -1e30)
    c['c_forced'] = forced.astype(np.float32)
    ni = np.arange(ncp)[:, None]
    ov = ((16 * ni <= j * 64 + 63) & (16 * ni + 31 >= j * 64) & (ni < n_cmp))
    c['c_overlap'] = ov.astype(np.float32).astype(bf)
    return c


_CACHE = {}


def make_in_maps(inputs, T, ncores):
    NT = T // 128
    consts = host_consts(T)
    wnames = ["ln_g", "mem_norm_g", "mem_w_kv", "mem_q_norm_g", "mem_k_norm_g", "w_out", "even_w_in",
              "mla_q_lat_g", "mla_kv_lat_g", "mla_w_uq", "mla_w_ukv", "mla_q_norm_g", "mla_k_norm_g",
              "nsa_q_norm_g", "nsa_k_norm_g", "nsa_cmp_w1", "nsa_cmp_w2", "odd_w_in", "dsa_q_norm_g",
              "dsa_k_norm_g", "mlstm_conv_b", "mlstm_i_bias", "mlstm_f_bias", "mlstm_h_norm_g"]
    shared = {k: np.ascontiguousarray(np.asarray(inputs[k], dtype=np.float32)) for k in wnames}
    shared["nsa_cmp_posT"] = np.ascontiguousarray(np.transpose(np.asarray(inputs["nsa_cmp_pos"], np.float32), (0, 1, 3, 2)))
    shared["mlstm_conv_wT"] = np.ascontiguousarray(np.transpose(np.asarray(inputs["mlstm_conv_w"], np.float32), (0, 2, 1)))
    shared.update(consts)
    maps = []
    for c in range(ncores):
        m = dict(shared)
        m["x"] = np.ascontiguousarray(np.asarray(inputs["x"][c, :T], np.float32))
        m["mem"] = np.ascontiguousarray(np.asarray(inputs["mem"][c], np.float32))
        pos = np.asarray(inputs["positions"][c, :T]).astype(np.int32)
        m["pos_t"] = np.ascontiguousarray(pos.reshape(NT, 128).T)
        maps.append(m)
    return maps


def kernel(**inputs):
    T = 4096
    key = ('full', T)
    if key not in _CACHE:
        _CACHE[key] = Builder(T, [0, 1, 2, 3]).build()
    nc = _CACHE[key]
    maps = make_in_maps(inputs, T, 8)
    res = run_bass_kernel_spmd(nc, maps, core_ids=list(range(8)))
    out = np.stack([np.asarray(r["out"], dtype=np.float32) for r in res.results], axis=0)
    return out
```
